# Optimizing a Trainium2 kernel written in Bass

```python
import math
import jax, jax.numpy as jnp
from jax import lax
import numpy as np

D_MODEL = 1024
BATCH = 2
SEQ = 8192
DEPTH = 2

N_BRANCHES = 4
HEAD_DIM = 64
N_HEADS = D_MODEL // (N_BRANCHES * HEAD_DIM)
W_MIX = N_HEADS * HEAD_DIM
DIFF_QK_DIM = HEAD_DIM // 2
FOX_FORGET_BIAS = 3.0
NSA_CMP_BLOCK = 32
NSA_CMP_STRIDE = 16
NSA_SLC_BLOCK = 64
NSA_TOPK = 16
NSA_WINDOW = 512
NSA_PHI_HIDDEN = 256
NSA_FORCE_SCORE = 1.0e4
GMLP_CHUNK = 128
Q_BLOCK = 128
D_FF = 2816
RMS_EPS = 1e-6
IN_SPLIT_SIZES = (W_MIX, W_MIX, W_MIX,
                  W_MIX, W_MIX, W_MIX, N_HEADS,
                  W_MIX, 6 * HEAD_DIM, 3 * N_HEADS,
                  2 * W_MIX,
                  N_BRANCHES * D_MODEL)
D_IN = sum(IN_SPLIT_SIZES)

kernel_name = "hybrid_diff_fox_nsa_gmlp_macaron"


def rms_norm(x, g=None):
    xf = x.astype(jnp.float32)
    y = xf * lax.rsqrt(jnp.mean(xf * xf, axis=-1, keepdims=True) + RMS_EPS)
    if g is not None:
        y = y * g.astype(jnp.float32)
    return y.astype(x.dtype)


def masked_softmax(s, mask):
    s = jnp.where(mask, s.astype(jnp.float32), -jnp.inf)
    m = jnp.max(s, axis=-1, keepdims=True)
    m = jnp.where(jnp.isfinite(m), m, 0.0)
    p = jnp.exp(s - m)
    return p / jnp.maximum(jnp.sum(p, axis=-1, keepdims=True), 1e-30)


def split_cols(z, sizes):
    idx = np.cumsum(np.array(sizes))[:-1].tolist()
    return jnp.split(z, idx, axis=-1)


def blocks_to_seq(o):
    nq, b, qb, h, d = o.shape
    return o.transpose(1, 0, 2, 3, 4).reshape(b, nq * qb, h, d)


def swiglu(h, w_in, w_out):
    a, b = jnp.split(h @ w_in, 2, axis=-1)
    return (jax.nn.silu(a) * b) @ w_out


def diff_attention(q, k, v, lam):
    T = q.shape[1]
    scale = q.shape[-1] ** -0.5
    kpos = jnp.arange(T)

    def block(i):
        q0 = i * Q_BLOCK
        qb = lax.dynamic_slice_in_dim(q, q0, Q_BLOCK, axis=1)
        s = jnp.einsum('bqhcd,bkhcd->bchqk', qb, k) * scale
        qpos = q0 + jnp.arange(Q_BLOCK)
        p = masked_softmax(s, kpos[None, :] <= qpos[:, None])
        w = p[:, 0] - lam * p[:, 1]
        return jnp.einsum('bhqk,bkhd->bqhd', w.astype(v.dtype), v)

    return blocks_to_seq(lax.map(block, jnp.arange(T // Q_BLOCK)))


def forgetting_attention(q, k, v, log_f):
    T = q.shape[1]
    scale = q.shape[-1] ** -0.5
    c = jnp.cumsum(log_f, axis=1).transpose(0, 2, 1)
    kpos = jnp.arange(T)

    def block(i):
        q0 = i * Q_BLOCK
        qb = lax.dynamic_slice_in_dim(q, q0, Q_BLOCK, axis=1)
        cq = lax.dynamic_slice_in_dim(c, q0, Q_BLOCK, axis=2)
        s = jnp.einsum('bqhd,bkhd->bhqk', qb, k).astype(jnp.float32) * scale
        s = s + cq[..., :, None] - c[..., None, :]
        qpos = q0 + jnp.arange(Q_BLOCK)
        p = masked_softmax(s, kpos[None, :] <= qpos[:, None])
        return jnp.einsum('bhqk,bkhd->bqhd', p.astype(v.dtype), v)

    return blocks_to_seq(lax.map(block, jnp.arange(T // Q_BLOCK)))


def nsa_compress(x, pe, w1, b1, w2, b2):
    B, T, D = x.shape
    xb = x.reshape(B, T // NSA_CMP_STRIDE, NSA_CMP_STRIDE, D)
    blocks = jnp.concatenate([xb[:, :-1], xb[:, 1:]], axis=2)
    zf = (blocks + pe).reshape(B, blocks.shape[1], NSA_CMP_BLOCK * D)
    return jax.nn.gelu(zf @ w1 + b1) @ w2 + b2


def nsa_attention(q, kc, vc, ks, vs, kw, vw, gates):
    B, T, H, D = q.shape
    nc = kc.shape[1]
    ns = T // NSA_SLC_BLOCK
    n_top = min(NSA_TOPK, ns)
    scale = D ** -0.5
    cidx = jnp.arange(nc)
    sidx = jnp.arange(ns)
    cmp_end = cidx * NSA_CMP_STRIDE + NSA_CMP_BLOCK - 1
    overlap = ((cidx[:, None] * NSA_CMP_STRIDE < (sidx[None, :] + 1) * NSA_SLC_BLOCK)
               & (cidx[:, None] * NSA_CMP_STRIDE + NSA_CMP_BLOCK > sidx[None, :] * NSA_SLC_BLOCK)
               ).astype(jnp.float32)
    ks_blocks = ks.reshape(B, ns, NSA_SLC_BLOCK, D)
    vs_blocks = vs.reshape(B, ns, NSA_SLC_BLOCK, D)
    kw_pad = jnp.pad(kw, ((0, 0), (NSA_WINDOW, 0), (0, 0)))
    vw_pad = jnp.pad(vw, ((0, 0), (NSA_WINDOW, 0), (0, 0)))
    tok_off = jnp.arange(NSA_SLC_BLOCK)
    win_off = jnp.arange(NSA_WINDOW + Q_BLOCK) - NSA_WINDOW
    gather = jax.vmap(lambda blocks, idx: blocks[idx])
    n_sel = n_top * NSA_SLC_BLOCK

    def block(i):
        q0 = i * Q_BLOCK
        qpos = q0 + jnp.arange(Q_BLOCK)
        qb = lax.dynamic_slice_in_dim(q, q0, Q_BLOCK, axis=1)
        gb = jax.nn.sigmoid(lax.dynamic_slice_in_dim(gates, q0, Q_BLOCK, axis=1).astype(jnp.float32))
        s_c = jnp.einsum('bqhd,bnd->bhqn', qb, kc) * scale
        p_c = masked_softmax(s_c, cmp_end[None, :] <= qpos[:, None])
        o_c = jnp.einsum('bhqn,bnd->bqhd', p_c.astype(vc.dtype), vc)
        imp = jnp.einsum('bhqn,nj->bqj', p_c, overlap)
        cur = qpos // NSA_SLC_BLOCK
        forced = ((sidx[None, :] == 0) | (sidx[None, :] == cur[:, None])
                  | (sidx[None, :] == cur[:, None] - 1))
        valid = sidx[None, :] * NSA_SLC_BLOCK <= qpos[:, None]
        imp = jnp.where(forced, NSA_FORCE_SCORE, jnp.where(valid, imp, -1.0))
        _, sel = lax.top_k(imp, n_top)
        kg = gather(ks_blocks, sel).reshape(B, Q_BLOCK, n_sel, D)
        vg = gather(vs_blocks, sel).reshape(B, Q_BLOCK, n_sel, D)
        tok = (sel[..., None] * NSA_SLC_BLOCK + tok_off).reshape(B, Q_BLOCK, n_sel)
        s_s = jnp.einsum('bqhd,bqmd->bhqm', qb, kg) * scale
        p_s = masked_softmax(s_s, (tok <= qpos[None, :, None])[:, None])
        o_s = jnp.einsum('bhqm,bqmd->bqhd', p_s.astype(vg.dtype), vg)
        kwb = lax.dynamic_slice_in_dim(kw_pad, q0, NSA_WINDOW + Q_BLOCK, axis=1)
        vwb = lax.dynamic_slice_in_dim(vw_pad, q0, NSA_WINDOW + Q_BLOCK, axis=1)
        kpos = q0 + win_off
        dist = qpos[:, None] - kpos[None, :]
        mask_w = (kpos[None, :] >= 0) & (dist >= 0) & (dist < NSA_WINDOW)
        s_w = jnp.einsum('bqhd,bkd->bhqk', qb, kwb) * scale
        p_w = masked_softmax(s_w, mask_w)
        o_w = jnp.einsum('bhqk,bkd->bqhd', p_w.astype(vwb.dtype), vwb)
        g = gb.astype(q.dtype)
        return g[..., 0:1] * o_c + g[..., 1:2] * o_s + g[..., 2:3] * o_w

    return blocks_to_seq(lax.map(block, jnp.arange(T // Q_BLOCK)))


def chunked_spatial_gating(uv, v_gain, w_s, b_s):
    B, T, _ = uv.shape
    u, v = jnp.split(jax.nn.gelu(uv), 2, axis=-1)
    v = rms_norm(v.reshape(B, T, N_HEADS, HEAD_DIM), v_gain.reshape(N_HEADS, HEAD_DIM))
    v = v.reshape(B, T // GMLP_CHUNK, GMLP_CHUNK, N_HEADS, HEAD_DIM)
    w = w_s * jnp.tril(jnp.ones((GMLP_CHUNK, GMLP_CHUNK), w_s.dtype))
    sv = jnp.einsum('gts,bcsgd->bctgd', w, v) + b_s.T[:, :, None]
    return u * sv.reshape(B, T, W_MIX)


def setup_inputs(seed: int = 0) -> dict:
    key = jax.random.key(seed)
    keys = list(jax.random.split(key, 32))

    def nrm(shape, scale):
        return jax.random.normal(keys.pop(), shape, jnp.float32) * scale

    L, D = DEPTH, D_MODEL
    return {
        "x": nrm((BATCH, SEQ, D), 1.0),
        "ffn1_norm": 1.0 + nrm((L, D), 0.02),
        "ffn1_w_in": nrm((L, D, 2 * D_FF), D ** -0.5),
        "ffn1_w_out": nrm((L, D_FF, D), D_FF ** -0.5),
        "mix_norm": 1.0 + nrm((L, D), 0.02),
        "w_in": nrm((L, D, D_IN), D ** -0.5),
        "diff_q_gain": 1.0 + nrm((L, DIFF_QK_DIM), 0.02),
        "diff_k_gain": 1.0 + nrm((L, DIFF_QK_DIM), 0.02),
        "diff_lambda": nrm((L, 4, DIFF_QK_DIM), 0.1),
        "fox_q_gain": 1.0 + nrm((L, HEAD_DIM), 0.02),
        "fox_k_gain": 1.0 + nrm((L, HEAD_DIM), 0.02),
        "fox_f_bias": FOX_FORGET_BIAS + nrm((L, N_HEADS), 0.1),
        "nsa_q_gain": 1.0 + nrm((L, HEAD_DIM), 0.02),
        "nsa_k_gain": 1.0 + nrm((L, HEAD_DIM), 0.02),
        "nsa_cmp_pe": nrm((L, 2, NSA_CMP_BLOCK, HEAD_DIM), 0.1),
        "nsa_phi_w1": nrm((L, 2, NSA_CMP_BLOCK * HEAD_DIM, NSA_PHI_HIDDEN), (NSA_CMP_BLOCK * HEAD_DIM) ** -0.5),
        "nsa_phi_b1": nrm((L, 2, NSA_PHI_HIDDEN), 0.02),
        "nsa_phi_w2": nrm((L, 2, NSA_PHI_HIDDEN, HEAD_DIM), NSA_PHI_HIDDEN ** -0.5),
        "nsa_phi_b2": nrm((L, 2, HEAD_DIM), 0.02),
        "gmlp_v_gain": 1.0 + nrm((L, W_MIX), 0.02),
        "gmlp_w_s": nrm((L, N_HEADS, GMLP_CHUNK, GMLP_CHUNK), GMLP_CHUNK ** -0.5),
        "gmlp_b_s": 1.0 + nrm((L, N_HEADS, GMLP_CHUNK), 0.02),
        "w_branch": nrm((L, N_BRANCHES, W_MIX, D), W_MIX ** -0.5),
        "w_out": nrm((L, D, D), D ** -0.5),
        "ffn2_norm": 1.0 + nrm((L, D), 0.02),
        "ffn2_w_in": nrm((L, D, 2 * D_FF), D ** -0.5),
        "ffn2_w_out": nrm((L, D_FF, D), D_FF ** -0.5),
    }


def reference(x, ffn1_norm, ffn1_w_in, ffn1_w_out, mix_norm, w_in, diff_q_gain, diff_k_gain,
              diff_lambda, fox_q_gain, fox_k_gain, fox_f_bias, nsa_q_gain, nsa_k_gain, nsa_cmp_pe,
              nsa_phi_w1, nsa_phi_b1, nsa_phi_w2, nsa_phi_b2, gmlp_v_gain, gmlp_w_s, gmlp_b_s,
              w_branch, w_out, ffn2_norm, ffn2_w_in, ffn2_w_out):
    B, T, _ = x.shape
    for l in range(DEPTH):
        x = x + 0.5 * swiglu(rms_norm(x, ffn1_norm[l]), ffn1_w_in[l], ffn1_w_out[l])

        h = rms_norm(x, mix_norm[l])
        z = h @ w_in[l]
        (a_q, a_k, a_v, f_q, f_k, f_v, f_logit, n_q, n_kv, n_g, g_uv, m_g) = split_cols(z, IN_SPLIT_SIZES)

        a_q = rms_norm(a_q.reshape(B, T, N_HEADS, 2, DIFF_QK_DIM), diff_q_gain[l])
        a_k = rms_norm(a_k.reshape(B, T, N_HEADS, 2, DIFF_QK_DIM), diff_k_gain[l])
        a_v = a_v.reshape(B, T, N_HEADS, HEAD_DIM)
        lam_init = 0.8 - 0.6 * math.exp(-0.3 * l)
        lam_p = diff_lambda[l].astype(jnp.float32)
        lam = (jnp.exp(jnp.sum(lam_p[0] * lam_p[1])) - jnp.exp(jnp.sum(lam_p[2] * lam_p[3])) + lam_init)
        o_a = rms_norm(diff_attention(a_q, a_k, a_v, lam)) * (1.0 - lam_init)

        f_q = rms_norm(f_q.reshape(B, T, N_HEADS, HEAD_DIM), fox_q_gain[l])
        f_k = rms_norm(f_k.reshape(B, T, N_HEADS, HEAD_DIM), fox_k_gain[l])
        f_v = f_v.reshape(B, T, N_HEADS, HEAD_DIM)
        log_f = jax.nn.log_sigmoid(f_logit.astype(jnp.float32) + fox_f_bias[l].astype(jnp.float32))
        o_b = forgetting_attention(f_q, f_k, f_v, log_f)

        n_q = rms_norm(n_q.reshape(B, T, N_HEADS, HEAD_DIM), nsa_q_gain[l])
        kc_in, vc_in, k_s, v_s, k_w, v_w = jnp.split(n_kv, 6, axis=-1)
        k_c = rms_norm(nsa_compress(kc_in, nsa_cmp_pe[l, 0], nsa_phi_w1[l, 0], nsa_phi_b1[l, 0],
                                    nsa_phi_w2[l, 0], nsa_phi_b2[l, 0]), nsa_k_gain[l])
        v_c = nsa_compress(vc_in, nsa_cmp_pe[l, 1], nsa_phi_w1[l, 1], nsa_phi_b1[l, 1],
                           nsa_phi_w2[l, 1], nsa_phi_b2[l, 1])
        o_c = nsa_attention(n_q, k_c, v_c, rms_norm(k_s, nsa_k_gain[l]), v_s,
                            rms_norm(k_w, nsa_k_gain[l]), v_w, n_g.reshape(B, T, N_HEADS, 3))

        o_d = chunked_spatial_gating(g_uv, gmlp_v_gain[l], gmlp_w_s[l], gmlp_b_s[l])

        branches = jnp.stack([o_a.reshape(B, T, W_MIX), o_b.reshape(B, T, W_MIX),
                              o_c.reshape(B, T, W_MIX), o_d], axis=2)
        proj = jnp.einsum('btni,nid->btnd', branches, w_branch[l])
        gates = jax.nn.sigmoid(m_g.reshape(B, T, N_BRANCHES, D_MODEL))
        x = x + jnp.sum(gates * proj, axis=2) @ w_out[l]

        x = x + 0.5 * swiglu(rms_norm(x, ffn2_norm[l]), ffn2_w_in[l], ffn2_w_out[l])
    return x
```

```python
import numpy as np
import concourse.bass as bass
import concourse.mybir as mybir
from concourse.bass_utils import run_bass_kernel_spmd

F32 = mybir.dt.float32
BF16 = mybir.dt.bfloat16
AF = mybir.ActivationFunctionType
ALU = mybir.AluOpType
AX = mybir.AxisListType

ENGS = ("pe", "act", "dve", "pool", "sp")


class Buf:
    __slots__ = ("name", "t", "w", "r", "dsem", "dcnt")

    def __init__(self, name, t):
        self.name = name
        self.t = t
        self.w = None
        self.r = []
        self.dsem = None
        self.dcnt = 0

    def __getitem__(self, idx):
        return self.t[idx]


class S:
    def __init__(self, nc, same_engine_sync=True):
        self.nc = nc
        self.e = {"pe": nc.tensor, "act": nc.scalar, "dve": nc.vector, "pool": nc.gpsimd, "sp": nc.sync}
        self.sem = {k: nc.alloc_semaphore("c_" + k) for k in ENGS}
        self.cnt = {k: 0 for k in ENGS}
        self.seen = {k: {} for k in ENGS}
        self.same = same_engine_sync
        self.nbuf = 0
        self.dma_sems = []
        self.ctx = []

    def sb(self, name, shape, dt):
        g = self.nc.sbuf_tensor(name, list(shape), dt)
        t = g.__enter__()
        self.ctx.append(g)
        return Buf(name, t)

    def ps(self, name, shape, dt):
        g = self.nc.psum_tensor(name, list(shape), dt)
        t = g.__enter__()
        self.ctx.append(g)
        return Buf(name, t)

    def sub(self, name, ap):
        return Buf(name, ap)

    def mark(self):
        return len(self.ctx)

    def release(self, m):
        self.barrier()
        while len(self.ctx) > m:
            self.ctx.pop().__exit__(None, None, None)

    def close(self):
        for g in reversed(self.ctx):
            g.__exit__(None, None, None)
        self.ctx = []

    def _need(self, E, deps):
        need = {}
        for d in deps:
            if d is None:
                continue
            if d[0] == "dma":
                b = d[1]
                key = ("dma", id(b))
                need[key] = (b, b.dcnt)
            else:
                F, c = d
                if F == E and (not self.same or E == "pe" or c > self.cnt[E]):
                    continue
                if c > need.get(F, (None, 0))[1]:
                    need[F] = (None, c)
        for key, (b, c) in need.items():
            if self.seen[E].get(key, 0) >= c:
                continue
            self.seen[E][key] = c
            if b is not None:
                self.e[E].wait_ge(b.dsem, c)
            else:
                self.e[E].wait_ge(self.sem[key], c)

    def op(self, E, fn, reads=(), writes=(), inc=True):
        deps = []
        for b in reads:
            deps.append(b.w)
        for b in writes:
            deps.append(b.w)
            deps.extend(b.r)
        self._need(E, deps)
        ins = fn()
        c = self.cnt[E] + 1
        if inc:
            ins.then_inc(self.sem[E], 1)
            self.cnt[E] = c
        for b in writes:
            b.w = (E, c)
            b.r = []
        for b in reads:
            if b not in writes:
                b.r = [x for x in b.r if x[0] != E] + [(E, c)]
        return ins

    def dma(self, Q, out, in_, reads=(), writes=(), **kw):
        deps = []
        for b in reads:
            deps.append(b.w)
        for b in writes:
            deps.append(b.w)
            deps.extend(b.r)
        self._need(Q, deps)
        owner = (list(writes) + list(reads))[0]
        if owner.dsem is None:
            owner.dsem = self.nc.alloc_semaphore("d_%d" % len(self.dma_sems))
            self.dma_sems.append(owner)
        ins = self.e[Q].dma_start(out=out, in_=in_, **kw)
        ins.then_inc(owner.dsem, 16)
        owner.dcnt += 16
        rec = ("dma", owner, owner.dcnt)
        for b in writes:
            b.w = rec
            b.r = []
        for b in reads:
            if b not in writes:
                b.r = b.r + [rec]
        return ins

    def barrier(self):
        for E in ENGS:
            for Fk in ENGS:
                if Fk == E:
                    continue
                c = self.cnt[Fk]
                if c and self.seen[E].get(Fk, 0) < c:
                    self.seen[E][Fk] = c
                    self.e[E].wait_ge(self.sem[Fk], c)
            for b in self.dma_sems:
                key = ("dma", id(b))
                if b.dcnt and self.seen[E].get(key, 0) < b.dcnt:
                    self.seen[E][key] = b.dcnt
                    self.e[E].wait_ge(b.dsem, b.dcnt)

    def finish(self):
        self.barrier()


D = 1024
DFF = 2816
NFC = DFF // 128
NKC = D // 128
EPS = 1e-6


def dram_in(nc, name, shape, dt=F32):
    return nc.dram_tensor(name, list(shape), dt, kind="ExternalInput").ap()


def dram_out(nc, name, shape, dt=F32):
    return nc.dram_tensor(name, list(shape), dt, kind="ExternalOutput").ap()


class RR:
    def __init__(self, items):
        self.items = items
        self.i = 0

    def next(self):
        b = self.items[self.i % len(self.items)]
        self.i += 1
        return b


def emit_rmsnorm_T(s, epsc, xt, gcol, hT, ident_b, pT_rr, hb_rr, scr, stat_rr, ntile, evac_engs=("dve", "act")):
    nc = s.nc
    hbs = []
    for j in range(ntile):
        st = stat_rr.next()
        hb = hb_rr.next()
        s.op("act", lambda: nc.scalar.activation(out=scr[:, :], in_=xt[j][:, :], func=AF.Square, scale=1.0 / 32.0,
                                                 accum_out=st[:, 0:1]),
             reads=[xt[j]], writes=[scr, st])
        s.op("act", lambda: nc.scalar.activation(out=st[:, 1:2], in_=st[:, 0:1], func=AF.Sqrt, bias=epsc[:, 0:1], scale=1.0),
             reads=[st, epsc], writes=[st])
        s.op("dve", lambda: nc.vector.reciprocal(out=st[:, 2:3], in_=st[:, 1:2]), reads=[st], writes=[st])
        s.op("act", lambda: nc.scalar.activation(out=hb[:, :], in_=xt[j][:, :], func=AF.Copy, scale=st[:, 2:3]),
             reads=[xt[j], st], writes=[hb])
        hbs.append(hb)
    k = 0
    for c in range(NKC):
        pT = pT_rr.next()
        for j in range(ntile):
            s.op("pe", lambda: nc.tensor.transpose(out=pT[:, j * 128:(j + 1) * 128], in_=hbs[j][:, c * 128:(c + 1) * 128],
                                                   identity=ident_b[:, :]),
                 reads=[hbs[j], ident_b], writes=[pT], inc=(j == ntile - 1))
        eng = evac_engs[k % len(evac_engs)]
        k += 1
        if eng == "dve":
            s.op("dve", lambda: nc.vector.tensor_scalar(out=hT[:, c, 0:ntile * 128], in0=pT[:, 0:ntile * 128],
                                                        scalar1=gcol[:, c:c + 1], scalar2=None, op0=ALU.mult),
                 reads=[pT, gcol], writes=[hT])
        else:
            s.op("act", lambda: nc.scalar.activation(out=hT[:, c, 0:ntile * 128], in_=pT[:, 0:ntile * 128],
                                                     func=AF.Copy, scale=gcol[:, c:c + 1]),
                 reads=[pT, gcol], writes=[hT])


def build_ffn(NT, TB=512):
    nc = bass.Bass("TRN2", target_bir_lowering=False)
    x = dram_in(nc, "x", [NT, D])
    g = dram_in(nc, "g", [D])
    w_in = dram_in(nc, "w_in", [D, 2 * DFF])
    w_out = dram_in(nc, "w_out", [DFF, D])
    ident = dram_in(nc, "ident", [128, 128], BF16)
    y = dram_out(nc, "y", [NT, D])
    s = S(nc)
    emit_ffn(s, x, g, w_in, w_out, ident, y, NT, TB)
    s.finish()
    s.close()
    return nc


def emit_ffn(s, x, g, w_in, w_out, ident, y, NT, TB=512):
    nc = s.nc
    ntile = TB // 128
    ident_b = s.sb("ident_b", [128, 128], BF16)
    s.dma("sp", ident_b[:, :], ident[:, :], writes=[ident_b])
    gcol = s.sb("gcol", [128, NKC], F32)
    epsc = s.sb("epsc", [128, 1], F32)
    s.op("dve", lambda: nc.vector.memset(epsc[:, :], EPS), writes=[epsc])
    s.dma("sp", gcol[:, :], g.rearrange("(c p) -> p c", p=128), writes=[gcol], allow_slow_non_contiguous=True)
    win_b = [s.sb("win_b%d" % c, [128, 2 * DFF], BF16) for c in range(NKC)]
    wout_b = [s.sb("wout_b%d" % f, [128, D], BF16) for f in range(NFC)]
    for c in range(NKC):
        for hf in range(2):
            s.dma("pool", win_b[c][:, hf * DFF:(hf + 1) * DFF], w_in[c * 128:(c + 1) * 128, hf * DFF:(hf + 1) * DFF],
                  writes=[win_b[c]])
    for f in range(NFC):
        s.dma("pool", wout_b[f][:, :], w_out[f * 128:(f + 1) * 128, :], writes=[wout_b[f]])
    xt = [s.sb("xt%d" % j, [128, D], F32) for j in range(ntile)]
    hb_rr = RR([s.sb("hb%d" % j, [128, D], BF16) for j in range(ntile)])
    scr = s.sb("scr", [128, D], BF16)
    stat_rr = RR([s.sb("st%d" % j, [128, 4], F32) for j in range(4)])
    hT = s.sb("hT", [128, NKC, TB], BF16)
    pT_rr = RR([s.ps("pT%d" % j, [128, TB], BF16) for j in range(2)])
    pa_rr = RR([s.ps("pa%d" % j, [128, TB], F32) for j in range(2)])
    pb_rr = RR([s.ps("pb%d" % j, [128, TB], F32) for j in range(2)])
    po_rr = RR([s.ps("po%d" % j, [128, 512], F32) for j in range(2)])
    sa_rr = RR([s.sb("sa%d" % j, [128, TB], F32) for j in range(2)])
    act = [s.sb("actT%d" % f, [128, TB], BF16) for f in range(NFC)]
    for tb in range(NT // TB):
        for j in range(ntile):
            r0 = tb * TB + j * 128
            s.dma("sp", xt[j][:, :], x[r0:r0 + 128, :], writes=[xt[j]])
        emit_rmsnorm_T(s, epsc, xt, gcol, hT, ident_b, pT_rr, hb_rr, scr, stat_rr, ntile)
        for f in range(NFC):
            pa = pa_rr.next()
            pb = pb_rr.next()
            for c in range(NKC):
                s.op("pe", lambda: nc.tensor.matmul(pa[:, :], lhsT=win_b[c][:, f * 128:(f + 1) * 128], rhs=hT[:, c, :],
                                                    start=(c == 0), stop=(c == NKC - 1)),
                     reads=[win_b[c], hT], writes=[pa], inc=(c == NKC - 1))
            for c in range(NKC):
                s.op("pe", lambda: nc.tensor.matmul(pb[:, :], lhsT=win_b[c][:, DFF + f * 128:DFF + (f + 1) * 128],
                                                    rhs=hT[:, c, :], start=(c == 0), stop=(c == NKC - 1)),
                     reads=[win_b[c], hT], writes=[pb], inc=(c == NKC - 1))
            sa = sa_rr.next()
            s.op("act", lambda: nc.scalar.activation(out=sa[:, :], in_=pa[:, :], func=AF.Silu), reads=[pa], writes=[sa])
            s.op("dve", lambda: nc.vector.tensor_tensor(out=act[f][:, :], in0=sa[:, :], in1=pb[:, :], op=ALU.mult),
                 reads=[sa, pb], writes=[act[f]])
        for j in range(ntile):
            for hf in range(2):
                po = po_rr.next()
                for f in range(NFC):
                    s.op("pe", lambda: nc.tensor.matmul(po[:, :], lhsT=act[f][:, j * 128:(j + 1) * 128],
                                                        rhs=wout_b[f][:, hf * 512:(hf + 1) * 512],
                                                        start=(f == 0), stop=(f == NFC - 1)),
                         reads=[act[f], wout_b[f]], writes=[po], inc=(f == NFC - 1))
                s.op("dve", lambda: nc.vector.scalar_tensor_tensor(out=xt[j][:, hf * 512:(hf + 1) * 512], in0=po[:, :],
                                                                   scalar=0.5, in1=xt[j][:, hf * 512:(hf + 1) * 512],
                                                                   op0=ALU.mult, op1=ALU.add),
                     reads=[po, xt[j]], writes=[xt[j]])
            r0 = tb * TB + j * 128
            s.dma("sp", y[r0:r0 + 128, :], xt[j][:, :], reads=[xt[j]])


FM_SRC = [[(0, 128)], [(128, 128)], [(256, 128)], [(384, 128)],
          [(768, 128)], [(896, 128)], [(1024, 128)], [(1152, 128)],
          [(1540, 128)], [(1668, 128)], [(1924, 64), (2052, 64)], [(1796, 128)]]
FM_GCOL = [0, 0, 1, 1, 2, 2, 3, 3, 4, 4, 5, None]
FM_BLK = [0, 0, 0, 0, 1, 1, 1, 1, 1, 1, 1, None]
TM_SRC = [[(512, 256), (1280, 256)],
          [(1536, 4), (2180, 12), (1988, 64), (2116, 64)],
          [(2192, 512)]]
NFM = 12
GELU_C = 1.5957691216057308


def build_proj(NT, TB=512):
    nc = bass.Bass("TRN2", target_bir_lowering=False)
    x = dram_in(nc, "x", [NT, D])
    g = dram_in(nc, "g", [D])
    w_in = dram_in(nc, "w_in", [D, 6800])
    ident = dram_in(nc, "ident", [128, 128], BF16)
    gains = dram_in(nc, "gains", [128, 6])
    blk = dram_in(nc, "blk", [2, 128, 128], BF16)
    vgain = dram_in(nc, "vgain", [128, 256])
    wsT = dram_in(nc, "wsT", [4, 128, 128])
    triu = dram_in(nc, "triu", [128, 128])
    bsT = dram_in(nc, "bsT", [128, 4])
    zfm = dram_out(nc, "zfm", [NFM, 128, NT], BF16)
    vab = dram_out(nc, "vab", [NT, 512], BF16)
    vsw = dram_out(nc, "vsw", [NT, 128], BF16)
    misc = dram_out(nc, "misc", [NT, 16])
    od = dram_out(nc, "od", [NT, 256], BF16)
    s = S(nc)
    ntile = TB // 128
    ident_b = s.sb("ident_b", [128, 128], BF16)
    s.dma("sp", ident_b[:, :], ident[:, :], writes=[ident_b])
    gcol = s.sb("gcol", [128, NKC], F32)
    s.dma("sp", gcol[:, :], g.rearrange("(c p) -> p c", p=128), writes=[gcol], allow_slow_non_contiguous=True)
    epsc = s.sb("epsc", [128, 1], F32)
    s.op("dve", lambda: nc.vector.memset(epsc[:, :], EPS), writes=[epsc])
    gn = s.sb("gn", [128, 6], F32)
    s.dma("sp", gn[:, :], gains[:, :], writes=[gn])
    for col, sc in ((0, 32.0 ** -0.5), (2, 0.125), (4, 0.125)):
        s.op("dve", lambda: nc.vector.tensor_scalar(out=gn[:, col:col + 1], in0=gn[:, col:col + 1], scalar1=sc,
                                                    scalar2=None, op0=ALU.mult), reads=[gn], writes=[gn])
    blk_b = [s.sb("blk%d" % i, [128, 128], BF16) for i in range(2)]
    for i in range(2):
        s.dma("sp", blk_b[i][:, :], blk[i], writes=[blk_b[i]])
    vg = s.sb("vg", [128, 256], F32)
    s.dma("sp", vg[:, :], vgain[:, :], writes=[vg])
    bcol = s.sb("bcol", [128, 4], F32)
    s.dma("sp", bcol[:, :], bsT[:, :], writes=[bcol])
    tri = s.sb("tri", [128, 128], F32)
    s.dma("sp", tri[:, :], triu[:, :], writes=[tri])
    wm = []
    wtmp = s.sb("wtmp", [128, 128], F32)
    for gi in range(4):
        w = s.sb("wm%d" % gi, [128, 128], BF16)
        s.dma("sp", wtmp[:, :], wsT[gi], writes=[wtmp])
        s.op("dve", lambda: nc.vector.tensor_tensor(out=w[:, :], in0=wtmp[:, :], in1=tri[:, :], op=ALU.mult),
             reads=[wtmp, tri], writes=[w])
        wm.append(w)
    wfm = [s.sb("wfm%d" % c, [128, NFM * 128], BF16) for c in range(NKC)]
    wtm = [s.sb("wtm%d" % c, [128, 1168], BF16) for c in range(NKC)]
    for c in range(NKC):
        for i, srcs in enumerate(FM_SRC):
            o = i * 128
            for (c0, n) in srcs:
                s.dma("pool", wfm[c][:, o:o + n], w_in[c * 128:(c + 1) * 128, c0:c0 + n], writes=[wfm[c]])
                o += n
        o = 0
        for srcs in TM_SRC:
            for (c0, n) in srcs:
                s.dma("pool", wtm[c][:, o:o + n], w_in[c * 128:(c + 1) * 128, c0:c0 + n], writes=[wtm[c]])
                o += n
    xt = [s.sb("xt%d" % j, [128, D], F32) for j in range(ntile)]
    hb_rr = RR([s.sb("hb%d" % j, [128, D], BF16) for j in range(ntile)])
    scr = s.sb("scr", [128, D], BF16)
    stat_rr = RR([s.sb("st%d" % j, [128, 4], F32) for j in range(4)])
    hT = s.sb("hT", [128, NKC, TB], BF16)
    pT_rr = RR([s.ps("pT%d" % j, [128, TB], BF16) for j in range(2)])
    pz_rr = RR([s.ps("pz%d" % j, [128, 512], F32) for j in range(3)])
    pq_rr = RR([s.ps("pq%d" % j, [128, 512], F32) for j in range(2)])
    sq_rr = RR([s.sb("sq%d" % j, [128, TB], BF16) for j in range(2)])
    rs_rr = RR([s.sb("rs%d" % j, [128, TB], F32) for j in range(2)])
    zo_rr = RR([s.sb("zo%d" % j, [128, TB], BF16) for j in range(3)])
    vab_rr = RR([s.sb("vabt%d" % j, [128, 512], BF16) for j in range(2)])
    vsw_rr = RR([s.sb("vswt%d" % j, [128, 128], BF16) for j in range(2)])
    msc_rr = RR([s.sb("msct%d" % j, [128, 16], F32) for j in range(2)])
    f_rr = RR([s.sb("gf%d" % j, [128, 512], F32) for j in range(4)])
    ge_rr = RR([s.sb("ge%d" % j, [128, 512], F32) for j in range(2)])
    vn_rr = RR([s.sb("vn%d" % j, [128, 256], BF16) for j in range(2)])
    od_rr = RR([s.sb("odt%d" % j, [128, 256], BF16) for j in range(2)])
    for tb in range(NT // TB):
        t0 = tb * TB
        for j in range(ntile):
            s.dma("sp", xt[j][:, :], x[t0 + j * 128:t0 + (j + 1) * 128, :], writes=[xt[j]])
        emit_rmsnorm_T(s, epsc, xt, gcol, hT, ident_b, pT_rr, hb_rr, scr, stat_rr, ntile)
        for i in range(NFM):
            pz = pz_rr.next()
            for c in range(NKC):
                s.op("pe", lambda: nc.tensor.matmul(pz[:, 0:TB], lhsT=wfm[c][:, i * 128:(i + 1) * 128], rhs=hT[:, c, :],
                                                    start=(c == 0), stop=(c == NKC - 1)),
                     reads=[wfm[c], hT], writes=[pz], inc=(c == NKC - 1))
            zo = zo_rr.next()
            if FM_GCOL[i] is None:
                s.op("act", lambda: nc.scalar.copy(out=zo[:, :], in_=pz[:, 0:TB]), reads=[pz], writes=[zo])
            else:
                gs = 32.0 if FM_BLK[i] == 0 else 64.0
                sq = sq_rr.next()
                s.op("act", lambda: nc.scalar.activation(out=sq[:, :], in_=pz[:, 0:TB], func=AF.Square),
                     reads=[pz], writes=[sq])
                pq = pq_rr.next()
                s.op("pe", lambda: nc.tensor.matmul(pq[:, 0:TB], lhsT=blk_b[FM_BLK[i]][:, :], rhs=sq[:, :],
                                                    start=True, stop=True), reads=[blk_b[FM_BLK[i]], sq], writes=[pq])
                rs = rs_rr.next()
                s.op("act", lambda: nc.scalar.activation(out=rs[:, :], in_=pq[:, 0:TB], func=AF.Sqrt, bias=epsc[:, 0:1],
                                                         scale=1.0 / gs), reads=[pq, epsc], writes=[rs])
                s.op("dve", lambda: nc.vector.reciprocal(out=rs[:, :], in_=rs[:, :]), reads=[rs], writes=[rs])
                gc = FM_GCOL[i]
                s.op("dve", lambda: nc.vector.scalar_tensor_tensor(out=zo[:, :], in0=pz[:, 0:TB], scalar=gn[:, gc:gc + 1],
                                                                   in1=rs[:, :], op0=ALU.mult, op1=ALU.mult),
                     reads=[pz, gn, rs], writes=[zo])
            s.dma("sp", zfm[i, :, t0:t0 + TB], zo[:, :], reads=[zo])
        for j in range(ntile):
            r0 = t0 + j * 128
            pz = pz_rr.next()
            for c in range(NKC):
                s.op("pe", lambda: nc.tensor.matmul(pz[:, :], lhsT=hT[:, c, j * 128:(j + 1) * 128], rhs=wtm[c][:, 0:512],
                                                    start=(c == 0), stop=(c == NKC - 1)),
                     reads=[wtm[c], hT], writes=[pz], inc=(c == NKC - 1))
            vt = vab_rr.next()
            s.op("act", lambda: nc.scalar.copy(out=vt[:, :], in_=pz[:, :]), reads=[pz], writes=[vt])
            s.dma("sp", vab[r0:r0 + 128, :], vt[:, :], reads=[vt])
            pz = pz_rr.next()
            for c in range(NKC):
                s.op("pe", lambda: nc.tensor.matmul(pz[:, 0:144], lhsT=hT[:, c, j * 128:(j + 1) * 128], rhs=wtm[c][:, 512:656],
                                                    start=(c == 0), stop=(c == NKC - 1)),
                     reads=[wtm[c], hT], writes=[pz], inc=(c == NKC - 1))
            mt = msc_rr.next()
            vs_ = vsw_rr.next()
            s.op("dve", lambda: nc.vector.tensor_copy(out=mt[:, :], in_=pz[:, 0:16]), reads=[pz], writes=[mt])
            s.op("dve", lambda: nc.vector.tensor_copy(out=vs_[:, :], in_=pz[:, 16:144]), reads=[pz], writes=[vs_])
            s.dma("sp", misc[r0:r0 + 128, :], mt[:, :], reads=[mt])
            s.dma("sp", vsw[r0:r0 + 128, :], vs_[:, :], reads=[vs_])
            pz = pz_rr.next()
            for c in range(NKC):
                s.op("pe", lambda: nc.tensor.matmul(pz[:, :], lhsT=hT[:, c, j * 128:(j + 1) * 128], rhs=wtm[c][:, 656:1168],
                                                    start=(c == 0), stop=(c == NKC - 1)),
                     reads=[wtm[c], hT], writes=[pz], inc=(c == NKC - 1))
            z2 = f_rr.next()
            s.op("act", lambda: nc.scalar.activation(out=z2[:, :], in_=pz[:, :], func=AF.Square), reads=[pz], writes=[z2])
            s.op("dve", lambda: nc.vector.tensor_scalar(out=z2[:, :], in0=z2[:, :], scalar1=0.044715, scalar2=1.0,
                                                        op0=ALU.mult, op1=ALU.add), reads=[z2], writes=[z2])
            s.op("dve", lambda: nc.vector.tensor_tensor(out=z2[:, :], in0=z2[:, :], in1=pz[:, :], op=ALU.mult),
                 reads=[z2, pz], writes=[z2])
            s.op("act", lambda: nc.scalar.activation(out=z2[:, :], in_=z2[:, :], func=AF.Sigmoid, scale=GELU_C),
                 reads=[z2], writes=[z2])
            ge = ge_rr.next()
            s.op("dve", lambda: nc.vector.tensor_tensor(out=ge[:, :], in0=z2[:, :], in1=pz[:, :], op=ALU.mult),
                 reads=[z2, pz], writes=[ge])
            sqv = f_rr.next()
            st = stat_rr.next()
            s.op("act", lambda: nc.scalar.activation(out=sqv[:, 0:256], in_=ge[:, 256:512], func=AF.Square),
                 reads=[ge], writes=[sqv])
            s.op("dve", lambda: nc.vector.tensor_reduce(out=st[:, 0:4], in_=sqv[:, 0:256].rearrange("p (g d) -> p g d", g=4),
                                                        axis=AX.X, op=ALU.add), reads=[sqv], writes=[st])
            s.op("act", lambda: nc.scalar.activation(out=st[:, 0:4], in_=st[:, 0:4], func=AF.Sqrt, bias=epsc[:, 0:1],
                                                     scale=1.0 / 64.0), reads=[st, epsc], writes=[st])
            s.op("dve", lambda: nc.vector.reciprocal(out=st[:, 0:4], in_=st[:, 0:4]), reads=[st], writes=[st])
            vn = vn_rr.next()
            for gi in range(4):
                s.op("dve", lambda: nc.vector.scalar_tensor_tensor(
                    out=vn[:, gi * 64:(gi + 1) * 64], in0=ge[:, 256 + gi * 64:256 + (gi + 1) * 64], scalar=st[:, gi:gi + 1],
                    in1=vg[:, gi * 64:(gi + 1) * 64], op0=ALU.mult, op1=ALU.mult), reads=[ge, st, vg], writes=[vn])
            pq = pq_rr.next()
            for gi in range(4):
                s.op("pe", lambda: nc.tensor.matmul(pq[:, gi * 64:(gi + 1) * 64], lhsT=wm[gi][:, :],
                                                    rhs=vn[:, gi * 64:(gi + 1) * 64], start=True, stop=True),
                     reads=[wm[gi], vn], writes=[pq], inc=(gi == 3))
            ot = od_rr.next()
            for gi in range(4):
                s.op("dve", lambda: nc.vector.scalar_tensor_tensor(
                    out=ot[:, gi * 64:(gi + 1) * 64], in0=pq[:, gi * 64:(gi + 1) * 64], scalar=bcol[:, gi:gi + 1],
                    in1=ge[:, gi * 64:(gi + 1) * 64], op0=ALU.add, op1=ALU.mult), reads=[pq, bcol, ge], writes=[ot])
            s.dma("sp", od[r0:r0 + 128, :], ot[:, :], reads=[ot])
    s.finish()
    s.close()
    return nc


def _bf(a):
    import ml_dtypes
    return np.ascontiguousarray(a).astype(ml_dtypes.bfloat16)


def proj_consts():
    blk = np.zeros((2, 128, 128), np.float32)
    for i in range(128):
        for j in range(128):
            if i // 32 == j // 32:
                blk[0, i, j] = 1
            if i // 64 == j // 64:
                blk[1, i, j] = 1
    triu = np.triu(np.ones((128, 128), np.float32))
    return dict(ident=_bf(np.eye(128, dtype=np.float32)), blk=_bf(blk), triu=triu)


def proj_params(g, w_in, dq, dk, fq, fk, nq, nk, vgain, w_s, b_s):
    gains = np.stack([np.tile(dq, 4), np.tile(dk, 4), np.tile(fq, 2), np.tile(fk, 2), np.tile(nq, 2), np.tile(nk, 2)], 1)
    return dict(g=np.ascontiguousarray(g), w_in=np.ascontiguousarray(w_in), gains=np.ascontiguousarray(gains, dtype=np.float32),
                vgain=np.ascontiguousarray(np.broadcast_to(vgain[None, :], (128, 256))),
                wsT=np.ascontiguousarray(w_s.transpose(0, 2, 1)), bsT=np.ascontiguousarray(b_s.T))


NEG = -30000.0


def emit_attn_phase(s, cm, T, nsub, qT, kT, vt, kparts, out_dram, finalize, bias_fn=None, name="a"):
    nc = s.nc
    NQB = T // 512
    for qb in range(NQB):
        q0 = qb * 512
        pos = [cm["po_rr"].next() for _ in range(nsub)]
        nt = 4 * qb + 4
        for t in range(nt):
            di = t - 4 * qb
            c0 = 128 * di if di > 0 else 0
            for i in range(nsub):
                p0, p1 = kparts[i]
                ps = cm["ps_rr"].next()
                s.op("pe", lambda: nc.tensor.matmul(ps[:, c0:512], lhsT=kT[p0:p1, t * 128:(t + 1) * 128],
                                                    rhs=qT[p0:p1, q0 + c0:q0 + 512], start=True, stop=(di < 0)),
                     reads=[kT, qT], writes=[ps], inc=(di < 0))
                if di >= 0:
                    s.op("pe", lambda: nc.tensor.matmul(ps[:, c0:c0 + 128], lhsT=cm["ident_b"][:, :], rhs=cm["tri_b"][:, :],
                                                        start=False, stop=True),
                         reads=[cm["ident_b"], cm["tri_b"]], writes=[ps])
                pt = cm["pt_rr"].next()
                if bias_fn is None:
                    s.op("act", lambda: nc.scalar.activation(out=pt[:, c0:512], in_=ps[:, c0:512], func=AF.Exp),
                         reads=[ps], writes=[pt])
                else:
                    bb, bap = bias_fn(qb, t)
                    s.op("act", lambda: nc.scalar.activation(out=pt[:, c0:512], in_=ps[:, c0:512], func=AF.Exp, bias=bap),
                         reads=[ps, bb], writes=[pt])
                s.op("pe", lambda: nc.tensor.matmul(pos[i][0:65, c0:512], lhsT=vt[:, t, :], rhs=pt[:, c0:512],
                                                    start=(t == 0), stop=(t == nt - 1)),
                     reads=[vt, pt], writes=[pos[i]])
        finalize(qb, pos)


def emit_o_to_tokmajor(s, cm, po, pf, col0):
    nc = s.nc
    oc = cm["oc_rr"].next()
    s.op("act", lambda: nc.scalar.copy(out=oc[0:65, :], in_=po[0:65, :]), reads=[po], writes=[oc])
    for j in range(4):
        s.op("pe", lambda: nc.tensor.transpose(out=pf[:, j, col0:col0 + 65], in_=oc[0:65, j * 128:(j + 1) * 128],
                                               identity=cm["ident_f"][0:65, 0:65]),
             reads=[oc, cm["ident_f"]], writes=[pf], inc=(j == 3))


def build_mix_ab(T):
    nc = bass.Bass("TRN2", target_bir_lowering=False)
    io = mix_decl(nc, T, with_c=False)
    s = S(nc)
    cm = mix_common(s, io)
    emit_mix_a(s, cm, io, T)
    emit_mix_b(s, cm, io, T)
    s.finish()
    s.close()
    return nc


def mix_decl(nc, T, with_c=True):
    NT = T // 128
    io = dict(
        identb=dram_in(nc, "identb", [128, 128], BF16), identf=dram_in(nc, "identf", [128, 128]),
        trib=dram_in(nc, "trib", [128, 128], BF16),
        qa=dram_in(nc, "qa", [64, T], BF16), ka=dram_in(nc, "ka", [64, T], BF16), va=dram_in(nc, "va", [128, NT, 65], BF16),
        lamp=dram_in(nc, "lamp", [128, 4, 32]), lami=dram_in(nc, "lami", [128, 2]),
        qb=dram_in(nc, "qb", [64, T], BF16), kb=dram_in(nc, "kb", [64, T], BF16), vb=dram_in(nc, "vb", [128, NT, 65], BF16),
        flog=dram_in(nc, "flog", [128, NT]), fbias=dram_in(nc, "fbias", [128, 1]),
        triuf=dram_in(nc, "triuf", [128, 128]), onesf=dram_in(nc, "onesf", [128, 128]),
        oa=dram_out(nc, "oa", [T, 64], BF16), ob=dram_out(nc, "ob", [T, 64], BF16),
    )
    return io


def mix_common(s, io):
    nc = s.nc
    cm = {}
    for nm, key, dt in (("ident_b", "identb", BF16), ("ident_f", "identf", F32), ("tri_b", "trib", BF16)):
        b = s.sb(nm, [128, 128], dt)
        s.dma("sp", b[:, :], io[key][:, :], writes=[b])
        cm[nm] = b
    cm["epsc"] = s.sb("epsc", [128, 1], F32)
    s.op("dve", lambda: nc.vector.memset(cm["epsc"][:, :], EPS), writes=[cm["epsc"]])
    cm["ps_rr"] = RR([s.ps("ps%d" % j, [128, 512], F32) for j in range(3)])
    cm["po_rr"] = RR([s.ps("po%d" % j, [128, 512], F32) for j in range(3)])
    cm["pf"] = s.ps("pf", [128, 4, 128], F32)
    cm["pf2"] = s.ps("pf2", [128, 4, 128], F32)
    cm["pt_rr"] = RR([s.sb("pt%d" % j, [128, 512], BF16) for j in range(4)])
    cm["oc_rr"] = RR([s.sb("oc%d" % j, [128, 512], F32) for j in range(2)])
    cm["st_rr"] = RR([s.sb("mst%d" % j, [128, 8], F32) for j in range(8)])
    cm["ot_rr"] = RR([s.sb("ot%d" % j, [128, 4, 64], BF16) for j in range(2)])
    cm["tmp_rr"] = RR([s.sb("tmp%d" % j, [128, 64], F32) for j in range(4)])
    return cm


def emit_mix_a(s, cm, io, T):
    nc = s.nc
    NT = T // 128
    m = s.mark()
    qT = s.sb("a_q", [64, T], BF16)
    kT = s.sb("a_k", [64, T], BF16)
    vt = s.sb("a_v", [128, NT, 65], BF16)
    s.dma("sp", qT[:, :], io["qa"][:, :], writes=[qT])
    s.dma("sp", kT[:, :], io["ka"][:, :], writes=[kT])
    s.dma("sp", vt[:, :, :], io["va"][:, :, :], writes=[vt])
    lp = s.sb("lp", [128, 4, 32], F32)
    li = s.sb("li", [128, 2], F32)
    lw = s.sb("lw", [128, 2, 32], F32)
    lam = s.sb("lam", [128, 4], F32)
    s.dma("sp", lp[:, :, :], io["lamp"][:, :, :], writes=[lp])
    s.dma("sp", li[:, :], io["lami"][:, :], writes=[li])
    s.op("dve", lambda: nc.vector.tensor_tensor(out=lw[:, 0, :], in0=lp[:, 0, :], in1=lp[:, 1, :], op=ALU.mult),
         reads=[lp], writes=[lw])
    s.op("dve", lambda: nc.vector.tensor_tensor(out=lw[:, 1, :], in0=lp[:, 2, :], in1=lp[:, 3, :], op=ALU.mult),
         reads=[lp], writes=[lw])
    s.op("dve", lambda: nc.vector.tensor_reduce(out=lam[:, 0:2], in_=lw[:, :, :], axis=AX.X, op=ALU.add),
         reads=[lw], writes=[lam])
    s.op("act", lambda: nc.scalar.activation(out=lam[:, 0:2], in_=lam[:, 0:2], func=AF.Exp), reads=[lam], writes=[lam])
    s.op("dve", lambda: nc.vector.tensor_tensor(out=lam[:, 2:3], in0=lam[:, 1:2], in1=lam[:, 0:1], op=ALU.subtract),
         reads=[lam], writes=[lam])
    s.op("dve", lambda: nc.vector.tensor_tensor(out=lam[:, 3:4], in0=lam[:, 2:3], in1=li[:, 0:1], op=ALU.subtract),
         reads=[lam, li], writes=[lam])

    def fin(qb, pos):
        pf = cm["pf"]
        pf2 = cm["pf2"]
        emit_o_to_tokmajor(s, cm, pos[0], pf, 0)
        emit_o_to_tokmajor(s, cm, pos[1], pf2, 0)
        ot = cm["ot_rr"].next()
        for j in range(4):
            st = cm["st_rr"].next()
            s.op("dve", lambda: nc.vector.tensor_scalar(out=st[:, 0:1], in0=pf[:, j, 64:65], scalar1=1e-30, scalar2=None,
                                                        op0=ALU.max), reads=[pf], writes=[st])
            s.op("dve", lambda: nc.vector.tensor_scalar(out=st[:, 1:2], in0=pf2[:, j, 64:65], scalar1=1e-30, scalar2=None,
                                                        op0=ALU.max), reads=[pf2], writes=[st])
            s.op("dve", lambda: nc.vector.reciprocal(out=st[:, 0:2], in_=st[:, 0:2]), reads=[st], writes=[st])
            t2 = cm["tmp_rr"].next()
            o = cm["tmp_rr"].next()
            s.op("dve", lambda: nc.vector.tensor_scalar(out=t2[:, :], in0=pf2[:, j, 0:64], scalar1=st[:, 1:2],
                                                        scalar2=lam[:, 3:4], op0=ALU.mult, op1=ALU.mult),
                 reads=[pf2, st, lam], writes=[t2])
            s.op("dve", lambda: nc.vector.scalar_tensor_tensor(out=o[:, :], in0=pf[:, j, 0:64], scalar=st[:, 0:1], in1=t2[:, :],
                                                               op0=ALU.mult, op1=ALU.add), reads=[pf, st, t2], writes=[o])
            s.op("act", lambda: nc.scalar.activation(out=t2[:, :], in_=o[:, :], func=AF.Square, accum_out=st[:, 2:3]),
                 reads=[o], writes=[t2, st])
            s.op("act", lambda: nc.scalar.activation(out=st[:, 3:4], in_=st[:, 2:3], func=AF.Ln, bias=cm["epsc"][:, 0:1],
                                                     scale=1.0 / 64.0), reads=[st, cm["epsc"]], writes=[st])
            s.op("act", lambda: nc.scalar.activation(out=st[:, 4:5], in_=st[:, 3:4], func=AF.Exp, scale=-0.5),
                 reads=[st], writes=[st])
            s.op("dve", lambda: nc.vector.tensor_scalar(out=ot[:, j, :], in0=o[:, :], scalar1=st[:, 4:5], scalar2=li[:, 1:2],
                                                        op0=ALU.mult, op1=ALU.mult), reads=[o, st, li], writes=[ot])
        s.dma("sp", io["oa"][qb * 512:(qb + 1) * 512, :].rearrange("(j p) d -> p j d", p=128), ot[:, :, :], reads=[ot])

    emit_attn_phase(s, cm, T, 2, qT, kT, vt, [(0, 32), (32, 64)], io["oa"], fin, name="a")
    s.release(m)


def emit_mix_b(s, cm, io, T):
    nc = s.nc
    NT = T // 128
    NQB = T // 512
    m = s.mark()
    qT = s.sb("b_q", [64, T], BF16)
    kT = s.sb("b_k", [64, T], BF16)
    vt = s.sb("b_v", [128, NT, 65], BF16)
    s.dma("sp", qT[:, :], io["qb"][:, :], writes=[qT])
    s.dma("sp", kT[:, :], io["kb"][:, :], writes=[kT])
    s.dma("sp", vt[:, :, :], io["vb"][:, :, :], writes=[vt])
    fl = s.sb("fl", [128, NT], F32)
    fb = s.sb("fb", [128, 2], F32)
    tu = s.sb("tu", [128, 128], F32)
    on = s.sb("on", [128, 128], F32)
    s.dma("sp", fl[:, :], io["flog"][:, :], writes=[fl])
    s.dma("sp", fb[:, 0:1], io["fbias"][:, :], writes=[fb])
    s.dma("sp", tu[:, :], io["triuf"][:, :], writes=[tu])
    s.dma("sp", on[:, :], io["onesf"][:, :], writes=[on])
    s.op("dve", lambda: nc.vector.tensor_scalar(out=fb[:, 1:2], in0=fb[:, 0:1], scalar1=-1.0, scalar2=None, op0=ALU.mult),
         reads=[fb], writes=[fb])
    s.op("act", lambda: nc.scalar.activation(out=fl[:, :], in_=fl[:, :], func=AF.Exp, bias=fb[:, 1:2], scale=-1.0),
         reads=[fl, fb], writes=[fl])
    s.op("act", lambda: nc.scalar.activation(out=fl[:, :], in_=fl[:, :], func=AF.Ln, bias=1.0, scale=1.0),
         reads=[fl], writes=[fl])
    pc = cm["pf"]
    pcv = pc[:, 0, :]
    s.op("pe", lambda: nc.tensor.matmul(pc[:, 0, 0:NT], lhsT=tu[:, :], rhs=fl[:, :], start=True, stop=True),
         reads=[tu, fl], writes=[pc])
    s.op("pe", lambda: nc.tensor.matmul(pc[:, 1, 0:NT], lhsT=on[:, :], rhs=fl[:, :], start=True, stop=True),
         reads=[on, fl], writes=[pc])
    cc = s.sb("cc", [128, NT], F32)
    inc_ = s.sb("inc", [128, NT], F32)
    tmpc = s.sb("tmpc", [128, NT], F32)
    s.op("dve", lambda: nc.vector.tensor_copy(out=inc_[:, :], in_=pc[:, 1, 0:NT]), reads=[pc], writes=[inc_])
    sh = 1
    while sh < NT:
        s.op("dve", lambda: nc.vector.tensor_copy(out=tmpc[:, :], in_=inc_[:, :]), reads=[inc_], writes=[tmpc])
        s.op("dve", lambda: nc.vector.tensor_tensor(out=inc_[:, sh:NT], in0=tmpc[:, sh:NT], in1=tmpc[:, 0:NT - sh], op=ALU.add),
             reads=[tmpc], writes=[inc_])
        sh *= 2
    s.op("dve", lambda: nc.vector.tensor_tensor(out=cc[:, :], in0=pc[:, 0, 0:NT], in1=inc_[:, :], op=ALU.add),
         reads=[pc, inc_], writes=[cc])
    s.op("dve", lambda: nc.vector.tensor_tensor(out=tmpc[:, :], in0=cc[:, :], in1=pc[:, 1, 0:NT], op=ALU.subtract),
         reads=[pc, cc], writes=[tmpc])
    btab = s.sb("btab", [128, NQB, NT], F32)
    for qb in range(NQB):
        s.op("dve", lambda: nc.vector.tensor_scalar(out=btab[:, qb, :], in0=tmpc[:, :], scalar1=inc_[:, 4 * qb + 1:4 * qb + 2],
                                                    scalar2=None, op0=ALU.subtract), reads=[tmpc, inc_], writes=[btab])

    def bias_fn(qb, t):
        return btab, btab[:, qb, t:t + 1]

    def fin(qb, pos):
        pf = cm["pf"]
        emit_o_to_tokmajor(s, cm, pos[0], pf, 0)
        ot = cm["ot_rr"].next()
        for j in range(4):
            st = cm["st_rr"].next()
            s.op("dve", lambda: nc.vector.tensor_scalar(out=st[:, 0:1], in0=pf[:, j, 64:65], scalar1=1e-30, scalar2=None,
                                                        op0=ALU.max), reads=[pf], writes=[st])
            s.op("dve", lambda: nc.vector.reciprocal(out=st[:, 0:1], in_=st[:, 0:1]), reads=[st], writes=[st])
            s.op("dve", lambda: nc.vector.tensor_scalar(out=ot[:, j, :], in0=pf[:, j, 0:64], scalar1=st[:, 0:1], scalar2=None,
                                                        op0=ALU.mult), reads=[pf, st], writes=[ot])
        s.dma("sp", io["ob"][qb * 512:(qb + 1) * 512, :].rearrange("(j p) d -> p j d", p=128), ot[:, :, :], reads=[ot])

    emit_attn_phase(s, cm, T, 1, qT, kT, vt, [(0, 64)], io["ob"], fin, bias_fn=bias_fn, name="b")
    s.release(m)


def mix_consts():
    k = np.arange(128)
    tri = np.where(k[:, None] > k[None, :], NEG, 0.0).astype(np.float32)
    return dict(identb=_bf(np.eye(128, dtype=np.float32)), identf=np.eye(128, dtype=np.float32), trib=_bf(tri),
                triuf=np.triu(np.ones((128, 128), np.float32)), onesf=np.ones((128, 128), np.float32))


def mix_decl_c(nc, io, T):
    NT = T // 128
    QL = NT // 4
    NCT = max(1, T // 2048)
    io.update(dict(
        qc=dram_in(nc, "qc", [128, QL, 512], BF16),
        kskw=dram_in(nc, "kskw", [128, T], BF16),
        vs=dram_in(nc, "vs", [128, NT, 65], BF16), vw=dram_in(nc, "vw", [128, NT, 65], BF16),
        kvin=dram_in(nc, "kvin", [128, T], BF16),
        w1=dram_in(nc, "w1", [2, 2048, 256]), b1=dram_in(nc, "b1", [128, 4]),
        peT=dram_in(nc, "peT", [128, 32]),
        w2=dram_in(nc, "w2", [2, 256, 64]), b2=dram_in(nc, "b2", [2, 64]), b2c=dram_in(nc, "b2c", [64, 1]),
        kgain=dram_in(nc, "kgain", [64, 1]),
        ng=dram_in(nc, "ng", [128, QL, 12]),
        cmask=dram_in(nc, "cmask", [128, QL, NCT, 128], BF16),
        smask=dram_in(nc, "smask", [128, 4, 128], BF16), wmask=dram_in(nc, "wmask", [128, 8, 128], BF16),
        impA=dram_in(nc, "impA", [128, QL, 128]), impB=dram_in(nc, "impB", [128, QL, 128]),
        emat=dram_in(nc, "emat", [128, NT, 128], BF16), ovl=dram_in(nc, "ovl", [128, NCT, 128], BF16),
        ones64=dram_in(nc, "ones64", [64, 64], BF16), onesrow=dram_in(nc, "onesrow", [1, 128], BF16),
        oc=dram_out(nc, "oc", [QL * 128, 256], BF16),
    ))
    return io


def emit_gelu(s, zin_ap, zin_b, out_ap, out_b, tmp, shape_sl):
    nc = s.nc
    t = tmp
    s.op("act", lambda: nc.scalar.activation(out=t[shape_sl], in_=zin_ap, func=AF.Square), reads=[zin_b], writes=[t])
    s.op("dve", lambda: nc.vector.tensor_scalar(out=t[shape_sl], in0=t[shape_sl], scalar1=0.044715, scalar2=1.0,
                                                op0=ALU.mult, op1=ALU.add), reads=[t], writes=[t])
    s.op("dve", lambda: nc.vector.tensor_tensor(out=t[shape_sl], in0=t[shape_sl], in1=zin_ap, op=ALU.mult),
         reads=[t, zin_b], writes=[t])
    s.op("act", lambda: nc.scalar.activation(out=t[shape_sl], in_=t[shape_sl], func=AF.Exp, scale=-GELU_C), reads=[t], writes=[t])
    s.op("dve", lambda: nc.vector.tensor_scalar(out=t[shape_sl], in0=t[shape_sl], scalar1=1.0, scalar2=None, op0=ALU.add),
         reads=[t], writes=[t])
    s.op("dve", lambda: nc.vector.reciprocal(out=t[shape_sl], in_=t[shape_sl]), reads=[t], writes=[t])
    s.op("dve", lambda: nc.vector.tensor_tensor(out=out_ap, in0=t[shape_sl], in1=zin_ap, op=ALU.mult),
         reads=[t, zin_b], writes=[out_b])


def emit_mix_c(s, cm, io, T):
    nc = s.nc
    NT = T // 128
    QL = NT // 4
    NCT = max(1, T // 2048)
    Nc = T // 16 - 1
    NCP = NCT * 128 if Nc > 128 else 128
    NCW = min(Nc, 511)
    assert Nc <= 511
    m = s.mark()
    ident_b = cm["ident_b"]
    ps_l = cm["ps_rr"].items
    po_l = cm["po_rr"].items
    pf, pf2 = cm["pf"], cm["pf2"]

    def ld(name, shape, dt, src, q="sp"):
        b = s.sb(name, shape, dt)
        idx = tuple(slice(None) for _ in shape)
        s.dma(q, b[idx], src, writes=[b])
        return b

    qc = ld("c_q", [128, QL, 512], BF16, io["qc"][:, :, :])
    kk = ld("c_kk", [128, T], BF16, io["kskw"][:, :])
    vs = ld("c_vs", [128, NT, 65], BF16, io["vs"][:, :, :])
    vw = ld("c_vw", [128, NT, 65], BF16, io["vw"][:, :, :])
    emat = ld("c_e", [128, NT, 128], BF16, io["emat"][:, :, :])
    ovl = ld("c_ovl", [128, NCT, 128], BF16, io["ovl"][:, :, :])
    smask = ld("c_sm", [128, 4, 128], BF16, io["smask"][:, :, :])
    wmask = ld("c_wm", [128, 8, 128], BF16, io["wmask"][:, :, :])
    ngt = ld("c_ng", [128, QL, 12], F32, io["ng"][:, :, :])
    ones64 = ld("c_o64", [64, 64], BF16, io["ones64"][:, :])
    onesrow = ld("c_orow", [1, 128], BF16, io["onesrow"][:, :])
    kgain = ld("c_kg", [64, 1], F32, io["kgain"][:, :])
    b2c = ld("c_b2c", [64, 1], F32, io["b2c"][:, :])
    b1 = ld("c_b1", [128, 4], F32, io["b1"][:, :])
    s.op("act", lambda: nc.scalar.activation(out=ngt[:, :, :], in_=ngt[:, :, :], func=AF.Exp, scale=-1.0), reads=[ngt], writes=[ngt])
    s.op("dve", lambda: nc.vector.tensor_scalar(out=ngt[:, :, :], in0=ngt[:, :, :], scalar1=1.0, scalar2=None, op0=ALU.add),
         reads=[ngt], writes=[ngt])
    s.op("dve", lambda: nc.vector.reciprocal(out=ngt[:, :, :], in_=ngt[:, :, :]), reads=[ngt], writes=[ngt])

    ktc = s.sb("c_ktc", [64, NCP], BF16)
    vc = s.sb("c_vc", [128, NCT, 65], BF16)
    s.op("dve", lambda: nc.vector.memset(ktc[:, :], 0.0), writes=[ktc])
    s.op("dve", lambda: nc.vector.memset(vc[:, :, :], 0.0), writes=[vc])
    s.op("dve", lambda: nc.vector.memset(vc[:, :, 64:65], 1.0), writes=[vc])

    m2 = s.mark()
    kvin = ld("c_kvin", [128, T], BF16, io["kvin"][:, :])
    w1sb = s.sb("c_w1", [128, 32, 256], BF16)
    for x in range(2):
        s.dma("pool", w1sb[x * 64:(x + 1) * 64, :, :], io["w1"][x].rearrange("(j d) f -> d j f", d=64), writes=[w1sb])
    peT = s.sb("c_pe", [128, 32], BF16)
    s.dma("pool", peT[:, :], io["peT"][:, :], writes=[peT])
    w2sb = s.sb("c_w2", [128, 2, 2, 64], BF16)
    for x in range(2):
        s.dma("pool", w2sb[:, x, :, :], io["w2"][x].rearrange("(hh f) d -> f hh d", f=128), writes=[w2sb])
    b2row = s.sb("c_b2r", [1, 64], BF16)
    s.dma("pool", b2row[:, :], io["b2"][1:2, :], writes=[b2row])
    hacc = [ps_l[0], ps_l[1], ps_l[2], po_l[0]]
    pcol = po_l[1]
    for x in range(2):
        for hh in range(2):
            hp = hacc[x * 2 + hh]
            for j in range(32):
                s.op("pe", lambda: nc.tensor.matmul(hp[:, 0:NCW], lhsT=w1sb[x * 64:(x + 1) * 64, j, hh * 128:(hh + 1) * 128],
                                                    rhs=kvin[x * 64:(x + 1) * 64, j:j + 16 * (NCW - 1) + 1:16],
                                                    start=(j == 0), stop=(j == 31)),
                     reads=[w1sb, kvin], writes=[hp], inc=(j == 31))
            for j in range(32):
                s.op("pe", lambda: nc.tensor.matmul(pcol[:, x * 2 + hh:x * 2 + hh + 1],
                                                    lhsT=w1sb[x * 64:(x + 1) * 64, j, hh * 128:(hh + 1) * 128],
                                                    rhs=peT[x * 64:(x + 1) * 64, j:j + 1], start=(j == 0), stop=(j == 31)),
                     reads=[w1sb, peT], writes=[pcol], inc=(j == 31))
    hbias = s.sb("c_hb", [128, 4], F32)
    s.op("dve", lambda: nc.vector.tensor_tensor(out=hbias[:, :], in0=pcol[:, 0:4], in1=b1[:, :], op=ALU.add),
         reads=[pcol, b1], writes=[hbias])
    gh = []
    for x in range(2):
        for hh in range(2):
            k = x * 2 + hh
            z = s.sb("c_z%d" % k, [128, 512], F32)
            tmp = s.sb("c_zt%d" % k, [128, 512], F32)
            gb = s.sb("c_g%d" % k, [128, 512], BF16)
            s.op("act", lambda: nc.scalar.activation(out=z[:, 0:NCW], in_=hacc[k][:, 0:NCW], func=AF.Identity,
                                                     bias=hbias[:, k:k + 1], scale=1.0), reads=[hacc[k], hbias], writes=[z])
            emit_gelu(s, z[:, 0:NCW], z, gb[:, 0:NCW], gb, tmp, (slice(None), slice(0, NCW)))
            gh.append(gb)
    pk = po_l[2]
    for hh in range(2):
        s.op("pe", lambda: nc.tensor.matmul(pk[0:64, 0:NCW], lhsT=w2sb[:, 0, hh, :], rhs=gh[hh][:, 0:NCW],
                                            start=(hh == 0), stop=(hh == 1)), reads=[w2sb, gh[hh]], writes=[pk], inc=(hh == 1))
    kz = s.sb("c_kz", [64, 512], F32)
    ksq = s.sb("c_ksq", [64, 512], BF16)
    krs = s.sb("c_krs", [64, 512], F32)
    s.op("act", lambda: nc.scalar.activation(out=kz[:, 0:NCW], in_=pk[0:64, 0:NCW], func=AF.Identity, bias=b2c[:, 0:1], scale=1.0),
         reads=[pk, b2c], writes=[kz])
    s.op("act", lambda: nc.scalar.activation(out=ksq[:, 0:NCW], in_=kz[:, 0:NCW], func=AF.Square), reads=[kz], writes=[ksq])
    pq = ps_l[0]
    s.op("pe", lambda: nc.tensor.matmul(pq[0:64, 0:NCW], lhsT=ones64[:, :], rhs=ksq[:, 0:NCW], start=True, stop=True),
         reads=[ones64, ksq], writes=[pq])
    s.op("act", lambda: nc.scalar.activation(out=krs[:, 0:NCW], in_=pq[0:64, 0:NCW], func=AF.Ln, bias=cm["epsc"][0:64, 0:1],
                                             scale=1.0 / 64.0), reads=[pq, cm["epsc"]], writes=[krs])
    s.op("act", lambda: nc.scalar.activation(out=krs[:, 0:NCW], in_=krs[:, 0:NCW], func=AF.Exp, scale=-0.5), reads=[krs], writes=[krs])
    s.op("dve", lambda: nc.vector.scalar_tensor_tensor(out=ktc[:, 0:NCW], in0=kz[:, 0:NCW], scalar=kgain[:, 0:1], in1=krs[:, 0:NCW],
                                                       op0=ALU.mult, op1=ALU.mult), reads=[kz, kgain, krs], writes=[ktc])
    for nt in range(NCT):
        n0 = nt * 128
        nn = min(128, Nc - n0)
        pv = ps_l[1 + nt % 2]
        for hh in range(2):
            s.op("pe", lambda: nc.tensor.matmul(pv[0:nn, 0:64], lhsT=gh[2 + hh][:, n0:n0 + nn], rhs=w2sb[:, 1, hh, :],
                                                start=(hh == 0), stop=False), reads=[gh[2 + hh], w2sb], writes=[pv], inc=False)
        s.op("pe", lambda: nc.tensor.matmul(pv[0:nn, 0:64], lhsT=onesrow[0:1, 0:nn], rhs=b2row[0:1, :], start=False, stop=True),
             reads=[onesrow, b2row], writes=[pv])
        s.op("act", lambda: nc.scalar.copy(out=vc[0:nn, nt, 0:64], in_=pv[0:nn, 0:64]), reads=[pv], writes=[vc])
    s.release(m2)

    cmk_rr = RR([s.sb("c_cmk%d" % j, [128, NCT, 128], BF16) for j in range(2)])
    ia_rr = RR([s.sb("c_ia%d" % j, [128, 128], F32) for j in range(2)])
    ib_rr = RR([s.sb("c_ib%d" % j, [128, 128], F32) for j in range(2)])
    imp_rr = RR([s.sb("c_imp%d" % j, [128, 128], F32) for j in range(2)])
    imp2_rr = RR([s.sb("c_impb%d" % j, [128, 128], F32) for j in range(2)])
    m8_rr = RR([s.sb("c_m8%d" % j, [128, 16], F32) for j in range(2)])
    mbT_rr = RR([s.sb("c_mbT%d" % j, [128, 128], BF16) for j in range(2)])
    oco_rr = RR([s.sb("c_oc%d" % j, [128, 4, 64], F32) for j in range(2)])
    gw_rr = RR([s.sb("c_gw%d" % j, [128, 12], F32) for j in range(2)])
    oo_rr = RR([s.sb("c_oo%d" % j, [128, 4, 64], F32) for j in range(2)])
    ob_rr = RR([s.sb("c_ob%d" % j, [128, 4, 64], BF16) for j in range(2)])

    def masked_tile(kbuf, prow, t, Q, masks, vbuf, vt_idx, po, first, last, extra=None):
        ps = cm["ps_rr"].next()
        nm = len(masks)
        s.op("pe", lambda: nc.tensor.matmul(ps[:, :], lhsT=kbuf[prow[0]:prow[1], t * 128:(t + 1) * 128], rhs=Q,
                                            start=True, stop=(nm == 0)), reads=[kbuf, qc], writes=[ps], inc=(nm == 0))
        for mi, (la, lb, ra, rb) in enumerate(masks):
            for h in range(4):
                lastm = (mi == nm - 1 and h == 3)
                s.op("pe", lambda: nc.tensor.matmul(ps[:, h * 128:(h + 1) * 128], lhsT=la, rhs=ra, start=False, stop=lastm),
                     reads=[lb, rb], writes=[ps], inc=lastm)
        pt = cm["pt_rr"].next()
        s.op("act", lambda: nc.scalar.activation(out=pt[:, :], in_=ps[:, :], func=AF.Exp), reads=[ps], writes=[pt])
        s.op("pe", lambda: nc.tensor.matmul(po[0:65, :], lhsT=vbuf[:, vt_idx, :], rhs=pt[:, :], start=first, stop=last),
             reads=[vbuf, pt], writes=[po])
        return pt

    for i in range(QL):
        Qlo = qc[0:64, i, :]
        Qhi = qc[64:128, i, :]
        cmk = cmk_rr.next()
        ia = ia_rr.next()
        ib = ib_rr.next()
        s.dma("sp", cmk[:, :, :], io["cmask"][:, i, :, :], writes=[cmk])
        s.dma("sp", ia[:, :], io["impA"][:, i, :], writes=[ia])
        s.dma("sp", ib[:, :], io["impB"][:, i, :], writes=[ib])
        po_c, po_s, po_w = po_l[0], po_l[1], po_l[2]
        nct = min(NCT, i // 4 + 1)
        for nt in range(nct):
            pt = masked_tile(ktc, (0, 64), nt, Qlo, [(ident_b[:, :], ident_b, cmk[:, nt, :], cmk)], vc, nt, po_c,
                             nt == 0, nt == nct - 1)
            for h in range(4):
                s.op("pe", lambda: nc.tensor.matmul(pf2[:, h, :], lhsT=pt[:, h * 128:(h + 1) * 128], rhs=ovl[:, nt, :],
                                                    start=(nt == 0 and h == 0), stop=(nt == nct - 1 and h == 3),
                                                    skip_group_check=True), reads=[pt, ovl], writes=[pf2],
                     inc=(h == 3))
        emit_o_to_tokmajor(s, cm, po_c, pf, 0)
        st = cm["st_rr"].next()
        rsum = cm["st_rr"].next()
        gw = gw_rr.next()
        s.op("dve", lambda: nc.vector.tensor_scalar(out=st[:, 0:4], in0=pf[:, :, 64], scalar1=1e-30, scalar2=None, op0=ALU.max),
             reads=[pf], writes=[st])
        s.op("dve", lambda: nc.vector.reciprocal(out=rsum[:, 0:4], in_=st[:, 0:4]), reads=[st], writes=[rsum])
        oco = oco_rr.next()
        s.op("dve", lambda: nc.vector.tensor_copy(out=oco[:, :, :], in_=pf[:, :, 0:64]), reads=[pf], writes=[oco])
        imp = imp_rr.next()
        s.op("dve", lambda: nc.vector.tensor_scalar(out=imp[:, :], in0=pf2[:, 0, :], scalar1=rsum[:, 0:1], scalar2=None, op0=ALU.mult),
             reads=[pf2, rsum], writes=[imp])
        for h in range(1, 4):
            s.op("dve", lambda: nc.vector.scalar_tensor_tensor(out=imp[:, :], in0=pf2[:, h, :], scalar=rsum[:, h:h + 1], in1=imp[:, :],
                                                               op0=ALU.mult, op1=ALU.add), reads=[pf2, rsum, imp], writes=[imp])
        s.op("dve", lambda: nc.vector.tensor_tensor(out=imp[:, :], in0=imp[:, :], in1=ia[:, :], op=ALU.mult), reads=[imp, ia], writes=[imp])
        s.op("dve", lambda: nc.vector.tensor_tensor(out=imp[:, :], in0=imp[:, :], in1=ib[:, :], op=ALU.add), reads=[imp, ib], writes=[imp])
        m8 = m8_rr.next()
        imp2 = imp2_rr.next()
        s.op("dve", lambda: nc.vector.max(out=m8[:, 0:8], in_=imp[:, :]), reads=[imp], writes=[m8])
        s.op("dve", lambda: nc.vector.match_replace(out=imp2[:, :], in_to_replace=m8[:, 0:8], in_values=imp[:, :], imm_value=-1e9),
             reads=[imp, m8], writes=[imp2])
        s.op("dve", lambda: nc.vector.max(out=m8[:, 8:16], in_=imp2[:, :]), reads=[imp2], writes=[m8])
        s.op("dve", lambda: nc.vector.tensor_scalar(out=imp2[:, :], in0=imp[:, :], scalar1=m8[:, 15:16], scalar2=NEG,
                                                    op0=ALU.is_lt, op1=ALU.mult), reads=[imp, m8], writes=[imp2])
        ptr = cm["ps_rr"].next()
        s.op("pe", lambda: nc.tensor.transpose(out=ptr[:, 0:128], in_=imp2[:, :], identity=cm["ident_f"][:, :]),
             reads=[imp2, cm["ident_f"]], writes=[ptr])
        mbT = mbT_rr.next()
        s.op("act", lambda: nc.scalar.copy(out=mbT[:, :], in_=ptr[:, 0:128]), reads=[ptr], writes=[mbT])
        nts = 4 * i + 4
        for t in range(nts):
            masks = [(emat[:, t, :], emat, mbT[:, :], mbT)]
            if t >= 4 * i:
                masks.append((ident_b[:, :], ident_b, smask[:, t - 4 * i, :], smask))
            masked_tile(kk, (0, 64), t, Qlo, masks, vs, t, po_s, t == 0, t == nts - 1)
        tl = [4 * (i - 1) + u for u in range(8) if 4 * (i - 1) + u >= 0]
        for t in tl:
            u = t - 4 * (i - 1)
            masked_tile(kk, (64, 128), t, Qhi, [(ident_b[:, :], ident_b, wmask[:, u, :], wmask)], vw, t, po_w,
                        t == tl[0], t == tl[-1])
        emit_o_to_tokmajor(s, cm, po_s, pf, 0)
        st2 = cm["st_rr"].next()
        s.op("dve", lambda: nc.vector.tensor_scalar(out=st2[:, 0:4], in0=pf[:, :, 64], scalar1=1e-30, scalar2=None, op0=ALU.max),
             reads=[pf], writes=[st2])
        s.op("dve", lambda: nc.vector.reciprocal(out=st2[:, 0:4], in_=st2[:, 0:4]), reads=[st2], writes=[st2])
        gv = ngt[:, i, :].rearrange("p (h b) -> p h b", b=3)
        gwv = gw[:, :].rearrange("p (h b) -> p h b", b=3)
        s.op("dve", lambda: nc.vector.tensor_tensor(out=gwv[:, :, 0], in0=gv[:, :, 0], in1=rsum[:, 0:4], op=ALU.mult),
             reads=[ngt, rsum], writes=[gw])
        s.op("dve", lambda: nc.vector.tensor_tensor(out=gwv[:, :, 1], in0=gv[:, :, 1], in1=st2[:, 0:4], op=ALU.mult),
             reads=[ngt, st2], writes=[gw])
        oo = oo_rr.next()
        for h in range(4):
            s.op("dve", lambda: nc.vector.tensor_scalar(out=oo[:, h, :], in0=oco[:, h, :], scalar1=gw[:, 3 * h:3 * h + 1], scalar2=None,
                                                        op0=ALU.mult), reads=[oco, gw], writes=[oo])
            s.op("dve", lambda: nc.vector.scalar_tensor_tensor(out=oo[:, h, :], in0=pf[:, h, 0:64], scalar=gw[:, 3 * h + 1:3 * h + 2],
                                                               in1=oo[:, h, :], op0=ALU.mult, op1=ALU.add), reads=[pf, gw, oo], writes=[oo])
        emit_o_to_tokmajor(s, cm, po_w, pf, 0)
        st3 = cm["st_rr"].next()
        s.op("dve", lambda: nc.vector.tensor_scalar(out=st3[:, 0:4], in0=pf[:, :, 64], scalar1=1e-30, scalar2=None, op0=ALU.max),
             reads=[pf], writes=[st3])
        s.op("dve", lambda: nc.vector.reciprocal(out=st3[:, 0:4], in_=st3[:, 0:4]), reads=[st3], writes=[st3])
        s.op("dve", lambda: nc.vector.tensor_tensor(out=gwv[:, :, 2], in0=gv[:, :, 2], in1=st3[:, 0:4], op=ALU.mult),
             reads=[ngt, st3], writes=[gw])
        ob = ob_rr.next()
        for h in range(4):
            s.op("dve", lambda: nc.vector.scalar_tensor_tensor(out=ob[:, h, :], in0=pf[:, h, 0:64], scalar=gw[:, 3 * h + 2:3 * h + 3],
                                                               in1=oo[:, h, :], op0=ALU.mult, op1=ALU.add), reads=[pf, gw, oo], writes=[ob])
        s.dma("sp", io["oc"][i * 128:(i + 1) * 128, :], ob[:, :, :].rearrange("p h d -> p (h d)"), reads=[ob])
    s.release(m)


def build_mix(T, parts="abc"):
    nc = bass.Bass("TRN2", target_bir_lowering=False)
    io = mix_decl(nc, T)
    if "c" in parts:
        mix_decl_c(nc, io, T)
    s = S(nc)
    cm = mix_common(s, io)
    if "a" in parts:
        emit_mix_a(s, cm, io, T)
    if "b" in parts:
        emit_mix_b(s, cm, io, T)
    if "c" in parts:
        emit_mix_c(s, cm, io, T)
    s.finish()
    s.close()
    return nc


def mix_consts_c(T, c):
    NT = T // 128
    QL = NT // 4
    NCT = max(1, T // 2048)
    Nc = T // 16 - 1
    NS = T // 64
    ar = np.arange(128)
    cmask = np.zeros((128, QL, NCT, 128), np.float32)
    impA = np.zeros((128, QL, 128), np.float32)
    impB = np.zeros((128, QL, 128), np.float32)
    for i in range(QL):
        qpos = 128 * (4 * i + c) + ar
        for nt in range(NCT):
            n = 128 * nt + ar
            ok = (16 * n[:, None] + 31 <= qpos[None, :]) & (n[:, None] < Nc)
            cmask[:, i, nt, :] = np.where(ok, 0.0, NEG)
        j = ar
        cur = qpos // 64
        forced = (j[None, :] == 0) | (j[None, :] == cur[:, None]) | (j[None, :] == cur[:, None] - 1)
        valid = (j[None, :] * 64 <= qpos[:, None]) & (j[None, :] < NS)
        impA[:, i, :] = (valid & ~forced).astype(np.float32)
        impB[:, i, :] = np.where(forced & (j[None, :] < NS), 1.0e4, np.where(valid, 0.0, -1.0))
    smask = np.zeros((128, 4, 128), np.float32)
    for u in range(4):
        kpos = 128 * u + ar
        qp = 128 * c + ar
        smask[:, u, :] = np.where(kpos[:, None] <= qp[None, :], 0.0, NEG)
    wmask = np.zeros((128, 8, 128), np.float32)
    for u in range(8):
        dist = 128 * (c + 4 - u) + ar[None, :] - ar[:, None]
        wmask[:, u, :] = np.where((dist >= 0) & (dist < 512), 0.0, NEG)
    emat = np.zeros((128, NT, 128), np.float32)
    for t in range(NT):
        for k in range(128):
            jj = 2 * t + k // 64
            if jj < 128:
                emat[jj, t, k] = 1.0
    ovl = np.zeros((128, NCT, 128), np.float32)
    for nt in range(NCT):
        n = 128 * nt + ar
        o = (n[:, None] * 16 < (ar[None, :] + 1) * 64) & (n[:, None] * 16 + 32 > ar[None, :] * 64) & (n[:, None] < Nc) \
            & (ar[None, :] < NS)
        ovl[:, nt, :] = o
    return dict(cmask=_bf(cmask), impA=impA, impB=impB, smask=_bf(smask), wmask=_bf(wmask), emat=_bf(emat), ovl=_bf(ovl),
                ones64=_bf(np.ones((64, 64), np.float32)), onesrow=_bf(np.ones((1, 128), np.float32)))


def build_merge(NT, TB=512):
    nc = bass.Bass("TRN2", target_bir_lowering=False)
    x = dram_in(nc, "x", [NT, D])
    g = dram_in(nc, "g", [D])
    w_in = dram_in(nc, "w_in", [D, 6800])
    w_br = dram_in(nc, "w_br", [4, 256, D])
    w_o = dram_in(nc, "w_o", [D, D])
    ident = dram_in(nc, "ident", [128, 128], BF16)
    obr = dram_in(nc, "obr", [NT, D], BF16)
    y = dram_out(nc, "y", [NT, D])
    s = S(nc)
    ntile = TB // 128
    ident_b = s.sb("ident_b", [128, 128], BF16)
    s.dma("sp", ident_b[:, :], ident[:, :], writes=[ident_b])
    gcol = s.sb("gcol", [128, NKC], F32)
    s.dma("sp", gcol[:, :], g.rearrange("(c p) -> p c", p=128), writes=[gcol], allow_slow_non_contiguous=True)
    epsc = s.sb("epsc", [128, 1], F32)
    s.op("dve", lambda: nc.vector.memset(epsc[:, :], EPS), writes=[epsc])
    wg = [s.sb("wg%d" % c, [128, 4096], BF16) for c in range(NKC)]
    wb = [s.sb("wb%d" % c, [128, D], BF16) for c in range(8)]
    wo = [s.sb("wo%d" % c, [128, D], BF16) for c in range(NKC)]
    for c in range(NKC):
        for hf in range(2):
            s.dma("pool", wg[c][:, hf * 2048:(hf + 1) * 2048], w_in[c * 128:(c + 1) * 128, 2704 + hf * 2048:2704 + (hf + 1) * 2048],
                  writes=[wg[c]])
    for n in range(4):
        for cc in range(2):
            s.dma("pool", wb[2 * n + cc][:, :], w_br[n, cc * 128:(cc + 1) * 128, :], writes=[wb[2 * n + cc]])
    for c in range(NKC):
        s.dma("pool", wo[c][:, :], w_o[c * 128:(c + 1) * 128, :], writes=[wo[c]])
    xt = [s.sb("xt%d" % j, [128, D], F32) for j in range(ntile)]
    ot = [s.sb("ot%d" % j, [128, D], BF16) for j in range(ntile)]
    hb_rr = RR([s.sb("hb%d" % j, [128, D], BF16) for j in range(ntile)])
    scr = s.sb("scr", [128, D], BF16)
    stat_rr = RR([s.sb("st%d" % j, [128, 4], F32) for j in range(4)])
    hT = s.sb("hT", [128, NKC, TB], BF16)
    oT = s.sb("oT", [128, 8, TB], BF16)
    mT = [s.sb("mT%d" % c, [128, TB], BF16) for c in range(8)]
    pT_rr = RR([s.ps("pT%d" % j, [128, TB], BF16) for j in range(2)])
    pg_rr = RR([s.ps("pg%d" % j, [128, 512], F32) for j in range(2)])
    pp_rr = RR([s.ps("pp%d" % j, [128, 512], F32) for j in range(2)])
    po_rr = RR([s.ps("po%d" % j, [128, 512], F32) for j in range(2)])
    sg_rr = RR([s.sb("sg%d" % j, [128, TB], F32) for j in range(3)])
    acc_rr = RR([s.sb("acc%d" % j, [128, TB], F32) for j in range(2)])
    for tb in range(NT // TB):
        t0 = tb * TB
        for j in range(ntile):
            s.dma("sp", xt[j][:, :], x[t0 + j * 128:t0 + (j + 1) * 128, :], writes=[xt[j]])
            s.dma("sp", ot[j][:, :], obr[t0 + j * 128:t0 + (j + 1) * 128, :], writes=[ot[j]])
        emit_rmsnorm_T(s, epsc, xt, gcol, hT, ident_b, pT_rr, hb_rr, scr, stat_rr, ntile)
        for c in range(8):
            pT = pT_rr.next()
            for j in range(ntile):
                s.op("pe", lambda: nc.tensor.transpose(out=pT[:, j * 128:(j + 1) * 128], in_=ot[j][:, c * 128:(c + 1) * 128],
                                                       identity=ident_b[:, :]), reads=[ot[j], ident_b], writes=[pT], inc=(j == ntile - 1))
            if c % 2 == 0:
                s.op("dve", lambda: nc.vector.tensor_copy(out=oT[:, c, :], in_=pT[:, 0:TB]), reads=[pT], writes=[oT])
            else:
                s.op("act", lambda: nc.scalar.copy(out=oT[:, c, :], in_=pT[:, 0:TB]), reads=[pT], writes=[oT])
        for dc in range(8):
            acc = acc_rr.next()
            for n in range(4):
                pg = pg_rr.next()
                pp = pp_rr.next()
                for c in range(NKC):
                    s.op("pe", lambda: nc.tensor.matmul(pg[:, 0:TB], lhsT=wg[c][:, n * 1024 + dc * 128:n * 1024 + (dc + 1) * 128],
                                                        rhs=hT[:, c, :], start=(c == 0), stop=(c == NKC - 1)),
                         reads=[wg[c], hT], writes=[pg], inc=(c == NKC - 1))
                for cc in range(2):
                    s.op("pe", lambda: nc.tensor.matmul(pp[:, 0:TB], lhsT=wb[2 * n + cc][:, dc * 128:(dc + 1) * 128],
                                                        rhs=oT[:, 2 * n + cc, :], start=(cc == 0), stop=(cc == 1)),
                         reads=[wb[2 * n + cc], oT], writes=[pp], inc=(cc == 1))
                sg = sg_rr.next()
                s.op("act", lambda: nc.scalar.activation(out=sg[:, :], in_=pg[:, 0:TB], func=AF.Sigmoid), reads=[pg], writes=[sg])
                if n == 0:
                    s.op("dve", lambda: nc.vector.tensor_tensor(out=acc[:, :], in0=sg[:, :], in1=pp[:, 0:TB], op=ALU.mult),
                         reads=[sg, pp], writes=[acc])
                else:
                    s.op("dve", lambda: nc.vector.tensor_tensor(out=sg[:, :], in0=sg[:, :], in1=pp[:, 0:TB], op=ALU.mult),
                         reads=[sg, pp], writes=[sg])
                    if n < 3:
                        s.op("pool", lambda: nc.gpsimd.tensor_tensor(out=acc[:, :], in0=acc[:, :], in1=sg[:, :], op=ALU.add),
                             reads=[acc, sg], writes=[acc])
                    else:
                        s.op("pool", lambda: nc.gpsimd.tensor_tensor(out=mT[dc][:, :], in0=acc[:, :], in1=sg[:, :], op=ALU.add),
                             reads=[acc, sg], writes=[mT[dc]])
        for j in range(ntile):
            for hf in range(2):
                po = po_rr.next()
                for dc in range(8):
                    s.op("pe", lambda: nc.tensor.matmul(po[:, :], lhsT=mT[dc][:, j * 128:(j + 1) * 128],
                                                        rhs=wo[dc][:, hf * 512:(hf + 1) * 512], start=(dc == 0), stop=(dc == 7)),
                         reads=[mT[dc], wo[dc]], writes=[po], inc=(dc == 7))
                s.op("dve", lambda: nc.vector.tensor_tensor(out=xt[j][:, hf * 512:(hf + 1) * 512], in0=po[:, :],
                                                            in1=xt[j][:, hf * 512:(hf + 1) * 512], op=ALU.add),
                     reads=[po, xt[j]], writes=[xt[j]])
            s.dma("sp", y[t0 + j * 128:t0 + (j + 1) * 128, :], xt[j][:, :], reads=[xt[j]])
    s.finish()
    s.close()
    return nc


B_, T_, L_ = 2, 8192, 2
NCORE = 8
NTOK = B_ * T_ // NCORE
_PROG = {}


def _prog(name, fn):
    if name not in _PROG:
        _PROG[name] = fn()
    return _PROG[name]


def _run(nc, in_maps):
    res = run_bass_kernel_spmd(nc, in_maps, core_ids=list(range(NCORE)))
    return res.results


def _vlay(v):
    T = v.shape[0]
    a = np.ones((T, 65), dtype=v.dtype)
    a[:, :64] = v
    return np.ascontiguousarray(a.reshape(T // 128, 128, 65).transpose(1, 0, 2))


def kernel(x, ffn1_norm, ffn1_w_in, ffn1_w_out, mix_norm, w_in, diff_q_gain, diff_k_gain, diff_lambda, fox_q_gain,
           fox_k_gain, fox_f_bias, nsa_q_gain, nsa_k_gain, nsa_cmp_pe, nsa_phi_w1, nsa_phi_b1, nsa_phi_w2, nsa_phi_b2,
           gmlp_v_gain, gmlp_w_s, gmlp_b_s, w_branch, w_out, ffn2_norm, ffn2_w_in, ffn2_w_out):
    import math
    f32 = np.float32
    A = lambda a: np.ascontiguousarray(np.asarray(a, dtype=f32))
    x = A(x).reshape(B_ * T_, D)
    T = T_
    NT = T // 128
    QL = NT // 4
    ident = _bf(np.eye(128, dtype=f32))
    pc = proj_consts()
    mc = mix_consts()
    mcc = [mix_consts_c(T, c) for c in range(4)]
    ffn_nc = _prog("ffn", lambda: build_ffn(NTOK))
    proj_nc = _prog("proj", lambda: build_proj(NTOK))
    mix_nc = _prog("mix", lambda: build_mix(T, "abc"))
    merge_nc = _prog("merge", lambda: build_merge(NTOK))

    def shard(a):
        return [a[i * NTOK:(i + 1) * NTOK] for i in range(NCORE)]

    def ffn(xx, g, wi, wo):
        xs = shard(xx)
        g, wi, wo = A(g), A(wi), A(wo)
        r = _run(ffn_nc, [dict(x=xs[i], g=g, w_in=wi, w_out=wo, ident=ident) for i in range(NCORE)])
        return np.concatenate([np.asarray(r[i]["y"]) for i in range(NCORE)], 0)

    for l in range(L_):
        x1 = ffn(x, ffn1_norm[l], ffn1_w_in[l], ffn1_w_out[l])
        pp = proj_params(A(mix_norm[l]), A(w_in[l]), A(diff_q_gain[l]), A(diff_k_gain[l]), A(fox_q_gain[l]), A(fox_k_gain[l]),
                         A(nsa_q_gain[l]), A(nsa_k_gain[l]), A(gmlp_v_gain[l]), A(gmlp_w_s[l]), A(gmlp_b_s[l]))
        xs = shard(x1)
        ins = []
        for i in range(NCORE):
            d = dict(pc)
            d.update(pp)
            d["x"] = xs[i]
            ins.append(d)
        r = _run(proj_nc, ins)
        lam_init = 0.8 - 0.6 * math.exp(-0.3 * l)
        lami = np.ascontiguousarray(np.broadcast_to(np.array([lam_init, 1.0 - lam_init], f32)[None], (128, 2)))
        lamp = np.ascontiguousarray(np.broadcast_to(A(diff_lambda[l])[None], (128, 4, 32)))
        pe = A(nsa_cmp_pe[l])
        b1 = A(nsa_phi_b1[l])
        b2 = A(nsa_phi_b2[l])
        nk = A(nsa_k_gain[l])
        cpar = dict(w1=A(nsa_phi_w1[l]), b1=np.ascontiguousarray(b1.reshape(2, 2, 128).transpose(2, 0, 1).reshape(128, 4)),
                    peT=np.ascontiguousarray(np.concatenate([pe[0].T, pe[1].T], 0)), w2=A(nsa_phi_w2[l]), b2=b2,
                    b2c=np.ascontiguousarray(b2[0][:, None]), kgain=np.ascontiguousarray(nk[:, None]))
        ins = []
        for b in range(B_):
            cs = range(4 * b, 4 * b + 4)
            zfm = np.concatenate([np.asarray(r[i]["zfm"]) for i in cs], 2)
            vab = np.concatenate([np.asarray(r[i]["vab"]) for i in cs], 0)
            vsw = np.concatenate([np.asarray(r[i]["vsw"]) for i in cs], 0)
            misc = np.concatenate([np.asarray(r[i]["misc"]) for i in cs], 0)
            qall = np.concatenate([zfm[8], zfm[9]], 0).reshape(4, 64, NT, 128)
            vs_l = _vlay(vsw[:, 0:64])
            vw_l = _vlay(vsw[:, 64:128])
            for c in range(4):
                h = c
                r0 = (h % 2) * 64
                d = dict(mc)
                d.update(mcc[c])
                d.update(cpar)
                qt_idx = np.array([4 * i + c for i in range(QL)])
                qsel = qall[:, :, qt_idx, :].transpose(1, 2, 0, 3).reshape(64, QL, 512)
                d.update(dict(
                    qa=np.ascontiguousarray(zfm[h // 2][r0:r0 + 64]), ka=np.ascontiguousarray(zfm[2 + h // 2][r0:r0 + 64]),
                    va=_vlay(vab[:, h * 64:(h + 1) * 64]),
                    qb=np.ascontiguousarray(zfm[4 + h // 2][r0:r0 + 64]), kb=np.ascontiguousarray(zfm[6 + h // 2][r0:r0 + 64]),
                    vb=_vlay(vab[:, 256 + h * 64:256 + (h + 1) * 64]),
                    lamp=lamp, lami=lami,
                    flog=np.ascontiguousarray(misc[:, h].reshape(NT, 128).T),
                    fbias=np.full((128, 1), A(fox_f_bias[l])[h], f32),
                    qc=np.ascontiguousarray(np.concatenate([qsel, qsel], 0)),
                    kskw=np.ascontiguousarray(zfm[10]), kvin=np.ascontiguousarray(zfm[11]), vs=vs_l, vw=vw_l,
                    ng=np.ascontiguousarray(misc[:, 4:16].reshape(NT, 128, 12)[qt_idx].transpose(1, 0, 2)),
                ))
                ins.append(d)
        rm = _run(mix_nc, ins)
        obr = np.zeros((B_ * T, D), dtype=ident.dtype)
        for b in range(B_):
            for c in range(4):
                i = 4 * b + c
                obr[b * T:(b + 1) * T, c * 64:(c + 1) * 64] = np.asarray(rm[i]["oa"])
                obr[b * T:(b + 1) * T, 256 + c * 64:256 + (c + 1) * 64] = np.asarray(rm[i]["ob"])
                oc = np.asarray(rm[i]["oc"]).reshape(QL, 128, 256)
                v = obr[b * T:(b + 1) * T, 512:768].reshape(NT, 128, 256)
                for k in range(QL):
                    v[4 * k + c] = oc[k]
                obr[i * NTOK:(i + 1) * NTOK, 768:1024] = np.asarray(r[i]["od"])
        obs = shard(obr)
        g, wi, wbr, wo = A(mix_norm[l]), A(w_in[l]), A(w_branch[l]), A(w_out[l])
        r3 = _run(merge_nc, [dict(x=xs[i], g=g, w_in=wi, w_br=wbr, w_o=wo, ident=ident, obr=np.ascontiguousarray(obs[i]))
                             for i in range(NCORE)])
        x2 = np.concatenate([np.asarray(r3[i]["y"]) for i in range(NCORE)], 0)
        x = ffn(x2, ffn2_norm[l], ffn2_w_in[l], ffn2_w_out[l])
    return x.reshape(B_, T_, D).astype(np.float32)
```

```python
import numpy as np
import concourse.bass as bass
import concourse.mybir as mybir
from concourse.bass_utils import run_bass_kernel_spmd

F32 = mybir.dt.float32
BF16 = mybir.dt.bfloat16
AF = mybir.ActivationFunctionType
ALU = mybir.AluOpType
AX = mybir.AxisListType

ENGS = ("pe", "act", "dve", "pool", "sp")


class Buf:
    __slots__ = ("name", "t", "w", "r", "dsem", "dcnt")

    def __init__(self, name, t):
        self.name = name
        self.t = t
        self.w = None
        self.r = []
        self.dsem = None
        self.dcnt = 0

    def __getitem__(self, idx):
        return self.t[idx]


class S:
    def __init__(self, nc, same_engine_sync=True):
        self.nc = nc
        self.e = {"pe": nc.tensor, "act": nc.scalar, "dve": nc.vector, "pool": nc.gpsimd, "sp": nc.sync}
        self.sem = {k: nc.alloc_semaphore("c_" + k) for k in ENGS}
        self.cnt = {k: 0 for k in ENGS}
        self.seen = {k: {} for k in ENGS}
        self.same = same_engine_sync
        self.nbuf = 0
        self.dma_sems = []
        self.ctx = []
        self.cbufs = []
        self.free_dsems = []

    def sb(self, name, shape, dt):
        self.nbuf += 1
        g = self.nc.sbuf_tensor("%s_%d" % (name, self.nbuf), list(shape), dt)
        t = g.__enter__()
        self.ctx.append(g)
        b = Buf(name, t)
        self.cbufs.append(b)
        return b

    def ps(self, name, shape, dt):
        self.nbuf += 1
        g = self.nc.psum_tensor("%s_%d" % (name, self.nbuf), list(shape), dt)
        t = g.__enter__()
        self.ctx.append(g)
        b = Buf(name, t)
        self.cbufs.append(b)
        return b

    def sub(self, name, ap):
        return Buf(name, ap)

    def mark(self):
        return len(self.ctx)

    def release(self, m):
        self.barrier()
        while len(self.ctx) > m:
            self.ctx.pop().__exit__(None, None, None)
            b = self.cbufs.pop()
            if b.dsem is not None:
                self.free_dsems.append((b.dsem, b.dcnt))
                self.dma_sems.remove(b)
                b.dsem = None

    def close(self):
        for g in reversed(self.ctx):
            g.__exit__(None, None, None)
        self.ctx = []
        self.cbufs = []

    def _need(self, E, deps):
        need = {}
        for d in deps:
            if d is None:
                continue
            if d[0] == "dma":
                b = d[1]
                key = ("dma", id(b))
                need[key] = (b, b.dcnt)
            else:
                F, c = d
                if F == E and (not self.same or E == "pe" or c > self.cnt[E]):
                    continue
                if c > need.get(F, (None, 0))[1]:
                    need[F] = (None, c)
        for key, (b, c) in need.items():
            if self.seen[E].get(key, 0) >= c:
                continue
            self.seen[E][key] = c
            if b is not None:
                self.e[E].wait_ge(b.dsem, c)
            else:
                self.e[E].wait_ge(self.sem[key], c)

    def op(self, E, fn, reads=(), writes=(), inc=True):
        deps = []
        for b in reads:
            deps.append(b.w)
        for b in writes:
            deps.append(b.w)
            deps.extend(b.r)
        self._need(E, deps)
        ins = fn()
        c = self.cnt[E] + 1
        if inc:
            ins.then_inc(self.sem[E], 1)
            self.cnt[E] = c
        for b in writes:
            b.w = (E, c)
            b.r = []
        for b in reads:
            if b not in writes:
                b.r = [x for x in b.r if x[0] != E] + [(E, c)]
        return ins

    def dma(self, Q, out, in_, reads=(), writes=(), **kw):
        deps = []
        for b in reads:
            deps.append(b.w)
        for b in writes:
            deps.append(b.w)
            deps.extend(b.r)
        self._need(Q, deps)
        owner = (list(writes) + list(reads))[0]
        if owner.dsem is None:
            owner.dsem = self._dsem(owner)
            self.dma_sems.append(owner)
        ins = self.e[Q].dma_start(out=out, in_=in_, **kw)
        ins.then_inc(owner.dsem, 16)
        owner.dcnt += 16
        rec = ("dma", owner, owner.dcnt)
        for b in writes:
            b.w = rec
            b.r = []
        for b in reads:
            if b not in writes:
                b.r = b.r + [rec]
        return ins

    def cc(self, kind, groups, in_ap, out_ap, reads=(), writes=()):
        deps = []
        for b in reads:
            deps.append(b.w)
        for b in writes:
            deps.append(b.w)
            deps.extend(b.r)
        self._need("pool", deps)
        owner = list(writes)[0]
        if owner.dsem is None:
            owner.dsem = self._dsem(owner)
            self.dma_sems.append(owner)
        ins = self.nc.gpsimd.collective_compute(kind, op=ALU.bypass, replica_groups=groups, ins=[in_ap], outs=[out_ap])
        ins.then_inc(owner.dsem, 16)
        owner.dcnt += 16
        rec = ("dma", owner, owner.dcnt)
        for b in writes:
            b.w = rec
            b.r = []
        for b in reads:
            if b not in writes:
                b.r = b.r + [rec]
        return ins

    def _dsem(self, owner):
        if self.free_dsems:
            sem, cnt = self.free_dsems.pop()
            owner.dcnt = cnt
            return sem
        self.nsem = getattr(self, "nsem", 0) + 1
        return self.nc.alloc_semaphore("d_%d" % self.nsem)

    def barrier(self):
        for E in ENGS:
            for Fk in ENGS:
                if Fk == E:
                    continue
                c = self.cnt[Fk]
                if c and self.seen[E].get(Fk, 0) < c:
                    self.seen[E][Fk] = c
                    self.e[E].wait_ge(self.sem[Fk], c)
            for b in self.dma_sems:
                key = ("dma", id(b))
                if b.dcnt and self.seen[E].get(key, 0) < b.dcnt:
                    self.seen[E][key] = b.dcnt
                    self.e[E].wait_ge(b.dsem, b.dcnt)

    def finish(self):
        self.barrier()


D = 1024
DFF = 2816
NFC = DFF // 128
NKC = D // 128
EPS = 1e-6


def dram_in(nc, name, shape, dt=F32):
    return nc.dram_tensor(name, list(shape), dt, kind="ExternalInput").ap()


def dram_out(nc, name, shape, dt=F32):
    return nc.dram_tensor(name, list(shape), dt, kind="ExternalOutput").ap()


class RR:
    def __init__(self, items):
        self.items = items
        self.i = 0

    def next(self):
        b = self.items[self.i % len(self.items)]
        self.i += 1
        return b


def emit_rmsnorm_T(s, epsc, xt, gcol, hT, ident_b, pT_rr, hb_rr, scr, stat_rr, ntile, evac_engs=("dve", "act")):
    nc = s.nc
    hbs = []
    for j in range(ntile):
        st = stat_rr.next()
        hb = hb_rr.next()
        s.op("act", lambda: nc.scalar.activation(out=scr[:, :], in_=xt[j][:, :], func=AF.Square, scale=1.0 / 32.0,
                                                 accum_out=st[:, 0:1]),
             reads=[xt[j]], writes=[scr, st])
        s.op("act", lambda: nc.scalar.activation(out=st[:, 1:2], in_=st[:, 0:1], func=AF.Sqrt, bias=epsc[:, 0:1], scale=1.0),
             reads=[st, epsc], writes=[st])
        s.op("dve", lambda: nc.vector.reciprocal(out=st[:, 2:3], in_=st[:, 1:2]), reads=[st], writes=[st])
        s.op("act", lambda: nc.scalar.activation(out=hb[:, :], in_=xt[j][:, :], func=AF.Copy, scale=st[:, 2:3]),
             reads=[xt[j], st], writes=[hb])
        hbs.append(hb)
    k = 0
    for c in range(NKC):
        pT = pT_rr.next()
        for j in range(ntile):
            s.op("pe", lambda: nc.tensor.transpose(out=pT[:, j * 128:(j + 1) * 128], in_=hbs[j][:, c * 128:(c + 1) * 128],
                                                   identity=ident_b[:, :]),
                 reads=[hbs[j], ident_b], writes=[pT], inc=(j == ntile - 1))
        eng = evac_engs[k % len(evac_engs)]
        k += 1
        if eng == "dve":
            s.op("dve", lambda: nc.vector.tensor_scalar(out=hT[:, c, 0:ntile * 128], in0=pT[:, 0:ntile * 128],
                                                        scalar1=gcol[:, c:c + 1], scalar2=None, op0=ALU.mult),
                 reads=[pT, gcol], writes=[hT])
        else:
            s.op("act", lambda: nc.scalar.activation(out=hT[:, c, 0:ntile * 128], in_=pT[:, 0:ntile * 128],
                                                     func=AF.Copy, scale=gcol[:, c:c + 1]),
                 reads=[pT, gcol], writes=[hT])


def build_ffn(NT, TB=512):
    nc = bass.Bass("TRN2", target_bir_lowering=False)
    x = dram_in(nc, "x", [NT, D])
    g = dram_in(nc, "g", [D])
    w_in = dram_in(nc, "w_in", [D, 2 * DFF])
    w_out = dram_in(nc, "w_out", [DFF, D])
    ident = dram_in(nc, "ident", [128, 128], BF16)
    y = dram_out(nc, "y", [NT, D])
    s = S(nc)
    emit_ffn(s, x, g, w_in, w_out, ident, y, NT, TB)
    s.finish()
    s.close()
    return nc


def emit_ffn(s, x, g, w_in, w_out, ident, y, NT, TB=512):
    nc = s.nc
    ntile = TB // 128
    m_ = s.mark()
    ident_b = s.sb("ident_b", [128, 128], BF16)
    s.dma("sp", ident_b[:, :], ident[:, :], writes=[ident_b])
    gcol = s.sb("gcol", [128, NKC], F32)
    epsc = s.sb("epsc", [128, 1], F32)
    s.op("dve", lambda: nc.vector.memset(epsc[:, :], EPS), writes=[epsc])
    s.dma("sp", gcol[:, :], g.rearrange("(c p) -> p c", p=128), writes=[gcol], allow_slow_non_contiguous=True)
    win_b = [s.sb("win_b%d" % c, [128, 2 * DFF], BF16) for c in range(NKC)]
    wout_b = [s.sb("wout_b%d" % f, [128, D], BF16) for f in range(NFC)]
    for c in range(NKC):
        for hf in range(2):
            s.dma("pool", win_b[c][:, hf * DFF:(hf + 1) * DFF], w_in[c * 128:(c + 1) * 128, hf * DFF:(hf + 1) * DFF],
                  writes=[win_b[c]])
    for f in range(NFC):
        s.dma("pool", wout_b[f][:, :], w_out[f * 128:(f + 1) * 128, :], writes=[wout_b[f]])
    xt = [s.sb("xt%d" % j, [128, D], F32) for j in range(ntile)]
    hb_rr = RR([s.sb("hb%d" % j, [128, D], BF16) for j in range(ntile)])
    scr = s.sb("scr", [128, D], BF16)
    stat_rr = RR([s.sb("st%d" % j, [128, 4], F32) for j in range(4)])
    hT = s.sb("hT", [128, NKC, TB], BF16)
    pT_rr = RR([s.ps("pT%d" % j, [128, TB], BF16) for j in range(2)])
    pa_rr = RR([s.ps("pa%d" % j, [128, TB], F32) for j in range(2)])
    pb_rr = RR([s.ps("pb%d" % j, [128, TB], F32) for j in range(2)])
    po_rr = RR([s.ps("po%d" % j, [128, 512], F32) for j in range(2)])
    sa_rr = RR([s.sb("sa%d" % j, [128, TB], F32) for j in range(2)])
    act = [s.sb("actT%d" % f, [128, TB], BF16) for f in range(NFC)]
    for tb in range(NT // TB):
        for j in range(ntile):
            r0 = tb * TB + j * 128
            s.dma("sp", xt[j][:, :], x[r0:r0 + 128, :], writes=[xt[j]])
        emit_rmsnorm_T(s, epsc, xt, gcol, hT, ident_b, pT_rr, hb_rr, scr, stat_rr, ntile)
        for f in range(NFC):
            pa = pa_rr.next()
            pb = pb_rr.next()
            for c in range(NKC):
                s.op("pe", lambda: nc.tensor.matmul(pa[:, :], lhsT=win_b[c][:, f * 128:(f + 1) * 128], rhs=hT[:, c, :],
                                                    start=(c == 0), stop=(c == NKC - 1)),
                     reads=[win_b[c], hT], writes=[pa], inc=(c == NKC - 1))
            for c in range(NKC):
                s.op("pe", lambda: nc.tensor.matmul(pb[:, :], lhsT=win_b[c][:, DFF + f * 128:DFF + (f + 1) * 128],
                                                    rhs=hT[:, c, :], start=(c == 0), stop=(c == NKC - 1)),
                     reads=[win_b[c], hT], writes=[pb], inc=(c == NKC - 1))
            sa = sa_rr.next()
            s.op("act", lambda: nc.scalar.activation(out=sa[:, :], in_=pa[:, :], func=AF.Silu), reads=[pa], writes=[sa])
            s.op("dve", lambda: nc.vector.tensor_tensor(out=act[f][:, :], in0=sa[:, :], in1=pb[:, :], op=ALU.mult),
                 reads=[sa, pb], writes=[act[f]])
        for j in range(ntile):
            for hf in range(2):
                po = po_rr.next()
                for f in range(NFC):
                    s.op("pe", lambda: nc.tensor.matmul(po[:, :], lhsT=act[f][:, j * 128:(j + 1) * 128],
                                                        rhs=wout_b[f][:, hf * 512:(hf + 1) * 512],
                                                        start=(f == 0), stop=(f == NFC - 1)),
                         reads=[act[f], wout_b[f]], writes=[po], inc=(f == NFC - 1))
                s.op("dve", lambda: nc.vector.scalar_tensor_tensor(out=xt[j][:, hf * 512:(hf + 1) * 512], in0=po[:, :],
                                                                   scalar=0.5, in1=xt[j][:, hf * 512:(hf + 1) * 512],
                                                                   op0=ALU.mult, op1=ALU.add),
                     reads=[po, xt[j]], writes=[xt[j]])
            r0 = tb * TB + j * 128
            s.dma("sp", y[r0:r0 + 128, :], xt[j][:, :], reads=[xt[j]])
    s.release(m_)


FM_SRC = [[(0, 128)], [(128, 128)], [(256, 128)], [(384, 128)],
          [(768, 128)], [(896, 128)], [(1024, 128)], [(1152, 128)],
          [(1540, 128)], [(1668, 128)], [(1924, 64), (2052, 64)], [(1796, 128)]]
FM_GCOL = [0, 0, 1, 1, 2, 2, 3, 3, 4, 4, 5, None]
FM_BLK = [0, 0, 0, 0, 1, 1, 1, 1, 1, 1, 1, None]
TM_SRC = [[(512, 256), (1280, 256)],
          [(1536, 4), (2180, 12), (1988, 64), (2116, 64)],
          [(2192, 512)]]
NFM = 12
GELU_C = 1.5957691216057308


def build_proj(NT, TB=512):
    nc = bass.Bass("TRN2", target_bir_lowering=False)
    a = dict(
        x=dram_in(nc, "x", [NT, D]), g=dram_in(nc, "g", [D]), w_in=dram_in(nc, "w_in", [D, 6800]),
        ident=dram_in(nc, "ident", [128, 128], BF16), gains=dram_in(nc, "gains", [128, 6]),
        blk=dram_in(nc, "blk", [2, 128, 128], BF16), vgain=dram_in(nc, "vgain", [128, 256]),
        wsT=dram_in(nc, "wsT", [4, 128, 128]), triu=dram_in(nc, "triu", [128, 128]), bsT=dram_in(nc, "bsT", [128, 4]),
        zfm=dram_out(nc, "zfm", [NFM, 128, NT], BF16), vab=dram_out(nc, "vab", [NT, 512], BF16),
        vsw=dram_out(nc, "vsw", [NT, 128], BF16), misc=dram_out(nc, "misc", [NT, 16]), od=dram_out(nc, "od", [NT, 256], BF16))
    s = S(nc)
    emit_proj(s, a, NT, TB)
    s.finish()
    s.close()
    return nc


def emit_proj(s, a, NT, TB=512):
    nc = s.nc
    x, g, w_in, ident, gains, blk, vgain, wsT, triu, bsT = (a[k] for k in
                                                            ("x", "g", "w_in", "ident", "gains", "blk", "vgain", "wsT", "triu", "bsT"))
    zfm, vab, vsw, misc, od = (a[k] for k in ("zfm", "vab", "vsw", "misc", "od"))
    m_ = s.mark()
    ntile = TB // 128
    ident_b = s.sb("ident_b", [128, 128], BF16)
    s.dma("sp", ident_b[:, :], ident[:, :], writes=[ident_b])
    gcol = s.sb("gcol", [128, NKC], F32)
    s.dma("sp", gcol[:, :], g.rearrange("(c p) -> p c", p=128), writes=[gcol], allow_slow_non_contiguous=True)
    epsc = s.sb("epsc", [128, 1], F32)
    s.op("dve", lambda: nc.vector.memset(epsc[:, :], EPS), writes=[epsc])
    gn = s.sb("gn", [128, 6], F32)
    s.dma("sp", gn[:, :], gains[:, :], writes=[gn])
    for col, sc in ((0, 32.0 ** -0.5), (2, 0.125), (4, 0.125)):
        s.op("dve", lambda: nc.vector.tensor_scalar(out=gn[:, col:col + 1], in0=gn[:, col:col + 1], scalar1=sc,
                                                    scalar2=None, op0=ALU.mult), reads=[gn], writes=[gn])
    blk_b = [s.sb("blk%d" % i, [128, 128], BF16) for i in range(2)]
    for i in range(2):
        s.dma("sp", blk_b[i][:, :], blk[i], writes=[blk_b[i]])
    vg = s.sb("vg", [128, 256], F32)
    s.dma("sp", vg[:, :], vgain[:, :], writes=[vg])
    bcol = s.sb("bcol", [128, 4], F32)
    s.dma("sp", bcol[:, :], bsT[:, :], writes=[bcol])
    tri = s.sb("tri", [128, 128], F32)
    s.dma("sp", tri[:, :], triu[:, :], writes=[tri])
    wm = []
    wtmp = s.sb("wtmp", [128, 128], F32)
    for gi in range(4):
        w = s.sb("wm%d" % gi, [128, 128], BF16)
        s.dma("sp", wtmp[:, :], wsT[gi], writes=[wtmp])
        s.op("dve", lambda: nc.vector.tensor_tensor(out=w[:, :], in0=wtmp[:, :], in1=tri[:, :], op=ALU.mult),
             reads=[wtmp, tri], writes=[w])
        wm.append(w)
    wfm = [s.sb("wfm%d" % c, [128, NFM * 128], BF16) for c in range(NKC)]
    wtm = [s.sb("wtm%d" % c, [128, 1168], BF16) for c in range(NKC)]
    for c in range(NKC):
        for i, srcs in enumerate(FM_SRC):
            o = i * 128
            for (c0, n) in srcs:
                s.dma("pool", wfm[c][:, o:o + n], w_in[c * 128:(c + 1) * 128, c0:c0 + n], writes=[wfm[c]])
                o += n
        o = 0
        for srcs in TM_SRC:
            for (c0, n) in srcs:
                s.dma("pool", wtm[c][:, o:o + n], w_in[c * 128:(c + 1) * 128, c0:c0 + n], writes=[wtm[c]])
                o += n
    xt = [s.sb("xt%d" % j, [128, D], F32) for j in range(ntile)]
    hb_rr = RR([s.sb("hb%d" % j, [128, D], BF16) for j in range(ntile)])
    scr = s.sb("scr", [128, D], BF16)
    stat_rr = RR([s.sb("st%d" % j, [128, 4], F32) for j in range(4)])
    hT = s.sb("hT", [128, NKC, TB], BF16)
    pT_rr = RR([s.ps("pT%d" % j, [128, TB], BF16) for j in range(2)])
    pz_rr = RR([s.ps("pz%d" % j, [128, 512], F32) for j in range(3)])
    pq_rr = RR([s.ps("pq%d" % j, [128, 512], F32) for j in range(2)])
    sq_rr = RR([s.sb("sq%d" % j, [128, TB], BF16) for j in range(2)])
    rs_rr = RR([s.sb("rs%d" % j, [128, TB], F32) for j in range(2)])
    zo_rr = RR([s.sb("zo%d" % j, [128, TB], BF16) for j in range(3)])
    vab_rr = RR([s.sb("vabt%d" % j, [128, 512], BF16) for j in range(2)])
    vsw_rr = RR([s.sb("vswt%d" % j, [128, 128], BF16) for j in range(2)])
    msc_rr = RR([s.sb("msct%d" % j, [128, 16], F32) for j in range(2)])
    f_rr = RR([s.sb("gf%d" % j, [128, 512], F32) for j in range(4)])
    ge_rr = RR([s.sb("ge%d" % j, [128, 512], F32) for j in range(2)])
    vn_rr = RR([s.sb("vn%d" % j, [128, 256], BF16) for j in range(2)])
    od_rr = RR([s.sb("odt%d" % j, [128, 256], BF16) for j in range(2)])
    for tb in range(NT // TB):
        t0 = tb * TB
        for j in range(ntile):
            s.dma("sp", xt[j][:, :], x[t0 + j * 128:t0 + (j + 1) * 128, :], writes=[xt[j]])
        emit_rmsnorm_T(s, epsc, xt, gcol, hT, ident_b, pT_rr, hb_rr, scr, stat_rr, ntile)
        for i in range(NFM):
            pz = pz_rr.next()
            for c in range(NKC):
                s.op("pe", lambda: nc.tensor.matmul(pz[:, 0:TB], lhsT=wfm[c][:, i * 128:(i + 1) * 128], rhs=hT[:, c, :],
                                                    start=(c == 0), stop=(c == NKC - 1)),
                     reads=[wfm[c], hT], writes=[pz], inc=(c == NKC - 1))
            zo = zo_rr.next()
            if FM_GCOL[i] is None:
                s.op("act", lambda: nc.scalar.copy(out=zo[:, :], in_=pz[:, 0:TB]), reads=[pz], writes=[zo])
            else:
                gs = 32.0 if FM_BLK[i] == 0 else 64.0
                sq = sq_rr.next()
                s.op("act", lambda: nc.scalar.activation(out=sq[:, :], in_=pz[:, 0:TB], func=AF.Square),
                     reads=[pz], writes=[sq])
                pq = pq_rr.next()
                s.op("pe", lambda: nc.tensor.matmul(pq[:, 0:TB], lhsT=blk_b[FM_BLK[i]][:, :], rhs=sq[:, :],
                                                    start=True, stop=True), reads=[blk_b[FM_BLK[i]], sq], writes=[pq])
                rs = rs_rr.next()
                s.op("act", lambda: nc.scalar.activation(out=rs[:, :], in_=pq[:, 0:TB], func=AF.Sqrt, bias=epsc[:, 0:1],
                                                         scale=1.0 / gs), reads=[pq, epsc], writes=[rs])
                s.op("dve", lambda: nc.vector.reciprocal(out=rs[:, :], in_=rs[:, :]), reads=[rs], writes=[rs])
                gc = FM_GCOL[i]
                s.op("dve", lambda: nc.vector.scalar_tensor_tensor(out=zo[:, :], in0=pz[:, 0:TB], scalar=gn[:, gc:gc + 1],
                                                                   in1=rs[:, :], op0=ALU.mult, op1=ALU.mult),
                     reads=[pz, gn, rs], writes=[zo])
            s.dma("sp", zfm[i, :, t0:t0 + TB], zo[:, :], reads=[zo])
        for j in range(ntile):
            r0 = t0 + j * 128
            pz = pz_rr.next()
            for c in range(NKC):
                s.op("pe", lambda: nc.tensor.matmul(pz[:, :], lhsT=hT[:, c, j * 128:(j + 1) * 128], rhs=wtm[c][:, 0:512],
                                                    start=(c == 0), stop=(c == NKC - 1)),
                     reads=[wtm[c], hT], writes=[pz], inc=(c == NKC - 1))
            vt = vab_rr.next()
            s.op("act", lambda: nc.scalar.copy(out=vt[:, :], in_=pz[:, :]), reads=[pz], writes=[vt])
            s.dma("sp", vab[r0:r0 + 128, :], vt[:, :], reads=[vt])
            pz = pz_rr.next()
            for c in range(NKC):
                s.op("pe", lambda: nc.tensor.matmul(pz[:, 0:144], lhsT=hT[:, c, j * 128:(j + 1) * 128], rhs=wtm[c][:, 512:656],
                                                    start=(c == 0), stop=(c == NKC - 1)),
                     reads=[wtm[c], hT], writes=[pz], inc=(c == NKC - 1))
            mt = msc_rr.next()
            vs_ = vsw_rr.next()
            s.op("dve", lambda: nc.vector.tensor_copy(out=mt[:, :], in_=pz[:, 0:16]), reads=[pz], writes=[mt])
            s.op("dve", lambda: nc.vector.tensor_copy(out=vs_[:, :], in_=pz[:, 16:144]), reads=[pz], writes=[vs_])
            s.dma("sp", misc[r0:r0 + 128, :], mt[:, :], reads=[mt])
            s.dma("sp", vsw[r0:r0 + 128, :], vs_[:, :], reads=[vs_])
            pz = pz_rr.next()
            for c in range(NKC):
                s.op("pe", lambda: nc.tensor.matmul(pz[:, :], lhsT=hT[:, c, j * 128:(j + 1) * 128], rhs=wtm[c][:, 656:1168],
                                                    start=(c == 0), stop=(c == NKC - 1)),
                     reads=[wtm[c], hT], writes=[pz], inc=(c == NKC - 1))
            z2 = f_rr.next()
            s.op("act", lambda: nc.scalar.activation(out=z2[:, :], in_=pz[:, :], func=AF.Square), reads=[pz], writes=[z2])
            s.op("dve", lambda: nc.vector.tensor_scalar(out=z2[:, :], in0=z2[:, :], scalar1=0.044715, scalar2=1.0,
                                                        op0=ALU.mult, op1=ALU.add), reads=[z2], writes=[z2])
            s.op("dve", lambda: nc.vector.tensor_tensor(out=z2[:, :], in0=z2[:, :], in1=pz[:, :], op=ALU.mult),
                 reads=[z2, pz], writes=[z2])
            s.op("act", lambda: nc.scalar.activation(out=z2[:, :], in_=z2[:, :], func=AF.Sigmoid, scale=GELU_C),
                 reads=[z2], writes=[z2])
            ge = ge_rr.next()
            s.op("dve", lambda: nc.vector.tensor_tensor(out=ge[:, :], in0=z2[:, :], in1=pz[:, :], op=ALU.mult),
                 reads=[z2, pz], writes=[ge])
            sqv = f_rr.next()
            st = stat_rr.next()
            s.op("act", lambda: nc.scalar.activation(out=sqv[:, 0:256], in_=ge[:, 256:512], func=AF.Square),
                 reads=[ge], writes=[sqv])
            s.op("dve", lambda: nc.vector.tensor_reduce(out=st[:, 0:4], in_=sqv[:, 0:256].rearrange("p (g d) -> p g d", g=4),
                                                        axis=AX.X, op=ALU.add), reads=[sqv], writes=[st])
            s.op("act", lambda: nc.scalar.activation(out=st[:, 0:4], in_=st[:, 0:4], func=AF.Sqrt, bias=epsc[:, 0:1],
                                                     scale=1.0 / 64.0), reads=[st, epsc], writes=[st])
            s.op("dve", lambda: nc.vector.reciprocal(out=st[:, 0:4], in_=st[:, 0:4]), reads=[st], writes=[st])
            vn = vn_rr.next()
            for gi in range(4):
                s.op("dve", lambda: nc.vector.scalar_tensor_tensor(
                    out=vn[:, gi * 64:(gi + 1) * 64], in0=ge[:, 256 + gi * 64:256 + (gi + 1) * 64], scalar=st[:, gi:gi + 1],
                    in1=vg[:, gi * 64:(gi + 1) * 64], op0=ALU.mult, op1=ALU.mult), reads=[ge, st, vg], writes=[vn])
            pq = pq_rr.next()
            for gi in range(4):
                s.op("pe", lambda: nc.tensor.matmul(pq[:, gi * 64:(gi + 1) * 64], lhsT=wm[gi][:, :],
                                                    rhs=vn[:, gi * 64:(gi + 1) * 64], start=True, stop=True),
                     reads=[wm[gi], vn], writes=[pq], inc=(gi == 3))
            ot = od_rr.next()
            for gi in range(4):
                s.op("dve", lambda: nc.vector.scalar_tensor_tensor(
                    out=ot[:, gi * 64:(gi + 1) * 64], in0=pq[:, gi * 64:(gi + 1) * 64], scalar=bcol[:, gi:gi + 1],
                    in1=ge[:, gi * 64:(gi + 1) * 64], op0=ALU.add, op1=ALU.mult), reads=[pq, bcol, ge], writes=[ot])
            s.dma("sp", od[r0:r0 + 128, :], ot[:, :], reads=[ot])
    s.release(m_)


def _bf(a):
    import ml_dtypes
    return np.ascontiguousarray(a).astype(ml_dtypes.bfloat16)


def proj_consts():
    blk = np.zeros((2, 128, 128), np.float32)
    for i in range(128):
        for j in range(128):
            if i // 32 == j // 32:
                blk[0, i, j] = 1
            if i // 64 == j // 64:
                blk[1, i, j] = 1
    triu = np.triu(np.ones((128, 128), np.float32))
    return dict(ident=_bf(np.eye(128, dtype=np.float32)), blk=_bf(blk), triu=triu)


def proj_params(g, w_in, dq, dk, fq, fk, nq, nk, vgain, w_s, b_s):
    gains = np.stack([np.tile(dq, 4), np.tile(dk, 4), np.tile(fq, 2), np.tile(fk, 2), np.tile(nq, 2), np.tile(nk, 2)], 1)
    return dict(g=np.ascontiguousarray(g), w_in=np.ascontiguousarray(w_in), gains=np.ascontiguousarray(gains, dtype=np.float32),
                vgain=np.ascontiguousarray(np.broadcast_to(vgain[None, :], (128, 256))),
                wsT=np.ascontiguousarray(w_s.transpose(0, 2, 1)), bsT=np.ascontiguousarray(b_s.T))


NEG = -30000.0


def load_vt(s, vt, io, key, T):
    nc = s.nc
    NT = T // 128
    if key + "_src" in io:
        src = io[key + "_src"]
        s.op("pool", lambda: nc.gpsimd.memset(vt[:, :, 64:65], 1.0), writes=[vt])
        step = 8
        for j0 in range(0, NT, step):
            j1 = min(NT, j0 + step)
            s.dma("sp", vt[:, j0:j1, 0:64], src[j0 * 128:j1 * 128, :].rearrange("(j p) d -> p j d", p=128), writes=[vt])
    else:
        s.dma("sp", vt[:, :, :], io[key][:, :, :], writes=[vt])


def emit_attn_phase(s, cm, T, nsub, qT, kT, vt, kparts, out_dram, finalize, bias_fn=None, name="a"):
    nc = s.nc
    NQB = T // 512
    for qb in range(NQB):
        q0 = qb * 512
        pos = [cm["po_rr"].next() for _ in range(nsub)]
        nt = 4 * qb + 4
        for t in range(nt):
            di = t - 4 * qb
            c0 = 128 * di if di > 0 else 0
            for i in range(nsub):
                p0, p1 = kparts[i]
                ps = cm["ps_rr"].next()
                s.op("pe", lambda: nc.tensor.matmul(ps[:, c0:512], lhsT=kT[p0:p1, t * 128:(t + 1) * 128],
                                                    rhs=qT[p0:p1, q0 + c0:q0 + 512], start=True, stop=(di < 0)),
                     reads=[kT, qT], writes=[ps], inc=(di < 0))
                if di >= 0:
                    s.op("pe", lambda: nc.tensor.matmul(ps[:, c0:c0 + 128], lhsT=cm["ident_b"][:, :], rhs=cm["tri_b"][:, :],
                                                        start=False, stop=True),
                         reads=[cm["ident_b"], cm["tri_b"]], writes=[ps])
                pt = cm["pt_rr"].next()
                if bias_fn is None:
                    s.op("act", lambda: nc.scalar.activation(out=pt[:, c0:512], in_=ps[:, c0:512], func=AF.Exp),
                         reads=[ps], writes=[pt])
                else:
                    bb, bap = bias_fn(qb, t)
                    s.op("act", lambda: nc.scalar.activation(out=pt[:, c0:512], in_=ps[:, c0:512], func=AF.Exp, bias=bap),
                         reads=[ps, bb], writes=[pt])
                s.op("pe", lambda: nc.tensor.matmul(pos[i][0:65, c0:512], lhsT=vt[:, t, :], rhs=pt[:, c0:512],
                                                    start=(t == 0), stop=(t == nt - 1)),
                     reads=[vt, pt], writes=[pos[i]])
        finalize(qb, pos)


def emit_o_to_tokmajor(s, cm, po, pf, col0):
    nc = s.nc
    oc = cm["oc_rr"].next()
    s.op("act", lambda: nc.scalar.copy(out=oc[0:65, :], in_=po[0:65, :]), reads=[po], writes=[oc])
    for j in range(4):
        s.op("pe", lambda: nc.tensor.transpose(out=pf[:, j, col0:col0 + 65], in_=oc[0:65, j * 128:(j + 1) * 128],
                                               identity=cm["ident_f"][0:65, 0:65]),
             reads=[oc, cm["ident_f"]], writes=[pf], inc=(j == 3))


def build_mix_ab(T):
    nc = bass.Bass("TRN2", target_bir_lowering=False)
    io = mix_decl(nc, T, with_c=False)
    s = S(nc)
    cm = mix_common(s, io)
    emit_mix_a(s, cm, io, T)
    emit_mix_b(s, cm, io, T)
    s.finish()
    s.close()
    return nc


def mix_decl(nc, T, with_c=True):
    NT = T // 128
    io = dict(
        identb=dram_in(nc, "identb", [128, 128], BF16), identf=dram_in(nc, "identf", [128, 128]),
        trib=dram_in(nc, "trib", [128, 128], BF16),
        qa=dram_in(nc, "qa", [64, T], BF16), ka=dram_in(nc, "ka", [64, T], BF16), va=dram_in(nc, "va", [128, NT, 65], BF16),
        lamp=dram_in(nc, "lamp", [128, 4, 32]), lami=dram_in(nc, "lami", [128, 2]),
        qb=dram_in(nc, "qb", [64, T], BF16), kb=dram_in(nc, "kb", [64, T], BF16), vb=dram_in(nc, "vb", [128, NT, 65], BF16),
        flog=dram_in(nc, "flog", [128, NT]), fbias=dram_in(nc, "fbias", [128, 1]),
        triuf=dram_in(nc, "triuf", [128, 128]), onesf=dram_in(nc, "onesf", [128, 128]),
        oa=dram_out(nc, "oa", [T, 64], BF16), ob=dram_out(nc, "ob", [T, 64], BF16),
    )
    return io


def mix_common(s, io):
    nc = s.nc
    cm = {}
    for nm, key, dt in (("ident_b", "identb", BF16), ("ident_f", "identf", F32), ("tri_b", "trib", BF16)):
        b = s.sb(nm, [128, 128], dt)
        s.dma("sp", b[:, :], io[key][:, :], writes=[b])
        cm[nm] = b
    cm["epsc"] = s.sb("epsc", [128, 1], F32)
    s.op("dve", lambda: nc.vector.memset(cm["epsc"][:, :], EPS), writes=[cm["epsc"]])
    cm["ps_rr"] = RR([s.ps("ps%d" % j, [128, 512], F32) for j in range(3)])
    cm["po_rr"] = RR([s.ps("po%d" % j, [128, 512], F32) for j in range(3)])
    cm["pf"] = s.ps("pf", [128, 4, 128], F32)
    cm["pf2"] = s.ps("pf2", [128, 4, 128], F32)
    cm["pt_rr"] = RR([s.sb("pt%d" % j, [128, 512], BF16) for j in range(4)])
    cm["oc_rr"] = RR([s.sb("oc%d" % j, [128, 512], F32) for j in range(2)])
    cm["st_rr"] = RR([s.sb("mst%d" % j, [128, 8], F32) for j in range(8)])
    cm["ot_rr"] = RR([s.sb("ot%d" % j, [128, 4, 64], BF16) for j in range(2)])
    cm["tmp_rr"] = RR([s.sb("tmp%d" % j, [128, 64], F32) for j in range(4)])
    return cm


def emit_mix_a(s, cm, io, T):
    nc = s.nc
    NT = T // 128
    m = s.mark()
    qT = s.sb("a_q", [64, T], BF16)
    kT = s.sb("a_k", [64, T], BF16)
    vt = s.sb("a_v", [128, NT, 65], BF16)
    s.dma("sp", qT[:, :], io["qa"][:, :], writes=[qT])
    s.dma("sp", kT[:, :], io["ka"][:, :], writes=[kT])
    load_vt(s, vt, io, "va", T)
    lp = s.sb("lp", [128, 4, 32], F32)
    li = s.sb("li", [128, 2], F32)
    lw = s.sb("lw", [128, 2, 32], F32)
    lam = s.sb("lam", [128, 4], F32)
    s.dma("sp", lp[:, :, :], io["lamp"][:, :, :], writes=[lp])
    s.dma("sp", li[:, :], io["lami"][:, :], writes=[li])
    s.op("dve", lambda: nc.vector.tensor_tensor(out=lw[:, 0, :], in0=lp[:, 0, :], in1=lp[:, 1, :], op=ALU.mult),
         reads=[lp], writes=[lw])
    s.op("dve", lambda: nc.vector.tensor_tensor(out=lw[:, 1, :], in0=lp[:, 2, :], in1=lp[:, 3, :], op=ALU.mult),
         reads=[lp], writes=[lw])
    s.op("dve", lambda: nc.vector.tensor_reduce(out=lam[:, 0:2], in_=lw[:, :, :], axis=AX.X, op=ALU.add),
         reads=[lw], writes=[lam])
    s.op("act", lambda: nc.scalar.activation(out=lam[:, 0:2], in_=lam[:, 0:2], func=AF.Exp), reads=[lam], writes=[lam])
    s.op("dve", lambda: nc.vector.tensor_tensor(out=lam[:, 2:3], in0=lam[:, 1:2], in1=lam[:, 0:1], op=ALU.subtract),
         reads=[lam], writes=[lam])
    s.op("dve", lambda: nc.vector.tensor_tensor(out=lam[:, 3:4], in0=lam[:, 2:3], in1=li[:, 0:1], op=ALU.subtract),
         reads=[lam, li], writes=[lam])

    def fin(qb, pos):
        pf = cm["pf"]
        pf2 = cm["pf2"]
        emit_o_to_tokmajor(s, cm, pos[0], pf, 0)
        emit_o_to_tokmajor(s, cm, pos[1], pf2, 0)
        ot = cm["ot_rr"].next()
        for j in range(4):
            st = cm["st_rr"].next()
            s.op("dve", lambda: nc.vector.tensor_scalar(out=st[:, 0:1], in0=pf[:, j, 64:65], scalar1=1e-30, scalar2=None,
                                                        op0=ALU.max), reads=[pf], writes=[st])
            s.op("dve", lambda: nc.vector.tensor_scalar(out=st[:, 1:2], in0=pf2[:, j, 64:65], scalar1=1e-30, scalar2=None,
                                                        op0=ALU.max), reads=[pf2], writes=[st])
            s.op("dve", lambda: nc.vector.reciprocal(out=st[:, 0:2], in_=st[:, 0:2]), reads=[st], writes=[st])
            t2 = cm["tmp_rr"].next()
            o = cm["tmp_rr"].next()
            s.op("dve", lambda: nc.vector.tensor_scalar(out=t2[:, :], in0=pf2[:, j, 0:64], scalar1=st[:, 1:2],
                                                        scalar2=lam[:, 3:4], op0=ALU.mult, op1=ALU.mult),
                 reads=[pf2, st, lam], writes=[t2])
            s.op("dve", lambda: nc.vector.scalar_tensor_tensor(out=o[:, :], in0=pf[:, j, 0:64], scalar=st[:, 0:1], in1=t2[:, :],
                                                               op0=ALU.mult, op1=ALU.add), reads=[pf, st, t2], writes=[o])
            s.op("act", lambda: nc.scalar.activation(out=t2[:, :], in_=o[:, :], func=AF.Square, accum_out=st[:, 2:3]),
                 reads=[o], writes=[t2, st])
            s.op("act", lambda: nc.scalar.activation(out=st[:, 3:4], in_=st[:, 2:3], func=AF.Ln, bias=cm["epsc"][:, 0:1],
                                                     scale=1.0 / 64.0), reads=[st, cm["epsc"]], writes=[st])
            s.op("act", lambda: nc.scalar.activation(out=st[:, 4:5], in_=st[:, 3:4], func=AF.Exp, scale=-0.5),
                 reads=[st], writes=[st])
            s.op("dve", lambda: nc.vector.tensor_scalar(out=ot[:, j, :], in0=o[:, :], scalar1=st[:, 4:5], scalar2=li[:, 1:2],
                                                        op0=ALU.mult, op1=ALU.mult), reads=[o, st, li], writes=[ot])
        s.dma("sp", io["oa"][qb * 512:(qb + 1) * 512, :].rearrange("(j p) d -> p j d", p=128), ot[:, :, :], reads=[ot])

    emit_attn_phase(s, cm, T, 2, qT, kT, vt, [(0, 32), (32, 64)], io["oa"], fin, name="a")
    s.release(m)


def emit_mix_b(s, cm, io, T):
    nc = s.nc
    NT = T // 128
    NQB = T // 512
    m = s.mark()
    qT = s.sb("b_q", [64, T], BF16)
    kT = s.sb("b_k", [64, T], BF16)
    vt = s.sb("b_v", [128, NT, 65], BF16)
    s.dma("sp", qT[:, :], io["qb"][:, :], writes=[qT])
    s.dma("sp", kT[:, :], io["kb"][:, :], writes=[kT])
    load_vt(s, vt, io, "vb", T)
    fl = s.sb("fl", [128, NT], F32)
    fb = s.sb("fb", [128, 2], F32)
    tu = s.sb("tu", [128, 128], F32)
    on = s.sb("on", [128, 128], F32)
    if "flog_sb" not in io:
        s.dma("sp", fl[:, :], io["flog"][:, :], writes=[fl])
    s.dma("sp", fb[:, 0:1], io["fbias"][:, :], writes=[fb])
    s.dma("sp", tu[:, :], io["triuf"][:, :], writes=[tu])
    s.dma("sp", on[:, :], io["onesf"][:, :], writes=[on])
    s.op("dve", lambda: nc.vector.tensor_scalar(out=fb[:, 1:2], in0=fb[:, 0:1], scalar1=-1.0, scalar2=None, op0=ALU.mult),
         reads=[fb], writes=[fb])
    if "flog_sb" in io:
        fsb, fap = io["flog_sb"]
        s.op("act", lambda: nc.scalar.activation(out=fl[:, :], in_=fap, func=AF.Exp, bias=fb[:, 1:2], scale=-1.0),
             reads=[fsb, fb], writes=[fl])
    else:
        s.op("act", lambda: nc.scalar.activation(out=fl[:, :], in_=fl[:, :], func=AF.Exp, bias=fb[:, 1:2], scale=-1.0),
             reads=[fl, fb], writes=[fl])
    s.op("act", lambda: nc.scalar.activation(out=fl[:, :], in_=fl[:, :], func=AF.Ln, bias=1.0, scale=1.0),
         reads=[fl], writes=[fl])
    pc = cm["pf"]
    pcv = pc[:, 0, :]
    s.op("pe", lambda: nc.tensor.matmul(pc[:, 0, 0:NT], lhsT=tu[:, :], rhs=fl[:, :], start=True, stop=True),
         reads=[tu, fl], writes=[pc])
    s.op("pe", lambda: nc.tensor.matmul(pc[:, 1, 0:NT], lhsT=on[:, :], rhs=fl[:, :], start=True, stop=True),
         reads=[on, fl], writes=[pc])
    cc = s.sb("cc", [128, NT], F32)
    inc_ = s.sb("inc", [128, NT], F32)
    tmpc = s.sb("tmpc", [128, NT], F32)
    s.op("dve", lambda: nc.vector.tensor_copy(out=inc_[:, :], in_=pc[:, 1, 0:NT]), reads=[pc], writes=[inc_])
    sh = 1
    while sh < NT:
        s.op("dve", lambda: nc.vector.tensor_copy(out=tmpc[:, :], in_=inc_[:, :]), reads=[inc_], writes=[tmpc])
        s.op("dve", lambda: nc.vector.tensor_tensor(out=inc_[:, sh:NT], in0=tmpc[:, sh:NT], in1=tmpc[:, 0:NT - sh], op=ALU.add),
             reads=[tmpc], writes=[inc_])
        sh *= 2
    s.op("dve", lambda: nc.vector.tensor_tensor(out=cc[:, :], in0=pc[:, 0, 0:NT], in1=inc_[:, :], op=ALU.add),
         reads=[pc, inc_], writes=[cc])
    s.op("dve", lambda: nc.vector.tensor_tensor(out=tmpc[:, :], in0=cc[:, :], in1=pc[:, 1, 0:NT], op=ALU.subtract),
         reads=[pc, cc], writes=[tmpc])
    btab = s.sb("btab", [128, NQB, NT], F32)
    for qb in range(NQB):
        s.op("dve", lambda: nc.vector.tensor_scalar(out=btab[:, qb, :], in0=tmpc[:, :], scalar1=inc_[:, 4 * qb + 1:4 * qb + 2],
                                                    scalar2=None, op0=ALU.subtract), reads=[tmpc, inc_], writes=[btab])

    def bias_fn(qb, t):
        return btab, btab[:, qb, t:t + 1]

    def fin(qb, pos):
        pf = cm["pf"]
        emit_o_to_tokmajor(s, cm, pos[0], pf, 0)
        ot = cm["ot_rr"].next()
        for j in range(4):
            st = cm["st_rr"].next()
            s.op("dve", lambda: nc.vector.tensor_scalar(out=st[:, 0:1], in0=pf[:, j, 64:65], scalar1=1e-30, scalar2=None,
                                                        op0=ALU.max), reads=[pf], writes=[st])
            s.op("dve", lambda: nc.vector.reciprocal(out=st[:, 0:1], in_=st[:, 0:1]), reads=[st], writes=[st])
            s.op("dve", lambda: nc.vector.tensor_scalar(out=ot[:, j, :], in0=pf[:, j, 0:64], scalar1=st[:, 0:1], scalar2=None,
                                                        op0=ALU.mult), reads=[pf, st], writes=[ot])
        s.dma("sp", io["ob"][qb * 512:(qb + 1) * 512, :].rearrange("(j p) d -> p j d", p=128), ot[:, :, :], reads=[ot])

    emit_attn_phase(s, cm, T, 1, qT, kT, vt, [(0, 64)], io["ob"], fin, bias_fn=bias_fn, name="b")
    s.release(m)


def mix_consts():
    k = np.arange(128)
    tri = np.where(k[:, None] > k[None, :], NEG, 0.0).astype(np.float32)
    return dict(identb=_bf(np.eye(128, dtype=np.float32)), identf=np.eye(128, dtype=np.float32), trib=_bf(tri),
                triuf=np.triu(np.ones((128, 128), np.float32)), onesf=np.ones((128, 128), np.float32))


def mix_decl_c(nc, io, T):
    NT = T // 128
    QL = NT // 4
    NCT = max(1, T // 2048)
    io.update(dict(
        qc=dram_in(nc, "qc", [128, QL, 512], BF16),
        kskw=dram_in(nc, "kskw", [128, T], BF16),
        vs=dram_in(nc, "vs", [128, NT, 65], BF16), vw=dram_in(nc, "vw", [128, NT, 65], BF16),
        kvin=dram_in(nc, "kvin", [128, T], BF16),
        w1=dram_in(nc, "w1", [2, 2048, 256]), b1=dram_in(nc, "b1", [128, 4]),
        peT=dram_in(nc, "peT", [128, 32]),
        w2=dram_in(nc, "w2", [2, 256, 64]), b2=dram_in(nc, "b2", [2, 64]), b2c=dram_in(nc, "b2c", [64, 1]),
        kgain=dram_in(nc, "kgain", [64, 1]),
        ng=dram_in(nc, "ng", [128, QL, 12]),
        cmask=dram_in(nc, "cmask", [128, QL, NCT, 128], BF16),
        smask=dram_in(nc, "smask", [128, 4, 128], BF16), wmask=dram_in(nc, "wmask", [128, 8, 128], BF16),
        impA=dram_in(nc, "impA", [128, QL, 128]), impB=dram_in(nc, "impB", [128, QL, 128]),
        emat=dram_in(nc, "emat", [128, NT, 128], BF16), ovl=dram_in(nc, "ovl", [128, NCT, 128], BF16),
        ones64=dram_in(nc, "ones64", [64, 64], BF16), onesrow=dram_in(nc, "onesrow", [1, 128], BF16),
        oc=dram_out(nc, "oc", [QL * 128, 256], BF16),
    ))
    return io


def emit_gelu(s, zin_ap, zin_b, out_ap, out_b, tmp, shape_sl):
    nc = s.nc
    t = tmp
    s.op("act", lambda: nc.scalar.activation(out=t[shape_sl], in_=zin_ap, func=AF.Square), reads=[zin_b], writes=[t])
    s.op("dve", lambda: nc.vector.tensor_scalar(out=t[shape_sl], in0=t[shape_sl], scalar1=0.044715, scalar2=1.0,
                                                op0=ALU.mult, op1=ALU.add), reads=[t], writes=[t])
    s.op("dve", lambda: nc.vector.tensor_tensor(out=t[shape_sl], in0=t[shape_sl], in1=zin_ap, op=ALU.mult),
         reads=[t, zin_b], writes=[t])
    s.op("act", lambda: nc.scalar.activation(out=t[shape_sl], in_=t[shape_sl], func=AF.Exp, scale=-GELU_C), reads=[t], writes=[t])
    s.op("dve", lambda: nc.vector.tensor_scalar(out=t[shape_sl], in0=t[shape_sl], scalar1=1.0, scalar2=None, op0=ALU.add),
         reads=[t], writes=[t])
    s.op("dve", lambda: nc.vector.reciprocal(out=t[shape_sl], in_=t[shape_sl]), reads=[t], writes=[t])
    s.op("dve", lambda: nc.vector.tensor_tensor(out=out_ap, in0=t[shape_sl], in1=zin_ap, op=ALU.mult),
         reads=[t, zin_b], writes=[out_b])


def emit_mix_c(s, cm, io, T, cs=None):
    fused = cs is not None
    cs = cs if fused else [None]
    nc = s.nc
    NT = T // 128
    QL = NT // 4
    NCT = max(1, T // 2048)
    Nc = T // 16 - 1
    NCP = NCT * 128 if Nc > 128 else 128
    NCW = min(Nc, 511)
    assert Nc <= 511
    m = s.mark()
    ident_b = cm["ident_b"]
    ps_l = cm["ps_rr"].items
    po_l = cm["po_rr"].items
    pf, pf2 = cm["pf"], cm["pf2"]

    def ld(name, shape, dt, src, q="sp"):
        b = s.sb(name, shape, dt)
        idx = tuple(slice(None) for _ in shape)
        s.dma(q, b[idx], src, writes=[b])
        return b

    qc = s.sb("c_q", [128, QL, 512], BF16)
    kk = ld("c_kk", [128, T], BF16, io["kskw"][:, :])
    vs = s.sb("c_vs", [128, NT, 65], BF16)
    vw = s.sb("c_vw", [128, NT, 65], BF16)
    load_vt(s, vs, io, "vs", T)
    load_vt(s, vw, io, "vw", T)
    emat = ld("c_e", [128, NT, 128], BF16, io["emat"][:, :, :])
    ovl = ld("c_ovl", [128, NCT, 128], BF16, io["ovl"][:, :, :])
    smask = s.sb("c_sm", [128, 4, 128], BF16)
    wmask = s.sb("c_wm", [128, 8, 128], BF16)
    ngt = s.sb("c_ng", [128, QL, 12], F32)
    ones64 = ld("c_o64", [64, 64], BF16, io["ones64"][:, :])
    onesrow = ld("c_orow", [1, 128], BF16, io["onesrow"][:, :])
    kgain = ld("c_kg", [64, 1], F32, io["kgain"][:, :])
    b2c = ld("c_b2c", [64, 1], F32, io["b2c"][:, :])
    b1 = ld("c_b1", [128, 4], F32, io["b1"][:, :])

    ktc = s.sb("c_ktc", [64, NCP], BF16)
    vc = s.sb("c_vc", [128, NCT, 65], BF16)
    s.op("dve", lambda: nc.vector.memset(ktc[:, :], 0.0), writes=[ktc])
    s.op("dve", lambda: nc.vector.memset(vc[:, :, :], 0.0), writes=[vc])
    s.op("dve", lambda: nc.vector.memset(vc[:, :, 64:65], 1.0), writes=[vc])

    m2 = s.mark()
    kvin = ld("c_kvin", [128, T], BF16, io["kvin"][:, :])
    w1sb = s.sb("c_w1", [128, 32, 256], BF16)
    for x in range(2):
        s.dma("pool", w1sb[x * 64:(x + 1) * 64, :, :], io["w1"][x].rearrange("(j d) f -> d j f", d=64), writes=[w1sb])
    peT = s.sb("c_pe", [128, 32], BF16)
    s.dma("pool", peT[:, :], io["peT"][:, :], writes=[peT])
    w2sb = s.sb("c_w2", [128, 2, 2, 64], BF16)
    for x in range(2):
        s.dma("pool", w2sb[:, x, :, :], io["w2"][x].rearrange("(hh f) d -> f hh d", f=128), writes=[w2sb])
    b2row = s.sb("c_b2r", [1, 64], BF16)
    s.dma("pool", b2row[:, :], io["b2"][1:2, :], writes=[b2row])
    hacc = [ps_l[0], ps_l[1], ps_l[2], po_l[0]]
    pcol = po_l[1]
    for x in range(2):
        for hh in range(2):
            hp = hacc[x * 2 + hh]
            for j in range(32):
                s.op("pe", lambda: nc.tensor.matmul(hp[:, 0:NCW], lhsT=w1sb[x * 64:(x + 1) * 64, j, hh * 128:(hh + 1) * 128],
                                                    rhs=kvin[x * 64:(x + 1) * 64, j:j + 16 * (NCW - 1) + 1:16],
                                                    start=(j == 0), stop=(j == 31)),
                     reads=[w1sb, kvin], writes=[hp], inc=(j == 31))
            for j in range(32):
                s.op("pe", lambda: nc.tensor.matmul(pcol[:, x * 2 + hh:x * 2 + hh + 1],
                                                    lhsT=w1sb[x * 64:(x + 1) * 64, j, hh * 128:(hh + 1) * 128],
                                                    rhs=peT[x * 64:(x + 1) * 64, j:j + 1], start=(j == 0), stop=(j == 31)),
                     reads=[w1sb, peT], writes=[pcol], inc=(j == 31))
    hbias = s.sb("c_hb", [128, 4], F32)
    s.op("dve", lambda: nc.vector.tensor_tensor(out=hbias[:, :], in0=pcol[:, 0:4], in1=b1[:, :], op=ALU.add),
         reads=[pcol, b1], writes=[hbias])
    gh = []
    for x in range(2):
        for hh in range(2):
            k = x * 2 + hh
            z = s.sb("c_z%d" % k, [128, 512], F32)
            tmp = s.sb("c_zt%d" % k, [128, 512], F32)
            gb = s.sb("c_g%d" % k, [128, 512], BF16)
            s.op("act", lambda: nc.scalar.activation(out=z[:, 0:NCW], in_=hacc[k][:, 0:NCW], func=AF.Identity,
                                                     bias=hbias[:, k:k + 1], scale=1.0), reads=[hacc[k], hbias], writes=[z])
            emit_gelu(s, z[:, 0:NCW], z, gb[:, 0:NCW], gb, tmp, (slice(None), slice(0, NCW)))
            gh.append(gb)
    pk = po_l[2]
    for hh in range(2):
        s.op("pe", lambda: nc.tensor.matmul(pk[0:64, 0:NCW], lhsT=w2sb[:, 0, hh, :], rhs=gh[hh][:, 0:NCW],
                                            start=(hh == 0), stop=(hh == 1)), reads=[w2sb, gh[hh]], writes=[pk], inc=(hh == 1))
    kz = s.sb("c_kz", [64, 512], F32)
    ksq = s.sb("c_ksq", [64, 512], BF16)
    krs = s.sb("c_krs", [64, 512], F32)
    s.op("act", lambda: nc.scalar.activation(out=kz[:, 0:NCW], in_=pk[0:64, 0:NCW], func=AF.Identity, bias=b2c[:, 0:1], scale=1.0),
         reads=[pk, b2c], writes=[kz])
    s.op("act", lambda: nc.scalar.activation(out=ksq[:, 0:NCW], in_=kz[:, 0:NCW], func=AF.Square), reads=[kz], writes=[ksq])
    pq = ps_l[0]
    s.op("pe", lambda: nc.tensor.matmul(pq[0:64, 0:NCW], lhsT=ones64[:, :], rhs=ksq[:, 0:NCW], start=True, stop=True),
         reads=[ones64, ksq], writes=[pq])
    s.op("act", lambda: nc.scalar.activation(out=krs[:, 0:NCW], in_=pq[0:64, 0:NCW], func=AF.Ln, bias=cm["epsc"][0:64, 0:1],
                                             scale=1.0 / 64.0), reads=[pq, cm["epsc"]], writes=[krs])
    s.op("act", lambda: nc.scalar.activation(out=krs[:, 0:NCW], in_=krs[:, 0:NCW], func=AF.Exp, scale=-0.5), reads=[krs], writes=[krs])
    s.op("dve", lambda: nc.vector.scalar_tensor_tensor(out=ktc[:, 0:NCW], in0=kz[:, 0:NCW], scalar=kgain[:, 0:1], in1=krs[:, 0:NCW],
                                                       op0=ALU.mult, op1=ALU.mult), reads=[kz, kgain, krs], writes=[ktc])
    for nt in range(NCT):
        n0 = nt * 128
        nn = min(128, Nc - n0)
        pv = ps_l[1 + nt % 2]
        for hh in range(2):
            s.op("pe", lambda: nc.tensor.matmul(pv[0:nn, 0:64], lhsT=gh[2 + hh][:, n0:n0 + nn], rhs=w2sb[:, 1, hh, :],
                                                start=(hh == 0), stop=False), reads=[gh[2 + hh], w2sb], writes=[pv], inc=False)
        s.op("pe", lambda: nc.tensor.matmul(pv[0:nn, 0:64], lhsT=onesrow[0:1, 0:nn], rhs=b2row[0:1, :], start=False, stop=True),
             reads=[onesrow, b2row], writes=[pv])
        s.op("act", lambda: nc.scalar.copy(out=vc[0:nn, nt, 0:64], in_=pv[0:nn, 0:64]), reads=[pv], writes=[vc])
    s.release(m2)

    cmk_rr = RR([s.sb("c_cmk%d" % j, [128, NCT, 128], BF16) for j in range(2)])
    ia_rr = RR([s.sb("c_ia%d" % j, [128, 128], F32) for j in range(2)])
    ib_rr = RR([s.sb("c_ib%d" % j, [128, 128], F32) for j in range(2)])
    imp_rr = RR([s.sb("c_imp%d" % j, [128, 128], F32) for j in range(2)])
    imp2_rr = RR([s.sb("c_impb%d" % j, [128, 128], F32) for j in range(2)])
    m8_rr = RR([s.sb("c_m8%d" % j, [128, 16], F32) for j in range(2)])
    mbT_rr = RR([s.sb("c_mbT%d" % j, [128, 128], BF16) for j in range(2)])
    oco_rr = RR([s.sb("c_oc%d" % j, [128, 4, 64], F32) for j in range(2)])
    gw_rr = RR([s.sb("c_gw%d" % j, [128, 12], F32) for j in range(2)])
    oo_rr = RR([s.sb("c_oo%d" % j, [128, 4, 64], F32) for j in range(2)])
    ob_rr = RR([s.sb("c_ob%d" % j, [128, 4, 64], BF16) for j in range(2)])

    def masked_tile(kbuf, prow, t, Q, masks, vbuf, vt_idx, po, first, last, extra=None):
        ps = cm["ps_rr"].next()
        nm = len(masks)
        s.op("pe", lambda: nc.tensor.matmul(ps[:, :], lhsT=kbuf[prow[0]:prow[1], t * 128:(t + 1) * 128], rhs=Q,
                                            start=True, stop=(nm == 0)), reads=[kbuf, qc], writes=[ps], inc=(nm == 0))
        for mi, (la, lb, ra, rb) in enumerate(masks):
            for h in range(4):
                lastm = (mi == nm - 1 and h == 3)
                s.op("pe", lambda: nc.tensor.matmul(ps[:, h * 128:(h + 1) * 128], lhsT=la, rhs=ra, start=False, stop=lastm),
                     reads=[lb, rb], writes=[ps], inc=lastm)
        pt = cm["pt_rr"].next()
        s.op("act", lambda: nc.scalar.activation(out=pt[:, :], in_=ps[:, :], func=AF.Exp), reads=[ps], writes=[pt])
        s.op("pe", lambda: nc.tensor.matmul(po[0:65, :], lhsT=vbuf[:, vt_idx, :], rhs=pt[:, :], start=first, stop=last),
             reads=[vbuf, pt], writes=[po])
        return pt

    for ci in cs:
        def gk(key):
            return io[key][ci] if fused else io[key]
        if fused:
            for i in range(QL):
                qt = 4 * i + ci
                for h in range(4):
                    srcq = io["zq"][h // 2][(h % 2) * 64:(h % 2) * 64 + 64, qt * 128:(qt + 1) * 128]
                    s.dma("sp", qc[0:64, i, h * 128:(h + 1) * 128], srcq, writes=[qc])
                    s.dma("sp", qc[64:128, i, h * 128:(h + 1) * 128], srcq, writes=[qc])
            msb, mview = io["misc_sb"]
            s.op("act", lambda: nc.scalar.activation(out=ngt[:, :, :], in_=mview[:, ci:NT:4, 4:16], func=AF.Exp, scale=-1.0),
                 reads=[msb], writes=[ngt])
        else:
            s.dma("sp", qc[:, :, :], io["qc"][:, :, :], writes=[qc])
            s.dma("sp", ngt[:, :, :], io["ng"][:, :, :], writes=[ngt])
            s.op("act", lambda: nc.scalar.activation(out=ngt[:, :, :], in_=ngt[:, :, :], func=AF.Exp, scale=-1.0),
                 reads=[ngt], writes=[ngt])
        s.op("dve", lambda: nc.vector.tensor_scalar(out=ngt[:, :, :], in0=ngt[:, :, :], scalar1=1.0, scalar2=None, op0=ALU.add),
             reads=[ngt], writes=[ngt])
        s.op("dve", lambda: nc.vector.reciprocal(out=ngt[:, :, :], in_=ngt[:, :, :]), reads=[ngt], writes=[ngt])
        s.dma("sp", smask[:, :, :], gk("smask")[:, :, :], writes=[smask])
        s.dma("sp", wmask[:, :, :], gk("wmask")[:, :, :], writes=[wmask])
        for i in range(QL):
            Qlo = qc[0:64, i, :]
            Qhi = qc[64:128, i, :]
            cmk = cmk_rr.next()
            ia = ia_rr.next()
            ib = ib_rr.next()
            s.dma("sp", cmk[:, :, :], gk("cmask")[:, i, :, :], writes=[cmk])
            s.dma("sp", ia[:, :], gk("impA")[:, i, :], writes=[ia])
            s.dma("sp", ib[:, :], gk("impB")[:, i, :], writes=[ib])
            po_c, po_s, po_w = po_l[0], po_l[1], po_l[2]
            nct = min(NCT, i // 4 + 1)
            for nt in range(nct):
                pt = masked_tile(ktc, (0, 64), nt, Qlo, [(ident_b[:, :], ident_b, cmk[:, nt, :], cmk)], vc, nt, po_c,
                                 nt == 0, nt == nct - 1)
                for h in range(4):
                    s.op("pe", lambda: nc.tensor.matmul(pf2[:, h, :], lhsT=pt[:, h * 128:(h + 1) * 128], rhs=ovl[:, nt, :],
                                                        start=(nt == 0 and h == 0), stop=(nt == nct - 1 and h == 3),
                                                        skip_group_check=True), reads=[pt, ovl], writes=[pf2],
                         inc=(h == 3))
            emit_o_to_tokmajor(s, cm, po_c, pf, 0)
            st = cm["st_rr"].next()
            rsum = cm["st_rr"].next()
            gw = gw_rr.next()
            s.op("dve", lambda: nc.vector.tensor_scalar(out=st[:, 0:4], in0=pf[:, :, 64], scalar1=1e-30, scalar2=None, op0=ALU.max),
                 reads=[pf], writes=[st])
            s.op("dve", lambda: nc.vector.reciprocal(out=rsum[:, 0:4], in_=st[:, 0:4]), reads=[st], writes=[rsum])
            oco = oco_rr.next()
            s.op("dve", lambda: nc.vector.tensor_copy(out=oco[:, :, :], in_=pf[:, :, 0:64]), reads=[pf], writes=[oco])
            imp = imp_rr.next()
            s.op("dve", lambda: nc.vector.tensor_scalar(out=imp[:, :], in0=pf2[:, 0, :], scalar1=rsum[:, 0:1], scalar2=None, op0=ALU.mult),
                 reads=[pf2, rsum], writes=[imp])
            for h in range(1, 4):
                s.op("dve", lambda: nc.vector.scalar_tensor_tensor(out=imp[:, :], in0=pf2[:, h, :], scalar=rsum[:, h:h + 1], in1=imp[:, :],
                                                                   op0=ALU.mult, op1=ALU.add), reads=[pf2, rsum, imp], writes=[imp])
            s.op("dve", lambda: nc.vector.tensor_tensor(out=imp[:, :], in0=imp[:, :], in1=ia[:, :], op=ALU.mult), reads=[imp, ia], writes=[imp])
            s.op("dve", lambda: nc.vector.tensor_tensor(out=imp[:, :], in0=imp[:, :], in1=ib[:, :], op=ALU.add), reads=[imp, ib], writes=[imp])
            m8 = m8_rr.next()
            imp2 = imp2_rr.next()
            s.op("dve", lambda: nc.vector.max(out=m8[:, 0:8], in_=imp[:, :]), reads=[imp], writes=[m8])
            s.op("dve", lambda: nc.vector.match_replace(out=imp2[:, :], in_to_replace=m8[:, 0:8], in_values=imp[:, :], imm_value=-1e9),
                 reads=[imp, m8], writes=[imp2])
            s.op("dve", lambda: nc.vector.max(out=m8[:, 8:16], in_=imp2[:, :]), reads=[imp2], writes=[m8])
            s.op("dve", lambda: nc.vector.tensor_scalar(out=imp2[:, :], in0=imp[:, :], scalar1=m8[:, 15:16], scalar2=NEG,
                                                        op0=ALU.is_lt, op1=ALU.mult), reads=[imp, m8], writes=[imp2])
            ptr = cm["ps_rr"].next()
            s.op("pe", lambda: nc.tensor.transpose(out=ptr[:, 0:128], in_=imp2[:, :], identity=cm["ident_f"][:, :]),
                 reads=[imp2, cm["ident_f"]], writes=[ptr])
            mbT = mbT_rr.next()
            s.op("act", lambda: nc.scalar.copy(out=mbT[:, :], in_=ptr[:, 0:128]), reads=[ptr], writes=[mbT])
            nts = 4 * i + 4
            for t in range(nts):
                masks = [(emat[:, t, :], emat, mbT[:, :], mbT)]
                if t >= 4 * i:
                    masks.append((ident_b[:, :], ident_b, smask[:, t - 4 * i, :], smask))
                masked_tile(kk, (0, 64), t, Qlo, masks, vs, t, po_s, t == 0, t == nts - 1)
            tl = [4 * (i - 1) + u for u in range(8) if 4 * (i - 1) + u >= 0]
            for t in tl:
                u = t - 4 * (i - 1)
                masked_tile(kk, (64, 128), t, Qhi, [(ident_b[:, :], ident_b, wmask[:, u, :], wmask)], vw, t, po_w,
                            t == tl[0], t == tl[-1])
            emit_o_to_tokmajor(s, cm, po_s, pf, 0)
            st2 = cm["st_rr"].next()
            s.op("dve", lambda: nc.vector.tensor_scalar(out=st2[:, 0:4], in0=pf[:, :, 64], scalar1=1e-30, scalar2=None, op0=ALU.max),
                 reads=[pf], writes=[st2])
            s.op("dve", lambda: nc.vector.reciprocal(out=st2[:, 0:4], in_=st2[:, 0:4]), reads=[st2], writes=[st2])
            gv = ngt[:, i, :].rearrange("p (h b) -> p h b", b=3)
            gwv = gw[:, :].rearrange("p (h b) -> p h b", b=3)
            s.op("dve", lambda: nc.vector.tensor_tensor(out=gwv[:, :, 0], in0=gv[:, :, 0], in1=rsum[:, 0:4], op=ALU.mult),
                 reads=[ngt, rsum], writes=[gw])
            s.op("dve", lambda: nc.vector.tensor_tensor(out=gwv[:, :, 1], in0=gv[:, :, 1], in1=st2[:, 0:4], op=ALU.mult),
                 reads=[ngt, st2], writes=[gw])
            oo = oo_rr.next()
            for h in range(4):
                s.op("dve", lambda: nc.vector.tensor_scalar(out=oo[:, h, :], in0=oco[:, h, :], scalar1=gw[:, 3 * h:3 * h + 1], scalar2=None,
                                                            op0=ALU.mult), reads=[oco, gw], writes=[oo])
                s.op("dve", lambda: nc.vector.scalar_tensor_tensor(out=oo[:, h, :], in0=pf[:, h, 0:64], scalar=gw[:, 3 * h + 1:3 * h + 2],
                                                                   in1=oo[:, h, :], op0=ALU.mult, op1=ALU.add), reads=[pf, gw, oo], writes=[oo])
            emit_o_to_tokmajor(s, cm, po_w, pf, 0)
            st3 = cm["st_rr"].next()
            s.op("dve", lambda: nc.vector.tensor_scalar(out=st3[:, 0:4], in0=pf[:, :, 64], scalar1=1e-30, scalar2=None, op0=ALU.max),
                 reads=[pf], writes=[st3])
            s.op("dve", lambda: nc.vector.reciprocal(out=st3[:, 0:4], in_=st3[:, 0:4]), reads=[st3], writes=[st3])
            s.op("dve", lambda: nc.vector.tensor_tensor(out=gwv[:, :, 2], in0=gv[:, :, 2], in1=st3[:, 0:4], op=ALU.mult),
                 reads=[ngt, st3], writes=[gw])
            ob = ob_rr.next()
            for h in range(4):
                s.op("dve", lambda: nc.vector.scalar_tensor_tensor(out=ob[:, h, :], in0=pf[:, h, 0:64], scalar=gw[:, 3 * h + 2:3 * h + 3],
                                                                   in1=oo[:, h, :], op0=ALU.mult, op1=ALU.add), reads=[pf, gw, oo], writes=[ob])
            orow = ((4 * i + ci) if fused else i) * 128
            s.dma("sp", io["oc"][orow:orow + 128, :], ob[:, :, :].rearrange("p h d -> p (h d)"), reads=[ob])
    s.release(m)


def build_mix(T, parts="abc"):
    nc = bass.Bass("TRN2", target_bir_lowering=False)
    io = mix_decl(nc, T)
    if "c" in parts:
        mix_decl_c(nc, io, T)
    s = S(nc)
    cm = mix_common(s, io)
    if "a" in parts:
        emit_mix_a(s, cm, io, T)
    if "b" in parts:
        emit_mix_b(s, cm, io, T)
    if "c" in parts:
        emit_mix_c(s, cm, io, T)
    s.finish()
    s.close()
    return nc


def mix_consts_c(T, c):
    NT = T // 128
    QL = NT // 4
    NCT = max(1, T // 2048)
    Nc = T // 16 - 1
    NS = T // 64
    ar = np.arange(128)
    cmask = np.zeros((128, QL, NCT, 128), np.float32)
    impA = np.zeros((128, QL, 128), np.float32)
    impB = np.zeros((128, QL, 128), np.float32)
    for i in range(QL):
        qpos = 128 * (4 * i + c) + ar
        for nt in range(NCT):
            n = 128 * nt + ar
            ok = (16 * n[:, None] + 31 <= qpos[None, :]) & (n[:, None] < Nc)
            cmask[:, i, nt, :] = np.where(ok, 0.0, NEG)
        j = ar
        cur = qpos // 64
        forced = (j[None, :] == 0) | (j[None, :] == cur[:, None]) | (j[None, :] == cur[:, None] - 1)
        valid = (j[None, :] * 64 <= qpos[:, None]) & (j[None, :] < NS)
        impA[:, i, :] = (valid & ~forced).astype(np.float32)
        impB[:, i, :] = np.where(forced & (j[None, :] < NS), 1.0e4, np.where(valid, 0.0, -1.0))
    smask = np.zeros((128, 4, 128), np.float32)
    for u in range(4):
        kpos = 128 * u + ar
        qp = 128 * c + ar
        smask[:, u, :] = np.where(kpos[:, None] <= qp[None, :], 0.0, NEG)
    wmask = np.zeros((128, 8, 128), np.float32)
    for u in range(8):
        dist = 128 * (c + 4 - u) + ar[None, :] - ar[:, None]
        wmask[:, u, :] = np.where((dist >= 0) & (dist < 512), 0.0, NEG)
    emat = np.zeros((128, NT, 128), np.float32)
    for t in range(NT):
        for k in range(128):
            jj = 2 * t + k // 64
            if jj < 128:
                emat[jj, t, k] = 1.0
    ovl = np.zeros((128, NCT, 128), np.float32)
    for nt in range(NCT):
        n = 128 * nt + ar
        o = (n[:, None] * 16 < (ar[None, :] + 1) * 64) & (n[:, None] * 16 + 32 > ar[None, :] * 64) & (n[:, None] < Nc) \
            & (ar[None, :] < NS)
        ovl[:, nt, :] = o
    return dict(cmask=_bf(cmask), impA=impA, impB=impB, smask=_bf(smask), wmask=_bf(wmask), emat=_bf(emat), ovl=_bf(ovl),
                ones64=_bf(np.ones((64, 64), np.float32)), onesrow=_bf(np.ones((1, 128), np.float32)))


def build_merge(NT, TB=512):
    nc = bass.Bass("TRN2", target_bir_lowering=False)
    x = dram_in(nc, "x", [NT, D])
    g = dram_in(nc, "g", [D])
    w_in = dram_in(nc, "w_in", [D, 6800])
    w_br = dram_in(nc, "w_br", [4, 256, D])
    w_o = dram_in(nc, "w_o", [D, D])
    ident = dram_in(nc, "ident", [128, 128], BF16)
    obr = dram_in(nc, "obr", [NT, D], BF16)
    y = dram_out(nc, "y", [NT, D])
    s = S(nc)
    emit_merge(s, x, g, w_in, w_br, w_o, ident, obr, y, NT, TB)
    s.finish()
    s.close()
    return nc


def emit_merge(s, x, g, w_in, w_br, w_o, ident, obr, y, NT, TB=512):
    nc = s.nc
    m_ = s.mark()
    ntile = TB // 128
    ident_b = s.sb("ident_b", [128, 128], BF16)
    s.dma("sp", ident_b[:, :], ident[:, :], writes=[ident_b])
    gcol = s.sb("gcol", [128, NKC], F32)
    s.dma("sp", gcol[:, :], g.rearrange("(c p) -> p c", p=128), writes=[gcol], allow_slow_non_contiguous=True)
    epsc = s.sb("epsc", [128, 1], F32)
    s.op("dve", lambda: nc.vector.memset(epsc[:, :], EPS), writes=[epsc])
    wg = [s.sb("wg%d" % c, [128, 4096], BF16) for c in range(NKC)]
    wb = [s.sb("wb%d" % c, [128, D], BF16) for c in range(8)]
    wo = [s.sb("wo%d" % c, [128, D], BF16) for c in range(NKC)]
    for c in range(NKC):
        for hf in range(2):
            s.dma("pool", wg[c][:, hf * 2048:(hf + 1) * 2048], w_in[c * 128:(c + 1) * 128, 2704 + hf * 2048:2704 + (hf + 1) * 2048],
                  writes=[wg[c]])
    for n in range(4):
        for cc in range(2):
            s.dma("pool", wb[2 * n + cc][:, :], w_br[n, cc * 128:(cc + 1) * 128, :], writes=[wb[2 * n + cc]])
    for c in range(NKC):
        s.dma("pool", wo[c][:, :], w_o[c * 128:(c + 1) * 128, :], writes=[wo[c]])
    xt = [s.sb("xt%d" % j, [128, D], F32) for j in range(ntile)]
    ot = [s.sb("ot%d" % j, [128, D], BF16) for j in range(ntile)]
    hb_rr = RR([s.sb("hb%d" % j, [128, D], BF16) for j in range(ntile)])
    scr = s.sb("scr", [128, D], BF16)
    stat_rr = RR([s.sb("st%d" % j, [128, 4], F32) for j in range(4)])
    hT = s.sb("hT", [128, NKC, TB], BF16)
    oT = s.sb("oT", [128, 8, TB], BF16)
    mT = [s.sb("mT%d" % c, [128, TB], BF16) for c in range(8)]
    pT_rr = RR([s.ps("pT%d" % j, [128, TB], BF16) for j in range(2)])
    pg_rr = RR([s.ps("pg%d" % j, [128, 512], F32) for j in range(2)])
    pp_rr = RR([s.ps("pp%d" % j, [128, 512], F32) for j in range(2)])
    po_rr = RR([s.ps("po%d" % j, [128, 512], F32) for j in range(2)])
    sg_rr = RR([s.sb("sg%d" % j, [128, TB], F32) for j in range(3)])
    acc_rr = RR([s.sb("acc%d" % j, [128, TB], F32) for j in range(2)])
    for tb in range(NT // TB):
        t0 = tb * TB
        for j in range(ntile):
            s.dma("sp", xt[j][:, :], x[t0 + j * 128:t0 + (j + 1) * 128, :], writes=[xt[j]])
            s.dma("sp", ot[j][:, :], obr[t0 + j * 128:t0 + (j + 1) * 128, :], writes=[ot[j]])
        emit_rmsnorm_T(s, epsc, xt, gcol, hT, ident_b, pT_rr, hb_rr, scr, stat_rr, ntile)
        for c in range(8):
            pT = pT_rr.next()
            for j in range(ntile):
                s.op("pe", lambda: nc.tensor.transpose(out=pT[:, j * 128:(j + 1) * 128], in_=ot[j][:, c * 128:(c + 1) * 128],
                                                       identity=ident_b[:, :]), reads=[ot[j], ident_b], writes=[pT], inc=(j == ntile - 1))
            if c % 2 == 0:
                s.op("dve", lambda: nc.vector.tensor_copy(out=oT[:, c, :], in_=pT[:, 0:TB]), reads=[pT], writes=[oT])
            else:
                s.op("act", lambda: nc.scalar.copy(out=oT[:, c, :], in_=pT[:, 0:TB]), reads=[pT], writes=[oT])
        for dc in range(8):
            acc = acc_rr.next()
            for n in range(4):
                pg = pg_rr.next()
                pp = pp_rr.next()
                for c in range(NKC):
                    s.op("pe", lambda: nc.tensor.matmul(pg[:, 0:TB], lhsT=wg[c][:, n * 1024 + dc * 128:n * 1024 + (dc + 1) * 128],
                                                        rhs=hT[:, c, :], start=(c == 0), stop=(c == NKC - 1)),
                         reads=[wg[c], hT], writes=[pg], inc=(c == NKC - 1))
                for cc in range(2):
                    s.op("pe", lambda: nc.tensor.matmul(pp[:, 0:TB], lhsT=wb[2 * n + cc][:, dc * 128:(dc + 1) * 128],
                                                        rhs=oT[:, 2 * n + cc, :], start=(cc == 0), stop=(cc == 1)),
                         reads=[wb[2 * n + cc], oT], writes=[pp], inc=(cc == 1))
                sg = sg_rr.next()
                s.op("act", lambda: nc.scalar.activation(out=sg[:, :], in_=pg[:, 0:TB], func=AF.Sigmoid), reads=[pg], writes=[sg])
                if n == 0:
                    s.op("dve", lambda: nc.vector.tensor_tensor(out=acc[:, :], in0=sg[:, :], in1=pp[:, 0:TB], op=ALU.mult),
                         reads=[sg, pp], writes=[acc])
                else:
                    s.op("dve", lambda: nc.vector.tensor_tensor(out=sg[:, :], in0=sg[:, :], in1=pp[:, 0:TB], op=ALU.mult),
                         reads=[sg, pp], writes=[sg])
                    if n < 3:
                        s.op("pool", lambda: nc.gpsimd.tensor_tensor(out=acc[:, :], in0=acc[:, :], in1=sg[:, :], op=ALU.add),
                             reads=[acc, sg], writes=[acc])
                    else:
                        s.op("pool", lambda: nc.gpsimd.tensor_tensor(out=mT[dc][:, :], in0=acc[:, :], in1=sg[:, :], op=ALU.add),
                             reads=[acc, sg], writes=[mT[dc]])
        for j in range(ntile):
            for hf in range(2):
                po = po_rr.next()
                for dc in range(8):
                    s.op("pe", lambda: nc.tensor.matmul(po[:, :], lhsT=mT[dc][:, j * 128:(j + 1) * 128],
                                                        rhs=wo[dc][:, hf * 512:(hf + 1) * 512], start=(dc == 0), stop=(dc == 7)),
                         reads=[mT[dc], wo[dc]], writes=[po], inc=(dc == 7))
                s.op("dve", lambda: nc.vector.tensor_tensor(out=xt[j][:, hf * 512:(hf + 1) * 512], in0=po[:, :],
                                                            in1=xt[j][:, hf * 512:(hf + 1) * 512], op=ALU.add),
                     reads=[po, xt[j]], writes=[xt[j]])
            s.dma("sp", y[t0 + j * 128:t0 + (j + 1) * 128, :], xt[j][:, :], reads=[xt[j]])
    s.release(m_)


PARAM_SHAPES = dict(
    ffn1_norm=("L", D), ffn1_w_in=("L", D, 2 * DFF), ffn1_w_out=("L", DFF, D), mix_norm=("L", D), w_in=("L", D, 6800),
    nsa_phi_w1=("L", 2, 2048, 256), nsa_phi_w2=("L", 2, 256, 64), nsa_phi_b2=("L", 2, 64),
    w_branch=("L", 4, 256, D), w_out=("L", D, D), ffn2_norm=("L", D), ffn2_w_in=("L", D, 2 * DFF), ffn2_w_out=("L", DFF, D),
    gains=("L", 128, 6), vgain=("L", 128, 256), wsT=("L", 4, 128, 128), bsT=("L", 128, 4), lamp=("L", 128, 4, 32),
    lami=("L", 128, 2), fbias=("L", 4, 128, 1), b1l=("L", 128, 4), peT=("L", 128, 32), b2c=("L", 64, 1), kgain=("L", 64, 1),
)


def fused_const_shapes(T):
    NT = T // 128
    QL = NT // 4
    NCT = max(1, T // 2048)
    return dict(
        ident=([128, 128], BF16), identf=([128, 128], F32), blk=([2, 128, 128], BF16), triu=([128, 128], F32),
        trib=([128, 128], BF16), triuf=([128, 128], F32), onesf=([128, 128], F32), ones64=([64, 64], BF16),
        onesrow=([1, 128], BF16), cmask=([4, 128, QL, NCT, 128], BF16), smask=([4, 128, 4, 128], BF16),
        wmask=([4, 128, 8, 128], BF16), impA=([4, 128, QL, 128], F32), impB=([4, 128, QL, 128], F32),
        emat=([128, NT, 128], BF16), ovl=([128, NCT, 128], BF16))


def fused_consts(T):
    pc = proj_consts()
    mc = mix_consts()
    cc = [mix_consts_c(T, c) for c in range(4)]
    d = dict(ident=pc["ident"], identf=mc["identf"], blk=pc["blk"], triu=pc["triu"], trib=mc["trib"], triuf=mc["triuf"],
             onesf=mc["onesf"], ones64=cc[0]["ones64"], onesrow=cc[0]["onesrow"], emat=cc[0]["emat"], ovl=cc[0]["ovl"])
    for k in ("cmask", "smask", "wmask", "impA", "impB"):
        d[k] = np.ascontiguousarray(np.stack([cc[c][k] for c in range(4)], 0))
    return d


def emit_mix_fused(s, F, l, T):
    nc = s.nc
    NT = T // 128
    m = s.mark()
    io0 = dict(identb=F["ident"], identf=F["identf"], trib=F["trib"])
    cm = mix_common(s, io0)
    misc_sb = s.sb("misc_sb", [128, NT, 16], F32)
    for j0 in range(0, NT, 8):
        j1 = min(NT, j0 + 8)
        s.dma("sp", misc_sb[:, j0:j1, :], F["misc"][j0 * 128:j1 * 128, :].rearrange("(j p) c -> p j c", p=128), writes=[misc_sb])
    zfm, vab, obr = F["zfm"], F["vab"], F["obr"]
    for h in range(4):
        r0 = (h % 2) * 64
        io = dict(io0)
        io.update(qa=zfm[h // 2][r0:r0 + 64, :], ka=zfm[2 + h // 2][r0:r0 + 64, :], va_src=vab[:, h * 64:(h + 1) * 64],
                  lamp=F["lamp"][l], lami=F["lami"][l], oa=obr[:, h * 64:(h + 1) * 64])
        emit_mix_a(s, cm, io, T)
        io = dict(io0)
        io.update(qb=zfm[4 + h // 2][r0:r0 + 64, :], kb=zfm[6 + h // 2][r0:r0 + 64, :],
                  vb_src=vab[:, 256 + h * 64:256 + (h + 1) * 64], flog_sb=(misc_sb, misc_sb[:, :, h]), fbias=F["fbias"][l, h],
                  triuf=F["triuf"], onesf=F["onesf"], ob=obr[:, 256 + h * 64:256 + (h + 1) * 64])
        emit_mix_b(s, cm, io, T)
    io = dict(io0)
    io.update(zq=(zfm[8], zfm[9]), kskw=zfm[10], kvin=zfm[11], vs_src=F["vsw"][:, 0:64], vw_src=F["vsw"][:, 64:128],
              misc_sb=(misc_sb, misc_sb), w1=F["nsa_phi_w1"][l], b1=F["b1l"][l], peT=F["peT"][l], w2=F["nsa_phi_w2"][l],
              b2=F["nsa_phi_b2"][l], b2c=F["b2c"][l], kgain=F["kgain"][l], oc=obr[:, 512:768])
    for k in ("cmask", "smask", "wmask", "impA", "impB", "emat", "ovl", "ones64", "onesrow"):
        io[k] = F[k]
    emit_mix_c(s, cm, io, T, cs=[0, 1, 2, 3])
    s.release(m)


def build_fused(T, L):
    nc = bass.Bass("TRN2", target_bir_lowering=False)
    F = {}
    F["x"] = dram_in(nc, "x", [T, D])
    for k, shp in PARAM_SHAPES.items():
        F[k] = dram_in(nc, k, [L if v == "L" else v for v in shp])
    for k, (shp, dt) in fused_const_shapes(T).items():
        F[k] = dram_in(nc, k, shp, dt)
    y = dram_out(nc, "y", [T, D])
    for k, shp, dt in (("xa", [T, D], F32), ("xb", [T, D], F32), ("xc", [T, D], F32), ("zfm", [NFM, 128, T], BF16),
                       ("vab", [T, 512], BF16), ("vsw", [T, 128], BF16), ("misc", [T, 16], F32), ("obr", [T, D], BF16)):
        F[k] = nc.dram_tensor("s_" + k, shp, dt).ap()
    s = S(nc)
    for l in range(L):
        x_in = F["x"] if l == 0 else F["xc"]
        emit_ffn(s, x_in, F["ffn1_norm"][l], F["ffn1_w_in"][l], F["ffn1_w_out"][l], F["ident"], F["xa"], T)
        a = dict(x=F["xa"], g=F["mix_norm"][l], w_in=F["w_in"][l], ident=F["ident"], gains=F["gains"][l], blk=F["blk"],
                 vgain=F["vgain"][l], wsT=F["wsT"][l], triu=F["triu"], bsT=F["bsT"][l], zfm=F["zfm"], vab=F["vab"],
                 vsw=F["vsw"], misc=F["misc"], od=F["obr"][:, 768:1024])
        emit_proj(s, a, T)
        emit_mix_fused(s, F, l, T)
        emit_merge(s, F["xa"], F["mix_norm"][l], F["w_in"][l], F["w_branch"][l], F["w_out"][l], F["ident"], F["obr"], F["xb"], T)
        x_out = y if l == L - 1 else F["xc"]
        emit_ffn(s, F["xb"], F["ffn2_norm"][l], F["ffn2_w_in"][l], F["ffn2_w_out"][l], F["ident"], x_out, T)
    s.finish()
    s.close()
    return nc


def fused_params(P, L):
    import math
    f32 = np.float32
    A = lambda a: np.ascontiguousarray(np.asarray(a, dtype=f32))
    d = {k: A(P[k]) for k in ("ffn1_norm", "ffn1_w_in", "ffn1_w_out", "mix_norm", "w_in", "nsa_phi_w1", "nsa_phi_w2",
                              "nsa_phi_b2", "w_branch", "w_out", "ffn2_norm", "ffn2_w_in", "ffn2_w_out")}
    tile = lambda v, n: np.tile(A(v), (1, n))
    d["gains"] = np.ascontiguousarray(np.stack([tile(P["diff_q_gain"], 4), tile(P["diff_k_gain"], 4), tile(P["fox_q_gain"], 2),
                                                tile(P["fox_k_gain"], 2), tile(P["nsa_q_gain"], 2), tile(P["nsa_k_gain"], 2)], 2))
    d["vgain"] = np.ascontiguousarray(np.broadcast_to(A(P["gmlp_v_gain"])[:, None, :], (L, 128, 256)))
    d["wsT"] = np.ascontiguousarray(A(P["gmlp_w_s"]).transpose(0, 1, 3, 2))
    d["bsT"] = np.ascontiguousarray(A(P["gmlp_b_s"]).transpose(0, 2, 1))
    d["lamp"] = np.ascontiguousarray(np.broadcast_to(A(P["diff_lambda"])[:, None], (L, 128, 4, 32)))
    li = np.array([[0.8 - 0.6 * math.exp(-0.3 * l), 1.0 - (0.8 - 0.6 * math.exp(-0.3 * l))] for l in range(L)], f32)
    d["lami"] = np.ascontiguousarray(np.broadcast_to(li[:, None, :], (L, 128, 2)))
    d["fbias"] = np.ascontiguousarray(np.broadcast_to(A(P["fox_f_bias"])[:, :, None, None], (L, 4, 128, 1)))
    d["b1l"] = np.ascontiguousarray(A(P["nsa_phi_b1"]).reshape(L, 2, 2, 128).transpose(0, 3, 1, 2).reshape(L, 128, 4))
    pe = A(P["nsa_cmp_pe"])
    d["peT"] = np.ascontiguousarray(pe.transpose(0, 1, 3, 2).reshape(L, 128, 32))
    d["b2c"] = np.ascontiguousarray(A(P["nsa_phi_b2"])[:, 0, :, None])
    d["kgain"] = np.ascontiguousarray(A(P["nsa_k_gain"])[:, :, None])
    return d


B_, T_, L_ = 2, 8192, 2
_PROG = {}


def kernel(**inputs):
    x = np.ascontiguousarray(np.asarray(inputs["x"], dtype=np.float32))
    if "fused" not in _PROG:
        _PROG["fused"] = build_fused(T_, L_)
        _PROG["consts"] = fused_consts(T_)
    nc = _PROG["fused"]
    par = fused_params(inputs, L_)
    in_maps = []
    for b in range(B_):
        d = dict(par)
        d.update(_PROG["consts"])
        d["x"] = x[b]
        in_maps.append(d)
    res = run_bass_kernel_spmd(nc, in_maps, core_ids=list(range(B_)))
    return np.stack([np.asarray(res.results[b]["y"], dtype=np.float32) for b in range(B_)], 0)
```

```python
import numpy as np
import concourse.bass as bass
import concourse.mybir as mybir
from concourse.bass_utils import run_bass_kernel_spmd

F32 = mybir.dt.float32
BF16 = mybir.dt.bfloat16
AF = mybir.ActivationFunctionType
ALU = mybir.AluOpType
AX = mybir.AxisListType

ENGS = ("pe", "act", "dve", "pool", "sp")


class Buf:
    __slots__ = ("name", "t", "w", "r", "dsem", "dcnt")

    def __init__(self, name, t):
        self.name = name
        self.t = t
        self.w = None
        self.r = []
        self.dsem = None
        self.dcnt = 0

    def __getitem__(self, idx):
        return self.t[idx]


class S:
    def __init__(self, nc, same_engine_sync=True):
        self.nc = nc
        self.e = {"pe": nc.tensor, "act": nc.scalar, "dve": nc.vector, "pool": nc.gpsimd, "sp": nc.sync}
        self.sem = {k: nc.alloc_semaphore("c_" + k) for k in ENGS}
        self.cnt = {k: 0 for k in ENGS}
        self.seen = {k: {} for k in ENGS}
        self.same = same_engine_sync
        self.nbuf = 0
        self.dma_sems = []
        self.ctx = []
        self.cbufs = []
        self.free_dsems = []

    def sb(self, name, shape, dt):
        self.nbuf += 1
        g = self.nc.sbuf_tensor("%s_%d" % (name, self.nbuf), list(shape), dt)
        t = g.__enter__()
        self.ctx.append(g)
        b = Buf(name, t)
        self.cbufs.append(b)
        return b

    def ps(self, name, shape, dt):
        self.nbuf += 1
        g = self.nc.psum_tensor("%s_%d" % (name, self.nbuf), list(shape), dt)
        t = g.__enter__()
        self.ctx.append(g)
        b = Buf(name, t)
        self.cbufs.append(b)
        return b

    def sub(self, name, ap):
        return Buf(name, ap)

    def mark(self):
        return len(self.ctx)

    def release(self, m):
        self.barrier()
        while len(self.ctx) > m:
            self.ctx.pop().__exit__(None, None, None)
            b = self.cbufs.pop()
            if b.dsem is not None:
                self.free_dsems.append((b.dsem, b.dcnt))
                self.dma_sems.remove(b)
                b.dsem = None

    def close(self):
        for g in reversed(self.ctx):
            g.__exit__(None, None, None)
        self.ctx = []
        self.cbufs = []

    def _need(self, E, deps):
        need = {}
        for d in deps:
            if d is None:
                continue
            if d[0] == "dma":
                b = d[1]
                key = ("dma", id(b))
                need[key] = (b, b.dcnt)
            else:
                F, c = d
                if F == E and (not self.same or E == "pe" or c > self.cnt[E]):
                    continue
                if c > need.get(F, (None, 0))[1]:
                    need[F] = (None, c)
        for key, (b, c) in need.items():
            if self.seen[E].get(key, 0) >= c:
                continue
            self.seen[E][key] = c
            if b is not None:
                self.e[E].wait_ge(b.dsem, c)
            else:
                self.e[E].wait_ge(self.sem[key], c)

    def op(self, E, fn, reads=(), writes=(), inc=True):
        deps = []
        for b in reads:
            deps.append(b.w)
        for b in writes:
            deps.append(b.w)
            deps.extend(b.r)
        self._need(E, deps)
        ins = fn()
        c = self.cnt[E] + 1
        if inc:
            ins.then_inc(self.sem[E], 1)
            self.cnt[E] = c
        for b in writes:
            b.w = (E, c)
            b.r = []
        for b in reads:
            if b not in writes:
                b.r = [x for x in b.r if x[0] != E] + [(E, c)]
        return ins

    def dma(self, Q, out, in_, reads=(), writes=(), **kw):
        deps = []
        for b in reads:
            deps.append(b.w)
        for b in writes:
            deps.append(b.w)
            deps.extend(b.r)
        self._need(Q, deps)
        owner = (list(writes) + list(reads))[0]
        if owner.dsem is None:
            owner.dsem = self._dsem(owner)
            self.dma_sems.append(owner)
        ins = self.e[Q].dma_start(out=out, in_=in_, **kw)
        ins.then_inc(owner.dsem, 16)
        owner.dcnt += 16
        rec = ("dma", owner, owner.dcnt)
        for b in writes:
            b.w = rec
            b.r = []
        for b in reads:
            if b not in writes:
                b.r = b.r + [rec]
        return ins

    def cc(self, kind, groups, in_ap, out_ap, reads=(), writes=()):
        deps = []
        for b in reads:
            deps.append(b.w)
        for b in writes:
            deps.append(b.w)
            deps.extend(b.r)
        self._need("pool", deps)
        owner = list(writes)[0]
        if owner.dsem is None:
            owner.dsem = self._dsem(owner)
            self.dma_sems.append(owner)
        ins = self.nc.gpsimd.collective_compute(kind, op=ALU.bypass, replica_groups=groups, ins=[in_ap], outs=[out_ap])
        ins.then_inc(owner.dsem, 16)
        owner.dcnt += 16
        rec = ("dma", owner, owner.dcnt)
        for b in writes:
            b.w = rec
            b.r = []
        for b in reads:
            if b not in writes:
                b.r = b.r + [rec]
        return ins

    def _dsem(self, owner):
        if self.free_dsems:
            sem, cnt = self.free_dsems.pop()
            owner.dcnt = cnt
            return sem
        self.nsem = getattr(self, "nsem", 0) + 1
        return self.nc.alloc_semaphore("d_%d" % self.nsem)

    def barrier(self):
        for E in ENGS:
            for Fk in ENGS:
                if Fk == E:
                    continue
                c = self.cnt[Fk]
                if c and self.seen[E].get(Fk, 0) < c:
                    self.seen[E][Fk] = c
                    self.e[E].wait_ge(self.sem[Fk], c)
            for b in self.dma_sems:
                key = ("dma", id(b))
                if b.dcnt and self.seen[E].get(key, 0) < b.dcnt:
                    self.seen[E][key] = b.dcnt
                    self.e[E].wait_ge(b.dsem, b.dcnt)

    def finish(self):
        self.barrier()


D = 1024
DFF = 2816
NFC = DFF // 128
NKC = D // 128
EPS = 1e-6


def dram_in(nc, name, shape, dt=F32):
    return nc.dram_tensor(name, list(shape), dt, kind="ExternalInput").ap()


def dram_out(nc, name, shape, dt=F32):
    return nc.dram_tensor(name, list(shape), dt, kind="ExternalOutput").ap()


class RR:
    def __init__(self, items):
        self.items = items
        self.i = 0

    def next(self):
        b = self.items[self.i % len(self.items)]
        self.i += 1
        return b


def emit_rmsnorm_T(s, epsc, xt, gcol, hT, ident_b, pT_rr, hb_rr, scr, stat_rr, ntile, evac_engs=("dve", "act")):
    nc = s.nc
    hbs = []
    for j in range(ntile):
        st = stat_rr.next()
        hb = hb_rr.next()
        s.op("act", lambda: nc.scalar.activation(out=scr[:, :], in_=xt[j][:, :], func=AF.Square, scale=1.0 / 32.0,
                                                 accum_out=st[:, 0:1]),
             reads=[xt[j]], writes=[scr, st])
        s.op("act", lambda: nc.scalar.activation(out=st[:, 1:2], in_=st[:, 0:1], func=AF.Sqrt, bias=epsc[:, 0:1], scale=1.0),
             reads=[st, epsc], writes=[st])
        s.op("dve", lambda: nc.vector.reciprocal(out=st[:, 2:3], in_=st[:, 1:2]), reads=[st], writes=[st])
        s.op("act", lambda: nc.scalar.activation(out=hb[:, :], in_=xt[j][:, :], func=AF.Copy, scale=st[:, 2:3]),
             reads=[xt[j], st], writes=[hb])
        hbs.append(hb)
    k = 0
    for c in range(NKC):
        pT = pT_rr.next()
        for j in range(ntile):
            s.op("pe", lambda: nc.tensor.transpose(out=pT[:, j * 128:(j + 1) * 128], in_=hbs[j][:, c * 128:(c + 1) * 128],
                                                   identity=ident_b[:, :]),
                 reads=[hbs[j], ident_b], writes=[pT], inc=(j == ntile - 1))
        eng = evac_engs[k % len(evac_engs)]
        k += 1
        if eng == "dve":
            s.op("dve", lambda: nc.vector.tensor_scalar(out=hT[:, c, 0:ntile * 128], in0=pT[:, 0:ntile * 128],
                                                        scalar1=gcol[:, c:c + 1], scalar2=None, op0=ALU.mult),
                 reads=[pT, gcol], writes=[hT])
        else:
            s.op("act", lambda: nc.scalar.activation(out=hT[:, c, 0:ntile * 128], in_=pT[:, 0:ntile * 128],
                                                     func=AF.Copy, scale=gcol[:, c:c + 1]),
                 reads=[pT, gcol], writes=[hT])


def build_ffn(NT, TB=512):
    nc = bass.Bass("TRN2", target_bir_lowering=False)
    x = dram_in(nc, "x", [NT, D])
    g = dram_in(nc, "g", [D])
    w_in = dram_in(nc, "w_in", [D, 2 * DFF])
    w_out = dram_in(nc, "w_out", [DFF, D])
    ident = dram_in(nc, "ident", [128, 128], BF16)
    y = dram_out(nc, "y", [NT, D])
    s = S(nc)
    emit_ffn(s, x, g, w_in, w_out, ident, y, NT, TB)
    s.finish()
    s.close()
    return nc


def emit_ffn(s, x, g, w_in, w_out, ident, y, NT, TB=512):
    nc = s.nc
    ntile = TB // 128
    m_ = s.mark()
    ident_b = s.sb("ident_b", [128, 128], BF16)
    s.dma("sp", ident_b[:, :], ident[:, :], writes=[ident_b])
    gcol = s.sb("gcol", [128, NKC], F32)
    epsc = s.sb("epsc", [128, 1], F32)
    s.op("dve", lambda: nc.vector.memset(epsc[:, :], EPS), writes=[epsc])
    s.dma("sp", gcol[:, :], g.rearrange("(c p) -> p c", p=128), writes=[gcol], allow_slow_non_contiguous=True)
    win_b = [s.sb("win_b%d" % c, [128, 2 * DFF], BF16) for c in range(NKC)]
    wout_b = [s.sb("wout_b%d" % f, [128, D], BF16) for f in range(NFC)]
    for c in range(NKC):
        for hf in range(2):
            s.dma("pool", win_b[c][:, hf * DFF:(hf + 1) * DFF], w_in[c * 128:(c + 1) * 128, hf * DFF:(hf + 1) * DFF],
                  writes=[win_b[c]])
    for f in range(NFC):
        s.dma("pool", wout_b[f][:, :], w_out[f * 128:(f + 1) * 128, :], writes=[wout_b[f]])
    xt = [s.sb("xt%d" % j, [128, D], F32) for j in range(ntile)]
    hb_rr = RR([s.sb("hb%d" % j, [128, D], BF16) for j in range(ntile)])
    scr = s.sb("scr", [128, D], BF16)
    stat_rr = RR([s.sb("st%d" % j, [128, 4], F32) for j in range(4)])
    hT = s.sb("hT", [128, NKC, TB], BF16)
    pT_rr = RR([s.ps("pT%d" % j, [128, TB], BF16) for j in range(2)])
    pa_rr = RR([s.ps("pa%d" % j, [128, TB], F32) for j in range(2)])
    pb_rr = RR([s.ps("pb%d" % j, [128, TB], F32) for j in range(2)])
    po_rr = RR([s.ps("po%d" % j, [128, 512], F32) for j in range(2)])
    sa_rr = RR([s.sb("sa%d" % j, [128, TB], F32) for j in range(2)])
    act = [s.sb("actT%d" % f, [128, TB], BF16) for f in range(NFC)]
    for tb in range(NT // TB):
        for j in range(ntile):
            r0 = tb * TB + j * 128
            s.dma("sp", xt[j][:, :], x[r0:r0 + 128, :], writes=[xt[j]])
        emit_rmsnorm_T(s, epsc, xt, gcol, hT, ident_b, pT_rr, hb_rr, scr, stat_rr, ntile)
        for f in range(NFC):
            pa = pa_rr.next()
            pb = pb_rr.next()
            for c in range(NKC):
                s.op("pe", lambda: nc.tensor.matmul(pa[:, :], lhsT=win_b[c][:, f * 128:(f + 1) * 128], rhs=hT[:, c, :],
                                                    start=(c == 0), stop=(c == NKC - 1)),
                     reads=[win_b[c], hT], writes=[pa], inc=(c == NKC - 1))
            for c in range(NKC):
                s.op("pe", lambda: nc.tensor.matmul(pb[:, :], lhsT=win_b[c][:, DFF + f * 128:DFF + (f + 1) * 128],
                                                    rhs=hT[:, c, :], start=(c == 0), stop=(c == NKC - 1)),
                     reads=[win_b[c], hT], writes=[pb], inc=(c == NKC - 1))
            sa = sa_rr.next()
            s.op("act", lambda: nc.scalar.activation(out=sa[:, :], in_=pa[:, :], func=AF.Silu), reads=[pa], writes=[sa])
            s.op("dve", lambda: nc.vector.tensor_tensor(out=act[f][:, :], in0=sa[:, :], in1=pb[:, :], op=ALU.mult),
                 reads=[sa, pb], writes=[act[f]])
        for j in range(ntile):
            for hf in range(2):
                po = po_rr.next()
                for f in range(NFC):
                    s.op("pe", lambda: nc.tensor.matmul(po[:, :], lhsT=act[f][:, j * 128:(j + 1) * 128],
                                                        rhs=wout_b[f][:, hf * 512:(hf + 1) * 512],
                                                        start=(f == 0), stop=(f == NFC - 1)),
                         reads=[act[f], wout_b[f]], writes=[po], inc=(f == NFC - 1))
                s.op("dve", lambda: nc.vector.scalar_tensor_tensor(out=xt[j][:, hf * 512:(hf + 1) * 512], in0=po[:, :],
                                                                   scalar=0.5, in1=xt[j][:, hf * 512:(hf + 1) * 512],
                                                                   op0=ALU.mult, op1=ALU.add),
                     reads=[po, xt[j]], writes=[xt[j]])
            r0 = tb * TB + j * 128
            s.dma("sp", y[r0:r0 + 128, :], xt[j][:, :], reads=[xt[j]])
    s.release(m_)


FM_SRC = [[(0, 128)], [(128, 128)], [(256, 128)], [(384, 128)],
          [(768, 128)], [(896, 128)], [(1024, 128)], [(1152, 128)],
          [(1540, 128)], [(1668, 128)], [(1924, 64), (2052, 64)], [(1796, 128)]]
FM_GCOL = [0, 0, 1, 1, 2, 2, 3, 3, 4, 4, 5, None]
FM_BLK = [0, 0, 0, 0, 1, 1, 1, 1, 1, 1, 1, None]
TM_SRC = [[(512, 256), (1280, 256)],
          [(1536, 4), (2180, 12), (1988, 64), (2116, 64)],
          [(2192, 512)]]
NFM = 12
GELU_C = 1.5957691216057308


def build_proj(NT, TB=512):
    nc = bass.Bass("TRN2", target_bir_lowering=False)
    a = dict(
        x=dram_in(nc, "x", [NT, D]), g=dram_in(nc, "g", [D]), w_in=dram_in(nc, "w_in", [D, 6800]),
        ident=dram_in(nc, "ident", [128, 128], BF16), gains=dram_in(nc, "gains", [128, 6]),
        blk=dram_in(nc, "blk", [2, 128, 128], BF16), vgain=dram_in(nc, "vgain", [128, 256]),
        wsT=dram_in(nc, "wsT", [4, 128, 128]), triu=dram_in(nc, "triu", [128, 128]), bsT=dram_in(nc, "bsT", [128, 4]),
        zfm=dram_out(nc, "zfm", [NFM, 128, NT], BF16), vab=dram_out(nc, "vab", [NT, 512], BF16),
        vsw=dram_out(nc, "vsw", [NT, 128], BF16), misc=dram_out(nc, "misc", [NT, 16]), od=dram_out(nc, "od", [NT, 256], BF16))
    s = S(nc)
    emit_proj(s, a, NT, TB)
    s.finish()
    s.close()
    return nc


def emit_proj(s, a, NT, TB=512):
    nc = s.nc
    x, g, w_in, ident, gains, blk, vgain, wsT, triu, bsT = (a[k] for k in
                                                            ("x", "g", "w_in", "ident", "gains", "blk", "vgain", "wsT", "triu", "bsT"))
    zfm, vab, vsw, misc, od = (a[k] for k in ("zfm", "vab", "vsw", "misc", "od"))
    m_ = s.mark()
    ntile = TB // 128
    ident_b = s.sb("ident_b", [128, 128], BF16)
    s.dma("sp", ident_b[:, :], ident[:, :], writes=[ident_b])
    gcol = s.sb("gcol", [128, NKC], F32)
    s.dma("sp", gcol[:, :], g.rearrange("(c p) -> p c", p=128), writes=[gcol], allow_slow_non_contiguous=True)
    epsc = s.sb("epsc", [128, 1], F32)
    s.op("dve", lambda: nc.vector.memset(epsc[:, :], EPS), writes=[epsc])
    gn = s.sb("gn", [128, 6], F32)
    s.dma("sp", gn[:, :], gains[:, :], writes=[gn])
    for col, sc in ((0, 32.0 ** -0.5), (2, 0.125), (4, 0.125)):
        s.op("dve", lambda: nc.vector.tensor_scalar(out=gn[:, col:col + 1], in0=gn[:, col:col + 1], scalar1=sc,
                                                    scalar2=None, op0=ALU.mult), reads=[gn], writes=[gn])
    blk_b = [s.sb("blk%d" % i, [128, 128], BF16) for i in range(2)]
    for i in range(2):
        s.dma("sp", blk_b[i][:, :], blk[i], writes=[blk_b[i]])
    vg = s.sb("vg", [128, 256], F32)
    s.dma("sp", vg[:, :], vgain[:, :], writes=[vg])
    bcol = s.sb("bcol", [128, 4], F32)
    s.dma("sp", bcol[:, :], bsT[:, :], writes=[bcol])
    tri = s.sb("tri", [128, 128], F32)
    s.dma("sp", tri[:, :], triu[:, :], writes=[tri])
    wm = []
    wtmp = s.sb("wtmp", [128, 128], F32)
    for gi in range(4):
        w = s.sb("wm%d" % gi, [128, 128], BF16)
        s.dma("sp", wtmp[:, :], wsT[gi], writes=[wtmp])
        s.op("dve", lambda: nc.vector.tensor_tensor(out=w[:, :], in0=wtmp[:, :], in1=tri[:, :], op=ALU.mult),
             reads=[wtmp, tri], writes=[w])
        wm.append(w)
    wfm = [s.sb("wfm%d" % c, [128, NFM * 128], BF16) for c in range(NKC)]
    wtm = [s.sb("wtm%d" % c, [128, 1168], BF16) for c in range(NKC)]
    for c in range(NKC):
        for i, srcs in enumerate(FM_SRC):
            o = i * 128
            for (c0, n) in srcs:
                s.dma("pool", wfm[c][:, o:o + n], w_in[c * 128:(c + 1) * 128, c0:c0 + n], writes=[wfm[c]])
                o += n
        o = 0
        for srcs in TM_SRC:
            for (c0, n) in srcs:
                s.dma("pool", wtm[c][:, o:o + n], w_in[c * 128:(c + 1) * 128, c0:c0 + n], writes=[wtm[c]])
                o += n
    xt = [s.sb("xt%d" % j, [128, D], F32) for j in range(ntile)]
    hb_rr = RR([s.sb("hb%d" % j, [128, D], BF16) for j in range(ntile)])
    scr = s.sb("scr", [128, D], BF16)
    stat_rr = RR([s.sb("st%d" % j, [128, 4], F32) for j in range(4)])
    hT = s.sb("hT", [128, NKC, TB], BF16)
    pT_rr = RR([s.ps("pT%d" % j, [128, TB], BF16) for j in range(2)])
    pz_rr = RR([s.ps("pz%d" % j, [128, 512], F32) for j in range(3)])
    pq_rr = RR([s.ps("pq%d" % j, [128, 512], F32) for j in range(2)])
    sq_rr = RR([s.sb("sq%d" % j, [128, TB], BF16) for j in range(2)])
    rs_rr = RR([s.sb("rs%d" % j, [128, TB], F32) for j in range(2)])
    zo_rr = RR([s.sb("zo%d" % j, [128, TB], BF16) for j in range(3)])
    vab_rr = RR([s.sb("vabt%d" % j, [128, 512], BF16) for j in range(2)])
    vsw_rr = RR([s.sb("vswt%d" % j, [128, 128], BF16) for j in range(2)])
    msc_rr = RR([s.sb("msct%d" % j, [128, 16], F32) for j in range(2)])
    f_rr = RR([s.sb("gf%d" % j, [128, 512], F32) for j in range(4)])
    ge_rr = RR([s.sb("ge%d" % j, [128, 512], F32) for j in range(2)])
    vn_rr = RR([s.sb("vn%d" % j, [128, 256], BF16) for j in range(2)])
    od_rr = RR([s.sb("odt%d" % j, [128, 256], BF16) for j in range(2)])
    for tb in range(NT // TB):
        t0 = tb * TB
        for j in range(ntile):
            s.dma("sp", xt[j][:, :], x[t0 + j * 128:t0 + (j + 1) * 128, :], writes=[xt[j]])
        emit_rmsnorm_T(s, epsc, xt, gcol, hT, ident_b, pT_rr, hb_rr, scr, stat_rr, ntile)
        for i in range(NFM):
            pz = pz_rr.next()
            for c in range(NKC):
                s.op("pe", lambda: nc.tensor.matmul(pz[:, 0:TB], lhsT=wfm[c][:, i * 128:(i + 1) * 128], rhs=hT[:, c, :],
                                                    start=(c == 0), stop=(c == NKC - 1)),
                     reads=[wfm[c], hT], writes=[pz], inc=(c == NKC - 1))
            zo = zo_rr.next()
            if FM_GCOL[i] is None:
                s.op("act", lambda: nc.scalar.copy(out=zo[:, :], in_=pz[:, 0:TB]), reads=[pz], writes=[zo])
            else:
                gs = 32.0 if FM_BLK[i] == 0 else 64.0
                sq = sq_rr.next()
                s.op("act", lambda: nc.scalar.activation(out=sq[:, :], in_=pz[:, 0:TB], func=AF.Square),
                     reads=[pz], writes=[sq])
                pq = pq_rr.next()
                s.op("pe", lambda: nc.tensor.matmul(pq[:, 0:TB], lhsT=blk_b[FM_BLK[i]][:, :], rhs=sq[:, :],
                                                    start=True, stop=True), reads=[blk_b[FM_BLK[i]], sq], writes=[pq])
                rs = rs_rr.next()
                s.op("act", lambda: nc.scalar.activation(out=rs[:, :], in_=pq[:, 0:TB], func=AF.Sqrt, bias=epsc[:, 0:1],
                                                         scale=1.0 / gs), reads=[pq, epsc], writes=[rs])
                s.op("dve", lambda: nc.vector.reciprocal(out=rs[:, :], in_=rs[:, :]), reads=[rs], writes=[rs])
                gc = FM_GCOL[i]
                s.op("dve", lambda: nc.vector.scalar_tensor_tensor(out=zo[:, :], in0=pz[:, 0:TB], scalar=gn[:, gc:gc + 1],
                                                                   in1=rs[:, :], op0=ALU.mult, op1=ALU.mult),
                     reads=[pz, gn, rs], writes=[zo])
            s.dma("sp", zfm[i, :, t0:t0 + TB], zo[:, :], reads=[zo])
        for j in range(ntile):
            r0 = t0 + j * 128
            pz = pz_rr.next()
            for c in range(NKC):
                s.op("pe", lambda: nc.tensor.matmul(pz[:, :], lhsT=hT[:, c, j * 128:(j + 1) * 128], rhs=wtm[c][:, 0:512],
                                                    start=(c == 0), stop=(c == NKC - 1)),
                     reads=[wtm[c], hT], writes=[pz], inc=(c == NKC - 1))
            vt = vab_rr.next()
            s.op("act", lambda: nc.scalar.copy(out=vt[:, :], in_=pz[:, :]), reads=[pz], writes=[vt])
            s.dma("sp", vab[r0:r0 + 128, :], vt[:, :], reads=[vt])
            pz = pz_rr.next()
            for c in range(NKC):
                s.op("pe", lambda: nc.tensor.matmul(pz[:, 0:144], lhsT=hT[:, c, j * 128:(j + 1) * 128], rhs=wtm[c][:, 512:656],
                                                    start=(c == 0), stop=(c == NKC - 1)),
                     reads=[wtm[c], hT], writes=[pz], inc=(c == NKC - 1))
            mt = msc_rr.next()
            vs_ = vsw_rr.next()
            s.op("dve", lambda: nc.vector.tensor_copy(out=mt[:, :], in_=pz[:, 0:16]), reads=[pz], writes=[mt])
            s.op("dve", lambda: nc.vector.tensor_copy(out=vs_[:, :], in_=pz[:, 16:144]), reads=[pz], writes=[vs_])
            s.dma("sp", misc[r0:r0 + 128, :], mt[:, :], reads=[mt])
            s.dma("sp", vsw[r0:r0 + 128, :], vs_[:, :], reads=[vs_])
            pz = pz_rr.next()
            for c in range(NKC):
                s.op("pe", lambda: nc.tensor.matmul(pz[:, :], lhsT=hT[:, c, j * 128:(j + 1) * 128], rhs=wtm[c][:, 656:1168],
                                                    start=(c == 0), stop=(c == NKC - 1)),
                     reads=[wtm[c], hT], writes=[pz], inc=(c == NKC - 1))
            z2 = f_rr.next()
            s.op("act", lambda: nc.scalar.activation(out=z2[:, :], in_=pz[:, :], func=AF.Square), reads=[pz], writes=[z2])
            s.op("dve", lambda: nc.vector.tensor_scalar(out=z2[:, :], in0=z2[:, :], scalar1=0.044715, scalar2=1.0,
                                                        op0=ALU.mult, op1=ALU.add), reads=[z2], writes=[z2])
            s.op("dve", lambda: nc.vector.tensor_tensor(out=z2[:, :], in0=z2[:, :], in1=pz[:, :], op=ALU.mult),
                 reads=[z2, pz], writes=[z2])
            s.op("act", lambda: nc.scalar.activation(out=z2[:, :], in_=z2[:, :], func=AF.Sigmoid, scale=GELU_C),
                 reads=[z2], writes=[z2])
            ge = ge_rr.next()
            s.op("dve", lambda: nc.vector.tensor_tensor(out=ge[:, :], in0=z2[:, :], in1=pz[:, :], op=ALU.mult),
                 reads=[z2, pz], writes=[ge])
            sqv = f_rr.next()
            st = stat_rr.next()
            s.op("act", lambda: nc.scalar.activation(out=sqv[:, 0:256], in_=ge[:, 256:512], func=AF.Square),
                 reads=[ge], writes=[sqv])
            s.op("dve", lambda: nc.vector.tensor_reduce(out=st[:, 0:4], in_=sqv[:, 0:256].rearrange("p (g d) -> p g d", g=4),
                                                        axis=AX.X, op=ALU.add), reads=[sqv], writes=[st])
            s.op("act", lambda: nc.scalar.activation(out=st[:, 0:4], in_=st[:, 0:4], func=AF.Sqrt, bias=epsc[:, 0:1],
                                                     scale=1.0 / 64.0), reads=[st, epsc], writes=[st])
            s.op("dve", lambda: nc.vector.reciprocal(out=st[:, 0:4], in_=st[:, 0:4]), reads=[st], writes=[st])
            vn = vn_rr.next()
            for gi in range(4):
                s.op("dve", lambda: nc.vector.scalar_tensor_tensor(
                    out=vn[:, gi * 64:(gi + 1) * 64], in0=ge[:, 256 + gi * 64:256 + (gi + 1) * 64], scalar=st[:, gi:gi + 1],
                    in1=vg[:, gi * 64:(gi + 1) * 64], op0=ALU.mult, op1=ALU.mult), reads=[ge, st, vg], writes=[vn])
            pq = pq_rr.next()
            for gi in range(4):
                s.op("pe", lambda: nc.tensor.matmul(pq[:, gi * 64:(gi + 1) * 64], lhsT=wm[gi][:, :],
                                                    rhs=vn[:, gi * 64:(gi + 1) * 64], start=True, stop=True),
                     reads=[wm[gi], vn], writes=[pq], inc=(gi == 3))
            ot = od_rr.next()
            for gi in range(4):
                s.op("dve", lambda: nc.vector.scalar_tensor_tensor(
                    out=ot[:, gi * 64:(gi + 1) * 64], in0=pq[:, gi * 64:(gi + 1) * 64], scalar=bcol[:, gi:gi + 1],
                    in1=ge[:, gi * 64:(gi + 1) * 64], op0=ALU.add, op1=ALU.mult), reads=[pq, bcol, ge], writes=[ot])
            s.dma("sp", od[r0:r0 + 128, :], ot[:, :], reads=[ot])
    s.release(m_)


def _bf(a):
    import ml_dtypes
    return np.ascontiguousarray(a).astype(ml_dtypes.bfloat16)


def proj_consts():
    blk = np.zeros((2, 128, 128), np.float32)
    for i in range(128):
        for j in range(128):
            if i // 32 == j // 32:
                blk[0, i, j] = 1
            if i // 64 == j // 64:
                blk[1, i, j] = 1
    triu = np.triu(np.ones((128, 128), np.float32))
    return dict(ident=_bf(np.eye(128, dtype=np.float32)), blk=_bf(blk), triu=triu)


def proj_params(g, w_in, dq, dk, fq, fk, nq, nk, vgain, w_s, b_s):
    gains = np.stack([np.tile(dq, 4), np.tile(dk, 4), np.tile(fq, 2), np.tile(fk, 2), np.tile(nq, 2), np.tile(nk, 2)], 1)
    return dict(g=np.ascontiguousarray(g), w_in=np.ascontiguousarray(w_in), gains=np.ascontiguousarray(gains, dtype=np.float32),
                vgain=np.ascontiguousarray(np.broadcast_to(vgain[None, :], (128, 256))),
                wsT=np.ascontiguousarray(w_s.transpose(0, 2, 1)), bsT=np.ascontiguousarray(b_s.T))


NEG = -30000.0


def load_vt(s, vt, io, key, T):
    nc = s.nc
    NT = T // 128
    if key + "_src" in io:
        src = io[key + "_src"]
        s.op("pool", lambda: nc.gpsimd.memset(vt[:, :, 64:65], 1.0), writes=[vt])
        step = 8
        for j0 in range(0, NT, step):
            j1 = min(NT, j0 + step)
            s.dma("sp", vt[:, j0:j1, 0:64], src[j0 * 128:j1 * 128, :].rearrange("(j p) d -> p j d", p=128), writes=[vt])
    else:
        s.dma("sp", vt[:, :, :], io[key][:, :, :], writes=[vt])


class Pipe:
    def __init__(self, lag):
        self.q = []
        self.lag = lag

    def push(self, fn):
        self.q.append(fn)
        while len(self.q) > self.lag:
            self.q.pop(0)()

    def flush(self):
        while self.q:
            self.q.pop(0)()


def emit_attn_phase(s, cm, T, nsub, qT, kT, vt, kparts, out_dram, finalize, bias_fn=None, name="a", lag=2):
    nc = s.nc
    NQB = T // 512
    pipe = Pipe(lag)
    fin_pending = None
    for qb in range(NQB):
        q0 = qb * 512
        pos = [cm["po_rr"].next() for _ in range(nsub)]
        nt = 4 * qb + 4
        njob = 0
        for t in range(nt):
            di = t - 4 * qb
            c0 = 128 * di if di > 0 else 0
            for i in range(nsub):
                p0, p1 = kparts[i]
                ps = cm["ps_rr"].next()
                s.op("pe", lambda: nc.tensor.matmul(ps[:, c0:512], lhsT=kT[p0:p1, t * 128:(t + 1) * 128],
                                                    rhs=qT[p0:p1, q0 + c0:q0 + 512], start=True, stop=(di < 0)),
                     reads=[kT, qT], writes=[ps], inc=(di < 0))
                if di >= 0:
                    s.op("pe", lambda: nc.tensor.matmul(ps[:, c0:c0 + 128], lhsT=cm["ident_b"][:, :], rhs=cm["tri_b"][:, :],
                                                        start=False, stop=True),
                         reads=[cm["ident_b"], cm["tri_b"]], writes=[ps])
                pt = cm["pt_rr"].next()
                if bias_fn is None:
                    s.op("act", lambda: nc.scalar.activation(out=pt[:, c0:512], in_=ps[:, c0:512], func=AF.Exp),
                         reads=[ps], writes=[pt])
                else:
                    bb, bap = bias_fn(qb, t)
                    s.op("act", lambda: nc.scalar.activation(out=pt[:, c0:512], in_=ps[:, c0:512], func=AF.Exp, bias=bap),
                         reads=[ps, bb], writes=[pt])

                def pv(po=pos[i], t=t, c0=c0, pt=pt, nt=nt):
                    s.op("pe", lambda: nc.tensor.matmul(po[0:65, c0:512], lhsT=vt[:, t, :], rhs=pt[:, c0:512],
                                                        start=(t == 0), stop=(t == nt - 1)),
                         reads=[vt, pt], writes=[po])
                pipe.push(pv)
                njob += 1
                if fin_pending is not None and njob == lag:
                    fin_pending()
                    fin_pending = None
        pipe.flush()
        if fin_pending is not None:
            fin_pending()
        fin_pending = (lambda qb=qb, pos=pos: finalize(qb, pos))
    if fin_pending is not None:
        fin_pending()


def emit_o_to_tokmajor(s, cm, po, pf, col0):
    nc = s.nc
    oc = cm["oc_rr"].next()
    s.op("act", lambda: nc.scalar.copy(out=oc[0:65, :], in_=po[0:65, :]), reads=[po], writes=[oc])
    for j in range(4):
        s.op("pe", lambda: nc.tensor.transpose(out=pf[:, j, col0:col0 + 65], in_=oc[0:65, j * 128:(j + 1) * 128],
                                               identity=cm["ident_f"][0:65, 0:65]),
             reads=[oc, cm["ident_f"]], writes=[pf], inc=(j == 3))


def build_mix_ab(T):
    nc = bass.Bass("TRN2", target_bir_lowering=False)
    io = mix_decl(nc, T, with_c=False)
    s = S(nc)
    cm = mix_common(s, io)
    emit_mix_a(s, cm, io, T)
    emit_mix_b(s, cm, io, T)
    s.finish()
    s.close()
    return nc


def mix_decl(nc, T, with_c=True):
    NT = T // 128
    io = dict(
        identb=dram_in(nc, "identb", [128, 128], BF16), identf=dram_in(nc, "identf", [128, 128]),
        trib=dram_in(nc, "trib", [128, 128], BF16),
        qa=dram_in(nc, "qa", [64, T], BF16), ka=dram_in(nc, "ka", [64, T], BF16), va=dram_in(nc, "va", [128, NT, 65], BF16),
        lamp=dram_in(nc, "lamp", [128, 4, 32]), lami=dram_in(nc, "lami", [128, 2]),
        qb=dram_in(nc, "qb", [64, T], BF16), kb=dram_in(nc, "kb", [64, T], BF16), vb=dram_in(nc, "vb", [128, NT, 65], BF16),
        flog=dram_in(nc, "flog", [128, NT]), fbias=dram_in(nc, "fbias", [128, 1]),
        triuf=dram_in(nc, "triuf", [128, 128]), onesf=dram_in(nc, "onesf", [128, 128]),
        oa=dram_out(nc, "oa", [T, 64], BF16), ob=dram_out(nc, "ob", [T, 64], BF16),
    )
    return io


def mix_common(s, io):
    nc = s.nc
    cm = {}
    for nm, key, dt in (("ident_b", "identb", BF16), ("ident_f", "identf", F32), ("tri_b", "trib", BF16)):
        b = s.sb(nm, [128, 128], dt)
        s.dma("sp", b[:, :], io[key][:, :], writes=[b])
        cm[nm] = b
    cm["epsc"] = s.sb("epsc", [128, 1], F32)
    s.op("dve", lambda: nc.vector.memset(cm["epsc"][:, :], EPS), writes=[cm["epsc"]])
    cm["ps_rr"] = RR([s.ps("ps%d" % j, [128, 512], F32) for j in range(3)])
    cm["po_rr"] = RR([s.ps("po%d" % j, [128, 512], F32) for j in range(3)])
    cm["pf"] = s.ps("pf", [128, 4, 128], F32)
    cm["pf2"] = s.ps("pf2", [128, 4, 128], F32)
    cm["pt_rr"] = RR([s.sb("pt%d" % j, [128, 512], BF16) for j in range(4)])
    cm["oc_rr"] = RR([s.sb("oc%d" % j, [128, 512], F32) for j in range(2)])
    cm["st_rr"] = RR([s.sb("mst%d" % j, [128, 8], F32) for j in range(8)])
    cm["ot_rr"] = RR([s.sb("ot%d" % j, [128, 4, 64], BF16) for j in range(2)])
    cm["tmp_rr"] = RR([s.sb("tmp%d" % j, [128, 64], F32) for j in range(4)])
    return cm


def emit_mix_a(s, cm, io, T):
    nc = s.nc
    NT = T // 128
    m = s.mark()
    qT = s.sb("a_q", [64, T], BF16)
    kT = s.sb("a_k", [64, T], BF16)
    vt = s.sb("a_v", [128, NT, 65], BF16)
    s.dma("sp", qT[:, :], io["qa"][:, :], writes=[qT])
    s.dma("sp", kT[:, :], io["ka"][:, :], writes=[kT])
    load_vt(s, vt, io, "va", T)
    lp = s.sb("lp", [128, 4, 32], F32)
    li = s.sb("li", [128, 2], F32)
    lw = s.sb("lw", [128, 2, 32], F32)
    lam = s.sb("lam", [128, 4], F32)
    s.dma("sp", lp[:, :, :], io["lamp"][:, :, :], writes=[lp])
    s.dma("sp", li[:, :], io["lami"][:, :], writes=[li])
    s.op("dve", lambda: nc.vector.tensor_tensor(out=lw[:, 0, :], in0=lp[:, 0, :], in1=lp[:, 1, :], op=ALU.mult),
         reads=[lp], writes=[lw])
    s.op("dve", lambda: nc.vector.tensor_tensor(out=lw[:, 1, :], in0=lp[:, 2, :], in1=lp[:, 3, :], op=ALU.mult),
         reads=[lp], writes=[lw])
    s.op("dve", lambda: nc.vector.tensor_reduce(out=lam[:, 0:2], in_=lw[:, :, :], axis=AX.X, op=ALU.add),
         reads=[lw], writes=[lam])
    s.op("act", lambda: nc.scalar.activation(out=lam[:, 0:2], in_=lam[:, 0:2], func=AF.Exp), reads=[lam], writes=[lam])
    s.op("dve", lambda: nc.vector.tensor_tensor(out=lam[:, 2:3], in0=lam[:, 1:2], in1=lam[:, 0:1], op=ALU.subtract),
         reads=[lam], writes=[lam])
    s.op("dve", lambda: nc.vector.tensor_tensor(out=lam[:, 3:4], in0=lam[:, 2:3], in1=li[:, 0:1], op=ALU.subtract),
         reads=[lam, li], writes=[lam])

    def fin(qb, pos):
        pf = cm["pf"]
        pf2 = cm["pf2"]
        emit_o_to_tokmajor(s, cm, pos[0], pf, 0)
        emit_o_to_tokmajor(s, cm, pos[1], pf2, 0)
        ot = cm["ot_rr"].next()
        for j in range(4):
            st = cm["st_rr"].next()
            s.op("dve", lambda: nc.vector.tensor_scalar(out=st[:, 0:1], in0=pf[:, j, 64:65], scalar1=1e-30, scalar2=None,
                                                        op0=ALU.max), reads=[pf], writes=[st])
            s.op("dve", lambda: nc.vector.tensor_scalar(out=st[:, 1:2], in0=pf2[:, j, 64:65], scalar1=1e-30, scalar2=None,
                                                        op0=ALU.max), reads=[pf2], writes=[st])
            s.op("dve", lambda: nc.vector.reciprocal(out=st[:, 0:2], in_=st[:, 0:2]), reads=[st], writes=[st])
            t2 = cm["tmp_rr"].next()
            o = cm["tmp_rr"].next()
            s.op("dve", lambda: nc.vector.tensor_scalar(out=t2[:, :], in0=pf2[:, j, 0:64], scalar1=st[:, 1:2],
                                                        scalar2=lam[:, 3:4], op0=ALU.mult, op1=ALU.mult),
                 reads=[pf2, st, lam], writes=[t2])
            s.op("dve", lambda: nc.vector.scalar_tensor_tensor(out=o[:, :], in0=pf[:, j, 0:64], scalar=st[:, 0:1], in1=t2[:, :],
                                                               op0=ALU.mult, op1=ALU.add), reads=[pf, st, t2], writes=[o])
            s.op("act", lambda: nc.scalar.activation(out=t2[:, :], in_=o[:, :], func=AF.Square, accum_out=st[:, 2:3]),
                 reads=[o], writes=[t2, st])
            s.op("act", lambda: nc.scalar.activation(out=st[:, 3:4], in_=st[:, 2:3], func=AF.Ln, bias=cm["epsc"][:, 0:1],
                                                     scale=1.0 / 64.0), reads=[st, cm["epsc"]], writes=[st])
            s.op("act", lambda: nc.scalar.activation(out=st[:, 4:5], in_=st[:, 3:4], func=AF.Exp, scale=-0.5),
                 reads=[st], writes=[st])
            s.op("dve", lambda: nc.vector.tensor_scalar(out=ot[:, j, :], in0=o[:, :], scalar1=st[:, 4:5], scalar2=li[:, 1:2],
                                                        op0=ALU.mult, op1=ALU.mult), reads=[o, st, li], writes=[ot])
        s.dma("sp", io["oa"][qb * 512:(qb + 1) * 512, :].rearrange("(j p) d -> p j d", p=128), ot[:, :, :], reads=[ot])

    emit_attn_phase(s, cm, T, 2, qT, kT, vt, [(0, 32), (32, 64)], io["oa"], fin, name="a")
    s.release(m)


def emit_mix_b(s, cm, io, T):
    nc = s.nc
    NT = T // 128
    NQB = T // 512
    m = s.mark()
    qT = s.sb("b_q", [64, T], BF16)
    kT = s.sb("b_k", [64, T], BF16)
    vt = s.sb("b_v", [128, NT, 65], BF16)
    s.dma("sp", qT[:, :], io["qb"][:, :], writes=[qT])
    s.dma("sp", kT[:, :], io["kb"][:, :], writes=[kT])
    load_vt(s, vt, io, "vb", T)
    fl = s.sb("fl", [128, NT], F32)
    fb = s.sb("fb", [128, 2], F32)
    tu = s.sb("tu", [128, 128], F32)
    on = s.sb("on", [128, 128], F32)
    if "flog_sb" not in io:
        s.dma("sp", fl[:, :], io["flog"][:, :], writes=[fl])
    s.dma("sp", fb[:, 0:1], io["fbias"][:, :], writes=[fb])
    s.dma("sp", tu[:, :], io["triuf"][:, :], writes=[tu])
    s.dma("sp", on[:, :], io["onesf"][:, :], writes=[on])
    s.op("dve", lambda: nc.vector.tensor_scalar(out=fb[:, 1:2], in0=fb[:, 0:1], scalar1=-1.0, scalar2=None, op0=ALU.mult),
         reads=[fb], writes=[fb])
    if "flog_sb" in io:
        fsb, fap = io["flog_sb"]
        s.op("act", lambda: nc.scalar.activation(out=fl[:, :], in_=fap, func=AF.Exp, bias=fb[:, 1:2], scale=-1.0),
             reads=[fsb, fb], writes=[fl])
    else:
        s.op("act", lambda: nc.scalar.activation(out=fl[:, :], in_=fl[:, :], func=AF.Exp, bias=fb[:, 1:2], scale=-1.0),
             reads=[fl, fb], writes=[fl])
    s.op("act", lambda: nc.scalar.activation(out=fl[:, :], in_=fl[:, :], func=AF.Ln, bias=1.0, scale=1.0),
         reads=[fl], writes=[fl])
    pc = cm["pf"]
    pcv = pc[:, 0, :]
    s.op("pe", lambda: nc.tensor.matmul(pc[:, 0, 0:NT], lhsT=tu[:, :], rhs=fl[:, :], start=True, stop=True),
         reads=[tu, fl], writes=[pc])
    s.op("pe", lambda: nc.tensor.matmul(pc[:, 1, 0:NT], lhsT=on[:, :], rhs=fl[:, :], start=True, stop=True),
         reads=[on, fl], writes=[pc])
    cc = s.sb("cc", [128, NT], F32)
    inc_ = s.sb("inc", [128, NT], F32)
    tmpc = s.sb("tmpc", [128, NT], F32)
    s.op("dve", lambda: nc.vector.tensor_copy(out=inc_[:, :], in_=pc[:, 1, 0:NT]), reads=[pc], writes=[inc_])
    sh = 1
    while sh < NT:
        s.op("dve", lambda: nc.vector.tensor_copy(out=tmpc[:, :], in_=inc_[:, :]), reads=[inc_], writes=[tmpc])
        s.op("dve", lambda: nc.vector.tensor_tensor(out=inc_[:, sh:NT], in0=tmpc[:, sh:NT], in1=tmpc[:, 0:NT - sh], op=ALU.add),
             reads=[tmpc], writes=[inc_])
        sh *= 2
    s.op("dve", lambda: nc.vector.tensor_tensor(out=cc[:, :], in0=pc[:, 0, 0:NT], in1=inc_[:, :], op=ALU.add),
         reads=[pc, inc_], writes=[cc])
    s.op("dve", lambda: nc.vector.tensor_tensor(out=tmpc[:, :], in0=cc[:, :], in1=pc[:, 1, 0:NT], op=ALU.subtract),
         reads=[pc, cc], writes=[tmpc])
    btab = s.sb("btab", [128, NQB, NT], F32)
    for qb in range(NQB):
        s.op("dve", lambda: nc.vector.tensor_scalar(out=btab[:, qb, :], in0=tmpc[:, :], scalar1=inc_[:, 4 * qb + 1:4 * qb + 2],
                                                    scalar2=None, op0=ALU.subtract), reads=[tmpc, inc_], writes=[btab])

    def bias_fn(qb, t):
        return btab, btab[:, qb, t:t + 1]

    def fin(qb, pos):
        pf = cm["pf"]
        emit_o_to_tokmajor(s, cm, pos[0], pf, 0)
        ot = cm["ot_rr"].next()
        for j in range(4):
            st = cm["st_rr"].next()
            s.op("dve", lambda: nc.vector.tensor_scalar(out=st[:, 0:1], in0=pf[:, j, 64:65], scalar1=1e-30, scalar2=None,
                                                        op0=ALU.max), reads=[pf], writes=[st])
            s.op("dve", lambda: nc.vector.reciprocal(out=st[:, 0:1], in_=st[:, 0:1]), reads=[st], writes=[st])
            s.op("dve", lambda: nc.vector.tensor_scalar(out=ot[:, j, :], in0=pf[:, j, 0:64], scalar1=st[:, 0:1], scalar2=None,
                                                        op0=ALU.mult), reads=[pf, st], writes=[ot])
        s.dma("sp", io["ob"][qb * 512:(qb + 1) * 512, :].rearrange("(j p) d -> p j d", p=128), ot[:, :, :], reads=[ot])

    emit_attn_phase(s, cm, T, 1, qT, kT, vt, [(0, 64)], io["ob"], fin, bias_fn=bias_fn, name="b")
    s.release(m)


def mix_consts():
    k = np.arange(128)
    tri = np.where(k[:, None] > k[None, :], NEG, 0.0).astype(np.float32)
    return dict(identb=_bf(np.eye(128, dtype=np.float32)), identf=np.eye(128, dtype=np.float32), trib=_bf(tri),
                triuf=np.triu(np.ones((128, 128), np.float32)), onesf=np.ones((128, 128), np.float32))


def mix_decl_c(nc, io, T):
    NT = T // 128
    QL = NT // 4
    NCT = max(1, T // 2048)
    io.update(dict(
        qc=dram_in(nc, "qc", [128, QL, 512], BF16),
        kskw=dram_in(nc, "kskw", [128, T], BF16),
        vs=dram_in(nc, "vs", [128, NT, 65], BF16), vw=dram_in(nc, "vw", [128, NT, 65], BF16),
        kvin=dram_in(nc, "kvin", [128, T], BF16),
        w1=dram_in(nc, "w1", [2, 2048, 256]), b1=dram_in(nc, "b1", [128, 4]),
        peT=dram_in(nc, "peT", [128, 32]),
        w2=dram_in(nc, "w2", [2, 256, 64]), b2=dram_in(nc, "b2", [2, 64]), b2c=dram_in(nc, "b2c", [64, 1]),
        kgain=dram_in(nc, "kgain", [64, 1]),
        ng=dram_in(nc, "ng", [128, QL, 12]),
        cmask=dram_in(nc, "cmask", [128, QL, NCT, 128], BF16),
        smask=dram_in(nc, "smask", [128, 4, 128], BF16), wmask=dram_in(nc, "wmask", [128, 8, 128], BF16),
        impA=dram_in(nc, "impA", [128, QL, 128]), impB=dram_in(nc, "impB", [128, QL, 128]),
        emat=dram_in(nc, "emat", [128, NT, 128], BF16), ovl=dram_in(nc, "ovl", [128, NCT, 128], BF16),
        ones64=dram_in(nc, "ones64", [64, 64], BF16), onesrow=dram_in(nc, "onesrow", [1, 128], BF16),
        oc=dram_out(nc, "oc", [QL * 128, 256], BF16),
    ))
    return io


def emit_gelu(s, zin_ap, zin_b, out_ap, out_b, tmp, shape_sl):
    nc = s.nc
    t = tmp
    s.op("act", lambda: nc.scalar.activation(out=t[shape_sl], in_=zin_ap, func=AF.Square), reads=[zin_b], writes=[t])
    s.op("dve", lambda: nc.vector.tensor_scalar(out=t[shape_sl], in0=t[shape_sl], scalar1=0.044715, scalar2=1.0,
                                                op0=ALU.mult, op1=ALU.add), reads=[t], writes=[t])
    s.op("dve", lambda: nc.vector.tensor_tensor(out=t[shape_sl], in0=t[shape_sl], in1=zin_ap, op=ALU.mult),
         reads=[t, zin_b], writes=[t])
    s.op("act", lambda: nc.scalar.activation(out=t[shape_sl], in_=t[shape_sl], func=AF.Exp, scale=-GELU_C), reads=[t], writes=[t])
    s.op("dve", lambda: nc.vector.tensor_scalar(out=t[shape_sl], in0=t[shape_sl], scalar1=1.0, scalar2=None, op0=ALU.add),
         reads=[t], writes=[t])
    s.op("dve", lambda: nc.vector.reciprocal(out=t[shape_sl], in_=t[shape_sl]), reads=[t], writes=[t])
    s.op("dve", lambda: nc.vector.tensor_tensor(out=out_ap, in0=t[shape_sl], in1=zin_ap, op=ALU.mult),
         reads=[t, zin_b], writes=[out_b])


def emit_mix_c(s, cm, io, T, cs=None):
    fused = cs is not None
    cs = cs if fused else [None]
    nc = s.nc
    NT = T // 128
    QL = NT // 4
    NCT = max(1, T // 2048)
    Nc = T // 16 - 1
    NCP = NCT * 128 if Nc > 128 else 128
    NCW = min(Nc, 511)
    assert Nc <= 511
    m = s.mark()
    ident_b = cm["ident_b"]
    ps_l = cm["ps_rr"].items
    po_l = cm["po_rr"].items
    pf, pf2 = cm["pf"], cm["pf2"]

    def ld(name, shape, dt, src, q="sp"):
        b = s.sb(name, shape, dt)
        idx = tuple(slice(None) for _ in shape)
        s.dma(q, b[idx], src, writes=[b])
        return b

    qc = s.sb("c_q", [128, QL, 512], BF16)
    kk = ld("c_kk", [128, T], BF16, io["kskw"][:, :])
    vs = s.sb("c_vs", [128, NT, 65], BF16)
    vw = s.sb("c_vw", [128, NT, 65], BF16)
    load_vt(s, vs, io, "vs", T)
    load_vt(s, vw, io, "vw", T)
    emat = ld("c_e", [128, NT, 128], BF16, io["emat"][:, :, :])
    ovl = ld("c_ovl", [128, NCT, 128], BF16, io["ovl"][:, :, :])
    smask = s.sb("c_sm", [128, 4, 128], BF16)
    wmask = s.sb("c_wm", [128, 8, 128], BF16)
    ngt = s.sb("c_ng", [128, QL, 12], F32)
    ones64 = ld("c_o64", [64, 64], BF16, io["ones64"][:, :])
    onesrow = ld("c_orow", [1, 128], BF16, io["onesrow"][:, :])
    kgain = ld("c_kg", [64, 1], F32, io["kgain"][:, :])
    b2c = ld("c_b2c", [64, 1], F32, io["b2c"][:, :])
    b1 = ld("c_b1", [128, 4], F32, io["b1"][:, :])

    ktc = s.sb("c_ktc", [64, NCP], BF16)
    vc = s.sb("c_vc", [128, NCT, 65], BF16)
    s.op("dve", lambda: nc.vector.memset(ktc[:, :], 0.0), writes=[ktc])
    s.op("dve", lambda: nc.vector.memset(vc[:, :, :], 0.0), writes=[vc])
    s.op("dve", lambda: nc.vector.memset(vc[:, :, 64:65], 1.0), writes=[vc])

    m2 = s.mark()
    kvin = ld("c_kvin", [128, T], BF16, io["kvin"][:, :])
    w1sb = s.sb("c_w1", [128, 32, 256], BF16)
    for x in range(2):
        s.dma("pool", w1sb[x * 64:(x + 1) * 64, :, :], io["w1"][x].rearrange("(j d) f -> d j f", d=64), writes=[w1sb])
    peT = s.sb("c_pe", [128, 32], BF16)
    s.dma("pool", peT[:, :], io["peT"][:, :], writes=[peT])
    w2sb = s.sb("c_w2", [128, 2, 2, 64], BF16)
    for x in range(2):
        s.dma("pool", w2sb[:, x, :, :], io["w2"][x].rearrange("(hh f) d -> f hh d", f=128), writes=[w2sb])
    b2row = s.sb("c_b2r", [1, 64], BF16)
    s.dma("pool", b2row[:, :], io["b2"][1:2, :], writes=[b2row])
    hacc = [ps_l[0], ps_l[1], ps_l[2], po_l[0]]
    pcol = po_l[1]
    for x in range(2):
        for hh in range(2):
            hp = hacc[x * 2 + hh]
            for j in range(32):
                s.op("pe", lambda: nc.tensor.matmul(hp[:, 0:NCW], lhsT=w1sb[x * 64:(x + 1) * 64, j, hh * 128:(hh + 1) * 128],
                                                    rhs=kvin[x * 64:(x + 1) * 64, j:j + 16 * (NCW - 1) + 1:16],
                                                    start=(j == 0), stop=(j == 31)),
                     reads=[w1sb, kvin], writes=[hp], inc=(j == 31))
            for j in range(32):
                s.op("pe", lambda: nc.tensor.matmul(pcol[:, x * 2 + hh:x * 2 + hh + 1],
                                                    lhsT=w1sb[x * 64:(x + 1) * 64, j, hh * 128:(hh + 1) * 128],
                                                    rhs=peT[x * 64:(x + 1) * 64, j:j + 1], start=(j == 0), stop=(j == 31)),
                     reads=[w1sb, peT], writes=[pcol], inc=(j == 31))
    hbias = s.sb("c_hb", [128, 4], F32)
    s.op("dve", lambda: nc.vector.tensor_tensor(out=hbias[:, :], in0=pcol[:, 0:4], in1=b1[:, :], op=ALU.add),
         reads=[pcol, b1], writes=[hbias])
    gh = []
    for x in range(2):
        for hh in range(2):
            k = x * 2 + hh
            z = s.sb("c_z%d" % k, [128, 512], F32)
            tmp = s.sb("c_zt%d" % k, [128, 512], F32)
            gb = s.sb("c_g%d" % k, [128, 512], BF16)
            s.op("act", lambda: nc.scalar.activation(out=z[:, 0:NCW], in_=hacc[k][:, 0:NCW], func=AF.Identity,
                                                     bias=hbias[:, k:k + 1], scale=1.0), reads=[hacc[k], hbias], writes=[z])
            emit_gelu(s, z[:, 0:NCW], z, gb[:, 0:NCW], gb, tmp, (slice(None), slice(0, NCW)))
            gh.append(gb)
    pk = po_l[2]
    for hh in range(2):
        s.op("pe", lambda: nc.tensor.matmul(pk[0:64, 0:NCW], lhsT=w2sb[:, 0, hh, :], rhs=gh[hh][:, 0:NCW],
                                            start=(hh == 0), stop=(hh == 1)), reads=[w2sb, gh[hh]], writes=[pk], inc=(hh == 1))
    kz = s.sb("c_kz", [64, 512], F32)
    ksq = s.sb("c_ksq", [64, 512], BF16)
    krs = s.sb("c_krs", [64, 512], F32)
    s.op("act", lambda: nc.scalar.activation(out=kz[:, 0:NCW], in_=pk[0:64, 0:NCW], func=AF.Identity, bias=b2c[:, 0:1], scale=1.0),
         reads=[pk, b2c], writes=[kz])
    s.op("act", lambda: nc.scalar.activation(out=ksq[:, 0:NCW], in_=kz[:, 0:NCW], func=AF.Square), reads=[kz], writes=[ksq])
    pq = ps_l[0]
    s.op("pe", lambda: nc.tensor.matmul(pq[0:64, 0:NCW], lhsT=ones64[:, :], rhs=ksq[:, 0:NCW], start=True, stop=True),
         reads=[ones64, ksq], writes=[pq])
    s.op("act", lambda: nc.scalar.activation(out=krs[:, 0:NCW], in_=pq[0:64, 0:NCW], func=AF.Ln, bias=cm["epsc"][0:64, 0:1],
                                             scale=1.0 / 64.0), reads=[pq, cm["epsc"]], writes=[krs])
    s.op("act", lambda: nc.scalar.activation(out=krs[:, 0:NCW], in_=krs[:, 0:NCW], func=AF.Exp, scale=-0.5), reads=[krs], writes=[krs])
    s.op("dve", lambda: nc.vector.scalar_tensor_tensor(out=ktc[:, 0:NCW], in0=kz[:, 0:NCW], scalar=kgain[:, 0:1], in1=krs[:, 0:NCW],
                                                       op0=ALU.mult, op1=ALU.mult), reads=[kz, kgain, krs], writes=[ktc])
    for nt in range(NCT):
        n0 = nt * 128
        nn = min(128, Nc - n0)
        pv = ps_l[1 + nt % 2]
        for hh in range(2):
            s.op("pe", lambda: nc.tensor.matmul(pv[0:nn, 0:64], lhsT=gh[2 + hh][:, n0:n0 + nn], rhs=w2sb[:, 1, hh, :],
                                                start=(hh == 0), stop=False), reads=[gh[2 + hh], w2sb], writes=[pv], inc=False)
        s.op("pe", lambda: nc.tensor.matmul(pv[0:nn, 0:64], lhsT=onesrow[0:1, 0:nn], rhs=b2row[0:1, :], start=False, stop=True),
             reads=[onesrow, b2row], writes=[pv])
        s.op("act", lambda: nc.scalar.copy(out=vc[0:nn, nt, 0:64], in_=pv[0:nn, 0:64]), reads=[pv], writes=[vc])
    s.release(m2)

    cmk_rr = RR([s.sb("c_cmk%d" % j, [128, NCT, 128], BF16) for j in range(2)])
    ia_rr = RR([s.sb("c_ia%d" % j, [128, 128], F32) for j in range(2)])
    ib_rr = RR([s.sb("c_ib%d" % j, [128, 128], F32) for j in range(2)])
    imp_rr = RR([s.sb("c_imp%d" % j, [128, 128], F32) for j in range(2)])
    imp2_rr = RR([s.sb("c_impb%d" % j, [128, 128], F32) for j in range(2)])
    m8_rr = RR([s.sb("c_m8%d" % j, [128, 16], F32) for j in range(2)])
    mbT_rr = RR([s.sb("c_mbT%d" % j, [128, 128], BF16) for j in range(2)])
    oco_rr = RR([s.sb("c_oc%d" % j, [128, 4, 64], F32) for j in range(2)])
    gw_rr = RR([s.sb("c_gw%d" % j, [128, 12], F32) for j in range(2)])
    oo_rr = RR([s.sb("c_oo%d" % j, [128, 4, 64], F32) for j in range(2)])
    ob_rr = RR([s.sb("c_ob%d" % j, [128, 4, 64], BF16) for j in range(2)])

    pipe = Pipe(2)

    def masked_tile(kbuf, prow, t, Q, masks, vbuf, vt_idx, po, first, last, extra=None):
        ps = cm["ps_rr"].next()
        nm = len(masks)
        s.op("pe", lambda: nc.tensor.matmul(ps[:, :], lhsT=kbuf[prow[0]:prow[1], t * 128:(t + 1) * 128], rhs=Q,
                                            start=True, stop=(nm == 0)), reads=[kbuf, qc], writes=[ps], inc=(nm == 0))
        for mi, (la, lb, ra, rb) in enumerate(masks):
            for h in range(4):
                lastm = (mi == nm - 1 and h == 3)
                s.op("pe", lambda: nc.tensor.matmul(ps[:, h * 128:(h + 1) * 128], lhsT=la, rhs=ra, start=False, stop=lastm),
                     reads=[lb, rb], writes=[ps], inc=lastm)
        pt = cm["pt_rr"].next()
        s.op("act", lambda: nc.scalar.activation(out=pt[:, :], in_=ps[:, :], func=AF.Exp), reads=[ps], writes=[pt])

        def back(pt=pt, po=po, vbuf=vbuf, vt_idx=vt_idx, first=first, last=last, extra=extra):
            s.op("pe", lambda: nc.tensor.matmul(po[0:65, :], lhsT=vbuf[:, vt_idx, :], rhs=pt[:, :], start=first, stop=last),
                 reads=[vbuf, pt], writes=[po])
            if extra is not None:
                extra(pt)
        pipe.push(back)

    for ci in cs:
        def gk(key):
            return io[key][ci] if fused else io[key]
        if fused:
            for i in range(QL):
                qt = 4 * i + ci
                for h in range(4):
                    srcq = io["zq"][h // 2][(h % 2) * 64:(h % 2) * 64 + 64, qt * 128:(qt + 1) * 128]
                    s.dma("sp", qc[0:64, i, h * 128:(h + 1) * 128], srcq, writes=[qc])
                    s.dma("sp", qc[64:128, i, h * 128:(h + 1) * 128], srcq, writes=[qc])
            msb, mview = io["misc_sb"]
            s.op("act", lambda: nc.scalar.activation(out=ngt[:, :, :], in_=mview[:, ci:NT:4, 4:16], func=AF.Exp, scale=-1.0),
                 reads=[msb], writes=[ngt])
        else:
            s.dma("sp", qc[:, :, :], io["qc"][:, :, :], writes=[qc])
            s.dma("sp", ngt[:, :, :], io["ng"][:, :, :], writes=[ngt])
            s.op("act", lambda: nc.scalar.activation(out=ngt[:, :, :], in_=ngt[:, :, :], func=AF.Exp, scale=-1.0),
                 reads=[ngt], writes=[ngt])
        s.op("dve", lambda: nc.vector.tensor_scalar(out=ngt[:, :, :], in0=ngt[:, :, :], scalar1=1.0, scalar2=None, op0=ALU.add),
             reads=[ngt], writes=[ngt])
        s.op("dve", lambda: nc.vector.reciprocal(out=ngt[:, :, :], in_=ngt[:, :, :]), reads=[ngt], writes=[ngt])
        s.dma("sp", smask[:, :, :], gk("smask")[:, :, :], writes=[smask])
        s.dma("sp", wmask[:, :, :], gk("wmask")[:, :, :], writes=[wmask])
        for i in range(QL):
            Qlo = qc[0:64, i, :]
            Qhi = qc[64:128, i, :]
            cmk = cmk_rr.next()
            ia = ia_rr.next()
            ib = ib_rr.next()
            s.dma("sp", cmk[:, :, :], gk("cmask")[:, i, :, :], writes=[cmk])
            s.dma("sp", ia[:, :], gk("impA")[:, i, :], writes=[ia])
            s.dma("sp", ib[:, :], gk("impB")[:, i, :], writes=[ib])
            po_c, po_s, po_w = po_l[0], po_l[1], po_l[2]
            nct = min(NCT, i // 4 + 1)
            for nt in range(nct):
                def imp_mm(pt, nt=nt, nct=nct):
                    for h in range(4):
                        s.op("pe", lambda: nc.tensor.matmul(pf2[:, h, :], lhsT=pt[:, h * 128:(h + 1) * 128], rhs=ovl[:, nt, :],
                                                            start=(nt == 0 and h == 0), stop=(nt == nct - 1 and h == 3),
                                                            skip_group_check=True), reads=[pt, ovl], writes=[pf2],
                             inc=(h == 3))
                masked_tile(ktc, (0, 64), nt, Qlo, [(ident_b[:, :], ident_b, cmk[:, nt, :], cmk)], vc, nt, po_c,
                            nt == 0, nt == nct - 1, extra=imp_mm)
            pipe.flush()
            emit_o_to_tokmajor(s, cm, po_c, pf, 0)
            st = cm["st_rr"].next()
            rsum = cm["st_rr"].next()
            gw = gw_rr.next()
            s.op("dve", lambda: nc.vector.tensor_scalar(out=st[:, 0:4], in0=pf[:, :, 64], scalar1=1e-30, scalar2=None, op0=ALU.max),
                 reads=[pf], writes=[st])
            s.op("dve", lambda: nc.vector.reciprocal(out=rsum[:, 0:4], in_=st[:, 0:4]), reads=[st], writes=[rsum])
            oco = oco_rr.next()
            s.op("dve", lambda: nc.vector.tensor_copy(out=oco[:, :, :], in_=pf[:, :, 0:64]), reads=[pf], writes=[oco])
            imp = imp_rr.next()
            s.op("dve", lambda: nc.vector.tensor_scalar(out=imp[:, :], in0=pf2[:, 0, :], scalar1=rsum[:, 0:1], scalar2=None, op0=ALU.mult),
                 reads=[pf2, rsum], writes=[imp])
            for h in range(1, 4):
                s.op("dve", lambda: nc.vector.scalar_tensor_tensor(out=imp[:, :], in0=pf2[:, h, :], scalar=rsum[:, h:h + 1], in1=imp[:, :],
                                                                   op0=ALU.mult, op1=ALU.add), reads=[pf2, rsum, imp], writes=[imp])
            s.op("dve", lambda: nc.vector.tensor_tensor(out=imp[:, :], in0=imp[:, :], in1=ia[:, :], op=ALU.mult), reads=[imp, ia], writes=[imp])
            s.op("dve", lambda: nc.vector.tensor_tensor(out=imp[:, :], in0=imp[:, :], in1=ib[:, :], op=ALU.add), reads=[imp, ib], writes=[imp])
            m8 = m8_rr.next()
            imp2 = imp2_rr.next()
            s.op("dve", lambda: nc.vector.max(out=m8[:, 0:8], in_=imp[:, :]), reads=[imp], writes=[m8])
            s.op("dve", lambda: nc.vector.match_replace(out=imp2[:, :], in_to_replace=m8[:, 0:8], in_values=imp[:, :], imm_value=-1e9),
                 reads=[imp, m8], writes=[imp2])
            s.op("dve", lambda: nc.vector.max(out=m8[:, 8:16], in_=imp2[:, :]), reads=[imp2], writes=[m8])
            s.op("dve", lambda: nc.vector.tensor_scalar(out=imp2[:, :], in0=imp[:, :], scalar1=m8[:, 15:16], scalar2=NEG,
                                                        op0=ALU.is_lt, op1=ALU.mult), reads=[imp, m8], writes=[imp2])
            tl = [4 * (i - 1) + u for u in range(8) if 4 * (i - 1) + u >= 0]
            for t in tl:
                u = t - 4 * (i - 1)
                masked_tile(kk, (64, 128), t, Qhi, [(ident_b[:, :], ident_b, wmask[:, u, :], wmask)], vw, t, po_w,
                            t == tl[0], t == tl[-1])
            ptr = cm["ps_rr"].next()
            s.op("pe", lambda: nc.tensor.transpose(out=ptr[:, 0:128], in_=imp2[:, :], identity=cm["ident_f"][:, :]),
                 reads=[imp2, cm["ident_f"]], writes=[ptr])
            mbT = mbT_rr.next()
            s.op("act", lambda: nc.scalar.copy(out=mbT[:, :], in_=ptr[:, 0:128]), reads=[ptr], writes=[mbT])
            nts = 4 * i + 4
            for t in range(nts):
                masks = [(emat[:, t, :], emat, mbT[:, :], mbT)]
                if t >= 4 * i:
                    masks.append((ident_b[:, :], ident_b, smask[:, t - 4 * i, :], smask))
                masked_tile(kk, (0, 64), t, Qlo, masks, vs, t, po_s, t == 0, t == nts - 1)
            pipe.flush()
            emit_o_to_tokmajor(s, cm, po_s, pf, 0)
            st2 = cm["st_rr"].next()
            s.op("dve", lambda: nc.vector.tensor_scalar(out=st2[:, 0:4], in0=pf[:, :, 64], scalar1=1e-30, scalar2=None, op0=ALU.max),
                 reads=[pf], writes=[st2])
            s.op("dve", lambda: nc.vector.reciprocal(out=st2[:, 0:4], in_=st2[:, 0:4]), reads=[st2], writes=[st2])
            gv = ngt[:, i, :].rearrange("p (h b) -> p h b", b=3)
            gwv = gw[:, :].rearrange("p (h b) -> p h b", b=3)
            s.op("dve", lambda: nc.vector.tensor_tensor(out=gwv[:, :, 0], in0=gv[:, :, 0], in1=rsum[:, 0:4], op=ALU.mult),
                 reads=[ngt, rsum], writes=[gw])
            s.op("dve", lambda: nc.vector.tensor_tensor(out=gwv[:, :, 1], in0=gv[:, :, 1], in1=st2[:, 0:4], op=ALU.mult),
                 reads=[ngt, st2], writes=[gw])
            oo = oo_rr.next()
            for h in range(4):
                s.op("dve", lambda: nc.vector.tensor_scalar(out=oo[:, h, :], in0=oco[:, h, :], scalar1=gw[:, 3 * h:3 * h + 1], scalar2=None,
                                                            op0=ALU.mult), reads=[oco, gw], writes=[oo])
                s.op("dve", lambda: nc.vector.scalar_tensor_tensor(out=oo[:, h, :], in0=pf[:, h, 0:64], scalar=gw[:, 3 * h + 1:3 * h + 2],
                                                                   in1=oo[:, h, :], op0=ALU.mult, op1=ALU.add), reads=[pf, gw, oo], writes=[oo])
            emit_o_to_tokmajor(s, cm, po_w, pf, 0)
            st3 = cm["st_rr"].next()
            s.op("dve", lambda: nc.vector.tensor_scalar(out=st3[:, 0:4], in0=pf[:, :, 64], scalar1=1e-30, scalar2=None, op0=ALU.max),
                 reads=[pf], writes=[st3])
            s.op("dve", lambda: nc.vector.reciprocal(out=st3[:, 0:4], in_=st3[:, 0:4]), reads=[st3], writes=[st3])
            s.op("dve", lambda: nc.vector.tensor_tensor(out=gwv[:, :, 2], in0=gv[:, :, 2], in1=st3[:, 0:4], op=ALU.mult),
                 reads=[ngt, st3], writes=[gw])
            ob = ob_rr.next()
            for h in range(4):
                s.op("dve", lambda: nc.vector.scalar_tensor_tensor(out=ob[:, h, :], in0=pf[:, h, 0:64], scalar=gw[:, 3 * h + 2:3 * h + 3],
                                                                   in1=oo[:, h, :], op0=ALU.mult, op1=ALU.add), reads=[pf, gw, oo], writes=[ob])
            orow = ((4 * i + ci) if fused else i) * 128
            s.dma("sp", io["oc"][orow:orow + 128, :], ob[:, :, :].rearrange("p h d -> p (h d)"), reads=[ob])
    s.release(m)


def build_mix(T, parts="abc"):
    nc = bass.Bass("TRN2", target_bir_lowering=False)
    io = mix_decl(nc, T)
    if "c" in parts:
        mix_decl_c(nc, io, T)
    s = S(nc)
    cm = mix_common(s, io)
    if "a" in parts:
        emit_mix_a(s, cm, io, T)
    if "b" in parts:
        emit_mix_b(s, cm, io, T)
    if "c" in parts:
        emit_mix_c(s, cm, io, T)
    s.finish()
    s.close()
    return nc


def mix_consts_c(T, c):
    NT = T // 128
    QL = NT // 4
    NCT = max(1, T // 2048)
    Nc = T // 16 - 1
    NS = T // 64
    ar = np.arange(128)
    cmask = np.zeros((128, QL, NCT, 128), np.float32)
    impA = np.zeros((128, QL, 128), np.float32)
    impB = np.zeros((128, QL, 128), np.float32)
    for i in range(QL):
        qpos = 128 * (4 * i + c) + ar
        for nt in range(NCT):
            n = 128 * nt + ar
            ok = (16 * n[:, None] + 31 <= qpos[None, :]) & (n[:, None] < Nc)
            cmask[:, i, nt, :] = np.where(ok, 0.0, NEG)
        j = ar
        cur = qpos // 64
        forced = (j[None, :] == 0) | (j[None, :] == cur[:, None]) | (j[None, :] == cur[:, None] - 1)
        valid = (j[None, :] * 64 <= qpos[:, None]) & (j[None, :] < NS)
        impA[:, i, :] = (valid & ~forced).astype(np.float32)
        impB[:, i, :] = np.where(forced & (j[None, :] < NS), 1.0e4, np.where(valid, 0.0, -1.0))
    smask = np.zeros((128, 4, 128), np.float32)
    for u in range(4):
        kpos = 128 * u + ar
        qp = 128 * c + ar
        smask[:, u, :] = np.where(kpos[:, None] <= qp[None, :], 0.0, NEG)
    wmask = np.zeros((128, 8, 128), np.float32)
    for u in range(8):
        dist = 128 * (c + 4 - u) + ar[None, :] - ar[:, None]
        wmask[:, u, :] = np.where((dist >= 0) & (dist < 512), 0.0, NEG)
    emat = np.zeros((128, NT, 128), np.float32)
    for t in range(NT):
        for k in range(128):
            jj = 2 * t + k // 64
            if jj < 128:
                emat[jj, t, k] = 1.0
    ovl = np.zeros((128, NCT, 128), np.float32)
    for nt in range(NCT):
        n = 128 * nt + ar
        o = (n[:, None] * 16 < (ar[None, :] + 1) * 64) & (n[:, None] * 16 + 32 > ar[None, :] * 64) & (n[:, None] < Nc) \
            & (ar[None, :] < NS)
        ovl[:, nt, :] = o
    return dict(cmask=_bf(cmask), impA=impA, impB=impB, smask=_bf(smask), wmask=_bf(wmask), emat=_bf(emat), ovl=_bf(ovl),
                ones64=_bf(np.ones((64, 64), np.float32)), onesrow=_bf(np.ones((1, 128), np.float32)))


def build_merge(NT, TB=512):
    nc = bass.Bass("TRN2", target_bir_lowering=False)
    x = dram_in(nc, "x", [NT, D])
    g = dram_in(nc, "g", [D])
    w_in = dram_in(nc, "w_in", [D, 6800])
    w_br = dram_in(nc, "w_br", [4, 256, D])
    w_o = dram_in(nc, "w_o", [D, D])
    ident = dram_in(nc, "ident", [128, 128], BF16)
    obr = dram_in(nc, "obr", [NT, D], BF16)
    y = dram_out(nc, "y", [NT, D])
    s = S(nc)
    emit_merge(s, x, g, w_in, w_br, w_o, ident, obr, y, NT, TB)
    s.finish()
    s.close()
    return nc


def emit_merge(s, x, g, w_in, w_br, w_o, ident, obr, y, NT, TB=512):
    nc = s.nc
    m_ = s.mark()
    ntile = TB // 128
    ident_b = s.sb("ident_b", [128, 128], BF16)
    s.dma("sp", ident_b[:, :], ident[:, :], writes=[ident_b])
    gcol = s.sb("gcol", [128, NKC], F32)
    s.dma("sp", gcol[:, :], g.rearrange("(c p) -> p c", p=128), writes=[gcol], allow_slow_non_contiguous=True)
    epsc = s.sb("epsc", [128, 1], F32)
    s.op("dve", lambda: nc.vector.memset(epsc[:, :], EPS), writes=[epsc])
    wg = [s.sb("wg%d" % c, [128, 4096], BF16) for c in range(NKC)]
    wb = [s.sb("wb%d" % c, [128, D], BF16) for c in range(8)]
    wo = [s.sb("wo%d" % c, [128, D], BF16) for c in range(NKC)]
    for c in range(NKC):
        for hf in range(2):
            s.dma("pool", wg[c][:, hf * 2048:(hf + 1) * 2048], w_in[c * 128:(c + 1) * 128, 2704 + hf * 2048:2704 + (hf + 1) * 2048],
                  writes=[wg[c]])
    for n in range(4):
        for cc in range(2):
            s.dma("pool", wb[2 * n + cc][:, :], w_br[n, cc * 128:(cc + 1) * 128, :], writes=[wb[2 * n + cc]])
    for c in range(NKC):
        s.dma("pool", wo[c][:, :], w_o[c * 128:(c + 1) * 128, :], writes=[wo[c]])
    xt = [s.sb("xt%d" % j, [128, D], F32) for j in range(ntile)]
    ot = [s.sb("ot%d" % j, [128, D], BF16) for j in range(ntile)]
    hb_rr = RR([s.sb("hb%d" % j, [128, D], BF16) for j in range(ntile)])
    scr = s.sb("scr", [128, D], BF16)
    stat_rr = RR([s.sb("st%d" % j, [128, 4], F32) for j in range(4)])
    hT = s.sb("hT", [128, NKC, TB], BF16)
    oT = s.sb("oT", [128, 8, TB], BF16)
    mT = [s.sb("mT%d" % c, [128, TB], BF16) for c in range(8)]
    pT_rr = RR([s.ps("pT%d" % j, [128, TB], BF16) for j in range(2)])
    pg_rr = RR([s.ps("pg%d" % j, [128, 512], F32) for j in range(2)])
    pp_rr = RR([s.ps("pp%d" % j, [128, 512], F32) for j in range(2)])
    po_rr = RR([s.ps("po%d" % j, [128, 512], F32) for j in range(2)])
    sg_rr = RR([s.sb("sg%d" % j, [128, TB], F32) for j in range(3)])
    acc_rr = RR([s.sb("acc%d" % j, [128, TB], F32) for j in range(2)])
    for tb in range(NT // TB):
        t0 = tb * TB
        for j in range(ntile):
            s.dma("sp", xt[j][:, :], x[t0 + j * 128:t0 + (j + 1) * 128, :], writes=[xt[j]])
            s.dma("sp", ot[j][:, :], obr[t0 + j * 128:t0 + (j + 1) * 128, :], writes=[ot[j]])
        emit_rmsnorm_T(s, epsc, xt, gcol, hT, ident_b, pT_rr, hb_rr, scr, stat_rr, ntile)
        for c in range(8):
            pT = pT_rr.next()
            for j in range(ntile):
                s.op("pe", lambda: nc.tensor.transpose(out=pT[:, j * 128:(j + 1) * 128], in_=ot[j][:, c * 128:(c + 1) * 128],
                                                       identity=ident_b[:, :]), reads=[ot[j], ident_b], writes=[pT], inc=(j == ntile - 1))
            if c % 2 == 0:
                s.op("dve", lambda: nc.vector.tensor_copy(out=oT[:, c, :], in_=pT[:, 0:TB]), reads=[pT], writes=[oT])
            else:
                s.op("act", lambda: nc.scalar.copy(out=oT[:, c, :], in_=pT[:, 0:TB]), reads=[pT], writes=[oT])
        for dc in range(8):
            acc = acc_rr.next()
            for n in range(4):
                pg = pg_rr.next()
                pp = pp_rr.next()
                for c in range(NKC):
                    s.op("pe", lambda: nc.tensor.matmul(pg[:, 0:TB], lhsT=wg[c][:, n * 1024 + dc * 128:n * 1024 + (dc + 1) * 128],
                                                        rhs=hT[:, c, :], start=(c == 0), stop=(c == NKC - 1)),
                         reads=[wg[c], hT], writes=[pg], inc=(c == NKC - 1))
                for cc in range(2):
                    s.op("pe", lambda: nc.tensor.matmul(pp[:, 0:TB], lhsT=wb[2 * n + cc][:, dc * 128:(dc + 1) * 128],
                                                        rhs=oT[:, 2 * n + cc, :], start=(cc == 0), stop=(cc == 1)),
                         reads=[wb[2 * n + cc], oT], writes=[pp], inc=(cc == 1))
                sg = sg_rr.next()
                s.op("act", lambda: nc.scalar.activation(out=sg[:, :], in_=pg[:, 0:TB], func=AF.Sigmoid), reads=[pg], writes=[sg])
                if n == 0:
                    s.op("dve", lambda: nc.vector.tensor_tensor(out=acc[:, :], in0=sg[:, :], in1=pp[:, 0:TB], op=ALU.mult),
                         reads=[sg, pp], writes=[acc])
                else:
                    s.op("dve", lambda: nc.vector.tensor_tensor(out=sg[:, :], in0=sg[:, :], in1=pp[:, 0:TB], op=ALU.mult),
                         reads=[sg, pp], writes=[sg])
                    if n < 3:
                        s.op("pool", lambda: nc.gpsimd.tensor_tensor(out=acc[:, :], in0=acc[:, :], in1=sg[:, :], op=ALU.add),
                             reads=[acc, sg], writes=[acc])
                    else:
                        s.op("pool", lambda: nc.gpsimd.tensor_tensor(out=mT[dc][:, :], in0=acc[:, :], in1=sg[:, :], op=ALU.add),
                             reads=[acc, sg], writes=[mT[dc]])
        for j in range(ntile):
            for hf in range(2):
                po = po_rr.next()
                for dc in range(8):
                    s.op("pe", lambda: nc.tensor.matmul(po[:, :], lhsT=mT[dc][:, j * 128:(j + 1) * 128],
                                                        rhs=wo[dc][:, hf * 512:(hf + 1) * 512], start=(dc == 0), stop=(dc == 7)),
                         reads=[mT[dc], wo[dc]], writes=[po], inc=(dc == 7))
                s.op("dve", lambda: nc.vector.tensor_tensor(out=xt[j][:, hf * 512:(hf + 1) * 512], in0=po[:, :],
                                                            in1=xt[j][:, hf * 512:(hf + 1) * 512], op=ALU.add),
                     reads=[po, xt[j]], writes=[xt[j]])
            s.dma("sp", y[t0 + j * 128:t0 + (j + 1) * 128, :], xt[j][:, :], reads=[xt[j]])
    s.release(m_)


PARAM_SHAPES = dict(
    ffn1_norm=("L", D), ffn1_w_in=("L", D, 2 * DFF), ffn1_w_out=("L", DFF, D), mix_norm=("L", D), w_in=("L", D, 6800),
    nsa_phi_w1=("L", 2, 2048, 256), nsa_phi_w2=("L", 2, 256, 64), nsa_phi_b2=("L", 2, 64),
    w_branch=("L", 4, 256, D), w_out=("L", D, D), ffn2_norm=("L", D), ffn2_w_in=("L", D, 2 * DFF), ffn2_w_out=("L", DFF, D),
    gains=("L", 128, 6), vgain=("L", 128, 256), wsT=("L", 4, 128, 128), bsT=("L", 128, 4), lamp=("L", 128, 4, 32),
    lami=("L", 128, 2), fbias=("L", 4, 128, 1), b1l=("L", 128, 4), peT=("L", 128, 32), b2c=("L", 64, 1), kgain=("L", 64, 1),
)


def fused_const_shapes(T):
    NT = T // 128
    QL = NT // 4
    NCT = max(1, T // 2048)
    return dict(
        ident=([128, 128], BF16), identf=([128, 128], F32), blk=([2, 128, 128], BF16), triu=([128, 128], F32),
        trib=([128, 128], BF16), triuf=([128, 128], F32), onesf=([128, 128], F32), ones64=([64, 64], BF16),
        onesrow=([1, 128], BF16), cmask=([4, 128, QL, NCT, 128], BF16), smask=([4, 128, 4, 128], BF16),
        wmask=([4, 128, 8, 128], BF16), impA=([4, 128, QL, 128], F32), impB=([4, 128, QL, 128], F32),
        emat=([128, NT, 128], BF16), ovl=([128, NCT, 128], BF16))


def fused_consts(T):
    pc = proj_consts()
    mc = mix_consts()
    cc = [mix_consts_c(T, c) for c in range(4)]
    d = dict(ident=pc["ident"], identf=mc["identf"], blk=pc["blk"], triu=pc["triu"], trib=mc["trib"], triuf=mc["triuf"],
             onesf=mc["onesf"], ones64=cc[0]["ones64"], onesrow=cc[0]["onesrow"], emat=cc[0]["emat"], ovl=cc[0]["ovl"])
    for k in ("cmask", "smask", "wmask", "impA", "impB"):
        d[k] = np.ascontiguousarray(np.stack([cc[c][k] for c in range(4)], 0))
    return d


def emit_mix_fused(s, F, l, T):
    nc = s.nc
    NT = T // 128
    m = s.mark()
    io0 = dict(identb=F["ident"], identf=F["identf"], trib=F["trib"])
    cm = mix_common(s, io0)
    misc_sb = s.sb("misc_sb", [128, NT, 16], F32)
    for j0 in range(0, NT, 8):
        j1 = min(NT, j0 + 8)
        s.dma("sp", misc_sb[:, j0:j1, :], F["misc"][j0 * 128:j1 * 128, :].rearrange("(j p) c -> p j c", p=128), writes=[misc_sb])
    zfm, vab, obr = F["zfm"], F["vab"], F["obr"]
    for h in range(4):
        r0 = (h % 2) * 64
        io = dict(io0)
        io.update(qa=zfm[h // 2][r0:r0 + 64, :], ka=zfm[2 + h // 2][r0:r0 + 64, :], va_src=vab[:, h * 64:(h + 1) * 64],
                  lamp=F["lamp"][l], lami=F["lami"][l], oa=obr[:, h * 64:(h + 1) * 64])
        emit_mix_a(s, cm, io, T)
        io = dict(io0)
        io.update(qb=zfm[4 + h // 2][r0:r0 + 64, :], kb=zfm[6 + h // 2][r0:r0 + 64, :],
                  vb_src=vab[:, 256 + h * 64:256 + (h + 1) * 64], flog_sb=(misc_sb, misc_sb[:, :, h]), fbias=F["fbias"][l, h],
                  triuf=F["triuf"], onesf=F["onesf"], ob=obr[:, 256 + h * 64:256 + (h + 1) * 64])
        emit_mix_b(s, cm, io, T)
    io = dict(io0)
    io.update(zq=(zfm[8], zfm[9]), kskw=zfm[10], kvin=zfm[11], vs_src=F["vsw"][:, 0:64], vw_src=F["vsw"][:, 64:128],
              misc_sb=(misc_sb, misc_sb), w1=F["nsa_phi_w1"][l], b1=F["b1l"][l], peT=F["peT"][l], w2=F["nsa_phi_w2"][l],
              b2=F["nsa_phi_b2"][l], b2c=F["b2c"][l], kgain=F["kgain"][l], oc=obr[:, 512:768])
    for k in ("cmask", "smask", "wmask", "impA", "impB", "emat", "ovl", "ones64", "onesrow"):
        io[k] = F[k]
    emit_mix_c(s, cm, io, T, cs=[0, 1, 2, 3])
    s.release(m)


def build_fused(T, L):
    nc = bass.Bass("TRN2", target_bir_lowering=False)
    F = {}
    F["x"] = dram_in(nc, "x", [T, D])
    for k, shp in PARAM_SHAPES.items():
        F[k] = dram_in(nc, k, [L if v == "L" else v for v in shp])
    for k, (shp, dt) in fused_const_shapes(T).items():
        F[k] = dram_in(nc, k, shp, dt)
    y = dram_out(nc, "y", [T, D])
    for k, shp, dt in (("xa", [T, D], F32), ("xb", [T, D], F32), ("xc", [T, D], F32), ("zfm", [NFM, 128, T], BF16),
                       ("vab", [T, 512], BF16), ("vsw", [T, 128], BF16), ("misc", [T, 16], F32), ("obr", [T, D], BF16)):
        F[k] = nc.dram_tensor("s_" + k, shp, dt).ap()
    s = S(nc)
    for l in range(L):
        x_in = F["x"] if l == 0 else F["xc"]
        emit_ffn(s, x_in, F["ffn1_norm"][l], F["ffn1_w_in"][l], F["ffn1_w_out"][l], F["ident"], F["xa"], T)
        a = dict(x=F["xa"], g=F["mix_norm"][l], w_in=F["w_in"][l], ident=F["ident"], gains=F["gains"][l], blk=F["blk"],
                 vgain=F["vgain"][l], wsT=F["wsT"][l], triu=F["triu"], bsT=F["bsT"][l], zfm=F["zfm"], vab=F["vab"],
                 vsw=F["vsw"], misc=F["misc"], od=F["obr"][:, 768:1024])
        emit_proj(s, a, T)
        emit_mix_fused(s, F, l, T)
        emit_merge(s, F["xa"], F["mix_norm"][l], F["w_in"][l], F["w_branch"][l], F["w_out"][l], F["ident"], F["obr"], F["xb"], T)
        x_out = y if l == L - 1 else F["xc"]
        emit_ffn(s, F["xb"], F["ffn2_norm"][l], F["ffn2_w_in"][l], F["ffn2_w_out"][l], F["ident"], x_out, T)
    s.finish()
    s.close()
    return nc


def fused_params(P, L):
    import math
    f32 = np.float32
    A = lambda a: np.ascontiguousarray(np.asarray(a, dtype=f32))
    d = {k: A(P[k]) for k in ("ffn1_norm", "ffn1_w_in", "ffn1_w_out", "mix_norm", "w_in", "nsa_phi_w1", "nsa_phi_w2",
                              "nsa_phi_b2", "w_branch", "w_out", "ffn2_norm", "ffn2_w_in", "ffn2_w_out")}
    tile = lambda v, n: np.tile(A(v), (1, n))
    d["gains"] = np.ascontiguousarray(np.stack([tile(P["diff_q_gain"], 4), tile(P["diff_k_gain"], 4), tile(P["fox_q_gain"], 2),
                                                tile(P["fox_k_gain"], 2), tile(P["nsa_q_gain"], 2), tile(P["nsa_k_gain"], 2)], 2))
    d["vgain"] = np.ascontiguousarray(np.broadcast_to(A(P["gmlp_v_gain"])[:, None, :], (L, 128, 256)))
    d["wsT"] = np.ascontiguousarray(A(P["gmlp_w_s"]).transpose(0, 1, 3, 2))
    d["bsT"] = np.ascontiguousarray(A(P["gmlp_b_s"]).transpose(0, 2, 1))
    d["lamp"] = np.ascontiguousarray(np.broadcast_to(A(P["diff_lambda"])[:, None], (L, 128, 4, 32)))
    li = np.array([[0.8 - 0.6 * math.exp(-0.3 * l), 1.0 - (0.8 - 0.6 * math.exp(-0.3 * l))] for l in range(L)], f32)
    d["lami"] = np.ascontiguousarray(np.broadcast_to(li[:, None, :], (L, 128, 2)))
    d["fbias"] = np.ascontiguousarray(np.broadcast_to(A(P["fox_f_bias"])[:, :, None, None], (L, 4, 128, 1)))
    d["b1l"] = np.ascontiguousarray(A(P["nsa_phi_b1"]).reshape(L, 2, 2, 128).transpose(0, 3, 1, 2).reshape(L, 128, 4))
    pe = A(P["nsa_cmp_pe"])
    d["peT"] = np.ascontiguousarray(pe.transpose(0, 1, 3, 2).reshape(L, 128, 32))
    d["b2c"] = np.ascontiguousarray(A(P["nsa_phi_b2"])[:, 0, :, None])
    d["kgain"] = np.ascontiguousarray(A(P["nsa_k_gain"])[:, :, None])
    return d


B_, T_, L_ = 2, 8192, 2
_PROG = {}


def kernel(**inputs):
    x = np.ascontiguousarray(np.asarray(inputs["x"], dtype=np.float32))
    if "fused" not in _PROG:
        _PROG["fused"] = build_fused(T_, L_)
        _PROG["consts"] = fused_consts(T_)
    nc = _PROG["fused"]
    par = fused_params(inputs, L_)
    in_maps = []
    for b in range(B_):
        d = dict(par)
        d.update(_PROG["consts"])
        d["x"] = x[b]
        in_maps.append(d)
    res = run_bass_kernel_spmd(nc, in_maps, core_ids=list(range(B_)))
    return np.stack([np.asarray(res.results[b]["y"], dtype=np.float32) for b in range(B_)], 0)
```

```python
import numpy as np
import concourse.bass as bass
import concourse.mybir as mybir
from concourse.bass_utils import run_bass_kernel_spmd

F32 = mybir.dt.float32
BF16 = mybir.dt.bfloat16
AF = mybir.ActivationFunctionType
ALU = mybir.AluOpType
AX = mybir.AxisListType

ENGS = ("pe", "act", "dve", "pool", "sp")


class Buf:
    __slots__ = ("name", "t", "w", "r", "dsem", "dcnt", "uid")
    _n = 0

    def __init__(self, name, t):
        Buf._n += 1
        self.uid = Buf._n
        self.name = name
        self.t = t
        self.w = None
        self.r = []
        self.dsem = None
        self.dcnt = 0

    def __getitem__(self, idx):
        return self.t[idx]


class S:
    def __init__(self, nc, same_engine_sync=True):
        self.nc = nc
        self.e = {"pe": nc.tensor, "act": nc.scalar, "dve": nc.vector, "pool": nc.gpsimd, "sp": nc.sync}
        self.sem = {k: nc.alloc_semaphore("c_" + k) for k in ENGS}
        self.cnt = {k: 0 for k in ENGS}
        self.seen = {k: {} for k in ENGS}
        self.same = same_engine_sync
        self.nbuf = 0
        self.dma_sems = []
        self.ctx = []
        self.cbufs = []
        self.free_dsems = []

    def sb(self, name, shape, dt):
        self.nbuf += 1
        g = self.nc.sbuf_tensor("%s_%d" % (name, self.nbuf), list(shape), dt)
        t = g.__enter__()
        self.ctx.append(g)
        b = Buf(name, t)
        self.cbufs.append(b)
        return b

    def ps(self, name, shape, dt):
        self.nbuf += 1
        g = self.nc.psum_tensor("%s_%d" % (name, self.nbuf), list(shape), dt)
        t = g.__enter__()
        self.ctx.append(g)
        b = Buf(name, t)
        self.cbufs.append(b)
        return b

    def sub(self, name, ap):
        return Buf(name, ap)

    def mark(self):
        return len(self.ctx)

    def release(self, m):
        self.barrier()
        while len(self.ctx) > m:
            self.ctx.pop().__exit__(None, None, None)
            b = self.cbufs.pop()
            if b.dsem is not None:
                self.free_dsems.append((b.dsem, b.dcnt))
                self.dma_sems.remove(b)
                b.dsem = None

    def close(self):
        for g in reversed(self.ctx):
            g.__exit__(None, None, None)
        self.ctx = []
        self.cbufs = []

    def _need(self, E, deps):
        need = {}
        for d in deps:
            if d is None:
                continue
            if d[0] == "dma":
                b = d[1]
                key = ("dma", b.uid)
                need[key] = (b, b.dcnt)
            else:
                F, c = d
                if F == E and (not self.same or E == "pe" or c > self.cnt[E]):
                    continue
                if c > need.get(F, (None, 0))[1]:
                    need[F] = (None, c)
        for key, (b, c) in need.items():
            if self.seen[E].get(key, 0) >= c:
                continue
            self.seen[E][key] = c
            if b is not None:
                self.e[E].wait_ge(b.dsem, c)
            else:
                self.e[E].wait_ge(self.sem[key], c)

    def op(self, E, fn, reads=(), writes=(), inc=True):
        deps = []
        for b in reads:
            deps.append(b.w)
        for b in writes:
            deps.append(b.w)
            deps.extend(b.r)
        self._need(E, deps)
        ins = fn()
        c = self.cnt[E] + 1
        if inc:
            ins.then_inc(self.sem[E], 1)
            self.cnt[E] = c
        for b in writes:
            b.w = (E, c)
            b.r = []
        for b in reads:
            if b not in writes:
                b.r = [x for x in b.r if x[0] != E] + [(E, c)]
        return ins

    def dma(self, Q, out, in_, reads=(), writes=(), **kw):
        deps = []
        for b in reads:
            deps.append(b.w)
        for b in writes:
            deps.append(b.w)
            deps.extend(b.r)
        self._need(Q, deps)
        owner = (list(writes) + list(reads))[0]
        if owner.dsem is None:
            owner.dsem = self._dsem(owner)
            self.dma_sems.append(owner)
        ins = self.e[Q].dma_start(out=out, in_=in_, **kw)
        ins.then_inc(owner.dsem, 16)
        owner.dcnt += 16
        rec = ("dma", owner, owner.dcnt)
        for b in writes:
            b.w = rec
            b.r = []
        for b in reads:
            if b not in writes:
                b.r = b.r + [rec]
        return ins

    def cc(self, kind, groups, in_ap, out_ap, reads=(), writes=()):
        deps = []
        for b in reads:
            deps.append(b.w)
        for b in writes:
            deps.append(b.w)
            deps.extend(b.r)
        self._need("pool", deps)
        owner = list(writes)[0]
        if owner.dsem is None:
            owner.dsem = self._dsem(owner)
            self.dma_sems.append(owner)
        ins = self.nc.gpsimd.collective_compute(kind, op=ALU.bypass, replica_groups=groups, ins=[in_ap], outs=[out_ap])
        ins.then_inc(owner.dsem, 16)
        owner.dcnt += 16
        rec = ("dma", owner, owner.dcnt)
        for b in writes:
            b.w = rec
            b.r = []
        for b in reads:
            if b not in writes:
                b.r = b.r + [rec]
        return ins

    def _dsem(self, owner):
        if self.free_dsems:
            sem, cnt = self.free_dsems.pop()
            owner.dcnt = cnt
            return sem
        self.nsem = getattr(self, "nsem", 0) + 1
        return self.nc.alloc_semaphore("d_%d" % self.nsem)

    def barrier(self):
        for E in ENGS:
            for Fk in ENGS:
                if Fk == E:
                    continue
                c = self.cnt[Fk]
                if c and self.seen[E].get(Fk, 0) < c:
                    self.seen[E][Fk] = c
                    self.e[E].wait_ge(self.sem[Fk], c)
            for b in self.dma_sems:
                key = ("dma", b.uid)
                if b.dcnt and self.seen[E].get(key, 0) < b.dcnt:
                    self.seen[E][key] = b.dcnt
                    self.e[E].wait_ge(b.dsem, b.dcnt)

    def finish(self):
        self.barrier()


D = 1024
DFF = 2816
NFC = DFF // 128
NKC = D // 128
EPS = 1e-6


def dram_in(nc, name, shape, dt=F32):
    return nc.dram_tensor(name, list(shape), dt, kind="ExternalInput").ap()


def dram_out(nc, name, shape, dt=F32):
    return nc.dram_tensor(name, list(shape), dt, kind="ExternalOutput").ap()


class RR:
    def __init__(self, items):
        self.items = items
        self.i = 0

    def next(self):
        b = self.items[self.i % len(self.items)]
        self.i += 1
        return b


def emit_norm(s, epsc, xt, hb_rr, scr, stat_rr, ntile):
    nc = s.nc
    st = stat_rr.next()
    hbs = [hb_rr.next() for _ in range(ntile)]
    for j in range(ntile):
        s.op("act", lambda: nc.scalar.activation(out=hbs[j][:, :], in_=xt[j][:, :], func=AF.Square, scale=1.0 / 32.0,
                                                 accum_out=st[:, j:j + 1]),
             reads=[xt[j]], writes=[hbs[j], st])
    s.op("act", lambda: nc.scalar.activation(out=st[:, 4:4 + ntile], in_=st[:, 0:ntile], func=AF.Ln, bias=epsc[:, 0:1], scale=1.0),
         reads=[st, epsc], writes=[st])
    s.op("act", lambda: nc.scalar.activation(out=st[:, 8:8 + ntile], in_=st[:, 4:4 + ntile], func=AF.Exp, scale=-0.5),
         reads=[st], writes=[st])
    for j in range(ntile):
        s.op("act", lambda: nc.scalar.activation(out=hbs[j][:, :], in_=xt[j][:, :], func=AF.Copy, scale=st[:, 8 + j:9 + j]),
             reads=[xt[j], st], writes=[hbs[j]])
    return hbs


def emit_transpose_T(s, hbs, gcol, hT, ident_b, pT_rr, ntile, evac_engs=("dve", "act")):
    nc = s.nc
    k = 0
    for c in range(NKC):
        pT = pT_rr.next()
        for j in range(ntile):
            s.op("pe", lambda: nc.tensor.transpose(out=pT[:, j * 128:(j + 1) * 128], in_=hbs[j][:, c * 128:(c + 1) * 128],
                                                   identity=ident_b[:, :]),
                 reads=[hbs[j], ident_b], writes=[pT], inc=(j == ntile - 1))
        eng = evac_engs[k % len(evac_engs)]
        k += 1
        if eng == "dve":
            s.op("dve", lambda: nc.vector.tensor_scalar(out=hT[:, c, 0:ntile * 128], in0=pT[:, 0:ntile * 128],
                                                        scalar1=gcol[:, c:c + 1], scalar2=None, op0=ALU.mult),
                 reads=[pT, gcol], writes=[hT])
        else:
            s.op("act", lambda: nc.scalar.activation(out=hT[:, c, 0:ntile * 128], in_=pT[:, 0:ntile * 128],
                                                     func=AF.Copy, scale=gcol[:, c:c + 1]),
                 reads=[pT, gcol], writes=[hT])


def emit_rmsnorm_T(s, epsc, xt, gcol, hT, ident_b, pT_rr, hb_rr, scr, stat_rr, ntile, evac_engs=("dve", "act")):
    hbs = emit_norm(s, epsc, xt, hb_rr, scr, stat_rr, ntile)
    emit_transpose_T(s, hbs, gcol, hT, ident_b, pT_rr, ntile, evac_engs)


def build_ffn(NT, TB=512):
    nc = bass.Bass("TRN2", target_bir_lowering=False)
    x = dram_in(nc, "x", [NT, D])
    g = dram_in(nc, "g", [D])
    w_in = dram_in(nc, "w_in", [D, 2 * DFF])
    w_out = dram_in(nc, "w_out", [DFF, D])
    ident = dram_in(nc, "ident", [128, 128], BF16)
    y = dram_out(nc, "y", [NT, D])
    s = S(nc)
    emit_ffn(s, x, g, w_in, w_out, ident, y, NT, TB)
    s.finish()
    s.close()
    return nc


def emit_ffn(s, x, g, w_in, w_out, ident, y, NT, TB=512):
    nc = s.nc
    ntile = TB // 128
    m_ = s.mark()
    ident_b = s.sb("ident_b", [128, 128], BF16)
    s.dma("sp", ident_b[:, :], ident[:, :], writes=[ident_b])
    gcol = s.sb("gcol", [128, NKC], F32)
    epsc = s.sb("epsc", [128, 1], F32)
    s.op("dve", lambda: nc.vector.memset(epsc[:, :], EPS), writes=[epsc])
    s.dma("sp", gcol[:, :], g.rearrange("(c p) -> p c", p=128), writes=[gcol], allow_slow_non_contiguous=True)
    win_b = [s.sb("win_b%d" % c, [128, 2 * DFF], BF16) for c in range(NKC)]
    wout_b = [s.sb("wout_b%d" % f, [128, D], BF16) for f in range(NFC)]
    for c in range(NKC):
        for hf in range(2):
            s.dma("pool", win_b[c][:, hf * DFF:(hf + 1) * DFF], w_in[c * 128:(c + 1) * 128, hf * DFF:(hf + 1) * DFF],
                  writes=[win_b[c]])
    for f in range(NFC):
        s.dma("pool", wout_b[f][:, :], w_out[f * 128:(f + 1) * 128, :], writes=[wout_b[f]])
    xn = [s.sb("xn%d" % j, [128, D], F32) for j in range(ntile)]
    xr_rr = RR([s.sb("xr%d" % j, [128, D], F32) for j in range(1)])
    hb_rr = RR([s.sb("hb%d" % j, [128, D], BF16) for j in range(2 * ntile)])
    stat_rr = RR([s.sb("st%d" % j, [128, 16], F32) for j in range(4)])
    hT = s.sb("hT", [128, NKC, TB], BF16)
    pT_rr = RR([s.ps("pT%d" % j, [128, TB], BF16) for j in range(2)])
    pa_rr = RR([s.ps("pa%d" % j, [128, TB], F32) for j in range(2)])
    pb_rr = RR([s.ps("pb%d" % j, [128, TB], F32) for j in range(2)])
    po_rr = RR([s.ps("po%d" % j, [128, 512], F32) for j in range(2)])
    sa_rr = RR([s.sb("sa%d" % j, [128, TB], F32) for j in range(2)])
    act = [s.sb("actT%d" % f, [128, TB], BF16) for f in range(NFC)]
    nblk = NT // TB

    def prep_a_rot(tb):
        for j in range(ntile):
            r0 = tb * TB + j * 128
            s.dma("sp", xn[j][:, :], x[r0:r0 + 128, :], writes=[xn[j]])
        return emit_norm(s, epsc, xn, hb_rr, None, stat_rr, ntile)

    hbs_next = prep_a_rot(0)
    emit_transpose_T(s, hbs_next, gcol, hT, ident_b, pT_rr, ntile)
    for tb in range(nblk):
        if tb + 1 < nblk:
            hbs_next = prep_a_rot(tb + 1)
        for f in range(NFC):
            pa = pa_rr.next()
            pb = pb_rr.next()
            for c in range(NKC):
                s.op("pe", lambda: nc.tensor.matmul(pa[:, :], lhsT=win_b[c][:, f * 128:(f + 1) * 128], rhs=hT[:, c, :],
                                                    start=(c == 0), stop=(c == NKC - 1)),
                     reads=[win_b[c], hT], writes=[pa], inc=(c == NKC - 1))
            for c in range(NKC):
                s.op("pe", lambda: nc.tensor.matmul(pb[:, :], lhsT=win_b[c][:, DFF + f * 128:DFF + (f + 1) * 128],
                                                    rhs=hT[:, c, :], start=(c == 0), stop=(c == NKC - 1)),
                     reads=[win_b[c], hT], writes=[pb], inc=(c == NKC - 1))
            sa = sa_rr.next()
            s.op("act", lambda: nc.scalar.activation(out=sa[:, :], in_=pa[:, :], func=AF.Silu), reads=[pa], writes=[sa])
            s.op("dve", lambda: nc.vector.tensor_tensor(out=act[f][:, :], in0=sa[:, :], in1=pb[:, :], op=ALU.mult),
                 reads=[sa, pb], writes=[act[f]])
        if tb + 1 < nblk:
            emit_transpose_T(s, hbs_next, gcol, hT, ident_b, pT_rr, ntile)
        for j in range(ntile):
            r0 = tb * TB + j * 128
            xr = xr_rr.next()
            s.dma("sp", xr[:, :], x[r0:r0 + 128, :], writes=[xr])
            for hf in range(2):
                po = po_rr.next()
                for f in range(NFC):
                    s.op("pe", lambda: nc.tensor.matmul(po[:, :], lhsT=act[f][:, j * 128:(j + 1) * 128],
                                                        rhs=wout_b[f][:, hf * 512:(hf + 1) * 512],
                                                        start=(f == 0), stop=(f == NFC - 1)),
                         reads=[act[f], wout_b[f]], writes=[po], inc=(f == NFC - 1))
                s.op("dve", lambda: nc.vector.scalar_tensor_tensor(out=xr[:, hf * 512:(hf + 1) * 512], in0=po[:, :],
                                                                   scalar=0.5, in1=xr[:, hf * 512:(hf + 1) * 512],
                                                                   op0=ALU.mult, op1=ALU.add),
                     reads=[po, xr], writes=[xr])
            s.dma("sp", y[r0:r0 + 128, :], xr[:, :], reads=[xr])
    s.release(m_)


FM_SRC = [[(0, 128)], [(128, 128)], [(256, 128)], [(384, 128)],
          [(768, 128)], [(896, 128)], [(1024, 128)], [(1152, 128)],
          [(1540, 128)], [(1668, 128)], [(1924, 64), (2052, 64)], [(1796, 128)]]
FM_GCOL = [0, 0, 1, 1, 2, 2, 3, 3, 4, 4, 5, None]
FM_BLK = [0, 0, 0, 0, 1, 1, 1, 1, 1, 1, 1, None]
TM_SRC = [[(512, 256), (1280, 256)],
          [(1536, 4), (2180, 12), (1988, 64), (2116, 64)],
          [(2192, 512)]]
NFM = 12
GELU_C = 1.5957691216057308


def build_proj(NT, TB=512):
    nc = bass.Bass("TRN2", target_bir_lowering=False)
    a = dict(
        x=dram_in(nc, "x", [NT, D]), g=dram_in(nc, "g", [D]), w_in=dram_in(nc, "w_in", [D, 6800]),
        ident=dram_in(nc, "ident", [128, 128], BF16), gains=dram_in(nc, "gains", [128, 6]),
        blk=dram_in(nc, "blk", [2, 128, 128], BF16), vgain=dram_in(nc, "vgain", [128, 256]),
        wsT=dram_in(nc, "wsT", [4, 128, 128]), triu=dram_in(nc, "triu", [128, 128]), bsT=dram_in(nc, "bsT", [128, 4]),
        zfm=dram_out(nc, "zfm", [NFM, 128, NT], BF16), vab=dram_out(nc, "vab", [NT, 512], BF16),
        vsw=dram_out(nc, "vsw", [NT, 128], BF16), misc=dram_out(nc, "misc", [NT, 16]), od=dram_out(nc, "od", [NT, 256], BF16))
    s = S(nc)
    emit_proj(s, a, NT, TB)
    s.finish()
    s.close()
    return nc


def emit_proj(s, a, NT, TB=512):
    nc = s.nc
    x, g, w_in, ident, gains, blk, vgain, wsT, triu, bsT = (a[k] for k in
                                                            ("x", "g", "w_in", "ident", "gains", "blk", "vgain", "wsT", "triu", "bsT"))
    zfm, vab, vsw, misc, od = (a[k] for k in ("zfm", "vab", "vsw", "misc", "od"))
    m_ = s.mark()
    ntile = TB // 128
    ident_b = s.sb("ident_b", [128, 128], BF16)
    s.dma("sp", ident_b[:, :], ident[:, :], writes=[ident_b])
    gcol = s.sb("gcol", [128, NKC], F32)
    s.dma("sp", gcol[:, :], g.rearrange("(c p) -> p c", p=128), writes=[gcol], allow_slow_non_contiguous=True)
    epsc = s.sb("epsc", [128, 1], F32)
    s.op("dve", lambda: nc.vector.memset(epsc[:, :], EPS), writes=[epsc])
    gn = s.sb("gn", [128, 6], F32)
    s.dma("sp", gn[:, :], gains[:, :], writes=[gn])
    for col, sc in ((0, 32.0 ** -0.5), (2, 0.125), (4, 0.125)):
        s.op("dve", lambda: nc.vector.tensor_scalar(out=gn[:, col:col + 1], in0=gn[:, col:col + 1], scalar1=sc,
                                                    scalar2=None, op0=ALU.mult), reads=[gn], writes=[gn])
    blk_b = [s.sb("blk%d" % i, [128, 128], BF16) for i in range(2)]
    for i in range(2):
        s.dma("sp", blk_b[i][:, :], blk[i], writes=[blk_b[i]])
    vg = s.sb("vg", [128, 256], F32)
    s.dma("sp", vg[:, :], vgain[:, :], writes=[vg])
    bcol = s.sb("bcol", [128, 4], F32)
    s.dma("sp", bcol[:, :], bsT[:, :], writes=[bcol])
    tri = s.sb("tri", [128, 128], F32)
    s.dma("sp", tri[:, :], triu[:, :], writes=[tri])
    wm = []
    wtmp = s.sb("wtmp", [128, 128], F32)
    for gi in range(4):
        w = s.sb("wm%d" % gi, [128, 128], BF16)
        s.dma("sp", wtmp[:, :], wsT[gi], writes=[wtmp])
        s.op("dve", lambda: nc.vector.tensor_tensor(out=w[:, :], in0=wtmp[:, :], in1=tri[:, :], op=ALU.mult),
             reads=[wtmp, tri], writes=[w])
        wm.append(w)
    wfm = [s.sb("wfm%d" % c, [128, NFM * 128], BF16) for c in range(NKC)]
    wtm = [s.sb("wtm%d" % c, [128, 1168], BF16) for c in range(NKC)]
    for c in range(NKC):
        for i, srcs in enumerate(FM_SRC):
            o = i * 128
            for (c0, n) in srcs:
                s.dma("pool", wfm[c][:, o:o + n], w_in[c * 128:(c + 1) * 128, c0:c0 + n], writes=[wfm[c]])
                o += n
        o = 0
        for srcs in TM_SRC:
            for (c0, n) in srcs:
                s.dma("pool", wtm[c][:, o:o + n], w_in[c * 128:(c + 1) * 128, c0:c0 + n], writes=[wtm[c]])
                o += n
    xts = [[s.sb("xt%d_%d" % (k, j), [128, D], F32) for j in range(ntile)] for k in range(2)]
    hb_rr = RR([s.sb("hb%d" % j, [128, D], BF16) for j in range(2 * ntile)])
    scr = s.sb("scr", [128, D], BF16)
    stat_rr = RR([s.sb("st%d" % j, [128, 16], F32) for j in range(4)])
    hTs = [s.sb("hT%d" % k, [128, NKC, TB], BF16) for k in range(2)]
    pT_rr = RR([s.ps("pT%d" % j, [128, TB], BF16) for j in range(2)])
    pz_rr = RR([s.ps("pz%d" % j, [128, 512], F32) for j in range(2)])
    ptm_rr = RR([s.ps("ptm%d" % j, [128, 512], F32) for j in range(2)])
    pq_rr = RR([s.ps("pq%d" % j, [128, 512], F32) for j in range(2)])
    sq_rr = RR([s.sb("sq%d" % j, [128, TB], BF16) for j in range(3)])
    rs_rr = RR([s.sb("rs%d" % j, [128, TB], F32) for j in range(2)])
    zo_rr = RR([s.sb("zo%d" % j, [128, TB], BF16) for j in range(3)])
    vab_rr = RR([s.sb("vabt%d" % j, [128, 512], BF16) for j in range(2)])
    vsw_rr = RR([s.sb("vswt%d" % j, [128, 128], BF16) for j in range(2)])
    msc_rr = RR([s.sb("msct%d" % j, [128, 16], F32) for j in range(2)])
    f_rr = RR([s.sb("gf%d" % j, [128, 512], F32) for j in range(4)])
    ge_rr = RR([s.sb("ge%d" % j, [128, 512], F32) for j in range(3)])
    vn_rr = RR([s.sb("vn%d" % j, [128, 256], BF16) for j in range(3)])
    od_rr = RR([s.sb("odt%d" % j, [128, 256], BF16) for j in range(2)])
    nblk = NT // TB

    def prep(tb):
        xt = xts[tb % 2]
        for j in range(ntile):
            s.dma("sp", xt[j][:, :], x[tb * TB + j * 128:tb * TB + (j + 1) * 128, :], writes=[xt[j]])
        emit_rmsnorm_T(s, epsc, xt, gcol, hTs[tb % 2], ident_b, pT_rr, hb_rr, scr, stat_rr, ntile)

    prep(0)
    pipe = Pipe(1)
    for tb in range(nblk):
        t0 = tb * TB
        hT = hTs[tb % 2]
        for i in range(NFM):
            pz = pz_rr.next()
            for c in range(NKC):
                s.op("pe", lambda: nc.tensor.matmul(pz[:, 0:TB], lhsT=wfm[c][:, i * 128:(i + 1) * 128], rhs=hT[:, c, :],
                                                    start=(c == 0), stop=(c == NKC - 1)),
                     reads=[wfm[c], hT], writes=[pz], inc=(c == NKC - 1))
            zo = zo_rr.next()
            if FM_GCOL[i] is None:
                s.op("act", lambda: nc.scalar.copy(out=zo[:, :], in_=pz[:, 0:TB]), reads=[pz], writes=[zo])
                s.dma("sp", zfm[i, :, t0:t0 + TB], zo[:, :], reads=[zo])
            else:
                sq = sq_rr.next()
                s.op("act", lambda: nc.scalar.activation(out=sq[:, :], in_=pz[:, 0:TB], func=AF.Square),
                     reads=[pz], writes=[sq])

                def back(i=i, pz=pz, sq=sq, zo=zo, t0=t0):
                    gs = 32.0 if FM_BLK[i] == 0 else 64.0
                    pq = pq_rr.next()
                    s.op("pe", lambda: nc.tensor.matmul(pq[:, 0:TB], lhsT=blk_b[FM_BLK[i]][:, :], rhs=sq[:, :],
                                                        start=True, stop=True), reads=[blk_b[FM_BLK[i]], sq], writes=[pq])
                    rs = rs_rr.next()
                    s.op("act", lambda: nc.scalar.activation(out=rs[:, :], in_=pq[:, 0:TB], func=AF.Ln, bias=epsc[:, 0:1],
                                                             scale=1.0 / gs), reads=[pq, epsc], writes=[rs])
                    s.op("act", lambda: nc.scalar.activation(out=rs[:, :], in_=rs[:, :], func=AF.Exp, scale=-0.5),
                         reads=[rs], writes=[rs])
                    gc = FM_GCOL[i]
                    s.op("dve", lambda: nc.vector.scalar_tensor_tensor(out=zo[:, :], in0=pz[:, 0:TB], scalar=gn[:, gc:gc + 1],
                                                                       in1=rs[:, :], op0=ALU.mult, op1=ALU.mult),
                         reads=[pz, gn, rs], writes=[zo])
                    s.dma("sp", zfm[i, :, t0:t0 + TB], zo[:, :], reads=[zo])
                pipe.push(back)
        if tb + 1 < nblk:
            prep(tb + 1)
        for j in range(ntile):
            r0 = t0 + j * 128
            pz = ptm_rr.next()
            for c in range(NKC):
                s.op("pe", lambda: nc.tensor.matmul(pz[:, :], lhsT=hT[:, c, j * 128:(j + 1) * 128], rhs=wtm[c][:, 0:512],
                                                    start=(c == 0), stop=(c == NKC - 1)),
                     reads=[wtm[c], hT], writes=[pz], inc=(c == NKC - 1))
            vt = vab_rr.next()
            s.op("act", lambda: nc.scalar.copy(out=vt[:, :], in_=pz[:, :]), reads=[pz], writes=[vt])
            s.dma("sp", vab[r0:r0 + 128, :], vt[:, :], reads=[vt])
            pz = ptm_rr.next()
            for c in range(NKC):
                s.op("pe", lambda: nc.tensor.matmul(pz[:, 0:144], lhsT=hT[:, c, j * 128:(j + 1) * 128], rhs=wtm[c][:, 512:656],
                                                    start=(c == 0), stop=(c == NKC - 1)),
                     reads=[wtm[c], hT], writes=[pz], inc=(c == NKC - 1))
            mt = msc_rr.next()
            vs_ = vsw_rr.next()
            s.op("dve", lambda: nc.vector.tensor_copy(out=mt[:, :], in_=pz[:, 0:16]), reads=[pz], writes=[mt])
            s.op("dve", lambda: nc.vector.tensor_copy(out=vs_[:, :], in_=pz[:, 16:144]), reads=[pz], writes=[vs_])
            s.dma("sp", misc[r0:r0 + 128, :], mt[:, :], reads=[mt])
            s.dma("sp", vsw[r0:r0 + 128, :], vs_[:, :], reads=[vs_])
            pz = ptm_rr.next()
            for c in range(NKC):
                s.op("pe", lambda: nc.tensor.matmul(pz[:, :], lhsT=hT[:, c, j * 128:(j + 1) * 128], rhs=wtm[c][:, 656:1168],
                                                    start=(c == 0), stop=(c == NKC - 1)),
                     reads=[wtm[c], hT], writes=[pz], inc=(c == NKC - 1))
            z2 = f_rr.next()
            s.op("act", lambda: nc.scalar.activation(out=z2[:, :], in_=pz[:, :], func=AF.Square), reads=[pz], writes=[z2])
            s.op("dve", lambda: nc.vector.tensor_scalar(out=z2[:, :], in0=z2[:, :], scalar1=0.044715, scalar2=1.0,
                                                        op0=ALU.mult, op1=ALU.add), reads=[z2], writes=[z2])
            s.op("dve", lambda: nc.vector.tensor_tensor(out=z2[:, :], in0=z2[:, :], in1=pz[:, :], op=ALU.mult),
                 reads=[z2, pz], writes=[z2])
            s.op("act", lambda: nc.scalar.activation(out=z2[:, :], in_=z2[:, :], func=AF.Exp, scale=-GELU_C),
                 reads=[z2], writes=[z2])
            s.op("dve", lambda: nc.vector.tensor_scalar(out=z2[:, :], in0=z2[:, :], scalar1=1.0, scalar2=None, op0=ALU.add),
                 reads=[z2], writes=[z2])
            s.op("dve", lambda: nc.vector.reciprocal(out=z2[:, :], in_=z2[:, :]), reads=[z2], writes=[z2])
            ge = ge_rr.next()
            s.op("dve", lambda: nc.vector.tensor_tensor(out=ge[:, :], in0=z2[:, :], in1=pz[:, :], op=ALU.mult),
                 reads=[z2, pz], writes=[ge])
            sqv = f_rr.next()
            st = stat_rr.next()
            s.op("act", lambda: nc.scalar.activation(out=sqv[:, 0:256], in_=ge[:, 256:512], func=AF.Square),
                 reads=[ge], writes=[sqv])
            s.op("dve", lambda: nc.vector.tensor_reduce(out=st[:, 0:4], in_=sqv[:, 0:256].rearrange("p (g d) -> p g d", g=4),
                                                        axis=AX.X, op=ALU.add), reads=[sqv], writes=[st])
            s.op("act", lambda: nc.scalar.activation(out=st[:, 0:4], in_=st[:, 0:4], func=AF.Ln, bias=epsc[:, 0:1],
                                                     scale=1.0 / 64.0), reads=[st, epsc], writes=[st])
            s.op("act", lambda: nc.scalar.activation(out=st[:, 0:4], in_=st[:, 0:4], func=AF.Exp, scale=-0.5),
                 reads=[st], writes=[st])
            vn = vn_rr.next()
            for gi in range(4):
                s.op("dve", lambda: nc.vector.scalar_tensor_tensor(
                    out=vn[:, gi * 64:(gi + 1) * 64], in0=ge[:, 256 + gi * 64:256 + (gi + 1) * 64], scalar=st[:, gi:gi + 1],
                    in1=vg[:, gi * 64:(gi + 1) * 64], op0=ALU.mult, op1=ALU.mult), reads=[ge, st, vg], writes=[vn])

            def back2(vn=vn, ge=ge, r0=r0):
                pq = pq_rr.next()
                for gi in range(4):
                    s.op("pe", lambda: nc.tensor.matmul(pq[:, gi * 64:(gi + 1) * 64], lhsT=wm[gi][:, :],
                                                        rhs=vn[:, gi * 64:(gi + 1) * 64], start=True, stop=True),
                         reads=[wm[gi], vn], writes=[pq], inc=(gi == 3))
                ot = od_rr.next()
                for gi in range(4):
                    s.op("dve", lambda: nc.vector.scalar_tensor_tensor(
                        out=ot[:, gi * 64:(gi + 1) * 64], in0=pq[:, gi * 64:(gi + 1) * 64], scalar=bcol[:, gi:gi + 1],
                        in1=ge[:, gi * 64:(gi + 1) * 64], op0=ALU.add, op1=ALU.mult), reads=[pq, bcol, ge], writes=[ot])
                s.dma("sp", od[r0:r0 + 128, :], ot[:, :], reads=[ot])
            pipe.push(back2)
    pipe.flush()
    s.release(m_)


def _bf(a):
    import ml_dtypes
    return np.ascontiguousarray(a).astype(ml_dtypes.bfloat16)


def proj_consts():
    blk = np.zeros((2, 128, 128), np.float32)
    for i in range(128):
        for j in range(128):
            if i // 32 == j // 32:
                blk[0, i, j] = 1
            if i // 64 == j // 64:
                blk[1, i, j] = 1
    triu = np.triu(np.ones((128, 128), np.float32))
    return dict(ident=_bf(np.eye(128, dtype=np.float32)), blk=_bf(blk), triu=triu)


def proj_params(g, w_in, dq, dk, fq, fk, nq, nk, vgain, w_s, b_s):
    gains = np.stack([np.tile(dq, 4), np.tile(dk, 4), np.tile(fq, 2), np.tile(fk, 2), np.tile(nq, 2), np.tile(nk, 2)], 1)
    return dict(g=np.ascontiguousarray(g), w_in=np.ascontiguousarray(w_in), gains=np.ascontiguousarray(gains, dtype=np.float32),
                vgain=np.ascontiguousarray(np.broadcast_to(vgain[None, :], (128, 256))),
                wsT=np.ascontiguousarray(w_s.transpose(0, 2, 1)), bsT=np.ascontiguousarray(b_s.T))


NEG = -30000.0


def load_vt(s, vt, io, key, T):
    nc = s.nc
    NT = T // 128
    s.op("pool", lambda: nc.gpsimd.memset(vt[:, :, 64:128], 0.0), writes=[vt])
    s.op("pool", lambda: nc.gpsimd.memset(vt[:, :, 64:65], 1.0), writes=[vt])
    if key + "_src" in io:
        src = io[key + "_src"]
        step = 8
        for j0 in range(0, NT, step):
            j1 = min(NT, j0 + step)
            s.dma("sp", vt[:, j0:j1, 0:64], src[j0 * 128:j1 * 128, :].rearrange("(j p) d -> p j d", p=128), writes=[vt])
    else:
        s.dma("sp", vt[:, :, 0:64], io[key][:, :, 0:64], writes=[vt])


class Pipe:
    def __init__(self, lag):
        self.q = []
        self.lag = lag

    def push(self, fn):
        self.q.append(fn)
        while len(self.q) > self.lag:
            self.q.pop(0)()

    def flush(self):
        while self.q:
            self.q.pop(0)()


def emit_attn_phase(s, cm, T, nsub, qT, kT, vt, kparts, out_dram, finalize, bias_fn=None, name="a", lag=2):
    nc = s.nc
    NQB = T // 512
    pipe = Pipe(lag)
    fin_pending = None
    for qb in range(NQB):
        q0 = qb * 512
        pos = [cm["po_rr"].next() for _ in range(nsub)]
        nt = 4 * qb + 4
        njob = 0
        for t in range(nt):
            di = t - 4 * qb
            c0 = 128 * di if di > 0 else 0
            for i in range(nsub):
                kz = kT[i]
                ps = cm["ps_rr"].next()
                s.op("pe", lambda: nc.tensor.matmul(ps[:, c0:512], lhsT=kz[:, t * 128:(t + 1) * 128],
                                                    rhs=qT[:, q0 + c0:q0 + 512], start=True, stop=(di < 0)),
                     reads=[kz, qT], writes=[ps], inc=(di < 0))
                if di >= 0:
                    s.op("pe", lambda: nc.tensor.matmul(ps[:, c0:c0 + 128], lhsT=cm["ident_b"][:, :], rhs=cm["tri_b"][:, :],
                                                        start=False, stop=True),
                         reads=[cm["ident_b"], cm["tri_b"]], writes=[ps])
                pt = cm["pt_rr"].next()
                if bias_fn is None:
                    s.op("act", lambda: nc.scalar.activation(out=pt[:, c0:512], in_=ps[:, c0:512], func=AF.Exp),
                         reads=[ps], writes=[pt])
                else:
                    bb, bap = bias_fn(qb, t)
                    s.op("act", lambda: nc.scalar.activation(out=pt[:, c0:512], in_=ps[:, c0:512], func=AF.Exp, bias=bap),
                         reads=[ps, bb], writes=[pt])

                def pv(po=pos[i], t=t, c0=c0, pt=pt, nt=nt):
                    s.op("pe", lambda: nc.tensor.matmul(po[:, c0:512], lhsT=vt[:, t, :], rhs=pt[:, c0:512],
                                                        start=(t == 0), stop=(t == nt - 1)),
                         reads=[vt, pt], writes=[po])
                pipe.push(pv)
                njob += 1
                if fin_pending is not None and njob == lag:
                    fin_pending()
                    fin_pending = None
        pipe.flush()
        if fin_pending is not None:
            fin_pending()
        fin_pending = (lambda qb=qb, pos=pos: finalize(qb, pos))
    if fin_pending is not None:
        fin_pending()


def emit_o_to_tokmajor(s, cm, po, pf, col0):
    nc = s.nc
    oc = cm["oc_rr"].next()
    s.op("act", lambda: nc.scalar.copy(out=oc[0:65, :], in_=po[0:65, :]), reads=[po], writes=[oc])
    for j in range(4):
        s.op("pe", lambda: nc.tensor.transpose(out=pf[:, j, col0:col0 + 65], in_=oc[0:65, j * 128:(j + 1) * 128],
                                               identity=cm["ident_f"][0:65, 0:65]),
             reads=[oc, cm["ident_f"]], writes=[pf], inc=(j == 3))


def build_mix_ab(T):
    nc = bass.Bass("TRN2", target_bir_lowering=False)
    io = mix_decl(nc, T, with_c=False)
    s = S(nc)
    cm = mix_common(s, io)
    emit_mix_a(s, cm, io, T)
    emit_mix_b(s, cm, io, T)
    s.finish()
    s.close()
    return nc


def mix_decl(nc, T, with_c=True):
    NT = T // 128
    io = dict(
        identb=dram_in(nc, "identb", [128, 128], BF16), identf=dram_in(nc, "identf", [128, 128]),
        trib=dram_in(nc, "trib", [128, 128], BF16),
        qa=dram_in(nc, "qa", [64, T], BF16), ka=dram_in(nc, "ka", [64, T], BF16), va=dram_in(nc, "va", [128, NT, 65], BF16),
        lamp=dram_in(nc, "lamp", [128, 4, 32]), lami=dram_in(nc, "lami", [128, 2]),
        qb=dram_in(nc, "qb", [64, T], BF16), kb=dram_in(nc, "kb", [64, T], BF16), vb=dram_in(nc, "vb", [128, NT, 65], BF16),
        flog=dram_in(nc, "flog", [128, NT]), fbias=dram_in(nc, "fbias", [128, 1]),
        triuf=dram_in(nc, "triuf", [128, 128]), onesf=dram_in(nc, "onesf", [128, 128]),
        oa=dram_out(nc, "oa", [T, 64], BF16), ob=dram_out(nc, "ob", [T, 64], BF16),
    )
    return io


def mix_common(s, io):
    nc = s.nc
    cm = {}
    for nm, key, dt in (("ident_b", "identb", BF16), ("ident_f", "identf", F32), ("tri_b", "trib", BF16)):
        b = s.sb(nm, [128, 128], dt)
        s.dma("sp", b[:, :], io[key][:, :], writes=[b])
        cm[nm] = b
    cm["epsc"] = s.sb("epsc", [128, 1], F32)
    s.op("dve", lambda: nc.vector.memset(cm["epsc"][:, :], EPS), writes=[cm["epsc"]])
    cm["ps_rr"] = RR([s.ps("ps%d" % j, [128, 512], F32) for j in range(3)])
    cm["po_rr"] = RR([s.ps("po%d" % j, [128, 512], F32) for j in range(3)])
    cm["pf"] = s.ps("pf", [128, 4, 128], F32)
    cm["pf2"] = s.ps("pf2", [128, 4, 128], F32)
    cm["pt_rr"] = RR([s.sb("pt%d" % j, [128, 512], BF16) for j in range(4)])
    cm["oc_rr"] = RR([s.sb("oc%d" % j, [128, 512], F32) for j in range(2)])
    cm["st_rr"] = RR([s.sb("mst%d" % j, [128, 8], F32) for j in range(8)])
    cm["ot_rr"] = RR([s.sb("ot%d" % j, [128, 4, 64], BF16) for j in range(2)])
    cm["tmp_rr"] = RR([s.sb("tmp%d" % j, [128, 64], F32) for j in range(4)])
    return cm


def emit_mix_a(s, cm, io, T):
    nc = s.nc
    NT = T // 128
    m = s.mark()
    qT = s.sb("a_q", [128, T], BF16)
    k1 = s.sb("a_k1", [128, T], BF16)
    k2 = s.sb("a_k2", [128, T], BF16)
    vt = s.sb("a_v", [128, NT, 128], BF16)
    s.op("pool", lambda: nc.gpsimd.memset(qT[64:128, :], 0.0), writes=[qT])
    s.op("dve", lambda: nc.vector.memset(k1[:, :], 0.0), writes=[k1])
    s.op("pool", lambda: nc.gpsimd.memset(k2[:, :], 0.0), writes=[k2])
    s.dma("sp", qT[0:64, :], io["qa"][:, :], writes=[qT])
    s.dma("sp", k1[0:32, :], io["ka"][0:32, :], writes=[k1])
    s.dma("sp", k2[32:64, :], io["ka"][32:64, :], writes=[k2])
    load_vt(s, vt, io, "va", T)
    lp = s.sb("lp", [128, 4, 32], F32)
    li = s.sb("li", [128, 2], F32)
    lw = s.sb("lw", [128, 2, 32], F32)
    lam = s.sb("lam", [128, 4], F32)
    s.dma("sp", lp[:, :, :], io["lamp"][:, :, :], writes=[lp])
    s.dma("sp", li[:, :], io["lami"][:, :], writes=[li])
    s.op("dve", lambda: nc.vector.tensor_tensor(out=lw[:, 0, :], in0=lp[:, 0, :], in1=lp[:, 1, :], op=ALU.mult),
         reads=[lp], writes=[lw])
    s.op("dve", lambda: nc.vector.tensor_tensor(out=lw[:, 1, :], in0=lp[:, 2, :], in1=lp[:, 3, :], op=ALU.mult),
         reads=[lp], writes=[lw])
    s.op("dve", lambda: nc.vector.tensor_reduce(out=lam[:, 0:2], in_=lw[:, :, :], axis=AX.X, op=ALU.add),
         reads=[lw], writes=[lam])
    s.op("act", lambda: nc.scalar.activation(out=lam[:, 0:2], in_=lam[:, 0:2], func=AF.Exp), reads=[lam], writes=[lam])
    s.op("dve", lambda: nc.vector.tensor_tensor(out=lam[:, 2:3], in0=lam[:, 1:2], in1=lam[:, 0:1], op=ALU.subtract),
         reads=[lam], writes=[lam])
    s.op("dve", lambda: nc.vector.tensor_tensor(out=lam[:, 3:4], in0=lam[:, 2:3], in1=li[:, 0:1], op=ALU.subtract),
         reads=[lam, li], writes=[lam])

    def fin(qb, pos):
        pf = cm["pf"]
        pf2 = cm["pf2"]
        emit_o_to_tokmajor(s, cm, pos[0], pf, 0)
        emit_o_to_tokmajor(s, cm, pos[1], pf2, 0)
        ot = cm["ot_rr"].next()
        for j in range(4):
            st = cm["st_rr"].next()
            s.op("dve", lambda: nc.vector.tensor_scalar(out=st[:, 0:1], in0=pf[:, j, 64:65], scalar1=1e-30, scalar2=None,
                                                        op0=ALU.max), reads=[pf], writes=[st])
            s.op("dve", lambda: nc.vector.tensor_scalar(out=st[:, 1:2], in0=pf2[:, j, 64:65], scalar1=1e-30, scalar2=None,
                                                        op0=ALU.max), reads=[pf2], writes=[st])
            s.op("dve", lambda: nc.vector.reciprocal(out=st[:, 0:2], in_=st[:, 0:2]), reads=[st], writes=[st])
            t2 = cm["tmp_rr"].next()
            o = cm["tmp_rr"].next()
            s.op("dve", lambda: nc.vector.tensor_scalar(out=t2[:, :], in0=pf2[:, j, 0:64], scalar1=st[:, 1:2],
                                                        scalar2=lam[:, 3:4], op0=ALU.mult, op1=ALU.mult),
                 reads=[pf2, st, lam], writes=[t2])
            s.op("dve", lambda: nc.vector.scalar_tensor_tensor(out=o[:, :], in0=pf[:, j, 0:64], scalar=st[:, 0:1], in1=t2[:, :],
                                                               op0=ALU.mult, op1=ALU.add), reads=[pf, st, t2], writes=[o])
            s.op("act", lambda: nc.scalar.activation(out=t2[:, :], in_=o[:, :], func=AF.Square, accum_out=st[:, 2:3]),
                 reads=[o], writes=[t2, st])
            s.op("act", lambda: nc.scalar.activation(out=st[:, 3:4], in_=st[:, 2:3], func=AF.Ln, bias=cm["epsc"][:, 0:1],
                                                     scale=1.0 / 64.0), reads=[st, cm["epsc"]], writes=[st])
            s.op("act", lambda: nc.scalar.activation(out=st[:, 4:5], in_=st[:, 3:4], func=AF.Exp, scale=-0.5),
                 reads=[st], writes=[st])
            s.op("dve", lambda: nc.vector.tensor_scalar(out=ot[:, j, :], in0=o[:, :], scalar1=st[:, 4:5], scalar2=li[:, 1:2],
                                                        op0=ALU.mult, op1=ALU.mult), reads=[o, st, li], writes=[ot])
        s.dma("sp", io["oa"][qb * 512:(qb + 1) * 512, :].rearrange("(j p) d -> p j d", p=128), ot[:, :, :], reads=[ot])

    emit_attn_phase(s, cm, T, 2, qT, [k1, k2], vt, None, io["oa"], fin, name="a")
    s.release(m)


def emit_mix_b(s, cm, io, T):
    nc = s.nc
    NT = T // 128
    NQB = T // 512
    m = s.mark()
    qT = s.sb("b_q", [128, T], BF16)
    kT = s.sb("b_k", [128, T], BF16)
    vt = s.sb("b_v", [128, NT, 128], BF16)
    s.op("pool", lambda: nc.gpsimd.memset(qT[64:128, :], 0.0), writes=[qT])
    s.op("dve", lambda: nc.vector.memset(kT[64:128, :], 0.0), writes=[kT])
    s.dma("sp", qT[0:64, :], io["qb"][:, :], writes=[qT])
    s.dma("sp", kT[0:64, :], io["kb"][:, :], writes=[kT])
    load_vt(s, vt, io, "vb", T)
    fl = s.sb("fl", [128, NT], F32)
    fb = s.sb("fb", [128, 2], F32)
    tu = s.sb("tu", [128, 128], F32)
    on = s.sb("on", [128, 128], F32)
    if "flog_sb" not in io:
        s.dma("sp", fl[:, :], io["flog"][:, :], writes=[fl])
    s.dma("sp", fb[:, 0:1], io["fbias"][:, :], writes=[fb])
    s.dma("sp", tu[:, :], io["triuf"][:, :], writes=[tu])
    s.dma("sp", on[:, :], io["onesf"][:, :], writes=[on])
    s.op("dve", lambda: nc.vector.tensor_scalar(out=fb[:, 1:2], in0=fb[:, 0:1], scalar1=-1.0, scalar2=None, op0=ALU.mult),
         reads=[fb], writes=[fb])
    if "flog_sb" in io:
        fsb, fap = io["flog_sb"]
        s.op("act", lambda: nc.scalar.activation(out=fl[:, :], in_=fap, func=AF.Exp, bias=fb[:, 1:2], scale=-1.0),
             reads=[fsb, fb], writes=[fl])
    else:
        s.op("act", lambda: nc.scalar.activation(out=fl[:, :], in_=fl[:, :], func=AF.Exp, bias=fb[:, 1:2], scale=-1.0),
             reads=[fl, fb], writes=[fl])
    s.op("act", lambda: nc.scalar.activation(out=fl[:, :], in_=fl[:, :], func=AF.Ln, bias=1.0, scale=1.0),
         reads=[fl], writes=[fl])
    pc = cm["pf"]
    pcv = pc[:, 0, :]
    s.op("pe", lambda: nc.tensor.matmul(pc[:, 0, 0:NT], lhsT=tu[:, :], rhs=fl[:, :], start=True, stop=True),
         reads=[tu, fl], writes=[pc])
    s.op("pe", lambda: nc.tensor.matmul(pc[:, 1, 0:NT], lhsT=on[:, :], rhs=fl[:, :], start=True, stop=True),
         reads=[on, fl], writes=[pc])
    cc = s.sb("cc", [128, NT], F32)
    inc_ = s.sb("inc", [128, NT], F32)
    tmpc = s.sb("tmpc", [128, NT], F32)
    s.op("dve", lambda: nc.vector.tensor_copy(out=inc_[:, :], in_=pc[:, 1, 0:NT]), reads=[pc], writes=[inc_])
    sh = 1
    while sh < NT:
        s.op("dve", lambda: nc.vector.tensor_copy(out=tmpc[:, :], in_=inc_[:, :]), reads=[inc_], writes=[tmpc])
        s.op("dve", lambda: nc.vector.tensor_tensor(out=inc_[:, sh:NT], in0=tmpc[:, sh:NT], in1=tmpc[:, 0:NT - sh], op=ALU.add),
             reads=[tmpc], writes=[inc_])
        sh *= 2
    s.op("dve", lambda: nc.vector.tensor_tensor(out=cc[:, :], in0=pc[:, 0, 0:NT], in1=inc_[:, :], op=ALU.add),
         reads=[pc, inc_], writes=[cc])
    s.op("dve", lambda: nc.vector.tensor_tensor(out=tmpc[:, :], in0=cc[:, :], in1=pc[:, 1, 0:NT], op=ALU.subtract),
         reads=[pc, cc], writes=[tmpc])
    btab = s.sb("btab", [128, NQB, NT], F32)
    for qb in range(NQB):
        s.op("dve", lambda: nc.vector.tensor_scalar(out=btab[:, qb, :], in0=tmpc[:, :], scalar1=inc_[:, 4 * qb + 1:4 * qb + 2],
                                                    scalar2=None, op0=ALU.subtract), reads=[tmpc, inc_], writes=[btab])

    def bias_fn(qb, t):
        return btab, btab[:, qb, t:t + 1]

    def fin(qb, pos):
        pf = cm["pf"]
        emit_o_to_tokmajor(s, cm, pos[0], pf, 0)
        ot = cm["ot_rr"].next()
        for j in range(4):
            st = cm["st_rr"].next()
            s.op("dve", lambda: nc.vector.tensor_scalar(out=st[:, 0:1], in0=pf[:, j, 64:65], scalar1=1e-30, scalar2=None,
                                                        op0=ALU.max), reads=[pf], writes=[st])
            s.op("dve", lambda: nc.vector.reciprocal(out=st[:, 0:1], in_=st[:, 0:1]), reads=[st], writes=[st])
            s.op("dve", lambda: nc.vector.tensor_scalar(out=ot[:, j, :], in0=pf[:, j, 0:64], scalar1=st[:, 0:1], scalar2=None,
                                                        op0=ALU.mult), reads=[pf, st], writes=[ot])
        s.dma("sp", io["ob"][qb * 512:(qb + 1) * 512, :].rearrange("(j p) d -> p j d", p=128), ot[:, :, :], reads=[ot])

    emit_attn_phase(s, cm, T, 1, qT, [kT], vt, None, io["ob"], fin, bias_fn=bias_fn, name="b")
    s.release(m)


def mix_consts():
    k = np.arange(128)
    tri = np.where(k[:, None] > k[None, :], NEG, 0.0).astype(np.float32)
    return dict(identb=_bf(np.eye(128, dtype=np.float32)), identf=np.eye(128, dtype=np.float32), trib=_bf(tri),
                triuf=np.triu(np.ones((128, 128), np.float32)), onesf=np.ones((128, 128), np.float32))


def mix_decl_c(nc, io, T):
    NT = T // 128
    QL = NT // 4
    NCT = max(1, T // 2048)
    io.update(dict(
        qc=dram_in(nc, "qc", [128, QL, 512], BF16),
        kskw=dram_in(nc, "kskw", [128, T], BF16),
        vs=dram_in(nc, "vs", [128, NT, 65], BF16), vw=dram_in(nc, "vw", [128, NT, 65], BF16),
        kvin=dram_in(nc, "kvin", [128, T], BF16),
        w1=dram_in(nc, "w1", [2, 2048, 256]), b1=dram_in(nc, "b1", [128, 4]),
        peT=dram_in(nc, "peT", [128, 32]),
        w2=dram_in(nc, "w2", [2, 256, 64]), b2=dram_in(nc, "b2", [2, 64]), b2c=dram_in(nc, "b2c", [64, 1]),
        kgain=dram_in(nc, "kgain", [64, 1]),
        ng=dram_in(nc, "ng", [128, QL, 12]),
        cmask=dram_in(nc, "cmask", [128, QL, NCT, 128], BF16),
        smask=dram_in(nc, "smask", [128, 4, 128], BF16), wmask=dram_in(nc, "wmask", [128, 8, 128], BF16),
        impA=dram_in(nc, "impA", [128, QL, 128]), impB=dram_in(nc, "impB", [128, QL, 128]),
        emat=dram_in(nc, "emat", [128, NT, 128], BF16), ovl=dram_in(nc, "ovl", [128, NCT, 128], BF16),
        ones64=dram_in(nc, "ones64", [64, 64], BF16), onesrow=dram_in(nc, "onesrow", [1, 128], BF16),
        oc=dram_out(nc, "oc", [QL * 128, 256], BF16),
    ))
    return io


def emit_gelu(s, zin_ap, zin_b, out_ap, out_b, tmp, shape_sl):
    nc = s.nc
    t = tmp
    s.op("act", lambda: nc.scalar.activation(out=t[shape_sl], in_=zin_ap, func=AF.Square), reads=[zin_b], writes=[t])
    s.op("dve", lambda: nc.vector.tensor_scalar(out=t[shape_sl], in0=t[shape_sl], scalar1=0.044715, scalar2=1.0,
                                                op0=ALU.mult, op1=ALU.add), reads=[t], writes=[t])
    s.op("dve", lambda: nc.vector.tensor_tensor(out=t[shape_sl], in0=t[shape_sl], in1=zin_ap, op=ALU.mult),
         reads=[t, zin_b], writes=[t])
    s.op("act", lambda: nc.scalar.activation(out=t[shape_sl], in_=t[shape_sl], func=AF.Exp, scale=-GELU_C), reads=[t], writes=[t])
    s.op("dve", lambda: nc.vector.tensor_scalar(out=t[shape_sl], in0=t[shape_sl], scalar1=1.0, scalar2=None, op0=ALU.add),
         reads=[t], writes=[t])
    s.op("dve", lambda: nc.vector.reciprocal(out=t[shape_sl], in_=t[shape_sl]), reads=[t], writes=[t])
    s.op("dve", lambda: nc.vector.tensor_tensor(out=out_ap, in0=t[shape_sl], in1=zin_ap, op=ALU.mult),
         reads=[t, zin_b], writes=[out_b])


def emit_mix_c(s, cm, io, T, cs=None):
    fused = cs is not None
    cs = cs if fused else [None]
    nc = s.nc
    NT = T // 128
    QL = NT // 4
    NCT = max(1, T // 2048)
    Nc = T // 16 - 1
    NCP = NCT * 128 if Nc > 128 else 128
    NCW = min(Nc, 511)
    assert Nc <= 511
    m = s.mark()
    ident_b = cm["ident_b"]
    ps_l = cm["ps_rr"].items
    po_l = cm["po_rr"].items
    pf, pf2 = cm["pf"], cm["pf2"]

    def ld(name, shape, dt, src, q="sp"):
        b = s.sb(name, shape, dt)
        idx = tuple(slice(None) for _ in shape)
        s.dma(q, b[idx], src, writes=[b])
        return b

    qc = s.sb("c_q", [128, QL, 512], BF16)
    qc2 = s.sb("c_q2", [128, QL, 512], BF16)
    s.op("pool", lambda: nc.gpsimd.memset(qc[64:128, :, :], 0.0), writes=[qc])
    s.op("dve", lambda: nc.vector.memset(qc2[0:64, :, :], 0.0), writes=[qc2])
    kk = ld("c_kk", [128, T], BF16, io["kskw"][:, :])
    vs = s.sb("c_vs", [128, NT, 128], BF16)
    vw = s.sb("c_vw", [128, NT, 128], BF16)
    load_vt(s, vs, io, "vs", T)
    load_vt(s, vw, io, "vw", T)
    emat = ld("c_e", [128, NT, 128], BF16, io["emat"][:, :, :])
    ovl = ld("c_ovl", [128, NCT, 128], BF16, io["ovl"][:, :, :])
    smask = s.sb("c_sm", [128, 4, 128], BF16)
    wmask = s.sb("c_wm", [128, 8, 128], BF16)
    ngt = s.sb("c_ng", [128, QL, 12], F32)
    ones64 = ld("c_o64", [64, 64], BF16, io["ones64"][:, :])
    onesrow = ld("c_orow", [1, 128], BF16, io["onesrow"][:, :])
    kgain = ld("c_kg", [64, 1], F32, io["kgain"][:, :])
    b2c = ld("c_b2c", [64, 1], F32, io["b2c"][:, :])
    b1 = ld("c_b1", [128, 4], F32, io["b1"][:, :])

    ktc = s.sb("c_ktc", [128, NCP], BF16)
    vc = s.sb("c_vc", [128, NCT, 128], BF16)
    s.op("dve", lambda: nc.vector.memset(ktc[:, :], 0.0), writes=[ktc])
    s.op("dve", lambda: nc.vector.memset(vc[:, :, :], 0.0), writes=[vc])
    s.op("dve", lambda: nc.vector.memset(vc[:, :, 64:65], 1.0), writes=[vc])

    m2 = s.mark()
    kvin = ld("c_kvin", [128, T], BF16, io["kvin"][:, :])
    w1sb = s.sb("c_w1", [128, 32, 256], BF16)
    for x in range(2):
        s.dma("pool", w1sb[x * 64:(x + 1) * 64, :, :], io["w1"][x].rearrange("(j d) f -> d j f", d=64), writes=[w1sb])
    peT = s.sb("c_pe", [128, 32], BF16)
    s.dma("pool", peT[:, :], io["peT"][:, :], writes=[peT])
    w2sb = s.sb("c_w2", [128, 2, 2, 64], BF16)
    for x in range(2):
        s.dma("pool", w2sb[:, x, :, :], io["w2"][x].rearrange("(hh f) d -> f hh d", f=128), writes=[w2sb])
    b2row = s.sb("c_b2r", [1, 64], BF16)
    s.dma("pool", b2row[:, :], io["b2"][1:2, :], writes=[b2row])
    hacc = [ps_l[0], ps_l[1], ps_l[2], po_l[0]]
    pcol = po_l[1]
    for x in range(2):
        for hh in range(2):
            hp = hacc[x * 2 + hh]
            for j in range(32):
                s.op("pe", lambda: nc.tensor.matmul(hp[:, 0:NCW], lhsT=w1sb[x * 64:(x + 1) * 64, j, hh * 128:(hh + 1) * 128],
                                                    rhs=kvin[x * 64:(x + 1) * 64, j:j + 16 * (NCW - 1) + 1:16],
                                                    start=(j == 0), stop=(j == 31)),
                     reads=[w1sb, kvin], writes=[hp], inc=(j == 31))
            for j in range(32):
                s.op("pe", lambda: nc.tensor.matmul(pcol[:, x * 2 + hh:x * 2 + hh + 1],
                                                    lhsT=w1sb[x * 64:(x + 1) * 64, j, hh * 128:(hh + 1) * 128],
                                                    rhs=peT[x * 64:(x + 1) * 64, j:j + 1], start=(j == 0), stop=(j == 31)),
                     reads=[w1sb, peT], writes=[pcol], inc=(j == 31))
    hbias = s.sb("c_hb", [128, 4], F32)
    s.op("dve", lambda: nc.vector.tensor_tensor(out=hbias[:, :], in0=pcol[:, 0:4], in1=b1[:, :], op=ALU.add),
         reads=[pcol, b1], writes=[hbias])
    gh = []
    for x in range(2):
        for hh in range(2):
            k = x * 2 + hh
            z = s.sb("c_z%d" % k, [128, 512], F32)
            tmp = s.sb("c_zt%d" % k, [128, 512], F32)
            gb = s.sb("c_g%d" % k, [128, 512], BF16)
            s.op("act", lambda: nc.scalar.activation(out=z[:, 0:NCW], in_=hacc[k][:, 0:NCW], func=AF.Identity,
                                                     bias=hbias[:, k:k + 1], scale=1.0), reads=[hacc[k], hbias], writes=[z])
            emit_gelu(s, z[:, 0:NCW], z, gb[:, 0:NCW], gb, tmp, (slice(None), slice(0, NCW)))
            gh.append(gb)
    pk = po_l[2]
    for hh in range(2):
        s.op("pe", lambda: nc.tensor.matmul(pk[0:64, 0:NCW], lhsT=w2sb[:, 0, hh, :], rhs=gh[hh][:, 0:NCW],
                                            start=(hh == 0), stop=(hh == 1)), reads=[w2sb, gh[hh]], writes=[pk], inc=(hh == 1))
    kz = s.sb("c_kz", [64, 512], F32)
    ksq = s.sb("c_ksq", [64, 512], BF16)
    krs = s.sb("c_krs", [64, 512], F32)
    s.op("act", lambda: nc.scalar.activation(out=kz[:, 0:NCW], in_=pk[0:64, 0:NCW], func=AF.Identity, bias=b2c[:, 0:1], scale=1.0),
         reads=[pk, b2c], writes=[kz])
    s.op("act", lambda: nc.scalar.activation(out=ksq[:, 0:NCW], in_=kz[:, 0:NCW], func=AF.Square), reads=[kz], writes=[ksq])
    pq = ps_l[0]
    s.op("pe", lambda: nc.tensor.matmul(pq[0:64, 0:NCW], lhsT=ones64[:, :], rhs=ksq[:, 0:NCW], start=True, stop=True),
         reads=[ones64, ksq], writes=[pq])
    s.op("act", lambda: nc.scalar.activation(out=krs[:, 0:NCW], in_=pq[0:64, 0:NCW], func=AF.Ln, bias=cm["epsc"][0:64, 0:1],
                                             scale=1.0 / 64.0), reads=[pq, cm["epsc"]], writes=[krs])
    s.op("act", lambda: nc.scalar.activation(out=krs[:, 0:NCW], in_=krs[:, 0:NCW], func=AF.Exp, scale=-0.5), reads=[krs], writes=[krs])
    s.op("dve", lambda: nc.vector.scalar_tensor_tensor(out=ktc[0:64, 0:NCW], in0=kz[:, 0:NCW], scalar=kgain[:, 0:1], in1=krs[:, 0:NCW],
                                                       op0=ALU.mult, op1=ALU.mult), reads=[kz, kgain, krs], writes=[ktc])
    for nt in range(NCT):
        n0 = nt * 128
        nn = min(128, Nc - n0)
        pv = ps_l[1 + nt % 2]
        for hh in range(2):
            s.op("pe", lambda: nc.tensor.matmul(pv[0:nn, 0:64], lhsT=gh[2 + hh][:, n0:n0 + nn], rhs=w2sb[:, 1, hh, :],
                                                start=(hh == 0), stop=False), reads=[gh[2 + hh], w2sb], writes=[pv], inc=False)
        s.op("pe", lambda: nc.tensor.matmul(pv[0:nn, 0:64], lhsT=onesrow[0:1, 0:nn], rhs=b2row[0:1, :], start=False, stop=True),
             reads=[onesrow, b2row], writes=[pv])
        s.op("act", lambda: nc.scalar.copy(out=vc[0:nn, nt, 0:64], in_=pv[0:nn, 0:64]), reads=[pv], writes=[vc])
    s.release(m2)

    cmk_rr = RR([s.sb("c_cmk%d" % j, [128, NCT, 128], BF16) for j in range(2)])
    ia_rr = RR([s.sb("c_ia%d" % j, [128, 128], F32) for j in range(2)])
    ib_rr = RR([s.sb("c_ib%d" % j, [128, 128], F32) for j in range(2)])
    imp_rr = RR([s.sb("c_imp%d" % j, [128, 128], F32) for j in range(2)])
    imp2_rr = RR([s.sb("c_impb%d" % j, [128, 128], F32) for j in range(2)])
    m8_rr = RR([s.sb("c_m8%d" % j, [128, 16], F32) for j in range(2)])
    mbT_rr = RR([s.sb("c_mbT%d" % j, [128, 128], BF16) for j in range(2)])
    oco_rr = RR([s.sb("c_oc%d" % j, [128, 4, 64], F32) for j in range(2)])
    gw_rr = RR([s.sb("c_gw%d" % j, [128, 12], F32) for j in range(2)])
    oo_rr = RR([s.sb("c_oo%d" % j, [128, 4, 64], F32) for j in range(2)])
    ob_rr = RR([s.sb("c_ob%d" % j, [128, 4, 64], BF16) for j in range(2)])

    pipe = Pipe(2)

    def masked_tile(kbuf, prow, t, Q, masks, vbuf, vt_idx, po, first, last, extra=None):
        ps = cm["ps_rr"].next()
        nm = len(masks)
        s.op("pe", lambda: nc.tensor.matmul(ps[:, :], lhsT=kbuf[:, t * 128:(t + 1) * 128], rhs=Q,
                                            start=True, stop=(nm == 0)), reads=[kbuf, qc, qc2], writes=[ps], inc=(nm == 0))
        for mi, (la, lb, ra, rb) in enumerate(masks):
            for h in range(4):
                lastm = (mi == nm - 1 and h == 3)
                s.op("pe", lambda: nc.tensor.matmul(ps[:, h * 128:(h + 1) * 128], lhsT=la, rhs=ra, start=False, stop=lastm),
                     reads=[lb, rb], writes=[ps], inc=lastm)
        pt = cm["pt_rr"].next()
        s.op("act", lambda: nc.scalar.activation(out=pt[:, :], in_=ps[:, :], func=AF.Exp), reads=[ps], writes=[pt])

        def back(pt=pt, po=po, vbuf=vbuf, vt_idx=vt_idx, first=first, last=last, extra=extra):
            s.op("pe", lambda: nc.tensor.matmul(po[:, :], lhsT=vbuf[:, vt_idx, :], rhs=pt[:, :], start=first, stop=last),
                 reads=[vbuf, pt], writes=[po])
            if extra is not None:
                extra(pt)
        pipe.push(back)

    for ci in cs:
        def gk(key):
            return io[key][ci] if fused else io[key]
        if fused:
            for i in range(QL):
                qt = 4 * i + ci
                for h in range(4):
                    srcq = io["zq"][h // 2][(h % 2) * 64:(h % 2) * 64 + 64, qt * 128:(qt + 1) * 128]
                    s.dma("sp", qc[0:64, i, h * 128:(h + 1) * 128], srcq, writes=[qc])
                    s.dma("sp", qc2[64:128, i, h * 128:(h + 1) * 128], srcq, writes=[qc2])
            msb, mview = io["misc_sb"]
            s.op("act", lambda: nc.scalar.activation(out=ngt[:, :, :], in_=mview[:, ci:NT:4, 4:16], func=AF.Exp, scale=-1.0),
                 reads=[msb], writes=[ngt])
        else:
            s.dma("sp", qc[0:64, :, :], io["qc"][0:64, :, :], writes=[qc])
            s.dma("sp", qc2[64:128, :, :], io["qc"][64:128, :, :], writes=[qc2])
            s.dma("sp", ngt[:, :, :], io["ng"][:, :, :], writes=[ngt])
            s.op("act", lambda: nc.scalar.activation(out=ngt[:, :, :], in_=ngt[:, :, :], func=AF.Exp, scale=-1.0),
                 reads=[ngt], writes=[ngt])
        s.op("dve", lambda: nc.vector.tensor_scalar(out=ngt[:, :, :], in0=ngt[:, :, :], scalar1=1.0, scalar2=None, op0=ALU.add),
             reads=[ngt], writes=[ngt])
        s.op("dve", lambda: nc.vector.reciprocal(out=ngt[:, :, :], in_=ngt[:, :, :]), reads=[ngt], writes=[ngt])
        s.dma("sp", smask[:, :, :], gk("smask")[:, :, :], writes=[smask])
        s.dma("sp", wmask[:, :, :], gk("wmask")[:, :, :], writes=[wmask])
        for i in range(QL):
            Qlo = qc[:, i, :]
            Qhi = qc2[:, i, :]
            cmk = cmk_rr.next()
            ia = ia_rr.next()
            ib = ib_rr.next()
            s.dma("sp", cmk[:, :, :], gk("cmask")[:, i, :, :], writes=[cmk])
            s.dma("sp", ia[:, :], gk("impA")[:, i, :], writes=[ia])
            s.dma("sp", ib[:, :], gk("impB")[:, i, :], writes=[ib])
            po_c, po_s, po_w = po_l[0], po_l[1], po_l[2]
            nct = min(NCT, i // 4 + 1)
            for nt in range(nct):
                def imp_mm(pt, nt=nt, nct=nct):
                    for h in range(4):
                        s.op("pe", lambda: nc.tensor.matmul(pf2[:, h, :], lhsT=pt[:, h * 128:(h + 1) * 128], rhs=ovl[:, nt, :],
                                                            start=(nt == 0 and h == 0), stop=(nt == nct - 1 and h == 3),
                                                            skip_group_check=True), reads=[pt, ovl], writes=[pf2],
                             inc=(h == 3))
                masked_tile(ktc, (0, 64), nt, Qlo, [(ident_b[:, :], ident_b, cmk[:, nt, :], cmk)], vc, nt, po_c,
                            nt == 0, nt == nct - 1, extra=imp_mm)
            pipe.flush()
            emit_o_to_tokmajor(s, cm, po_c, pf, 0)
            st = cm["st_rr"].next()
            rsum = cm["st_rr"].next()
            gw = gw_rr.next()
            s.op("dve", lambda: nc.vector.tensor_scalar(out=st[:, 0:4], in0=pf[:, :, 64], scalar1=1e-30, scalar2=None, op0=ALU.max),
                 reads=[pf], writes=[st])
            s.op("dve", lambda: nc.vector.reciprocal(out=rsum[:, 0:4], in_=st[:, 0:4]), reads=[st], writes=[rsum])
            oco = oco_rr.next()
            s.op("dve", lambda: nc.vector.tensor_copy(out=oco[:, :, :], in_=pf[:, :, 0:64]), reads=[pf], writes=[oco])
            imp = imp_rr.next()
            s.op("dve", lambda: nc.vector.tensor_scalar(out=imp[:, :], in0=pf2[:, 0, :], scalar1=rsum[:, 0:1], scalar2=None, op0=ALU.mult),
                 reads=[pf2, rsum], writes=[imp])
            for h in range(1, 4):
                s.op("dve", lambda: nc.vector.scalar_tensor_tensor(out=imp[:, :], in0=pf2[:, h, :], scalar=rsum[:, h:h + 1], in1=imp[:, :],
                                                                   op0=ALU.mult, op1=ALU.add), reads=[pf2, rsum, imp], writes=[imp])
            s.op("dve", lambda: nc.vector.tensor_tensor(out=imp[:, :], in0=imp[:, :], in1=ia[:, :], op=ALU.mult), reads=[imp, ia], writes=[imp])
            s.op("dve", lambda: nc.vector.tensor_tensor(out=imp[:, :], in0=imp[:, :], in1=ib[:, :], op=ALU.add), reads=[imp, ib], writes=[imp])
            m8 = m8_rr.next()
            imp2 = imp2_rr.next()
            s.op("dve", lambda: nc.vector.max(out=m8[:, 0:8], in_=imp[:, :]), reads=[imp], writes=[m8])
            s.op("dve", lambda: nc.vector.match_replace(out=imp2[:, :], in_to_replace=m8[:, 0:8], in_values=imp[:, :], imm_value=-1e9),
                 reads=[imp, m8], writes=[imp2])
            s.op("dve", lambda: nc.vector.max(out=m8[:, 8:16], in_=imp2[:, :]), reads=[imp2], writes=[m8])
            s.op("dve", lambda: nc.vector.tensor_scalar(out=imp2[:, :], in0=imp[:, :], scalar1=m8[:, 15:16], scalar2=NEG,
                                                        op0=ALU.is_lt, op1=ALU.mult), reads=[imp, m8], writes=[imp2])
            tl = [4 * (i - 1) + u for u in range(8) if 4 * (i - 1) + u >= 0]
            for t in tl:
                u = t - 4 * (i - 1)
                masked_tile(kk, (64, 128), t, Qhi, [(ident_b[:, :], ident_b, wmask[:, u, :], wmask)], vw, t, po_w,
                            t == tl[0], t == tl[-1])
            ptr = cm["ps_rr"].next()
            s.op("pe", lambda: nc.tensor.transpose(out=ptr[:, 0:128], in_=imp2[:, :], identity=cm["ident_f"][:, :]),
                 reads=[imp2, cm["ident_f"]], writes=[ptr])
            mbT = mbT_rr.next()
            s.op("act", lambda: nc.scalar.copy(out=mbT[:, :], in_=ptr[:, 0:128]), reads=[ptr], writes=[mbT])
            nts = 4 * i + 4
            for t in range(nts):
                masks = [(emat[:, t, :], emat, mbT[:, :], mbT)]
                if t >= 4 * i:
                    masks.append((ident_b[:, :], ident_b, smask[:, t - 4 * i, :], smask))
                masked_tile(kk, (0, 64), t, Qlo, masks, vs, t, po_s, t == 0, t == nts - 1)
            pipe.flush()
            emit_o_to_tokmajor(s, cm, po_s, pf, 0)
            st2 = cm["st_rr"].next()
            s.op("dve", lambda: nc.vector.tensor_scalar(out=st2[:, 0:4], in0=pf[:, :, 64], scalar1=1e-30, scalar2=None, op0=ALU.max),
                 reads=[pf], writes=[st2])
            s.op("dve", lambda: nc.vector.reciprocal(out=st2[:, 0:4], in_=st2[:, 0:4]), reads=[st2], writes=[st2])
            gv = ngt[:, i, :].rearrange("p (h b) -> p h b", b=3)
            gwv = gw[:, :].rearrange("p (h b) -> p h b", b=3)
            s.op("dve", lambda: nc.vector.tensor_tensor(out=gwv[:, :, 0], in0=gv[:, :, 0], in1=rsum[:, 0:4], op=ALU.mult),
                 reads=[ngt, rsum], writes=[gw])
            s.op("dve", lambda: nc.vector.tensor_tensor(out=gwv[:, :, 1], in0=gv[:, :, 1], in1=st2[:, 0:4], op=ALU.mult),
                 reads=[ngt, st2], writes=[gw])
            oo = oo_rr.next()
            for h in range(4):
                s.op("dve", lambda: nc.vector.tensor_scalar(out=oo[:, h, :], in0=oco[:, h, :], scalar1=gw[:, 3 * h:3 * h + 1], scalar2=None,
                                                            op0=ALU.mult), reads=[oco, gw], writes=[oo])
                s.op("dve", lambda: nc.vector.scalar_tensor_tensor(out=oo[:, h, :], in0=pf[:, h, 0:64], scalar=gw[:, 3 * h + 1:3 * h + 2],
                                                                   in1=oo[:, h, :], op0=ALU.mult, op1=ALU.add), reads=[pf, gw, oo], writes=[oo])
            emit_o_to_tokmajor(s, cm, po_w, pf, 0)
            st3 = cm["st_rr"].next()
            s.op("dve", lambda: nc.vector.tensor_scalar(out=st3[:, 0:4], in0=pf[:, :, 64], scalar1=1e-30, scalar2=None, op0=ALU.max),
                 reads=[pf], writes=[st3])
            s.op("dve", lambda: nc.vector.reciprocal(out=st3[:, 0:4], in_=st3[:, 0:4]), reads=[st3], writes=[st3])
            s.op("dve", lambda: nc.vector.tensor_tensor(out=gwv[:, :, 2], in0=gv[:, :, 2], in1=st3[:, 0:4], op=ALU.mult),
                 reads=[ngt, st3], writes=[gw])
            ob = ob_rr.next()
            for h in range(4):
                s.op("dve", lambda: nc.vector.scalar_tensor_tensor(out=ob[:, h, :], in0=pf[:, h, 0:64], scalar=gw[:, 3 * h + 2:3 * h + 3],
                                                                   in1=oo[:, h, :], op0=ALU.mult, op1=ALU.add), reads=[pf, gw, oo], writes=[ob])
            orow = ((4 * i + ci) if fused else i) * 128
            s.dma("sp", io["oc"][orow:orow + 128, :], ob[:, :, :].rearrange("p h d -> p (h d)"), reads=[ob])
    s.release(m)


def build_mix(T, parts="abc"):
    nc = bass.Bass("TRN2", target_bir_lowering=False)
    io = mix_decl(nc, T)
    if "c" in parts:
        mix_decl_c(nc, io, T)
    s = S(nc)
    cm = mix_common(s, io)
    if "a" in parts:
        emit_mix_a(s, cm, io, T)
    if "b" in parts:
        emit_mix_b(s, cm, io, T)
    if "c" in parts:
        emit_mix_c(s, cm, io, T)
    s.finish()
    s.close()
    return nc


def mix_consts_c(T, c):
    NT = T // 128
    QL = NT // 4
    NCT = max(1, T // 2048)
    Nc = T // 16 - 1
    NS = T // 64
    ar = np.arange(128)
    cmask = np.zeros((128, QL, NCT, 128), np.float32)
    impA = np.zeros((128, QL, 128), np.float32)
    impB = np.zeros((128, QL, 128), np.float32)
    for i in range(QL):
        qpos = 128 * (4 * i + c) + ar
        for nt in range(NCT):
            n = 128 * nt + ar
            ok = (16 * n[:, None] + 31 <= qpos[None, :]) & (n[:, None] < Nc)
            cmask[:, i, nt, :] = np.where(ok, 0.0, NEG)
        j = ar
        cur = qpos // 64
        forced = (j[None, :] == 0) | (j[None, :] == cur[:, None]) | (j[None, :] == cur[:, None] - 1)
        valid = (j[None, :] * 64 <= qpos[:, None]) & (j[None, :] < NS)
        impA[:, i, :] = (valid & ~forced).astype(np.float32)
        impB[:, i, :] = np.where(forced & (j[None, :] < NS), 1.0e4, np.where(valid, 0.0, -1.0))
    smask = np.zeros((128, 4, 128), np.float32)
    for u in range(4):
        kpos = 128 * u + ar
        qp = 128 * c + ar
        smask[:, u, :] = np.where(kpos[:, None] <= qp[None, :], 0.0, NEG)
    wmask = np.zeros((128, 8, 128), np.float32)
    for u in range(8):
        dist = 128 * (c + 4 - u) + ar[None, :] - ar[:, None]
        wmask[:, u, :] = np.where((dist >= 0) & (dist < 512), 0.0, NEG)
    emat = np.zeros((128, NT, 128), np.float32)
    for t in range(NT):
        for k in range(128):
            jj = 2 * t + k // 64
            if jj < 128:
                emat[jj, t, k] = 1.0
    ovl = np.zeros((128, NCT, 128), np.float32)
    for nt in range(NCT):
        n = 128 * nt + ar
        o = (n[:, None] * 16 < (ar[None, :] + 1) * 64) & (n[:, None] * 16 + 32 > ar[None, :] * 64) & (n[:, None] < Nc) \
            & (ar[None, :] < NS)
        ovl[:, nt, :] = o
    return dict(cmask=_bf(cmask), impA=impA, impB=impB, smask=_bf(smask), wmask=_bf(wmask), emat=_bf(emat), ovl=_bf(ovl),
                ones64=_bf(np.ones((64, 64), np.float32)), onesrow=_bf(np.ones((1, 128), np.float32)))


def build_merge(NT, TB=512):
    nc = bass.Bass("TRN2", target_bir_lowering=False)
    x = dram_in(nc, "x", [NT, D])
    g = dram_in(nc, "g", [D])
    w_in = dram_in(nc, "w_in", [D, 6800])
    w_br = dram_in(nc, "w_br", [4, 256, D])
    w_o = dram_in(nc, "w_o", [D, D])
    ident = dram_in(nc, "ident", [128, 128], BF16)
    obr = dram_in(nc, "obr", [NT, D], BF16)
    y = dram_out(nc, "y", [NT, D])
    s = S(nc)
    emit_merge(s, x, g, w_in, w_br, w_o, ident, obr, y, NT, TB)
    s.finish()
    s.close()
    return nc


def emit_merge(s, x, g, w_in, w_br, w_o, ident, obr, y, NT, TB=512):
    nc = s.nc
    m_ = s.mark()
    ntile = TB // 128
    ident_b = s.sb("ident_b", [128, 128], BF16)
    s.dma("sp", ident_b[:, :], ident[:, :], writes=[ident_b])
    gcol = s.sb("gcol", [128, NKC], F32)
    s.dma("sp", gcol[:, :], g.rearrange("(c p) -> p c", p=128), writes=[gcol], allow_slow_non_contiguous=True)
    epsc = s.sb("epsc", [128, 1], F32)
    s.op("dve", lambda: nc.vector.memset(epsc[:, :], EPS), writes=[epsc])
    wg = [s.sb("wg%d" % c, [128, 4096], BF16) for c in range(NKC)]
    wb = [s.sb("wb%d" % c, [128, D], BF16) for c in range(8)]
    wo = [s.sb("wo%d" % c, [128, D], BF16) for c in range(NKC)]
    for c in range(NKC):
        for hf in range(2):
            s.dma("pool", wg[c][:, hf * 2048:(hf + 1) * 2048], w_in[c * 128:(c + 1) * 128, 2704 + hf * 2048:2704 + (hf + 1) * 2048],
                  writes=[wg[c]])
    for n in range(4):
        for cc in range(2):
            s.dma("pool", wb[2 * n + cc][:, :], w_br[n, cc * 128:(cc + 1) * 128, :], writes=[wb[2 * n + cc]])
    for c in range(NKC):
        s.dma("pool", wo[c][:, :], w_o[c * 128:(c + 1) * 128, :], writes=[wo[c]])
    xn = [s.sb("xn%d" % j, [128, D], F32) for j in range(ntile)]
    xr_rr = RR([s.sb("xr%d" % j, [128, D], F32) for j in range(2)])
    ots = [[s.sb("ot%d_%d" % (k, j), [128, D], BF16) for j in range(ntile)] for k in range(2)]
    hb_rr = RR([s.sb("hb%d" % j, [128, D], BF16) for j in range(2 * ntile)])
    stat_rr = RR([s.sb("st%d" % j, [128, 16], F32) for j in range(4)])
    hT = s.sb("hT", [128, NKC, TB], BF16)
    oT = s.sb("oT", [128, 8, TB], BF16)
    mT = [s.sb("mT%d" % c, [128, TB], BF16) for c in range(8)]
    pT_rr = RR([s.ps("pT%d" % j, [128, TB], BF16) for j in range(2)])
    pg_rr = RR([s.ps("pg%d" % j, [128, 512], F32) for j in range(2)])
    pp_rr = RR([s.ps("pp%d" % j, [128, 512], F32) for j in range(2)])
    po_rr = RR([s.ps("po%d" % j, [128, 512], F32) for j in range(2)])
    sg_rr = RR([s.sb("sg%d" % j, [128, TB], F32) for j in range(3)])
    acc_rr = RR([s.sb("acc%d" % j, [128, TB], F32) for j in range(2)])
    nblk = NT // TB

    def prep_a(tb):
        ot = ots[tb % 2]
        for j in range(ntile):
            r0 = tb * TB + j * 128
            s.dma("sp", xn[j][:, :], x[r0:r0 + 128, :], writes=[xn[j]])
            s.dma("sp", ot[j][:, :], obr[r0:r0 + 128, :], writes=[ot[j]])
        return emit_norm(s, epsc, xn, hb_rr, None, stat_rr, ntile)

    def prep_b(tb, hbs):
        ot = ots[tb % 2]
        emit_transpose_T(s, hbs, gcol, hT, ident_b, pT_rr, ntile)
        for c in range(8):
            pT = pT_rr.next()
            for j in range(ntile):
                s.op("pe", lambda: nc.tensor.transpose(out=pT[:, j * 128:(j + 1) * 128], in_=ot[j][:, c * 128:(c + 1) * 128],
                                                       identity=ident_b[:, :]), reads=[ot[j], ident_b], writes=[pT], inc=(j == ntile - 1))
            if c % 2 == 0:
                s.op("dve", lambda: nc.vector.tensor_copy(out=oT[:, c, :], in_=pT[:, 0:TB]), reads=[pT], writes=[oT])
            else:
                s.op("act", lambda: nc.scalar.copy(out=oT[:, c, :], in_=pT[:, 0:TB]), reads=[pT], writes=[oT])

    hbs_next = prep_a(0)
    prep_b(0, hbs_next)
    for tb in range(nblk):
        t0 = tb * TB
        if tb + 1 < nblk:
            hbs_next = prep_a(tb + 1)
        for dc in range(8):
            acc = acc_rr.next()
            for n in range(4):
                pg = pg_rr.next()
                pp = pp_rr.next()
                for c in range(NKC):
                    s.op("pe", lambda: nc.tensor.matmul(pg[:, 0:TB], lhsT=wg[c][:, n * 1024 + dc * 128:n * 1024 + (dc + 1) * 128],
                                                        rhs=hT[:, c, :], start=(c == 0), stop=(c == NKC - 1)),
                         reads=[wg[c], hT], writes=[pg], inc=(c == NKC - 1))
                for cc in range(2):
                    s.op("pe", lambda: nc.tensor.matmul(pp[:, 0:TB], lhsT=wb[2 * n + cc][:, dc * 128:(dc + 1) * 128],
                                                        rhs=oT[:, 2 * n + cc, :], start=(cc == 0), stop=(cc == 1)),
                         reads=[wb[2 * n + cc], oT], writes=[pp], inc=(cc == 1))
                sg = sg_rr.next()
                s.op("act", lambda: nc.scalar.activation(out=sg[:, :], in_=pg[:, 0:TB], func=AF.Sigmoid), reads=[pg], writes=[sg])
                if n == 0:
                    s.op("dve", lambda: nc.vector.tensor_tensor(out=acc[:, :], in0=sg[:, :], in1=pp[:, 0:TB], op=ALU.mult),
                         reads=[sg, pp], writes=[acc])
                else:
                    s.op("dve", lambda: nc.vector.tensor_tensor(out=sg[:, :], in0=sg[:, :], in1=pp[:, 0:TB], op=ALU.mult),
                         reads=[sg, pp], writes=[sg])
                    if n < 3:
                        s.op("pool", lambda: nc.gpsimd.tensor_tensor(out=acc[:, :], in0=acc[:, :], in1=sg[:, :], op=ALU.add),
                             reads=[acc, sg], writes=[acc])
                    else:
                        s.op("pool", lambda: nc.gpsimd.tensor_tensor(out=mT[dc][:, :], in0=acc[:, :], in1=sg[:, :], op=ALU.add),
                             reads=[acc, sg], writes=[mT[dc]])
        if tb + 1 < nblk:
            prep_b(tb + 1, hbs_next)
        for j in range(ntile):
            xr = xr_rr.next()
            s.dma("sp", xr[:, :], x[t0 + j * 128:t0 + (j + 1) * 128, :], writes=[xr])
            for hf in range(2):
                po = po_rr.next()
                for dc in range(8):
                    s.op("pe", lambda: nc.tensor.matmul(po[:, :], lhsT=mT[dc][:, j * 128:(j + 1) * 128],
                                                        rhs=wo[dc][:, hf * 512:(hf + 1) * 512], start=(dc == 0), stop=(dc == 7)),
                         reads=[mT[dc], wo[dc]], writes=[po], inc=(dc == 7))
                s.op("dve", lambda: nc.vector.tensor_tensor(out=xr[:, hf * 512:(hf + 1) * 512], in0=po[:, :],
                                                            in1=xr[:, hf * 512:(hf + 1) * 512], op=ALU.add),
                     reads=[po, xr], writes=[xr])
            s.dma("sp", y[t0 + j * 128:t0 + (j + 1) * 128, :], xr[:, :], reads=[xr])
    s.release(m_)


PARAM_SHAPES = dict(
    ffn1_norm=("L", D), ffn1_w_in=("L", D, 2 * DFF), ffn1_w_out=("L", DFF, D), mix_norm=("L", D), w_in=("L", D, 6800),
    nsa_phi_w1=("L", 2, 2048, 256), nsa_phi_w2=("L", 2, 256, 64), nsa_phi_b2=("L", 2, 64),
    w_branch=("L", 4, 256, D), w_out=("L", D, D), ffn2_norm=("L", D), ffn2_w_in=("L", D, 2 * DFF), ffn2_w_out=("L", DFF, D),
    gains=("L", 128, 6), vgain=("L", 128, 256), wsT=("L", 4, 128, 128), bsT=("L", 128, 4), lamp=("L", 128, 4, 32),
    lami=("L", 128, 2), fbias=("L", 4, 128, 1), b1l=("L", 128, 4), peT=("L", 128, 32), b2c=("L", 64, 1), kgain=("L", 64, 1),
)


def fused_const_shapes(T):
    NT = T // 128
    QL = NT // 4
    NCT = max(1, T // 2048)
    return dict(
        ident=([128, 128], BF16), identf=([128, 128], F32), blk=([2, 128, 128], BF16), triu=([128, 128], F32),
        trib=([128, 128], BF16), triuf=([128, 128], F32), onesf=([128, 128], F32), ones64=([64, 64], BF16),
        onesrow=([1, 128], BF16), cmask=([4, 128, QL, NCT, 128], BF16), smask=([4, 128, 4, 128], BF16),
        wmask=([4, 128, 8, 128], BF16), impA=([4, 128, QL, 128], F32), impB=([4, 128, QL, 128], F32),
        emat=([128, NT, 128], BF16), ovl=([128, NCT, 128], BF16))


def fused_consts(T):
    pc = proj_consts()
    mc = mix_consts()
    cc = [mix_consts_c(T, c) for c in range(4)]
    d = dict(ident=pc["ident"], identf=mc["identf"], blk=pc["blk"], triu=pc["triu"], trib=mc["trib"], triuf=mc["triuf"],
             onesf=mc["onesf"], ones64=cc[0]["ones64"], onesrow=cc[0]["onesrow"], emat=cc[0]["emat"], ovl=cc[0]["ovl"])
    for k in ("cmask", "smask", "wmask", "impA", "impB"):
        d[k] = np.ascontiguousarray(np.stack([cc[c][k] for c in range(4)], 0))
    return d


def emit_mix_fused(s, F, l, T):
    nc = s.nc
    NT = T // 128
    m = s.mark()
    io0 = dict(identb=F["ident"], identf=F["identf"], trib=F["trib"])
    cm = mix_common(s, io0)
    misc_sb = s.sb("misc_sb", [128, NT, 16], F32)
    for j0 in range(0, NT, 8):
        j1 = min(NT, j0 + 8)
        s.dma("sp", misc_sb[:, j0:j1, :], F["misc"][j0 * 128:j1 * 128, :].rearrange("(j p) c -> p j c", p=128), writes=[misc_sb])
    zfm, vab, obr = F["zfm"], F["vab"], F["obr"]
    for h in range(4):
        r0 = (h % 2) * 64
        io = dict(io0)
        io.update(qa=zfm[h // 2][r0:r0 + 64, :], ka=zfm[2 + h // 2][r0:r0 + 64, :], va_src=vab[:, h * 64:(h + 1) * 64],
                  lamp=F["lamp"][l], lami=F["lami"][l], oa=obr[:, h * 64:(h + 1) * 64])
        emit_mix_a(s, cm, io, T)
        io = dict(io0)
        io.update(qb=zfm[4 + h // 2][r0:r0 + 64, :], kb=zfm[6 + h // 2][r0:r0 + 64, :],
                  vb_src=vab[:, 256 + h * 64:256 + (h + 1) * 64], flog_sb=(misc_sb, misc_sb[:, :, h]), fbias=F["fbias"][l, h],
                  triuf=F["triuf"], onesf=F["onesf"], ob=obr[:, 256 + h * 64:256 + (h + 1) * 64])
        emit_mix_b(s, cm, io, T)
    io = dict(io0)
    io.update(zq=(zfm[8], zfm[9]), kskw=zfm[10], kvin=zfm[11], vs_src=F["vsw"][:, 0:64], vw_src=F["vsw"][:, 64:128],
              misc_sb=(misc_sb, misc_sb), w1=F["nsa_phi_w1"][l], b1=F["b1l"][l], peT=F["peT"][l], w2=F["nsa_phi_w2"][l],
              b2=F["nsa_phi_b2"][l], b2c=F["b2c"][l], kgain=F["kgain"][l], oc=obr[:, 512:768])
    for k in ("cmask", "smask", "wmask", "impA", "impB", "emat", "ovl", "ones64", "onesrow"):
        io[k] = F[k]
    emit_mix_c(s, cm, io, T, cs=[0, 1, 2, 3])
    s.release(m)


def build_fused(T, L):
    nc = bass.Bass("TRN2", target_bir_lowering=False)
    F = {}
    F["x"] = dram_in(nc, "x", [T, D])
    for k, shp in PARAM_SHAPES.items():
        F[k] = dram_in(nc, k, [L if v == "L" else v for v in shp])
    for k, (shp, dt) in fused_const_shapes(T).items():
        F[k] = dram_in(nc, k, shp, dt)
    y = dram_out(nc, "y", [T, D])
    for k, shp, dt in (("xa", [T, D], F32), ("xb", [T, D], F32), ("xc", [T, D], F32), ("zfm", [NFM, 128, T], BF16),
                       ("vab", [T, 512], BF16), ("vsw", [T, 128], BF16), ("misc", [T, 16], F32), ("obr", [T, D], BF16)):
        F[k] = nc.dram_tensor("s_" + k, shp, dt).ap()
    s = S(nc)
    for l in range(L):
        x_in = F["x"] if l == 0 else F["xc"]
        emit_ffn(s, x_in, F["ffn1_norm"][l], F["ffn1_w_in"][l], F["ffn1_w_out"][l], F["ident"], F["xa"], T)
        a = dict(x=F["xa"], g=F["mix_norm"][l], w_in=F["w_in"][l], ident=F["ident"], gains=F["gains"][l], blk=F["blk"],
                 vgain=F["vgain"][l], wsT=F["wsT"][l], triu=F["triu"], bsT=F["bsT"][l], zfm=F["zfm"], vab=F["vab"],
                 vsw=F["vsw"], misc=F["misc"], od=F["obr"][:, 768:1024])
        emit_proj(s, a, T)
        emit_mix_fused(s, F, l, T)
        emit_merge(s, F["xa"], F["mix_norm"][l], F["w_in"][l], F["w_branch"][l], F["w_out"][l], F["ident"], F["obr"], F["xb"], T)
        x_out = y if l == L - 1 else F["xc"]
        emit_ffn(s, F["xb"], F["ffn2_norm"][l], F["ffn2_w_in"][l], F["ffn2_w_out"][l], F["ident"], x_out, T)
    s.finish()
    s.close()
    return nc


def fused_params(P, L):
    import math
    f32 = np.float32
    A = lambda a: np.ascontiguousarray(np.asarray(a, dtype=f32))
    d = {k: A(P[k]) for k in ("ffn1_norm", "ffn1_w_in", "ffn1_w_out", "mix_norm", "w_in", "nsa_phi_w1", "nsa_phi_w2",
                              "nsa_phi_b2", "w_branch", "w_out", "ffn2_norm", "ffn2_w_in", "ffn2_w_out")}
    tile = lambda v, n: np.tile(A(v), (1, n))
    d["gains"] = np.ascontiguousarray(np.stack([tile(P["diff_q_gain"], 4), tile(P["diff_k_gain"], 4), tile(P["fox_q_gain"], 2),
                                                tile(P["fox_k_gain"], 2), tile(P["nsa_q_gain"], 2), tile(P["nsa_k_gain"], 2)], 2))
    d["vgain"] = np.ascontiguousarray(np.broadcast_to(A(P["gmlp_v_gain"])[:, None, :], (L, 128, 256)))
    d["wsT"] = np.ascontiguousarray(A(P["gmlp_w_s"]).transpose(0, 1, 3, 2))
    d["bsT"] = np.ascontiguousarray(A(P["gmlp_b_s"]).transpose(0, 2, 1))
    d["lamp"] = np.ascontiguousarray(np.broadcast_to(A(P["diff_lambda"])[:, None], (L, 128, 4, 32)))
    li = np.array([[0.8 - 0.6 * math.exp(-0.3 * l), 1.0 - (0.8 - 0.6 * math.exp(-0.3 * l))] for l in range(L)], f32)
    d["lami"] = np.ascontiguousarray(np.broadcast_to(li[:, None, :], (L, 128, 2)))
    d["fbias"] = np.ascontiguousarray(np.broadcast_to(A(P["fox_f_bias"])[:, :, None, None], (L, 4, 128, 1)))
    d["b1l"] = np.ascontiguousarray(A(P["nsa_phi_b1"]).reshape(L, 2, 2, 128).transpose(0, 3, 1, 2).reshape(L, 128, 4))
    pe = A(P["nsa_cmp_pe"])
    d["peT"] = np.ascontiguousarray(pe.transpose(0, 1, 3, 2).reshape(L, 128, 32))
    d["b2c"] = np.ascontiguousarray(A(P["nsa_phi_b2"])[:, 0, :, None])
    d["kgain"] = np.ascontiguousarray(A(P["nsa_k_gain"])[:, :, None])
    return d


B_, T_, L_ = 2, 8192, 2
_PROG = {}


def kernel(**inputs):
    x = np.ascontiguousarray(np.asarray(inputs["x"], dtype=np.float32))
    if "fused" not in _PROG:
        _PROG["fused"] = build_fused(T_, L_)
        _PROG["consts"] = fused_consts(T_)
    nc = _PROG["fused"]
    par = fused_params(inputs, L_)
    in_maps = []
    for b in range(B_):
        d = dict(par)
        d.update(_PROG["consts"])
        d["x"] = x[b]
        in_maps.append(d)
    res = run_bass_kernel_spmd(nc, in_maps, core_ids=list(range(B_)))
    return np.stack([np.asarray(res.results[b]["y"], dtype=np.float32) for b in range(B_)], 0)
```

```python
import numpy as np
import concourse.bass as bass
import concourse.mybir as mybir
from concourse.bass_utils import run_bass_kernel_spmd

F32 = mybir.dt.float32
BF16 = mybir.dt.bfloat16
AF = mybir.ActivationFunctionType
ALU = mybir.AluOpType
AX = mybir.AxisListType

ENGS = ("pe", "act", "dve", "pool", "sp")


class Buf:
    __slots__ = ("name", "t", "w", "r", "dsem", "dcnt", "uid")
    _n = 0

    def __init__(self, name, t):
        Buf._n += 1
        self.uid = Buf._n
        self.name = name
        self.t = t
        self.w = None
        self.r = []
        self.dsem = None
        self.dcnt = 0

    def __getitem__(self, idx):
        return self.t[idx]


class S:
    def __init__(self, nc, same_engine_sync=True):
        self.nc = nc
        self.e = {"pe": nc.tensor, "act": nc.scalar, "dve": nc.vector, "pool": nc.gpsimd, "sp": nc.sync}
        self.sem = {k: nc.alloc_semaphore("c_" + k) for k in ENGS}
        self.cnt = {k: 0 for k in ENGS}
        self.seen = {k: {} for k in ENGS}
        self.same = same_engine_sync
        self.nbuf = 0
        self.dma_sems = []
        self.ctx = []
        self.cbufs = []
        self.free_dsems = []

    def sb(self, name, shape, dt):
        self.nbuf += 1
        g = self.nc.sbuf_tensor("%s_%d" % (name, self.nbuf), list(shape), dt)
        t = g.__enter__()
        self.ctx.append(g)
        b = Buf(name, t)
        self.cbufs.append(b)
        return b

    def ps(self, name, shape, dt):
        self.nbuf += 1
        g = self.nc.psum_tensor("%s_%d" % (name, self.nbuf), list(shape), dt)
        t = g.__enter__()
        self.ctx.append(g)
        b = Buf(name, t)
        self.cbufs.append(b)
        return b

    def sub(self, name, ap):
        return Buf(name, ap)

    def mark(self):
        return len(self.ctx)

    def release(self, m):
        self.barrier()
        while len(self.ctx) > m:
            self.ctx.pop().__exit__(None, None, None)
            b = self.cbufs.pop()
            if b.dsem is not None:
                self.free_dsems.append((b.dsem, b.dcnt))
                self.dma_sems.remove(b)
                b.dsem = None

    def close(self):
        for g in reversed(self.ctx):
            g.__exit__(None, None, None)
        self.ctx = []
        self.cbufs = []

    def _need(self, E, deps):
        need = {}
        for d in deps:
            if d is None:
                continue
            if d[0] == "dma":
                b = d[1]
                key = ("dma", b.uid)
                need[key] = (b, b.dcnt)
            else:
                F, c = d
                if F == E and (not self.same or E == "pe" or c > self.cnt[E]):
                    continue
                if c > need.get(F, (None, 0))[1]:
                    need[F] = (None, c)
        for key, (b, c) in need.items():
            if self.seen[E].get(key, 0) >= c:
                continue
            self.seen[E][key] = c
            if b is not None:
                self.e[E].wait_ge(b.dsem, c)
            else:
                self.e[E].wait_ge(self.sem[key], c)

    def op(self, E, fn, reads=(), writes=(), inc=True):
        deps = []
        for b in reads:
            deps.append(b.w)
        for b in writes:
            deps.append(b.w)
            deps.extend(b.r)
        self._need(E, deps)
        ins = fn()
        c = self.cnt[E] + 1
        if inc:
            ins.then_inc(self.sem[E], 1)
            self.cnt[E] = c
        for b in writes:
            b.w = (E, c)
            b.r = []
        for b in reads:
            if b not in writes:
                b.r = [x for x in b.r if x[0] != E] + [(E, c)]
        return ins

    def dma(self, Q, out, in_, reads=(), writes=(), **kw):
        deps = []
        for b in reads:
            deps.append(b.w)
        for b in writes:
            deps.append(b.w)
            deps.extend(b.r)
        self._need(Q, deps)
        owner = (list(writes) + list(reads))[0]
        if owner.dsem is None:
            owner.dsem = self._dsem(owner)
            self.dma_sems.append(owner)
        ins = self.e[Q].dma_start(out=out, in_=in_, **kw)
        ins.then_inc(owner.dsem, 16)
        owner.dcnt += 16
        rec = ("dma", owner, owner.dcnt)
        for b in writes:
            b.w = rec
            b.r = []
        for b in reads:
            if b not in writes:
                b.r = b.r + [rec]
        return ins

    def cc(self, kind, groups, in_ap, out_ap, reads=(), writes=()):
        deps = []
        for b in reads:
            deps.append(b.w)
        for b in writes:
            deps.append(b.w)
            deps.extend(b.r)
        self._need("pool", deps)
        owner = list(writes)[0]
        if owner.dsem is None:
            owner.dsem = self._dsem(owner)
            self.dma_sems.append(owner)
        ins = self.nc.gpsimd.collective_compute(kind, op=ALU.bypass, replica_groups=groups, ins=[in_ap], outs=[out_ap])
        ins.then_inc(owner.dsem, 16)
        owner.dcnt += 16
        rec = ("dma", owner, owner.dcnt)
        for b in writes:
            b.w = rec
            b.r = []
        for b in reads:
            if b not in writes:
                b.r = b.r + [rec]
        return ins

    def _dsem(self, owner):
        if self.free_dsems:
            sem, cnt = self.free_dsems.pop()
            owner.dcnt = cnt
            return sem
        self.nsem = getattr(self, "nsem", 0) + 1
        return self.nc.alloc_semaphore("d_%d" % self.nsem)

    def barrier(self):
        for E in ENGS:
            for Fk in ENGS:
                if Fk == E:
                    continue
                c = self.cnt[Fk]
                if c and self.seen[E].get(Fk, 0) < c:
                    self.seen[E][Fk] = c
                    self.e[E].wait_ge(self.sem[Fk], c)
            for b in self.dma_sems:
                key = ("dma", b.uid)
                if b.dcnt and self.seen[E].get(key, 0) < b.dcnt:
                    self.seen[E][key] = b.dcnt
                    self.e[E].wait_ge(b.dsem, b.dcnt)

    def finish(self):
        self.barrier()


D = 1024
DFF = 2816
NFC = DFF // 128
NKC = D // 128
EPS = 1e-6


def dram_in(nc, name, shape, dt=F32):
    return nc.dram_tensor(name, list(shape), dt, kind="ExternalInput").ap()


def dram_out(nc, name, shape, dt=F32):
    return nc.dram_tensor(name, list(shape), dt, kind="ExternalOutput").ap()


class RR:
    def __init__(self, items):
        self.items = items
        self.i = 0

    def next(self):
        b = self.items[self.i % len(self.items)]
        self.i += 1
        return b


def emit_norm(s, epsc, xt, hb_rr, scr, stat_rr, ntile):
    nc = s.nc
    st = stat_rr.next()
    hbs = [hb_rr.next() for _ in range(ntile)]
    for j in range(ntile):
        s.op("act", lambda: nc.scalar.activation(out=hbs[j][:, :], in_=xt[j][:, :], func=AF.Square, scale=1.0 / 32.0,
                                                 accum_out=st[:, j:j + 1]),
             reads=[xt[j]], writes=[hbs[j], st])
    s.op("act", lambda: nc.scalar.activation(out=st[:, 4:4 + ntile], in_=st[:, 0:ntile], func=AF.Ln, bias=epsc[:, 0:1], scale=1.0),
         reads=[st, epsc], writes=[st])
    s.op("act", lambda: nc.scalar.activation(out=st[:, 8:8 + ntile], in_=st[:, 4:4 + ntile], func=AF.Exp, scale=-0.5),
         reads=[st], writes=[st])
    for j in range(ntile):
        s.op("act", lambda: nc.scalar.activation(out=hbs[j][:, :], in_=xt[j][:, :], func=AF.Copy, scale=st[:, 8 + j:9 + j]),
             reads=[xt[j], st], writes=[hbs[j]])
    return hbs


def emit_transpose_T(s, hbs, gcol, hT, ident_b, pT_rr, ntile, evac_engs=("dve", "act")):
    nc = s.nc
    k = 0
    for c in range(NKC):
        pT = pT_rr.next()
        for j in range(ntile):
            s.op("pe", lambda: nc.tensor.transpose(out=pT[:, j * 128:(j + 1) * 128], in_=hbs[j][:, c * 128:(c + 1) * 128],
                                                   identity=ident_b[:, :]),
                 reads=[hbs[j], ident_b], writes=[pT], inc=(j == ntile - 1))
        eng = evac_engs[k % len(evac_engs)]
        k += 1
        if eng == "dve":
            s.op("dve", lambda: nc.vector.tensor_scalar(out=hT[:, c, 0:ntile * 128], in0=pT[:, 0:ntile * 128],
                                                        scalar1=gcol[:, c:c + 1], scalar2=None, op0=ALU.mult),
                 reads=[pT, gcol], writes=[hT])
        else:
            s.op("act", lambda: nc.scalar.activation(out=hT[:, c, 0:ntile * 128], in_=pT[:, 0:ntile * 128],
                                                     func=AF.Copy, scale=gcol[:, c:c + 1]),
                 reads=[pT, gcol], writes=[hT])


def emit_rmsnorm_T(s, epsc, xt, gcol, hT, ident_b, pT_rr, hb_rr, scr, stat_rr, ntile, evac_engs=("dve", "act")):
    hbs = emit_norm(s, epsc, xt, hb_rr, scr, stat_rr, ntile)
    emit_transpose_T(s, hbs, gcol, hT, ident_b, pT_rr, ntile, evac_engs)


def build_ffn(NT, TB=512):
    nc = bass.Bass("TRN2", target_bir_lowering=False)
    x = dram_in(nc, "x", [NT, D])
    g = dram_in(nc, "g", [D])
    w_in = dram_in(nc, "w_in", [D, 2 * DFF])
    w_out = dram_in(nc, "w_out", [DFF, D])
    ident = dram_in(nc, "ident", [128, 128], BF16)
    y = dram_out(nc, "y", [NT, D])
    s = S(nc)
    emit_ffn(s, x, g, w_in, w_out, ident, y, NT, TB)
    s.finish()
    s.close()
    return nc


def emit_ffn(s, x, g, w_in, w_out, ident, y, NT, TB=512):
    nc = s.nc
    ntile = TB // 128
    m_ = s.mark()
    ident_b = s.sb("ident_b", [128, 128], BF16)
    s.dma("sp", ident_b[:, :], ident[:, :], writes=[ident_b])
    gcol = s.sb("gcol", [128, NKC], F32)
    epsc = s.sb("epsc", [128, 1], F32)
    s.op("dve", lambda: nc.vector.memset(epsc[:, :], EPS), writes=[epsc])
    s.dma("sp", gcol[:, :], g.rearrange("(c p) -> p c", p=128), writes=[gcol], allow_slow_non_contiguous=True)
    win_b = [s.sb("win_b%d" % c, [128, 2 * DFF], BF16) for c in range(NKC)]
    wout_b = [s.sb("wout_b%d" % f, [128, D], BF16) for f in range(NFC)]
    for c in range(NKC):
        for hf in range(2):
            s.dma("pool", win_b[c][:, hf * DFF:(hf + 1) * DFF], w_in[c * 128:(c + 1) * 128, hf * DFF:(hf + 1) * DFF],
                  writes=[win_b[c]])
    for f in range(NFC):
        s.dma("pool", wout_b[f][:, :], w_out[f * 128:(f + 1) * 128, :], writes=[wout_b[f]])
    xn = [s.sb("xn%d" % j, [128, D], F32) for j in range(ntile)]
    xr_rr = RR([s.sb("xr%d" % j, [128, D], F32) for j in range(1)])
    hb_rr = RR([s.sb("hb%d" % j, [128, D], BF16) for j in range(2 * ntile)])
    stat_rr = RR([s.sb("st%d" % j, [128, 16], F32) for j in range(4)])
    hT = s.sb("hT", [128, NKC, TB], BF16)
    pT_rr = RR([s.ps("pT%d" % j, [128, TB], BF16) for j in range(2)])
    pa_rr = RR([s.ps("pa%d" % j, [128, TB], F32) for j in range(2)])
    pb_rr = RR([s.ps("pb%d" % j, [128, TB], F32) for j in range(2)])
    po_rr = RR([s.ps("po%d" % j, [128, 512], F32) for j in range(2)])
    sa_rr = RR([s.sb("sa%d" % j, [128, TB], F32) for j in range(2)])
    act = [s.sb("actT%d" % f, [128, TB], BF16) for f in range(NFC)]
    nblk = NT // TB

    def prep_a_rot(tb):
        for j in range(ntile):
            r0 = tb * TB + j * 128
            s.dma("sp", xn[j][:, :], x[r0:r0 + 128, :], writes=[xn[j]])
        return emit_norm(s, epsc, xn, hb_rr, None, stat_rr, ntile)

    hbs_next = prep_a_rot(0)
    emit_transpose_T(s, hbs_next, gcol, hT, ident_b, pT_rr, ntile)
    for tb in range(nblk):
        if tb + 1 < nblk:
            hbs_next = prep_a_rot(tb + 1)
        for f in range(NFC):
            pa = pa_rr.next()
            pb = pb_rr.next()
            for c in range(NKC):
                s.op("pe", lambda: nc.tensor.matmul(pa[:, :], lhsT=win_b[c][:, f * 128:(f + 1) * 128], rhs=hT[:, c, :],
                                                    start=(c == 0), stop=(c == NKC - 1)),
                     reads=[win_b[c], hT], writes=[pa], inc=(c == NKC - 1))
            for c in range(NKC):
                s.op("pe", lambda: nc.tensor.matmul(pb[:, :], lhsT=win_b[c][:, DFF + f * 128:DFF + (f + 1) * 128],
                                                    rhs=hT[:, c, :], start=(c == 0), stop=(c == NKC - 1)),
                     reads=[win_b[c], hT], writes=[pb], inc=(c == NKC - 1))
            sa = sa_rr.next()
            s.op("act", lambda: nc.scalar.activation(out=sa[:, :], in_=pa[:, :], func=AF.Silu), reads=[pa], writes=[sa])
            s.op("dve", lambda: nc.vector.tensor_tensor(out=act[f][:, :], in0=sa[:, :], in1=pb[:, :], op=ALU.mult),
                 reads=[sa, pb], writes=[act[f]])
        if tb + 1 < nblk:
            emit_transpose_T(s, hbs_next, gcol, hT, ident_b, pT_rr, ntile)
        for j in range(ntile):
            r0 = tb * TB + j * 128
            xr = xr_rr.next()
            s.dma("sp", xr[:, :], x[r0:r0 + 128, :], writes=[xr])
            for hf in range(2):
                po = po_rr.next()
                for f in range(NFC):
                    s.op("pe", lambda: nc.tensor.matmul(po[:, :], lhsT=act[f][:, j * 128:(j + 1) * 128],
                                                        rhs=wout_b[f][:, hf * 512:(hf + 1) * 512],
                                                        start=(f == 0), stop=(f == NFC - 1)),
                         reads=[act[f], wout_b[f]], writes=[po], inc=(f == NFC - 1))
                s.op("dve", lambda: nc.vector.scalar_tensor_tensor(out=xr[:, hf * 512:(hf + 1) * 512], in0=po[:, :],
                                                                   scalar=0.5, in1=xr[:, hf * 512:(hf + 1) * 512],
                                                                   op0=ALU.mult, op1=ALU.add),
                     reads=[po, xr], writes=[xr])
            s.dma("sp", y[r0:r0 + 128, :], xr[:, :], reads=[xr])
    s.release(m_)


FM_SRC = [[(0, 128)], [(128, 128)], [(256, 128)], [(384, 128)],
          [(768, 128)], [(896, 128)], [(1024, 128)], [(1152, 128)],
          [(1540, 128)], [(1668, 128)], [(1924, 64), (2052, 64)], [(1796, 128)]]
FM_GCOL = [0, 0, 1, 1, 2, 2, 3, 3, 4, 4, 5, None]
FM_BLK = [0, 0, 0, 0, 1, 1, 1, 1, 1, 1, 1, None]
TM_SRC = [[(512, 256), (1280, 256)],
          [(1536, 4), (2180, 12), (1988, 64), (2116, 64)],
          [(2192, 512)]]
NFM = 12
GELU_C = 1.5957691216057308


def build_proj(NT, TB=512):
    nc = bass.Bass("TRN2", target_bir_lowering=False)
    a = dict(
        x=dram_in(nc, "x", [NT, D]), g=dram_in(nc, "g", [D]), w_in=dram_in(nc, "w_in", [D, 6800]),
        ident=dram_in(nc, "ident", [128, 128], BF16), gains=dram_in(nc, "gains", [128, 6]),
        blk=dram_in(nc, "blk", [2, 128, 128], BF16), vgain=dram_in(nc, "vgain", [128, 256]),
        wsT=dram_in(nc, "wsT", [4, 128, 128]), triu=dram_in(nc, "triu", [128, 128]), bsT=dram_in(nc, "bsT", [128, 4]),
        zfm=dram_out(nc, "zfm", [NFM, 128, NT], BF16), vab=dram_out(nc, "vab", [NT, 512], BF16),
        vsw=dram_out(nc, "vsw", [NT, 128], BF16), misc=dram_out(nc, "misc", [NT, 16]), od=dram_out(nc, "od", [NT, 256], BF16))
    s = S(nc)
    emit_proj(s, a, NT, TB)
    s.finish()
    s.close()
    return nc


def emit_proj(s, a, NT, TB=512):
    nc = s.nc
    x, g, w_in, ident, gains, blk, vgain, wsT, triu, bsT = (a[k] for k in
                                                            ("x", "g", "w_in", "ident", "gains", "blk", "vgain", "wsT", "triu", "bsT"))
    zfm, vab, vsw, misc, od = (a[k] for k in ("zfm", "vab", "vsw", "misc", "od"))
    m_ = s.mark()
    ntile = TB // 128
    ident_b = s.sb("ident_b", [128, 128], BF16)
    s.dma("sp", ident_b[:, :], ident[:, :], writes=[ident_b])
    gcol = s.sb("gcol", [128, NKC], F32)
    s.dma("sp", gcol[:, :], g.rearrange("(c p) -> p c", p=128), writes=[gcol], allow_slow_non_contiguous=True)
    epsc = s.sb("epsc", [128, 1], F32)
    s.op("dve", lambda: nc.vector.memset(epsc[:, :], EPS), writes=[epsc])
    gn = s.sb("gn", [128, 6], F32)
    s.dma("sp", gn[:, :], gains[:, :], writes=[gn])
    for col, sc in ((0, 32.0 ** -0.5), (2, 0.125), (4, 0.125)):
        s.op("dve", lambda: nc.vector.tensor_scalar(out=gn[:, col:col + 1], in0=gn[:, col:col + 1], scalar1=sc,
                                                    scalar2=None, op0=ALU.mult), reads=[gn], writes=[gn])
    blk_b = [s.sb("blk%d" % i, [128, 128], BF16) for i in range(2)]
    for i in range(2):
        s.dma("sp", blk_b[i][:, :], blk[i], writes=[blk_b[i]])
    vg = s.sb("vg", [128, 256], F32)
    s.dma("sp", vg[:, :], vgain[:, :], writes=[vg])
    bcol = s.sb("bcol", [128, 4], F32)
    s.dma("sp", bcol[:, :], bsT[:, :], writes=[bcol])
    tri = s.sb("tri", [128, 128], F32)
    s.dma("sp", tri[:, :], triu[:, :], writes=[tri])
    wm = []
    wtmp = s.sb("wtmp", [128, 128], F32)
    for gi in range(4):
        w = s.sb("wm%d" % gi, [128, 128], BF16)
        s.dma("sp", wtmp[:, :], wsT[gi], writes=[wtmp])
        s.op("dve", lambda: nc.vector.tensor_tensor(out=w[:, :], in0=wtmp[:, :], in1=tri[:, :], op=ALU.mult),
             reads=[wtmp, tri], writes=[w])
        wm.append(w)
    wfm = [s.sb("wfm%d" % c, [128, NFM * 128], BF16) for c in range(NKC)]
    wtm = [s.sb("wtm%d" % c, [128, 1168], BF16) for c in range(NKC)]
    FM_RUNS = [(0, 0, 512), (512, 768, 512), (1024, 1540, 256), (1280, 1924, 64), (1344, 2052, 64), (1408, 1796, 128)]
    TM_RUNS = [(0, 512, 256), (256, 1280, 256), (512, 1536, 4), (516, 2180, 12), (528, 1988, 64), (592, 2116, 64), (656, 2192, 512)]
    for c in range(NKC):
        for (o, c0, n) in FM_RUNS:
            s.dma("pool", wfm[c][:, o:o + n], w_in[c * 128:(c + 1) * 128, c0:c0 + n], writes=[wfm[c]])
        for (o, c0, n) in TM_RUNS:
            s.dma("pool", wtm[c][:, o:o + n], w_in[c * 128:(c + 1) * 128, c0:c0 + n], writes=[wtm[c]])
    xts = [[s.sb("xt%d_%d" % (k, j), [128, D], F32) for j in range(ntile)] for k in range(2)]
    hb_rr = RR([s.sb("hb%d" % j, [128, D], BF16) for j in range(2 * ntile)])
    scr = s.sb("scr", [128, D], BF16)
    stat_rr = RR([s.sb("st%d" % j, [128, 16], F32) for j in range(4)])
    hTs = [s.sb("hT%d" % k, [128, NKC, TB], BF16) for k in range(2)]
    pT_rr = RR([s.ps("pT%d" % j, [128, TB], BF16) for j in range(2)])
    pz_rr = RR([s.ps("pz%d" % j, [128, 512], F32) for j in range(2)])
    ptm_rr = RR([s.ps("ptm%d" % j, [128, 512], F32) for j in range(2)])
    pq_rr = RR([s.ps("pq%d" % j, [128, 512], F32) for j in range(2)])
    sq_rr = RR([s.sb("sq%d" % j, [128, TB], BF16) for j in range(3)])
    rs_rr = RR([s.sb("rs%d" % j, [128, TB], F32) for j in range(2)])
    zo_rr = RR([s.sb("zo%d" % j, [128, TB], BF16) for j in range(3)])
    vab_rr = RR([s.sb("vabt%d" % j, [128, 512], BF16) for j in range(2)])
    vsw_rr = RR([s.sb("vswt%d" % j, [128, 128], BF16) for j in range(2)])
    msc_rr = RR([s.sb("msct%d" % j, [128, 16], F32) for j in range(2)])
    f_rr = RR([s.sb("gf%d" % j, [128, 512], F32) for j in range(4)])
    ge_rr = RR([s.sb("ge%d" % j, [128, 512], F32) for j in range(3)])
    zs_rr = RR([s.sb("zs%d" % j, [128, 512], F32) for j in range(2)])
    vn_rr = RR([s.sb("vn%d" % j, [128, 256], BF16) for j in range(3)])
    od_rr = RR([s.sb("odt%d" % j, [128, 256], BF16) for j in range(2)])
    nblk = NT // TB

    def prep(tb):
        xt = xts[tb % 2]
        for j in range(ntile):
            s.dma("sp", xt[j][:, :], x[tb * TB + j * 128:tb * TB + (j + 1) * 128, :], writes=[xt[j]])
        emit_rmsnorm_T(s, epsc, xt, gcol, hTs[tb % 2], ident_b, pT_rr, hb_rr, scr, stat_rr, ntile)

    prep(0)
    pipe = Pipe(1)
    for tb in range(nblk):
        t0 = tb * TB
        hT = hTs[tb % 2]
        for i in range(NFM):
            pz = pz_rr.next()
            for c in range(NKC):
                s.op("pe", lambda: nc.tensor.matmul(pz[:, 0:TB], lhsT=wfm[c][:, i * 128:(i + 1) * 128], rhs=hT[:, c, :],
                                                    start=(c == 0), stop=(c == NKC - 1)),
                     reads=[wfm[c], hT], writes=[pz], inc=(c == NKC - 1))
            zo = zo_rr.next()
            if FM_GCOL[i] is None:
                s.op("act", lambda: nc.scalar.copy(out=zo[:, :], in_=pz[:, 0:TB]), reads=[pz], writes=[zo])
                s.dma("sp", zfm[i, :, t0:t0 + TB], zo[:, :], reads=[zo])
            else:
                sq = sq_rr.next()
                s.op("act", lambda: nc.scalar.activation(out=sq[:, :], in_=pz[:, 0:TB], func=AF.Square),
                     reads=[pz], writes=[sq])

                def back(i=i, pz=pz, sq=sq, zo=zo, t0=t0):
                    gs = 32.0 if FM_BLK[i] == 0 else 64.0
                    pq = pq_rr.next()
                    s.op("pe", lambda: nc.tensor.matmul(pq[:, 0:TB], lhsT=blk_b[FM_BLK[i]][:, :], rhs=sq[:, :],
                                                        start=True, stop=True), reads=[blk_b[FM_BLK[i]], sq], writes=[pq])
                    rs = rs_rr.next()
                    s.op("act", lambda: nc.scalar.activation(out=rs[:, :], in_=pq[:, 0:TB], func=AF.Ln, bias=epsc[:, 0:1],
                                                             scale=1.0 / gs), reads=[pq, epsc], writes=[rs])
                    s.op("act", lambda: nc.scalar.activation(out=rs[:, :], in_=rs[:, :], func=AF.Exp, scale=-0.5),
                         reads=[rs], writes=[rs])
                    gc = FM_GCOL[i]
                    s.op("dve", lambda: nc.vector.scalar_tensor_tensor(out=zo[:, :], in0=pz[:, 0:TB], scalar=gn[:, gc:gc + 1],
                                                                       in1=rs[:, :], op0=ALU.mult, op1=ALU.mult),
                         reads=[pz, gn, rs], writes=[zo])
                    s.dma("sp", zfm[i, :, t0:t0 + TB], zo[:, :], reads=[zo])
                pipe.push(back)
        if tb + 1 < nblk:
            prep(tb + 1)
        for j in range(ntile):
            r0 = t0 + j * 128
            pz = ptm_rr.next()
            for c in range(NKC):
                s.op("pe", lambda: nc.tensor.matmul(pz[:, :], lhsT=hT[:, c, j * 128:(j + 1) * 128], rhs=wtm[c][:, 0:512],
                                                    start=(c == 0), stop=(c == NKC - 1)),
                     reads=[wtm[c], hT], writes=[pz], inc=(c == NKC - 1))
            vt = vab_rr.next()
            s.op("act", lambda: nc.scalar.copy(out=vt[:, :], in_=pz[:, :]), reads=[pz], writes=[vt])
            s.dma("sp", vab[r0:r0 + 128, :], vt[:, :], reads=[vt])
            pz = ptm_rr.next()
            for c in range(NKC):
                s.op("pe", lambda: nc.tensor.matmul(pz[:, 0:144], lhsT=hT[:, c, j * 128:(j + 1) * 128], rhs=wtm[c][:, 512:656],
                                                    start=(c == 0), stop=(c == NKC - 1)),
                     reads=[wtm[c], hT], writes=[pz], inc=(c == NKC - 1))
            mt = msc_rr.next()
            vs_ = vsw_rr.next()
            s.op("dve", lambda: nc.vector.tensor_copy(out=mt[:, :], in_=pz[:, 0:16]), reads=[pz], writes=[mt])
            s.op("dve", lambda: nc.vector.tensor_copy(out=vs_[:, :], in_=pz[:, 16:144]), reads=[pz], writes=[vs_])
            s.dma("sp", misc[r0:r0 + 128, :], mt[:, :], reads=[mt])
            s.dma("sp", vsw[r0:r0 + 128, :], vs_[:, :], reads=[vs_])
            pz = ptm_rr.next()
            for c in range(NKC):
                s.op("pe", lambda: nc.tensor.matmul(pz[:, :], lhsT=hT[:, c, j * 128:(j + 1) * 128], rhs=wtm[c][:, 656:1168],
                                                    start=(c == 0), stop=(c == NKC - 1)),
                     reads=[wtm[c], hT], writes=[pz], inc=(c == NKC - 1))
            zs = zs_rr.next()
            s.op("act", lambda: nc.scalar.copy(out=zs[:, :], in_=pz[:, :]), reads=[pz], writes=[zs])
            z2 = f_rr.next()
            s.op("act", lambda: nc.scalar.activation(out=z2[:, :], in_=zs[:, :], func=AF.Square), reads=[zs], writes=[z2])
            s.op("dve", lambda: nc.vector.tensor_scalar(out=z2[:, :], in0=z2[:, :], scalar1=0.044715, scalar2=1.0,
                                                        op0=ALU.mult, op1=ALU.add), reads=[z2], writes=[z2])
            s.op("dve", lambda: nc.vector.tensor_tensor(out=z2[:, :], in0=z2[:, :], in1=zs[:, :], op=ALU.mult),
                 reads=[z2, zs], writes=[z2])
            s.op("act", lambda: nc.scalar.activation(out=z2[:, :], in_=z2[:, :], func=AF.Exp, scale=-GELU_C),
                 reads=[z2], writes=[z2])
            s.op("act", lambda: nc.scalar.activation(out=z2[:, :], in_=z2[:, :], func=AF.Ln, bias=1.0, scale=1.0),
                 reads=[z2], writes=[z2])
            s.op("act", lambda: nc.scalar.activation(out=z2[:, :], in_=z2[:, :], func=AF.Exp, scale=-1.0),
                 reads=[z2], writes=[z2])
            ge = ge_rr.next()
            s.op("dve", lambda: nc.vector.tensor_tensor(out=ge[:, :], in0=z2[:, :], in1=zs[:, :], op=ALU.mult),
                 reads=[z2, zs], writes=[ge])
            sqv = f_rr.next()
            st = stat_rr.next()
            s.op("act", lambda: nc.scalar.activation(out=sqv[:, 0:256], in_=ge[:, 256:512], func=AF.Square),
                 reads=[ge], writes=[sqv])
            s.op("dve", lambda: nc.vector.tensor_reduce(out=st[:, 0:4], in_=sqv[:, 0:256].rearrange("p (g d) -> p g d", g=4),
                                                        axis=AX.X, op=ALU.add), reads=[sqv], writes=[st])
            s.op("act", lambda: nc.scalar.activation(out=st[:, 0:4], in_=st[:, 0:4], func=AF.Ln, bias=epsc[:, 0:1],
                                                     scale=1.0 / 64.0), reads=[st, epsc], writes=[st])
            s.op("act", lambda: nc.scalar.activation(out=st[:, 0:4], in_=st[:, 0:4], func=AF.Exp, scale=-0.5),
                 reads=[st], writes=[st])
            vn = vn_rr.next()
            for gi in range(4):
                s.op("dve", lambda: nc.vector.scalar_tensor_tensor(
                    out=vn[:, gi * 64:(gi + 1) * 64], in0=ge[:, 256 + gi * 64:256 + (gi + 1) * 64], scalar=st[:, gi:gi + 1],
                    in1=vg[:, gi * 64:(gi + 1) * 64], op0=ALU.mult, op1=ALU.mult), reads=[ge, st, vg], writes=[vn])

            def back2(vn=vn, ge=ge, r0=r0):
                pq = pq_rr.next()
                for gi in range(4):
                    s.op("pe", lambda: nc.tensor.matmul(pq[:, gi * 64:(gi + 1) * 64], lhsT=wm[gi][:, :],
                                                        rhs=vn[:, gi * 64:(gi + 1) * 64], start=True, stop=True),
                         reads=[wm[gi], vn], writes=[pq], inc=(gi == 3))
                ot = od_rr.next()
                for gi in range(4):
                    s.op("dve", lambda: nc.vector.scalar_tensor_tensor(
                        out=ot[:, gi * 64:(gi + 1) * 64], in0=pq[:, gi * 64:(gi + 1) * 64], scalar=bcol[:, gi:gi + 1],
                        in1=ge[:, gi * 64:(gi + 1) * 64], op0=ALU.add, op1=ALU.mult), reads=[pq, bcol, ge], writes=[ot])
                s.dma("sp", od[r0:r0 + 128, :], ot[:, :], reads=[ot])
            pipe.push(back2)
    pipe.flush()
    s.release(m_)


def _bf(a):
    import ml_dtypes
    return np.ascontiguousarray(a).astype(ml_dtypes.bfloat16)


def proj_consts():
    blk = np.zeros((2, 128, 128), np.float32)
    for i in range(128):
        for j in range(128):
            if i // 32 == j // 32:
                blk[0, i, j] = 1
            if i // 64 == j // 64:
                blk[1, i, j] = 1
    triu = np.triu(np.ones((128, 128), np.float32))
    return dict(ident=_bf(np.eye(128, dtype=np.float32)), blk=_bf(blk), triu=triu)


def proj_params(g, w_in, dq, dk, fq, fk, nq, nk, vgain, w_s, b_s):
    gains = np.stack([np.tile(dq, 4), np.tile(dk, 4), np.tile(fq, 2), np.tile(fk, 2), np.tile(nq, 2), np.tile(nk, 2)], 1)
    return dict(g=np.ascontiguousarray(g), w_in=np.ascontiguousarray(w_in), gains=np.ascontiguousarray(gains, dtype=np.float32),
                vgain=np.ascontiguousarray(np.broadcast_to(vgain[None, :], (128, 256))),
                wsT=np.ascontiguousarray(w_s.transpose(0, 2, 1)), bsT=np.ascontiguousarray(b_s.T))


NEG = -30000.0


def load_vt(s, vt, io, key, T):
    nc = s.nc
    NT = T // 128
    s.op("pool", lambda: nc.gpsimd.memset(vt[:, :, 64:128], 0.0), writes=[vt])
    s.op("pool", lambda: nc.gpsimd.memset(vt[:, :, 64:65], 1.0), writes=[vt])
    if key + "_src" in io:
        src = io[key + "_src"]
        step = 8
        for j0 in range(0, NT, step):
            j1 = min(NT, j0 + step)
            s.dma("sp", vt[:, j0:j1, 0:64], src[j0 * 128:j1 * 128, :].rearrange("(j p) d -> p j d", p=128), writes=[vt])
    else:
        s.dma("sp", vt[:, :, 0:64], io[key][:, :, 0:64], writes=[vt])


class Pipe:
    def __init__(self, lag):
        self.q = []
        self.lag = lag

    def push(self, fn):
        self.q.append(fn)
        while len(self.q) > self.lag:
            self.q.pop(0)()

    def flush(self):
        while self.q:
            self.q.pop(0)()


def emit_attn_phase(s, cm, T, nsub, qT, kT, vt, kparts, out_dram, finalize, bias_fn=None, name="a", lag=2):
    nc = s.nc
    NQB = T // 512
    pipe = Pipe(lag)
    fin_pending = None
    for qb in range(NQB):
        q0 = qb * 512
        pos = [cm["po_rr"].next() for _ in range(nsub)]
        nt = 4 * qb + 4
        njob = 0
        for t in range(nt):
            di = t - 4 * qb
            c0 = 128 * di if di > 0 else 0
            for i in range(nsub):
                kz = kT[i]
                ps = cm["ps_rr"].next()
                s.op("pe", lambda: nc.tensor.matmul(ps[:, c0:512], lhsT=kz[:, t * 128:(t + 1) * 128],
                                                    rhs=qT[:, q0 + c0:q0 + 512], start=True, stop=(di < 0)),
                     reads=[kz, qT], writes=[ps], inc=(di < 0))
                if di >= 0:
                    s.op("pe", lambda: nc.tensor.matmul(ps[:, c0:c0 + 128], lhsT=cm["ident_b"][:, :], rhs=cm["tri_b"][:, :],
                                                        start=False, stop=True),
                         reads=[cm["ident_b"], cm["tri_b"]], writes=[ps])
                pt = cm["pt_rr"].next()
                if bias_fn is None:
                    s.op("act", lambda: nc.scalar.activation(out=pt[:, c0:512], in_=ps[:, c0:512], func=AF.Exp),
                         reads=[ps], writes=[pt])
                else:
                    bb, bap = bias_fn(qb, t)
                    s.op("act", lambda: nc.scalar.activation(out=pt[:, c0:512], in_=ps[:, c0:512], func=AF.Exp, bias=bap),
                         reads=[ps, bb], writes=[pt])

                def pv(po=pos[i], t=t, c0=c0, pt=pt, nt=nt):
                    s.op("pe", lambda: nc.tensor.matmul(po[:, c0:512], lhsT=vt[:, t, :], rhs=pt[:, c0:512],
                                                        start=(t == 0), stop=(t == nt - 1)),
                         reads=[vt, pt], writes=[po])
                pipe.push(pv)
                njob += 1
                if fin_pending is not None and njob == lag:
                    fin_pending()
                    fin_pending = None
        pipe.flush()
        if fin_pending is not None:
            fin_pending()
        fin_pending = (lambda qb=qb, pos=pos: finalize(qb, pos))
    if fin_pending is not None:
        fin_pending()


def emit_o_to_tokmajor(s, cm, po, pf, col0):
    nc = s.nc
    oc = cm["oc_rr"].next()
    s.op("act", lambda: nc.scalar.copy(out=oc[0:65, :], in_=po[0:65, :]), reads=[po], writes=[oc])
    for j in range(4):
        s.op("pe", lambda: nc.tensor.transpose(out=pf[:, j, col0:col0 + 65], in_=oc[0:65, j * 128:(j + 1) * 128],
                                               identity=cm["ident_f"][0:65, 0:65]),
             reads=[oc, cm["ident_f"]], writes=[pf], inc=(j == 3))


def build_mix_ab(T):
    nc = bass.Bass("TRN2", target_bir_lowering=False)
    io = mix_decl(nc, T, with_c=False)
    s = S(nc)
    cm = mix_common(s, io)
    emit_mix_a(s, cm, io, T)
    emit_mix_b(s, cm, io, T)
    s.finish()
    s.close()
    return nc


def mix_decl(nc, T, with_c=True):
    NT = T // 128
    io = dict(
        identb=dram_in(nc, "identb", [128, 128], BF16), identf=dram_in(nc, "identf", [128, 128]),
        trib=dram_in(nc, "trib", [128, 128], BF16),
        qa=dram_in(nc, "qa", [64, T], BF16), ka=dram_in(nc, "ka", [64, T], BF16), va=dram_in(nc, "va", [128, NT, 65], BF16),
        lamp=dram_in(nc, "lamp", [128, 4, 32]), lami=dram_in(nc, "lami", [128, 2]),
        qb=dram_in(nc, "qb", [64, T], BF16), kb=dram_in(nc, "kb", [64, T], BF16), vb=dram_in(nc, "vb", [128, NT, 65], BF16),
        flog=dram_in(nc, "flog", [128, NT]), fbias=dram_in(nc, "fbias", [128, 1]),
        triuf=dram_in(nc, "triuf", [128, 128]), onesf=dram_in(nc, "onesf", [128, 128]),
        oa=dram_out(nc, "oa", [T, 64], BF16), ob=dram_out(nc, "ob", [T, 64], BF16),
    )
    return io


def mix_common(s, io, n_ps=3, with_pf2=True):
    nc = s.nc
    cm = {}
    for nm, key, dt in (("ident_b", "identb", BF16), ("ident_f", "identf", F32), ("tri_b", "trib", BF16)):
        b = s.sb(nm, [128, 128], dt)
        s.dma("sp", b[:, :], io[key][:, :], writes=[b])
        cm[nm] = b
    cm["epsc"] = s.sb("epsc", [128, 1], F32)
    s.op("dve", lambda: nc.vector.memset(cm["epsc"][:, :], EPS), writes=[cm["epsc"]])
    cm["ps_rr"] = RR([s.ps("ps%d" % j, [128, 512], F32) for j in range(n_ps)])
    cm["po_rr"] = RR([s.ps("po%d" % j, [128, 512], F32) for j in range(3)])
    cm["pf"] = s.ps("pf", [128, 4, 128], F32)
    if with_pf2:
        cm["pf2"] = s.ps("pf2", [128, 4, 128], F32)
    cm["lag"] = n_ps - 1
    cm["o1s_rr"] = RR([s.sb("o1s%d" % j, [128, 4, 65], F32) for j in range(2)])
    cm["pt_rr"] = RR([s.sb("pt%d" % j, [128, 512], BF16) for j in range(n_ps + 2)])
    cm["oc_rr"] = RR([s.sb("oc%d" % j, [128, 512], F32) for j in range(2)])
    cm["st_rr"] = RR([s.sb("mst%d" % j, [128, 8], F32) for j in range(8)])
    cm["ot_rr"] = RR([s.sb("ot%d" % j, [128, 4, 64], BF16) for j in range(2)])
    cm["tmp_rr"] = RR([s.sb("tmp%d" % j, [128, 64], F32) for j in range(4)])
    return cm


def emit_mix_a(s, cm, io, T):
    nc = s.nc
    NT = T // 128
    m = s.mark()
    qT = s.sb("a_q", [128, T], BF16)
    k1 = s.sb("a_k1", [128, T], BF16)
    k2 = s.sb("a_k2", [128, T], BF16)
    vt = s.sb("a_v", [128, NT, 128], BF16)
    s.op("pool", lambda: nc.gpsimd.memset(qT[64:128, :], 0.0), writes=[qT])
    s.op("dve", lambda: nc.vector.memset(k1[:, :], 0.0), writes=[k1])
    s.op("pool", lambda: nc.gpsimd.memset(k2[:, :], 0.0), writes=[k2])
    s.dma("sp", qT[0:64, :], io["qa"][:, :], writes=[qT])
    s.dma("sp", k1[0:32, :], io["ka"][0:32, :], writes=[k1])
    s.dma("sp", k2[32:64, :], io["ka"][32:64, :], writes=[k2])
    load_vt(s, vt, io, "va", T)
    lp = s.sb("lp", [128, 4, 32], F32)
    li = s.sb("li", [128, 2], F32)
    lw = s.sb("lw", [128, 2, 32], F32)
    lam = s.sb("lam", [128, 4], F32)
    s.dma("sp", lp[:, :, :], io["lamp"][:, :, :], writes=[lp])
    s.dma("sp", li[:, :], io["lami"][:, :], writes=[li])
    s.op("dve", lambda: nc.vector.tensor_tensor(out=lw[:, 0, :], in0=lp[:, 0, :], in1=lp[:, 1, :], op=ALU.mult),
         reads=[lp], writes=[lw])
    s.op("dve", lambda: nc.vector.tensor_tensor(out=lw[:, 1, :], in0=lp[:, 2, :], in1=lp[:, 3, :], op=ALU.mult),
         reads=[lp], writes=[lw])
    s.op("dve", lambda: nc.vector.tensor_reduce(out=lam[:, 0:2], in_=lw[:, :, :], axis=AX.X, op=ALU.add),
         reads=[lw], writes=[lam])
    s.op("act", lambda: nc.scalar.activation(out=lam[:, 0:2], in_=lam[:, 0:2], func=AF.Exp), reads=[lam], writes=[lam])
    s.op("dve", lambda: nc.vector.tensor_tensor(out=lam[:, 2:3], in0=lam[:, 1:2], in1=lam[:, 0:1], op=ALU.subtract),
         reads=[lam], writes=[lam])
    s.op("dve", lambda: nc.vector.tensor_tensor(out=lam[:, 3:4], in0=lam[:, 2:3], in1=li[:, 0:1], op=ALU.subtract),
         reads=[lam, li], writes=[lam])

    def fin(qb, pos):
        if "pf2" in cm:
            pf = cm["pf"]
            pf2 = cm["pf2"]
            emit_o_to_tokmajor(s, cm, pos[0], pf, 0)
            emit_o_to_tokmajor(s, cm, pos[1], pf2, 0)
        else:
            pf2 = cm["pf"]
            emit_o_to_tokmajor(s, cm, pos[0], pf2, 0)
            pf = cm["o1s_rr"].next()
            s.op("dve", lambda: nc.vector.tensor_copy(out=pf[:, :, :], in_=pf2[:, :, 0:65]), reads=[pf2], writes=[pf])
            emit_o_to_tokmajor(s, cm, pos[1], pf2, 0)
        ot = cm["ot_rr"].next()
        for j in range(4):
            st = cm["st_rr"].next()
            s.op("dve", lambda: nc.vector.tensor_scalar(out=st[:, 0:1], in0=pf[:, j, 64:65], scalar1=1e-30, scalar2=None,
                                                        op0=ALU.max), reads=[pf], writes=[st])
            s.op("dve", lambda: nc.vector.tensor_scalar(out=st[:, 1:2], in0=pf2[:, j, 64:65], scalar1=1e-30, scalar2=None,
                                                        op0=ALU.max), reads=[pf2], writes=[st])
            s.op("dve", lambda: nc.vector.reciprocal(out=st[:, 0:2], in_=st[:, 0:2]), reads=[st], writes=[st])
            t2 = cm["tmp_rr"].next()
            o = cm["tmp_rr"].next()
            s.op("dve", lambda: nc.vector.tensor_scalar(out=t2[:, :], in0=pf2[:, j, 0:64], scalar1=st[:, 1:2],
                                                        scalar2=lam[:, 3:4], op0=ALU.mult, op1=ALU.mult),
                 reads=[pf2, st, lam], writes=[t2])
            s.op("dve", lambda: nc.vector.scalar_tensor_tensor(out=o[:, :], in0=pf[:, j, 0:64], scalar=st[:, 0:1], in1=t2[:, :],
                                                               op0=ALU.mult, op1=ALU.add), reads=[pf, st, t2], writes=[o])
            s.op("act", lambda: nc.scalar.activation(out=t2[:, :], in_=o[:, :], func=AF.Square, accum_out=st[:, 2:3]),
                 reads=[o], writes=[t2, st])
            s.op("act", lambda: nc.scalar.activation(out=st[:, 3:4], in_=st[:, 2:3], func=AF.Ln, bias=cm["epsc"][:, 0:1],
                                                     scale=1.0 / 64.0), reads=[st, cm["epsc"]], writes=[st])
            s.op("act", lambda: nc.scalar.activation(out=st[:, 4:5], in_=st[:, 3:4], func=AF.Exp, scale=-0.5),
                 reads=[st], writes=[st])
            s.op("dve", lambda: nc.vector.tensor_scalar(out=ot[:, j, :], in0=o[:, :], scalar1=st[:, 4:5], scalar2=li[:, 1:2],
                                                        op0=ALU.mult, op1=ALU.mult), reads=[o, st, li], writes=[ot])
        s.dma("sp", io["oa"][qb * 512:(qb + 1) * 512, :].rearrange("(j p) d -> p j d", p=128), ot[:, :, :], reads=[ot])

    emit_attn_phase(s, cm, T, 2, qT, [k1, k2], vt, None, io["oa"], fin, name="a", lag=cm["lag"])
    s.release(m)


def emit_mix_b(s, cm, io, T):
    nc = s.nc
    NT = T // 128
    NQB = T // 512
    m = s.mark()
    qT = s.sb("b_q", [128, T], BF16)
    kT = s.sb("b_k", [128, T], BF16)
    vt = s.sb("b_v", [128, NT, 128], BF16)
    s.op("pool", lambda: nc.gpsimd.memset(qT[64:128, :], 0.0), writes=[qT])
    s.op("dve", lambda: nc.vector.memset(kT[64:128, :], 0.0), writes=[kT])
    s.dma("sp", qT[0:64, :], io["qb"][:, :], writes=[qT])
    s.dma("sp", kT[0:64, :], io["kb"][:, :], writes=[kT])
    load_vt(s, vt, io, "vb", T)
    fl = s.sb("fl", [128, NT], F32)
    fb = s.sb("fb", [128, 2], F32)
    tu = s.sb("tu", [128, 128], F32)
    on = s.sb("on", [128, 128], F32)
    if "flog_sb" not in io:
        s.dma("sp", fl[:, :], io["flog"][:, :], writes=[fl])
    s.dma("sp", fb[:, 0:1], io["fbias"][:, :], writes=[fb])
    s.dma("sp", tu[:, :], io["triuf"][:, :], writes=[tu])
    s.dma("sp", on[:, :], io["onesf"][:, :], writes=[on])
    s.op("dve", lambda: nc.vector.tensor_scalar(out=fb[:, 1:2], in0=fb[:, 0:1], scalar1=-1.0, scalar2=None, op0=ALU.mult),
         reads=[fb], writes=[fb])
    if "flog_sb" in io:
        fsb, fap = io["flog_sb"]
        s.op("act", lambda: nc.scalar.activation(out=fl[:, :], in_=fap, func=AF.Exp, bias=fb[:, 1:2], scale=-1.0),
             reads=[fsb, fb], writes=[fl])
    else:
        s.op("act", lambda: nc.scalar.activation(out=fl[:, :], in_=fl[:, :], func=AF.Exp, bias=fb[:, 1:2], scale=-1.0),
             reads=[fl, fb], writes=[fl])
    s.op("act", lambda: nc.scalar.activation(out=fl[:, :], in_=fl[:, :], func=AF.Ln, bias=1.0, scale=1.0),
         reads=[fl], writes=[fl])
    pc = cm["pf"]
    pcv = pc[:, 0, :]
    s.op("pe", lambda: nc.tensor.matmul(pc[:, 0, 0:NT], lhsT=tu[:, :], rhs=fl[:, :], start=True, stop=True),
         reads=[tu, fl], writes=[pc])
    s.op("pe", lambda: nc.tensor.matmul(pc[:, 1, 0:NT], lhsT=on[:, :], rhs=fl[:, :], start=True, stop=True),
         reads=[on, fl], writes=[pc])
    cc = s.sb("cc", [128, NT], F32)
    inc_ = s.sb("inc", [128, NT], F32)
    tmpc = s.sb("tmpc", [128, NT], F32)
    s.op("dve", lambda: nc.vector.tensor_copy(out=inc_[:, :], in_=pc[:, 1, 0:NT]), reads=[pc], writes=[inc_])
    sh = 1
    while sh < NT:
        s.op("dve", lambda: nc.vector.tensor_copy(out=tmpc[:, :], in_=inc_[:, :]), reads=[inc_], writes=[tmpc])
        s.op("dve", lambda: nc.vector.tensor_tensor(out=inc_[:, sh:NT], in0=tmpc[:, sh:NT], in1=tmpc[:, 0:NT - sh], op=ALU.add),
             reads=[tmpc], writes=[inc_])
        sh *= 2
    s.op("dve", lambda: nc.vector.tensor_tensor(out=cc[:, :], in0=pc[:, 0, 0:NT], in1=inc_[:, :], op=ALU.add),
         reads=[pc, inc_], writes=[cc])
    s.op("dve", lambda: nc.vector.tensor_tensor(out=tmpc[:, :], in0=cc[:, :], in1=pc[:, 1, 0:NT], op=ALU.subtract),
         reads=[pc, cc], writes=[tmpc])
    btab = s.sb("btab", [128, NQB, NT], F32)
    for qb in range(NQB):
        s.op("dve", lambda: nc.vector.tensor_scalar(out=btab[:, qb, :], in0=tmpc[:, :], scalar1=inc_[:, 4 * qb + 1:4 * qb + 2],
                                                    scalar2=None, op0=ALU.subtract), reads=[tmpc, inc_], writes=[btab])

    def bias_fn(qb, t):
        return btab, btab[:, qb, t:t + 1]

    def fin(qb, pos):
        pf = cm["pf"]
        emit_o_to_tokmajor(s, cm, pos[0], pf, 0)
        ot = cm["ot_rr"].next()
        for j in range(4):
            st = cm["st_rr"].next()
            s.op("dve", lambda: nc.vector.tensor_scalar(out=st[:, 0:1], in0=pf[:, j, 64:65], scalar1=1e-30, scalar2=None,
                                                        op0=ALU.max), reads=[pf], writes=[st])
            s.op("dve", lambda: nc.vector.reciprocal(out=st[:, 0:1], in_=st[:, 0:1]), reads=[st], writes=[st])
            s.op("dve", lambda: nc.vector.tensor_scalar(out=ot[:, j, :], in0=pf[:, j, 0:64], scalar1=st[:, 0:1], scalar2=None,
                                                        op0=ALU.mult), reads=[pf, st], writes=[ot])
        s.dma("sp", io["ob"][qb * 512:(qb + 1) * 512, :].rearrange("(j p) d -> p j d", p=128), ot[:, :, :], reads=[ot])

    emit_attn_phase(s, cm, T, 1, qT, [kT], vt, None, io["ob"], fin, bias_fn=bias_fn, name="b", lag=cm["lag"])
    s.release(m)


def mix_consts():
    k = np.arange(128)
    tri = np.where(k[:, None] > k[None, :], NEG, 0.0).astype(np.float32)
    return dict(identb=_bf(np.eye(128, dtype=np.float32)), identf=np.eye(128, dtype=np.float32), trib=_bf(tri),
                triuf=np.triu(np.ones((128, 128), np.float32)), onesf=np.ones((128, 128), np.float32))


def mix_decl_c(nc, io, T):
    NT = T // 128
    QL = NT // 4
    NCT = max(1, T // 2048)
    io.update(dict(
        qc=dram_in(nc, "qc", [128, QL, 512], BF16),
        kskw=dram_in(nc, "kskw", [128, T], BF16),
        vs=dram_in(nc, "vs", [128, NT, 65], BF16), vw=dram_in(nc, "vw", [128, NT, 65], BF16),
        kvin=dram_in(nc, "kvin", [128, T], BF16),
        w1=dram_in(nc, "w1", [2, 2048, 256]), b1=dram_in(nc, "b1", [128, 4]),
        peT=dram_in(nc, "peT", [128, 32]),
        w2=dram_in(nc, "w2", [2, 256, 64]), b2=dram_in(nc, "b2", [2, 64]), b2c=dram_in(nc, "b2c", [64, 1]),
        kgain=dram_in(nc, "kgain", [64, 1]),
        ng=dram_in(nc, "ng", [128, QL, 12]),
        cmask=dram_in(nc, "cmask", [128, QL, NCT, 128], BF16),
        smask=dram_in(nc, "smask", [128, 4, 128], BF16), wmask=dram_in(nc, "wmask", [128, 8, 128], BF16),
        impA=dram_in(nc, "impA", [128, QL, 128]), impB=dram_in(nc, "impB", [128, QL, 128]),
        emat=dram_in(nc, "emat", [128, NT, 128], BF16), ovl=dram_in(nc, "ovl", [128, NCT, 128], BF16),
        ones64=dram_in(nc, "ones64", [64, 64], BF16), onesrow=dram_in(nc, "onesrow", [1, 128], BF16),
        oc=dram_out(nc, "oc", [QL * 128, 256], BF16),
    ))
    return io


def emit_gelu(s, zin_ap, zin_b, out_ap, out_b, tmp, shape_sl):
    nc = s.nc
    t = tmp
    s.op("act", lambda: nc.scalar.activation(out=t[shape_sl], in_=zin_ap, func=AF.Square), reads=[zin_b], writes=[t])
    s.op("dve", lambda: nc.vector.tensor_scalar(out=t[shape_sl], in0=t[shape_sl], scalar1=0.044715, scalar2=1.0,
                                                op0=ALU.mult, op1=ALU.add), reads=[t], writes=[t])
    s.op("dve", lambda: nc.vector.tensor_tensor(out=t[shape_sl], in0=t[shape_sl], in1=zin_ap, op=ALU.mult),
         reads=[t, zin_b], writes=[t])
    s.op("act", lambda: nc.scalar.activation(out=t[shape_sl], in_=t[shape_sl], func=AF.Exp, scale=-GELU_C), reads=[t], writes=[t])
    s.op("dve", lambda: nc.vector.tensor_scalar(out=t[shape_sl], in0=t[shape_sl], scalar1=1.0, scalar2=None, op0=ALU.add),
         reads=[t], writes=[t])
    s.op("dve", lambda: nc.vector.reciprocal(out=t[shape_sl], in_=t[shape_sl]), reads=[t], writes=[t])
    s.op("dve", lambda: nc.vector.tensor_tensor(out=out_ap, in0=t[shape_sl], in1=zin_ap, op=ALU.mult),
         reads=[t, zin_b], writes=[out_b])


def emit_mix_c(s, cm, io, T, cs=None):
    fused = cs is not None
    cs = cs if fused else [None]
    nc = s.nc
    NT = T // 128
    QL = NT // 4
    NCT = max(1, T // 2048)
    Nc = T // 16 - 1
    NCP = NCT * 128 if Nc > 128 else 128
    NCW = min(Nc, 511)
    assert Nc <= 511
    m = s.mark()
    ident_b = cm["ident_b"]
    ps_l = cm["ps_rr"].items
    po_l = cm["po_rr"].items
    pf, pf2 = cm["pf"], cm["pf2"]

    def ld(name, shape, dt, src, q="sp"):
        b = s.sb(name, shape, dt)
        idx = tuple(slice(None) for _ in shape)
        s.dma(q, b[idx], src, writes=[b])
        return b

    qc = s.sb("c_q", [128, QL, 512], BF16)
    qc2 = s.sb("c_q2", [128, QL, 512], BF16)
    s.op("pool", lambda: nc.gpsimd.memset(qc[64:128, :, :], 0.0), writes=[qc])
    s.op("dve", lambda: nc.vector.memset(qc2[0:64, :, :], 0.0), writes=[qc2])
    kk = ld("c_kk", [128, T], BF16, io["kskw"][:, :])
    vs = s.sb("c_vs", [128, NT, 128], BF16)
    vw = s.sb("c_vw", [128, NT, 128], BF16)
    load_vt(s, vs, io, "vs", T)
    load_vt(s, vw, io, "vw", T)
    emat = ld("c_e", [128, NT, 128], BF16, io["emat"][:, :, :])
    ovl = ld("c_ovl", [128, NCT, 128], BF16, io["ovl"][:, :, :])
    smask = s.sb("c_sm", [128, 4, 128], BF16)
    wmask = s.sb("c_wm", [128, 8, 128], BF16)
    ngt = s.sb("c_ng", [128, QL, 12], F32)
    ones64 = ld("c_o64", [64, 64], BF16, io["ones64"][:, :])
    onesrow = ld("c_orow", [1, 128], BF16, io["onesrow"][:, :])
    kgain = ld("c_kg", [64, 1], F32, io["kgain"][:, :])
    b2c = ld("c_b2c", [64, 1], F32, io["b2c"][:, :])
    b1 = ld("c_b1", [128, 4], F32, io["b1"][:, :])

    ktc = s.sb("c_ktc", [128, NCP], BF16)
    vc = s.sb("c_vc", [128, NCT, 128], BF16)
    s.op("dve", lambda: nc.vector.memset(ktc[:, :], 0.0), writes=[ktc])
    s.op("dve", lambda: nc.vector.memset(vc[:, :, :], 0.0), writes=[vc])
    s.op("dve", lambda: nc.vector.memset(vc[:, :, 64:65], 1.0), writes=[vc])

    m2 = s.mark()
    kvin = ld("c_kvin", [128, T], BF16, io["kvin"][:, :])
    w1sb = s.sb("c_w1", [128, 32, 256], BF16)
    for x in range(2):
        s.dma("pool", w1sb[x * 64:(x + 1) * 64, :, :], io["w1"][x].rearrange("(j d) f -> d j f", d=64), writes=[w1sb])
    peT = s.sb("c_pe", [128, 32], BF16)
    s.dma("pool", peT[:, :], io["peT"][:, :], writes=[peT])
    w2sb = s.sb("c_w2", [128, 2, 2, 64], BF16)
    for x in range(2):
        s.dma("pool", w2sb[:, x, :, :], io["w2"][x].rearrange("(hh f) d -> f hh d", f=128), writes=[w2sb])
    b2row = s.sb("c_b2r", [1, 64], BF16)
    s.dma("pool", b2row[:, :], io["b2"][1:2, :], writes=[b2row])
    hacc = [ps_l[0], ps_l[1], ps_l[2], po_l[0]]
    pcol = po_l[1]
    for x in range(2):
        for hh in range(2):
            hp = hacc[x * 2 + hh]
            for j in range(32):
                s.op("pe", lambda: nc.tensor.matmul(hp[:, 0:NCW], lhsT=w1sb[x * 64:(x + 1) * 64, j, hh * 128:(hh + 1) * 128],
                                                    rhs=kvin[x * 64:(x + 1) * 64, j:j + 16 * (NCW - 1) + 1:16],
                                                    start=(j == 0), stop=(j == 31)),
                     reads=[w1sb, kvin], writes=[hp], inc=(j == 31))
            for j in range(32):
                s.op("pe", lambda: nc.tensor.matmul(pcol[:, x * 2 + hh:x * 2 + hh + 1],
                                                    lhsT=w1sb[x * 64:(x + 1) * 64, j, hh * 128:(hh + 1) * 128],
                                                    rhs=peT[x * 64:(x + 1) * 64, j:j + 1], start=(j == 0), stop=(j == 31)),
                     reads=[w1sb, peT], writes=[pcol], inc=(j == 31))
    hbias = s.sb("c_hb", [128, 4], F32)
    s.op("dve", lambda: nc.vector.tensor_tensor(out=hbias[:, :], in0=pcol[:, 0:4], in1=b1[:, :], op=ALU.add),
         reads=[pcol, b1], writes=[hbias])
    gh = []
    for x in range(2):
        for hh in range(2):
            k = x * 2 + hh
            z = s.sb("c_z%d" % k, [128, 512], F32)
            tmp = s.sb("c_zt%d" % k, [128, 512], F32)
            gb = s.sb("c_g%d" % k, [128, 512], BF16)
            s.op("act", lambda: nc.scalar.activation(out=z[:, 0:NCW], in_=hacc[k][:, 0:NCW], func=AF.Identity,
                                                     bias=hbias[:, k:k + 1], scale=1.0), reads=[hacc[k], hbias], writes=[z])
            emit_gelu(s, z[:, 0:NCW], z, gb[:, 0:NCW], gb, tmp, (slice(None), slice(0, NCW)))
            gh.append(gb)
    pk = po_l[2]
    for hh in range(2):
        s.op("pe", lambda: nc.tensor.matmul(pk[0:64, 0:NCW], lhsT=w2sb[:, 0, hh, :], rhs=gh[hh][:, 0:NCW],
                                            start=(hh == 0), stop=(hh == 1)), reads=[w2sb, gh[hh]], writes=[pk], inc=(hh == 1))
    kz = s.sb("c_kz", [64, 512], F32)
    ksq = s.sb("c_ksq", [64, 512], BF16)
    krs = s.sb("c_krs", [64, 512], F32)
    s.op("act", lambda: nc.scalar.activation(out=kz[:, 0:NCW], in_=pk[0:64, 0:NCW], func=AF.Identity, bias=b2c[:, 0:1], scale=1.0),
         reads=[pk, b2c], writes=[kz])
    s.op("act", lambda: nc.scalar.activation(out=ksq[:, 0:NCW], in_=kz[:, 0:NCW], func=AF.Square), reads=[kz], writes=[ksq])
    pq = ps_l[0]
    s.op("pe", lambda: nc.tensor.matmul(pq[0:64, 0:NCW], lhsT=ones64[:, :], rhs=ksq[:, 0:NCW], start=True, stop=True),
         reads=[ones64, ksq], writes=[pq])
    s.op("act", lambda: nc.scalar.activation(out=krs[:, 0:NCW], in_=pq[0:64, 0:NCW], func=AF.Ln, bias=cm["epsc"][0:64, 0:1],
                                             scale=1.0 / 64.0), reads=[pq, cm["epsc"]], writes=[krs])
    s.op("act", lambda: nc.scalar.activation(out=krs[:, 0:NCW], in_=krs[:, 0:NCW], func=AF.Exp, scale=-0.5), reads=[krs], writes=[krs])
    s.op("dve", lambda: nc.vector.scalar_tensor_tensor(out=ktc[0:64, 0:NCW], in0=kz[:, 0:NCW], scalar=kgain[:, 0:1], in1=krs[:, 0:NCW],
                                                       op0=ALU.mult, op1=ALU.mult), reads=[kz, kgain, krs], writes=[ktc])
    for nt in range(NCT):
        n0 = nt * 128
        nn = min(128, Nc - n0)
        pv = ps_l[1 + nt % 2]
        for hh in range(2):
            s.op("pe", lambda: nc.tensor.matmul(pv[0:nn, 0:64], lhsT=gh[2 + hh][:, n0:n0 + nn], rhs=w2sb[:, 1, hh, :],
                                                start=(hh == 0), stop=False), reads=[gh[2 + hh], w2sb], writes=[pv], inc=False)
        s.op("pe", lambda: nc.tensor.matmul(pv[0:nn, 0:64], lhsT=onesrow[0:1, 0:nn], rhs=b2row[0:1, :], start=False, stop=True),
             reads=[onesrow, b2row], writes=[pv])
        s.op("act", lambda: nc.scalar.copy(out=vc[0:nn, nt, 0:64], in_=pv[0:nn, 0:64]), reads=[pv], writes=[vc])
    s.release(m2)

    cmk_rr = RR([s.sb("c_cmk%d" % j, [128, NCT, 128], BF16) for j in range(2)])
    ia_rr = RR([s.sb("c_ia%d" % j, [128, 128], F32) for j in range(2)])
    ib_rr = RR([s.sb("c_ib%d" % j, [128, 128], F32) for j in range(2)])
    imp_rr = RR([s.sb("c_imp%d" % j, [128, 128], F32) for j in range(2)])
    imp2_rr = RR([s.sb("c_impb%d" % j, [128, 128], F32) for j in range(2)])
    m8_rr = RR([s.sb("c_m8%d" % j, [128, 16], F32) for j in range(2)])
    mbT_rr = RR([s.sb("c_mbT%d" % j, [128, 128], BF16) for j in range(2)])
    oco_rr = RR([s.sb("c_oc%d" % j, [128, 4, 64], F32) for j in range(2)])
    gw_rr = RR([s.sb("c_gw%d" % j, [128, 12], F32) for j in range(2)])
    oo_rr = RR([s.sb("c_oo%d" % j, [128, 4, 64], F32) for j in range(2)])
    ob_rr = RR([s.sb("c_ob%d" % j, [128, 4, 64], BF16) for j in range(2)])

    pipe = Pipe(2)

    def masked_tile(kbuf, prow, t, Q, masks, vbuf, vt_idx, po, first, last, extra=None):
        ps = cm["ps_rr"].next()
        nm = len(masks)
        s.op("pe", lambda: nc.tensor.matmul(ps[:, :], lhsT=kbuf[:, t * 128:(t + 1) * 128], rhs=Q,
                                            start=True, stop=(nm == 0)), reads=[kbuf, qc, qc2], writes=[ps], inc=(nm == 0))
        for mi, (la, lb, ra, rb) in enumerate(masks):
            for h in range(4):
                lastm = (mi == nm - 1 and h == 3)
                s.op("pe", lambda: nc.tensor.matmul(ps[:, h * 128:(h + 1) * 128], lhsT=la, rhs=ra, start=False, stop=lastm),
                     reads=[lb, rb], writes=[ps], inc=lastm)
        pt = cm["pt_rr"].next()
        s.op("act", lambda: nc.scalar.activation(out=pt[:, :], in_=ps[:, :], func=AF.Exp), reads=[ps], writes=[pt])

        def back(pt=pt, po=po, vbuf=vbuf, vt_idx=vt_idx, first=first, last=last, extra=extra):
            s.op("pe", lambda: nc.tensor.matmul(po[:, :], lhsT=vbuf[:, vt_idx, :], rhs=pt[:, :], start=first, stop=last),
                 reads=[vbuf, pt], writes=[po])
            if extra is not None:
                extra(pt)
        pipe.push(back)

    for ci in cs:
        def gk(key):
            return io[key][ci] if fused else io[key]
        if fused:
            for h in range(4):
                r0 = (h % 2) * 64
                srcq = io["zq"][h // 2][r0:r0 + 64, :].rearrange("d (i c q) -> d i c q", c=4, q=128)[:, :, ci, :]
                s.dma("sp", qc[0:64, :, h * 128:(h + 1) * 128], srcq, writes=[qc])
                s.dma("sp", qc2[64:128, :, h * 128:(h + 1) * 128], srcq, writes=[qc2])
            msb, mview = io["misc_sb"]
            s.op("act", lambda: nc.scalar.activation(out=ngt[:, :, :], in_=mview[:, ci:NT:4, 4:16], func=AF.Exp, scale=-1.0),
                 reads=[msb], writes=[ngt])
        else:
            s.dma("sp", qc[0:64, :, :], io["qc"][0:64, :, :], writes=[qc])
            s.dma("sp", qc2[64:128, :, :], io["qc"][64:128, :, :], writes=[qc2])
            s.dma("sp", ngt[:, :, :], io["ng"][:, :, :], writes=[ngt])
            s.op("act", lambda: nc.scalar.activation(out=ngt[:, :, :], in_=ngt[:, :, :], func=AF.Exp, scale=-1.0),
                 reads=[ngt], writes=[ngt])
        s.op("dve", lambda: nc.vector.tensor_scalar(out=ngt[:, :, :], in0=ngt[:, :, :], scalar1=1.0, scalar2=None, op0=ALU.add),
             reads=[ngt], writes=[ngt])
        s.op("dve", lambda: nc.vector.reciprocal(out=ngt[:, :, :], in_=ngt[:, :, :]), reads=[ngt], writes=[ngt])
        s.dma("sp", smask[:, :, :], gk("smask")[:, :, :], writes=[smask])
        s.dma("sp", wmask[:, :, :], gk("wmask")[:, :, :], writes=[wmask])
        for i in range(QL):
            Qlo = qc[:, i, :]
            Qhi = qc2[:, i, :]
            cmk = cmk_rr.next()
            ia = ia_rr.next()
            ib = ib_rr.next()
            s.dma("sp", cmk[:, :, :], gk("cmask")[:, i, :, :], writes=[cmk])
            s.dma("sp", ia[:, :], gk("impA")[:, i, :], writes=[ia])
            s.dma("sp", ib[:, :], gk("impB")[:, i, :], writes=[ib])
            po_c, po_s, po_w = po_l[0], po_l[1], po_l[2]
            nct = min(NCT, i // 4 + 1)
            for nt in range(nct):
                def imp_mm(pt, nt=nt, nct=nct):
                    for h in range(4):
                        s.op("pe", lambda: nc.tensor.matmul(pf2[:, h, :], lhsT=pt[:, h * 128:(h + 1) * 128], rhs=ovl[:, nt, :],
                                                            start=(nt == 0 and h == 0), stop=(nt == nct - 1 and h == 3),
                                                            skip_group_check=True), reads=[pt, ovl], writes=[pf2],
                             inc=(h == 3))
                masked_tile(ktc, (0, 64), nt, Qlo, [(ident_b[:, :], ident_b, cmk[:, nt, :], cmk)], vc, nt, po_c,
                            nt == 0, nt == nct - 1, extra=imp_mm)
            pipe.flush()
            emit_o_to_tokmajor(s, cm, po_c, pf, 0)
            st = cm["st_rr"].next()
            rsum = cm["st_rr"].next()
            gw = gw_rr.next()
            s.op("dve", lambda: nc.vector.tensor_scalar(out=st[:, 0:4], in0=pf[:, :, 64], scalar1=1e-30, scalar2=None, op0=ALU.max),
                 reads=[pf], writes=[st])
            s.op("dve", lambda: nc.vector.reciprocal(out=rsum[:, 0:4], in_=st[:, 0:4]), reads=[st], writes=[rsum])
            oco = oco_rr.next()
            s.op("dve", lambda: nc.vector.tensor_copy(out=oco[:, :, :], in_=pf[:, :, 0:64]), reads=[pf], writes=[oco])
            imp = imp_rr.next()
            s.op("dve", lambda: nc.vector.tensor_scalar(out=imp[:, :], in0=pf2[:, 0, :], scalar1=rsum[:, 0:1], scalar2=None, op0=ALU.mult),
                 reads=[pf2, rsum], writes=[imp])
            for h in range(1, 4):
                s.op("dve", lambda: nc.vector.scalar_tensor_tensor(out=imp[:, :], in0=pf2[:, h, :], scalar=rsum[:, h:h + 1], in1=imp[:, :],
                                                                   op0=ALU.mult, op1=ALU.add), reads=[pf2, rsum, imp], writes=[imp])
            s.op("dve", lambda: nc.vector.tensor_tensor(out=imp[:, :], in0=imp[:, :], in1=ia[:, :], op=ALU.mult), reads=[imp, ia], writes=[imp])
            s.op("dve", lambda: nc.vector.tensor_tensor(out=imp[:, :], in0=imp[:, :], in1=ib[:, :], op=ALU.add), reads=[imp, ib], writes=[imp])
            m8 = m8_rr.next()
            imp2 = imp2_rr.next()
            s.op("dve", lambda: nc.vector.max(out=m8[:, 0:8], in_=imp[:, :]), reads=[imp], writes=[m8])
            s.op("dve", lambda: nc.vector.match_replace(out=imp2[:, :], in_to_replace=m8[:, 0:8], in_values=imp[:, :], imm_value=-1e9),
                 reads=[imp, m8], writes=[imp2])
            s.op("dve", lambda: nc.vector.max(out=m8[:, 8:16], in_=imp2[:, :]), reads=[imp2], writes=[m8])
            s.op("dve", lambda: nc.vector.tensor_scalar(out=imp2[:, :], in0=imp[:, :], scalar1=m8[:, 15:16], scalar2=NEG,
                                                        op0=ALU.is_lt, op1=ALU.mult), reads=[imp, m8], writes=[imp2])
            tl = [4 * (i - 1) + u for u in range(8) if 4 * (i - 1) + u >= 0]
            for t in tl:
                u = t - 4 * (i - 1)
                masked_tile(kk, (64, 128), t, Qhi, [(ident_b[:, :], ident_b, wmask[:, u, :], wmask)], vw, t, po_w,
                            t == tl[0], t == tl[-1])
            ptr = cm["ps_rr"].next()
            s.op("pe", lambda: nc.tensor.transpose(out=ptr[:, 0:128], in_=imp2[:, :], identity=cm["ident_f"][:, :]),
                 reads=[imp2, cm["ident_f"]], writes=[ptr])
            mbT = mbT_rr.next()
            s.op("act", lambda: nc.scalar.copy(out=mbT[:, :], in_=ptr[:, 0:128]), reads=[ptr], writes=[mbT])
            nts = 4 * i + 4
            for t in range(nts):
                masks = [(emat[:, t, :], emat, mbT[:, :], mbT)]
                if t >= 4 * i:
                    masks.append((ident_b[:, :], ident_b, smask[:, t - 4 * i, :], smask))
                masked_tile(kk, (0, 64), t, Qlo, masks, vs, t, po_s, t == 0, t == nts - 1)
            pipe.flush()
            emit_o_to_tokmajor(s, cm, po_s, pf, 0)
            st2 = cm["st_rr"].next()
            s.op("dve", lambda: nc.vector.tensor_scalar(out=st2[:, 0:4], in0=pf[:, :, 64], scalar1=1e-30, scalar2=None, op0=ALU.max),
                 reads=[pf], writes=[st2])
            s.op("dve", lambda: nc.vector.reciprocal(out=st2[:, 0:4], in_=st2[:, 0:4]), reads=[st2], writes=[st2])
            gv = ngt[:, i, :].rearrange("p (h b) -> p h b", b=3)
            gwv = gw[:, :].rearrange("p (h b) -> p h b", b=3)
            s.op("dve", lambda: nc.vector.tensor_tensor(out=gwv[:, :, 0], in0=gv[:, :, 0], in1=rsum[:, 0:4], op=ALU.mult),
                 reads=[ngt, rsum], writes=[gw])
            s.op("dve", lambda: nc.vector.tensor_tensor(out=gwv[:, :, 1], in0=gv[:, :, 1], in1=st2[:, 0:4], op=ALU.mult),
                 reads=[ngt, st2], writes=[gw])
            oo = oo_rr.next()
            for h in range(4):
                s.op("dve", lambda: nc.vector.tensor_scalar(out=oo[:, h, :], in0=oco[:, h, :], scalar1=gw[:, 3 * h:3 * h + 1], scalar2=None,
                                                            op0=ALU.mult), reads=[oco, gw], writes=[oo])
                s.op("dve", lambda: nc.vector.scalar_tensor_tensor(out=oo[:, h, :], in0=pf[:, h, 0:64], scalar=gw[:, 3 * h + 1:3 * h + 2],
                                                                   in1=oo[:, h, :], op0=ALU.mult, op1=ALU.add), reads=[pf, gw, oo], writes=[oo])
            emit_o_to_tokmajor(s, cm, po_w, pf, 0)
            st3 = cm["st_rr"].next()
            s.op("dve", lambda: nc.vector.tensor_scalar(out=st3[:, 0:4], in0=pf[:, :, 64], scalar1=1e-30, scalar2=None, op0=ALU.max),
                 reads=[pf], writes=[st3])
            s.op("dve", lambda: nc.vector.reciprocal(out=st3[:, 0:4], in_=st3[:, 0:4]), reads=[st3], writes=[st3])
            s.op("dve", lambda: nc.vector.tensor_tensor(out=gwv[:, :, 2], in0=gv[:, :, 2], in1=st3[:, 0:4], op=ALU.mult),
                 reads=[ngt, st3], writes=[gw])
            ob = ob_rr.next()
            for h in range(4):
                s.op("dve", lambda: nc.vector.scalar_tensor_tensor(out=ob[:, h, :], in0=pf[:, h, 0:64], scalar=gw[:, 3 * h + 2:3 * h + 3],
                                                                   in1=oo[:, h, :], op0=ALU.mult, op1=ALU.add), reads=[pf, gw, oo], writes=[ob])
            orow = ((4 * i + ci) if fused else i) * 128
            s.dma("sp", io["oc"][orow:orow + 128, :], ob[:, :, :].rearrange("p h d -> p (h d)"), reads=[ob])
    s.release(m)


def build_mix(T, parts="abc"):
    nc = bass.Bass("TRN2", target_bir_lowering=False)
    io = mix_decl(nc, T)
    if "c" in parts:
        mix_decl_c(nc, io, T)
    s = S(nc)
    cm = mix_common(s, io)
    if "a" in parts:
        emit_mix_a(s, cm, io, T)
    if "b" in parts:
        emit_mix_b(s, cm, io, T)
    if "c" in parts:
        emit_mix_c(s, cm, io, T)
    s.finish()
    s.close()
    return nc


def mix_consts_c(T, c):
    NT = T // 128
    QL = NT // 4
    NCT = max(1, T // 2048)
    Nc = T // 16 - 1
    NS = T // 64
    ar = np.arange(128)
    cmask = np.zeros((128, QL, NCT, 128), np.float32)
    impA = np.zeros((128, QL, 128), np.float32)
    impB = np.zeros((128, QL, 128), np.float32)
    for i in range(QL):
        qpos = 128 * (4 * i + c) + ar
        for nt in range(NCT):
            n = 128 * nt + ar
            ok = (16 * n[:, None] + 31 <= qpos[None, :]) & (n[:, None] < Nc)
            cmask[:, i, nt, :] = np.where(ok, 0.0, NEG)
        j = ar
        cur = qpos // 64
        forced = (j[None, :] == 0) | (j[None, :] == cur[:, None]) | (j[None, :] == cur[:, None] - 1)
        valid = (j[None, :] * 64 <= qpos[:, None]) & (j[None, :] < NS)
        impA[:, i, :] = (valid & ~forced).astype(np.float32)
        impB[:, i, :] = np.where(forced & (j[None, :] < NS), 1.0e4, np.where(valid, 0.0, -1.0))
    smask = np.zeros((128, 4, 128), np.float32)
    for u in range(4):
        kpos = 128 * u + ar
        qp = 128 * c + ar
        smask[:, u, :] = np.where(kpos[:, None] <= qp[None, :], 0.0, NEG)
    wmask = np.zeros((128, 8, 128), np.float32)
    for u in range(8):
        dist = 128 * (c + 4 - u) + ar[None, :] - ar[:, None]
        wmask[:, u, :] = np.where((dist >= 0) & (dist < 512), 0.0, NEG)
    emat = np.zeros((128, NT, 128), np.float32)
    for t in range(NT):
        for k in range(128):
            jj = 2 * t + k // 64
            if jj < 128:
                emat[jj, t, k] = 1.0
    ovl = np.zeros((128, NCT, 128), np.float32)
    for nt in range(NCT):
        n = 128 * nt + ar
        o = (n[:, None] * 16 < (ar[None, :] + 1) * 64) & (n[:, None] * 16 + 32 > ar[None, :] * 64) & (n[:, None] < Nc) \
            & (ar[None, :] < NS)
        ovl[:, nt, :] = o
    return dict(cmask=_bf(cmask), impA=impA, impB=impB, smask=_bf(smask), wmask=_bf(wmask), emat=_bf(emat), ovl=_bf(ovl),
                ones64=_bf(np.ones((64, 64), np.float32)), onesrow=_bf(np.ones((1, 128), np.float32)))


def build_merge(NT, TB=512):
    nc = bass.Bass("TRN2", target_bir_lowering=False)
    x = dram_in(nc, "x", [NT, D])
    g = dram_in(nc, "g", [D])
    w_in = dram_in(nc, "w_in", [D, 6800])
    w_br = dram_in(nc, "w_br", [4, 256, D])
    w_o = dram_in(nc, "w_o", [D, D])
    ident = dram_in(nc, "ident", [128, 128], BF16)
    obr = dram_in(nc, "obr", [NT, D], BF16)
    y = dram_out(nc, "y", [NT, D])
    s = S(nc)
    emit_merge(s, x, g, w_in, w_br, w_o, ident, obr, y, NT, TB)
    s.finish()
    s.close()
    return nc


def emit_merge(s, x, g, w_in, w_br, w_o, ident, obr, y, NT, TB=512):
    nc = s.nc
    m_ = s.mark()
    ntile = TB // 128
    ident_b = s.sb("ident_b", [128, 128], BF16)
    s.dma("sp", ident_b[:, :], ident[:, :], writes=[ident_b])
    gcol = s.sb("gcol", [128, NKC], F32)
    s.dma("sp", gcol[:, :], g.rearrange("(c p) -> p c", p=128), writes=[gcol], allow_slow_non_contiguous=True)
    epsc = s.sb("epsc", [128, 1], F32)
    s.op("dve", lambda: nc.vector.memset(epsc[:, :], EPS), writes=[epsc])
    wg = [s.sb("wg%d" % c, [128, 4096], BF16) for c in range(NKC)]
    wb = [s.sb("wb%d" % c, [128, D], BF16) for c in range(8)]
    wo = [s.sb("wo%d" % c, [128, D], BF16) for c in range(NKC)]
    for c in range(NKC):
        for hf in range(2):
            s.dma("pool", wg[c][:, hf * 2048:(hf + 1) * 2048], w_in[c * 128:(c + 1) * 128, 2704 + hf * 2048:2704 + (hf + 1) * 2048],
                  writes=[wg[c]])
    for n in range(4):
        for cc in range(2):
            s.dma("pool", wb[2 * n + cc][:, :], w_br[n, cc * 128:(cc + 1) * 128, :], writes=[wb[2 * n + cc]])
    for c in range(NKC):
        s.dma("pool", wo[c][:, :], w_o[c * 128:(c + 1) * 128, :], writes=[wo[c]])
    xn = [s.sb("xn%d" % j, [128, D], F32) for j in range(ntile)]
    xr_rr = RR([s.sb("xr%d" % j, [128, D], F32) for j in range(2)])
    ots = [[s.sb("ot%d_%d" % (k, j), [128, D], BF16) for j in range(ntile)] for k in range(2)]
    hb_rr = RR([s.sb("hb%d" % j, [128, D], BF16) for j in range(2 * ntile)])
    stat_rr = RR([s.sb("st%d" % j, [128, 16], F32) for j in range(4)])
    hT = s.sb("hT", [128, NKC, TB], BF16)
    oT = s.sb("oT", [128, 8, TB], BF16)
    mT = [s.sb("mT%d" % c, [128, TB], BF16) for c in range(8)]
    pT_rr = RR([s.ps("pT%d" % j, [128, TB], BF16) for j in range(2)])
    pg_rr = RR([s.ps("pg%d" % j, [128, 512], F32) for j in range(2)])
    pp_rr = RR([s.ps("pp%d" % j, [128, 512], F32) for j in range(2)])
    po_rr = RR([s.ps("po%d" % j, [128, 512], F32) for j in range(2)])
    sg_rr = RR([s.sb("sg%d" % j, [128, TB], F32) for j in range(3)])
    acc_rr = RR([s.sb("acc%d" % j, [128, TB], F32) for j in range(2)])
    nblk = NT // TB

    def prep_a(tb):
        ot = ots[tb % 2]
        for j in range(ntile):
            r0 = tb * TB + j * 128
            s.dma("sp", xn[j][:, :], x[r0:r0 + 128, :], writes=[xn[j]])
            s.dma("sp", ot[j][:, :], obr[r0:r0 + 128, :], writes=[ot[j]])
        return emit_norm(s, epsc, xn, hb_rr, None, stat_rr, ntile)

    def prep_b(tb, hbs):
        ot = ots[tb % 2]
        emit_transpose_T(s, hbs, gcol, hT, ident_b, pT_rr, ntile)
        for c in range(8):
            pT = pT_rr.next()
            for j in range(ntile):
                s.op("pe", lambda: nc.tensor.transpose(out=pT[:, j * 128:(j + 1) * 128], in_=ot[j][:, c * 128:(c + 1) * 128],
                                                       identity=ident_b[:, :]), reads=[ot[j], ident_b], writes=[pT], inc=(j == ntile - 1))
            if c % 2 == 0:
                s.op("dve", lambda: nc.vector.tensor_copy(out=oT[:, c, :], in_=pT[:, 0:TB]), reads=[pT], writes=[oT])
            else:
                s.op("act", lambda: nc.scalar.copy(out=oT[:, c, :], in_=pT[:, 0:TB]), reads=[pT], writes=[oT])

    hbs_next = prep_a(0)
    prep_b(0, hbs_next)
    for tb in range(nblk):
        t0 = tb * TB
        if tb + 1 < nblk:
            hbs_next = prep_a(tb + 1)
        for dc in range(8):
            acc = acc_rr.next()
            for n in range(4):
                pg = pg_rr.next()
                pp = pp_rr.next()
                for c in range(NKC):
                    s.op("pe", lambda: nc.tensor.matmul(pg[:, 0:TB], lhsT=wg[c][:, n * 1024 + dc * 128:n * 1024 + (dc + 1) * 128],
                                                        rhs=hT[:, c, :], start=(c == 0), stop=(c == NKC - 1)),
                         reads=[wg[c], hT], writes=[pg], inc=(c == NKC - 1))
                for cc in range(2):
                    s.op("pe", lambda: nc.tensor.matmul(pp[:, 0:TB], lhsT=wb[2 * n + cc][:, dc * 128:(dc + 1) * 128],
                                                        rhs=oT[:, 2 * n + cc, :], start=(cc == 0), stop=(cc == 1)),
                         reads=[wb[2 * n + cc], oT], writes=[pp], inc=(cc == 1))
                sg = sg_rr.next()
                s.op("act", lambda: nc.scalar.activation(out=sg[:, :], in_=pg[:, 0:TB], func=AF.Sigmoid), reads=[pg], writes=[sg])
                if n == 0:
                    s.op("dve", lambda: nc.vector.tensor_tensor(out=acc[:, :], in0=sg[:, :], in1=pp[:, 0:TB], op=ALU.mult),
                         reads=[sg, pp], writes=[acc])
                else:
                    s.op("dve", lambda: nc.vector.tensor_tensor(out=sg[:, :], in0=sg[:, :], in1=pp[:, 0:TB], op=ALU.mult),
                         reads=[sg, pp], writes=[sg])
                    if n < 3:
                        s.op("pool", lambda: nc.gpsimd.tensor_tensor(out=acc[:, :], in0=acc[:, :], in1=sg[:, :], op=ALU.add),
                             reads=[acc, sg], writes=[acc])
                    else:
                        s.op("pool", lambda: nc.gpsimd.tensor_tensor(out=mT[dc][:, :], in0=acc[:, :], in1=sg[:, :], op=ALU.add),
                             reads=[acc, sg], writes=[mT[dc]])
        if tb + 1 < nblk:
            prep_b(tb + 1, hbs_next)
        for j in range(ntile):
            xr = xr_rr.next()
            s.dma("sp", xr[:, :], x[t0 + j * 128:t0 + (j + 1) * 128, :], writes=[xr])
            for hf in range(2):
                po = po_rr.next()
                for dc in range(8):
                    s.op("pe", lambda: nc.tensor.matmul(po[:, :], lhsT=mT[dc][:, j * 128:(j + 1) * 128],
                                                        rhs=wo[dc][:, hf * 512:(hf + 1) * 512], start=(dc == 0), stop=(dc == 7)),
                         reads=[mT[dc], wo[dc]], writes=[po], inc=(dc == 7))
                s.op("dve", lambda: nc.vector.tensor_tensor(out=xr[:, hf * 512:(hf + 1) * 512], in0=po[:, :],
                                                            in1=xr[:, hf * 512:(hf + 1) * 512], op=ALU.add),
                     reads=[po, xr], writes=[xr])
            s.dma("sp", y[t0 + j * 128:t0 + (j + 1) * 128, :], xr[:, :], reads=[xr])
    s.release(m_)


PARAM_SHAPES = dict(
    ffn1_norm=("L", D), ffn1_w_in=("L", D, 2 * DFF), ffn1_w_out=("L", DFF, D), mix_norm=("L", D), w_in=("L", D, 6800),
    nsa_phi_w1=("L", 2, 2048, 256), nsa_phi_w2=("L", 2, 256, 64), nsa_phi_b2=("L", 2, 64),
    w_branch=("L", 4, 256, D), w_out=("L", D, D), ffn2_norm=("L", D), ffn2_w_in=("L", D, 2 * DFF), ffn2_w_out=("L", DFF, D),
    gains=("L", 128, 6), vgain=("L", 128, 256), wsT=("L", 4, 128, 128), bsT=("L", 128, 4), lamp=("L", 128, 4, 32),
    lami=("L", 128, 2), fbias=("L", 4, 128, 1), b1l=("L", 128, 4), peT=("L", 128, 32), b2c=("L", 64, 1), kgain=("L", 64, 1),
)


def fused_const_shapes(T):
    NT = T // 128
    QL = NT // 4
    NCT = max(1, T // 2048)
    return dict(
        ident=([128, 128], BF16), identf=([128, 128], F32), blk=([2, 128, 128], BF16), triu=([128, 128], F32),
        trib=([128, 128], BF16), triuf=([128, 128], F32), onesf=([128, 128], F32), ones64=([64, 64], BF16),
        onesrow=([1, 128], BF16), cmask=([4, 128, QL, NCT, 128], BF16), smask=([4, 128, 4, 128], BF16),
        wmask=([4, 128, 8, 128], BF16), impA=([4, 128, QL, 128], F32), impB=([4, 128, QL, 128], F32),
        emat=([128, NT, 128], BF16), ovl=([128, NCT, 128], BF16))


def fused_consts(T):
    pc = proj_consts()
    mc = mix_consts()
    cc = [mix_consts_c(T, c) for c in range(4)]
    d = dict(ident=pc["ident"], identf=mc["identf"], blk=pc["blk"], triu=pc["triu"], trib=mc["trib"], triuf=mc["triuf"],
             onesf=mc["onesf"], ones64=cc[0]["ones64"], onesrow=cc[0]["onesrow"], emat=cc[0]["emat"], ovl=cc[0]["ovl"])
    for k in ("cmask", "smask", "wmask", "impA", "impB"):
        d[k] = np.ascontiguousarray(np.stack([cc[c][k] for c in range(4)], 0))
    return d


def emit_mix_fused(s, F, l, T):
    nc = s.nc
    NT = T // 128
    m = s.mark()
    io0 = dict(identb=F["ident"], identf=F["identf"], trib=F["trib"])
    misc_sb = s.sb("misc_sb", [128, NT, 16], F32)
    m_ab = s.mark()
    cm = mix_common(s, io0, n_ps=4, with_pf2=False)
    for j0 in range(0, NT, 8):
        j1 = min(NT, j0 + 8)
        s.dma("sp", misc_sb[:, j0:j1, :], F["misc"][j0 * 128:j1 * 128, :].rearrange("(j p) c -> p j c", p=128), writes=[misc_sb])
    zfm, vab, obr = F["zfm"], F["vab"], F["obr"]
    for h in range(4):
        r0 = (h % 2) * 64
        io = dict(io0)
        io.update(qa=zfm[h // 2][r0:r0 + 64, :], ka=zfm[2 + h // 2][r0:r0 + 64, :], va_src=vab[:, h * 64:(h + 1) * 64],
                  lamp=F["lamp"][l], lami=F["lami"][l], oa=obr[:, h * 64:(h + 1) * 64])
        emit_mix_a(s, cm, io, T)
        io = dict(io0)
        io.update(qb=zfm[4 + h // 2][r0:r0 + 64, :], kb=zfm[6 + h // 2][r0:r0 + 64, :],
                  vb_src=vab[:, 256 + h * 64:256 + (h + 1) * 64], flog_sb=(misc_sb, misc_sb[:, :, h]), fbias=F["fbias"][l, h],
                  triuf=F["triuf"], onesf=F["onesf"], ob=obr[:, 256 + h * 64:256 + (h + 1) * 64])
        emit_mix_b(s, cm, io, T)
    s.release(m_ab)
    cm = mix_common(s, io0, n_ps=3, with_pf2=True)
    io = dict(io0)
    io.update(zq=(zfm[8], zfm[9]), kskw=zfm[10], kvin=zfm[11], vs_src=F["vsw"][:, 0:64], vw_src=F["vsw"][:, 64:128],
              misc_sb=(misc_sb, misc_sb), w1=F["nsa_phi_w1"][l], b1=F["b1l"][l], peT=F["peT"][l], w2=F["nsa_phi_w2"][l],
              b2=F["nsa_phi_b2"][l], b2c=F["b2c"][l], kgain=F["kgain"][l], oc=obr[:, 512:768])
    for k in ("cmask", "smask", "wmask", "impA", "impB", "emat", "ovl", "ones64", "onesrow"):
        io[k] = F[k]
    emit_mix_c(s, cm, io, T, cs=[0, 1, 2, 3])
    s.release(m)


def build_fused(T, L):
    nc = bass.Bass("TRN2", target_bir_lowering=False)
    F = {}
    F["x"] = dram_in(nc, "x", [T, D])
    for k, shp in PARAM_SHAPES.items():
        F[k] = dram_in(nc, k, [L if v == "L" else v for v in shp])
    for k, (shp, dt) in fused_const_shapes(T).items():
        F[k] = dram_in(nc, k, shp, dt)
    y = dram_out(nc, "y", [T, D])
    for k, shp, dt in (("xa", [T, D], F32), ("xb", [T, D], F32), ("xc", [T, D], F32), ("zfm", [NFM, 128, T], BF16),
                       ("vab", [T, 512], BF16), ("vsw", [T, 128], BF16), ("misc", [T, 16], F32), ("obr", [T, D], BF16)):
        F[k] = nc.dram_tensor("s_" + k, shp, dt).ap()
    s = S(nc)
    for l in range(L):
        x_in = F["x"] if l == 0 else F["xc"]
        emit_ffn(s, x_in, F["ffn1_norm"][l], F["ffn1_w_in"][l], F["ffn1_w_out"][l], F["ident"], F["xa"], T)
        a = dict(x=F["xa"], g=F["mix_norm"][l], w_in=F["w_in"][l], ident=F["ident"], gains=F["gains"][l], blk=F["blk"],
                 vgain=F["vgain"][l], wsT=F["wsT"][l], triu=F["triu"], bsT=F["bsT"][l], zfm=F["zfm"], vab=F["vab"],
                 vsw=F["vsw"], misc=F["misc"], od=F["obr"][:, 768:1024])
        emit_proj(s, a, T)
        emit_mix_fused(s, F, l, T)
        emit_merge(s, F["xa"], F["mix_norm"][l], F["w_in"][l], F["w_branch"][l], F["w_out"][l], F["ident"], F["obr"], F["xb"], T)
        x_out = y if l == L - 1 else F["xc"]
        emit_ffn(s, F["xb"], F["ffn2_norm"][l], F["ffn2_w_in"][l], F["ffn2_w_out"][l], F["ident"], x_out, T)
    s.finish()
    s.close()
    return nc


def fused_params(P, L):
    import math
    f32 = np.float32
    A = lambda a: np.ascontiguousarray(np.asarray(a, dtype=f32))
    d = {k: A(P[k]) for k in ("ffn1_norm", "ffn1_w_in", "ffn1_w_out", "mix_norm", "w_in", "nsa_phi_w1", "nsa_phi_w2",
                              "nsa_phi_b2", "w_branch", "w_out", "ffn2_norm", "ffn2_w_in", "ffn2_w_out")}
    tile = lambda v, n: np.tile(A(v), (1, n))
    d["gains"] = np.ascontiguousarray(np.stack([tile(P["diff_q_gain"], 4), tile(P["diff_k_gain"], 4), tile(P["fox_q_gain"], 2),
                                                tile(P["fox_k_gain"], 2), tile(P["nsa_q_gain"], 2), tile(P["nsa_k_gain"], 2)], 2))
    d["vgain"] = np.ascontiguousarray(np.broadcast_to(A(P["gmlp_v_gain"])[:, None, :], (L, 128, 256)))
    d["wsT"] = np.ascontiguousarray(A(P["gmlp_w_s"]).transpose(0, 1, 3, 2))
    d["bsT"] = np.ascontiguousarray(A(P["gmlp_b_s"]).transpose(0, 2, 1))
    d["lamp"] = np.ascontiguousarray(np.broadcast_to(A(P["diff_lambda"])[:, None], (L, 128, 4, 32)))
    li = np.array([[0.8 - 0.6 * math.exp(-0.3 * l), 1.0 - (0.8 - 0.6 * math.exp(-0.3 * l))] for l in range(L)], f32)
    d["lami"] = np.ascontiguousarray(np.broadcast_to(li[:, None, :], (L, 128, 2)))
    d["fbias"] = np.ascontiguousarray(np.broadcast_to(A(P["fox_f_bias"])[:, :, None, None], (L, 4, 128, 1)))
    d["b1l"] = np.ascontiguousarray(A(P["nsa_phi_b1"]).reshape(L, 2, 2, 128).transpose(0, 3, 1, 2).reshape(L, 128, 4))
    pe = A(P["nsa_cmp_pe"])
    d["peT"] = np.ascontiguousarray(pe.transpose(0, 1, 3, 2).reshape(L, 128, 32))
    d["b2c"] = np.ascontiguousarray(A(P["nsa_phi_b2"])[:, 0, :, None])
    d["kgain"] = np.ascontiguousarray(A(P["nsa_k_gain"])[:, :, None])
    return d


B_, T_, L_ = 2, 8192, 2
_PROG = {}


def kernel(**inputs):
    x = np.ascontiguousarray(np.asarray(inputs["x"], dtype=np.float32))
    if "fused" not in _PROG:
        _PROG["fused"] = build_fused(T_, L_)
        _PROG["consts"] = fused_consts(T_)
    nc = _PROG["fused"]
    par = fused_params(inputs, L_)
    in_maps = []
    for b in range(B_):
        d = dict(par)
        d.update(_PROG["consts"])
        d["x"] = x[b]
        in_maps.append(d)
    res = run_bass_kernel_spmd(nc, in_maps, core_ids=list(range(B_)))
    return np.stack([np.asarray(res.results[b]["y"], dtype=np.float32) for b in range(B_)], 0)
```

```python
import numpy as np
import concourse.bass as bass
import concourse.mybir as mybir
from concourse.bass_utils import run_bass_kernel_spmd

F32 = mybir.dt.float32
BF16 = mybir.dt.bfloat16
AF = mybir.ActivationFunctionType
ALU = mybir.AluOpType
AX = mybir.AxisListType

ENGS = ("pe", "act", "dve", "pool", "sp")


class Buf:
    __slots__ = ("name", "t", "w", "r", "dsem", "dcnt", "uid")
    _n = 0

    def __init__(self, name, t):
        Buf._n += 1
        self.uid = Buf._n
        self.name = name
        self.t = t
        self.w = None
        self.r = []
        self.dsem = None
        self.dcnt = 0

    def __getitem__(self, idx):
        return self.t[idx]


class S:
    def __init__(self, nc, same_engine_sync=True):
        self.nc = nc
        self.e = {"pe": nc.tensor, "act": nc.scalar, "dve": nc.vector, "pool": nc.gpsimd, "sp": nc.sync}
        self.sem = {k: nc.alloc_semaphore("c_" + k) for k in ENGS}
        self.cnt = {k: 0 for k in ENGS}
        self.seen = {k: {} for k in ENGS}
        self.same = same_engine_sync
        self.nbuf = 0
        self.dma_sems = []
        self.ctx = []
        self.cbufs = []
        self.free_dsems = []

    def sb(self, name, shape, dt):
        self.nbuf += 1
        g = self.nc.sbuf_tensor("%s_%d" % (name, self.nbuf), list(shape), dt)
        t = g.__enter__()
        self.ctx.append(g)
        b = Buf(name, t)
        self.cbufs.append(b)
        return b

    def ps(self, name, shape, dt):
        self.nbuf += 1
        g = self.nc.psum_tensor("%s_%d" % (name, self.nbuf), list(shape), dt)
        t = g.__enter__()
        self.ctx.append(g)
        b = Buf(name, t)
        self.cbufs.append(b)
        return b

    def sub(self, name, ap):
        return Buf(name, ap)

    def mark(self):
        return len(self.ctx)

    def release(self, m):
        self.barrier()
        while len(self.ctx) > m:
            self.ctx.pop().__exit__(None, None, None)
            b = self.cbufs.pop()
            if b.dsem is not None:
                self.free_dsems.append((b.dsem, b.dcnt))
                self.dma_sems.remove(b)
                b.dsem = None

    def close(self):
        for g in reversed(self.ctx):
            g.__exit__(None, None, None)
        self.ctx = []
        self.cbufs = []

    def _need(self, E, deps):
        need = {}
        for d in deps:
            if d is None:
                continue
            if d[0] == "dma":
                b = d[1]
                key = ("dma", b.uid)
                need[key] = (b, b.dcnt)
            else:
                F, c = d
                if F == E and (not self.same or E == "pe" or c > self.cnt[E]):
                    continue
                if c > need.get(F, (None, 0))[1]:
                    need[F] = (None, c)
        for key, (b, c) in need.items():
            if self.seen[E].get(key, 0) >= c:
                continue
            self.seen[E][key] = c
            if b is not None:
                self.e[E].wait_ge(b.dsem, c)
            else:
                self.e[E].wait_ge(self.sem[key], c)

    def op(self, E, fn, reads=(), writes=(), inc=True):
        deps = []
        for b in reads:
            deps.append(b.w)
        for b in writes:
            deps.append(b.w)
            deps.extend(b.r)
        self._need(E, deps)
        ins = fn()
        c = self.cnt[E] + 1
        if inc:
            ins.then_inc(self.sem[E], 1)
            self.cnt[E] = c
        for b in writes:
            b.w = (E, c)
            b.r = []
        for b in reads:
            if b not in writes:
                b.r = [x for x in b.r if x[0] != E] + [(E, c)]
        return ins

    def dma(self, Q, out, in_, reads=(), writes=(), **kw):
        deps = []
        for b in reads:
            deps.append(b.w)
        for b in writes:
            deps.append(b.w)
            deps.extend(b.r)
        self._need(Q, deps)
        owner = (list(writes) + list(reads))[0]
        if owner.dsem is None:
            owner.dsem = self._dsem(owner)
            self.dma_sems.append(owner)
        ins = self.e[Q].dma_start(out=out, in_=in_, **kw)
        ins.then_inc(owner.dsem, 16)
        owner.dcnt += 16
        rec = ("dma", owner, owner.dcnt)
        for b in writes:
            b.w = rec
            b.r = []
        for b in reads:
            if b not in writes:
                b.r = b.r + [rec]
        return ins

    def cc(self, kind, groups, in_ap, out_ap, reads=(), writes=()):
        deps = []
        for b in reads:
            deps.append(b.w)
        for b in writes:
            deps.append(b.w)
            deps.extend(b.r)
        self._need("pool", deps)
        owner = list(writes)[0]
        if owner.dsem is None:
            owner.dsem = self._dsem(owner)
            self.dma_sems.append(owner)
        ins = self.nc.gpsimd.collective_compute(kind, op=ALU.bypass, replica_groups=groups, ins=[in_ap], outs=[out_ap])
        ins.then_inc(owner.dsem, 16)
        owner.dcnt += 16
        rec = ("dma", owner, owner.dcnt)
        for b in writes:
            b.w = rec
            b.r = []
        for b in reads:
            if b not in writes:
                b.r = b.r + [rec]
        return ins

    def _dsem(self, owner):
        if self.free_dsems:
            sem, cnt = self.free_dsems.pop()
            owner.dcnt = cnt
            return sem
        self.nsem = getattr(self, "nsem", 0) + 1
        return self.nc.alloc_semaphore("d_%d" % self.nsem)

    def barrier(self):
        for E in ENGS:
            for Fk in ENGS:
                if Fk == E:
                    continue
                c = self.cnt[Fk]
                if c and self.seen[E].get(Fk, 0) < c:
                    self.seen[E][Fk] = c
                    self.e[E].wait_ge(self.sem[Fk], c)
            for b in self.dma_sems:
                key = ("dma", b.uid)
                if b.dcnt and self.seen[E].get(key, 0) < b.dcnt:
                    self.seen[E][key] = b.dcnt
                    self.e[E].wait_ge(b.dsem, b.dcnt)

    def finish(self):
        self.barrier()


D = 1024
DFF = 2816
NFC = DFF // 128
NKC = D // 128
EPS = 1e-6


def dram_in(nc, name, shape, dt=F32):
    return nc.dram_tensor(name, list(shape), dt, kind="ExternalInput").ap()


def dram_out(nc, name, shape, dt=F32):
    return nc.dram_tensor(name, list(shape), dt, kind="ExternalOutput").ap()


class RR:
    def __init__(self, items):
        self.items = items
        self.i = 0

    def next(self):
        b = self.items[self.i % len(self.items)]
        self.i += 1
        return b


def emit_norm(s, epsc, xt, hb_rr, scr, stat_rr, ntile):
    nc = s.nc
    st = stat_rr.next()
    hbs = [hb_rr.next() for _ in range(ntile)]
    for j in range(ntile):
        s.op("act", lambda: nc.scalar.activation(out=hbs[j][:, :], in_=xt[j][:, :], func=AF.Square, scale=1.0 / 32.0,
                                                 accum_out=st[:, j:j + 1]),
             reads=[xt[j]], writes=[hbs[j], st])
    s.op("act", lambda: nc.scalar.activation(out=st[:, 4:4 + ntile], in_=st[:, 0:ntile], func=AF.Ln, bias=epsc[:, 0:1], scale=1.0),
         reads=[st, epsc], writes=[st])
    s.op("act", lambda: nc.scalar.activation(out=st[:, 8:8 + ntile], in_=st[:, 4:4 + ntile], func=AF.Exp, scale=-0.5),
         reads=[st], writes=[st])
    for j in range(ntile):
        s.op("act", lambda: nc.scalar.activation(out=hbs[j][:, :], in_=xt[j][:, :], func=AF.Copy, scale=st[:, 8 + j:9 + j]),
             reads=[xt[j], st], writes=[hbs[j]])
    return hbs


def emit_transpose_T(s, hbs, gcol, hT, ident_b, pT_rr, ntile, evac_engs=("dve", "act")):
    nc = s.nc
    k = 0
    for c in range(NKC):
        pT = pT_rr.next()
        for j in range(ntile):
            s.op("pe", lambda: nc.tensor.transpose(out=pT[:, j * 128:(j + 1) * 128], in_=hbs[j][:, c * 128:(c + 1) * 128],
                                                   identity=ident_b[:, :]),
                 reads=[hbs[j], ident_b], writes=[pT], inc=(j == ntile - 1))
        eng = evac_engs[k % len(evac_engs)]
        k += 1
        if eng == "dve":
            s.op("dve", lambda: nc.vector.tensor_scalar(out=hT[:, c, 0:ntile * 128], in0=pT[:, 0:ntile * 128],
                                                        scalar1=gcol[:, c:c + 1], scalar2=None, op0=ALU.mult),
                 reads=[pT, gcol], writes=[hT])
        else:
            s.op("act", lambda: nc.scalar.activation(out=hT[:, c, 0:ntile * 128], in_=pT[:, 0:ntile * 128],
                                                     func=AF.Copy, scale=gcol[:, c:c + 1]),
                 reads=[pT, gcol], writes=[hT])


def emit_rmsnorm_T(s, epsc, xt, gcol, hT, ident_b, pT_rr, hb_rr, scr, stat_rr, ntile, evac_engs=("dve", "act")):
    hbs = emit_norm(s, epsc, xt, hb_rr, scr, stat_rr, ntile)
    emit_transpose_T(s, hbs, gcol, hT, ident_b, pT_rr, ntile, evac_engs)


def build_ffn(NT, TB=512):
    nc = bass.Bass("TRN2", target_bir_lowering=False)
    x = dram_in(nc, "x", [NT, D])
    g = dram_in(nc, "g", [D])
    w_in = dram_in(nc, "w_in", [D, 2 * DFF])
    w_out = dram_in(nc, "w_out", [DFF, D])
    ident = dram_in(nc, "ident", [128, 128], BF16)
    y = dram_out(nc, "y", [NT, D])
    s = S(nc)
    emit_ffn(s, x, g, w_in, w_out, ident, y, NT, TB)
    s.finish()
    s.close()
    return nc


def emit_ffn(s, x, g, w_in, w_out, ident, y, NT, TB=512):
    nc = s.nc
    ntile = TB // 128
    m_ = s.mark()
    ident_b = s.sb("ident_b", [128, 128], BF16)
    s.dma("sp", ident_b[:, :], ident[:, :], writes=[ident_b])
    gcol = s.sb("gcol", [128, NKC], F32)
    epsc = s.sb("epsc", [128, 1], F32)
    s.op("dve", lambda: nc.vector.memset(epsc[:, :], EPS), writes=[epsc])
    s.dma("sp", gcol[:, :], g.rearrange("(c p) -> p c", p=128), writes=[gcol], allow_slow_non_contiguous=True)
    win_b = [s.sb("win_b%d" % c, [128, 2 * DFF], BF16) for c in range(NKC)]
    wout_b = [s.sb("wout_b%d" % f, [128, D], BF16) for f in range(NFC)]
    for c in range(NKC):
        for hf in range(2):
            s.dma("pool", win_b[c][:, hf * DFF:(hf + 1) * DFF], w_in[c * 128:(c + 1) * 128, hf * DFF:(hf + 1) * DFF],
                  writes=[win_b[c]])
    for f in range(NFC):
        s.dma("pool", wout_b[f][:, :], w_out[f * 128:(f + 1) * 128, :], writes=[wout_b[f]])
    xn = [s.sb("xn%d" % j, [128, D], F32) for j in range(ntile)]
    xr_rr = RR([s.sb("xr%d" % j, [128, D], F32) for j in range(1)])
    hb_rr = RR([s.sb("hb%d" % j, [128, D], BF16) for j in range(2 * ntile)])
    stat_rr = RR([s.sb("st%d" % j, [128, 16], F32) for j in range(4)])
    hT = s.sb("hT", [128, NKC, TB], BF16)
    pT_rr = RR([s.ps("pT%d" % j, [128, TB], BF16) for j in range(2)])
    pa_rr = RR([s.ps("pa%d" % j, [128, TB], F32) for j in range(2)])
    pb_rr = RR([s.ps("pb%d" % j, [128, TB], F32) for j in range(2)])
    po_rr = RR([s.ps("po%d" % j, [128, 512], F32) for j in range(2)])
    sa_rr = RR([s.sb("sa%d" % j, [128, TB], F32) for j in range(2)])
    act = [s.sb("actT%d" % f, [128, TB], BF16) for f in range(NFC)]
    nblk = NT // TB

    def prep_a_rot(tb):
        for j in range(ntile):
            r0 = tb * TB + j * 128
            s.dma("sp", xn[j][:, :], x[r0:r0 + 128, :], writes=[xn[j]])
        return emit_norm(s, epsc, xn, hb_rr, None, stat_rr, ntile)

    hbs_next = prep_a_rot(0)
    emit_transpose_T(s, hbs_next, gcol, hT, ident_b, pT_rr, ntile)
    for tb in range(nblk):
        if tb + 1 < nblk:
            hbs_next = prep_a_rot(tb + 1)
        for f in range(NFC):
            pa = pa_rr.next()
            pb = pb_rr.next()
            for c in range(NKC):
                s.op("pe", lambda: nc.tensor.matmul(pa[:, :], lhsT=win_b[c][:, f * 128:(f + 1) * 128], rhs=hT[:, c, :],
                                                    start=(c == 0), stop=(c == NKC - 1)),
                     reads=[win_b[c], hT], writes=[pa], inc=(c == NKC - 1))
            for c in range(NKC):
                s.op("pe", lambda: nc.tensor.matmul(pb[:, :], lhsT=win_b[c][:, DFF + f * 128:DFF + (f + 1) * 128],
                                                    rhs=hT[:, c, :], start=(c == 0), stop=(c == NKC - 1)),
                     reads=[win_b[c], hT], writes=[pb], inc=(c == NKC - 1))
            sa = sa_rr.next()
            s.op("act", lambda: nc.scalar.activation(out=sa[:, :], in_=pa[:, :], func=AF.Silu), reads=[pa], writes=[sa])
            s.op("dve", lambda: nc.vector.tensor_tensor(out=act[f][:, :], in0=sa[:, :], in1=pb[:, :], op=ALU.mult),
                 reads=[sa, pb], writes=[act[f]])
        if tb + 1 < nblk:
            emit_transpose_T(s, hbs_next, gcol, hT, ident_b, pT_rr, ntile)
        for j in range(ntile):
            r0 = tb * TB + j * 128
            xr = xr_rr.next()
            s.dma("sp", xr[:, :], x[r0:r0 + 128, :], writes=[xr])
            for hf in range(2):
                po = po_rr.next()
                for f in range(NFC):
                    s.op("pe", lambda: nc.tensor.matmul(po[:, :], lhsT=act[f][:, j * 128:(j + 1) * 128],
                                                        rhs=wout_b[f][:, hf * 512:(hf + 1) * 512],
                                                        start=(f == 0), stop=(f == NFC - 1)),
                         reads=[act[f], wout_b[f]], writes=[po], inc=(f == NFC - 1))
                s.op("dve", lambda: nc.vector.scalar_tensor_tensor(out=xr[:, hf * 512:(hf + 1) * 512], in0=po[:, :],
                                                                   scalar=0.5, in1=xr[:, hf * 512:(hf + 1) * 512],
                                                                   op0=ALU.mult, op1=ALU.add),
                     reads=[po, xr], writes=[xr])
            s.dma("sp", y[r0:r0 + 128, :], xr[:, :], reads=[xr])
    s.release(m_)


FM_SRC = [[(0, 128)], [(128, 128)], [(256, 128)], [(384, 128)],
          [(768, 128)], [(896, 128)], [(1024, 128)], [(1152, 128)],
          [(1540, 128)], [(1668, 128)], [(1924, 64), (2052, 64)], [(1796, 128)]]
FM_GCOL = [0, 0, 1, 1, 2, 2, 3, 3, 4, 4, 5, None]
FM_BLK = [0, 0, 0, 0, 1, 1, 1, 1, 1, 1, 1, None]
TM_SRC = [[(512, 256), (1280, 256)],
          [(1536, 4), (2180, 12), (1988, 64), (2116, 64)],
          [(2192, 512)]]
NFM = 12
GELU_C = 1.5957691216057308


def build_proj(NT, TB=512):
    nc = bass.Bass("TRN2", target_bir_lowering=False)
    a = dict(
        x=dram_in(nc, "x", [NT, D]), g=dram_in(nc, "g", [D]), w_in=dram_in(nc, "w_in", [D, 6800]),
        ident=dram_in(nc, "ident", [128, 128], BF16), gains=dram_in(nc, "gains", [128, 6]),
        blk=dram_in(nc, "blk", [2, 128, 128], BF16), vgain=dram_in(nc, "vgain", [128, 256]),
        wsT=dram_in(nc, "wsT", [4, 128, 128]), triu=dram_in(nc, "triu", [128, 128]), bsT=dram_in(nc, "bsT", [128, 4]),
        zfm=dram_out(nc, "zfm", [NFM, 128, NT], BF16), vab=dram_out(nc, "vab", [NT, 512], BF16),
        vsw=dram_out(nc, "vsw", [NT, 128], BF16), misc=dram_out(nc, "misc", [NT, 16]), od=dram_out(nc, "od", [NT, 256], BF16))
    s = S(nc)
    emit_proj(s, a, NT, TB)
    s.finish()
    s.close()
    return nc


def emit_proj(s, a, NT, TB=512):
    nc = s.nc
    x, g, w_in, ident, gains, blk, vgain, wsT, triu, bsT = (a[k] for k in
                                                            ("x", "g", "w_in", "ident", "gains", "blk", "vgain", "wsT", "triu", "bsT"))
    zfm, vab, vsw, misc, od = (a[k] for k in ("zfm", "vab", "vsw", "misc", "od"))
    m_ = s.mark()
    ntile = TB // 128
    ident_b = s.sb("ident_b", [128, 128], BF16)
    s.dma("sp", ident_b[:, :], ident[:, :], writes=[ident_b])
    gcol = s.sb("gcol", [128, NKC], F32)
    s.dma("sp", gcol[:, :], g.rearrange("(c p) -> p c", p=128), writes=[gcol], allow_slow_non_contiguous=True)
    epsc = s.sb("epsc", [128, 1], F32)
    s.op("dve", lambda: nc.vector.memset(epsc[:, :], EPS), writes=[epsc])
    gn = s.sb("gn", [128, 6], F32)
    s.dma("sp", gn[:, :], gains[:, :], writes=[gn])
    for col, sc in ((0, 32.0 ** -0.5), (2, 0.125), (4, 0.125)):
        s.op("dve", lambda: nc.vector.tensor_scalar(out=gn[:, col:col + 1], in0=gn[:, col:col + 1], scalar1=sc,
                                                    scalar2=None, op0=ALU.mult), reads=[gn], writes=[gn])
    blk_b = [s.sb("blk%d" % i, [128, 128], BF16) for i in range(2)]
    for i in range(2):
        s.dma("sp", blk_b[i][:, :], blk[i], writes=[blk_b[i]])
    vg = s.sb("vg", [128, 256], F32)
    s.dma("sp", vg[:, :], vgain[:, :], writes=[vg])
    bcol = s.sb("bcol", [128, 4], F32)
    s.dma("sp", bcol[:, :], bsT[:, :], writes=[bcol])
    tri = s.sb("tri", [128, 128], F32)
    s.dma("sp", tri[:, :], triu[:, :], writes=[tri])
    wm = []
    wtmp = s.sb("wtmp", [128, 128], F32)
    for gi in range(4):
        w = s.sb("wm%d" % gi, [128, 128], BF16)
        s.dma("sp", wtmp[:, :], wsT[gi], writes=[wtmp])
        s.op("dve", lambda: nc.vector.tensor_tensor(out=w[:, :], in0=wtmp[:, :], in1=tri[:, :], op=ALU.mult),
             reads=[wtmp, tri], writes=[w])
        wm.append(w)
    wfm = [s.sb("wfm%d" % c, [128, NFM * 128], BF16) for c in range(NKC)]
    wtm = [s.sb("wtm%d" % c, [128, 1168], BF16) for c in range(NKC)]
    FM_RUNS = [(0, 0, 512), (512, 768, 512), (1024, 1540, 256), (1280, 1924, 64), (1344, 2052, 64), (1408, 1796, 128)]
    TM_RUNS = [(0, 512, 256), (256, 1280, 256), (512, 1536, 4), (516, 2180, 12), (528, 1988, 64), (592, 2116, 64), (656, 2192, 512)]
    for c in range(NKC):
        for (o, c0, n) in FM_RUNS:
            s.dma("pool", wfm[c][:, o:o + n], w_in[c * 128:(c + 1) * 128, c0:c0 + n], writes=[wfm[c]])
        for (o, c0, n) in TM_RUNS:
            s.dma("pool", wtm[c][:, o:o + n], w_in[c * 128:(c + 1) * 128, c0:c0 + n], writes=[wtm[c]])
    xts = [[s.sb("xt%d_%d" % (k, j), [128, D], F32) for j in range(ntile)] for k in range(2)]
    hb_rr = RR([s.sb("hb%d" % j, [128, D], BF16) for j in range(2 * ntile)])
    scr = s.sb("scr", [128, D], BF16)
    stat_rr = RR([s.sb("st%d" % j, [128, 16], F32) for j in range(4)])
    hTs = [s.sb("hT%d" % k, [128, NKC, TB], BF16) for k in range(2)]
    pT_rr = RR([s.ps("pT%d" % j, [128, TB], BF16) for j in range(2)])
    pz_rr = RR([s.ps("pz%d" % j, [128, 512], F32) for j in range(2)])
    ptm_rr = RR([s.ps("ptm%d" % j, [128, 512], F32) for j in range(2)])
    pq_rr = RR([s.ps("pq%d" % j, [128, 512], F32) for j in range(2)])
    sq_rr = RR([s.sb("sq%d" % j, [128, TB], BF16) for j in range(4)])
    rs_rr = RR([s.sb("rs%d" % j, [128, TB], F32) for j in range(2)])
    zo_rr = RR([s.sb("zo%d" % j, [128, TB], BF16) for j in range(4)])
    vab_rr = RR([s.sb("vabt%d" % j, [128, 512], BF16) for j in range(2)])
    vsw_rr = RR([s.sb("vswt%d" % j, [128, 128], BF16) for j in range(2)])
    msc_rr = RR([s.sb("msct%d" % j, [128, 16], F32) for j in range(2)])
    f_rr = RR([s.sb("gf%d" % j, [128, 512], F32) for j in range(4)])
    ge_rr = RR([s.sb("ge%d" % j, [128, 512], F32) for j in range(3)])
    zs_rr = RR([s.sb("zs%d" % j, [128, 512], F32) for j in range(2)])
    zf_rr = RR([s.sb("zf%d" % j, [128, TB], F32) for j in range(3)])
    vn_rr = RR([s.sb("vn%d" % j, [128, 256], BF16) for j in range(3)])
    od_rr = RR([s.sb("odt%d" % j, [128, 256], BF16) for j in range(2)])
    nblk = NT // TB

    def prep(tb):
        xt = xts[tb % 2]
        for j in range(ntile):
            s.dma("sp", xt[j][:, :], x[tb * TB + j * 128:tb * TB + (j + 1) * 128, :], writes=[xt[j]])
        emit_rmsnorm_T(s, epsc, xt, gcol, hTs[tb % 2], ident_b, pT_rr, hb_rr, scr, stat_rr, ntile)

    prep(0)
    pipe = Pipe(2)
    for tb in range(nblk):
        t0 = tb * TB
        hT = hTs[tb % 2]
        for i in range(NFM):
            pz = pz_rr.next()
            for c in range(NKC):
                s.op("pe", lambda: nc.tensor.matmul(pz[:, 0:TB], lhsT=wfm[c][:, i * 128:(i + 1) * 128], rhs=hT[:, c, :],
                                                    start=(c == 0), stop=(c == NKC - 1)),
                     reads=[wfm[c], hT], writes=[pz], inc=(c == NKC - 1))
            zo = zo_rr.next()
            if FM_GCOL[i] is None:
                s.op("dve", lambda: nc.vector.tensor_copy(out=zo[:, :], in_=pz[:, 0:TB]), reads=[pz], writes=[zo])
                s.dma("sp", zfm[i, :, t0:t0 + TB], zo[:, :], reads=[zo])
            else:
                zf = zf_rr.next()
                s.op("dve", lambda: nc.vector.tensor_copy(out=zf[:, :], in_=pz[:, 0:TB]), reads=[pz], writes=[zf])
                sq = sq_rr.next()
                s.op("act", lambda: nc.scalar.activation(out=sq[:, :], in_=zf[:, :], func=AF.Square),
                     reads=[zf], writes=[sq])

                def back(i=i, zf=zf, sq=sq, zo=zo, t0=t0):
                    gs = 32.0 if FM_BLK[i] == 0 else 64.0
                    pq = pq_rr.next()
                    s.op("pe", lambda: nc.tensor.matmul(pq[:, 0:TB], lhsT=blk_b[FM_BLK[i]][:, :], rhs=sq[:, :],
                                                        start=True, stop=True), reads=[blk_b[FM_BLK[i]], sq], writes=[pq])
                    rs = rs_rr.next()
                    s.op("act", lambda: nc.scalar.activation(out=rs[:, :], in_=pq[:, 0:TB], func=AF.Ln, bias=epsc[:, 0:1],
                                                             scale=1.0 / gs), reads=[pq, epsc], writes=[rs])
                    s.op("act", lambda: nc.scalar.activation(out=rs[:, :], in_=rs[:, :], func=AF.Exp, scale=-0.5),
                         reads=[rs], writes=[rs])
                    gc = FM_GCOL[i]
                    s.op("dve", lambda: nc.vector.scalar_tensor_tensor(out=zo[:, :], in0=zf[:, :], scalar=gn[:, gc:gc + 1],
                                                                       in1=rs[:, :], op0=ALU.mult, op1=ALU.mult),
                         reads=[zf, gn, rs], writes=[zo])
                    s.dma("sp", zfm[i, :, t0:t0 + TB], zo[:, :], reads=[zo])
                pipe.push(back)
        if tb + 1 < nblk:
            prep(tb + 1)
        for j in range(ntile):
            r0 = t0 + j * 128
            pz = ptm_rr.next()
            for c in range(NKC):
                s.op("pe", lambda: nc.tensor.matmul(pz[:, :], lhsT=hT[:, c, j * 128:(j + 1) * 128], rhs=wtm[c][:, 0:512],
                                                    start=(c == 0), stop=(c == NKC - 1)),
                     reads=[wtm[c], hT], writes=[pz], inc=(c == NKC - 1))
            vt = vab_rr.next()
            s.op("act", lambda: nc.scalar.copy(out=vt[:, :], in_=pz[:, :]), reads=[pz], writes=[vt])
            s.dma("sp", vab[r0:r0 + 128, :], vt[:, :], reads=[vt])
            pz = ptm_rr.next()
            for c in range(NKC):
                s.op("pe", lambda: nc.tensor.matmul(pz[:, 0:144], lhsT=hT[:, c, j * 128:(j + 1) * 128], rhs=wtm[c][:, 512:656],
                                                    start=(c == 0), stop=(c == NKC - 1)),
                     reads=[wtm[c], hT], writes=[pz], inc=(c == NKC - 1))
            mt = msc_rr.next()
            vs_ = vsw_rr.next()
            s.op("dve", lambda: nc.vector.tensor_copy(out=mt[:, :], in_=pz[:, 0:16]), reads=[pz], writes=[mt])
            s.op("dve", lambda: nc.vector.tensor_copy(out=vs_[:, :], in_=pz[:, 16:144]), reads=[pz], writes=[vs_])
            s.dma("sp", misc[r0:r0 + 128, :], mt[:, :], reads=[mt])
            s.dma("sp", vsw[r0:r0 + 128, :], vs_[:, :], reads=[vs_])
            pz = ptm_rr.next()
            for c in range(NKC):
                s.op("pe", lambda: nc.tensor.matmul(pz[:, :], lhsT=hT[:, c, j * 128:(j + 1) * 128], rhs=wtm[c][:, 656:1168],
                                                    start=(c == 0), stop=(c == NKC - 1)),
                     reads=[wtm[c], hT], writes=[pz], inc=(c == NKC - 1))
            zs = zs_rr.next()
            s.op("act", lambda: nc.scalar.copy(out=zs[:, :], in_=pz[:, :]), reads=[pz], writes=[zs])
            z2 = f_rr.next()
            s.op("act", lambda: nc.scalar.activation(out=z2[:, :], in_=zs[:, :], func=AF.Square), reads=[zs], writes=[z2])
            s.op("dve", lambda: nc.vector.tensor_scalar(out=z2[:, :], in0=z2[:, :], scalar1=0.044715, scalar2=1.0,
                                                        op0=ALU.mult, op1=ALU.add), reads=[z2], writes=[z2])
            s.op("dve", lambda: nc.vector.tensor_tensor(out=z2[:, :], in0=z2[:, :], in1=zs[:, :], op=ALU.mult),
                 reads=[z2, zs], writes=[z2])
            s.op("act", lambda: nc.scalar.activation(out=z2[:, :], in_=z2[:, :], func=AF.Exp, scale=-GELU_C),
                 reads=[z2], writes=[z2])
            s.op("act", lambda: nc.scalar.activation(out=z2[:, :], in_=z2[:, :], func=AF.Ln, bias=1.0, scale=1.0),
                 reads=[z2], writes=[z2])
            s.op("act", lambda: nc.scalar.activation(out=z2[:, :], in_=z2[:, :], func=AF.Exp, scale=-1.0),
                 reads=[z2], writes=[z2])
            ge = ge_rr.next()
            s.op("dve", lambda: nc.vector.tensor_tensor(out=ge[:, :], in0=z2[:, :], in1=zs[:, :], op=ALU.mult),
                 reads=[z2, zs], writes=[ge])
            sqv = f_rr.next()
            st = stat_rr.next()
            s.op("act", lambda: nc.scalar.activation(out=sqv[:, 0:256], in_=ge[:, 256:512], func=AF.Square),
                 reads=[ge], writes=[sqv])
            s.op("dve", lambda: nc.vector.tensor_reduce(out=st[:, 0:4], in_=sqv[:, 0:256].rearrange("p (g d) -> p g d", g=4),
                                                        axis=AX.X, op=ALU.add), reads=[sqv], writes=[st])
            s.op("act", lambda: nc.scalar.activation(out=st[:, 0:4], in_=st[:, 0:4], func=AF.Ln, bias=epsc[:, 0:1],
                                                     scale=1.0 / 64.0), reads=[st, epsc], writes=[st])
            s.op("act", lambda: nc.scalar.activation(out=st[:, 0:4], in_=st[:, 0:4], func=AF.Exp, scale=-0.5),
                 reads=[st], writes=[st])
            vn = vn_rr.next()
            for gi in range(4):
                s.op("dve", lambda: nc.vector.scalar_tensor_tensor(
                    out=vn[:, gi * 64:(gi + 1) * 64], in0=ge[:, 256 + gi * 64:256 + (gi + 1) * 64], scalar=st[:, gi:gi + 1],
                    in1=vg[:, gi * 64:(gi + 1) * 64], op0=ALU.mult, op1=ALU.mult), reads=[ge, st, vg], writes=[vn])

            def back2(vn=vn, ge=ge, r0=r0):
                pq = pq_rr.next()
                for gi in range(4):
                    s.op("pe", lambda: nc.tensor.matmul(pq[:, gi * 64:(gi + 1) * 64], lhsT=wm[gi][:, :],
                                                        rhs=vn[:, gi * 64:(gi + 1) * 64], start=True, stop=True),
                         reads=[wm[gi], vn], writes=[pq], inc=(gi == 3))
                ot = od_rr.next()
                for gi in range(4):
                    s.op("dve", lambda: nc.vector.scalar_tensor_tensor(
                        out=ot[:, gi * 64:(gi + 1) * 64], in0=pq[:, gi * 64:(gi + 1) * 64], scalar=bcol[:, gi:gi + 1],
                        in1=ge[:, gi * 64:(gi + 1) * 64], op0=ALU.add, op1=ALU.mult), reads=[pq, bcol, ge], writes=[ot])
                s.dma("sp", od[r0:r0 + 128, :], ot[:, :], reads=[ot])
            pipe.push(back2)
    pipe.flush()
    s.release(m_)


def _bf(a):
    import ml_dtypes
    return np.ascontiguousarray(a).astype(ml_dtypes.bfloat16)


def proj_consts():
    blk = np.zeros((2, 128, 128), np.float32)
    for i in range(128):
        for j in range(128):
            if i // 32 == j // 32:
                blk[0, i, j] = 1
            if i // 64 == j // 64:
                blk[1, i, j] = 1
    triu = np.triu(np.ones((128, 128), np.float32))
    return dict(ident=_bf(np.eye(128, dtype=np.float32)), blk=_bf(blk), triu=triu)


def proj_params(g, w_in, dq, dk, fq, fk, nq, nk, vgain, w_s, b_s):
    gains = np.stack([np.tile(dq, 4), np.tile(dk, 4), np.tile(fq, 2), np.tile(fk, 2), np.tile(nq, 2), np.tile(nk, 2)], 1)
    return dict(g=np.ascontiguousarray(g), w_in=np.ascontiguousarray(w_in), gains=np.ascontiguousarray(gains, dtype=np.float32),
                vgain=np.ascontiguousarray(np.broadcast_to(vgain[None, :], (128, 256))),
                wsT=np.ascontiguousarray(w_s.transpose(0, 2, 1)), bsT=np.ascontiguousarray(b_s.T))


NEG = -30000.0


def load_vt(s, vt, io, key, T, init=True, load=True):
    nc = s.nc
    NT = T // 128
    if init:
        s.op("pool", lambda: nc.gpsimd.memset(vt[:, :, 64:128], 0.0), writes=[vt])
        s.op("pool", lambda: nc.gpsimd.memset(vt[:, :, 64:65], 1.0), writes=[vt])
    if not load:
        return
    if key + "_src" in io:
        src = io[key + "_src"]
        step = 8
        for j0 in range(0, NT, step):
            j1 = min(NT, j0 + step)
            s.dma("sp", vt[:, j0:j1, 0:64], src[j0 * 128:j1 * 128, :].rearrange("(j p) d -> p j d", p=128), writes=[vt])
    else:
        s.dma("sp", vt[:, :, 0:64], io[key][:, :, 0:64], writes=[vt])


class Pipe:
    def __init__(self, lag):
        self.q = []
        self.lag = lag

    def push(self, fn):
        self.q.append(fn)
        while len(self.q) > self.lag:
            self.q.pop(0)()

    def flush(self):
        while self.q:
            self.q.pop(0)()


def emit_attn_phase(s, cm, T, nsub, qT, kT, vt, kparts, out_dram, finalize, bias_fn=None, name="a", lag=2):
    nc = s.nc
    NQB = T // 512
    pipe = Pipe(lag)
    fin_pending = None
    for qb in range(NQB):
        q0 = qb * 512
        pos = [cm["po_rr"].next() for _ in range(nsub)]
        nt = 4 * qb + 4
        njob = 0
        for t in range(nt):
            di = t - 4 * qb
            c0 = 128 * di if di > 0 else 0
            for i in range(nsub):
                kz = kT[i]
                ps = cm["ps_rr"].next()
                s.op("pe", lambda: nc.tensor.matmul(ps[:, c0:512], lhsT=kz[:, t * 128:(t + 1) * 128],
                                                    rhs=qT[:, q0 + c0:q0 + 512], start=True, stop=(di < 0)),
                     reads=[kz, qT], writes=[ps], inc=(di < 0))
                if di >= 0:
                    s.op("pe", lambda: nc.tensor.matmul(ps[:, c0:c0 + 128], lhsT=cm["ident_b"][:, :], rhs=cm["tri_b"][:, :],
                                                        start=False, stop=True),
                         reads=[cm["ident_b"], cm["tri_b"]], writes=[ps])
                pt = cm["pt_rr"].next()
                if bias_fn is None:
                    s.op("act", lambda: nc.scalar.activation(out=pt[:, c0:512], in_=ps[:, c0:512], func=AF.Exp),
                         reads=[ps], writes=[pt])
                else:
                    bb, bap = bias_fn(qb, t)
                    s.op("act", lambda: nc.scalar.activation(out=pt[:, c0:512], in_=ps[:, c0:512], func=AF.Exp, bias=bap),
                         reads=[ps, bb], writes=[pt])

                def pv(po=pos[i], t=t, c0=c0, pt=pt, nt=nt):
                    s.op("pe", lambda: nc.tensor.matmul(po[:, c0:512], lhsT=vt[:, t, :], rhs=pt[:, c0:512],
                                                        start=(t == 0), stop=(t == nt - 1)),
                         reads=[vt, pt], writes=[po])
                pipe.push(pv)
                njob += 1
                if fin_pending is not None and njob == lag:
                    fin_pending()
                    fin_pending = None
        pipe.flush()
        if fin_pending is not None:
            fin_pending()
        fin_pending = (lambda qb=qb, pos=pos: finalize(qb, pos))
    if fin_pending is not None:
        fin_pending()


def emit_o_to_tokmajor(s, cm, po, pf, col0):
    nc = s.nc
    oc = cm["oc_rr"].next()
    s.op("act", lambda: nc.scalar.copy(out=oc[0:65, :], in_=po[0:65, :]), reads=[po], writes=[oc])
    for j in range(4):
        s.op("pe", lambda: nc.tensor.transpose(out=pf[:, j, col0:col0 + 65], in_=oc[0:65, j * 128:(j + 1) * 128],
                                               identity=cm["ident_f"][0:65, 0:65]),
             reads=[oc, cm["ident_f"]], writes=[pf], inc=(j == 3))


def build_mix_ab(T):
    nc = bass.Bass("TRN2", target_bir_lowering=False)
    io = mix_decl(nc, T, with_c=False)
    s = S(nc)
    cm = mix_common(s, io)
    emit_mix_a(s, cm, io, T)
    emit_mix_b(s, cm, io, T)
    s.finish()
    s.close()
    return nc


def mix_decl(nc, T, with_c=True):
    NT = T // 128
    io = dict(
        identb=dram_in(nc, "identb", [128, 128], BF16), identf=dram_in(nc, "identf", [128, 128]),
        trib=dram_in(nc, "trib", [128, 128], BF16),
        qa=dram_in(nc, "qa", [64, T], BF16), ka=dram_in(nc, "ka", [64, T], BF16), va=dram_in(nc, "va", [128, NT, 65], BF16),
        lamp=dram_in(nc, "lamp", [128, 4, 32]), lami=dram_in(nc, "lami", [128, 2]),
        qb=dram_in(nc, "qb", [64, T], BF16), kb=dram_in(nc, "kb", [64, T], BF16), vb=dram_in(nc, "vb", [128, NT, 65], BF16),
        flog=dram_in(nc, "flog", [128, NT]), fbias=dram_in(nc, "fbias", [128, 1]),
        triuf=dram_in(nc, "triuf", [128, 128]), onesf=dram_in(nc, "onesf", [128, 128]),
        oa=dram_out(nc, "oa", [T, 64], BF16), ob=dram_out(nc, "ob", [T, 64], BF16),
    )
    return io


def mix_common(s, io, n_ps=3, with_pf2=True):
    nc = s.nc
    cm = {}
    for nm, key, dt in (("ident_b", "identb", BF16), ("ident_f", "identf", F32), ("tri_b", "trib", BF16)):
        b = s.sb(nm, [128, 128], dt)
        s.dma("sp", b[:, :], io[key][:, :], writes=[b])
        cm[nm] = b
    cm["epsc"] = s.sb("epsc", [128, 1], F32)
    s.op("dve", lambda: nc.vector.memset(cm["epsc"][:, :], EPS), writes=[cm["epsc"]])
    cm["ps_rr"] = RR([s.ps("ps%d" % j, [128, 512], F32) for j in range(n_ps)])
    cm["po_rr"] = RR([s.ps("po%d" % j, [128, 512], F32) for j in range(3)])
    cm["pf"] = s.ps("pf", [128, 4, 128], F32)
    if with_pf2:
        cm["pf2"] = s.ps("pf2", [128, 4, 128], F32)
    cm["lag"] = n_ps - 1
    cm["o1s_rr"] = RR([s.sb("o1s%d" % j, [128, 4, 65], F32) for j in range(2)])
    cm["pt_rr"] = RR([s.sb("pt%d" % j, [128, 512], BF16) for j in range(n_ps + 2)])
    cm["oc_rr"] = RR([s.sb("oc%d" % j, [128, 512], F32) for j in range(2)])
    cm["st_rr"] = RR([s.sb("mst%d" % j, [128, 8], F32) for j in range(8)])
    cm["ot_rr"] = RR([s.sb("ot%d" % j, [128, 4, 64], BF16) for j in range(2)])
    cm["tmp_rr"] = RR([s.sb("tmp%d" % j, [128, 64], F32) for j in range(4)])
    return cm


def emit_mix_a(s, cm, io, T, stage="all", bufs=None):
    nc = s.nc
    NT = T // 128
    if stage in ("all", "alloc"):
        if stage == "all":
            m = s.mark()
        b = dict(qT=s.sb("a_q", [128, T], BF16), k1=s.sb("a_k1", [128, T], BF16), k2=s.sb("a_k2", [128, T], BF16),
                 vt=s.sb("a_v", [128, NT, 128], BF16), lp=s.sb("lp", [128, 4, 32], F32), li=s.sb("li", [128, 2], F32),
                 lw=s.sb("lw", [128, 2, 32], F32), lam=s.sb("lam", [128, 4], F32))
        s.op("pool", lambda: nc.gpsimd.memset(b["qT"][64:128, :], 0.0), writes=[b["qT"]])
        s.op("dve", lambda: nc.vector.memset(b["k1"][:, :], 0.0), writes=[b["k1"]])
        s.op("pool", lambda: nc.gpsimd.memset(b["k2"][:, :], 0.0), writes=[b["k2"]])
        load_vt(s, b["vt"], io, "va", T, init=True, load=False)
        if stage == "alloc":
            return b
        bufs = b
    qT, k1, k2, vt, lp, li, lw, lam = (bufs[k] for k in ("qT", "k1", "k2", "vt", "lp", "li", "lw", "lam"))
    if stage in ("all", "load"):
        s.dma("sp", qT[0:64, :], io["qa"][:, :], writes=[qT])
        s.dma("sp", k1[0:32, :], io["ka"][0:32, :], writes=[k1])
        s.dma("sp", k2[32:64, :], io["ka"][32:64, :], writes=[k2])
        load_vt(s, vt, io, "va", T, init=False)
        s.dma("sp", lp[:, :, :], io["lamp"][:, :, :], writes=[lp])
        s.dma("sp", li[:, :], io["lami"][:, :], writes=[li])
        s.op("dve", lambda: nc.vector.tensor_tensor(out=lw[:, 0, :], in0=lp[:, 0, :], in1=lp[:, 1, :], op=ALU.mult),
             reads=[lp], writes=[lw])
        s.op("dve", lambda: nc.vector.tensor_tensor(out=lw[:, 1, :], in0=lp[:, 2, :], in1=lp[:, 3, :], op=ALU.mult),
             reads=[lp], writes=[lw])
        s.op("dve", lambda: nc.vector.tensor_reduce(out=lam[:, 0:2], in_=lw[:, :, :], axis=AX.X, op=ALU.add),
             reads=[lw], writes=[lam])
        s.op("act", lambda: nc.scalar.activation(out=lam[:, 0:2], in_=lam[:, 0:2], func=AF.Exp), reads=[lam], writes=[lam])
        s.op("dve", lambda: nc.vector.tensor_tensor(out=lam[:, 2:3], in0=lam[:, 1:2], in1=lam[:, 0:1], op=ALU.subtract),
             reads=[lam], writes=[lam])
        s.op("dve", lambda: nc.vector.tensor_tensor(out=lam[:, 3:4], in0=lam[:, 2:3], in1=li[:, 0:1], op=ALU.subtract),
             reads=[lam, li], writes=[lam])
        if stage == "load":
            return

    def fin(qb, pos):
        if "pf2" in cm:
            pf = cm["pf"]
            pf2 = cm["pf2"]
            emit_o_to_tokmajor(s, cm, pos[0], pf, 0)
            emit_o_to_tokmajor(s, cm, pos[1], pf2, 0)
        else:
            pf2 = cm["pf"]
            emit_o_to_tokmajor(s, cm, pos[0], pf2, 0)
            pf = cm["o1s_rr"].next()
            s.op("dve", lambda: nc.vector.tensor_copy(out=pf[:, :, :], in_=pf2[:, :, 0:65]), reads=[pf2], writes=[pf])
            emit_o_to_tokmajor(s, cm, pos[1], pf2, 0)
        ot = cm["ot_rr"].next()
        for j in range(4):
            st = cm["st_rr"].next()
            s.op("dve", lambda: nc.vector.tensor_scalar(out=st[:, 0:1], in0=pf[:, j, 64:65], scalar1=1e-30, scalar2=None,
                                                        op0=ALU.max), reads=[pf], writes=[st])
            s.op("dve", lambda: nc.vector.tensor_scalar(out=st[:, 1:2], in0=pf2[:, j, 64:65], scalar1=1e-30, scalar2=None,
                                                        op0=ALU.max), reads=[pf2], writes=[st])
            s.op("dve", lambda: nc.vector.reciprocal(out=st[:, 0:2], in_=st[:, 0:2]), reads=[st], writes=[st])
            t2 = cm["tmp_rr"].next()
            o = cm["tmp_rr"].next()
            s.op("dve", lambda: nc.vector.tensor_scalar(out=t2[:, :], in0=pf2[:, j, 0:64], scalar1=st[:, 1:2],
                                                        scalar2=lam[:, 3:4], op0=ALU.mult, op1=ALU.mult),
                 reads=[pf2, st, lam], writes=[t2])
            s.op("dve", lambda: nc.vector.scalar_tensor_tensor(out=o[:, :], in0=pf[:, j, 0:64], scalar=st[:, 0:1], in1=t2[:, :],
                                                               op0=ALU.mult, op1=ALU.add), reads=[pf, st, t2], writes=[o])
            s.op("act", lambda: nc.scalar.activation(out=t2[:, :], in_=o[:, :], func=AF.Square, accum_out=st[:, 2:3]),
                 reads=[o], writes=[t2, st])
            s.op("act", lambda: nc.scalar.activation(out=st[:, 3:4], in_=st[:, 2:3], func=AF.Ln, bias=cm["epsc"][:, 0:1],
                                                     scale=1.0 / 64.0), reads=[st, cm["epsc"]], writes=[st])
            s.op("act", lambda: nc.scalar.activation(out=st[:, 4:5], in_=st[:, 3:4], func=AF.Exp, scale=-0.5),
                 reads=[st], writes=[st])
            s.op("dve", lambda: nc.vector.tensor_scalar(out=ot[:, j, :], in0=o[:, :], scalar1=st[:, 4:5], scalar2=li[:, 1:2],
                                                        op0=ALU.mult, op1=ALU.mult), reads=[o, st, li], writes=[ot])
        s.dma("sp", io["oa"][qb * 512:(qb + 1) * 512, :].rearrange("(j p) d -> p j d", p=128), ot[:, :, :], reads=[ot])

    emit_attn_phase(s, cm, T, 2, qT, [k1, k2], vt, None, io["oa"], fin, name="a", lag=cm["lag"])
    if stage == "all":
        s.release(m)


def emit_mix_b(s, cm, io, T, stage="all", bufs=None):
    nc = s.nc
    NT = T // 128
    NQB = T // 512
    if stage in ("all", "alloc"):
        if stage == "all":
            m = s.mark()
        b = dict(qT=s.sb("b_q", [128, T], BF16), kT=s.sb("b_k", [128, T], BF16), vt=s.sb("b_v", [128, NT, 128], BF16),
                 fl=s.sb("fl", [128, NT], F32), fb=s.sb("fb", [128, 2], F32), tu=s.sb("tu", [128, 128], F32),
                 on=s.sb("on", [128, 128], F32), cc=s.sb("cc", [128, NT], F32), inc=s.sb("inc", [128, NT], F32),
                 tmpc=s.sb("tmpc", [128, NT], F32), btab=s.sb("btab", [128, NQB, NT], F32))
        s.op("pool", lambda: nc.gpsimd.memset(b["qT"][64:128, :], 0.0), writes=[b["qT"]])
        s.op("dve", lambda: nc.vector.memset(b["kT"][64:128, :], 0.0), writes=[b["kT"]])
        load_vt(s, b["vt"], io, "vb", T, init=True, load=False)
        s.dma("sp", b["tu"][:, :], io["triuf"][:, :], writes=[b["tu"]])
        s.dma("sp", b["on"][:, :], io["onesf"][:, :], writes=[b["on"]])
        if stage == "alloc":
            return b
        bufs = b
    qT, kT, vt, fl, fb, tu, on, cc, inc_, tmpc, btab = (bufs[k] for k in ("qT", "kT", "vt", "fl", "fb", "tu", "on", "cc", "inc",
                                                                            "tmpc", "btab"))
    if stage in ("all", "load"):
        s.dma("sp", qT[0:64, :], io["qb"][:, :], writes=[qT])
        s.dma("sp", kT[0:64, :], io["kb"][:, :], writes=[kT])
        load_vt(s, vt, io, "vb", T, init=False)
        if stage == "load":
            return
    if "flog_sb" not in io:
        s.dma("sp", fl[:, :], io["flog"][:, :], writes=[fl])
    s.dma("sp", fb[:, 0:1], io["fbias"][:, :], writes=[fb])
    s.op("dve", lambda: nc.vector.tensor_scalar(out=fb[:, 1:2], in0=fb[:, 0:1], scalar1=-1.0, scalar2=None, op0=ALU.mult),
         reads=[fb], writes=[fb])
    if "flog_sb" in io:
        fsb, fap = io["flog_sb"]
        s.op("act", lambda: nc.scalar.activation(out=fl[:, :], in_=fap, func=AF.Exp, bias=fb[:, 1:2], scale=-1.0),
             reads=[fsb, fb], writes=[fl])
    else:
        s.op("act", lambda: nc.scalar.activation(out=fl[:, :], in_=fl[:, :], func=AF.Exp, bias=fb[:, 1:2], scale=-1.0),
             reads=[fl, fb], writes=[fl])
    s.op("act", lambda: nc.scalar.activation(out=fl[:, :], in_=fl[:, :], func=AF.Ln, bias=1.0, scale=1.0),
         reads=[fl], writes=[fl])
    pc = cm["pf"]
    pcv = pc[:, 0, :]
    s.op("pe", lambda: nc.tensor.matmul(pc[:, 0, 0:NT], lhsT=tu[:, :], rhs=fl[:, :], start=True, stop=True),
         reads=[tu, fl], writes=[pc])
    s.op("pe", lambda: nc.tensor.matmul(pc[:, 1, 0:NT], lhsT=on[:, :], rhs=fl[:, :], start=True, stop=True),
         reads=[on, fl], writes=[pc])
    s.op("dve", lambda: nc.vector.tensor_copy(out=inc_[:, :], in_=pc[:, 1, 0:NT]), reads=[pc], writes=[inc_])
    sh = 1
    while sh < NT:
        s.op("dve", lambda: nc.vector.tensor_copy(out=tmpc[:, :], in_=inc_[:, :]), reads=[inc_], writes=[tmpc])
        s.op("dve", lambda: nc.vector.tensor_tensor(out=inc_[:, sh:NT], in0=tmpc[:, sh:NT], in1=tmpc[:, 0:NT - sh], op=ALU.add),
             reads=[tmpc], writes=[inc_])
        sh *= 2
    s.op("dve", lambda: nc.vector.tensor_tensor(out=cc[:, :], in0=pc[:, 0, 0:NT], in1=inc_[:, :], op=ALU.add),
         reads=[pc, inc_], writes=[cc])
    s.op("dve", lambda: nc.vector.tensor_tensor(out=tmpc[:, :], in0=cc[:, :], in1=pc[:, 1, 0:NT], op=ALU.subtract),
         reads=[pc, cc], writes=[tmpc])
    for qb in range(NQB):
        s.op("dve", lambda: nc.vector.tensor_scalar(out=btab[:, qb, :], in0=tmpc[:, :], scalar1=inc_[:, 4 * qb + 1:4 * qb + 2],
                                                    scalar2=None, op0=ALU.subtract), reads=[tmpc, inc_], writes=[btab])

    def bias_fn(qb, t):
        return btab, btab[:, qb, t:t + 1]

    def fin(qb, pos):
        pf = cm["pf"]
        emit_o_to_tokmajor(s, cm, pos[0], pf, 0)
        ot = cm["ot_rr"].next()
        for j in range(4):
            st = cm["st_rr"].next()
            s.op("dve", lambda: nc.vector.tensor_scalar(out=st[:, 0:1], in0=pf[:, j, 64:65], scalar1=1e-30, scalar2=None,
                                                        op0=ALU.max), reads=[pf], writes=[st])
            s.op("dve", lambda: nc.vector.reciprocal(out=st[:, 0:1], in_=st[:, 0:1]), reads=[st], writes=[st])
            s.op("dve", lambda: nc.vector.tensor_scalar(out=ot[:, j, :], in0=pf[:, j, 0:64], scalar1=st[:, 0:1], scalar2=None,
                                                        op0=ALU.mult), reads=[pf, st], writes=[ot])
        s.dma("sp", io["ob"][qb * 512:(qb + 1) * 512, :].rearrange("(j p) d -> p j d", p=128), ot[:, :, :], reads=[ot])

    emit_attn_phase(s, cm, T, 1, qT, [kT], vt, None, io["ob"], fin, bias_fn=bias_fn, name="b", lag=cm["lag"])
    if stage == "all":
        s.release(m)


def mix_consts():
    k = np.arange(128)
    tri = np.where(k[:, None] > k[None, :], NEG, 0.0).astype(np.float32)
    return dict(identb=_bf(np.eye(128, dtype=np.float32)), identf=np.eye(128, dtype=np.float32), trib=_bf(tri),
                triuf=np.triu(np.ones((128, 128), np.float32)), onesf=np.ones((128, 128), np.float32))


def mix_decl_c(nc, io, T):
    NT = T // 128
    QL = NT // 4
    NCT = max(1, T // 2048)
    io.update(dict(
        qc=dram_in(nc, "qc", [128, QL, 512], BF16),
        kskw=dram_in(nc, "kskw", [128, T], BF16),
        vs=dram_in(nc, "vs", [128, NT, 65], BF16), vw=dram_in(nc, "vw", [128, NT, 65], BF16),
        kvin=dram_in(nc, "kvin", [128, T], BF16),
        w1=dram_in(nc, "w1", [2, 2048, 256]), b1=dram_in(nc, "b1", [128, 4]),
        peT=dram_in(nc, "peT", [128, 32]),
        w2=dram_in(nc, "w2", [2, 256, 64]), b2=dram_in(nc, "b2", [2, 64]), b2c=dram_in(nc, "b2c", [64, 1]),
        kgain=dram_in(nc, "kgain", [64, 1]),
        ng=dram_in(nc, "ng", [128, QL, 12]),
        cmask=dram_in(nc, "cmask", [128, QL, NCT, 128], BF16),
        smask=dram_in(nc, "smask", [128, 4, 128], BF16), wmask=dram_in(nc, "wmask", [128, 8, 128], BF16),
        impA=dram_in(nc, "impA", [128, QL, 128]), impB=dram_in(nc, "impB", [128, QL, 128]),
        emat=dram_in(nc, "emat", [128, NT, 128], BF16), ovl=dram_in(nc, "ovl", [128, NCT, 128], BF16),
        ones64=dram_in(nc, "ones64", [64, 64], BF16), onesrow=dram_in(nc, "onesrow", [1, 128], BF16),
        oc=dram_out(nc, "oc", [QL * 128, 256], BF16),
    ))
    return io


def emit_gelu(s, zin_ap, zin_b, out_ap, out_b, tmp, shape_sl):
    nc = s.nc
    t = tmp
    s.op("act", lambda: nc.scalar.activation(out=t[shape_sl], in_=zin_ap, func=AF.Square), reads=[zin_b], writes=[t])
    s.op("dve", lambda: nc.vector.tensor_scalar(out=t[shape_sl], in0=t[shape_sl], scalar1=0.044715, scalar2=1.0,
                                                op0=ALU.mult, op1=ALU.add), reads=[t], writes=[t])
    s.op("dve", lambda: nc.vector.tensor_tensor(out=t[shape_sl], in0=t[shape_sl], in1=zin_ap, op=ALU.mult),
         reads=[t, zin_b], writes=[t])
    s.op("act", lambda: nc.scalar.activation(out=t[shape_sl], in_=t[shape_sl], func=AF.Exp, scale=-GELU_C), reads=[t], writes=[t])
    s.op("dve", lambda: nc.vector.tensor_scalar(out=t[shape_sl], in0=t[shape_sl], scalar1=1.0, scalar2=None, op0=ALU.add),
         reads=[t], writes=[t])
    s.op("dve", lambda: nc.vector.reciprocal(out=t[shape_sl], in_=t[shape_sl]), reads=[t], writes=[t])
    s.op("dve", lambda: nc.vector.tensor_tensor(out=out_ap, in0=t[shape_sl], in1=zin_ap, op=ALU.mult),
         reads=[t, zin_b], writes=[out_b])


def emit_mix_c(s, cm, io, T, cs=None):
    fused = cs is not None
    cs = cs if fused else [None]
    nc = s.nc
    NT = T // 128
    QL = NT // 4
    NCT = max(1, T // 2048)
    Nc = T // 16 - 1
    NCP = NCT * 128 if Nc > 128 else 128
    NCW = min(Nc, 511)
    assert Nc <= 511
    m = s.mark()
    ident_b = cm["ident_b"]
    ps_l = cm["ps_rr"].items
    po_l = cm["po_rr"].items
    pf, pf2 = cm["pf"], cm["pf2"]

    def ld(name, shape, dt, src, q="sp"):
        b = s.sb(name, shape, dt)
        idx = tuple(slice(None) for _ in shape)
        s.dma(q, b[idx], src, writes=[b])
        return b

    qc = s.sb("c_q", [128, QL, 512], BF16)
    qc2 = s.sb("c_q2", [128, QL, 512], BF16)
    s.op("pool", lambda: nc.gpsimd.memset(qc[64:128, :, :], 0.0), writes=[qc])
    s.op("dve", lambda: nc.vector.memset(qc2[0:64, :, :], 0.0), writes=[qc2])
    kk = ld("c_kk", [128, T], BF16, io["kskw"][:, :])
    vs = s.sb("c_vs", [128, NT, 128], BF16)
    vw = s.sb("c_vw", [128, NT, 128], BF16)
    load_vt(s, vs, io, "vs", T)
    load_vt(s, vw, io, "vw", T)
    emat = ld("c_e", [128, NT, 128], BF16, io["emat"][:, :, :])
    ovl = ld("c_ovl", [128, NCT, 128], BF16, io["ovl"][:, :, :])
    smask = s.sb("c_sm", [128, 4, 128], BF16)
    wmask = s.sb("c_wm", [128, 8, 128], BF16)
    ngt = s.sb("c_ng", [128, QL, 12], F32)
    ones64 = ld("c_o64", [64, 64], BF16, io["ones64"][:, :])
    onesrow = ld("c_orow", [1, 128], BF16, io["onesrow"][:, :])
    kgain = ld("c_kg", [64, 1], F32, io["kgain"][:, :])
    b2c = ld("c_b2c", [64, 1], F32, io["b2c"][:, :])
    b1 = ld("c_b1", [128, 4], F32, io["b1"][:, :])

    ktc = s.sb("c_ktc", [128, NCP], BF16)
    vc = s.sb("c_vc", [128, NCT, 128], BF16)
    s.op("dve", lambda: nc.vector.memset(ktc[:, :], 0.0), writes=[ktc])
    s.op("dve", lambda: nc.vector.memset(vc[:, :, :], 0.0), writes=[vc])
    s.op("dve", lambda: nc.vector.memset(vc[:, :, 64:65], 1.0), writes=[vc])

    m2 = s.mark()
    kvin = ld("c_kvin", [128, T], BF16, io["kvin"][:, :])
    w1sb = s.sb("c_w1", [128, 32, 256], BF16)
    for x in range(2):
        s.dma("pool", w1sb[x * 64:(x + 1) * 64, :, :], io["w1"][x].rearrange("(j d) f -> d j f", d=64), writes=[w1sb])
    peT = s.sb("c_pe", [128, 32], BF16)
    s.dma("pool", peT[:, :], io["peT"][:, :], writes=[peT])
    w2sb = s.sb("c_w2", [128, 2, 2, 64], BF16)
    for x in range(2):
        s.dma("pool", w2sb[:, x, :, :], io["w2"][x].rearrange("(hh f) d -> f hh d", f=128), writes=[w2sb])
    b2row = s.sb("c_b2r", [1, 64], BF16)
    s.dma("pool", b2row[:, :], io["b2"][1:2, :], writes=[b2row])
    hacc = [ps_l[0], ps_l[1], ps_l[2], po_l[0]]
    pcol = po_l[1]
    for x in range(2):
        for hh in range(2):
            hp = hacc[x * 2 + hh]
            for j in range(32):
                s.op("pe", lambda: nc.tensor.matmul(hp[:, 0:NCW], lhsT=w1sb[x * 64:(x + 1) * 64, j, hh * 128:(hh + 1) * 128],
                                                    rhs=kvin[x * 64:(x + 1) * 64, j:j + 16 * (NCW - 1) + 1:16],
                                                    start=(j == 0), stop=(j == 31)),
                     reads=[w1sb, kvin], writes=[hp], inc=(j == 31))
            for j in range(32):
                s.op("pe", lambda: nc.tensor.matmul(pcol[:, x * 2 + hh:x * 2 + hh + 1],
                                                    lhsT=w1sb[x * 64:(x + 1) * 64, j, hh * 128:(hh + 1) * 128],
                                                    rhs=peT[x * 64:(x + 1) * 64, j:j + 1], start=(j == 0), stop=(j == 31)),
                     reads=[w1sb, peT], writes=[pcol], inc=(j == 31))
    hbias = s.sb("c_hb", [128, 4], F32)
    s.op("dve", lambda: nc.vector.tensor_tensor(out=hbias[:, :], in0=pcol[:, 0:4], in1=b1[:, :], op=ALU.add),
         reads=[pcol, b1], writes=[hbias])
    gh = []
    for x in range(2):
        for hh in range(2):
            k = x * 2 + hh
            z = s.sb("c_z%d" % k, [128, 512], F32)
            tmp = s.sb("c_zt%d" % k, [128, 512], F32)
            gb = s.sb("c_g%d" % k, [128, 512], BF16)
            s.op("act", lambda: nc.scalar.activation(out=z[:, 0:NCW], in_=hacc[k][:, 0:NCW], func=AF.Identity,
                                                     bias=hbias[:, k:k + 1], scale=1.0), reads=[hacc[k], hbias], writes=[z])
            emit_gelu(s, z[:, 0:NCW], z, gb[:, 0:NCW], gb, tmp, (slice(None), slice(0, NCW)))
            gh.append(gb)
    pk = po_l[2]
    for hh in range(2):
        s.op("pe", lambda: nc.tensor.matmul(pk[0:64, 0:NCW], lhsT=w2sb[:, 0, hh, :], rhs=gh[hh][:, 0:NCW],
                                            start=(hh == 0), stop=(hh == 1)), reads=[w2sb, gh[hh]], writes=[pk], inc=(hh == 1))
    kz = s.sb("c_kz", [64, 512], F32)
    ksq = s.sb("c_ksq", [64, 512], BF16)
    krs = s.sb("c_krs", [64, 512], F32)
    s.op("act", lambda: nc.scalar.activation(out=kz[:, 0:NCW], in_=pk[0:64, 0:NCW], func=AF.Identity, bias=b2c[:, 0:1], scale=1.0),
         reads=[pk, b2c], writes=[kz])
    s.op("act", lambda: nc.scalar.activation(out=ksq[:, 0:NCW], in_=kz[:, 0:NCW], func=AF.Square), reads=[kz], writes=[ksq])
    pq = ps_l[0]
    s.op("pe", lambda: nc.tensor.matmul(pq[0:64, 0:NCW], lhsT=ones64[:, :], rhs=ksq[:, 0:NCW], start=True, stop=True),
         reads=[ones64, ksq], writes=[pq])
    s.op("act", lambda: nc.scalar.activation(out=krs[:, 0:NCW], in_=pq[0:64, 0:NCW], func=AF.Ln, bias=cm["epsc"][0:64, 0:1],
                                             scale=1.0 / 64.0), reads=[pq, cm["epsc"]], writes=[krs])
    s.op("act", lambda: nc.scalar.activation(out=krs[:, 0:NCW], in_=krs[:, 0:NCW], func=AF.Exp, scale=-0.5), reads=[krs], writes=[krs])
    s.op("dve", lambda: nc.vector.scalar_tensor_tensor(out=ktc[0:64, 0:NCW], in0=kz[:, 0:NCW], scalar=kgain[:, 0:1], in1=krs[:, 0:NCW],
                                                       op0=ALU.mult, op1=ALU.mult), reads=[kz, kgain, krs], writes=[ktc])
    for nt in range(NCT):
        n0 = nt * 128
        nn = min(128, Nc - n0)
        pv = ps_l[1 + nt % 2]
        for hh in range(2):
            s.op("pe", lambda: nc.tensor.matmul(pv[0:nn, 0:64], lhsT=gh[2 + hh][:, n0:n0 + nn], rhs=w2sb[:, 1, hh, :],
                                                start=(hh == 0), stop=False), reads=[gh[2 + hh], w2sb], writes=[pv], inc=False)
        s.op("pe", lambda: nc.tensor.matmul(pv[0:nn, 0:64], lhsT=onesrow[0:1, 0:nn], rhs=b2row[0:1, :], start=False, stop=True),
             reads=[onesrow, b2row], writes=[pv])
        s.op("act", lambda: nc.scalar.copy(out=vc[0:nn, nt, 0:64], in_=pv[0:nn, 0:64]), reads=[pv], writes=[vc])
    s.release(m2)

    cmk_rr = RR([s.sb("c_cmk%d" % j, [128, NCT, 128], BF16) for j in range(2)])
    ia_rr = RR([s.sb("c_ia%d" % j, [128, 128], F32) for j in range(2)])
    ib_rr = RR([s.sb("c_ib%d" % j, [128, 128], F32) for j in range(2)])
    imp_rr = RR([s.sb("c_imp%d" % j, [128, 128], F32) for j in range(2)])
    imp2_rr = RR([s.sb("c_impb%d" % j, [128, 128], F32) for j in range(2)])
    m8_rr = RR([s.sb("c_m8%d" % j, [128, 16], F32) for j in range(2)])
    mbT_rr = RR([s.sb("c_mbT%d" % j, [128, 128], BF16) for j in range(2)])
    oco_rr = RR([s.sb("c_oc%d" % j, [128, 4, 64], F32) for j in range(2)])
    gw_rr = RR([s.sb("c_gw%d" % j, [128, 12], F32) for j in range(2)])
    oo_rr = RR([s.sb("c_oo%d" % j, [128, 4, 64], F32) for j in range(2)])
    ob_rr = RR([s.sb("c_ob%d" % j, [128, 4, 64], BF16) for j in range(2)])

    pipe = Pipe(2)

    def masked_tile(kbuf, prow, t, Q, masks, vbuf, vt_idx, po, first, last, extra=None):
        ps = cm["ps_rr"].next()
        nm = len(masks)
        s.op("pe", lambda: nc.tensor.matmul(ps[:, :], lhsT=kbuf[:, t * 128:(t + 1) * 128], rhs=Q,
                                            start=True, stop=(nm == 0)), reads=[kbuf, qc, qc2], writes=[ps], inc=(nm == 0))
        for mi, (la, lb, ra, rb) in enumerate(masks):
            for h in range(4):
                lastm = (mi == nm - 1 and h == 3)
                s.op("pe", lambda: nc.tensor.matmul(ps[:, h * 128:(h + 1) * 128], lhsT=la, rhs=ra, start=False, stop=lastm),
                     reads=[lb, rb], writes=[ps], inc=lastm)
        pt = cm["pt_rr"].next()
        s.op("act", lambda: nc.scalar.activation(out=pt[:, :], in_=ps[:, :], func=AF.Exp), reads=[ps], writes=[pt])

        def back(pt=pt, po=po, vbuf=vbuf, vt_idx=vt_idx, first=first, last=last, extra=extra):
            s.op("pe", lambda: nc.tensor.matmul(po[:, :], lhsT=vbuf[:, vt_idx, :], rhs=pt[:, :], start=first, stop=last),
                 reads=[vbuf, pt], writes=[po])
            if extra is not None:
                extra(pt)
        pipe.push(back)

    for ci in cs:
        def gk(key):
            return io[key][ci] if fused else io[key]
        if fused:
            for h in range(4):
                r0 = (h % 2) * 64
                srcq = io["zq"][h // 2][r0:r0 + 64, :].rearrange("d (i c q) -> d i c q", c=4, q=128)[:, :, ci, :]
                s.dma("sp", qc[0:64, :, h * 128:(h + 1) * 128], srcq, writes=[qc])
                s.dma("sp", qc2[64:128, :, h * 128:(h + 1) * 128], srcq, writes=[qc2])
            msb, mview = io["misc_sb"]
            s.op("act", lambda: nc.scalar.activation(out=ngt[:, :, :], in_=mview[:, ci:NT:4, 4:16], func=AF.Exp, scale=-1.0),
                 reads=[msb], writes=[ngt])
        else:
            s.dma("sp", qc[0:64, :, :], io["qc"][0:64, :, :], writes=[qc])
            s.dma("sp", qc2[64:128, :, :], io["qc"][64:128, :, :], writes=[qc2])
            s.dma("sp", ngt[:, :, :], io["ng"][:, :, :], writes=[ngt])
            s.op("act", lambda: nc.scalar.activation(out=ngt[:, :, :], in_=ngt[:, :, :], func=AF.Exp, scale=-1.0),
                 reads=[ngt], writes=[ngt])
        s.op("dve", lambda: nc.vector.tensor_scalar(out=ngt[:, :, :], in0=ngt[:, :, :], scalar1=1.0, scalar2=None, op0=ALU.add),
             reads=[ngt], writes=[ngt])
        s.op("dve", lambda: nc.vector.reciprocal(out=ngt[:, :, :], in_=ngt[:, :, :]), reads=[ngt], writes=[ngt])
        s.dma("sp", smask[:, :, :], gk("smask")[:, :, :], writes=[smask])
        s.dma("sp", wmask[:, :, :], gk("wmask")[:, :, :], writes=[wmask])
        for i in range(QL):
            Qlo = qc[:, i, :]
            Qhi = qc2[:, i, :]
            cmk = cmk_rr.next()
            ia = ia_rr.next()
            ib = ib_rr.next()
            s.dma("sp", cmk[:, :, :], gk("cmask")[:, i, :, :], writes=[cmk])
            s.dma("sp", ia[:, :], gk("impA")[:, i, :], writes=[ia])
            s.dma("sp", ib[:, :], gk("impB")[:, i, :], writes=[ib])
            po_c, po_s, po_w = po_l[0], po_l[1], po_l[2]
            nct = min(NCT, i // 4 + 1)
            for nt in range(nct):
                def imp_mm(pt, nt=nt, nct=nct):
                    for h in range(4):
                        s.op("pe", lambda: nc.tensor.matmul(pf2[:, h, :], lhsT=pt[:, h * 128:(h + 1) * 128], rhs=ovl[:, nt, :],
                                                            start=(nt == 0 and h == 0), stop=(nt == nct - 1 and h == 3),
                                                            skip_group_check=True), reads=[pt, ovl], writes=[pf2],
                             inc=(h == 3))
                masked_tile(ktc, (0, 64), nt, Qlo, [(ident_b[:, :], ident_b, cmk[:, nt, :], cmk)], vc, nt, po_c,
                            nt == 0, nt == nct - 1, extra=imp_mm)
            pipe.flush()
            emit_o_to_tokmajor(s, cm, po_c, pf, 0)
            st = cm["st_rr"].next()
            rsum = cm["st_rr"].next()
            gw = gw_rr.next()
            s.op("dve", lambda: nc.vector.tensor_scalar(out=st[:, 0:4], in0=pf[:, :, 64], scalar1=1e-30, scalar2=None, op0=ALU.max),
                 reads=[pf], writes=[st])
            s.op("dve", lambda: nc.vector.reciprocal(out=rsum[:, 0:4], in_=st[:, 0:4]), reads=[st], writes=[rsum])
            oco = oco_rr.next()
            s.op("dve", lambda: nc.vector.tensor_copy(out=oco[:, :, :], in_=pf[:, :, 0:64]), reads=[pf], writes=[oco])
            imp = imp_rr.next()
            s.op("dve", lambda: nc.vector.tensor_scalar(out=imp[:, :], in0=pf2[:, 0, :], scalar1=rsum[:, 0:1], scalar2=None, op0=ALU.mult),
                 reads=[pf2, rsum], writes=[imp])
            for h in range(1, 4):
                s.op("dve", lambda: nc.vector.scalar_tensor_tensor(out=imp[:, :], in0=pf2[:, h, :], scalar=rsum[:, h:h + 1], in1=imp[:, :],
                                                                   op0=ALU.mult, op1=ALU.add), reads=[pf2, rsum, imp], writes=[imp])
            s.op("dve", lambda: nc.vector.tensor_tensor(out=imp[:, :], in0=imp[:, :], in1=ia[:, :], op=ALU.mult), reads=[imp, ia], writes=[imp])
            s.op("dve", lambda: nc.vector.tensor_tensor(out=imp[:, :], in0=imp[:, :], in1=ib[:, :], op=ALU.add), reads=[imp, ib], writes=[imp])
            m8 = m8_rr.next()
            imp2 = imp2_rr.next()
            s.op("dve", lambda: nc.vector.max(out=m8[:, 0:8], in_=imp[:, :]), reads=[imp], writes=[m8])
            s.op("dve", lambda: nc.vector.match_replace(out=imp2[:, :], in_to_replace=m8[:, 0:8], in_values=imp[:, :], imm_value=-1e9),
                 reads=[imp, m8], writes=[imp2])
            s.op("dve", lambda: nc.vector.max(out=m8[:, 8:16], in_=imp2[:, :]), reads=[imp2], writes=[m8])
            s.op("dve", lambda: nc.vector.tensor_scalar(out=imp2[:, :], in0=imp[:, :], scalar1=m8[:, 15:16], scalar2=NEG,
                                                        op0=ALU.is_lt, op1=ALU.mult), reads=[imp, m8], writes=[imp2])
            tl = [4 * (i - 1) + u for u in range(8) if 4 * (i - 1) + u >= 0]
            for t in tl:
                u = t - 4 * (i - 1)
                masked_tile(kk, (64, 128), t, Qhi, [(ident_b[:, :], ident_b, wmask[:, u, :], wmask)], vw, t, po_w,
                            t == tl[0], t == tl[-1])
            ptr = cm["ps_rr"].next()
            s.op("pe", lambda: nc.tensor.transpose(out=ptr[:, 0:128], in_=imp2[:, :], identity=cm["ident_f"][:, :]),
                 reads=[imp2, cm["ident_f"]], writes=[ptr])
            mbT = mbT_rr.next()
            s.op("act", lambda: nc.scalar.copy(out=mbT[:, :], in_=ptr[:, 0:128]), reads=[ptr], writes=[mbT])
            nts = 4 * i + 4
            for t in range(nts):
                masks = [(emat[:, t, :], emat, mbT[:, :], mbT)]
                if t >= 4 * i:
                    masks.append((ident_b[:, :], ident_b, smask[:, t - 4 * i, :], smask))
                masked_tile(kk, (0, 64), t, Qlo, masks, vs, t, po_s, t == 0, t == nts - 1)
            pipe.flush()
            emit_o_to_tokmajor(s, cm, po_s, pf, 0)
            st2 = cm["st_rr"].next()
            s.op("dve", lambda: nc.vector.tensor_scalar(out=st2[:, 0:4], in0=pf[:, :, 64], scalar1=1e-30, scalar2=None, op0=ALU.max),
                 reads=[pf], writes=[st2])
            s.op("dve", lambda: nc.vector.reciprocal(out=st2[:, 0:4], in_=st2[:, 0:4]), reads=[st2], writes=[st2])
            gv = ngt[:, i, :].rearrange("p (h b) -> p h b", b=3)
            gwv = gw[:, :].rearrange("p (h b) -> p h b", b=3)
            s.op("dve", lambda: nc.vector.tensor_tensor(out=gwv[:, :, 0], in0=gv[:, :, 0], in1=rsum[:, 0:4], op=ALU.mult),
                 reads=[ngt, rsum], writes=[gw])
            s.op("dve", lambda: nc.vector.tensor_tensor(out=gwv[:, :, 1], in0=gv[:, :, 1], in1=st2[:, 0:4], op=ALU.mult),
                 reads=[ngt, st2], writes=[gw])
            oo = oo_rr.next()
            for h in range(4):
                s.op("dve", lambda: nc.vector.tensor_scalar(out=oo[:, h, :], in0=oco[:, h, :], scalar1=gw[:, 3 * h:3 * h + 1], scalar2=None,
                                                            op0=ALU.mult), reads=[oco, gw], writes=[oo])
                s.op("dve", lambda: nc.vector.scalar_tensor_tensor(out=oo[:, h, :], in0=pf[:, h, 0:64], scalar=gw[:, 3 * h + 1:3 * h + 2],
                                                                   in1=oo[:, h, :], op0=ALU.mult, op1=ALU.add), reads=[pf, gw, oo], writes=[oo])
            emit_o_to_tokmajor(s, cm, po_w, pf, 0)
            st3 = cm["st_rr"].next()
            s.op("dve", lambda: nc.vector.tensor_scalar(out=st3[:, 0:4], in0=pf[:, :, 64], scalar1=1e-30, scalar2=None, op0=ALU.max),
                 reads=[pf], writes=[st3])
            s.op("dve", lambda: nc.vector.reciprocal(out=st3[:, 0:4], in_=st3[:, 0:4]), reads=[st3], writes=[st3])
            s.op("dve", lambda: nc.vector.tensor_tensor(out=gwv[:, :, 2], in0=gv[:, :, 2], in1=st3[:, 0:4], op=ALU.mult),
                 reads=[ngt, st3], writes=[gw])
            ob = ob_rr.next()
            for h in range(4):
                s.op("dve", lambda: nc.vector.scalar_tensor_tensor(out=ob[:, h, :], in0=pf[:, h, 0:64], scalar=gw[:, 3 * h + 2:3 * h + 3],
                                                                   in1=oo[:, h, :], op0=ALU.mult, op1=ALU.add), reads=[pf, gw, oo], writes=[ob])
            orow = ((4 * i + ci) if fused else i) * 128
            s.dma("sp", io["oc"][orow:orow + 128, :], ob[:, :, :].rearrange("p h d -> p (h d)"), reads=[ob])
    s.release(m)


def build_mix(T, parts="abc"):
    nc = bass.Bass("TRN2", target_bir_lowering=False)
    io = mix_decl(nc, T)
    if "c" in parts:
        mix_decl_c(nc, io, T)
    s = S(nc)
    cm = mix_common(s, io)
    if "a" in parts:
        emit_mix_a(s, cm, io, T)
    if "b" in parts:
        emit_mix_b(s, cm, io, T)
    if "c" in parts:
        emit_mix_c(s, cm, io, T)
    s.finish()
    s.close()
    return nc


def mix_consts_c(T, c):
    NT = T // 128
    QL = NT // 4
    NCT = max(1, T // 2048)
    Nc = T // 16 - 1
    NS = T // 64
    ar = np.arange(128)
    cmask = np.zeros((128, QL, NCT, 128), np.float32)
    impA = np.zeros((128, QL, 128), np.float32)
    impB = np.zeros((128, QL, 128), np.float32)
    for i in range(QL):
        qpos = 128 * (4 * i + c) + ar
        for nt in range(NCT):
            n = 128 * nt + ar
            ok = (16 * n[:, None] + 31 <= qpos[None, :]) & (n[:, None] < Nc)
            cmask[:, i, nt, :] = np.where(ok, 0.0, NEG)
        j = ar
        cur = qpos // 64
        forced = (j[None, :] == 0) | (j[None, :] == cur[:, None]) | (j[None, :] == cur[:, None] - 1)
        valid = (j[None, :] * 64 <= qpos[:, None]) & (j[None, :] < NS)
        impA[:, i, :] = (valid & ~forced).astype(np.float32)
        impB[:, i, :] = np.where(forced & (j[None, :] < NS), 1.0e4, np.where(valid, 0.0, -1.0))
    smask = np.zeros((128, 4, 128), np.float32)
    for u in range(4):
        kpos = 128 * u + ar
        qp = 128 * c + ar
        smask[:, u, :] = np.where(kpos[:, None] <= qp[None, :], 0.0, NEG)
    wmask = np.zeros((128, 8, 128), np.float32)
    for u in range(8):
        dist = 128 * (c + 4 - u) + ar[None, :] - ar[:, None]
        wmask[:, u, :] = np.where((dist >= 0) & (dist < 512), 0.0, NEG)
    emat = np.zeros((128, NT, 128), np.float32)
    for t in range(NT):
        for k in range(128):
            jj = 2 * t + k // 64
            if jj < 128:
                emat[jj, t, k] = 1.0
    ovl = np.zeros((128, NCT, 128), np.float32)
    for nt in range(NCT):
        n = 128 * nt + ar
        o = (n[:, None] * 16 < (ar[None, :] + 1) * 64) & (n[:, None] * 16 + 32 > ar[None, :] * 64) & (n[:, None] < Nc) \
            & (ar[None, :] < NS)
        ovl[:, nt, :] = o
    return dict(cmask=_bf(cmask), impA=impA, impB=impB, smask=_bf(smask), wmask=_bf(wmask), emat=_bf(emat), ovl=_bf(ovl),
                ones64=_bf(np.ones((64, 64), np.float32)), onesrow=_bf(np.ones((1, 128), np.float32)))


def build_merge(NT, TB=512):
    nc = bass.Bass("TRN2", target_bir_lowering=False)
    x = dram_in(nc, "x", [NT, D])
    g = dram_in(nc, "g", [D])
    w_in = dram_in(nc, "w_in", [D, 6800])
    w_br = dram_in(nc, "w_br", [4, 256, D])
    w_o = dram_in(nc, "w_o", [D, D])
    ident = dram_in(nc, "ident", [128, 128], BF16)
    obr = dram_in(nc, "obr", [NT, D], BF16)
    y = dram_out(nc, "y", [NT, D])
    s = S(nc)
    emit_merge(s, x, g, w_in, w_br, w_o, ident, obr, y, NT, TB)
    s.finish()
    s.close()
    return nc


def emit_merge(s, x, g, w_in, w_br, w_o, ident, obr, y, NT, TB=512):
    nc = s.nc
    m_ = s.mark()
    ntile = TB // 128
    ident_b = s.sb("ident_b", [128, 128], BF16)
    s.dma("sp", ident_b[:, :], ident[:, :], writes=[ident_b])
    gcol = s.sb("gcol", [128, NKC], F32)
    s.dma("sp", gcol[:, :], g.rearrange("(c p) -> p c", p=128), writes=[gcol], allow_slow_non_contiguous=True)
    epsc = s.sb("epsc", [128, 1], F32)
    s.op("dve", lambda: nc.vector.memset(epsc[:, :], EPS), writes=[epsc])
    wg = [s.sb("wg%d" % c, [128, 4096], BF16) for c in range(NKC)]
    wb = [s.sb("wb%d" % c, [128, D], BF16) for c in range(8)]
    wo = [s.sb("wo%d" % c, [128, D], BF16) for c in range(NKC)]
    for c in range(NKC):
        for hf in range(2):
            s.dma("pool", wg[c][:, hf * 2048:(hf + 1) * 2048], w_in[c * 128:(c + 1) * 128, 2704 + hf * 2048:2704 + (hf + 1) * 2048],
                  writes=[wg[c]])
    for n in range(4):
        for cc in range(2):
            s.dma("pool", wb[2 * n + cc][:, :], w_br[n, cc * 128:(cc + 1) * 128, :], writes=[wb[2 * n + cc]])
    for c in range(NKC):
        s.dma("pool", wo[c][:, :], w_o[c * 128:(c + 1) * 128, :], writes=[wo[c]])
    xn = [s.sb("xn%d" % j, [128, D], F32) for j in range(ntile)]
    xr_rr = RR([s.sb("xr%d" % j, [128, D], F32) for j in range(2)])
    ots = [[s.sb("ot%d_%d" % (k, j), [128, D], BF16) for j in range(ntile)] for k in range(2)]
    hb_rr = RR([s.sb("hb%d" % j, [128, D], BF16) for j in range(2 * ntile)])
    stat_rr = RR([s.sb("st%d" % j, [128, 16], F32) for j in range(4)])
    hT = s.sb("hT", [128, NKC, TB], BF16)
    oT = s.sb("oT", [128, 8, TB], BF16)
    mT = [s.sb("mT%d" % c, [128, TB], BF16) for c in range(8)]
    pT_rr = RR([s.ps("pT%d" % j, [128, TB], BF16) for j in range(2)])
    pg_rr = RR([s.ps("pg%d" % j, [128, 512], F32) for j in range(2)])
    pp_rr = RR([s.ps("pp%d" % j, [128, 512], F32) for j in range(2)])
    po_rr = RR([s.ps("po%d" % j, [128, 512], F32) for j in range(2)])
    sg_rr = RR([s.sb("sg%d" % j, [128, TB], F32) for j in range(3)])
    acc_rr = RR([s.sb("acc%d" % j, [128, TB], F32) for j in range(2)])
    nblk = NT // TB

    def prep_a(tb):
        ot = ots[tb % 2]
        for j in range(ntile):
            r0 = tb * TB + j * 128
            s.dma("sp", xn[j][:, :], x[r0:r0 + 128, :], writes=[xn[j]])
            s.dma("sp", ot[j][:, :], obr[r0:r0 + 128, :], writes=[ot[j]])
        return emit_norm(s, epsc, xn, hb_rr, None, stat_rr, ntile)

    def prep_b(tb, hbs):
        ot = ots[tb % 2]
        emit_transpose_T(s, hbs, gcol, hT, ident_b, pT_rr, ntile)
        for c in range(8):
            pT = pT_rr.next()
            for j in range(ntile):
                s.op("pe", lambda: nc.tensor.transpose(out=pT[:, j * 128:(j + 1) * 128], in_=ot[j][:, c * 128:(c + 1) * 128],
                                                       identity=ident_b[:, :]), reads=[ot[j], ident_b], writes=[pT], inc=(j == ntile - 1))
            if c % 2 == 0:
                s.op("dve", lambda: nc.vector.tensor_copy(out=oT[:, c, :], in_=pT[:, 0:TB]), reads=[pT], writes=[oT])
            else:
                s.op("act", lambda: nc.scalar.copy(out=oT[:, c, :], in_=pT[:, 0:TB]), reads=[pT], writes=[oT])

    hbs_next = prep_a(0)
    prep_b(0, hbs_next)
    for tb in range(nblk):
        t0 = tb * TB
        if tb + 1 < nblk:
            hbs_next = prep_a(tb + 1)
        for dc in range(8):
            acc = acc_rr.next()
            for n in range(4):
                pg = pg_rr.next()
                pp = pp_rr.next()
                for c in range(NKC):
                    s.op("pe", lambda: nc.tensor.matmul(pg[:, 0:TB], lhsT=wg[c][:, n * 1024 + dc * 128:n * 1024 + (dc + 1) * 128],
                                                        rhs=hT[:, c, :], start=(c == 0), stop=(c == NKC - 1)),
                         reads=[wg[c], hT], writes=[pg], inc=(c == NKC - 1))
                for cc in range(2):
                    s.op("pe", lambda: nc.tensor.matmul(pp[:, 0:TB], lhsT=wb[2 * n + cc][:, dc * 128:(dc + 1) * 128],
                                                        rhs=oT[:, 2 * n + cc, :], start=(cc == 0), stop=(cc == 1)),
                         reads=[wb[2 * n + cc], oT], writes=[pp], inc=(cc == 1))
                sg = sg_rr.next()
                s.op("act", lambda: nc.scalar.activation(out=sg[:, :], in_=pg[:, 0:TB], func=AF.Sigmoid), reads=[pg], writes=[sg])
                if n == 0:
                    s.op("dve", lambda: nc.vector.tensor_tensor(out=acc[:, :], in0=sg[:, :], in1=pp[:, 0:TB], op=ALU.mult),
                         reads=[sg, pp], writes=[acc])
                else:
                    s.op("dve", lambda: nc.vector.tensor_tensor(out=sg[:, :], in0=sg[:, :], in1=pp[:, 0:TB], op=ALU.mult),
                         reads=[sg, pp], writes=[sg])
                    if n < 3:
                        s.op("pool", lambda: nc.gpsimd.tensor_tensor(out=acc[:, :], in0=acc[:, :], in1=sg[:, :], op=ALU.add),
                             reads=[acc, sg], writes=[acc])
                    else:
                        s.op("pool", lambda: nc.gpsimd.tensor_tensor(out=mT[dc][:, :], in0=acc[:, :], in1=sg[:, :], op=ALU.add),
                             reads=[acc, sg], writes=[mT[dc]])
        if tb + 1 < nblk:
            prep_b(tb + 1, hbs_next)
        for j in range(ntile):
            xr = xr_rr.next()
            s.dma("sp", xr[:, :], x[t0 + j * 128:t0 + (j + 1) * 128, :], writes=[xr])
            for hf in range(2):
                po = po_rr.next()
                for dc in range(8):
                    s.op("pe", lambda: nc.tensor.matmul(po[:, :], lhsT=mT[dc][:, j * 128:(j + 1) * 128],
                                                        rhs=wo[dc][:, hf * 512:(hf + 1) * 512], start=(dc == 0), stop=(dc == 7)),
                         reads=[mT[dc], wo[dc]], writes=[po], inc=(dc == 7))
                s.op("dve", lambda: nc.vector.tensor_tensor(out=xr[:, hf * 512:(hf + 1) * 512], in0=po[:, :],
                                                            in1=xr[:, hf * 512:(hf + 1) * 512], op=ALU.add),
                     reads=[po, xr], writes=[xr])
            s.dma("sp", y[t0 + j * 128:t0 + (j + 1) * 128, :], xr[:, :], reads=[xr])
    s.release(m_)


PARAM_SHAPES = dict(
    ffn1_norm=("L", D), ffn1_w_in=("L", D, 2 * DFF), ffn1_w_out=("L", DFF, D), mix_norm=("L", D), w_in=("L", D, 6800),
    nsa_phi_w1=("L", 2, 2048, 256), nsa_phi_w2=("L", 2, 256, 64), nsa_phi_b2=("L", 2, 64),
    w_branch=("L", 4, 256, D), w_out=("L", D, D), ffn2_norm=("L", D), ffn2_w_in=("L", D, 2 * DFF), ffn2_w_out=("L", DFF, D),
    gains=("L", 128, 6), vgain=("L", 128, 256), wsT=("L", 4, 128, 128), bsT=("L", 128, 4), lamp=("L", 128, 4, 32),
    lami=("L", 128, 2), fbias=("L", 4, 128, 1), b1l=("L", 128, 4), peT=("L", 128, 32), b2c=("L", 64, 1), kgain=("L", 64, 1),
)


def fused_const_shapes(T):
    NT = T // 128
    QL = NT // 4
    NCT = max(1, T // 2048)
    return dict(
        ident=([128, 128], BF16), identf=([128, 128], F32), blk=([2, 128, 128], BF16), triu=([128, 128], F32),
        trib=([128, 128], BF16), triuf=([128, 128], F32), onesf=([128, 128], F32), ones64=([64, 64], BF16),
        onesrow=([1, 128], BF16), cmask=([4, 128, QL, NCT, 128], BF16), smask=([4, 128, 4, 128], BF16),
        wmask=([4, 128, 8, 128], BF16), impA=([4, 128, QL, 128], F32), impB=([4, 128, QL, 128], F32),
        emat=([128, NT, 128], BF16), ovl=([128, NCT, 128], BF16))


def fused_consts(T):
    pc = proj_consts()
    mc = mix_consts()
    cc = [mix_consts_c(T, c) for c in range(4)]
    d = dict(ident=pc["ident"], identf=mc["identf"], blk=pc["blk"], triu=pc["triu"], trib=mc["trib"], triuf=mc["triuf"],
             onesf=mc["onesf"], ones64=cc[0]["ones64"], onesrow=cc[0]["onesrow"], emat=cc[0]["emat"], ovl=cc[0]["ovl"])
    for k in ("cmask", "smask", "wmask", "impA", "impB"):
        d[k] = np.ascontiguousarray(np.stack([cc[c][k] for c in range(4)], 0))
    return d


def emit_mix_fused(s, F, l, T):
    nc = s.nc
    NT = T // 128
    m = s.mark()
    io0 = dict(identb=F["ident"], identf=F["identf"], trib=F["trib"])
    misc_sb = s.sb("misc_sb", [128, NT, 16], F32)
    m_ab = s.mark()
    cm = mix_common(s, io0, n_ps=4, with_pf2=False)
    for j0 in range(0, NT, 8):
        j1 = min(NT, j0 + 8)
        s.dma("sp", misc_sb[:, j0:j1, :], F["misc"][j0 * 128:j1 * 128, :].rearrange("(j p) c -> p j c", p=128), writes=[misc_sb])
    zfm, vab, obr = F["zfm"], F["vab"], F["obr"]

    def io_a(h):
        r0 = (h % 2) * 64
        io = dict(io0)
        io.update(qa=zfm[h // 2][r0:r0 + 64, :], ka=zfm[2 + h // 2][r0:r0 + 64, :], va_src=vab[:, h * 64:(h + 1) * 64],
                  lamp=F["lamp"][l], lami=F["lami"][l], oa=obr[:, h * 64:(h + 1) * 64])
        return io

    def io_b(h):
        r0 = (h % 2) * 64
        io = dict(io0)
        io.update(qb=zfm[4 + h // 2][r0:r0 + 64, :], kb=zfm[6 + h // 2][r0:r0 + 64, :],
                  vb_src=vab[:, 256 + h * 64:256 + (h + 1) * 64], flog_sb=(misc_sb, misc_sb[:, :, h]), fbias=F["fbias"][l, h],
                  triuf=F["triuf"], onesf=F["onesf"], ob=obr[:, 256 + h * 64:256 + (h + 1) * 64])
        return io

    ba = emit_mix_a(s, cm, io_a(0), T, stage="alloc")
    bb = emit_mix_b(s, cm, io_b(0), T, stage="alloc")
    emit_mix_a(s, cm, io_a(0), T, stage="load", bufs=ba)
    emit_mix_b(s, cm, io_b(0), T, stage="load", bufs=bb)
    for h in range(4):
        emit_mix_a(s, cm, io_a(h), T, stage="compute", bufs=ba)
        if h + 1 < 4:
            emit_mix_a(s, cm, io_a(h + 1), T, stage="load", bufs=ba)
        emit_mix_b(s, cm, io_b(h), T, stage="compute", bufs=bb)
        if h + 1 < 4:
            emit_mix_b(s, cm, io_b(h + 1), T, stage="load", bufs=bb)
    s.release(m_ab)
    cm = mix_common(s, io0, n_ps=3, with_pf2=True)
    io = dict(io0)
    io.update(zq=(zfm[8], zfm[9]), kskw=zfm[10], kvin=zfm[11], vs_src=F["vsw"][:, 0:64], vw_src=F["vsw"][:, 64:128],
              misc_sb=(misc_sb, misc_sb), w1=F["nsa_phi_w1"][l], b1=F["b1l"][l], peT=F["peT"][l], w2=F["nsa_phi_w2"][l],
              b2=F["nsa_phi_b2"][l], b2c=F["b2c"][l], kgain=F["kgain"][l], oc=obr[:, 512:768])
    for k in ("cmask", "smask", "wmask", "impA", "impB", "emat", "ovl", "ones64", "onesrow"):
        io[k] = F[k]
    emit_mix_c(s, cm, io, T, cs=[0, 1, 2, 3])
    s.release(m)


def build_fused(T, L):
    nc = bass.Bass("TRN2", target_bir_lowering=False)
    F = {}
    F["x"] = dram_in(nc, "x", [T, D])
    for k, shp in PARAM_SHAPES.items():
        F[k] = dram_in(nc, k, [L if v == "L" else v for v in shp])
    for k, (shp, dt) in fused_const_shapes(T).items():
        F[k] = dram_in(nc, k, shp, dt)
    y = dram_out(nc, "y", [T, D])
    for k, shp, dt in (("xa", [T, D], F32), ("xb", [T, D], F32), ("xc", [T, D], F32), ("zfm", [NFM, 128, T], BF16),
                       ("vab", [T, 512], BF16), ("vsw", [T, 128], BF16), ("misc", [T, 16], F32), ("obr", [T, D], BF16)):
        F[k] = nc.dram_tensor("s_" + k, shp, dt).ap()
    s = S(nc)
    for l in range(L):
        x_in = F["x"] if l == 0 else F["xc"]
        emit_ffn(s, x_in, F["ffn1_norm"][l], F["ffn1_w_in"][l], F["ffn1_w_out"][l], F["ident"], F["xa"], T)
        a = dict(x=F["xa"], g=F["mix_norm"][l], w_in=F["w_in"][l], ident=F["ident"], gains=F["gains"][l], blk=F["blk"],
                 vgain=F["vgain"][l], wsT=F["wsT"][l], triu=F["triu"], bsT=F["bsT"][l], zfm=F["zfm"], vab=F["vab"],
                 vsw=F["vsw"], misc=F["misc"], od=F["obr"][:, 768:1024])
        emit_proj(s, a, T)
        emit_mix_fused(s, F, l, T)
        emit_merge(s, F["xa"], F["mix_norm"][l], F["w_in"][l], F["w_branch"][l], F["w_out"][l], F["ident"], F["obr"], F["xb"], T)
        x_out = y if l == L - 1 else F["xc"]
        emit_ffn(s, F["xb"], F["ffn2_norm"][l], F["ffn2_w_in"][l], F["ffn2_w_out"][l], F["ident"], x_out, T)
    s.finish()
    s.close()
    return nc


def fused_params(P, L):
    import math
    f32 = np.float32
    A = lambda a: np.ascontiguousarray(np.asarray(a, dtype=f32))
    d = {k: A(P[k]) for k in ("ffn1_norm", "ffn1_w_in", "ffn1_w_out", "mix_norm", "w_in", "nsa_phi_w1", "nsa_phi_w2",
                              "nsa_phi_b2", "w_branch", "w_out", "ffn2_norm", "ffn2_w_in", "ffn2_w_out")}
    tile = lambda v, n: np.tile(A(v), (1, n))
    d["gains"] = np.ascontiguousarray(np.stack([tile(P["diff_q_gain"], 4), tile(P["diff_k_gain"], 4), tile(P["fox_q_gain"], 2),
                                                tile(P["fox_k_gain"], 2), tile(P["nsa_q_gain"], 2), tile(P["nsa_k_gain"], 2)], 2))
    d["vgain"] = np.ascontiguousarray(np.broadcast_to(A(P["gmlp_v_gain"])[:, None, :], (L, 128, 256)))
    d["wsT"] = np.ascontiguousarray(A(P["gmlp_w_s"]).transpose(0, 1, 3, 2))
    d["bsT"] = np.ascontiguousarray(A(P["gmlp_b_s"]).transpose(0, 2, 1))
    d["lamp"] = np.ascontiguousarray(np.broadcast_to(A(P["diff_lambda"])[:, None], (L, 128, 4, 32)))
    li = np.array([[0.8 - 0.6 * math.exp(-0.3 * l), 1.0 - (0.8 - 0.6 * math.exp(-0.3 * l))] for l in range(L)], f32)
    d["lami"] = np.ascontiguousarray(np.broadcast_to(li[:, None, :], (L, 128, 2)))
    d["fbias"] = np.ascontiguousarray(np.broadcast_to(A(P["fox_f_bias"])[:, :, None, None], (L, 4, 128, 1)))
    d["b1l"] = np.ascontiguousarray(A(P["nsa_phi_b1"]).reshape(L, 2, 2, 128).transpose(0, 3, 1, 2).reshape(L, 128, 4))
    pe = A(P["nsa_cmp_pe"])
    d["peT"] = np.ascontiguousarray(pe.transpose(0, 1, 3, 2).reshape(L, 128, 32))
    d["b2c"] = np.ascontiguousarray(A(P["nsa_phi_b2"])[:, 0, :, None])
    d["kgain"] = np.ascontiguousarray(A(P["nsa_k_gain"])[:, :, None])
    return d


B_, T_, L_ = 2, 8192, 2
_PROG = {}


def kernel(**inputs):
    x = np.ascontiguousarray(np.asarray(inputs["x"], dtype=np.float32))
    if "fused" not in _PROG:
        _PROG["fused"] = build_fused(T_, L_)
        _PROG["consts"] = fused_consts(T_)
    nc = _PROG["fused"]
    par = fused_params(inputs, L_)
    in_maps = []
    for b in range(B_):
        d = dict(par)
        d.update(_PROG["consts"])
        d["x"] = x[b]
        in_maps.append(d)
    res = run_bass_kernel_spmd(nc, in_maps, core_ids=list(range(B_)))
    return np.stack([np.asarray(res.results[b]["y"], dtype=np.float32) for b in range(B_)], 0)
```

```python
import numpy as np
import concourse.bass as bass
import concourse.mybir as mybir
from concourse.bass_utils import run_bass_kernel_spmd

F32 = mybir.dt.float32
BF16 = mybir.dt.bfloat16
AF = mybir.ActivationFunctionType
ALU = mybir.AluOpType
AX = mybir.AxisListType

ENGS = ("pe", "act", "dve", "pool", "sp")


class Buf:
    __slots__ = ("name", "t", "w", "r", "dsem", "dcnt", "uid")
    _n = 0

    def __init__(self, name, t):
        Buf._n += 1
        self.uid = Buf._n
        self.name = name
        self.t = t
        self.w = None
        self.r = []
        self.dsem = None
        self.dcnt = 0

    def __getitem__(self, idx):
        return self.t[idx]


class S:
    def __init__(self, nc, same_engine_sync=True):
        self.nc = nc
        self.e = {"pe": nc.tensor, "act": nc.scalar, "dve": nc.vector, "pool": nc.gpsimd, "sp": nc.sync}
        self.sem = {k: nc.alloc_semaphore("c_" + k) for k in ENGS}
        self.cnt = {k: 0 for k in ENGS}
        self.seen = {k: {} for k in ENGS}
        self.same = same_engine_sync
        self.nbuf = 0
        self.dma_sems = []
        self.ctx = []
        self.cbufs = []
        self.free_dsems = []

    def sb(self, name, shape, dt):
        self.nbuf += 1
        g = self.nc.sbuf_tensor("%s_%d" % (name, self.nbuf), list(shape), dt)
        t = g.__enter__()
        self.ctx.append(g)
        b = Buf(name, t)
        self.cbufs.append(b)
        return b

    def ps(self, name, shape, dt):
        self.nbuf += 1
        g = self.nc.psum_tensor("%s_%d" % (name, self.nbuf), list(shape), dt)
        t = g.__enter__()
        self.ctx.append(g)
        b = Buf(name, t)
        self.cbufs.append(b)
        return b

    def sub(self, name, ap):
        return Buf(name, ap)

    def mark(self):
        return len(self.ctx)

    def release(self, m):
        self.barrier()
        while len(self.ctx) > m:
            self.ctx.pop().__exit__(None, None, None)
            b = self.cbufs.pop()
            if b.dsem is not None:
                self.free_dsems.append((b.dsem, b.dcnt))
                self.dma_sems.remove(b)
                b.dsem = None

    def close(self):
        for g in reversed(self.ctx):
            g.__exit__(None, None, None)
        self.ctx = []
        self.cbufs = []

    def _need(self, E, deps):
        need = {}
        for d in deps:
            if d is None:
                continue
            if d[0] == "dma":
                b = d[1]
                key = ("dma", b.uid)
                need[key] = (b, b.dcnt)
            else:
                F, c = d
                if F == E and (not self.same or E == "pe" or c > self.cnt[E]):
                    continue
                if c > need.get(F, (None, 0))[1]:
                    need[F] = (None, c)
        for key, (b, c) in need.items():
            if self.seen[E].get(key, 0) >= c:
                continue
            self.seen[E][key] = c
            if b is not None:
                self.e[E].wait_ge(b.dsem, c)
            else:
                self.e[E].wait_ge(self.sem[key], c)

    def op(self, E, fn, reads=(), writes=(), inc=True):
        deps = []
        for b in reads:
            deps.append(b.w)
        for b in writes:
            deps.append(b.w)
            deps.extend(b.r)
        self._need(E, deps)
        ins = fn()
        c = self.cnt[E] + 1
        if inc:
            ins.then_inc(self.sem[E], 1)
            self.cnt[E] = c
        for b in writes:
            b.w = (E, c)
            b.r = []
        for b in reads:
            if b not in writes:
                b.r = [x for x in b.r if x[0] != E] + [(E, c)]
        return ins

    def dma(self, Q, out, in_, reads=(), writes=(), **kw):
        deps = []
        for b in reads:
            deps.append(b.w)
        for b in writes:
            deps.append(b.w)
            deps.extend(b.r)
        self._need(Q, deps)
        owner = (list(writes) + list(reads))[0]
        if owner.dsem is None:
            owner.dsem = self._dsem(owner)
            self.dma_sems.append(owner)
        ins = self.e[Q].dma_start(out=out, in_=in_, **kw)
        ins.then_inc(owner.dsem, 16)
        owner.dcnt += 16
        rec = ("dma", owner, owner.dcnt)
        for b in writes:
            b.w = rec
            b.r = []
        for b in reads:
            if b not in writes:
                b.r = b.r + [rec]
        return ins

    def cc(self, kind, groups, in_ap, out_ap, reads=(), writes=()):
        deps = []
        for b in reads:
            deps.append(b.w)
        for b in writes:
            deps.append(b.w)
            deps.extend(b.r)
        self._need("pool", deps)
        owner = list(writes)[0]
        if owner.dsem is None:
            owner.dsem = self._dsem(owner)
            self.dma_sems.append(owner)
        ins = self.nc.gpsimd.collective_compute(kind, op=ALU.bypass, replica_groups=groups, ins=[in_ap], outs=[out_ap])
        ins.then_inc(owner.dsem, 16)
        owner.dcnt += 16
        rec = ("dma", owner, owner.dcnt)
        for b in writes:
            b.w = rec
            b.r = []
        for b in reads:
            if b not in writes:
                b.r = b.r + [rec]
        return ins

    def _dsem(self, owner):
        if self.free_dsems:
            sem, cnt = self.free_dsems.pop()
            owner.dcnt = cnt
            return sem
        self.nsem = getattr(self, "nsem", 0) + 1
        return self.nc.alloc_semaphore("d_%d" % self.nsem)

    def barrier(self):
        for E in ENGS:
            for Fk in ENGS:
                if Fk == E:
                    continue
                c = self.cnt[Fk]
                if c and self.seen[E].get(Fk, 0) < c:
                    self.seen[E][Fk] = c
                    self.e[E].wait_ge(self.sem[Fk], c)
            for b in self.dma_sems:
                key = ("dma", b.uid)
                if b.dcnt and self.seen[E].get(key, 0) < b.dcnt:
                    self.seen[E][key] = b.dcnt
                    self.e[E].wait_ge(b.dsem, b.dcnt)

    def finish(self):
        self.barrier()


D = 1024
DFF = 2816
NFC = DFF // 128
NKC = D // 128
EPS = 1e-6


def dram_in(nc, name, shape, dt=F32):
    return nc.dram_tensor(name, list(shape), dt, kind="ExternalInput").ap()


def dram_out(nc, name, shape, dt=F32):
    return nc.dram_tensor(name, list(shape), dt, kind="ExternalOutput").ap()


class RR:
    def __init__(self, items):
        self.items = items
        self.i = 0

    def next(self):
        b = self.items[self.i % len(self.items)]
        self.i += 1
        return b


def emit_norm(s, epsc, xt, hb_rr, scr, stat_rr, ntile):
    nc = s.nc
    st = stat_rr.next()
    hbs = [hb_rr.next() for _ in range(ntile)]
    for j in range(ntile):
        s.op("act", lambda: nc.scalar.activation(out=hbs[j][:, :], in_=xt[j][:, :], func=AF.Square, scale=1.0 / 32.0,
                                                 accum_out=st[:, j:j + 1]),
             reads=[xt[j]], writes=[hbs[j], st])
    s.op("act", lambda: nc.scalar.activation(out=st[:, 4:4 + ntile], in_=st[:, 0:ntile], func=AF.Ln, bias=epsc[:, 0:1], scale=1.0),
         reads=[st, epsc], writes=[st])
    s.op("act", lambda: nc.scalar.activation(out=st[:, 8:8 + ntile], in_=st[:, 4:4 + ntile], func=AF.Exp, scale=-0.5),
         reads=[st], writes=[st])
    for j in range(ntile):
        s.op("act", lambda: nc.scalar.activation(out=hbs[j][:, :], in_=xt[j][:, :], func=AF.Copy, scale=st[:, 8 + j:9 + j]),
             reads=[xt[j], st], writes=[hbs[j]])
    return hbs


def emit_transpose_T(s, hbs, gcol, hT, ident_b, pT_rr, ntile, evac_engs=("dve", "act")):
    nc = s.nc
    k = 0
    for c in range(NKC):
        pT = pT_rr.next()
        for j in range(ntile):
            s.op("pe", lambda: nc.tensor.transpose(out=pT[:, j * 128:(j + 1) * 128], in_=hbs[j][:, c * 128:(c + 1) * 128],
                                                   identity=ident_b[:, :]),
                 reads=[hbs[j], ident_b], writes=[pT], inc=(j == ntile - 1))
        eng = evac_engs[k % len(evac_engs)]
        k += 1
        if eng == "dve":
            s.op("dve", lambda: nc.vector.tensor_scalar(out=hT[:, c, 0:ntile * 128], in0=pT[:, 0:ntile * 128],
                                                        scalar1=gcol[:, c:c + 1], scalar2=None, op0=ALU.mult),
                 reads=[pT, gcol], writes=[hT])
        else:
            s.op("act", lambda: nc.scalar.activation(out=hT[:, c, 0:ntile * 128], in_=pT[:, 0:ntile * 128],
                                                     func=AF.Copy, scale=gcol[:, c:c + 1]),
                 reads=[pT, gcol], writes=[hT])


def emit_rmsnorm_T(s, epsc, xt, gcol, hT, ident_b, pT_rr, hb_rr, scr, stat_rr, ntile, evac_engs=("dve", "act")):
    hbs = emit_norm(s, epsc, xt, hb_rr, scr, stat_rr, ntile)
    emit_transpose_T(s, hbs, gcol, hT, ident_b, pT_rr, ntile, evac_engs)


def build_ffn(NT, TB=512):
    nc = bass.Bass("TRN2", target_bir_lowering=False)
    x = dram_in(nc, "x", [NT, D])
    g = dram_in(nc, "g", [D])
    w_in = dram_in(nc, "w_in", [D, 2 * DFF])
    w_out = dram_in(nc, "w_out", [DFF, D])
    ident = dram_in(nc, "ident", [128, 128], BF16)
    y = dram_out(nc, "y", [NT, D])
    s = S(nc)
    emit_ffn(s, x, g, w_in, w_out, ident, y, NT, TB)
    s.finish()
    s.close()
    return nc


def emit_ffn(s, x, g, w_in, w_out, ident, y, NT, TB=512):
    nc = s.nc
    ntile = TB // 128
    m_ = s.mark()
    ident_b = s.sb("ident_b", [128, 128], BF16)
    s.dma("sp", ident_b[:, :], ident[:, :], writes=[ident_b])
    gcol = s.sb("gcol", [128, NKC], F32)
    epsc = s.sb("epsc", [128, 1], F32)
    s.op("dve", lambda: nc.vector.memset(epsc[:, :], EPS), writes=[epsc])
    s.dma("sp", gcol[:, :], g.rearrange("(c p) -> p c", p=128), writes=[gcol], allow_slow_non_contiguous=True)
    win_b = [s.sb("win_b%d" % c, [128, 2 * DFF], BF16) for c in range(NKC)]
    wout_b = [s.sb("wout_b%d" % f, [128, D], BF16) for f in range(NFC)]
    for c in range(NKC):
        for hf in range(2):
            s.dma("pool", win_b[c][:, hf * DFF:(hf + 1) * DFF], w_in[c * 128:(c + 1) * 128, hf * DFF:(hf + 1) * DFF],
                  writes=[win_b[c]])
    for f in range(NFC):
        s.dma("pool", wout_b[f][:, :], w_out[f * 128:(f + 1) * 128, :], writes=[wout_b[f]])
    xn = [s.sb("xn%d" % j, [128, D], F32) for j in range(ntile)]
    xr_rr = RR([s.sb("xr%d" % j, [128, D], F32) for j in range(1)])
    hb_rr = RR([s.sb("hb%d" % j, [128, D], BF16) for j in range(2 * ntile)])
    stat_rr = RR([s.sb("st%d" % j, [128, 16], F32) for j in range(4)])
    hT = s.sb("hT", [128, NKC, TB], BF16)
    pT_rr = RR([s.ps("pT%d" % j, [128, TB], BF16) for j in range(2)])
    pa_rr = RR([s.ps("pa%d" % j, [128, TB], F32) for j in range(2)])
    pb_rr = RR([s.ps("pb%d" % j, [128, TB], F32) for j in range(2)])
    po_rr = RR([s.ps("po%d" % j, [128, 512], F32) for j in range(2)])
    sa_rr = RR([s.sb("sa%d" % j, [128, TB], F32) for j in range(2)])
    act = [s.sb("actT%d" % f, [128, TB], BF16) for f in range(NFC)]
    nblk = NT // TB

    def prep_a_rot(tb):
        for j in range(ntile):
            r0 = tb * TB + j * 128
            s.dma("sp", xn[j][:, :], x[r0:r0 + 128, :], writes=[xn[j]])
        return emit_norm(s, epsc, xn, hb_rr, None, stat_rr, ntile)

    hbs_next = prep_a_rot(0)
    emit_transpose_T(s, hbs_next, gcol, hT, ident_b, pT_rr, ntile)
    for tb in range(nblk):
        if tb + 1 < nblk:
            hbs_next = prep_a_rot(tb + 1)
        for f in range(NFC):
            pa = pa_rr.next()
            pb = pb_rr.next()
            for c in range(NKC):
                s.op("pe", lambda: nc.tensor.matmul(pa[:, :], lhsT=win_b[c][:, f * 128:(f + 1) * 128], rhs=hT[:, c, :],
                                                    start=(c == 0), stop=(c == NKC - 1)),
                     reads=[win_b[c], hT], writes=[pa], inc=(c == NKC - 1))
            for c in range(NKC):
                s.op("pe", lambda: nc.tensor.matmul(pb[:, :], lhsT=win_b[c][:, DFF + f * 128:DFF + (f + 1) * 128],
                                                    rhs=hT[:, c, :], start=(c == 0), stop=(c == NKC - 1)),
                     reads=[win_b[c], hT], writes=[pb], inc=(c == NKC - 1))
            sa = sa_rr.next()
            s.op("act", lambda: nc.scalar.activation(out=sa[:, :], in_=pa[:, :], func=AF.Silu), reads=[pa], writes=[sa])
            s.op("dve", lambda: nc.vector.tensor_tensor(out=act[f][:, :], in0=sa[:, :], in1=pb[:, :], op=ALU.mult),
                 reads=[sa, pb], writes=[act[f]])
        if tb + 1 < nblk:
            emit_transpose_T(s, hbs_next, gcol, hT, ident_b, pT_rr, ntile)
        for j in range(ntile):
            r0 = tb * TB + j * 128
            xr = xr_rr.next()
            s.dma("sp", xr[:, :], x[r0:r0 + 128, :], writes=[xr])
            for hf in range(2):
                po = po_rr.next()
                for f in range(NFC):
                    s.op("pe", lambda: nc.tensor.matmul(po[:, :], lhsT=act[f][:, j * 128:(j + 1) * 128],
                                                        rhs=wout_b[f][:, hf * 512:(hf + 1) * 512],
                                                        start=(f == 0), stop=(f == NFC - 1)),
                         reads=[act[f], wout_b[f]], writes=[po], inc=(f == NFC - 1))
                s.op("dve", lambda: nc.vector.scalar_tensor_tensor(out=xr[:, hf * 512:(hf + 1) * 512], in0=po[:, :],
                                                                   scalar=0.5, in1=xr[:, hf * 512:(hf + 1) * 512],
                                                                   op0=ALU.mult, op1=ALU.add),
                     reads=[po, xr], writes=[xr])
            s.dma("sp", y[r0:r0 + 128, :], xr[:, :], reads=[xr])
    s.release(m_)


FM_SRC = [[(0, 128)], [(128, 128)], [(256, 128)], [(384, 128)],
          [(768, 128)], [(896, 128)], [(1024, 128)], [(1152, 128)],
          [(1540, 128)], [(1668, 128)], [(1924, 64), (2052, 64)], [(1796, 128)]]
FM_GCOL = [0, 0, 1, 1, 2, 2, 3, 3, 4, 4, 5, None]
FM_BLK = [0, 0, 0, 0, 1, 1, 1, 1, 1, 1, 1, None]
TM_SRC = [[(512, 256), (1280, 256)],
          [(1536, 4), (2180, 12), (1988, 64), (2116, 64)],
          [(2192, 512)]]
NFM = 12
GELU_C = 1.5957691216057308


def build_proj(NT, TB=512):
    nc = bass.Bass("TRN2", target_bir_lowering=False)
    a = dict(
        x=dram_in(nc, "x", [NT, D]), g=dram_in(nc, "g", [D]), w_in=dram_in(nc, "w_in", [D, 6800]),
        ident=dram_in(nc, "ident", [128, 128], BF16), gains=dram_in(nc, "gains", [128, 6]),
        blk=dram_in(nc, "blk", [2, 128, 128], BF16), vgain=dram_in(nc, "vgain", [128, 256]),
        wsT=dram_in(nc, "wsT", [4, 128, 128]), triu=dram_in(nc, "triu", [128, 128]), bsT=dram_in(nc, "bsT", [128, 4]),
        zfm=dram_out(nc, "zfm", [NFM, 128, NT], BF16), vab=dram_out(nc, "vab", [NT, 512], BF16),
        vsw=dram_out(nc, "vsw", [NT, 128], BF16), misc=dram_out(nc, "misc", [NT, 16]), od=dram_out(nc, "od", [NT, 256], BF16))
    s = S(nc)
    emit_proj(s, a, NT, TB)
    s.finish()
    s.close()
    return nc


def emit_proj(s, a, NT, TB=512):
    nc = s.nc
    x, g, w_in, ident, gains, blk, vgain, wsT, triu, bsT = (a[k] for k in
                                                            ("x", "g", "w_in", "ident", "gains", "blk", "vgain", "wsT", "triu", "bsT"))
    zfm, vab, vsw, misc, od = (a[k] for k in ("zfm", "vab", "vsw", "misc", "od"))
    m_ = s.mark()
    ntile = TB // 128
    ident_b = s.sb("ident_b", [128, 128], BF16)
    s.dma("sp", ident_b[:, :], ident[:, :], writes=[ident_b])
    gcol = s.sb("gcol", [128, NKC], F32)
    s.dma("sp", gcol[:, :], g.rearrange("(c p) -> p c", p=128), writes=[gcol], allow_slow_non_contiguous=True)
    epsc = s.sb("epsc", [128, 1], F32)
    s.op("dve", lambda: nc.vector.memset(epsc[:, :], EPS), writes=[epsc])
    gn = s.sb("gn", [128, 6], F32)
    s.dma("sp", gn[:, :], gains[:, :], writes=[gn])
    for col, sc in ((0, 32.0 ** -0.5), (2, 0.125), (4, 0.125)):
        s.op("dve", lambda: nc.vector.tensor_scalar(out=gn[:, col:col + 1], in0=gn[:, col:col + 1], scalar1=sc,
                                                    scalar2=None, op0=ALU.mult), reads=[gn], writes=[gn])
    blk_b = [s.sb("blk%d" % i, [128, 128], BF16) for i in range(2)]
    for i in range(2):
        s.dma("sp", blk_b[i][:, :], blk[i], writes=[blk_b[i]])
    vg = s.sb("vg", [128, 256], F32)
    s.dma("sp", vg[:, :], vgain[:, :], writes=[vg])
    bcol = s.sb("bcol", [128, 4], F32)
    s.dma("sp", bcol[:, :], bsT[:, :], writes=[bcol])
    tri = s.sb("tri", [128, 128], F32)
    s.dma("sp", tri[:, :], triu[:, :], writes=[tri])
    wm = []
    wtmp = s.sb("wtmp", [128, 128], F32)
    for gi in range(4):
        w = s.sb("wm%d" % gi, [128, 128], BF16)
        s.dma("sp", wtmp[:, :], wsT[gi], writes=[wtmp])
        s.op("dve", lambda: nc.vector.tensor_tensor(out=w[:, :], in0=wtmp[:, :], in1=tri[:, :], op=ALU.mult),
             reads=[wtmp, tri], writes=[w])
        wm.append(w)
    wfm = [s.sb("wfm%d" % c, [128, NFM * 128], BF16) for c in range(NKC)]
    wtm = [s.sb("wtm%d" % c, [128, 1168], BF16) for c in range(NKC)]
    FM_RUNS = [(0, 0, 512), (512, 768, 512), (1024, 1540, 256), (1280, 1924, 64), (1344, 2052, 64), (1408, 1796, 128)]
    TM_RUNS = [(0, 512, 256), (256, 1280, 256), (512, 1536, 4), (516, 2180, 12), (528, 1988, 64), (592, 2116, 64), (656, 2192, 512)]
    for c in range(NKC):
        for (o, c0, n) in FM_RUNS:
            s.dma("pool", wfm[c][:, o:o + n], w_in[c * 128:(c + 1) * 128, c0:c0 + n], writes=[wfm[c]])
        for (o, c0, n) in TM_RUNS:
            s.dma("pool", wtm[c][:, o:o + n], w_in[c * 128:(c + 1) * 128, c0:c0 + n], writes=[wtm[c]])
    xts = [[s.sb("xt%d_%d" % (k, j), [128, D], F32) for j in range(ntile)] for k in range(2)]
    hb_rr = RR([s.sb("hb%d" % j, [128, D], BF16) for j in range(2 * ntile)])
    scr = s.sb("scr", [128, D], BF16)
    stat_rr = RR([s.sb("st%d" % j, [128, 16], F32) for j in range(4)])
    hTs = [s.sb("hT%d" % k, [128, NKC, TB], BF16) for k in range(2)]
    pT_rr = RR([s.ps("pT%d" % j, [128, TB], BF16) for j in range(2)])
    pz_rr = RR([s.ps("pz%d" % j, [128, 512], F32) for j in range(2)])
    ptm_rr = RR([s.ps("ptm%d" % j, [128, 512], F32) for j in range(2)])
    pq_rr = RR([s.ps("pq%d" % j, [128, 512], F32) for j in range(2)])
    sq_rr = RR([s.sb("sq%d" % j, [128, TB], BF16) for j in range(4)])
    rs_rr = RR([s.sb("rs%d" % j, [128, TB], F32) for j in range(2)])
    zo_rr = RR([s.sb("zo%d" % j, [128, TB], BF16) for j in range(4)])
    vab_rr = RR([s.sb("vabt%d" % j, [128, 512], BF16) for j in range(2)])
    vsw_rr = RR([s.sb("vswt%d" % j, [128, 128], BF16) for j in range(2)])
    msc_rr = RR([s.sb("msct%d" % j, [128, 16], F32) for j in range(2)])
    f_rr = RR([s.sb("gf%d" % j, [128, 512], F32) for j in range(4)])
    ge_rr = RR([s.sb("ge%d" % j, [128, 512], F32) for j in range(3)])
    zs_rr = RR([s.sb("zs%d" % j, [128, 512], F32) for j in range(2)])
    zf_rr = RR([s.sb("zf%d" % j, [128, TB], F32) for j in range(3)])
    vn_rr = RR([s.sb("vn%d" % j, [128, 256], BF16) for j in range(3)])
    od_rr = RR([s.sb("odt%d" % j, [128, 256], BF16) for j in range(2)])
    nblk = NT // TB

    def prep(tb):
        xt = xts[tb % 2]
        for j in range(ntile):
            s.dma("sp", xt[j][:, :], x[tb * TB + j * 128:tb * TB + (j + 1) * 128, :], writes=[xt[j]])
        emit_rmsnorm_T(s, epsc, xt, gcol, hTs[tb % 2], ident_b, pT_rr, hb_rr, scr, stat_rr, ntile)

    prep(0)
    pipe = Pipe(2)
    for tb in range(nblk):
        t0 = tb * TB
        hT = hTs[tb % 2]
        for i in range(NFM):
            pz = pz_rr.next()
            for c in range(NKC):
                s.op("pe", lambda: nc.tensor.matmul(pz[:, 0:TB], lhsT=wfm[c][:, i * 128:(i + 1) * 128], rhs=hT[:, c, :],
                                                    start=(c == 0), stop=(c == NKC - 1)),
                     reads=[wfm[c], hT], writes=[pz], inc=(c == NKC - 1))
            zo = zo_rr.next()
            if FM_GCOL[i] is None:
                s.op("dve", lambda: nc.vector.tensor_copy(out=zo[:, :], in_=pz[:, 0:TB]), reads=[pz], writes=[zo])
                s.dma("sp", zfm[i, :, t0:t0 + TB], zo[:, :], reads=[zo])
            else:
                zf = zf_rr.next()
                s.op("dve", lambda: nc.vector.tensor_copy(out=zf[:, :], in_=pz[:, 0:TB]), reads=[pz], writes=[zf])
                sq = sq_rr.next()
                s.op("act", lambda: nc.scalar.activation(out=sq[:, :], in_=zf[:, :], func=AF.Square),
                     reads=[zf], writes=[sq])

                def back(i=i, zf=zf, sq=sq, zo=zo, t0=t0):
                    gs = 32.0 if FM_BLK[i] == 0 else 64.0
                    pq = pq_rr.next()
                    s.op("pe", lambda: nc.tensor.matmul(pq[:, 0:TB], lhsT=blk_b[FM_BLK[i]][:, :], rhs=sq[:, :],
                                                        start=True, stop=True), reads=[blk_b[FM_BLK[i]], sq], writes=[pq])
                    rs = rs_rr.next()
                    s.op("act", lambda: nc.scalar.activation(out=rs[:, :], in_=pq[:, 0:TB], func=AF.Ln, bias=epsc[:, 0:1],
                                                             scale=1.0 / gs), reads=[pq, epsc], writes=[rs])
                    s.op("act", lambda: nc.scalar.activation(out=rs[:, :], in_=rs[:, :], func=AF.Exp, scale=-0.5),
                         reads=[rs], writes=[rs])
                    gc = FM_GCOL[i]
                    s.op("dve", lambda: nc.vector.scalar_tensor_tensor(out=zo[:, :], in0=zf[:, :], scalar=gn[:, gc:gc + 1],
                                                                       in1=rs[:, :], op0=ALU.mult, op1=ALU.mult),
                         reads=[zf, gn, rs], writes=[zo])
                    s.dma("sp", zfm[i, :, t0:t0 + TB], zo[:, :], reads=[zo])
                pipe.push(back)
        if tb + 1 < nblk:
            prep(tb + 1)
        for j in range(ntile):
            r0 = t0 + j * 128
            pz = ptm_rr.next()
            for c in range(NKC):
                s.op("pe", lambda: nc.tensor.matmul(pz[:, :], lhsT=hT[:, c, j * 128:(j + 1) * 128], rhs=wtm[c][:, 0:512],
                                                    start=(c == 0), stop=(c == NKC - 1)),
                     reads=[wtm[c], hT], writes=[pz], inc=(c == NKC - 1))
            vt = vab_rr.next()
            s.op("act", lambda: nc.scalar.copy(out=vt[:, :], in_=pz[:, :]), reads=[pz], writes=[vt])
            s.dma("sp", vab[r0:r0 + 128, :], vt[:, :], reads=[vt])
            pz = ptm_rr.next()
            for c in range(NKC):
                s.op("pe", lambda: nc.tensor.matmul(pz[:, 0:144], lhsT=hT[:, c, j * 128:(j + 1) * 128], rhs=wtm[c][:, 512:656],
                                                    start=(c == 0), stop=(c == NKC - 1)),
                     reads=[wtm[c], hT], writes=[pz], inc=(c == NKC - 1))
            mt = msc_rr.next()
            vs_ = vsw_rr.next()
            s.op("dve", lambda: nc.vector.tensor_copy(out=mt[:, :], in_=pz[:, 0:16]), reads=[pz], writes=[mt])
            s.op("dve", lambda: nc.vector.tensor_copy(out=vs_[:, :], in_=pz[:, 16:144]), reads=[pz], writes=[vs_])
            s.dma("sp", misc[r0:r0 + 128, :], mt[:, :], reads=[mt])
            s.dma("sp", vsw[r0:r0 + 128, :], vs_[:, :], reads=[vs_])
            pz = ptm_rr.next()
            for c in range(NKC):
                s.op("pe", lambda: nc.tensor.matmul(pz[:, :], lhsT=hT[:, c, j * 128:(j + 1) * 128], rhs=wtm[c][:, 656:1168],
                                                    start=(c == 0), stop=(c == NKC - 1)),
                     reads=[wtm[c], hT], writes=[pz], inc=(c == NKC - 1))
            zs = zs_rr.next()
            s.op("act", lambda: nc.scalar.copy(out=zs[:, :], in_=pz[:, :]), reads=[pz], writes=[zs])
            z2 = f_rr.next()
            s.op("act", lambda: nc.scalar.activation(out=z2[:, :], in_=zs[:, :], func=AF.Square), reads=[zs], writes=[z2])
            s.op("dve", lambda: nc.vector.tensor_scalar(out=z2[:, :], in0=z2[:, :], scalar1=0.044715, scalar2=1.0,
                                                        op0=ALU.mult, op1=ALU.add), reads=[z2], writes=[z2])
            s.op("dve", lambda: nc.vector.tensor_tensor(out=z2[:, :], in0=z2[:, :], in1=zs[:, :], op=ALU.mult),
                 reads=[z2, zs], writes=[z2])
            s.op("act", lambda: nc.scalar.activation(out=z2[:, :], in_=z2[:, :], func=AF.Exp, scale=-GELU_C),
                 reads=[z2], writes=[z2])
            s.op("act", lambda: nc.scalar.activation(out=z2[:, :], in_=z2[:, :], func=AF.Ln, bias=1.0, scale=1.0),
                 reads=[z2], writes=[z2])
            s.op("act", lambda: nc.scalar.activation(out=z2[:, :], in_=z2[:, :], func=AF.Exp, scale=-1.0),
                 reads=[z2], writes=[z2])
            ge = ge_rr.next()
            s.op("dve", lambda: nc.vector.tensor_tensor(out=ge[:, :], in0=z2[:, :], in1=zs[:, :], op=ALU.mult),
                 reads=[z2, zs], writes=[ge])
            sqv = f_rr.next()
            st = stat_rr.next()
            s.op("act", lambda: nc.scalar.activation(out=sqv[:, 0:256], in_=ge[:, 256:512], func=AF.Square),
                 reads=[ge], writes=[sqv])
            s.op("dve", lambda: nc.vector.tensor_reduce(out=st[:, 0:4], in_=sqv[:, 0:256].rearrange("p (g d) -> p g d", g=4),
                                                        axis=AX.X, op=ALU.add), reads=[sqv], writes=[st])
            s.op("act", lambda: nc.scalar.activation(out=st[:, 0:4], in_=st[:, 0:4], func=AF.Ln, bias=epsc[:, 0:1],
                                                     scale=1.0 / 64.0), reads=[st, epsc], writes=[st])
            s.op("act", lambda: nc.scalar.activation(out=st[:, 0:4], in_=st[:, 0:4], func=AF.Exp, scale=-0.5),
                 reads=[st], writes=[st])
            vn = vn_rr.next()
            for gi in range(4):
                s.op("dve", lambda: nc.vector.scalar_tensor_tensor(
                    out=vn[:, gi * 64:(gi + 1) * 64], in0=ge[:, 256 + gi * 64:256 + (gi + 1) * 64], scalar=st[:, gi:gi + 1],
                    in1=vg[:, gi * 64:(gi + 1) * 64], op0=ALU.mult, op1=ALU.mult), reads=[ge, st, vg], writes=[vn])

            def back2(vn=vn, ge=ge, r0=r0):
                pq = pq_rr.next()
                for gi in range(4):
                    s.op("pe", lambda: nc.tensor.matmul(pq[:, gi * 64:(gi + 1) * 64], lhsT=wm[gi][:, :],
                                                        rhs=vn[:, gi * 64:(gi + 1) * 64], start=True, stop=True),
                         reads=[wm[gi], vn], writes=[pq], inc=(gi == 3))
                ot = od_rr.next()
                for gi in range(4):
                    s.op("dve", lambda: nc.vector.scalar_tensor_tensor(
                        out=ot[:, gi * 64:(gi + 1) * 64], in0=pq[:, gi * 64:(gi + 1) * 64], scalar=bcol[:, gi:gi + 1],
                        in1=ge[:, gi * 64:(gi + 1) * 64], op0=ALU.add, op1=ALU.mult), reads=[pq, bcol, ge], writes=[ot])
                s.dma("sp", od[r0:r0 + 128, :], ot[:, :], reads=[ot])
            pipe.push(back2)
    pipe.flush()
    s.release(m_)


def _bf(a):
    import ml_dtypes
    return np.ascontiguousarray(a).astype(ml_dtypes.bfloat16)


def proj_consts():
    blk = np.zeros((2, 128, 128), np.float32)
    for i in range(128):
        for j in range(128):
            if i // 32 == j // 32:
                blk[0, i, j] = 1
            if i // 64 == j // 64:
                blk[1, i, j] = 1
    triu = np.triu(np.ones((128, 128), np.float32))
    return dict(ident=_bf(np.eye(128, dtype=np.float32)), blk=_bf(blk), triu=triu)


def proj_params(g, w_in, dq, dk, fq, fk, nq, nk, vgain, w_s, b_s):
    gains = np.stack([np.tile(dq, 4), np.tile(dk, 4), np.tile(fq, 2), np.tile(fk, 2), np.tile(nq, 2), np.tile(nk, 2)], 1)
    return dict(g=np.ascontiguousarray(g), w_in=np.ascontiguousarray(w_in), gains=np.ascontiguousarray(gains, dtype=np.float32),
                vgain=np.ascontiguousarray(np.broadcast_to(vgain[None, :], (128, 256))),
                wsT=np.ascontiguousarray(w_s.transpose(0, 2, 1)), bsT=np.ascontiguousarray(b_s.T))


NEG = -30000.0


def load_vt(s, vt, io, key, T, init=True, load=True):
    nc = s.nc
    NT = T // 128
    if init:
        s.op("pool", lambda: nc.gpsimd.memset(vt[:, :, 64:128], 0.0), writes=[vt])
        s.op("pool", lambda: nc.gpsimd.memset(vt[:, :, 64:65], 1.0), writes=[vt])
    if not load:
        return
    if key + "_src" in io:
        src = io[key + "_src"]
        step = 8
        for j0 in range(0, NT, step):
            j1 = min(NT, j0 + step)
            s.dma("sp", vt[:, j0:j1, 0:64], src[j0 * 128:j1 * 128, :].rearrange("(j p) d -> p j d", p=128), writes=[vt])
    else:
        s.dma("sp", vt[:, :, 0:64], io[key][:, :, 0:64], writes=[vt])


class Pipe:
    def __init__(self, lag):
        self.q = []
        self.lag = lag

    def push(self, fn):
        self.q.append(fn)
        while len(self.q) > self.lag:
            self.q.pop(0)()

    def flush(self):
        while self.q:
            self.q.pop(0)()


def emit_attn_phase(s, cm, T, nsub, qT, kT, vt, kparts, out_dram, finalize, bias_fn=None, name="a", lag=2):
    nc = s.nc
    NQB = T // 512
    pipe = Pipe(lag)
    fin_pending = None
    for qb in range(NQB):
        q0 = qb * 512
        pos = [cm["po_rr"].next() for _ in range(nsub)]
        nt = 4 * qb + 4
        njob = 0
        for t in range(nt):
            di = t - 4 * qb
            c0 = 128 * di if di > 0 else 0
            for i in range(nsub):
                kz = kT[i]
                ps = cm["ps_rr"].next()
                s.op("pe", lambda: nc.tensor.matmul(ps[:, c0:512], lhsT=kz[:, t * 128:(t + 1) * 128],
                                                    rhs=qT[:, q0 + c0:q0 + 512], start=True, stop=(di < 0)),
                     reads=[kz, qT], writes=[ps], inc=(di < 0))
                if di >= 0:
                    s.op("pe", lambda: nc.tensor.matmul(ps[:, c0:c0 + 128], lhsT=cm["ident_b"][:, :], rhs=cm["tri_b"][:, :],
                                                        start=False, stop=True),
                         reads=[cm["ident_b"], cm["tri_b"]], writes=[ps])
                pt = cm["pt_rr"].next()
                if bias_fn is None:
                    s.op("act", lambda: nc.scalar.activation(out=pt[:, c0:512], in_=ps[:, c0:512], func=AF.Exp),
                         reads=[ps], writes=[pt])
                else:
                    bb, bap = bias_fn(qb, t)
                    s.op("act", lambda: nc.scalar.activation(out=pt[:, c0:512], in_=ps[:, c0:512], func=AF.Exp, bias=bap),
                         reads=[ps, bb], writes=[pt])

                def pv(po=pos[i], t=t, c0=c0, pt=pt, nt=nt):
                    s.op("pe", lambda: nc.tensor.matmul(po[:, c0:512], lhsT=vt[:, t, :], rhs=pt[:, c0:512],
                                                        start=(t == 0), stop=(t == nt - 1)),
                         reads=[vt, pt], writes=[po])
                pipe.push(pv)
                njob += 1
                if fin_pending is not None and njob == lag:
                    fin_pending()
                    fin_pending = None
        pipe.flush()
        if fin_pending is not None:
            fin_pending()
        fin_pending = (lambda qb=qb, pos=pos: finalize(qb, pos))
    if fin_pending is not None:
        fin_pending()


def emit_o_to_tokmajor(s, cm, po, pf, col0):
    nc = s.nc
    oc = cm["oc_rr"].next()
    s.op("dve", lambda: nc.vector.tensor_copy(out=oc[0:65, :], in_=po[0:65, :]), reads=[po], writes=[oc])
    for j in range(4):
        s.op("pe", lambda: nc.tensor.transpose(out=pf[:, j, col0:col0 + 65], in_=oc[0:65, j * 128:(j + 1) * 128],
                                               identity=cm["ident_f"][0:65, 0:65]),
             reads=[oc, cm["ident_f"]], writes=[pf], inc=(j == 3))


def build_mix_ab(T):
    nc = bass.Bass("TRN2", target_bir_lowering=False)
    io = mix_decl(nc, T, with_c=False)
    s = S(nc)
    cm = mix_common(s, io)
    emit_mix_a(s, cm, io, T)
    emit_mix_b(s, cm, io, T)
    s.finish()
    s.close()
    return nc


def mix_decl(nc, T, with_c=True):
    NT = T // 128
    io = dict(
        identb=dram_in(nc, "identb", [128, 128], BF16), identf=dram_in(nc, "identf", [128, 128]),
        trib=dram_in(nc, "trib", [128, 128], BF16),
        qa=dram_in(nc, "qa", [64, T], BF16), ka=dram_in(nc, "ka", [64, T], BF16), va=dram_in(nc, "va", [128, NT, 65], BF16),
        lamp=dram_in(nc, "lamp", [128, 4, 32]), lami=dram_in(nc, "lami", [128, 2]),
        qb=dram_in(nc, "qb", [64, T], BF16), kb=dram_in(nc, "kb", [64, T], BF16), vb=dram_in(nc, "vb", [128, NT, 65], BF16),
        flog=dram_in(nc, "flog", [128, NT]), fbias=dram_in(nc, "fbias", [128, 1]),
        triuf=dram_in(nc, "triuf", [128, 128]), onesf=dram_in(nc, "onesf", [128, 128]),
        oa=dram_out(nc, "oa", [T, 64], BF16), ob=dram_out(nc, "ob", [T, 64], BF16),
    )
    return io


def mix_common(s, io, n_ps=3, with_pf2=True):
    nc = s.nc
    cm = {}
    for nm, key, dt in (("ident_b", "identb", BF16), ("ident_f", "identf", F32), ("tri_b", "trib", BF16)):
        b = s.sb(nm, [128, 128], dt)
        s.dma("sp", b[:, :], io[key][:, :], writes=[b])
        cm[nm] = b
    cm["epsc"] = s.sb("epsc", [128, 1], F32)
    s.op("dve", lambda: nc.vector.memset(cm["epsc"][:, :], EPS), writes=[cm["epsc"]])
    cm["ps_rr"] = RR([s.ps("ps%d" % j, [128, 512], F32) for j in range(n_ps)])
    cm["po_rr"] = RR([s.ps("po%d" % j, [128, 512], F32) for j in range(3)])
    cm["pf"] = s.ps("pf", [128, 4, 128], F32)
    if with_pf2:
        cm["pf2"] = s.ps("pf2", [128, 4, 128], F32)
    cm["lag"] = n_ps - 1
    cm["o1s_rr"] = RR([s.sb("o1s%d" % j, [128, 4, 65], F32) for j in range(2)])
    cm["pt_rr"] = RR([s.sb("pt%d" % j, [128, 512], BF16) for j in range(n_ps + 2)])
    cm["oc_rr"] = RR([s.sb("oc%d" % j, [128, 512], F32) for j in range(2)])
    cm["st_rr"] = RR([s.sb("mst%d" % j, [128, 8], F32) for j in range(8)])
    cm["ot_rr"] = RR([s.sb("ot%d" % j, [128, 4, 64], BF16) for j in range(2)])
    cm["tmp_rr"] = RR([s.sb("tmp%d" % j, [128, 64], F32) for j in range(4)])
    return cm


def emit_mix_a(s, cm, io, T, stage="all", bufs=None):
    nc = s.nc
    NT = T // 128
    if stage in ("all", "alloc"):
        if stage == "all":
            m = s.mark()
        b = dict(qT=s.sb("a_q", [128, T], BF16), k1=s.sb("a_k1", [128, T], BF16), k2=s.sb("a_k2", [128, T], BF16),
                 vt=s.sb("a_v", [128, NT, 128], BF16), lp=s.sb("lp", [128, 4, 32], F32), li=s.sb("li", [128, 2], F32),
                 lw=s.sb("lw", [128, 2, 32], F32), lam=s.sb("lam", [128, 4], F32))
        s.op("pool", lambda: nc.gpsimd.memset(b["qT"][64:128, :], 0.0), writes=[b["qT"]])
        s.op("dve", lambda: nc.vector.memset(b["k1"][:, :], 0.0), writes=[b["k1"]])
        s.op("pool", lambda: nc.gpsimd.memset(b["k2"][:, :], 0.0), writes=[b["k2"]])
        load_vt(s, b["vt"], io, "va", T, init=True, load=False)
        if stage == "alloc":
            return b
        bufs = b
    qT, k1, k2, vt, lp, li, lw, lam = (bufs[k] for k in ("qT", "k1", "k2", "vt", "lp", "li", "lw", "lam"))
    if stage in ("all", "load"):
        s.dma("sp", qT[0:64, :], io["qa"][:, :], writes=[qT])
        s.dma("sp", k1[0:32, :], io["ka"][0:32, :], writes=[k1])
        s.dma("sp", k2[32:64, :], io["ka"][32:64, :], writes=[k2])
        load_vt(s, vt, io, "va", T, init=False)
        s.dma("sp", lp[:, :, :], io["lamp"][:, :, :], writes=[lp])
        s.dma("sp", li[:, :], io["lami"][:, :], writes=[li])
        s.op("dve", lambda: nc.vector.tensor_tensor(out=lw[:, 0, :], in0=lp[:, 0, :], in1=lp[:, 1, :], op=ALU.mult),
             reads=[lp], writes=[lw])
        s.op("dve", lambda: nc.vector.tensor_tensor(out=lw[:, 1, :], in0=lp[:, 2, :], in1=lp[:, 3, :], op=ALU.mult),
             reads=[lp], writes=[lw])
        s.op("dve", lambda: nc.vector.tensor_reduce(out=lam[:, 0:2], in_=lw[:, :, :], axis=AX.X, op=ALU.add),
             reads=[lw], writes=[lam])
        s.op("act", lambda: nc.scalar.activation(out=lam[:, 0:2], in_=lam[:, 0:2], func=AF.Exp), reads=[lam], writes=[lam])
        s.op("dve", lambda: nc.vector.tensor_tensor(out=lam[:, 2:3], in0=lam[:, 1:2], in1=lam[:, 0:1], op=ALU.subtract),
             reads=[lam], writes=[lam])
        s.op("dve", lambda: nc.vector.tensor_tensor(out=lam[:, 3:4], in0=lam[:, 2:3], in1=li[:, 0:1], op=ALU.subtract),
             reads=[lam, li], writes=[lam])
        if stage == "load":
            return

    def fin(qb, pos):
        if "pf2" in cm:
            pf = cm["pf"]
            pf2 = cm["pf2"]
            emit_o_to_tokmajor(s, cm, pos[0], pf, 0)
            emit_o_to_tokmajor(s, cm, pos[1], pf2, 0)
        else:
            pf2 = cm["pf"]
            emit_o_to_tokmajor(s, cm, pos[0], pf2, 0)
            pf = cm["o1s_rr"].next()
            s.op("dve", lambda: nc.vector.tensor_copy(out=pf[:, :, :], in_=pf2[:, :, 0:65]), reads=[pf2], writes=[pf])
            emit_o_to_tokmajor(s, cm, pos[1], pf2, 0)
        ot = cm["ot_rr"].next()
        for j in range(4):
            st = cm["st_rr"].next()
            s.op("dve", lambda: nc.vector.tensor_scalar(out=st[:, 0:1], in0=pf[:, j, 64:65], scalar1=1e-30, scalar2=None,
                                                        op0=ALU.max), reads=[pf], writes=[st])
            s.op("dve", lambda: nc.vector.tensor_scalar(out=st[:, 1:2], in0=pf2[:, j, 64:65], scalar1=1e-30, scalar2=None,
                                                        op0=ALU.max), reads=[pf2], writes=[st])
            s.op("dve", lambda: nc.vector.reciprocal(out=st[:, 0:2], in_=st[:, 0:2]), reads=[st], writes=[st])
            t2 = cm["tmp_rr"].next()
            o = cm["tmp_rr"].next()
            s.op("dve", lambda: nc.vector.tensor_scalar(out=t2[:, :], in0=pf2[:, j, 0:64], scalar1=st[:, 1:2],
                                                        scalar2=lam[:, 3:4], op0=ALU.mult, op1=ALU.mult),
                 reads=[pf2, st, lam], writes=[t2])
            s.op("dve", lambda: nc.vector.scalar_tensor_tensor(out=o[:, :], in0=pf[:, j, 0:64], scalar=st[:, 0:1], in1=t2[:, :],
                                                               op0=ALU.mult, op1=ALU.add), reads=[pf, st, t2], writes=[o])
            s.op("act", lambda: nc.scalar.activation(out=t2[:, :], in_=o[:, :], func=AF.Square, accum_out=st[:, 2:3]),
                 reads=[o], writes=[t2, st])
            s.op("act", lambda: nc.scalar.activation(out=st[:, 3:4], in_=st[:, 2:3], func=AF.Ln, bias=cm["epsc"][:, 0:1],
                                                     scale=1.0 / 64.0), reads=[st, cm["epsc"]], writes=[st])
            s.op("act", lambda: nc.scalar.activation(out=st[:, 4:5], in_=st[:, 3:4], func=AF.Exp, scale=-0.5),
                 reads=[st], writes=[st])
            s.op("dve", lambda: nc.vector.tensor_scalar(out=ot[:, j, :], in0=o[:, :], scalar1=st[:, 4:5], scalar2=li[:, 1:2],
                                                        op0=ALU.mult, op1=ALU.mult), reads=[o, st, li], writes=[ot])
        s.dma("sp", io["oa"][qb * 512:(qb + 1) * 512, :].rearrange("(j p) d -> p j d", p=128), ot[:, :, :], reads=[ot])

    emit_attn_phase(s, cm, T, 2, qT, [k1, k2], vt, None, io["oa"], fin, name="a", lag=cm["lag"])
    if stage == "all":
        s.release(m)


def emit_mix_b(s, cm, io, T, stage="all", bufs=None):
    nc = s.nc
    NT = T // 128
    NQB = T // 512
    if stage in ("all", "alloc"):
        if stage == "all":
            m = s.mark()
        b = dict(qT=s.sb("b_q", [128, T], BF16), kT=s.sb("b_k", [128, T], BF16), vt=s.sb("b_v", [128, NT, 128], BF16),
                 fl=s.sb("fl", [128, NT], F32), fb=s.sb("fb", [128, 2], F32), tu=s.sb("tu", [128, 128], F32),
                 on=s.sb("on", [128, 128], F32), cc=s.sb("cc", [128, NT], F32), inc=s.sb("inc", [128, NT], F32),
                 tmpc=s.sb("tmpc", [128, NT], F32), btab=s.sb("btab", [128, NQB, NT], F32))
        s.op("pool", lambda: nc.gpsimd.memset(b["qT"][64:128, :], 0.0), writes=[b["qT"]])
        s.op("dve", lambda: nc.vector.memset(b["kT"][64:128, :], 0.0), writes=[b["kT"]])
        load_vt(s, b["vt"], io, "vb", T, init=True, load=False)
        s.dma("sp", b["tu"][:, :], io["triuf"][:, :], writes=[b["tu"]])
        s.dma("sp", b["on"][:, :], io["onesf"][:, :], writes=[b["on"]])
        if stage == "alloc":
            return b
        bufs = b
    qT, kT, vt, fl, fb, tu, on, cc, inc_, tmpc, btab = (bufs[k] for k in ("qT", "kT", "vt", "fl", "fb", "tu", "on", "cc", "inc",
                                                                            "tmpc", "btab"))
    if stage in ("all", "load"):
        s.dma("sp", qT[0:64, :], io["qb"][:, :], writes=[qT])
        s.dma("sp", kT[0:64, :], io["kb"][:, :], writes=[kT])
        load_vt(s, vt, io, "vb", T, init=False)
        if stage == "load":
            return
    if "flog_sb" not in io:
        s.dma("sp", fl[:, :], io["flog"][:, :], writes=[fl])
    s.dma("sp", fb[:, 0:1], io["fbias"][:, :], writes=[fb])
    s.op("dve", lambda: nc.vector.tensor_scalar(out=fb[:, 1:2], in0=fb[:, 0:1], scalar1=-1.0, scalar2=None, op0=ALU.mult),
         reads=[fb], writes=[fb])
    if "flog_sb" in io:
        fsb, fap = io["flog_sb"]
        s.op("act", lambda: nc.scalar.activation(out=fl[:, :], in_=fap, func=AF.Exp, bias=fb[:, 1:2], scale=-1.0),
             reads=[fsb, fb], writes=[fl])
    else:
        s.op("act", lambda: nc.scalar.activation(out=fl[:, :], in_=fl[:, :], func=AF.Exp, bias=fb[:, 1:2], scale=-1.0),
             reads=[fl, fb], writes=[fl])
    s.op("act", lambda: nc.scalar.activation(out=fl[:, :], in_=fl[:, :], func=AF.Ln, bias=1.0, scale=1.0),
         reads=[fl], writes=[fl])
    pc = cm["pf"]
    pcv = pc[:, 0, :]
    s.op("pe", lambda: nc.tensor.matmul(pc[:, 0, 0:NT], lhsT=tu[:, :], rhs=fl[:, :], start=True, stop=True),
         reads=[tu, fl], writes=[pc])
    s.op("pe", lambda: nc.tensor.matmul(pc[:, 1, 0:NT], lhsT=on[:, :], rhs=fl[:, :], start=True, stop=True),
         reads=[on, fl], writes=[pc])
    s.op("dve", lambda: nc.vector.tensor_copy(out=inc_[:, :], in_=pc[:, 1, 0:NT]), reads=[pc], writes=[inc_])
    sh = 1
    while sh < NT:
        s.op("dve", lambda: nc.vector.tensor_copy(out=tmpc[:, :], in_=inc_[:, :]), reads=[inc_], writes=[tmpc])
        s.op("dve", lambda: nc.vector.tensor_tensor(out=inc_[:, sh:NT], in0=tmpc[:, sh:NT], in1=tmpc[:, 0:NT - sh], op=ALU.add),
             reads=[tmpc], writes=[inc_])
        sh *= 2
    s.op("dve", lambda: nc.vector.tensor_tensor(out=cc[:, :], in0=pc[:, 0, 0:NT], in1=inc_[:, :], op=ALU.add),
         reads=[pc, inc_], writes=[cc])
    s.op("dve", lambda: nc.vector.tensor_tensor(out=tmpc[:, :], in0=cc[:, :], in1=pc[:, 1, 0:NT], op=ALU.subtract),
         reads=[pc, cc], writes=[tmpc])
    for qb in range(NQB):
        s.op("dve", lambda: nc.vector.tensor_scalar(out=btab[:, qb, :], in0=tmpc[:, :], scalar1=inc_[:, 4 * qb + 1:4 * qb + 2],
                                                    scalar2=None, op0=ALU.subtract), reads=[tmpc, inc_], writes=[btab])

    def bias_fn(qb, t):
        return btab, btab[:, qb, t:t + 1]

    def fin(qb, pos):
        pf = cm["pf"]
        emit_o_to_tokmajor(s, cm, pos[0], pf, 0)
        ot = cm["ot_rr"].next()
        for j in range(4):
            st = cm["st_rr"].next()
            s.op("dve", lambda: nc.vector.tensor_scalar(out=st[:, 0:1], in0=pf[:, j, 64:65], scalar1=1e-30, scalar2=None,
                                                        op0=ALU.max), reads=[pf], writes=[st])
            s.op("dve", lambda: nc.vector.reciprocal(out=st[:, 0:1], in_=st[:, 0:1]), reads=[st], writes=[st])
            s.op("dve", lambda: nc.vector.tensor_scalar(out=ot[:, j, :], in0=pf[:, j, 0:64], scalar1=st[:, 0:1], scalar2=None,
                                                        op0=ALU.mult), reads=[pf, st], writes=[ot])
        s.dma("sp", io["ob"][qb * 512:(qb + 1) * 512, :].rearrange("(j p) d -> p j d", p=128), ot[:, :, :], reads=[ot])

    emit_attn_phase(s, cm, T, 1, qT, [kT], vt, None, io["ob"], fin, bias_fn=bias_fn, name="b", lag=cm["lag"])
    if stage == "all":
        s.release(m)


def mix_consts():
    k = np.arange(128)
    tri = np.where(k[:, None] > k[None, :], NEG, 0.0).astype(np.float32)
    return dict(identb=_bf(np.eye(128, dtype=np.float32)), identf=np.eye(128, dtype=np.float32), trib=_bf(tri),
                triuf=np.triu(np.ones((128, 128), np.float32)), onesf=np.ones((128, 128), np.float32))


def mix_decl_c(nc, io, T):
    NT = T // 128
    QL = NT // 4
    NCT = max(1, T // 2048)
    io.update(dict(
        qc=dram_in(nc, "qc", [128, QL, 512], BF16),
        kskw=dram_in(nc, "kskw", [128, T], BF16),
        vs=dram_in(nc, "vs", [128, NT, 65], BF16), vw=dram_in(nc, "vw", [128, NT, 65], BF16),
        kvin=dram_in(nc, "kvin", [128, T], BF16),
        w1=dram_in(nc, "w1", [2, 2048, 256]), b1=dram_in(nc, "b1", [128, 4]),
        peT=dram_in(nc, "peT", [128, 32]),
        w2=dram_in(nc, "w2", [2, 256, 64]), b2=dram_in(nc, "b2", [2, 64]), b2c=dram_in(nc, "b2c", [64, 1]),
        kgain=dram_in(nc, "kgain", [64, 1]),
        ng=dram_in(nc, "ng", [128, QL, 12]),
        cmask=dram_in(nc, "cmask", [128, QL, NCT, 128], BF16),
        smask=dram_in(nc, "smask", [128, 4, 128], BF16), wmask=dram_in(nc, "wmask", [128, 8, 128], BF16),
        impA=dram_in(nc, "impA", [128, QL, 128]), impB=dram_in(nc, "impB", [128, QL, 128]),
        emat=dram_in(nc, "emat", [128, NT, 128], BF16), ovl=dram_in(nc, "ovl", [128, NCT, 128], BF16),
        ones64=dram_in(nc, "ones64", [64, 64], BF16), onesrow=dram_in(nc, "onesrow", [1, 128], BF16),
        oc=dram_out(nc, "oc", [QL * 128, 256], BF16),
    ))
    return io


def emit_gelu(s, zin_ap, zin_b, out_ap, out_b, tmp, shape_sl):
    nc = s.nc
    t = tmp
    s.op("act", lambda: nc.scalar.activation(out=t[shape_sl], in_=zin_ap, func=AF.Square), reads=[zin_b], writes=[t])
    s.op("dve", lambda: nc.vector.tensor_scalar(out=t[shape_sl], in0=t[shape_sl], scalar1=0.044715, scalar2=1.0,
                                                op0=ALU.mult, op1=ALU.add), reads=[t], writes=[t])
    s.op("dve", lambda: nc.vector.tensor_tensor(out=t[shape_sl], in0=t[shape_sl], in1=zin_ap, op=ALU.mult),
         reads=[t, zin_b], writes=[t])
    s.op("act", lambda: nc.scalar.activation(out=t[shape_sl], in_=t[shape_sl], func=AF.Exp, scale=-GELU_C), reads=[t], writes=[t])
    s.op("dve", lambda: nc.vector.tensor_scalar(out=t[shape_sl], in0=t[shape_sl], scalar1=1.0, scalar2=None, op0=ALU.add),
         reads=[t], writes=[t])
    s.op("dve", lambda: nc.vector.reciprocal(out=t[shape_sl], in_=t[shape_sl]), reads=[t], writes=[t])
    s.op("dve", lambda: nc.vector.tensor_tensor(out=out_ap, in0=t[shape_sl], in1=zin_ap, op=ALU.mult),
         reads=[t, zin_b], writes=[out_b])


def emit_mix_c(s, cm, io, T, cs=None):
    fused = cs is not None
    cs = cs if fused else [None]
    nc = s.nc
    NT = T // 128
    QL = NT // 4
    NCT = max(1, T // 2048)
    Nc = T // 16 - 1
    NCP = NCT * 128 if Nc > 128 else 128
    NCW = min(Nc, 511)
    assert Nc <= 511
    m = s.mark()
    ident_b = cm["ident_b"]
    ps_l = cm["ps_rr"].items
    po_l = cm["po_rr"].items
    pf, pf2 = cm["pf"], cm["pf2"]

    def ld(name, shape, dt, src, q="sp"):
        b = s.sb(name, shape, dt)
        idx = tuple(slice(None) for _ in shape)
        s.dma(q, b[idx], src, writes=[b])
        return b

    qc = s.sb("c_q", [128, QL, 512], BF16)
    qc2 = s.sb("c_q2", [128, QL, 512], BF16)
    s.op("pool", lambda: nc.gpsimd.memset(qc[64:128, :, :], 0.0), writes=[qc])
    s.op("dve", lambda: nc.vector.memset(qc2[0:64, :, :], 0.0), writes=[qc2])
    kk = ld("c_kk", [128, T], BF16, io["kskw"][:, :])
    vs = s.sb("c_vs", [128, NT, 128], BF16)
    vw = s.sb("c_vw", [128, NT, 128], BF16)
    load_vt(s, vs, io, "vs", T)
    load_vt(s, vw, io, "vw", T)
    emat = ld("c_e", [128, NT, 128], BF16, io["emat"][:, :, :])
    ovl = ld("c_ovl", [128, NCT, 128], BF16, io["ovl"][:, :, :])
    smask = s.sb("c_sm", [128, 4, 128], BF16)
    wmask = s.sb("c_wm", [128, 8, 128], BF16)
    ngt = s.sb("c_ng", [128, QL, 12], F32)
    ones64 = ld("c_o64", [64, 64], BF16, io["ones64"][:, :])
    onesrow = ld("c_orow", [1, 128], BF16, io["onesrow"][:, :])
    kgain = ld("c_kg", [64, 1], F32, io["kgain"][:, :])
    b2c = ld("c_b2c", [64, 1], F32, io["b2c"][:, :])
    b1 = ld("c_b1", [128, 4], F32, io["b1"][:, :])

    ktc = s.sb("c_ktc", [128, NCP], BF16)
    vc = s.sb("c_vc", [128, NCT, 128], BF16)
    s.op("dve", lambda: nc.vector.memset(ktc[:, :], 0.0), writes=[ktc])
    s.op("dve", lambda: nc.vector.memset(vc[:, :, :], 0.0), writes=[vc])
    s.op("dve", lambda: nc.vector.memset(vc[:, :, 64:65], 1.0), writes=[vc])

    m2 = s.mark()
    kvin = ld("c_kvin", [128, T], BF16, io["kvin"][:, :])
    w1sb = s.sb("c_w1", [128, 32, 256], BF16)
    for x in range(2):
        s.dma("pool", w1sb[x * 64:(x + 1) * 64, :, :], io["w1"][x].rearrange("(j d) f -> d j f", d=64), writes=[w1sb])
    peT = s.sb("c_pe", [128, 32], BF16)
    s.dma("pool", peT[:, :], io["peT"][:, :], writes=[peT])
    w2sb = s.sb("c_w2", [128, 2, 2, 64], BF16)
    for x in range(2):
        s.dma("pool", w2sb[:, x, :, :], io["w2"][x].rearrange("(hh f) d -> f hh d", f=128), writes=[w2sb])
    b2row = s.sb("c_b2r", [1, 64], BF16)
    s.dma("pool", b2row[:, :], io["b2"][1:2, :], writes=[b2row])
    hacc = [ps_l[0], ps_l[1], ps_l[2], po_l[0]]
    pcol = po_l[1]
    for x in range(2):
        for hh in range(2):
            hp = hacc[x * 2 + hh]
            for j in range(32):
                s.op("pe", lambda: nc.tensor.matmul(hp[:, 0:NCW], lhsT=w1sb[x * 64:(x + 1) * 64, j, hh * 128:(hh + 1) * 128],
                                                    rhs=kvin[x * 64:(x + 1) * 64, j:j + 16 * (NCW - 1) + 1:16],
                                                    start=(j == 0), stop=(j == 31)),
                     reads=[w1sb, kvin], writes=[hp], inc=(j == 31))
            for j in range(32):
                s.op("pe", lambda: nc.tensor.matmul(pcol[:, x * 2 + hh:x * 2 + hh + 1],
                                                    lhsT=w1sb[x * 64:(x + 1) * 64, j, hh * 128:(hh + 1) * 128],
                                                    rhs=peT[x * 64:(x + 1) * 64, j:j + 1], start=(j == 0), stop=(j == 31)),
                     reads=[w1sb, peT], writes=[pcol], inc=(j == 31))
    hbias = s.sb("c_hb", [128, 4], F32)
    s.op("dve", lambda: nc.vector.tensor_tensor(out=hbias[:, :], in0=pcol[:, 0:4], in1=b1[:, :], op=ALU.add),
         reads=[pcol, b1], writes=[hbias])
    gh = []
    for x in range(2):
        for hh in range(2):
            k = x * 2 + hh
            z = s.sb("c_z%d" % k, [128, 512], F32)
            tmp = s.sb("c_zt%d" % k, [128, 512], F32)
            gb = s.sb("c_g%d" % k, [128, 512], BF16)
            s.op("act", lambda: nc.scalar.activation(out=z[:, 0:NCW], in_=hacc[k][:, 0:NCW], func=AF.Identity,
                                                     bias=hbias[:, k:k + 1], scale=1.0), reads=[hacc[k], hbias], writes=[z])
            emit_gelu(s, z[:, 0:NCW], z, gb[:, 0:NCW], gb, tmp, (slice(None), slice(0, NCW)))
            gh.append(gb)
    pk = po_l[2]
    for hh in range(2):
        s.op("pe", lambda: nc.tensor.matmul(pk[0:64, 0:NCW], lhsT=w2sb[:, 0, hh, :], rhs=gh[hh][:, 0:NCW],
                                            start=(hh == 0), stop=(hh == 1)), reads=[w2sb, gh[hh]], writes=[pk], inc=(hh == 1))
    kz = s.sb("c_kz", [64, 512], F32)
    ksq = s.sb("c_ksq", [64, 512], BF16)
    krs = s.sb("c_krs", [64, 512], F32)
    s.op("act", lambda: nc.scalar.activation(out=kz[:, 0:NCW], in_=pk[0:64, 0:NCW], func=AF.Identity, bias=b2c[:, 0:1], scale=1.0),
         reads=[pk, b2c], writes=[kz])
    s.op("act", lambda: nc.scalar.activation(out=ksq[:, 0:NCW], in_=kz[:, 0:NCW], func=AF.Square), reads=[kz], writes=[ksq])
    pq = ps_l[0]
    s.op("pe", lambda: nc.tensor.matmul(pq[0:64, 0:NCW], lhsT=ones64[:, :], rhs=ksq[:, 0:NCW], start=True, stop=True),
         reads=[ones64, ksq], writes=[pq])
    s.op("act", lambda: nc.scalar.activation(out=krs[:, 0:NCW], in_=pq[0:64, 0:NCW], func=AF.Ln, bias=cm["epsc"][0:64, 0:1],
                                             scale=1.0 / 64.0), reads=[pq, cm["epsc"]], writes=[krs])
    s.op("act", lambda: nc.scalar.activation(out=krs[:, 0:NCW], in_=krs[:, 0:NCW], func=AF.Exp, scale=-0.5), reads=[krs], writes=[krs])
    s.op("dve", lambda: nc.vector.scalar_tensor_tensor(out=ktc[0:64, 0:NCW], in0=kz[:, 0:NCW], scalar=kgain[:, 0:1], in1=krs[:, 0:NCW],
                                                       op0=ALU.mult, op1=ALU.mult), reads=[kz, kgain, krs], writes=[ktc])
    for nt in range(NCT):
        n0 = nt * 128
        nn = min(128, Nc - n0)
        pv = ps_l[1 + nt % 2]
        for hh in range(2):
            s.op("pe", lambda: nc.tensor.matmul(pv[0:nn, 0:64], lhsT=gh[2 + hh][:, n0:n0 + nn], rhs=w2sb[:, 1, hh, :],
                                                start=(hh == 0), stop=False), reads=[gh[2 + hh], w2sb], writes=[pv], inc=False)
        s.op("pe", lambda: nc.tensor.matmul(pv[0:nn, 0:64], lhsT=onesrow[0:1, 0:nn], rhs=b2row[0:1, :], start=False, stop=True),
             reads=[onesrow, b2row], writes=[pv])
        s.op("act", lambda: nc.scalar.copy(out=vc[0:nn, nt, 0:64], in_=pv[0:nn, 0:64]), reads=[pv], writes=[vc])
    s.release(m2)

    cmk_rr = RR([s.sb("c_cmk%d" % j, [128, NCT, 128], BF16) for j in range(2)])
    ia_rr = RR([s.sb("c_ia%d" % j, [128, 128], F32) for j in range(2)])
    ib_rr = RR([s.sb("c_ib%d" % j, [128, 128], F32) for j in range(2)])
    imp_rr = RR([s.sb("c_imp%d" % j, [128, 128], F32) for j in range(2)])
    imp2_rr = RR([s.sb("c_impb%d" % j, [128, 128], F32) for j in range(2)])
    m8_rr = RR([s.sb("c_m8%d" % j, [128, 16], F32) for j in range(2)])
    mbT_rr = RR([s.sb("c_mbT%d" % j, [128, 128], BF16) for j in range(2)])
    oco_rr = RR([s.sb("c_oc%d" % j, [128, 4, 64], F32) for j in range(2)])
    gw_rr = RR([s.sb("c_gw%d" % j, [128, 12], F32) for j in range(2)])
    oo_rr = RR([s.sb("c_oo%d" % j, [128, 4, 64], F32) for j in range(2)])
    ob_rr = RR([s.sb("c_ob%d" % j, [128, 4, 64], BF16) for j in range(2)])

    pipe = Pipe(2)

    def masked_tile(kbuf, prow, t, Q, masks, vbuf, vt_idx, po, first, last, extra=None):
        ps = cm["ps_rr"].next()
        nm = len(masks)
        s.op("pe", lambda: nc.tensor.matmul(ps[:, :], lhsT=kbuf[:, t * 128:(t + 1) * 128], rhs=Q,
                                            start=True, stop=(nm == 0)), reads=[kbuf, qc, qc2], writes=[ps], inc=(nm == 0))
        for mi, (la, lb, ra, rb) in enumerate(masks):
            for h in range(4):
                lastm = (mi == nm - 1 and h == 3)
                s.op("pe", lambda: nc.tensor.matmul(ps[:, h * 128:(h + 1) * 128], lhsT=la, rhs=ra, start=False, stop=lastm),
                     reads=[lb, rb], writes=[ps], inc=lastm)
        pt = cm["pt_rr"].next()
        s.op("act", lambda: nc.scalar.activation(out=pt[:, :], in_=ps[:, :], func=AF.Exp), reads=[ps], writes=[pt])

        def back(pt=pt, po=po, vbuf=vbuf, vt_idx=vt_idx, first=first, last=last, extra=extra):
            s.op("pe", lambda: nc.tensor.matmul(po[:, :], lhsT=vbuf[:, vt_idx, :], rhs=pt[:, :], start=first, stop=last),
                 reads=[vbuf, pt], writes=[po])
            if extra is not None:
                extra(pt)
        pipe.push(back)

    for ci in cs:
        def gk(key):
            return io[key][ci] if fused else io[key]
        if fused:
            for h in range(4):
                r0 = (h % 2) * 64
                srcq = io["zq"][h // 2][r0:r0 + 64, :].rearrange("d (i c q) -> d i c q", c=4, q=128)[:, :, ci, :]
                s.dma("sp", qc[0:64, :, h * 128:(h + 1) * 128], srcq, writes=[qc])
                s.dma("sp", qc2[64:128, :, h * 128:(h + 1) * 128], srcq, writes=[qc2])
            msb, mview = io["misc_sb"]
            s.op("act", lambda: nc.scalar.activation(out=ngt[:, :, :], in_=mview[:, ci:NT:4, 4:16], func=AF.Exp, scale=-1.0),
                 reads=[msb], writes=[ngt])
        else:
            s.dma("sp", qc[0:64, :, :], io["qc"][0:64, :, :], writes=[qc])
            s.dma("sp", qc2[64:128, :, :], io["qc"][64:128, :, :], writes=[qc2])
            s.dma("sp", ngt[:, :, :], io["ng"][:, :, :], writes=[ngt])
            s.op("act", lambda: nc.scalar.activation(out=ngt[:, :, :], in_=ngt[:, :, :], func=AF.Exp, scale=-1.0),
                 reads=[ngt], writes=[ngt])
        s.op("dve", lambda: nc.vector.tensor_scalar(out=ngt[:, :, :], in0=ngt[:, :, :], scalar1=1.0, scalar2=None, op0=ALU.add),
             reads=[ngt], writes=[ngt])
        s.op("dve", lambda: nc.vector.reciprocal(out=ngt[:, :, :], in_=ngt[:, :, :]), reads=[ngt], writes=[ngt])
        s.dma("sp", smask[:, :, :], gk("smask")[:, :, :], writes=[smask])
        s.dma("sp", wmask[:, :, :], gk("wmask")[:, :, :], writes=[wmask])
        for i in range(QL):
            Qlo = qc[:, i, :]
            Qhi = qc2[:, i, :]
            cmk = cmk_rr.next()
            ia = ia_rr.next()
            ib = ib_rr.next()
            s.dma("sp", cmk[:, :, :], gk("cmask")[:, i, :, :], writes=[cmk])
            s.dma("sp", ia[:, :], gk("impA")[:, i, :], writes=[ia])
            s.dma("sp", ib[:, :], gk("impB")[:, i, :], writes=[ib])
            po_c, po_s, po_w = po_l[0], po_l[1], po_l[2]
            nct = min(NCT, i // 4 + 1)
            for nt in range(nct):
                def imp_mm(pt, nt=nt, nct=nct):
                    for h in range(4):
                        s.op("pe", lambda: nc.tensor.matmul(pf2[:, h, :], lhsT=pt[:, h * 128:(h + 1) * 128], rhs=ovl[:, nt, :],
                                                            start=(nt == 0 and h == 0), stop=(nt == nct - 1 and h == 3),
                                                            skip_group_check=True), reads=[pt, ovl], writes=[pf2],
                             inc=(h == 3))
                masked_tile(ktc, (0, 64), nt, Qlo, [(ident_b[:, :], ident_b, cmk[:, nt, :], cmk)], vc, nt, po_c,
                            nt == 0, nt == nct - 1, extra=imp_mm)
            pipe.flush()
            emit_o_to_tokmajor(s, cm, po_c, pf, 0)
            st = cm["st_rr"].next()
            rsum = cm["st_rr"].next()
            gw = gw_rr.next()
            s.op("dve", lambda: nc.vector.tensor_scalar(out=st[:, 0:4], in0=pf[:, :, 64], scalar1=1e-30, scalar2=None, op0=ALU.max),
                 reads=[pf], writes=[st])
            s.op("dve", lambda: nc.vector.reciprocal(out=rsum[:, 0:4], in_=st[:, 0:4]), reads=[st], writes=[rsum])
            oco = oco_rr.next()
            s.op("dve", lambda: nc.vector.tensor_copy(out=oco[:, :, :], in_=pf[:, :, 0:64]), reads=[pf], writes=[oco])
            imp = imp_rr.next()
            s.op("dve", lambda: nc.vector.tensor_scalar(out=imp[:, :], in0=pf2[:, 0, :], scalar1=rsum[:, 0:1], scalar2=None, op0=ALU.mult),
                 reads=[pf2, rsum], writes=[imp])
            for h in range(1, 4):
                s.op("dve", lambda: nc.vector.scalar_tensor_tensor(out=imp[:, :], in0=pf2[:, h, :], scalar=rsum[:, h:h + 1], in1=imp[:, :],
                                                                   op0=ALU.mult, op1=ALU.add), reads=[pf2, rsum, imp], writes=[imp])
            s.op("dve", lambda: nc.vector.tensor_tensor(out=imp[:, :], in0=imp[:, :], in1=ia[:, :], op=ALU.mult), reads=[imp, ia], writes=[imp])
            s.op("dve", lambda: nc.vector.tensor_tensor(out=imp[:, :], in0=imp[:, :], in1=ib[:, :], op=ALU.add), reads=[imp, ib], writes=[imp])
            m8 = m8_rr.next()
            imp2 = imp2_rr.next()
            s.op("dve", lambda: nc.vector.max(out=m8[:, 0:8], in_=imp[:, :]), reads=[imp], writes=[m8])
            s.op("dve", lambda: nc.vector.match_replace(out=imp2[:, :], in_to_replace=m8[:, 0:8], in_values=imp[:, :], imm_value=-1e9),
                 reads=[imp, m8], writes=[imp2])
            s.op("dve", lambda: nc.vector.max(out=m8[:, 8:16], in_=imp2[:, :]), reads=[imp2], writes=[m8])
            s.op("dve", lambda: nc.vector.tensor_scalar(out=imp2[:, :], in0=imp[:, :], scalar1=m8[:, 15:16], scalar2=NEG,
                                                        op0=ALU.is_lt, op1=ALU.mult), reads=[imp, m8], writes=[imp2])
            tl = [4 * (i - 1) + u for u in range(8) if 4 * (i - 1) + u >= 0]
            for t in tl:
                u = t - 4 * (i - 1)
                masked_tile(kk, (64, 128), t, Qhi, [(ident_b[:, :], ident_b, wmask[:, u, :], wmask)], vw, t, po_w,
                            t == tl[0], t == tl[-1])
            ptr = cm["ps_rr"].next()
            s.op("pe", lambda: nc.tensor.transpose(out=ptr[:, 0:128], in_=imp2[:, :], identity=cm["ident_f"][:, :]),
                 reads=[imp2, cm["ident_f"]], writes=[ptr])
            mbT = mbT_rr.next()
            s.op("dve", lambda: nc.vector.tensor_copy(out=mbT[:, :], in_=ptr[:, 0:128]), reads=[ptr], writes=[mbT])
            nts = 4 * i + 4
            for t in range(nts):
                masks = [(emat[:, t, :], emat, mbT[:, :], mbT)]
                if t >= 4 * i:
                    masks.append((ident_b[:, :], ident_b, smask[:, t - 4 * i, :], smask))
                masked_tile(kk, (0, 64), t, Qlo, masks, vs, t, po_s, t == 0, t == nts - 1)
            pipe.flush()
            emit_o_to_tokmajor(s, cm, po_s, pf, 0)
            st2 = cm["st_rr"].next()
            s.op("dve", lambda: nc.vector.tensor_scalar(out=st2[:, 0:4], in0=pf[:, :, 64], scalar1=1e-30, scalar2=None, op0=ALU.max),
                 reads=[pf], writes=[st2])
            s.op("dve", lambda: nc.vector.reciprocal(out=st2[:, 0:4], in_=st2[:, 0:4]), reads=[st2], writes=[st2])
            gv = ngt[:, i, :].rearrange("p (h b) -> p h b", b=3)
            gwv = gw[:, :].rearrange("p (h b) -> p h b", b=3)
            s.op("dve", lambda: nc.vector.tensor_tensor(out=gwv[:, :, 0], in0=gv[:, :, 0], in1=rsum[:, 0:4], op=ALU.mult),
                 reads=[ngt, rsum], writes=[gw])
            s.op("dve", lambda: nc.vector.tensor_tensor(out=gwv[:, :, 1], in0=gv[:, :, 1], in1=st2[:, 0:4], op=ALU.mult),
                 reads=[ngt, st2], writes=[gw])
            oo = oo_rr.next()
            for h in range(4):
                s.op("dve", lambda: nc.vector.tensor_scalar(out=oo[:, h, :], in0=oco[:, h, :], scalar1=gw[:, 3 * h:3 * h + 1], scalar2=None,
                                                            op0=ALU.mult), reads=[oco, gw], writes=[oo])
                s.op("dve", lambda: nc.vector.scalar_tensor_tensor(out=oo[:, h, :], in0=pf[:, h, 0:64], scalar=gw[:, 3 * h + 1:3 * h + 2],
                                                                   in1=oo[:, h, :], op0=ALU.mult, op1=ALU.add), reads=[pf, gw, oo], writes=[oo])
            emit_o_to_tokmajor(s, cm, po_w, pf, 0)
            st3 = cm["st_rr"].next()
            s.op("dve", lambda: nc.vector.tensor_scalar(out=st3[:, 0:4], in0=pf[:, :, 64], scalar1=1e-30, scalar2=None, op0=ALU.max),
                 reads=[pf], writes=[st3])
            s.op("dve", lambda: nc.vector.reciprocal(out=st3[:, 0:4], in_=st3[:, 0:4]), reads=[st3], writes=[st3])
            s.op("dve", lambda: nc.vector.tensor_tensor(out=gwv[:, :, 2], in0=gv[:, :, 2], in1=st3[:, 0:4], op=ALU.mult),
                 reads=[ngt, st3], writes=[gw])
            ob = ob_rr.next()
            for h in range(4):
                s.op("dve", lambda: nc.vector.scalar_tensor_tensor(out=ob[:, h, :], in0=pf[:, h, 0:64], scalar=gw[:, 3 * h + 2:3 * h + 3],
                                                                   in1=oo[:, h, :], op0=ALU.mult, op1=ALU.add), reads=[pf, gw, oo], writes=[ob])
            orow = ((4 * i + ci) if fused else i) * 128
            s.dma("sp", io["oc"][orow:orow + 128, :], ob[:, :, :].rearrange("p h d -> p (h d)"), reads=[ob])
    s.release(m)


def build_mix(T, parts="abc"):
    nc = bass.Bass("TRN2", target_bir_lowering=False)
    io = mix_decl(nc, T)
    if "c" in parts:
        mix_decl_c(nc, io, T)
    s = S(nc)
    cm = mix_common(s, io)
    if "a" in parts:
        emit_mix_a(s, cm, io, T)
    if "b" in parts:
        emit_mix_b(s, cm, io, T)
    if "c" in parts:
        emit_mix_c(s, cm, io, T)
    s.finish()
    s.close()
    return nc


def mix_consts_c(T, c):
    NT = T // 128
    QL = NT // 4
    NCT = max(1, T // 2048)
    Nc = T // 16 - 1
    NS = T // 64
    ar = np.arange(128)
    cmask = np.zeros((128, QL, NCT, 128), np.float32)
    impA = np.zeros((128, QL, 128), np.float32)
    impB = np.zeros((128, QL, 128), np.float32)
    for i in range(QL):
        qpos = 128 * (4 * i + c) + ar
        for nt in range(NCT):
            n = 128 * nt + ar
            ok = (16 * n[:, None] + 31 <= qpos[None, :]) & (n[:, None] < Nc)
            cmask[:, i, nt, :] = np.where(ok, 0.0, NEG)
        j = ar
        cur = qpos // 64
        forced = (j[None, :] == 0) | (j[None, :] == cur[:, None]) | (j[None, :] == cur[:, None] - 1)
        valid = (j[None, :] * 64 <= qpos[:, None]) & (j[None, :] < NS)
        impA[:, i, :] = (valid & ~forced).astype(np.float32)
        impB[:, i, :] = np.where(forced & (j[None, :] < NS), 1.0e4, np.where(valid, 0.0, -1.0))
    smask = np.zeros((128, 4, 128), np.float32)
    for u in range(4):
        kpos = 128 * u + ar
        qp = 128 * c + ar
        smask[:, u, :] = np.where(kpos[:, None] <= qp[None, :], 0.0, NEG)
    wmask = np.zeros((128, 8, 128), np.float32)
    for u in range(8):
        dist = 128 * (c + 4 - u) + ar[None, :] - ar[:, None]
        wmask[:, u, :] = np.where((dist >= 0) & (dist < 512), 0.0, NEG)
    emat = np.zeros((128, NT, 128), np.float32)
    for t in range(NT):
        for k in range(128):
            jj = 2 * t + k // 64
            if jj < 128:
                emat[jj, t, k] = 1.0
    ovl = np.zeros((128, NCT, 128), np.float32)
    for nt in range(NCT):
        n = 128 * nt + ar
        o = (n[:, None] * 16 < (ar[None, :] + 1) * 64) & (n[:, None] * 16 + 32 > ar[None, :] * 64) & (n[:, None] < Nc) \
            & (ar[None, :] < NS)
        ovl[:, nt, :] = o
    return dict(cmask=_bf(cmask), impA=impA, impB=impB, smask=_bf(smask), wmask=_bf(wmask), emat=_bf(emat), ovl=_bf(ovl),
                ones64=_bf(np.ones((64, 64), np.float32)), onesrow=_bf(np.ones((1, 128), np.float32)))


def build_merge(NT, TB=512):
    nc = bass.Bass("TRN2", target_bir_lowering=False)
    x = dram_in(nc, "x", [NT, D])
    g = dram_in(nc, "g", [D])
    w_in = dram_in(nc, "w_in", [D, 6800])
    w_br = dram_in(nc, "w_br", [4, 256, D])
    w_o = dram_in(nc, "w_o", [D, D])
    ident = dram_in(nc, "ident", [128, 128], BF16)
    obr = dram_in(nc, "obr", [NT, D], BF16)
    y = dram_out(nc, "y", [NT, D])
    s = S(nc)
    emit_merge(s, x, g, w_in, w_br, w_o, ident, obr, y, NT, TB)
    s.finish()
    s.close()
    return nc


def emit_merge(s, x, g, w_in, w_br, w_o, ident, obr, y, NT, TB=512):
    nc = s.nc
    m_ = s.mark()
    ntile = TB // 128
    ident_b = s.sb("ident_b", [128, 128], BF16)
    s.dma("sp", ident_b[:, :], ident[:, :], writes=[ident_b])
    gcol = s.sb("gcol", [128, NKC], F32)
    s.dma("sp", gcol[:, :], g.rearrange("(c p) -> p c", p=128), writes=[gcol], allow_slow_non_contiguous=True)
    epsc = s.sb("epsc", [128, 1], F32)
    s.op("dve", lambda: nc.vector.memset(epsc[:, :], EPS), writes=[epsc])
    wg = [s.sb("wg%d" % c, [128, 4096], BF16) for c in range(NKC)]
    wb = [s.sb("wb%d" % c, [128, D], BF16) for c in range(8)]
    wo = [s.sb("wo%d" % c, [128, D], BF16) for c in range(NKC)]
    for c in range(NKC):
        for hf in range(2):
            s.dma("pool", wg[c][:, hf * 2048:(hf + 1) * 2048], w_in[c * 128:(c + 1) * 128, 2704 + hf * 2048:2704 + (hf + 1) * 2048],
                  writes=[wg[c]])
    for n in range(4):
        for cc in range(2):
            s.dma("pool", wb[2 * n + cc][:, :], w_br[n, cc * 128:(cc + 1) * 128, :], writes=[wb[2 * n + cc]])
    for c in range(NKC):
        s.dma("pool", wo[c][:, :], w_o[c * 128:(c + 1) * 128, :], writes=[wo[c]])
    xn = [s.sb("xn%d" % j, [128, D], F32) for j in range(ntile)]
    xr_rr = RR([s.sb("xr%d" % j, [128, D], F32) for j in range(2)])
    ots = [[s.sb("ot%d_%d" % (k, j), [128, D], BF16) for j in range(ntile)] for k in range(2)]
    hb_rr = RR([s.sb("hb%d" % j, [128, D], BF16) for j in range(2 * ntile)])
    stat_rr = RR([s.sb("st%d" % j, [128, 16], F32) for j in range(4)])
    hT = s.sb("hT", [128, NKC, TB], BF16)
    oT = s.sb("oT", [128, 8, TB], BF16)
    mT = [s.sb("mT%d" % c, [128, TB], BF16) for c in range(8)]
    pT_rr = RR([s.ps("pT%d" % j, [128, TB], BF16) for j in range(2)])
    pg_rr = RR([s.ps("pg%d" % j, [128, 512], F32) for j in range(2)])
    pp_rr = RR([s.ps("pp%d" % j, [128, 512], F32) for j in range(2)])
    po_rr = RR([s.ps("po%d" % j, [128, 512], F32) for j in range(2)])
    sg_rr = RR([s.sb("sg%d" % j, [128, TB], F32) for j in range(3)])
    acc_rr = RR([s.sb("acc%d" % j, [128, TB], F32) for j in range(2)])
    nblk = NT // TB

    def prep_a(tb):
        ot = ots[tb % 2]
        for j in range(ntile):
            r0 = tb * TB + j * 128
            s.dma("sp", xn[j][:, :], x[r0:r0 + 128, :], writes=[xn[j]])
            s.dma("sp", ot[j][:, :], obr[r0:r0 + 128, :], writes=[ot[j]])
        return emit_norm(s, epsc, xn, hb_rr, None, stat_rr, ntile)

    def prep_b(tb, hbs):
        ot = ots[tb % 2]
        emit_transpose_T(s, hbs, gcol, hT, ident_b, pT_rr, ntile)
        for c in range(8):
            pT = pT_rr.next()
            for j in range(ntile):
                s.op("pe", lambda: nc.tensor.transpose(out=pT[:, j * 128:(j + 1) * 128], in_=ot[j][:, c * 128:(c + 1) * 128],
                                                       identity=ident_b[:, :]), reads=[ot[j], ident_b], writes=[pT], inc=(j == ntile - 1))
            if c % 2 == 0:
                s.op("dve", lambda: nc.vector.tensor_copy(out=oT[:, c, :], in_=pT[:, 0:TB]), reads=[pT], writes=[oT])
            else:
                s.op("act", lambda: nc.scalar.copy(out=oT[:, c, :], in_=pT[:, 0:TB]), reads=[pT], writes=[oT])

    hbs_next = prep_a(0)
    prep_b(0, hbs_next)
    for tb in range(nblk):
        t0 = tb * TB
        if tb + 1 < nblk:
            hbs_next = prep_a(tb + 1)
        for dc in range(8):
            acc = acc_rr.next()
            for n in range(4):
                pg = pg_rr.next()
                pp = pp_rr.next()
                for c in range(NKC):
                    s.op("pe", lambda: nc.tensor.matmul(pg[:, 0:TB], lhsT=wg[c][:, n * 1024 + dc * 128:n * 1024 + (dc + 1) * 128],
                                                        rhs=hT[:, c, :], start=(c == 0), stop=(c == NKC - 1)),
                         reads=[wg[c], hT], writes=[pg], inc=(c == NKC - 1))
                for cc in range(2):
                    s.op("pe", lambda: nc.tensor.matmul(pp[:, 0:TB], lhsT=wb[2 * n + cc][:, dc * 128:(dc + 1) * 128],
                                                        rhs=oT[:, 2 * n + cc, :], start=(cc == 0), stop=(cc == 1)),
                         reads=[wb[2 * n + cc], oT], writes=[pp], inc=(cc == 1))
                sg = sg_rr.next()
                s.op("act", lambda: nc.scalar.activation(out=sg[:, :], in_=pg[:, 0:TB], func=AF.Sigmoid), reads=[pg], writes=[sg])
                if n == 0:
                    s.op("dve", lambda: nc.vector.tensor_tensor(out=acc[:, :], in0=sg[:, :], in1=pp[:, 0:TB], op=ALU.mult),
                         reads=[sg, pp], writes=[acc])
                else:
                    s.op("dve", lambda: nc.vector.tensor_tensor(out=sg[:, :], in0=sg[:, :], in1=pp[:, 0:TB], op=ALU.mult),
                         reads=[sg, pp], writes=[sg])
                    if n < 3:
                        s.op("pool", lambda: nc.gpsimd.tensor_tensor(out=acc[:, :], in0=acc[:, :], in1=sg[:, :], op=ALU.add),
                             reads=[acc, sg], writes=[acc])
                    else:
                        s.op("pool", lambda: nc.gpsimd.tensor_tensor(out=mT[dc][:, :], in0=acc[:, :], in1=sg[:, :], op=ALU.add),
                             reads=[acc, sg], writes=[mT[dc]])
        if tb + 1 < nblk:
            prep_b(tb + 1, hbs_next)
        for j in range(ntile):
            xr = xr_rr.next()
            s.dma("sp", xr[:, :], x[t0 + j * 128:t0 + (j + 1) * 128, :], writes=[xr])
            for hf in range(2):
                po = po_rr.next()
                for dc in range(8):
                    s.op("pe", lambda: nc.tensor.matmul(po[:, :], lhsT=mT[dc][:, j * 128:(j + 1) * 128],
                                                        rhs=wo[dc][:, hf * 512:(hf + 1) * 512], start=(dc == 0), stop=(dc == 7)),
                         reads=[mT[dc], wo[dc]], writes=[po], inc=(dc == 7))
                s.op("dve", lambda: nc.vector.tensor_tensor(out=xr[:, hf * 512:(hf + 1) * 512], in0=po[:, :],
                                                            in1=xr[:, hf * 512:(hf + 1) * 512], op=ALU.add),
                     reads=[po, xr], writes=[xr])
            s.dma("sp", y[t0 + j * 128:t0 + (j + 1) * 128, :], xr[:, :], reads=[xr])
    s.release(m_)


PARAM_SHAPES = dict(
    ffn1_norm=("L", D), ffn1_w_in=("L", D, 2 * DFF), ffn1_w_out=("L", DFF, D), mix_norm=("L", D), w_in=("L", D, 6800),
    nsa_phi_w1=("L", 2, 2048, 256), nsa_phi_w2=("L", 2, 256, 64), nsa_phi_b2=("L", 2, 64),
    w_branch=("L", 4, 256, D), w_out=("L", D, D), ffn2_norm=("L", D), ffn2_w_in=("L", D, 2 * DFF), ffn2_w_out=("L", DFF, D),
    gains=("L", 128, 6), vgain=("L", 128, 256), wsT=("L", 4, 128, 128), bsT=("L", 128, 4), lamp=("L", 128, 4, 32),
    lami=("L", 128, 2), fbias=("L", 4, 128, 1), b1l=("L", 128, 4), peT=("L", 128, 32), b2c=("L", 64, 1), kgain=("L", 64, 1),
)


def fused_const_shapes(T):
    NT = T // 128
    QL = NT // 4
    NCT = max(1, T // 2048)
    return dict(
        ident=([128, 128], BF16), identf=([128, 128], F32), blk=([2, 128, 128], BF16), triu=([128, 128], F32),
        trib=([128, 128], BF16), triuf=([128, 128], F32), onesf=([128, 128], F32), ones64=([64, 64], BF16),
        onesrow=([1, 128], BF16), cmask=([4, 128, QL, NCT, 128], BF16), smask=([4, 128, 4, 128], BF16),
        wmask=([4, 128, 8, 128], BF16), impA=([4, 128, QL, 128], F32), impB=([4, 128, QL, 128], F32),
        emat=([128, NT, 128], BF16), ovl=([128, NCT, 128], BF16))


def fused_consts(T):
    pc = proj_consts()
    mc = mix_consts()
    cc = [mix_consts_c(T, c) for c in range(4)]
    d = dict(ident=pc["ident"], identf=mc["identf"], blk=pc["blk"], triu=pc["triu"], trib=mc["trib"], triuf=mc["triuf"],
             onesf=mc["onesf"], ones64=cc[0]["ones64"], onesrow=cc[0]["onesrow"], emat=cc[0]["emat"], ovl=cc[0]["ovl"])
    for k in ("cmask", "smask", "wmask", "impA", "impB"):
        d[k] = np.ascontiguousarray(np.stack([cc[c][k] for c in range(4)], 0))
    return d


def emit_mix_fused(s, F, l, T):
    nc = s.nc
    NT = T // 128
    m = s.mark()
    io0 = dict(identb=F["ident"], identf=F["identf"], trib=F["trib"])
    misc_sb = s.sb("misc_sb", [128, NT, 16], F32)
    m_ab = s.mark()
    cm = mix_common(s, io0, n_ps=4, with_pf2=False)
    for j0 in range(0, NT, 8):
        j1 = min(NT, j0 + 8)
        s.dma("sp", misc_sb[:, j0:j1, :], F["misc"][j0 * 128:j1 * 128, :].rearrange("(j p) c -> p j c", p=128), writes=[misc_sb])
    zfm, vab, obr = F["zfm"], F["vab"], F["obr"]

    def io_a(h):
        r0 = (h % 2) * 64
        io = dict(io0)
        io.update(qa=zfm[h // 2][r0:r0 + 64, :], ka=zfm[2 + h // 2][r0:r0 + 64, :], va_src=vab[:, h * 64:(h + 1) * 64],
                  lamp=F["lamp"][l], lami=F["lami"][l], oa=obr[:, h * 64:(h + 1) * 64])
        return io

    def io_b(h):
        r0 = (h % 2) * 64
        io = dict(io0)
        io.update(qb=zfm[4 + h // 2][r0:r0 + 64, :], kb=zfm[6 + h // 2][r0:r0 + 64, :],
                  vb_src=vab[:, 256 + h * 64:256 + (h + 1) * 64], flog_sb=(misc_sb, misc_sb[:, :, h]), fbias=F["fbias"][l, h],
                  triuf=F["triuf"], onesf=F["onesf"], ob=obr[:, 256 + h * 64:256 + (h + 1) * 64])
        return io

    ba = emit_mix_a(s, cm, io_a(0), T, stage="alloc")
    bb = emit_mix_b(s, cm, io_b(0), T, stage="alloc")
    emit_mix_a(s, cm, io_a(0), T, stage="load", bufs=ba)
    emit_mix_b(s, cm, io_b(0), T, stage="load", bufs=bb)
    for h in range(4):
        emit_mix_a(s, cm, io_a(h), T, stage="compute", bufs=ba)
        if h + 1 < 4:
            emit_mix_a(s, cm, io_a(h + 1), T, stage="load", bufs=ba)
        emit_mix_b(s, cm, io_b(h), T, stage="compute", bufs=bb)
        if h + 1 < 4:
            emit_mix_b(s, cm, io_b(h + 1), T, stage="load", bufs=bb)
    s.release(m_ab)
    cm = mix_common(s, io0, n_ps=3, with_pf2=True)
    io = dict(io0)
    io.update(zq=(zfm[8], zfm[9]), kskw=zfm[10], kvin=zfm[11], vs_src=F["vsw"][:, 0:64], vw_src=F["vsw"][:, 64:128],
              misc_sb=(misc_sb, misc_sb), w1=F["nsa_phi_w1"][l], b1=F["b1l"][l], peT=F["peT"][l], w2=F["nsa_phi_w2"][l],
              b2=F["nsa_phi_b2"][l], b2c=F["b2c"][l], kgain=F["kgain"][l], oc=obr[:, 512:768])
    for k in ("cmask", "smask", "wmask", "impA", "impB", "emat", "ovl", "ones64", "onesrow"):
        io[k] = F[k]
    emit_mix_c(s, cm, io, T, cs=[0, 1, 2, 3])
    s.release(m)


def build_fused(T, L):
    nc = bass.Bass("TRN2", target_bir_lowering=False)
    F = {}
    F["x"] = dram_in(nc, "x", [T, D])
    for k, shp in PARAM_SHAPES.items():
        F[k] = dram_in(nc, k, [L if v == "L" else v for v in shp])
    for k, (shp, dt) in fused_const_shapes(T).items():
        F[k] = dram_in(nc, k, shp, dt)
    y = dram_out(nc, "y", [T, D])
    for k, shp, dt in (("xa", [T, D], F32), ("xb", [T, D], F32), ("xc", [T, D], F32), ("zfm", [NFM, 128, T], BF16),
                       ("vab", [T, 512], BF16), ("vsw", [T, 128], BF16), ("misc", [T, 16], F32), ("obr", [T, D], BF16)):
        F[k] = nc.dram_tensor("s_" + k, shp, dt).ap()
    s = S(nc)
    for l in range(L):
        x_in = F["x"] if l == 0 else F["xc"]
        emit_ffn(s, x_in, F["ffn1_norm"][l], F["ffn1_w_in"][l], F["ffn1_w_out"][l], F["ident"], F["xa"], T)
        a = dict(x=F["xa"], g=F["mix_norm"][l], w_in=F["w_in"][l], ident=F["ident"], gains=F["gains"][l], blk=F["blk"],
                 vgain=F["vgain"][l], wsT=F["wsT"][l], triu=F["triu"], bsT=F["bsT"][l], zfm=F["zfm"], vab=F["vab"],
                 vsw=F["vsw"], misc=F["misc"], od=F["obr"][:, 768:1024])
        emit_proj(s, a, T)
        emit_mix_fused(s, F, l, T)
        emit_merge(s, F["xa"], F["mix_norm"][l], F["w_in"][l], F["w_branch"][l], F["w_out"][l], F["ident"], F["obr"], F["xb"], T)
        x_out = y if l == L - 1 else F["xc"]
        emit_ffn(s, F["xb"], F["ffn2_norm"][l], F["ffn2_w_in"][l], F["ffn2_w_out"][l], F["ident"], x_out, T)
    s.finish()
    s.close()
    return nc


def fused_params(P, L):
    import math
    f32 = np.float32
    A = lambda a: np.ascontiguousarray(np.asarray(a, dtype=f32))
    d = {k: A(P[k]) for k in ("ffn1_norm", "ffn1_w_in", "ffn1_w_out", "mix_norm", "w_in", "nsa_phi_w1", "nsa_phi_w2",
                              "nsa_phi_b2", "w_branch", "w_out", "ffn2_norm", "ffn2_w_in", "ffn2_w_out")}
    tile = lambda v, n: np.tile(A(v), (1, n))
    d["gains"] = np.ascontiguousarray(np.stack([tile(P["diff_q_gain"], 4), tile(P["diff_k_gain"], 4), tile(P["fox_q_gain"], 2),
                                                tile(P["fox_k_gain"], 2), tile(P["nsa_q_gain"], 2), tile(P["nsa_k_gain"], 2)], 2))
    d["vgain"] = np.ascontiguousarray(np.broadcast_to(A(P["gmlp_v_gain"])[:, None, :], (L, 128, 256)))
    d["wsT"] = np.ascontiguousarray(A(P["gmlp_w_s"]).transpose(0, 1, 3, 2))
    d["bsT"] = np.ascontiguousarray(A(P["gmlp_b_s"]).transpose(0, 2, 1))
    d["lamp"] = np.ascontiguousarray(np.broadcast_to(A(P["diff_lambda"])[:, None], (L, 128, 4, 32)))
    li = np.array([[0.8 - 0.6 * math.exp(-0.3 * l), 1.0 - (0.8 - 0.6 * math.exp(-0.3 * l))] for l in range(L)], f32)
    d["lami"] = np.ascontiguousarray(np.broadcast_to(li[:, None, :], (L, 128, 2)))
    d["fbias"] = np.ascontiguousarray(np.broadcast_to(A(P["fox_f_bias"])[:, :, None, None], (L, 4, 128, 1)))
    d["b1l"] = np.ascontiguousarray(A(P["nsa_phi_b1"]).reshape(L, 2, 2, 128).transpose(0, 3, 1, 2).reshape(L, 128, 4))
    pe = A(P["nsa_cmp_pe"])
    d["peT"] = np.ascontiguousarray(pe.transpose(0, 1, 3, 2).reshape(L, 128, 32))
    d["b2c"] = np.ascontiguousarray(A(P["nsa_phi_b2"])[:, 0, :, None])
    d["kgain"] = np.ascontiguousarray(A(P["nsa_k_gain"])[:, :, None])
    return d


B_, T_, L_ = 2, 8192, 2
_PROG = {}


def kernel(**inputs):
    x = np.ascontiguousarray(np.asarray(inputs["x"], dtype=np.float32))
    if "fused" not in _PROG:
        _PROG["fused"] = build_fused(T_, L_)
        _PROG["consts"] = fused_consts(T_)
    nc = _PROG["fused"]
    par = fused_params(inputs, L_)
    in_maps = []
    for b in range(B_):
        d = dict(par)
        d.update(_PROG["consts"])
        d["x"] = x[b]
        in_maps.append(d)
    res = run_bass_kernel_spmd(nc, in_maps, core_ids=list(range(B_)))
    return np.stack([np.asarray(res.results[b]["y"], dtype=np.float32) for b in range(B_)], 0)
```

```python
import numpy as np
import concourse.bass as bass
import concourse.mybir as mybir
from concourse.bass_utils import run_bass_kernel_spmd

F32 = mybir.dt.float32
BF16 = mybir.dt.bfloat16
AF = mybir.ActivationFunctionType
ALU = mybir.AluOpType
AX = mybir.AxisListType

ENGS = ("pe", "act", "dve", "pool", "sp")


class Buf:
    __slots__ = ("name", "t", "w", "r", "dsem", "dcnt", "uid")
    _n = 0

    def __init__(self, name, t):
        Buf._n += 1
        self.uid = Buf._n
        self.name = name
        self.t = t
        self.w = None
        self.r = []
        self.dsem = None
        self.dcnt = 0

    def __getitem__(self, idx):
        return self.t[idx]


class S:
    def __init__(self, nc, same_engine_sync=True):
        self.nc = nc
        self.e = {"pe": nc.tensor, "act": nc.scalar, "dve": nc.vector, "pool": nc.gpsimd, "sp": nc.sync}
        self.sem = {k: nc.alloc_semaphore("c_" + k) for k in ENGS}
        self.cnt = {k: 0 for k in ENGS}
        self.seen = {k: {} for k in ENGS}
        self.same = same_engine_sync
        self.nbuf = 0
        self.dma_sems = []
        self.ctx = []
        self.cbufs = []
        self.free_dsems = []

    def sb(self, name, shape, dt):
        self.nbuf += 1
        g = self.nc.sbuf_tensor("%s_%d" % (name, self.nbuf), list(shape), dt)
        t = g.__enter__()
        self.ctx.append(g)
        b = Buf(name, t)
        self.cbufs.append(b)
        return b

    def ps(self, name, shape, dt):
        self.nbuf += 1
        g = self.nc.psum_tensor("%s_%d" % (name, self.nbuf), list(shape), dt)
        t = g.__enter__()
        self.ctx.append(g)
        b = Buf(name, t)
        self.cbufs.append(b)
        return b

    def sub(self, name, ap):
        return Buf(name, ap)

    def mark(self):
        return len(self.ctx)

    def release(self, m):
        self.barrier()
        while len(self.ctx) > m:
            self.ctx.pop().__exit__(None, None, None)
            b = self.cbufs.pop()
            if b.dsem is not None:
                self.free_dsems.append((b.dsem, b.dcnt))
                self.dma_sems.remove(b)
                b.dsem = None

    def close(self):
        for g in reversed(self.ctx):
            g.__exit__(None, None, None)
        self.ctx = []
        self.cbufs = []

    def _need(self, E, deps):
        need = {}
        for d in deps:
            if d is None:
                continue
            if d[0] == "dma":
                b = d[1]
                key = ("dma", b.uid)
                need[key] = (b, b.dcnt)
            else:
                F, c = d
                if F == E and (not self.same or E == "pe" or c > self.cnt[E]):
                    continue
                if c > need.get(F, (None, 0))[1]:
                    need[F] = (None, c)
        for key, (b, c) in need.items():
            if self.seen[E].get(key, 0) >= c:
                continue
            self.seen[E][key] = c
            if b is not None:
                self.e[E].wait_ge(b.dsem, c)
            else:
                self.e[E].wait_ge(self.sem[key], c)

    def op(self, E, fn, reads=(), writes=(), inc=True):
        deps = []
        for b in reads:
            deps.append(b.w)
        for b in writes:
            deps.append(b.w)
            deps.extend(b.r)
        self._need(E, deps)
        ins = fn()
        c = self.cnt[E] + 1
        if inc:
            ins.then_inc(self.sem[E], 1)
            self.cnt[E] = c
        for b in writes:
            b.w = (E, c)
            b.r = []
        for b in reads:
            if b not in writes:
                b.r = [x for x in b.r if x[0] != E] + [(E, c)]
        return ins

    def dma(self, Q, out, in_, reads=(), writes=(), **kw):
        deps = []
        for b in reads:
            deps.append(b.w)
        for b in writes:
            deps.append(b.w)
            deps.extend(b.r)
        self._need(Q, deps)
        owner = (list(writes) + list(reads))[0]
        if owner.dsem is None:
            owner.dsem = self._dsem(owner)
            self.dma_sems.append(owner)
        ins = self.e[Q].dma_start(out=out, in_=in_, **kw)
        ins.then_inc(owner.dsem, 16)
        owner.dcnt += 16
        rec = ("dma", owner, owner.dcnt)
        for b in writes:
            b.w = rec
            b.r = []
        for b in reads:
            if b not in writes:
                b.r = b.r + [rec]
        return ins

    def cc(self, kind, groups, in_ap, out_ap, reads=(), writes=()):
        deps = []
        for b in reads:
            deps.append(b.w)
        for b in writes:
            deps.append(b.w)
            deps.extend(b.r)
        self._need("pool", deps)
        owner = list(writes)[0]
        if owner.dsem is None:
            owner.dsem = self._dsem(owner)
            self.dma_sems.append(owner)
        ins = self.nc.gpsimd.collective_compute(kind, op=ALU.bypass, replica_groups=groups, ins=[in_ap], outs=[out_ap])
        ins.then_inc(owner.dsem, 16)
        owner.dcnt += 16
        rec = ("dma", owner, owner.dcnt)
        for b in writes:
            b.w = rec
            b.r = []
        for b in reads:
            if b not in writes:
                b.r = b.r + [rec]
        return ins

    def _dsem(self, owner):
        if self.free_dsems:
            sem, cnt = self.free_dsems.pop()
            owner.dcnt = cnt
            return sem
        self.nsem = getattr(self, "nsem", 0) + 1
        return self.nc.alloc_semaphore("d_%d" % self.nsem)

    def barrier(self):
        for E in ENGS:
            for Fk in ENGS:
                if Fk == E:
                    continue
                c = self.cnt[Fk]
                if c and self.seen[E].get(Fk, 0) < c:
                    self.seen[E][Fk] = c
                    self.e[E].wait_ge(self.sem[Fk], c)
            for b in self.dma_sems:
                key = ("dma", b.uid)
                if b.dcnt and self.seen[E].get(key, 0) < b.dcnt:
                    self.seen[E][key] = b.dcnt
                    self.e[E].wait_ge(b.dsem, b.dcnt)

    def finish(self):
        self.barrier()


D = 1024
DFF = 2816
NFC = DFF // 128
NKC = D // 128
EPS = 1e-6


def dram_in(nc, name, shape, dt=F32):
    return nc.dram_tensor(name, list(shape), dt, kind="ExternalInput").ap()


def dram_out(nc, name, shape, dt=F32):
    return nc.dram_tensor(name, list(shape), dt, kind="ExternalOutput").ap()


class RR:
    def __init__(self, items):
        self.items = items
        self.i = 0

    def next(self):
        b = self.items[self.i % len(self.items)]
        self.i += 1
        return b


def emit_norm(s, epsc, xt, hb_rr, scr, stat_rr, ntile):
    nc = s.nc
    st = stat_rr.next()
    hbs = [hb_rr.next() for _ in range(ntile)]
    for j in range(ntile):
        s.op("act", lambda: nc.scalar.activation(out=hbs[j][:, :], in_=xt[j][:, :], func=AF.Square, scale=1.0 / 32.0,
                                                 accum_out=st[:, j:j + 1]),
             reads=[xt[j]], writes=[hbs[j], st])
    s.op("act", lambda: nc.scalar.activation(out=st[:, 4:4 + ntile], in_=st[:, 0:ntile], func=AF.Ln, bias=epsc[:, 0:1], scale=1.0),
         reads=[st, epsc], writes=[st])
    s.op("act", lambda: nc.scalar.activation(out=st[:, 8:8 + ntile], in_=st[:, 4:4 + ntile], func=AF.Exp, scale=-0.5),
         reads=[st], writes=[st])
    for j in range(ntile):
        s.op("act", lambda: nc.scalar.activation(out=hbs[j][:, :], in_=xt[j][:, :], func=AF.Copy, scale=st[:, 8 + j:9 + j]),
             reads=[xt[j], st], writes=[hbs[j]])
    return hbs


def emit_transpose_T(s, hbs, gcol, hT, ident_b, pT_rr, ntile, evac_engs=("dve", "act")):
    nc = s.nc
    k = 0
    for c in range(NKC):
        pT = pT_rr.next()
        for j in range(ntile):
            s.op("pe", lambda: nc.tensor.transpose(out=pT[:, j * 128:(j + 1) * 128], in_=hbs[j][:, c * 128:(c + 1) * 128],
                                                   identity=ident_b[:, :]),
                 reads=[hbs[j], ident_b], writes=[pT], inc=(j == ntile - 1))
        eng = evac_engs[k % len(evac_engs)]
        k += 1
        if eng == "dve":
            s.op("dve", lambda: nc.vector.tensor_scalar(out=hT[:, c, 0:ntile * 128], in0=pT[:, 0:ntile * 128],
                                                        scalar1=gcol[:, c:c + 1], scalar2=None, op0=ALU.mult),
                 reads=[pT, gcol], writes=[hT])
        else:
            s.op("act", lambda: nc.scalar.activation(out=hT[:, c, 0:ntile * 128], in_=pT[:, 0:ntile * 128],
                                                     func=AF.Copy, scale=gcol[:, c:c + 1]),
                 reads=[pT, gcol], writes=[hT])


def emit_rmsnorm_T(s, epsc, xt, gcol, hT, ident_b, pT_rr, hb_rr, scr, stat_rr, ntile, evac_engs=("dve", "act")):
    hbs = emit_norm(s, epsc, xt, hb_rr, scr, stat_rr, ntile)
    emit_transpose_T(s, hbs, gcol, hT, ident_b, pT_rr, ntile, evac_engs)


def build_ffn(NT, TB=512):
    nc = bass.Bass("TRN2", target_bir_lowering=False)
    x = dram_in(nc, "x", [NT, D])
    g = dram_in(nc, "g", [D])
    w_in = dram_in(nc, "w_in", [D, 2 * DFF])
    w_out = dram_in(nc, "w_out", [DFF, D])
    ident = dram_in(nc, "ident", [128, 128], BF16)
    y = dram_out(nc, "y", [NT, D])
    s = S(nc)
    emit_ffn(s, x, g, w_in, w_out, ident, y, NT, TB)
    s.finish()
    s.close()
    return nc


def emit_ffn(s, x, g, w_in, w_out, ident, y, NT, TB=512):
    nc = s.nc
    ntile = TB // 128
    m_ = s.mark()
    ident_b = s.sb("ident_b", [128, 128], BF16)
    s.dma("sp", ident_b[:, :], ident[:, :], writes=[ident_b])
    gcol = s.sb("gcol", [128, NKC], F32)
    epsc = s.sb("epsc", [128, 1], F32)
    s.op("dve", lambda: nc.vector.memset(epsc[:, :], EPS), writes=[epsc])
    s.dma("sp", gcol[:, :], g.rearrange("(c p) -> p c", p=128), writes=[gcol], allow_slow_non_contiguous=True)
    win_b = [s.sb("win_b%d" % c, [128, 2 * DFF], BF16) for c in range(NKC)]
    wout_b = [s.sb("wout_b%d" % f, [128, D], BF16) for f in range(NFC)]
    for c in range(NKC):
        for hf in range(2):
            s.dma("pool", win_b[c][:, hf * DFF:(hf + 1) * DFF], w_in[c * 128:(c + 1) * 128, hf * DFF:(hf + 1) * DFF],
                  writes=[win_b[c]])
    for f in range(NFC):
        s.dma("pool", wout_b[f][:, :], w_out[f * 128:(f + 1) * 128, :], writes=[wout_b[f]])
    xn = [s.sb("xn%d" % j, [128, D], F32) for j in range(ntile)]
    xr_rr = RR([s.sb("xr%d" % j, [128, D], F32) for j in range(1)])
    hb_rr = RR([s.sb("hb%d" % j, [128, D], BF16) for j in range(2 * ntile)])
    stat_rr = RR([s.sb("st%d" % j, [128, 16], F32) for j in range(4)])
    hT = s.sb("hT", [128, NKC, TB], BF16)
    pT_rr = RR([s.ps("pT%d" % j, [128, TB], BF16) for j in range(2)])
    pa_rr = RR([s.ps("pa%d" % j, [128, TB], F32) for j in range(2)])
    pb_rr = RR([s.ps("pb%d" % j, [128, TB], F32) for j in range(2)])
    po_rr = RR([s.ps("po%d" % j, [128, 512], F32) for j in range(2)])
    sa_rr = RR([s.sb("sa%d" % j, [128, TB], F32) for j in range(2)])
    act = [s.sb("actT%d" % f, [128, TB], BF16) for f in range(NFC)]
    nblk = NT // TB

    def prep_a_rot(tb):
        for j in range(ntile):
            r0 = tb * TB + j * 128
            s.dma("sp", xn[j][:, :], x[r0:r0 + 128, :], writes=[xn[j]])
        return emit_norm(s, epsc, xn, hb_rr, None, stat_rr, ntile)

    hbs_next = prep_a_rot(0)
    emit_transpose_T(s, hbs_next, gcol, hT, ident_b, pT_rr, ntile)
    for tb in range(nblk):
        if tb + 1 < nblk:
            hbs_next = prep_a_rot(tb + 1)
        for f in range(NFC):
            pa = pa_rr.next()
            pb = pb_rr.next()
            for c in range(NKC):
                s.op("pe", lambda: nc.tensor.matmul(pa[:, :], lhsT=win_b[c][:, f * 128:(f + 1) * 128], rhs=hT[:, c, :],
                                                    start=(c == 0), stop=(c == NKC - 1)),
                     reads=[win_b[c], hT], writes=[pa], inc=(c == NKC - 1))
            for c in range(NKC):
                s.op("pe", lambda: nc.tensor.matmul(pb[:, :], lhsT=win_b[c][:, DFF + f * 128:DFF + (f + 1) * 128],
                                                    rhs=hT[:, c, :], start=(c == 0), stop=(c == NKC - 1)),
                     reads=[win_b[c], hT], writes=[pb], inc=(c == NKC - 1))
            sa = sa_rr.next()
            s.op("act", lambda: nc.scalar.activation(out=sa[:, :], in_=pa[:, :], func=AF.Silu), reads=[pa], writes=[sa])
            s.op("dve", lambda: nc.vector.tensor_tensor(out=act[f][:, :], in0=sa[:, :], in1=pb[:, :], op=ALU.mult),
                 reads=[sa, pb], writes=[act[f]])
        if tb + 1 < nblk:
            emit_transpose_T(s, hbs_next, gcol, hT, ident_b, pT_rr, ntile)
        for j in range(ntile):
            r0 = tb * TB + j * 128
            xr = xr_rr.next()
            s.dma("sp", xr[:, :], x[r0:r0 + 128, :], writes=[xr])
            for hf in range(2):
                po = po_rr.next()
                for f in range(NFC):
                    s.op("pe", lambda: nc.tensor.matmul(po[:, :], lhsT=act[f][:, j * 128:(j + 1) * 128],
                                                        rhs=wout_b[f][:, hf * 512:(hf + 1) * 512],
                                                        start=(f == 0), stop=(f == NFC - 1)),
                         reads=[act[f], wout_b[f]], writes=[po], inc=(f == NFC - 1))
                s.op("dve", lambda: nc.vector.scalar_tensor_tensor(out=xr[:, hf * 512:(hf + 1) * 512], in0=po[:, :],
                                                                   scalar=0.5, in1=xr[:, hf * 512:(hf + 1) * 512],
                                                                   op0=ALU.mult, op1=ALU.add),
                     reads=[po, xr], writes=[xr])
            s.dma("sp", y[r0:r0 + 128, :], xr[:, :], reads=[xr])
    s.release(m_)


FM_SRC = [[(0, 128)], [(128, 128)], [(256, 128)], [(384, 128)],
          [(768, 128)], [(896, 128)], [(1024, 128)], [(1152, 128)],
          [(1540, 128)], [(1668, 128)], [(1924, 64), (2052, 64)], [(1796, 128)]]
FM_GCOL = [0, 0, 1, 1, 2, 2, 3, 3, 4, 4, 5, None]
FM_BLK = [0, 0, 0, 0, 1, 1, 1, 1, 1, 1, 1, None]
TM_SRC = [[(512, 256), (1280, 256)],
          [(1536, 4), (2180, 12), (1988, 64), (2116, 64)],
          [(2192, 512)]]
NFM = 12
GELU_C = 1.5957691216057308


def build_proj(NT, TB=512):
    nc = bass.Bass("TRN2", target_bir_lowering=False)
    a = dict(
        x=dram_in(nc, "x", [NT, D]), g=dram_in(nc, "g", [D]), w_in=dram_in(nc, "w_in", [D, 6800]),
        ident=dram_in(nc, "ident", [128, 128], BF16), gains=dram_in(nc, "gains", [128, 6]),
        blk=dram_in(nc, "blk", [2, 128, 128], BF16), vgain=dram_in(nc, "vgain", [128, 256]),
        wsT=dram_in(nc, "wsT", [4, 128, 128]), triu=dram_in(nc, "triu", [128, 128]), bsT=dram_in(nc, "bsT", [128, 4]),
        zfm=dram_out(nc, "zfm", [NFM, 128, NT], BF16), vab=dram_out(nc, "vab", [NT, 512], BF16),
        vsw=dram_out(nc, "vsw", [NT, 128], BF16), misc=dram_out(nc, "misc", [NT, 16]), od=dram_out(nc, "od", [NT, 256], BF16))
    s = S(nc)
    emit_proj(s, a, NT, TB)
    s.finish()
    s.close()
    return nc


def emit_proj(s, a, NT, TB=512):
    nc = s.nc
    x, g, w_in, ident, gains, blk, vgain, wsT, triu, bsT = (a[k] for k in
                                                            ("x", "g", "w_in", "ident", "gains", "blk", "vgain", "wsT", "triu", "bsT"))
    zfm, vab, vsw, misc, od = (a[k] for k in ("zfm", "vab", "vsw", "misc", "od"))
    m_ = s.mark()
    ntile = TB // 128
    ident_b = s.sb("ident_b", [128, 128], BF16)
    s.dma("sp", ident_b[:, :], ident[:, :], writes=[ident_b])
    gcol = s.sb("gcol", [128, NKC], F32)
    s.dma("sp", gcol[:, :], g.rearrange("(c p) -> p c", p=128), writes=[gcol], allow_slow_non_contiguous=True)
    epsc = s.sb("epsc", [128, 1], F32)
    s.op("dve", lambda: nc.vector.memset(epsc[:, :], EPS), writes=[epsc])
    gn = s.sb("gn", [128, 6], F32)
    s.dma("sp", gn[:, :], gains[:, :], writes=[gn])
    for col, sc in ((0, 32.0 ** -0.5), (2, 0.125), (4, 0.125)):
        s.op("dve", lambda: nc.vector.tensor_scalar(out=gn[:, col:col + 1], in0=gn[:, col:col + 1], scalar1=sc,
                                                    scalar2=None, op0=ALU.mult), reads=[gn], writes=[gn])
    blk_b = [s.sb("blk%d" % i, [128, 128], BF16) for i in range(2)]
    for i in range(2):
        s.dma("sp", blk_b[i][:, :], blk[i], writes=[blk_b[i]])
    vg = s.sb("vg", [128, 256], F32)
    s.dma("sp", vg[:, :], vgain[:, :], writes=[vg])
    bcol = s.sb("bcol", [128, 4], F32)
    s.dma("sp", bcol[:, :], bsT[:, :], writes=[bcol])
    tri = s.sb("tri", [128, 128], F32)
    s.dma("sp", tri[:, :], triu[:, :], writes=[tri])
    wm = []
    wtmp = s.sb("wtmp", [128, 128], F32)
    for gi in range(4):
        w = s.sb("wm%d" % gi, [128, 128], BF16)
        s.dma("sp", wtmp[:, :], wsT[gi], writes=[wtmp])
        s.op("dve", lambda: nc.vector.tensor_tensor(out=w[:, :], in0=wtmp[:, :], in1=tri[:, :], op=ALU.mult),
             reads=[wtmp, tri], writes=[w])
        wm.append(w)
    wfm = [s.sb("wfm%d" % c, [128, NFM * 128], BF16) for c in range(NKC)]
    wtm = [s.sb("wtm%d" % c, [128, 1168], BF16) for c in range(NKC)]
    FM_RUNS = [(0, 0, 512), (512, 768, 512), (1024, 1540, 256), (1280, 1924, 64), (1344, 2052, 64), (1408, 1796, 128)]
    TM_RUNS = [(0, 512, 256), (256, 1280, 256), (512, 1536, 4), (516, 2180, 12), (528, 1988, 64), (592, 2116, 64), (656, 2192, 512)]
    for c in range(NKC):
        for (o, c0, n) in FM_RUNS:
            s.dma("pool", wfm[c][:, o:o + n], w_in[c * 128:(c + 1) * 128, c0:c0 + n], writes=[wfm[c]])
        for (o, c0, n) in TM_RUNS:
            s.dma("pool", wtm[c][:, o:o + n], w_in[c * 128:(c + 1) * 128, c0:c0 + n], writes=[wtm[c]])
    xts = [[s.sb("xt%d_%d" % (k, j), [128, D], F32) for j in range(ntile)] for k in range(2)]
    hb_rr = RR([s.sb("hb%d" % j, [128, D], BF16) for j in range(2 * ntile)])
    scr = s.sb("scr", [128, D], BF16)
    stat_rr = RR([s.sb("st%d" % j, [128, 16], F32) for j in range(4)])
    hTs = [s.sb("hT%d" % k, [128, NKC, TB], BF16) for k in range(2)]
    pT_rr = RR([s.ps("pT%d" % j, [128, TB], BF16) for j in range(2)])
    pz_rr = RR([s.ps("pz%d" % j, [128, 512], F32) for j in range(2)])
    ptm_rr = RR([s.ps("ptm%d" % j, [128, 512], F32) for j in range(2)])
    pq_rr = RR([s.ps("pq%d" % j, [128, 512], F32) for j in range(2)])
    sq_rr = RR([s.sb("sq%d" % j, [128, TB], BF16) for j in range(4)])
    rs_rr = RR([s.sb("rs%d" % j, [128, TB], F32) for j in range(2)])
    zo_rr = RR([s.sb("zo%d" % j, [128, TB], BF16) for j in range(4)])
    vab_rr = RR([s.sb("vabt%d" % j, [128, 512], BF16) for j in range(2)])
    vsw_rr = RR([s.sb("vswt%d" % j, [128, 128], BF16) for j in range(2)])
    msc_rr = RR([s.sb("msct%d" % j, [128, 16], F32) for j in range(2)])
    f_rr = RR([s.sb("gf%d" % j, [128, 512], F32) for j in range(4)])
    ge_rr = RR([s.sb("ge%d" % j, [128, 512], F32) for j in range(3)])
    zs_rr = RR([s.sb("zs%d" % j, [128, 512], F32) for j in range(2)])
    zf_rr = RR([s.sb("zf%d" % j, [128, TB], F32) for j in range(3)])
    vn_rr = RR([s.sb("vn%d" % j, [128, 256], BF16) for j in range(3)])
    od_rr = RR([s.sb("odt%d" % j, [128, 256], BF16) for j in range(2)])
    nblk = NT // TB

    def prep(tb):
        xt = xts[tb % 2]
        for j in range(ntile):
            s.dma("sp", xt[j][:, :], x[tb * TB + j * 128:tb * TB + (j + 1) * 128, :], writes=[xt[j]])
        emit_rmsnorm_T(s, epsc, xt, gcol, hTs[tb % 2], ident_b, pT_rr, hb_rr, scr, stat_rr, ntile)

    prep(0)
    pipe = Pipe(2)
    for tb in range(nblk):
        t0 = tb * TB
        hT = hTs[tb % 2]
        for i in range(NFM):
            pz = pz_rr.next()
            for c in range(NKC):
                s.op("pe", lambda: nc.tensor.matmul(pz[:, 0:TB], lhsT=wfm[c][:, i * 128:(i + 1) * 128], rhs=hT[:, c, :],
                                                    start=(c == 0), stop=(c == NKC - 1)),
                     reads=[wfm[c], hT], writes=[pz], inc=(c == NKC - 1))
            zo = zo_rr.next()
            if FM_GCOL[i] is None:
                s.op("dve", lambda: nc.vector.tensor_copy(out=zo[:, :], in_=pz[:, 0:TB]), reads=[pz], writes=[zo])
                s.dma("sp", zfm[i, :, t0:t0 + TB], zo[:, :], reads=[zo])
            else:
                zf = zf_rr.next()
                s.op("dve", lambda: nc.vector.tensor_copy(out=zf[:, :], in_=pz[:, 0:TB]), reads=[pz], writes=[zf])
                sq = sq_rr.next()
                s.op("act", lambda: nc.scalar.activation(out=sq[:, :], in_=zf[:, :], func=AF.Square),
                     reads=[zf], writes=[sq])

                def back(i=i, zf=zf, sq=sq, zo=zo, t0=t0):
                    gs = 32.0 if FM_BLK[i] == 0 else 64.0
                    pq = pq_rr.next()
                    s.op("pe", lambda: nc.tensor.matmul(pq[:, 0:TB], lhsT=blk_b[FM_BLK[i]][:, :], rhs=sq[:, :],
                                                        start=True, stop=True), reads=[blk_b[FM_BLK[i]], sq], writes=[pq])
                    rs = rs_rr.next()
                    s.op("act", lambda: nc.scalar.activation(out=rs[:, :], in_=pq[:, 0:TB], func=AF.Ln, bias=epsc[:, 0:1],
                                                             scale=1.0 / gs), reads=[pq, epsc], writes=[rs])
                    s.op("act", lambda: nc.scalar.activation(out=rs[:, :], in_=rs[:, :], func=AF.Exp, scale=-0.5),
                         reads=[rs], writes=[rs])
                    gc = FM_GCOL[i]
                    s.op("dve", lambda: nc.vector.scalar_tensor_tensor(out=zo[:, :], in0=zf[:, :], scalar=gn[:, gc:gc + 1],
                                                                       in1=rs[:, :], op0=ALU.mult, op1=ALU.mult),
                         reads=[zf, gn, rs], writes=[zo])
                    s.dma("sp", zfm[i, :, t0:t0 + TB], zo[:, :], reads=[zo])
                pipe.push(back)
        if tb + 1 < nblk:
            prep(tb + 1)
        for j in range(ntile):
            r0 = t0 + j * 128
            pz = ptm_rr.next()
            for c in range(NKC):
                s.op("pe", lambda: nc.tensor.matmul(pz[:, :], lhsT=hT[:, c, j * 128:(j + 1) * 128], rhs=wtm[c][:, 0:512],
                                                    start=(c == 0), stop=(c == NKC - 1)),
                     reads=[wtm[c], hT], writes=[pz], inc=(c == NKC - 1))
            vt = vab_rr.next()
            s.op("act", lambda: nc.scalar.copy(out=vt[:, :], in_=pz[:, :]), reads=[pz], writes=[vt])
            s.dma("sp", vab[r0:r0 + 128, :], vt[:, :], reads=[vt])
            pz = ptm_rr.next()
            for c in range(NKC):
                s.op("pe", lambda: nc.tensor.matmul(pz[:, 0:144], lhsT=hT[:, c, j * 128:(j + 1) * 128], rhs=wtm[c][:, 512:656],
                                                    start=(c == 0), stop=(c == NKC - 1)),
                     reads=[wtm[c], hT], writes=[pz], inc=(c == NKC - 1))
            mt = msc_rr.next()
            vs_ = vsw_rr.next()
            s.op("dve", lambda: nc.vector.tensor_copy(out=mt[:, :], in_=pz[:, 0:16]), reads=[pz], writes=[mt])
            s.op("dve", lambda: nc.vector.tensor_copy(out=vs_[:, :], in_=pz[:, 16:144]), reads=[pz], writes=[vs_])
            s.dma("sp", misc[r0:r0 + 128, :], mt[:, :], reads=[mt])
            s.dma("sp", vsw[r0:r0 + 128, :], vs_[:, :], reads=[vs_])
            pz = ptm_rr.next()
            for c in range(NKC):
                s.op("pe", lambda: nc.tensor.matmul(pz[:, :], lhsT=hT[:, c, j * 128:(j + 1) * 128], rhs=wtm[c][:, 656:1168],
                                                    start=(c == 0), stop=(c == NKC - 1)),
                     reads=[wtm[c], hT], writes=[pz], inc=(c == NKC - 1))
            zs = zs_rr.next()
            s.op("act", lambda: nc.scalar.copy(out=zs[:, :], in_=pz[:, :]), reads=[pz], writes=[zs])
            z2 = f_rr.next()
            s.op("act", lambda: nc.scalar.activation(out=z2[:, :], in_=zs[:, :], func=AF.Square), reads=[zs], writes=[z2])
            s.op("dve", lambda: nc.vector.tensor_scalar(out=z2[:, :], in0=z2[:, :], scalar1=0.044715, scalar2=1.0,
                                                        op0=ALU.mult, op1=ALU.add), reads=[z2], writes=[z2])
            s.op("dve", lambda: nc.vector.tensor_tensor(out=z2[:, :], in0=z2[:, :], in1=zs[:, :], op=ALU.mult),
                 reads=[z2, zs], writes=[z2])
            s.op("act", lambda: nc.scalar.activation(out=z2[:, :], in_=z2[:, :], func=AF.Exp, scale=-GELU_C),
                 reads=[z2], writes=[z2])
            s.op("act", lambda: nc.scalar.activation(out=z2[:, :], in_=z2[:, :], func=AF.Ln, bias=1.0, scale=1.0),
                 reads=[z2], writes=[z2])
            s.op("act", lambda: nc.scalar.activation(out=z2[:, :], in_=z2[:, :], func=AF.Exp, scale=-1.0),
                 reads=[z2], writes=[z2])
            ge = ge_rr.next()
            s.op("dve", lambda: nc.vector.tensor_tensor(out=ge[:, :], in0=z2[:, :], in1=zs[:, :], op=ALU.mult),
                 reads=[z2, zs], writes=[ge])
            sqv = f_rr.next()
            st = stat_rr.next()
            s.op("act", lambda: nc.scalar.activation(out=sqv[:, 0:256], in_=ge[:, 256:512], func=AF.Square),
                 reads=[ge], writes=[sqv])
            s.op("dve", lambda: nc.vector.tensor_reduce(out=st[:, 0:4], in_=sqv[:, 0:256].rearrange("p (g d) -> p g d", g=4),
                                                        axis=AX.X, op=ALU.add), reads=[sqv], writes=[st])
            s.op("act", lambda: nc.scalar.activation(out=st[:, 0:4], in_=st[:, 0:4], func=AF.Ln, bias=epsc[:, 0:1],
                                                     scale=1.0 / 64.0), reads=[st, epsc], writes=[st])
            s.op("act", lambda: nc.scalar.activation(out=st[:, 0:4], in_=st[:, 0:4], func=AF.Exp, scale=-0.5),
                 reads=[st], writes=[st])
            vn = vn_rr.next()
            for gi in range(4):
                s.op("dve", lambda: nc.vector.scalar_tensor_tensor(
                    out=vn[:, gi * 64:(gi + 1) * 64], in0=ge[:, 256 + gi * 64:256 + (gi + 1) * 64], scalar=st[:, gi:gi + 1],
                    in1=vg[:, gi * 64:(gi + 1) * 64], op0=ALU.mult, op1=ALU.mult), reads=[ge, st, vg], writes=[vn])

            def back2(vn=vn, ge=ge, r0=r0):
                pq = pq_rr.next()
                for gi in range(4):
                    s.op("pe", lambda: nc.tensor.matmul(pq[:, gi * 64:(gi + 1) * 64], lhsT=wm[gi][:, :],
                                                        rhs=vn[:, gi * 64:(gi + 1) * 64], start=True, stop=True),
                         reads=[wm[gi], vn], writes=[pq], inc=(gi == 3))
                ot = od_rr.next()
                for gi in range(4):
                    s.op("dve", lambda: nc.vector.scalar_tensor_tensor(
                        out=ot[:, gi * 64:(gi + 1) * 64], in0=pq[:, gi * 64:(gi + 1) * 64], scalar=bcol[:, gi:gi + 1],
                        in1=ge[:, gi * 64:(gi + 1) * 64], op0=ALU.add, op1=ALU.mult), reads=[pq, bcol, ge], writes=[ot])
                s.dma("sp", od[r0:r0 + 128, :], ot[:, :], reads=[ot])
            pipe.push(back2)
    pipe.flush()
    s.release(m_)


def _bf(a):
    import ml_dtypes
    return np.ascontiguousarray(a).astype(ml_dtypes.bfloat16)


def proj_consts():
    blk = np.zeros((2, 128, 128), np.float32)
    for i in range(128):
        for j in range(128):
            if i // 32 == j // 32:
                blk[0, i, j] = 1
            if i // 64 == j // 64:
                blk[1, i, j] = 1
    triu = np.triu(np.ones((128, 128), np.float32))
    return dict(ident=_bf(np.eye(128, dtype=np.float32)), blk=_bf(blk), triu=triu)


def proj_params(g, w_in, dq, dk, fq, fk, nq, nk, vgain, w_s, b_s):
    gains = np.stack([np.tile(dq, 4), np.tile(dk, 4), np.tile(fq, 2), np.tile(fk, 2), np.tile(nq, 2), np.tile(nk, 2)], 1)
    return dict(g=np.ascontiguousarray(g), w_in=np.ascontiguousarray(w_in), gains=np.ascontiguousarray(gains, dtype=np.float32),
                vgain=np.ascontiguousarray(np.broadcast_to(vgain[None, :], (128, 256))),
                wsT=np.ascontiguousarray(w_s.transpose(0, 2, 1)), bsT=np.ascontiguousarray(b_s.T))


NEG = -30000.0


def load_vt(s, vt, io, key, T, init=True, load=True):
    nc = s.nc
    NT = T // 128
    if init:
        s.op("pool", lambda: nc.gpsimd.memset(vt[:, :, 64:128], 0.0), writes=[vt])
        s.op("pool", lambda: nc.gpsimd.memset(vt[:, :, 64:65], 1.0), writes=[vt])
    if not load:
        return
    if key + "_src" in io:
        src = io[key + "_src"]
        step = 8
        for j0 in range(0, NT, step):
            j1 = min(NT, j0 + step)
            s.dma("sp", vt[:, j0:j1, 0:64], src[j0 * 128:j1 * 128, :].rearrange("(j p) d -> p j d", p=128), writes=[vt])
    else:
        s.dma("sp", vt[:, :, 0:64], io[key][:, :, 0:64], writes=[vt])


class Pipe:
    def __init__(self, lag):
        self.q = []
        self.lag = lag

    def push(self, fn):
        self.q.append(fn)
        while len(self.q) > self.lag:
            self.q.pop(0)()

    def flush(self):
        while self.q:
            self.q.pop(0)()


def emit_attn_phase(s, cm, T, nsub, qT, kT, vt, kparts, out_dram, finalize, bias_fn=None, name="a", lag=2):
    nc = s.nc
    NQB = T // 512
    pipe = Pipe(lag)
    fin_pending = None
    for qb in range(NQB):
        q0 = qb * 512
        pos = [cm["po_rr"].next() for _ in range(nsub)]
        nt = 4 * qb + 4
        njob = 0
        for t in range(nt):
            di = t - 4 * qb
            c0 = 128 * di if di > 0 else 0
            for i in range(nsub):
                kz = kT[i]
                ps = cm["ps_rr"].next()
                s.op("pe", lambda: nc.tensor.matmul(ps[:, c0:512], lhsT=kz[:, t * 128:(t + 1) * 128],
                                                    rhs=qT[:, q0 + c0:q0 + 512], start=True, stop=(di < 0)),
                     reads=[kz, qT], writes=[ps], inc=(di < 0))
                if di >= 0:
                    s.op("pe", lambda: nc.tensor.matmul(ps[:, c0:c0 + 128], lhsT=cm["ident_b"][:, :], rhs=cm["tri_b"][:, :],
                                                        start=False, stop=True),
                         reads=[cm["ident_b"], cm["tri_b"]], writes=[ps])
                pt = cm["pt_rr"].next()
                if bias_fn is None:
                    s.op("act", lambda: nc.scalar.activation(out=pt[:, c0:512], in_=ps[:, c0:512], func=AF.Exp),
                         reads=[ps], writes=[pt])
                else:
                    bb, bap = bias_fn(qb, t)
                    s.op("act", lambda: nc.scalar.activation(out=pt[:, c0:512], in_=ps[:, c0:512], func=AF.Exp, bias=bap),
                         reads=[ps, bb], writes=[pt])

                def pv(po=pos[i], t=t, c0=c0, pt=pt, nt=nt):
                    s.op("pe", lambda: nc.tensor.matmul(po[:, c0:512], lhsT=vt[:, t, :], rhs=pt[:, c0:512],
                                                        start=(t == 0), stop=(t == nt - 1)),
                         reads=[vt, pt], writes=[po])
                pipe.push(pv)
                njob += 1
                if fin_pending is not None and njob == lag:
                    fin_pending()
                    fin_pending = None
        if fin_pending is not None:
            pipe.flush()
            fin_pending()
        fin_pending = (lambda qb=qb, pos=pos: finalize(qb, pos))
    pipe.flush()
    if fin_pending is not None:
        fin_pending()


def emit_o_to_tokmajor(s, cm, po, pf, col0):
    nc = s.nc
    oc = cm["oc_rr"].next()
    s.op("dve", lambda: nc.vector.tensor_copy(out=oc[0:65, :], in_=po[0:65, :]), reads=[po], writes=[oc])
    for j in range(4):
        s.op("pe", lambda: nc.tensor.transpose(out=pf[:, j, col0:col0 + 65], in_=oc[0:65, j * 128:(j + 1) * 128],
                                               identity=cm["ident_f"][0:65, 0:65]),
             reads=[oc, cm["ident_f"]], writes=[pf], inc=(j == 3))


def build_mix_ab(T):
    nc = bass.Bass("TRN2", target_bir_lowering=False)
    io = mix_decl(nc, T, with_c=False)
    s = S(nc)
    cm = mix_common(s, io)
    emit_mix_a(s, cm, io, T)
    emit_mix_b(s, cm, io, T)
    s.finish()
    s.close()
    return nc


def mix_decl(nc, T, with_c=True):
    NT = T // 128
    io = dict(
        identb=dram_in(nc, "identb", [128, 128], BF16), identf=dram_in(nc, "identf", [128, 128]),
        trib=dram_in(nc, "trib", [128, 128], BF16),
        qa=dram_in(nc, "qa", [64, T], BF16), ka=dram_in(nc, "ka", [64, T], BF16), va=dram_in(nc, "va", [128, NT, 65], BF16),
        lamp=dram_in(nc, "lamp", [128, 4, 32]), lami=dram_in(nc, "lami", [128, 2]),
        qb=dram_in(nc, "qb", [64, T], BF16), kb=dram_in(nc, "kb", [64, T], BF16), vb=dram_in(nc, "vb", [128, NT, 65], BF16),
        flog=dram_in(nc, "flog", [128, NT]), fbias=dram_in(nc, "fbias", [128, 1]),
        triuf=dram_in(nc, "triuf", [128, 128]), onesf=dram_in(nc, "onesf", [128, 128]),
        oa=dram_out(nc, "oa", [T, 64], BF16), ob=dram_out(nc, "ob", [T, 64], BF16),
    )
    return io


def mix_common(s, io, n_ps=3, with_pf2=True):
    nc = s.nc
    cm = {}
    for nm, key, dt in (("ident_b", "identb", BF16), ("ident_f", "identf", F32), ("tri_b", "trib", BF16)):
        b = s.sb(nm, [128, 128], dt)
        s.dma("sp", b[:, :], io[key][:, :], writes=[b])
        cm[nm] = b
    cm["epsc"] = s.sb("epsc", [128, 1], F32)
    s.op("dve", lambda: nc.vector.memset(cm["epsc"][:, :], EPS), writes=[cm["epsc"]])
    cm["ps_rr"] = RR([s.ps("ps%d" % j, [128, 512], F32) for j in range(n_ps)])
    cm["po_rr"] = RR([s.ps("po%d" % j, [128, 512], F32) for j in range(3)])
    cm["pf"] = s.ps("pf", [128, 4, 128], F32)
    if with_pf2:
        cm["pf2"] = s.ps("pf2", [128, 4, 128], F32)
    cm["lag"] = n_ps - 1
    cm["o1s_rr"] = RR([s.sb("o1s%d" % j, [128, 4, 65], F32) for j in range(2)])
    cm["pt_rr"] = RR([s.sb("pt%d" % j, [128, 512], BF16) for j in range(n_ps + 2)])
    cm["oc_rr"] = RR([s.sb("oc%d" % j, [128, 512], F32) for j in range(2)])
    cm["st_rr"] = RR([s.sb("mst%d" % j, [128, 8], F32) for j in range(8)])
    cm["ot_rr"] = RR([s.sb("ot%d" % j, [128, 4, 64], BF16) for j in range(2)])
    cm["tmp_rr"] = RR([s.sb("tmp%d" % j, [128, 64], F32) for j in range(4)])
    return cm


def emit_mix_a(s, cm, io, T, stage="all", bufs=None):
    nc = s.nc
    NT = T // 128
    if stage in ("all", "alloc"):
        if stage == "all":
            m = s.mark()
        b = dict(qT=s.sb("a_q", [128, T], BF16), k1=s.sb("a_k1", [128, T], BF16), k2=s.sb("a_k2", [128, T], BF16),
                 vt=s.sb("a_v", [128, NT, 128], BF16), lp=s.sb("lp", [128, 4, 32], F32), li=s.sb("li", [128, 2], F32),
                 lw=s.sb("lw", [128, 2, 32], F32), lam=s.sb("lam", [128, 4], F32))
        s.op("pool", lambda: nc.gpsimd.memset(b["qT"][64:128, :], 0.0), writes=[b["qT"]])
        s.op("dve", lambda: nc.vector.memset(b["k1"][:, :], 0.0), writes=[b["k1"]])
        s.op("pool", lambda: nc.gpsimd.memset(b["k2"][:, :], 0.0), writes=[b["k2"]])
        load_vt(s, b["vt"], io, "va", T, init=True, load=False)
        if stage == "alloc":
            return b
        bufs = b
    qT, k1, k2, vt, lp, li, lw, lam = (bufs[k] for k in ("qT", "k1", "k2", "vt", "lp", "li", "lw", "lam"))
    if stage in ("all", "load"):
        s.dma("sp", qT[0:64, :], io["qa"][:, :], writes=[qT])
        s.dma("sp", k1[0:32, :], io["ka"][0:32, :], writes=[k1])
        s.dma("sp", k2[32:64, :], io["ka"][32:64, :], writes=[k2])
        load_vt(s, vt, io, "va", T, init=False)
        s.dma("sp", lp[:, :, :], io["lamp"][:, :, :], writes=[lp])
        s.dma("sp", li[:, :], io["lami"][:, :], writes=[li])
        s.op("dve", lambda: nc.vector.tensor_tensor(out=lw[:, 0, :], in0=lp[:, 0, :], in1=lp[:, 1, :], op=ALU.mult),
             reads=[lp], writes=[lw])
        s.op("dve", lambda: nc.vector.tensor_tensor(out=lw[:, 1, :], in0=lp[:, 2, :], in1=lp[:, 3, :], op=ALU.mult),
             reads=[lp], writes=[lw])
        s.op("dve", lambda: nc.vector.tensor_reduce(out=lam[:, 0:2], in_=lw[:, :, :], axis=AX.X, op=ALU.add),
             reads=[lw], writes=[lam])
        s.op("act", lambda: nc.scalar.activation(out=lam[:, 0:2], in_=lam[:, 0:2], func=AF.Exp), reads=[lam], writes=[lam])
        s.op("dve", lambda: nc.vector.tensor_tensor(out=lam[:, 2:3], in0=lam[:, 1:2], in1=lam[:, 0:1], op=ALU.subtract),
             reads=[lam], writes=[lam])
        s.op("dve", lambda: nc.vector.tensor_tensor(out=lam[:, 3:4], in0=lam[:, 2:3], in1=li[:, 0:1], op=ALU.subtract),
             reads=[lam, li], writes=[lam])
        if stage == "load":
            return

    def fin(qb, pos):
        if "pf2" in cm:
            pf = cm["pf"]
            pf2 = cm["pf2"]
            emit_o_to_tokmajor(s, cm, pos[0], pf, 0)
            emit_o_to_tokmajor(s, cm, pos[1], pf2, 0)
        else:
            pf2 = cm["pf"]
            emit_o_to_tokmajor(s, cm, pos[0], pf2, 0)
            pf = cm["o1s_rr"].next()
            s.op("dve", lambda: nc.vector.tensor_copy(out=pf[:, :, :], in_=pf2[:, :, 0:65]), reads=[pf2], writes=[pf])
            emit_o_to_tokmajor(s, cm, pos[1], pf2, 0)
        ot = cm["ot_rr"].next()
        for j in range(4):
            st = cm["st_rr"].next()
            s.op("dve", lambda: nc.vector.tensor_scalar(out=st[:, 0:1], in0=pf[:, j, 64:65], scalar1=1e-30, scalar2=None,
                                                        op0=ALU.max), reads=[pf], writes=[st])
            s.op("dve", lambda: nc.vector.tensor_scalar(out=st[:, 1:2], in0=pf2[:, j, 64:65], scalar1=1e-30, scalar2=None,
                                                        op0=ALU.max), reads=[pf2], writes=[st])
            s.op("dve", lambda: nc.vector.reciprocal(out=st[:, 0:2], in_=st[:, 0:2]), reads=[st], writes=[st])
            t2 = cm["tmp_rr"].next()
            o = cm["tmp_rr"].next()
            s.op("dve", lambda: nc.vector.tensor_scalar(out=t2[:, :], in0=pf2[:, j, 0:64], scalar1=st[:, 1:2],
                                                        scalar2=lam[:, 3:4], op0=ALU.mult, op1=ALU.mult),
                 reads=[pf2, st, lam], writes=[t2])
            s.op("dve", lambda: nc.vector.scalar_tensor_tensor(out=o[:, :], in0=pf[:, j, 0:64], scalar=st[:, 0:1], in1=t2[:, :],
                                                               op0=ALU.mult, op1=ALU.add), reads=[pf, st, t2], writes=[o])
            s.op("act", lambda: nc.scalar.activation(out=t2[:, :], in_=o[:, :], func=AF.Square, accum_out=st[:, 2:3]),
                 reads=[o], writes=[t2, st])
            s.op("act", lambda: nc.scalar.activation(out=st[:, 3:4], in_=st[:, 2:3], func=AF.Ln, bias=cm["epsc"][:, 0:1],
                                                     scale=1.0 / 64.0), reads=[st, cm["epsc"]], writes=[st])
            s.op("act", lambda: nc.scalar.activation(out=st[:, 4:5], in_=st[:, 3:4], func=AF.Exp, scale=-0.5),
                 reads=[st], writes=[st])
            s.op("dve", lambda: nc.vector.tensor_scalar(out=ot[:, j, :], in0=o[:, :], scalar1=st[:, 4:5], scalar2=li[:, 1:2],
                                                        op0=ALU.mult, op1=ALU.mult), reads=[o, st, li], writes=[ot])
        s.dma("sp", io["oa"][qb * 512:(qb + 1) * 512, :].rearrange("(j p) d -> p j d", p=128), ot[:, :, :], reads=[ot])

    emit_attn_phase(s, cm, T, 2, qT, [k1, k2], vt, None, io["oa"], fin, name="a", lag=cm["lag"])
    if stage == "all":
        s.release(m)


def emit_mix_b(s, cm, io, T, stage="all", bufs=None):
    nc = s.nc
    NT = T // 128
    NQB = T // 512
    if stage in ("all", "alloc"):
        if stage == "all":
            m = s.mark()
        b = dict(qT=s.sb("b_q", [128, T], BF16), kT=s.sb("b_k", [128, T], BF16), vt=s.sb("b_v", [128, NT, 128], BF16),
                 fl=s.sb("fl", [128, NT], F32), fb=s.sb("fb", [128, 2], F32), tu=s.sb("tu", [128, 128], F32),
                 on=s.sb("on", [128, 128], F32), cc=s.sb("cc", [128, NT], F32), inc=s.sb("inc", [128, NT], F32),
                 tmpc=s.sb("tmpc", [128, NT], F32), btab=s.sb("btab", [128, NQB, NT], F32))
        s.op("pool", lambda: nc.gpsimd.memset(b["qT"][64:128, :], 0.0), writes=[b["qT"]])
        s.op("dve", lambda: nc.vector.memset(b["kT"][64:128, :], 0.0), writes=[b["kT"]])
        load_vt(s, b["vt"], io, "vb", T, init=True, load=False)
        s.dma("sp", b["tu"][:, :], io["triuf"][:, :], writes=[b["tu"]])
        s.dma("sp", b["on"][:, :], io["onesf"][:, :], writes=[b["on"]])
        if stage == "alloc":
            return b
        bufs = b
    qT, kT, vt, fl, fb, tu, on, cc, inc_, tmpc, btab = (bufs[k] for k in ("qT", "kT", "vt", "fl", "fb", "tu", "on", "cc", "inc",
                                                                            "tmpc", "btab"))
    if stage in ("all", "load"):
        s.dma("sp", qT[0:64, :], io["qb"][:, :], writes=[qT])
        s.dma("sp", kT[0:64, :], io["kb"][:, :], writes=[kT])
        load_vt(s, vt, io, "vb", T, init=False)
        if stage == "load":
            return
    if "flog_sb" not in io:
        s.dma("sp", fl[:, :], io["flog"][:, :], writes=[fl])
    s.dma("sp", fb[:, 0:1], io["fbias"][:, :], writes=[fb])
    s.op("dve", lambda: nc.vector.tensor_scalar(out=fb[:, 1:2], in0=fb[:, 0:1], scalar1=-1.0, scalar2=None, op0=ALU.mult),
         reads=[fb], writes=[fb])
    if "flog_sb" in io:
        fsb, fap = io["flog_sb"]
        s.op("act", lambda: nc.scalar.activation(out=fl[:, :], in_=fap, func=AF.Exp, bias=fb[:, 1:2], scale=-1.0),
             reads=[fsb, fb], writes=[fl])
    else:
        s.op("act", lambda: nc.scalar.activation(out=fl[:, :], in_=fl[:, :], func=AF.Exp, bias=fb[:, 1:2], scale=-1.0),
             reads=[fl, fb], writes=[fl])
    s.op("act", lambda: nc.scalar.activation(out=fl[:, :], in_=fl[:, :], func=AF.Ln, bias=1.0, scale=1.0),
         reads=[fl], writes=[fl])
    pc = cm["pf"]
    pcv = pc[:, 0, :]
    s.op("pe", lambda: nc.tensor.matmul(pc[:, 0, 0:NT], lhsT=tu[:, :], rhs=fl[:, :], start=True, stop=True),
         reads=[tu, fl], writes=[pc])
    s.op("pe", lambda: nc.tensor.matmul(pc[:, 1, 0:NT], lhsT=on[:, :], rhs=fl[:, :], start=True, stop=True),
         reads=[on, fl], writes=[pc])
    s.op("dve", lambda: nc.vector.tensor_copy(out=inc_[:, :], in_=pc[:, 1, 0:NT]), reads=[pc], writes=[inc_])
    sh = 1
    while sh < NT:
        s.op("dve", lambda: nc.vector.tensor_copy(out=tmpc[:, :], in_=inc_[:, :]), reads=[inc_], writes=[tmpc])
        s.op("dve", lambda: nc.vector.tensor_tensor(out=inc_[:, sh:NT], in0=tmpc[:, sh:NT], in1=tmpc[:, 0:NT - sh], op=ALU.add),
             reads=[tmpc], writes=[inc_])
        sh *= 2
    s.op("dve", lambda: nc.vector.tensor_tensor(out=cc[:, :], in0=pc[:, 0, 0:NT], in1=inc_[:, :], op=ALU.add),
         reads=[pc, inc_], writes=[cc])
    s.op("dve", lambda: nc.vector.tensor_tensor(out=tmpc[:, :], in0=cc[:, :], in1=pc[:, 1, 0:NT], op=ALU.subtract),
         reads=[pc, cc], writes=[tmpc])
    for qb in range(NQB):
        s.op("dve", lambda: nc.vector.tensor_scalar(out=btab[:, qb, :], in0=tmpc[:, :], scalar1=inc_[:, 4 * qb + 1:4 * qb + 2],
                                                    scalar2=None, op0=ALU.subtract), reads=[tmpc, inc_], writes=[btab])

    def bias_fn(qb, t):
        return btab, btab[:, qb, t:t + 1]

    def fin(qb, pos):
        pf = cm["pf"]
        emit_o_to_tokmajor(s, cm, pos[0], pf, 0)
        ot = cm["ot_rr"].next()
        for j in range(4):
            st = cm["st_rr"].next()
            s.op("dve", lambda: nc.vector.tensor_scalar(out=st[:, 0:1], in0=pf[:, j, 64:65], scalar1=1e-30, scalar2=None,
                                                        op0=ALU.max), reads=[pf], writes=[st])
            s.op("dve", lambda: nc.vector.reciprocal(out=st[:, 0:1], in_=st[:, 0:1]), reads=[st], writes=[st])
            s.op("dve", lambda: nc.vector.tensor_scalar(out=ot[:, j, :], in0=pf[:, j, 0:64], scalar1=st[:, 0:1], scalar2=None,
                                                        op0=ALU.mult), reads=[pf, st], writes=[ot])
        s.dma("sp", io["ob"][qb * 512:(qb + 1) * 512, :].rearrange("(j p) d -> p j d", p=128), ot[:, :, :], reads=[ot])

    emit_attn_phase(s, cm, T, 1, qT, [kT], vt, None, io["ob"], fin, bias_fn=bias_fn, name="b", lag=cm["lag"])
    if stage == "all":
        s.release(m)


def mix_consts():
    k = np.arange(128)
    tri = np.where(k[:, None] > k[None, :], NEG, 0.0).astype(np.float32)
    return dict(identb=_bf(np.eye(128, dtype=np.float32)), identf=np.eye(128, dtype=np.float32), trib=_bf(tri),
                triuf=np.triu(np.ones((128, 128), np.float32)), onesf=np.ones((128, 128), np.float32))


def mix_decl_c(nc, io, T):
    NT = T // 128
    QL = NT // 4
    NCT = max(1, T // 2048)
    io.update(dict(
        qc=dram_in(nc, "qc", [128, QL, 512], BF16),
        kskw=dram_in(nc, "kskw", [128, T], BF16),
        vs=dram_in(nc, "vs", [128, NT, 65], BF16), vw=dram_in(nc, "vw", [128, NT, 65], BF16),
        kvin=dram_in(nc, "kvin", [128, T], BF16),
        w1=dram_in(nc, "w1", [2, 2048, 256]), b1=dram_in(nc, "b1", [128, 4]),
        peT=dram_in(nc, "peT", [128, 32]),
        w2=dram_in(nc, "w2", [2, 256, 64]), b2=dram_in(nc, "b2", [2, 64]), b2c=dram_in(nc, "b2c", [64, 1]),
        kgain=dram_in(nc, "kgain", [64, 1]),
        ng=dram_in(nc, "ng", [128, QL, 12]),
        cmask=dram_in(nc, "cmask", [128, QL, NCT, 128], BF16),
        smask=dram_in(nc, "smask", [128, 4, 128], BF16), wmask=dram_in(nc, "wmask", [128, 8, 128], BF16),
        impA=dram_in(nc, "impA", [128, QL, 128]), impB=dram_in(nc, "impB", [128, QL, 128]),
        emat=dram_in(nc, "emat", [128, NT, 128], BF16), ovl=dram_in(nc, "ovl", [128, NCT, 128], BF16),
        ones64=dram_in(nc, "ones64", [64, 64], BF16), onesrow=dram_in(nc, "onesrow", [1, 128], BF16),
        oc=dram_out(nc, "oc", [QL * 128, 256], BF16),
    ))
    return io


def emit_gelu(s, zin_ap, zin_b, out_ap, out_b, tmp, shape_sl):
    nc = s.nc
    t = tmp
    s.op("act", lambda: nc.scalar.activation(out=t[shape_sl], in_=zin_ap, func=AF.Square), reads=[zin_b], writes=[t])
    s.op("dve", lambda: nc.vector.tensor_scalar(out=t[shape_sl], in0=t[shape_sl], scalar1=0.044715, scalar2=1.0,
                                                op0=ALU.mult, op1=ALU.add), reads=[t], writes=[t])
    s.op("dve", lambda: nc.vector.tensor_tensor(out=t[shape_sl], in0=t[shape_sl], in1=zin_ap, op=ALU.mult),
         reads=[t, zin_b], writes=[t])
    s.op("act", lambda: nc.scalar.activation(out=t[shape_sl], in_=t[shape_sl], func=AF.Exp, scale=-GELU_C), reads=[t], writes=[t])
    s.op("dve", lambda: nc.vector.tensor_scalar(out=t[shape_sl], in0=t[shape_sl], scalar1=1.0, scalar2=None, op0=ALU.add),
         reads=[t], writes=[t])
    s.op("dve", lambda: nc.vector.reciprocal(out=t[shape_sl], in_=t[shape_sl]), reads=[t], writes=[t])
    s.op("dve", lambda: nc.vector.tensor_tensor(out=out_ap, in0=t[shape_sl], in1=zin_ap, op=ALU.mult),
         reads=[t, zin_b], writes=[out_b])


def emit_mix_c(s, cm, io, T, cs=None):
    fused = cs is not None
    cs = cs if fused else [None]
    nc = s.nc
    NT = T // 128
    QL = NT // 4
    NCT = max(1, T // 2048)
    Nc = T // 16 - 1
    NCP = NCT * 128 if Nc > 128 else 128
    NCW = min(Nc, 511)
    assert Nc <= 511
    m = s.mark()
    ident_b = cm["ident_b"]
    ps_l = cm["ps_rr"].items
    po_l = cm["po_rr"].items
    pf, pf2 = cm["pf"], cm["pf2"]

    def ld(name, shape, dt, src, q="sp"):
        b = s.sb(name, shape, dt)
        idx = tuple(slice(None) for _ in shape)
        s.dma(q, b[idx], src, writes=[b])
        return b

    qc = s.sb("c_q", [128, QL, 512], BF16)
    qc2 = s.sb("c_q2", [128, QL, 512], BF16)
    s.op("pool", lambda: nc.gpsimd.memset(qc[64:128, :, :], 0.0), writes=[qc])
    s.op("dve", lambda: nc.vector.memset(qc2[0:64, :, :], 0.0), writes=[qc2])
    kk = ld("c_kk", [128, T], BF16, io["kskw"][:, :])
    vs = s.sb("c_vs", [128, NT, 128], BF16)
    vw = s.sb("c_vw", [128, NT, 128], BF16)
    load_vt(s, vs, io, "vs", T)
    load_vt(s, vw, io, "vw", T)
    emat = ld("c_e", [128, NT, 128], BF16, io["emat"][:, :, :])
    ovl = ld("c_ovl", [128, NCT, 128], BF16, io["ovl"][:, :, :])
    smask = s.sb("c_sm", [128, 4, 128], BF16)
    wmask = s.sb("c_wm", [128, 8, 128], BF16)
    ngt = s.sb("c_ng", [128, QL, 12], F32)
    ones64 = ld("c_o64", [64, 64], BF16, io["ones64"][:, :])
    onesrow = ld("c_orow", [1, 128], BF16, io["onesrow"][:, :])
    kgain = ld("c_kg", [64, 1], F32, io["kgain"][:, :])
    b2c = ld("c_b2c", [64, 1], F32, io["b2c"][:, :])
    b1 = ld("c_b1", [128, 4], F32, io["b1"][:, :])

    ktc = s.sb("c_ktc", [128, NCP], BF16)
    vc = s.sb("c_vc", [128, NCT, 128], BF16)
    s.op("dve", lambda: nc.vector.memset(ktc[:, :], 0.0), writes=[ktc])
    s.op("dve", lambda: nc.vector.memset(vc[:, :, :], 0.0), writes=[vc])
    s.op("dve", lambda: nc.vector.memset(vc[:, :, 64:65], 1.0), writes=[vc])

    m2 = s.mark()
    kvin = ld("c_kvin", [128, T], BF16, io["kvin"][:, :])
    w1sb = s.sb("c_w1", [128, 32, 256], BF16)
    for x in range(2):
        s.dma("pool", w1sb[x * 64:(x + 1) * 64, :, :], io["w1"][x].rearrange("(j d) f -> d j f", d=64), writes=[w1sb])
    peT = s.sb("c_pe", [128, 32], BF16)
    s.dma("pool", peT[:, :], io["peT"][:, :], writes=[peT])
    w2sb = s.sb("c_w2", [128, 2, 2, 64], BF16)
    for x in range(2):
        s.dma("pool", w2sb[:, x, :, :], io["w2"][x].rearrange("(hh f) d -> f hh d", f=128), writes=[w2sb])
    b2row = s.sb("c_b2r", [1, 64], BF16)
    s.dma("pool", b2row[:, :], io["b2"][1:2, :], writes=[b2row])
    hacc = [ps_l[0], ps_l[1], ps_l[2], po_l[0]]
    pcol = po_l[1]
    for x in range(2):
        for hh in range(2):
            hp = hacc[x * 2 + hh]
            for j in range(32):
                s.op("pe", lambda: nc.tensor.matmul(hp[:, 0:NCW], lhsT=w1sb[x * 64:(x + 1) * 64, j, hh * 128:(hh + 1) * 128],
                                                    rhs=kvin[x * 64:(x + 1) * 64, j:j + 16 * (NCW - 1) + 1:16],
                                                    start=(j == 0), stop=(j == 31)),
                     reads=[w1sb, kvin], writes=[hp], inc=(j == 31))
            for j in range(32):
                s.op("pe", lambda: nc.tensor.matmul(pcol[:, x * 2 + hh:x * 2 + hh + 1],
                                                    lhsT=w1sb[x * 64:(x + 1) * 64, j, hh * 128:(hh + 1) * 128],
                                                    rhs=peT[x * 64:(x + 1) * 64, j:j + 1], start=(j == 0), stop=(j == 31)),
                     reads=[w1sb, peT], writes=[pcol], inc=(j == 31))
    hbias = s.sb("c_hb", [128, 4], F32)
    s.op("dve", lambda: nc.vector.tensor_tensor(out=hbias[:, :], in0=pcol[:, 0:4], in1=b1[:, :], op=ALU.add),
         reads=[pcol, b1], writes=[hbias])
    gh = []
    for x in range(2):
        for hh in range(2):
            k = x * 2 + hh
            z = s.sb("c_z%d" % k, [128, 512], F32)
            tmp = s.sb("c_zt%d" % k, [128, 512], F32)
            gb = s.sb("c_g%d" % k, [128, 512], BF16)
            s.op("act", lambda: nc.scalar.activation(out=z[:, 0:NCW], in_=hacc[k][:, 0:NCW], func=AF.Identity,
                                                     bias=hbias[:, k:k + 1], scale=1.0), reads=[hacc[k], hbias], writes=[z])
            emit_gelu(s, z[:, 0:NCW], z, gb[:, 0:NCW], gb, tmp, (slice(None), slice(0, NCW)))
            gh.append(gb)
    pk = po_l[2]
    for hh in range(2):
        s.op("pe", lambda: nc.tensor.matmul(pk[0:64, 0:NCW], lhsT=w2sb[:, 0, hh, :], rhs=gh[hh][:, 0:NCW],
                                            start=(hh == 0), stop=(hh == 1)), reads=[w2sb, gh[hh]], writes=[pk], inc=(hh == 1))
    kz = s.sb("c_kz", [64, 512], F32)
    ksq = s.sb("c_ksq", [64, 512], BF16)
    krs = s.sb("c_krs", [64, 512], F32)
    s.op("act", lambda: nc.scalar.activation(out=kz[:, 0:NCW], in_=pk[0:64, 0:NCW], func=AF.Identity, bias=b2c[:, 0:1], scale=1.0),
         reads=[pk, b2c], writes=[kz])
    s.op("act", lambda: nc.scalar.activation(out=ksq[:, 0:NCW], in_=kz[:, 0:NCW], func=AF.Square), reads=[kz], writes=[ksq])
    pq = ps_l[0]
    s.op("pe", lambda: nc.tensor.matmul(pq[0:64, 0:NCW], lhsT=ones64[:, :], rhs=ksq[:, 0:NCW], start=True, stop=True),
         reads=[ones64, ksq], writes=[pq])
    s.op("act", lambda: nc.scalar.activation(out=krs[:, 0:NCW], in_=pq[0:64, 0:NCW], func=AF.Ln, bias=cm["epsc"][0:64, 0:1],
                                             scale=1.0 / 64.0), reads=[pq, cm["epsc"]], writes=[krs])
    s.op("act", lambda: nc.scalar.activation(out=krs[:, 0:NCW], in_=krs[:, 0:NCW], func=AF.Exp, scale=-0.5), reads=[krs], writes=[krs])
    s.op("dve", lambda: nc.vector.scalar_tensor_tensor(out=ktc[0:64, 0:NCW], in0=kz[:, 0:NCW], scalar=kgain[:, 0:1], in1=krs[:, 0:NCW],
                                                       op0=ALU.mult, op1=ALU.mult), reads=[kz, kgain, krs], writes=[ktc])
    for nt in range(NCT):
        n0 = nt * 128
        nn = min(128, Nc - n0)
        pv = ps_l[1 + nt % 2]
        for hh in range(2):
            s.op("pe", lambda: nc.tensor.matmul(pv[0:nn, 0:64], lhsT=gh[2 + hh][:, n0:n0 + nn], rhs=w2sb[:, 1, hh, :],
                                                start=(hh == 0), stop=False), reads=[gh[2 + hh], w2sb], writes=[pv], inc=False)
        s.op("pe", lambda: nc.tensor.matmul(pv[0:nn, 0:64], lhsT=onesrow[0:1, 0:nn], rhs=b2row[0:1, :], start=False, stop=True),
             reads=[onesrow, b2row], writes=[pv])
        s.op("act", lambda: nc.scalar.copy(out=vc[0:nn, nt, 0:64], in_=pv[0:nn, 0:64]), reads=[pv], writes=[vc])
    s.release(m2)

    cmk_rr = RR([s.sb("c_cmk%d" % j, [128, NCT, 128], BF16) for j in range(2)])
    ia_rr = RR([s.sb("c_ia%d" % j, [128, 128], F32) for j in range(2)])
    ib_rr = RR([s.sb("c_ib%d" % j, [128, 128], F32) for j in range(2)])
    imp_rr = RR([s.sb("c_imp%d" % j, [128, 128], F32) for j in range(2)])
    imp2_rr = RR([s.sb("c_impb%d" % j, [128, 128], F32) for j in range(2)])
    m8_rr = RR([s.sb("c_m8%d" % j, [128, 16], F32) for j in range(2)])
    mbT_rr = RR([s.sb("c_mbT%d" % j, [128, 128], BF16) for j in range(2)])
    oco_rr = RR([s.sb("c_oc%d" % j, [128, 4, 64], F32) for j in range(2)])
    gw_rr = RR([s.sb("c_gw%d" % j, [128, 12], F32) for j in range(2)])
    oo_rr = RR([s.sb("c_oo%d" % j, [128, 4, 64], F32) for j in range(2)])
    ob_rr = RR([s.sb("c_ob%d" % j, [128, 4, 64], BF16) for j in range(2)])

    pipe = Pipe(2)

    def masked_tile(kbuf, prow, t, Q, masks, vbuf, vt_idx, po, first, last, extra=None):
        ps = cm["ps_rr"].next()
        nm = len(masks)
        s.op("pe", lambda: nc.tensor.matmul(ps[:, :], lhsT=kbuf[:, t * 128:(t + 1) * 128], rhs=Q,
                                            start=True, stop=(nm == 0)), reads=[kbuf, qc, qc2], writes=[ps], inc=(nm == 0))
        for mi, (la, lb, ra, rb) in enumerate(masks):
            for h in range(4):
                lastm = (mi == nm - 1 and h == 3)
                s.op("pe", lambda: nc.tensor.matmul(ps[:, h * 128:(h + 1) * 128], lhsT=la, rhs=ra, start=False, stop=lastm),
                     reads=[lb, rb], writes=[ps], inc=lastm)
        pt = cm["pt_rr"].next()
        s.op("act", lambda: nc.scalar.activation(out=pt[:, :], in_=ps[:, :], func=AF.Exp), reads=[ps], writes=[pt])

        def back(pt=pt, po=po, vbuf=vbuf, vt_idx=vt_idx, first=first, last=last, extra=extra):
            s.op("pe", lambda: nc.tensor.matmul(po[:, :], lhsT=vbuf[:, vt_idx, :], rhs=pt[:, :], start=first, stop=last),
                 reads=[vbuf, pt], writes=[po])
            if extra is not None:
                extra(pt)
        pipe.push(back)

    for ci in cs:
        def gk(key):
            return io[key][ci] if fused else io[key]
        if fused:
            for h in range(4):
                r0 = (h % 2) * 64
                srcq = io["zq"][h // 2][r0:r0 + 64, :].rearrange("d (i c q) -> d i c q", c=4, q=128)[:, :, ci, :]
                s.dma("sp", qc[0:64, :, h * 128:(h + 1) * 128], srcq, writes=[qc])
                s.dma("sp", qc2[64:128, :, h * 128:(h + 1) * 128], srcq, writes=[qc2])
            msb, mview = io["misc_sb"]
            s.op("act", lambda: nc.scalar.activation(out=ngt[:, :, :], in_=mview[:, ci:NT:4, 4:16], func=AF.Exp, scale=-1.0),
                 reads=[msb], writes=[ngt])
        else:
            s.dma("sp", qc[0:64, :, :], io["qc"][0:64, :, :], writes=[qc])
            s.dma("sp", qc2[64:128, :, :], io["qc"][64:128, :, :], writes=[qc2])
            s.dma("sp", ngt[:, :, :], io["ng"][:, :, :], writes=[ngt])
            s.op("act", lambda: nc.scalar.activation(out=ngt[:, :, :], in_=ngt[:, :, :], func=AF.Exp, scale=-1.0),
                 reads=[ngt], writes=[ngt])
        s.op("dve", lambda: nc.vector.tensor_scalar(out=ngt[:, :, :], in0=ngt[:, :, :], scalar1=1.0, scalar2=None, op0=ALU.add),
             reads=[ngt], writes=[ngt])
        s.op("dve", lambda: nc.vector.reciprocal(out=ngt[:, :, :], in_=ngt[:, :, :]), reads=[ngt], writes=[ngt])
        s.dma("sp", smask[:, :, :], gk("smask")[:, :, :], writes=[smask])
        s.dma("sp", wmask[:, :, :], gk("wmask")[:, :, :], writes=[wmask])
        for i in range(QL):
            Qlo = qc[:, i, :]
            Qhi = qc2[:, i, :]
            cmk = cmk_rr.next()
            ia = ia_rr.next()
            ib = ib_rr.next()
            s.dma("sp", cmk[:, :, :], gk("cmask")[:, i, :, :], writes=[cmk])
            s.dma("sp", ia[:, :], gk("impA")[:, i, :], writes=[ia])
            s.dma("sp", ib[:, :], gk("impB")[:, i, :], writes=[ib])
            po_c, po_s, po_w = po_l[0], po_l[1], po_l[2]
            nct = min(NCT, i // 4 + 1)
            for nt in range(nct):
                def imp_mm(pt, nt=nt, nct=nct):
                    for h in range(4):
                        s.op("pe", lambda: nc.tensor.matmul(pf2[:, h, :], lhsT=pt[:, h * 128:(h + 1) * 128], rhs=ovl[:, nt, :],
                                                            start=(nt == 0 and h == 0), stop=(nt == nct - 1 and h == 3),
                                                            skip_group_check=True), reads=[pt, ovl], writes=[pf2],
                             inc=(h == 3))
                masked_tile(ktc, (0, 64), nt, Qlo, [(ident_b[:, :], ident_b, cmk[:, nt, :], cmk)], vc, nt, po_c,
                            nt == 0, nt == nct - 1, extra=imp_mm)
            pipe.flush()
            emit_o_to_tokmajor(s, cm, po_c, pf, 0)
            st = cm["st_rr"].next()
            rsum = cm["st_rr"].next()
            gw = gw_rr.next()
            s.op("dve", lambda: nc.vector.tensor_scalar(out=st[:, 0:4], in0=pf[:, :, 64], scalar1=1e-30, scalar2=None, op0=ALU.max),
                 reads=[pf], writes=[st])
            s.op("dve", lambda: nc.vector.reciprocal(out=rsum[:, 0:4], in_=st[:, 0:4]), reads=[st], writes=[rsum])
            oco = oco_rr.next()
            s.op("dve", lambda: nc.vector.tensor_copy(out=oco[:, :, :], in_=pf[:, :, 0:64]), reads=[pf], writes=[oco])
            imp = imp_rr.next()
            s.op("dve", lambda: nc.vector.tensor_scalar(out=imp[:, :], in0=pf2[:, 0, :], scalar1=rsum[:, 0:1], scalar2=None, op0=ALU.mult),
                 reads=[pf2, rsum], writes=[imp])
            for h in range(1, 4):
                s.op("dve", lambda: nc.vector.scalar_tensor_tensor(out=imp[:, :], in0=pf2[:, h, :], scalar=rsum[:, h:h + 1], in1=imp[:, :],
                                                                   op0=ALU.mult, op1=ALU.add), reads=[pf2, rsum, imp], writes=[imp])
            s.op("dve", lambda: nc.vector.tensor_tensor(out=imp[:, :], in0=imp[:, :], in1=ia[:, :], op=ALU.mult), reads=[imp, ia], writes=[imp])
            s.op("dve", lambda: nc.vector.tensor_tensor(out=imp[:, :], in0=imp[:, :], in1=ib[:, :], op=ALU.add), reads=[imp, ib], writes=[imp])
            m8 = m8_rr.next()
            imp2 = imp2_rr.next()
            s.op("dve", lambda: nc.vector.max(out=m8[:, 0:8], in_=imp[:, :]), reads=[imp], writes=[m8])
            s.op("dve", lambda: nc.vector.match_replace(out=imp2[:, :], in_to_replace=m8[:, 0:8], in_values=imp[:, :], imm_value=-1e9),
                 reads=[imp, m8], writes=[imp2])
            s.op("dve", lambda: nc.vector.max(out=m8[:, 8:16], in_=imp2[:, :]), reads=[imp2], writes=[m8])
            s.op("dve", lambda: nc.vector.tensor_scalar(out=imp2[:, :], in0=imp[:, :], scalar1=m8[:, 15:16], scalar2=NEG,
                                                        op0=ALU.is_lt, op1=ALU.mult), reads=[imp, m8], writes=[imp2])
            tl = [4 * (i - 1) + u for u in range(8) if 4 * (i - 1) + u >= 0]
            for t in tl:
                u = t - 4 * (i - 1)
                masked_tile(kk, (64, 128), t, Qhi, [(ident_b[:, :], ident_b, wmask[:, u, :], wmask)], vw, t, po_w,
                            t == tl[0], t == tl[-1])
            ptr = cm["ps_rr"].next()
            s.op("pe", lambda: nc.tensor.transpose(out=ptr[:, 0:128], in_=imp2[:, :], identity=cm["ident_f"][:, :]),
                 reads=[imp2, cm["ident_f"]], writes=[ptr])
            mbT = mbT_rr.next()
            s.op("dve", lambda: nc.vector.tensor_copy(out=mbT[:, :], in_=ptr[:, 0:128]), reads=[ptr], writes=[mbT])
            nts = 4 * i + 4
            for t in range(nts):
                masks = [(emat[:, t, :], emat, mbT[:, :], mbT)]
                if t >= 4 * i:
                    masks.append((ident_b[:, :], ident_b, smask[:, t - 4 * i, :], smask))
                masked_tile(kk, (0, 64), t, Qlo, masks, vs, t, po_s, t == 0, t == nts - 1)
            pipe.flush()
            emit_o_to_tokmajor(s, cm, po_s, pf, 0)
            st2 = cm["st_rr"].next()
            s.op("dve", lambda: nc.vector.tensor_scalar(out=st2[:, 0:4], in0=pf[:, :, 64], scalar1=1e-30, scalar2=None, op0=ALU.max),
                 reads=[pf], writes=[st2])
            s.op("dve", lambda: nc.vector.reciprocal(out=st2[:, 0:4], in_=st2[:, 0:4]), reads=[st2], writes=[st2])
            gv = ngt[:, i, :].rearrange("p (h b) -> p h b", b=3)
            gwv = gw[:, :].rearrange("p (h b) -> p h b", b=3)
            s.op("dve", lambda: nc.vector.tensor_tensor(out=gwv[:, :, 0], in0=gv[:, :, 0], in1=rsum[:, 0:4], op=ALU.mult),
                 reads=[ngt, rsum], writes=[gw])
            s.op("dve", lambda: nc.vector.tensor_tensor(out=gwv[:, :, 1], in0=gv[:, :, 1], in1=st2[:, 0:4], op=ALU.mult),
                 reads=[ngt, st2], writes=[gw])
            oo = oo_rr.next()
            for h in range(4):
                s.op("dve", lambda: nc.vector.tensor_scalar(out=oo[:, h, :], in0=oco[:, h, :], scalar1=gw[:, 3 * h:3 * h + 1], scalar2=None,
                                                            op0=ALU.mult), reads=[oco, gw], writes=[oo])
                s.op("dve", lambda: nc.vector.scalar_tensor_tensor(out=oo[:, h, :], in0=pf[:, h, 0:64], scalar=gw[:, 3 * h + 1:3 * h + 2],
                                                                   in1=oo[:, h, :], op0=ALU.mult, op1=ALU.add), reads=[pf, gw, oo], writes=[oo])
            emit_o_to_tokmajor(s, cm, po_w, pf, 0)
            st3 = cm["st_rr"].next()
            s.op("dve", lambda: nc.vector.tensor_scalar(out=st3[:, 0:4], in0=pf[:, :, 64], scalar1=1e-30, scalar2=None, op0=ALU.max),
                 reads=[pf], writes=[st3])
            s.op("dve", lambda: nc.vector.reciprocal(out=st3[:, 0:4], in_=st3[:, 0:4]), reads=[st3], writes=[st3])
            s.op("dve", lambda: nc.vector.tensor_tensor(out=gwv[:, :, 2], in0=gv[:, :, 2], in1=st3[:, 0:4], op=ALU.mult),
                 reads=[ngt, st3], writes=[gw])
            ob = ob_rr.next()
            for h in range(4):
                s.op("dve", lambda: nc.vector.scalar_tensor_tensor(out=ob[:, h, :], in0=pf[:, h, 0:64], scalar=gw[:, 3 * h + 2:3 * h + 3],
                                                                   in1=oo[:, h, :], op0=ALU.mult, op1=ALU.add), reads=[pf, gw, oo], writes=[ob])
            orow = ((4 * i + ci) if fused else i) * 128
            s.dma("sp", io["oc"][orow:orow + 128, :], ob[:, :, :].rearrange("p h d -> p (h d)"), reads=[ob])
    s.release(m)


def build_mix(T, parts="abc"):
    nc = bass.Bass("TRN2", target_bir_lowering=False)
    io = mix_decl(nc, T)
    if "c" in parts:
        mix_decl_c(nc, io, T)
    s = S(nc)
    cm = mix_common(s, io)
    if "a" in parts:
        emit_mix_a(s, cm, io, T)
    if "b" in parts:
        emit_mix_b(s, cm, io, T)
    if "c" in parts:
        emit_mix_c(s, cm, io, T)
    s.finish()
    s.close()
    return nc


def mix_consts_c(T, c):
    NT = T // 128
    QL = NT // 4
    NCT = max(1, T // 2048)
    Nc = T // 16 - 1
    NS = T // 64
    ar = np.arange(128)
    cmask = np.zeros((128, QL, NCT, 128), np.float32)
    impA = np.zeros((128, QL, 128), np.float32)
    impB = np.zeros((128, QL, 128), np.float32)
    for i in range(QL):
        qpos = 128 * (4 * i + c) + ar
        for nt in range(NCT):
            n = 128 * nt + ar
            ok = (16 * n[:, None] + 31 <= qpos[None, :]) & (n[:, None] < Nc)
            cmask[:, i, nt, :] = np.where(ok, 0.0, NEG)
        j = ar
        cur = qpos // 64
        forced = (j[None, :] == 0) | (j[None, :] == cur[:, None]) | (j[None, :] == cur[:, None] - 1)
        valid = (j[None, :] * 64 <= qpos[:, None]) & (j[None, :] < NS)
        impA[:, i, :] = (valid & ~forced).astype(np.float32)
        impB[:, i, :] = np.where(forced & (j[None, :] < NS), 1.0e4, np.where(valid, 0.0, -1.0))
    smask = np.zeros((128, 4, 128), np.float32)
    for u in range(4):
        kpos = 128 * u + ar
        qp = 128 * c + ar
        smask[:, u, :] = np.where(kpos[:, None] <= qp[None, :], 0.0, NEG)
    wmask = np.zeros((128, 8, 128), np.float32)
    for u in range(8):
        dist = 128 * (c + 4 - u) + ar[None, :] - ar[:, None]
        wmask[:, u, :] = np.where((dist >= 0) & (dist < 512), 0.0, NEG)
    emat = np.zeros((128, NT, 128), np.float32)
    for t in range(NT):
        for k in range(128):
            jj = 2 * t + k // 64
            if jj < 128:
                emat[jj, t, k] = 1.0
    ovl = np.zeros((128, NCT, 128), np.float32)
    for nt in range(NCT):
        n = 128 * nt + ar
        o = (n[:, None] * 16 < (ar[None, :] + 1) * 64) & (n[:, None] * 16 + 32 > ar[None, :] * 64) & (n[:, None] < Nc) \
            & (ar[None, :] < NS)
        ovl[:, nt, :] = o
    return dict(cmask=_bf(cmask), impA=impA, impB=impB, smask=_bf(smask), wmask=_bf(wmask), emat=_bf(emat), ovl=_bf(ovl),
                ones64=_bf(np.ones((64, 64), np.float32)), onesrow=_bf(np.ones((1, 128), np.float32)))


def build_merge(NT, TB=512):
    nc = bass.Bass("TRN2", target_bir_lowering=False)
    x = dram_in(nc, "x", [NT, D])
    g = dram_in(nc, "g", [D])
    w_in = dram_in(nc, "w_in", [D, 6800])
    w_br = dram_in(nc, "w_br", [4, 256, D])
    w_o = dram_in(nc, "w_o", [D, D])
    ident = dram_in(nc, "ident", [128, 128], BF16)
    obr = dram_in(nc, "obr", [NT, D], BF16)
    y = dram_out(nc, "y", [NT, D])
    s = S(nc)
    emit_merge(s, x, g, w_in, w_br, w_o, ident, obr, y, NT, TB)
    s.finish()
    s.close()
    return nc


def emit_merge(s, x, g, w_in, w_br, w_o, ident, obr, y, NT, TB=512):
    nc = s.nc
    m_ = s.mark()
    ntile = TB // 128
    ident_b = s.sb("ident_b", [128, 128], BF16)
    s.dma("sp", ident_b[:, :], ident[:, :], writes=[ident_b])
    gcol = s.sb("gcol", [128, NKC], F32)
    s.dma("sp", gcol[:, :], g.rearrange("(c p) -> p c", p=128), writes=[gcol], allow_slow_non_contiguous=True)
    epsc = s.sb("epsc", [128, 1], F32)
    s.op("dve", lambda: nc.vector.memset(epsc[:, :], EPS), writes=[epsc])
    wg = [s.sb("wg%d" % c, [128, 4096], BF16) for c in range(NKC)]
    wb = [s.sb("wb%d" % c, [128, D], BF16) for c in range(8)]
    wo = [s.sb("wo%d" % c, [128, D], BF16) for c in range(NKC)]
    for c in range(NKC):
        for hf in range(2):
            s.dma("pool", wg[c][:, hf * 2048:(hf + 1) * 2048], w_in[c * 128:(c + 1) * 128, 2704 + hf * 2048:2704 + (hf + 1) * 2048],
                  writes=[wg[c]])
    for n in range(4):
        for cc in range(2):
            s.dma("pool", wb[2 * n + cc][:, :], w_br[n, cc * 128:(cc + 1) * 128, :], writes=[wb[2 * n + cc]])
    for c in range(NKC):
        s.dma("pool", wo[c][:, :], w_o[c * 128:(c + 1) * 128, :], writes=[wo[c]])
    xn = [s.sb("xn%d" % j, [128, D], F32) for j in range(ntile)]
    xr_rr = RR([s.sb("xr%d" % j, [128, D], F32) for j in range(2)])
    ots = [[s.sb("ot%d_%d" % (k, j), [128, D], BF16) for j in range(ntile)] for k in range(2)]
    hb_rr = RR([s.sb("hb%d" % j, [128, D], BF16) for j in range(2 * ntile)])
    stat_rr = RR([s.sb("st%d" % j, [128, 16], F32) for j in range(4)])
    hT = s.sb("hT", [128, NKC, TB], BF16)
    oT = s.sb("oT", [128, 8, TB], BF16)
    mT = [s.sb("mT%d" % c, [128, TB], BF16) for c in range(8)]
    pT_rr = RR([s.ps("pT%d" % j, [128, TB], BF16) for j in range(2)])
    pg_rr = RR([s.ps("pg%d" % j, [128, 512], F32) for j in range(2)])
    pp_rr = RR([s.ps("pp%d" % j, [128, 512], F32) for j in range(2)])
    po_rr = RR([s.ps("po%d" % j, [128, 512], F32) for j in range(2)])
    sg_rr = RR([s.sb("sg%d" % j, [128, TB], F32) for j in range(3)])
    acc_rr = RR([s.sb("acc%d" % j, [128, TB], F32) for j in range(2)])
    nblk = NT // TB

    def prep_a(tb):
        ot = ots[tb % 2]
        for j in range(ntile):
            r0 = tb * TB + j * 128
            s.dma("sp", xn[j][:, :], x[r0:r0 + 128, :], writes=[xn[j]])
            s.dma("sp", ot[j][:, :], obr[r0:r0 + 128, :], writes=[ot[j]])
        return emit_norm(s, epsc, xn, hb_rr, None, stat_rr, ntile)

    def prep_b(tb, hbs):
        ot = ots[tb % 2]
        emit_transpose_T(s, hbs, gcol, hT, ident_b, pT_rr, ntile)
        for c in range(8):
            pT = pT_rr.next()
            for j in range(ntile):
                s.op("pe", lambda: nc.tensor.transpose(out=pT[:, j * 128:(j + 1) * 128], in_=ot[j][:, c * 128:(c + 1) * 128],
                                                       identity=ident_b[:, :]), reads=[ot[j], ident_b], writes=[pT], inc=(j == ntile - 1))
            if c % 2 == 0:
                s.op("dve", lambda: nc.vector.tensor_copy(out=oT[:, c, :], in_=pT[:, 0:TB]), reads=[pT], writes=[oT])
            else:
                s.op("act", lambda: nc.scalar.copy(out=oT[:, c, :], in_=pT[:, 0:TB]), reads=[pT], writes=[oT])

    hbs_next = prep_a(0)
    prep_b(0, hbs_next)
    for tb in range(nblk):
        t0 = tb * TB
        if tb + 1 < nblk:
            hbs_next = prep_a(tb + 1)
        for dc in range(8):
            acc = acc_rr.next()
            for n in range(4):
                pg = pg_rr.next()
                pp = pp_rr.next()
                for c in range(NKC):
                    s.op("pe", lambda: nc.tensor.matmul(pg[:, 0:TB], lhsT=wg[c][:, n * 1024 + dc * 128:n * 1024 + (dc + 1) * 128],
                                                        rhs=hT[:, c, :], start=(c == 0), stop=(c == NKC - 1)),
                         reads=[wg[c], hT], writes=[pg], inc=(c == NKC - 1))
                for cc in range(2):
                    s.op("pe", lambda: nc.tensor.matmul(pp[:, 0:TB], lhsT=wb[2 * n + cc][:, dc * 128:(dc + 1) * 128],
                                                        rhs=oT[:, 2 * n + cc, :], start=(cc == 0), stop=(cc == 1)),
                         reads=[wb[2 * n + cc], oT], writes=[pp], inc=(cc == 1))
                sg = sg_rr.next()
                s.op("act", lambda: nc.scalar.activation(out=sg[:, :], in_=pg[:, 0:TB], func=AF.Sigmoid), reads=[pg], writes=[sg])
                if n == 0:
                    s.op("dve", lambda: nc.vector.tensor_tensor(out=acc[:, :], in0=sg[:, :], in1=pp[:, 0:TB], op=ALU.mult),
                         reads=[sg, pp], writes=[acc])
                else:
                    s.op("dve", lambda: nc.vector.tensor_tensor(out=sg[:, :], in0=sg[:, :], in1=pp[:, 0:TB], op=ALU.mult),
                         reads=[sg, pp], writes=[sg])
                    if n < 3:
                        s.op("pool", lambda: nc.gpsimd.tensor_tensor(out=acc[:, :], in0=acc[:, :], in1=sg[:, :], op=ALU.add),
                             reads=[acc, sg], writes=[acc])
                    else:
                        s.op("pool", lambda: nc.gpsimd.tensor_tensor(out=mT[dc][:, :], in0=acc[:, :], in1=sg[:, :], op=ALU.add),
                             reads=[acc, sg], writes=[mT[dc]])
        if tb + 1 < nblk:
            prep_b(tb + 1, hbs_next)
        for j in range(ntile):
            xr = xr_rr.next()
            s.dma("sp", xr[:, :], x[t0 + j * 128:t0 + (j + 1) * 128, :], writes=[xr])
            for hf in range(2):
                po = po_rr.next()
                for dc in range(8):
                    s.op("pe", lambda: nc.tensor.matmul(po[:, :], lhsT=mT[dc][:, j * 128:(j + 1) * 128],
                                                        rhs=wo[dc][:, hf * 512:(hf + 1) * 512], start=(dc == 0), stop=(dc == 7)),
                         reads=[mT[dc], wo[dc]], writes=[po], inc=(dc == 7))
                s.op("dve", lambda: nc.vector.tensor_tensor(out=xr[:, hf * 512:(hf + 1) * 512], in0=po[:, :],
                                                            in1=xr[:, hf * 512:(hf + 1) * 512], op=ALU.add),
                     reads=[po, xr], writes=[xr])
            s.dma("sp", y[t0 + j * 128:t0 + (j + 1) * 128, :], xr[:, :], reads=[xr])
    s.release(m_)


PARAM_SHAPES = dict(
    ffn1_norm=("L", D), ffn1_w_in=("L", D, 2 * DFF), ffn1_w_out=("L", DFF, D), mix_norm=("L", D), w_in=("L", D, 6800),
    nsa_phi_w1=("L", 2, 2048, 256), nsa_phi_w2=("L", 2, 256, 64), nsa_phi_b2=("L", 2, 64),
    w_branch=("L", 4, 256, D), w_out=("L", D, D), ffn2_norm=("L", D), ffn2_w_in=("L", D, 2 * DFF), ffn2_w_out=("L", DFF, D),
    gains=("L", 128, 6), vgain=("L", 128, 256), wsT=("L", 4, 128, 128), bsT=("L", 128, 4), lamp=("L", 128, 4, 32),
    lami=("L", 128, 2), fbias=("L", 4, 128, 1), b1l=("L", 128, 4), peT=("L", 128, 32), b2c=("L", 64, 1), kgain=("L", 64, 1),
)


def fused_const_shapes(T):
    NT = T // 128
    QL = NT // 4
    NCT = max(1, T // 2048)
    return dict(
        ident=([128, 128], BF16), identf=([128, 128], F32), blk=([2, 128, 128], BF16), triu=([128, 128], F32),
        trib=([128, 128], BF16), triuf=([128, 128], F32), onesf=([128, 128], F32), ones64=([64, 64], BF16),
        onesrow=([1, 128], BF16), cmask=([4, 128, QL, NCT, 128], BF16), smask=([4, 128, 4, 128], BF16),
        wmask=([4, 128, 8, 128], BF16), impA=([4, 128, QL, 128], F32), impB=([4, 128, QL, 128], F32),
        emat=([128, NT, 128], BF16), ovl=([128, NCT, 128], BF16))


def fused_consts(T):
    pc = proj_consts()
    mc = mix_consts()
    cc = [mix_consts_c(T, c) for c in range(4)]
    d = dict(ident=pc["ident"], identf=mc["identf"], blk=pc["blk"], triu=pc["triu"], trib=mc["trib"], triuf=mc["triuf"],
             onesf=mc["onesf"], ones64=cc[0]["ones64"], onesrow=cc[0]["onesrow"], emat=cc[0]["emat"], ovl=cc[0]["ovl"])
    for k in ("cmask", "smask", "wmask", "impA", "impB"):
        d[k] = np.ascontiguousarray(np.stack([cc[c][k] for c in range(4)], 0))
    return d


def emit_mix_fused(s, F, l, T):
    nc = s.nc
    NT = T // 128
    m = s.mark()
    io0 = dict(identb=F["ident"], identf=F["identf"], trib=F["trib"])
    misc_sb = s.sb("misc_sb", [128, NT, 16], F32)
    m_ab = s.mark()
    cm = mix_common(s, io0, n_ps=4, with_pf2=False)
    for j0 in range(0, NT, 8):
        j1 = min(NT, j0 + 8)
        s.dma("sp", misc_sb[:, j0:j1, :], F["misc"][j0 * 128:j1 * 128, :].rearrange("(j p) c -> p j c", p=128), writes=[misc_sb])
    zfm, vab, obr = F["zfm"], F["vab"], F["obr"]

    def io_a(h):
        r0 = (h % 2) * 64
        io = dict(io0)
        io.update(qa=zfm[h // 2][r0:r0 + 64, :], ka=zfm[2 + h // 2][r0:r0 + 64, :], va_src=vab[:, h * 64:(h + 1) * 64],
                  lamp=F["lamp"][l], lami=F["lami"][l], oa=obr[:, h * 64:(h + 1) * 64])
        return io

    def io_b(h):
        r0 = (h % 2) * 64
        io = dict(io0)
        io.update(qb=zfm[4 + h // 2][r0:r0 + 64, :], kb=zfm[6 + h // 2][r0:r0 + 64, :],
                  vb_src=vab[:, 256 + h * 64:256 + (h + 1) * 64], flog_sb=(misc_sb, misc_sb[:, :, h]), fbias=F["fbias"][l, h],
                  triuf=F["triuf"], onesf=F["onesf"], ob=obr[:, 256 + h * 64:256 + (h + 1) * 64])
        return io

    ba = emit_mix_a(s, cm, io_a(0), T, stage="alloc")
    bb = emit_mix_b(s, cm, io_b(0), T, stage="alloc")
    emit_mix_a(s, cm, io_a(0), T, stage="load", bufs=ba)
    emit_mix_b(s, cm, io_b(0), T, stage="load", bufs=bb)
    for h in range(4):
        emit_mix_a(s, cm, io_a(h), T, stage="compute", bufs=ba)
        if h + 1 < 4:
            emit_mix_a(s, cm, io_a(h + 1), T, stage="load", bufs=ba)
        emit_mix_b(s, cm, io_b(h), T, stage="compute", bufs=bb)
        if h + 1 < 4:
            emit_mix_b(s, cm, io_b(h + 1), T, stage="load", bufs=bb)
    s.release(m_ab)
    cm = mix_common(s, io0, n_ps=3, with_pf2=True)
    io = dict(io0)
    io.update(zq=(zfm[8], zfm[9]), kskw=zfm[10], kvin=zfm[11], vs_src=F["vsw"][:, 0:64], vw_src=F["vsw"][:, 64:128],
              misc_sb=(misc_sb, misc_sb), w1=F["nsa_phi_w1"][l], b1=F["b1l"][l], peT=F["peT"][l], w2=F["nsa_phi_w2"][l],
              b2=F["nsa_phi_b2"][l], b2c=F["b2c"][l], kgain=F["kgain"][l], oc=obr[:, 512:768])
    for k in ("cmask", "smask", "wmask", "impA", "impB", "emat", "ovl", "ones64", "onesrow"):
        io[k] = F[k]
    emit_mix_c(s, cm, io, T, cs=[0, 1, 2, 3])
    s.release(m)


def build_fused(T, L):
    nc = bass.Bass("TRN2", target_bir_lowering=False)
    F = {}
    F["x"] = dram_in(nc, "x", [T, D])
    for k, shp in PARAM_SHAPES.items():
        F[k] = dram_in(nc, k, [L if v == "L" else v for v in shp])
    for k, (shp, dt) in fused_const_shapes(T).items():
        F[k] = dram_in(nc, k, shp, dt)
    y = dram_out(nc, "y", [T, D])
    for k, shp, dt in (("xa", [T, D], F32), ("xb", [T, D], F32), ("xc", [T, D], F32), ("zfm", [NFM, 128, T], BF16),
                       ("vab", [T, 512], BF16), ("vsw", [T, 128], BF16), ("misc", [T, 16], F32), ("obr", [T, D], BF16)):
        F[k] = nc.dram_tensor("s_" + k, shp, dt).ap()
    s = S(nc)
    for l in range(L):
        x_in = F["x"] if l == 0 else F["xc"]
        emit_ffn(s, x_in, F["ffn1_norm"][l], F["ffn1_w_in"][l], F["ffn1_w_out"][l], F["ident"], F["xa"], T)
        a = dict(x=F["xa"], g=F["mix_norm"][l], w_in=F["w_in"][l], ident=F["ident"], gains=F["gains"][l], blk=F["blk"],
                 vgain=F["vgain"][l], wsT=F["wsT"][l], triu=F["triu"], bsT=F["bsT"][l], zfm=F["zfm"], vab=F["vab"],
                 vsw=F["vsw"], misc=F["misc"], od=F["obr"][:, 768:1024])
        emit_proj(s, a, T)
        emit_mix_fused(s, F, l, T)
        emit_merge(s, F["xa"], F["mix_norm"][l], F["w_in"][l], F["w_branch"][l], F["w_out"][l], F["ident"], F["obr"], F["xb"], T)
        x_out = y if l == L - 1 else F["xc"]
        emit_ffn(s, F["xb"], F["ffn2_norm"][l], F["ffn2_w_in"][l], F["ffn2_w_out"][l], F["ident"], x_out, T)
    s.finish()
    s.close()
    return nc


def fused_params(P, L):
    import math
    f32 = np.float32
    A = lambda a: np.ascontiguousarray(np.asarray(a, dtype=f32))
    d = {k: A(P[k]) for k in ("ffn1_norm", "ffn1_w_in", "ffn1_w_out", "mix_norm", "w_in", "nsa_phi_w1", "nsa_phi_w2",
                              "nsa_phi_b2", "w_branch", "w_out", "ffn2_norm", "ffn2_w_in", "ffn2_w_out")}
    tile = lambda v, n: np.tile(A(v), (1, n))
    d["gains"] = np.ascontiguousarray(np.stack([tile(P["diff_q_gain"], 4), tile(P["diff_k_gain"], 4), tile(P["fox_q_gain"], 2),
                                                tile(P["fox_k_gain"], 2), tile(P["nsa_q_gain"], 2), tile(P["nsa_k_gain"], 2)], 2))
    d["vgain"] = np.ascontiguousarray(np.broadcast_to(A(P["gmlp_v_gain"])[:, None, :], (L, 128, 256)))
    d["wsT"] = np.ascontiguousarray(A(P["gmlp_w_s"]).transpose(0, 1, 3, 2))
    d["bsT"] = np.ascontiguousarray(A(P["gmlp_b_s"]).transpose(0, 2, 1))
    d["lamp"] = np.ascontiguousarray(np.broadcast_to(A(P["diff_lambda"])[:, None], (L, 128, 4, 32)))
    li = np.array([[0.8 - 0.6 * math.exp(-0.3 * l), 1.0 - (0.8 - 0.6 * math.exp(-0.3 * l))] for l in range(L)], f32)
    d["lami"] = np.ascontiguousarray(np.broadcast_to(li[:, None, :], (L, 128, 2)))
    d["fbias"] = np.ascontiguousarray(np.broadcast_to(A(P["fox_f_bias"])[:, :, None, None], (L, 4, 128, 1)))
    d["b1l"] = np.ascontiguousarray(A(P["nsa_phi_b1"]).reshape(L, 2, 2, 128).transpose(0, 3, 1, 2).reshape(L, 128, 4))
    pe = A(P["nsa_cmp_pe"])
    d["peT"] = np.ascontiguousarray(pe.transpose(0, 1, 3, 2).reshape(L, 128, 32))
    d["b2c"] = np.ascontiguousarray(A(P["nsa_phi_b2"])[:, 0, :, None])
    d["kgain"] = np.ascontiguousarray(A(P["nsa_k_gain"])[:, :, None])
    return d


B_, T_, L_ = 2, 8192, 2
_PROG = {}


def kernel(**inputs):
    x = np.ascontiguousarray(np.asarray(inputs["x"], dtype=np.float32))
    if "fused" not in _PROG:
        _PROG["fused"] = build_fused(T_, L_)
        _PROG["consts"] = fused_consts(T_)
    nc = _PROG["fused"]
    par = fused_params(inputs, L_)
    in_maps = []
    for b in range(B_):
        d = dict(par)
        d.update(_PROG["consts"])
        d["x"] = x[b]
        in_maps.append(d)
    res = run_bass_kernel_spmd(nc, in_maps, core_ids=list(range(B_)))
    return np.stack([np.asarray(res.results[b]["y"], dtype=np.float32) for b in range(B_)], 0)
```

```python
import numpy as np
import concourse.bass as bass
import concourse.mybir as mybir
from concourse.bass_utils import run_bass_kernel_spmd

F32 = mybir.dt.float32
BF16 = mybir.dt.bfloat16
AF = mybir.ActivationFunctionType
ALU = mybir.AluOpType
AX = mybir.AxisListType

ENGS = ("pe", "act", "dve", "pool", "sp")


class Buf:
    __slots__ = ("name", "t", "w", "r", "dsem", "dcnt", "uid")
    _n = 0

    def __init__(self, name, t):
        Buf._n += 1
        self.uid = Buf._n
        self.name = name
        self.t = t
        self.w = None
        self.r = []
        self.dsem = {}
        self.dcnt = {}

    def __getitem__(self, idx):
        return self.t[idx]


class S:
    def __init__(self, nc, same_engine_sync=True):
        self.nc = nc
        self.e = {"pe": nc.tensor, "act": nc.scalar, "dve": nc.vector, "pool": nc.gpsimd, "sp": nc.sync}
        self.sem = {k: nc.alloc_semaphore("c_" + k) for k in ENGS}
        self.cnt = {k: 0 for k in ENGS}
        self.seen = {k: {} for k in ENGS}
        self.same = same_engine_sync
        self.nbuf = 0
        self.nsem = 0
        self.dma_sems = []
        self.ctx = []
        self.cbufs = []
        self.free_dsems = {"hw": [], "sw": []}

    def sb(self, name, shape, dt):
        self.nbuf += 1
        g = self.nc.sbuf_tensor("%s_%d" % (name, self.nbuf), list(shape), dt)
        t = g.__enter__()
        self.ctx.append(g)
        b = Buf(name, t)
        self.cbufs.append(b)
        return b

    def ps(self, name, shape, dt):
        self.nbuf += 1
        g = self.nc.psum_tensor("%s_%d" % (name, self.nbuf), list(shape), dt)
        t = g.__enter__()
        self.ctx.append(g)
        b = Buf(name, t)
        self.cbufs.append(b)
        return b

    def sub(self, name, ap):
        return Buf(name, ap)

    def mark(self):
        return len(self.ctx)

    def release(self, m):
        self.barrier()
        while len(self.ctx) > m:
            self.ctx.pop().__exit__(None, None, None)
            b = self.cbufs.pop()
            for kind, sem in b.dsem.items():
                self.free_dsems[kind].append((sem, b.dcnt[kind]))
                self.dma_sems.remove((b, kind))
            b.dsem = {}

    def close(self):
        for g in reversed(self.ctx):
            g.__exit__(None, None, None)
        self.ctx = []
        self.cbufs = []

    def _need(self, E, deps):
        need = {}
        for d in deps:
            if d is None:
                continue
            if d[0] == "dma":
                b, kind = d[1], d[2]
                if kind not in b.dsem:
                    continue
                key = ("dma", b.uid, kind)
                need[key] = ((b, kind), b.dcnt[kind])
            else:
                F, c = d
                if F == E and (not self.same or E == "pe" or c > self.cnt[E]):
                    continue
                if c > need.get(F, (None, 0))[1]:
                    need[F] = (None, c)
        for key, (bk, c) in need.items():
            if self.seen[E].get(key, 0) >= c:
                continue
            self.seen[E][key] = c
            if bk is not None:
                self.e[E].wait_ge(bk[0].dsem[bk[1]], c)
            else:
                self.e[E].wait_ge(self.sem[key], c)

    def op(self, E, fn, reads=(), writes=(), inc=True):
        deps = []
        for b in reads:
            deps.append(b.w)
        for b in writes:
            deps.append(b.w)
            deps.extend(b.r)
        self._need(E, deps)
        ins = fn()
        c = self.cnt[E] + 1
        if inc:
            ins.then_inc(self.sem[E], 1)
            self.cnt[E] = c
        for b in writes:
            b.w = (E, c)
            b.r = []
        for b in reads:
            if b not in writes:
                b.r = [x for x in b.r if x[0] != E] + [(E, c)]
        return ins

    def _dsem(self, owner, kind):
        if kind not in owner.dsem:
            if self.free_dsems[kind]:
                sem, cnt = self.free_dsems[kind].pop()
            else:
                self.nsem += 1
                sem, cnt = self.nc.alloc_semaphore("d%s_%d" % (kind, self.nsem)), 0
            owner.dsem[kind] = sem
            owner.dcnt[kind] = cnt
            self.dma_sems.append((owner, kind))
        return owner.dsem[kind]

    def dma(self, Q, out, in_, reads=(), writes=(), **kw):
        deps = []
        for b in reads:
            deps.append(b.w)
        for b in writes:
            deps.append(b.w)
            deps.extend(b.r)
        self._need(Q, deps)
        owner = (list(writes) + list(reads))[0]
        kind = "sw" if Q == "pool" else "hw"
        sem = self._dsem(owner, kind)
        ins = self.e[Q].dma_start(out=out, in_=in_, **kw)
        ins.then_inc(sem, 16)
        owner.dcnt[kind] += 16
        rec = ("dma", owner, kind)
        for b in writes:
            b.w = rec
            b.r = []
        for b in reads:
            if b not in writes:
                b.r = [x for x in b.r if not (x[0] == "dma" and x[1] is owner and x[2] == kind)] + [rec]
        return ins

    def barrier(self):
        for E in ENGS:
            for Fk in ENGS:
                if Fk == E:
                    continue
                c = self.cnt[Fk]
                if c and self.seen[E].get(Fk, 0) < c:
                    self.seen[E][Fk] = c
                    self.e[E].wait_ge(self.sem[Fk], c)
            for (b, kind) in self.dma_sems:
                key = ("dma", b.uid, kind)
                c = b.dcnt[kind]
                if c and self.seen[E].get(key, 0) < c:
                    self.seen[E][key] = c
                    self.e[E].wait_ge(b.dsem[kind], c)

    def finish(self):
        self.barrier()


D = 1024
DFF = 2816
NFC = DFF // 128
NKC = D // 128
EPS = 1e-6


def dram_in(nc, name, shape, dt=F32):
    return nc.dram_tensor(name, list(shape), dt, kind="ExternalInput").ap()


def dram_out(nc, name, shape, dt=F32):
    return nc.dram_tensor(name, list(shape), dt, kind="ExternalOutput").ap()


class RR:
    def __init__(self, items):
        self.items = items
        self.i = 0

    def next(self):
        b = self.items[self.i % len(self.items)]
        self.i += 1
        return b


def emit_norm(s, epsc, xt, hb_rr, scr, stat_rr, ntile):
    nc = s.nc
    st = stat_rr.next()
    hbs = [hb_rr.next() for _ in range(ntile)]
    for j in range(ntile):
        s.op("act", lambda: nc.scalar.activation(out=hbs[j][:, :], in_=xt[j][:, :], func=AF.Square, scale=1.0 / 32.0,
                                                 accum_out=st[:, j:j + 1]),
             reads=[xt[j]], writes=[hbs[j], st])
    s.op("act", lambda: nc.scalar.activation(out=st[:, 4:4 + ntile], in_=st[:, 0:ntile], func=AF.Ln, bias=epsc[:, 0:1], scale=1.0),
         reads=[st, epsc], writes=[st])
    s.op("act", lambda: nc.scalar.activation(out=st[:, 8:8 + ntile], in_=st[:, 4:4 + ntile], func=AF.Exp, scale=-0.5),
         reads=[st], writes=[st])
    for j in range(ntile):
        s.op("act", lambda: nc.scalar.activation(out=hbs[j][:, :], in_=xt[j][:, :], func=AF.Copy, scale=st[:, 8 + j:9 + j]),
             reads=[xt[j], st], writes=[hbs[j]])
    return hbs


def emit_transpose_T(s, hbs, gcol, hT, ident_b, pT_rr, ntile, evac_engs=("dve", "act")):
    nc = s.nc
    k = 0
    for c in range(NKC):
        pT = pT_rr.next()
        for j in range(ntile):
            s.op("pe", lambda: nc.tensor.transpose(out=pT[:, j * 128:(j + 1) * 128], in_=hbs[j][:, c * 128:(c + 1) * 128],
                                                   identity=ident_b[:, :]),
                 reads=[hbs[j], ident_b], writes=[pT], inc=(j == ntile - 1))
        eng = evac_engs[k % len(evac_engs)]
        k += 1
        if eng == "dve":
            s.op("dve", lambda: nc.vector.tensor_scalar(out=hT[:, c, 0:ntile * 128], in0=pT[:, 0:ntile * 128],
                                                        scalar1=gcol[:, c:c + 1], scalar2=None, op0=ALU.mult),
                 reads=[pT, gcol], writes=[hT])
        else:
            s.op("act", lambda: nc.scalar.activation(out=hT[:, c, 0:ntile * 128], in_=pT[:, 0:ntile * 128],
                                                     func=AF.Copy, scale=gcol[:, c:c + 1]),
                 reads=[pT, gcol], writes=[hT])


def emit_rmsnorm_T(s, epsc, xt, gcol, hT, ident_b, pT_rr, hb_rr, scr, stat_rr, ntile, evac_engs=("dve", "act")):
    hbs = emit_norm(s, epsc, xt, hb_rr, scr, stat_rr, ntile)
    emit_transpose_T(s, hbs, gcol, hT, ident_b, pT_rr, ntile, evac_engs)


def build_ffn(NT, TB=512):
    nc = bass.Bass("TRN2", target_bir_lowering=False)
    x = dram_in(nc, "x", [NT, D])
    g = dram_in(nc, "g", [D])
    w_in = dram_in(nc, "w_in", [D, 2 * DFF])
    w_out = dram_in(nc, "w_out", [DFF, D])
    ident = dram_in(nc, "ident", [128, 128], BF16)
    y = dram_out(nc, "y", [NT, D])
    s = S(nc)
    emit_ffn(s, x, g, w_in, w_out, ident, y, NT, TB)
    s.finish()
    s.close()
    return nc


def emit_ffn(s, x, g, w_in, w_out, ident, y, NT, TB=512):
    nc = s.nc
    ntile = TB // 128
    m_ = s.mark()
    ident_b = s.sb("ident_b", [128, 128], BF16)
    s.dma("sp", ident_b[:, :], ident[:, :], writes=[ident_b])
    gcol = s.sb("gcol", [128, NKC], F32)
    epsc = s.sb("epsc", [128, 1], F32)
    s.op("dve", lambda: nc.vector.memset(epsc[:, :], EPS), writes=[epsc])
    s.dma("sp", gcol[:, :], g.rearrange("(c p) -> p c", p=128), writes=[gcol], allow_slow_non_contiguous=True)
    win_b = [s.sb("win_b%d" % c, [128, 2 * DFF], BF16) for c in range(NKC)]
    wout_b = [s.sb("wout_b%d" % f, [128, D], BF16) for f in range(NFC)]
    for c in range(NKC):
        for hf in range(2):
            s.dma("pool", win_b[c][:, hf * DFF:(hf + 1) * DFF], w_in[c * 128:(c + 1) * 128, hf * DFF:(hf + 1) * DFF],
                  writes=[win_b[c]])
    for f in range(NFC):
        s.dma("pool", wout_b[f][:, :], w_out[f * 128:(f + 1) * 128, :], writes=[wout_b[f]])
    xn = [s.sb("xn%d" % j, [128, D], F32) for j in range(ntile)]
    xr_rr = RR([s.sb("xr%d" % j, [128, D], F32) for j in range(1)])
    hb_rr = RR([s.sb("hb%d" % j, [128, D], BF16) for j in range(2 * ntile)])
    stat_rr = RR([s.sb("st%d" % j, [128, 16], F32) for j in range(4)])
    hT = s.sb("hT", [128, NKC, TB], BF16)
    pT_rr = RR([s.ps("pT%d" % j, [128, TB], BF16) for j in range(2)])
    pa_rr = RR([s.ps("pa%d" % j, [128, TB], F32) for j in range(2)])
    pb_rr = RR([s.ps("pb%d" % j, [128, TB], F32) for j in range(2)])
    po_rr = RR([s.ps("po%d" % j, [128, 512], F32) for j in range(2)])
    sa_rr = RR([s.sb("sa%d" % j, [128, TB], F32) for j in range(2)])
    act = [s.sb("actT%d" % f, [128, TB], BF16) for f in range(NFC)]
    nblk = NT // TB

    def prep_a_rot(tb):
        for j in range(ntile):
            r0 = tb * TB + j * 128
            s.dma("sp", xn[j][:, :], x[r0:r0 + 128, :], writes=[xn[j]])
        return emit_norm(s, epsc, xn, hb_rr, None, stat_rr, ntile)

    hbs_next = prep_a_rot(0)
    emit_transpose_T(s, hbs_next, gcol, hT, ident_b, pT_rr, ntile)
    for tb in range(nblk):
        if tb + 1 < nblk:
            hbs_next = prep_a_rot(tb + 1)
        for f in range(NFC):
            pa = pa_rr.next()
            pb = pb_rr.next()
            for c in range(NKC):
                s.op("pe", lambda: nc.tensor.matmul(pa[:, :], lhsT=win_b[c][:, f * 128:(f + 1) * 128], rhs=hT[:, c, :],
                                                    start=(c == 0), stop=(c == NKC - 1)),
                     reads=[win_b[c], hT], writes=[pa], inc=(c == NKC - 1))
            for c in range(NKC):
                s.op("pe", lambda: nc.tensor.matmul(pb[:, :], lhsT=win_b[c][:, DFF + f * 128:DFF + (f + 1) * 128],
                                                    rhs=hT[:, c, :], start=(c == 0), stop=(c == NKC - 1)),
                     reads=[win_b[c], hT], writes=[pb], inc=(c == NKC - 1))
            sa = sa_rr.next()
            s.op("act", lambda: nc.scalar.activation(out=sa[:, :], in_=pa[:, :], func=AF.Silu), reads=[pa], writes=[sa])
            s.op("dve", lambda: nc.vector.tensor_tensor(out=act[f][:, :], in0=sa[:, :], in1=pb[:, :], op=ALU.mult),
                 reads=[sa, pb], writes=[act[f]])
        if tb + 1 < nblk:
            emit_transpose_T(s, hbs_next, gcol, hT, ident_b, pT_rr, ntile)
        for j in range(ntile):
            r0 = tb * TB + j * 128
            xr = xr_rr.next()
            s.dma("sp", xr[:, :], x[r0:r0 + 128, :], writes=[xr])
            for hf in range(2):
                po = po_rr.next()
                for f in range(NFC):
                    s.op("pe", lambda: nc.tensor.matmul(po[:, :], lhsT=act[f][:, j * 128:(j + 1) * 128],
                                                        rhs=wout_b[f][:, hf * 512:(hf + 1) * 512],
                                                        start=(f == 0), stop=(f == NFC - 1)),
                         reads=[act[f], wout_b[f]], writes=[po], inc=(f == NFC - 1))
                s.op("dve", lambda: nc.vector.scalar_tensor_tensor(out=xr[:, hf * 512:(hf + 1) * 512], in0=po[:, :],
                                                                   scalar=0.5, in1=xr[:, hf * 512:(hf + 1) * 512],
                                                                   op0=ALU.mult, op1=ALU.add),
                     reads=[po, xr], writes=[xr])
            s.dma("sp", y[r0:r0 + 128, :], xr[:, :], reads=[xr])
    s.release(m_)


FM_SRC = [[(0, 128)], [(128, 128)], [(256, 128)], [(384, 128)],
          [(768, 128)], [(896, 128)], [(1024, 128)], [(1152, 128)],
          [(1540, 128)], [(1668, 128)], [(1924, 64), (2052, 64)], [(1796, 128)]]
FM_GCOL = [0, 0, 1, 1, 2, 2, 3, 3, 4, 4, 5, None]
FM_BLK = [0, 0, 0, 0, 1, 1, 1, 1, 1, 1, 1, None]
TM_SRC = [[(512, 256), (1280, 256)],
          [(1536, 4), (2180, 12), (1988, 64), (2116, 64)],
          [(2192, 512)]]
NFM = 12
GELU_C = 1.5957691216057308


def build_proj(NT, TB=512):
    nc = bass.Bass("TRN2", target_bir_lowering=False)
    a = dict(
        x=dram_in(nc, "x", [NT, D]), g=dram_in(nc, "g", [D]), w_in=dram_in(nc, "w_in", [D, 6800]),
        ident=dram_in(nc, "ident", [128, 128], BF16), gains=dram_in(nc, "gains", [128, 6]),
        blk=dram_in(nc, "blk", [2, 128, 128], BF16), vgain=dram_in(nc, "vgain", [128, 256]),
        wsT=dram_in(nc, "wsT", [4, 128, 128]), triu=dram_in(nc, "triu", [128, 128]), bsT=dram_in(nc, "bsT", [128, 4]),
        zfm=dram_out(nc, "zfm", [NFM, 128, NT], BF16), vab=dram_out(nc, "vab", [NT, 512], BF16),
        vsw=dram_out(nc, "vsw", [NT, 128], BF16), misc=dram_out(nc, "misc", [NT, 16]), od=dram_out(nc, "od", [NT, 256], BF16))
    s = S(nc)
    emit_proj(s, a, NT, TB)
    s.finish()
    s.close()
    return nc


def emit_proj(s, a, NT, TB=512):
    nc = s.nc
    x, g, w_in, ident, gains, blk, vgain, wsT, triu, bsT = (a[k] for k in
                                                            ("x", "g", "w_in", "ident", "gains", "blk", "vgain", "wsT", "triu", "bsT"))
    zfm, vab, vsw, misc, od = (a[k] for k in ("zfm", "vab", "vsw", "misc", "od"))
    m_ = s.mark()
    ntile = TB // 128
    ident_b = s.sb("ident_b", [128, 128], BF16)
    s.dma("sp", ident_b[:, :], ident[:, :], writes=[ident_b])
    gcol = s.sb("gcol", [128, NKC], F32)
    s.dma("sp", gcol[:, :], g.rearrange("(c p) -> p c", p=128), writes=[gcol], allow_slow_non_contiguous=True)
    epsc = s.sb("epsc", [128, 1], F32)
    s.op("dve", lambda: nc.vector.memset(epsc[:, :], EPS), writes=[epsc])
    gn = s.sb("gn", [128, 6], F32)
    s.dma("sp", gn[:, :], gains[:, :], writes=[gn])
    for col, sc in ((0, 32.0 ** -0.5), (2, 0.125), (4, 0.125)):
        s.op("dve", lambda: nc.vector.tensor_scalar(out=gn[:, col:col + 1], in0=gn[:, col:col + 1], scalar1=sc,
                                                    scalar2=None, op0=ALU.mult), reads=[gn], writes=[gn])
    blk_b = [s.sb("blk%d" % i, [128, 128], BF16) for i in range(2)]
    for i in range(2):
        s.dma("sp", blk_b[i][:, :], blk[i], writes=[blk_b[i]])
    vg = s.sb("vg", [128, 256], F32)
    s.dma("sp", vg[:, :], vgain[:, :], writes=[vg])
    bcol = s.sb("bcol", [128, 4], F32)
    s.dma("sp", bcol[:, :], bsT[:, :], writes=[bcol])
    tri = s.sb("tri", [128, 128], F32)
    s.dma("sp", tri[:, :], triu[:, :], writes=[tri])
    wm = []
    wtmp = s.sb("wtmp", [128, 128], F32)
    for gi in range(4):
        w = s.sb("wm%d" % gi, [128, 128], BF16)
        s.dma("sp", wtmp[:, :], wsT[gi], writes=[wtmp])
        s.op("dve", lambda: nc.vector.tensor_tensor(out=w[:, :], in0=wtmp[:, :], in1=tri[:, :], op=ALU.mult),
             reads=[wtmp, tri], writes=[w])
        wm.append(w)
    wfm = [s.sb("wfm%d" % c, [128, NFM * 128], BF16) for c in range(NKC)]
    wtm = [s.sb("wtm%d" % c, [128, 1168], BF16) for c in range(NKC)]
    FM_RUNS = [(0, 0, 512), (512, 768, 512), (1024, 1540, 256), (1280, 1924, 64), (1344, 2052, 64), (1408, 1796, 128)]
    TM_RUNS = [(0, 512, 256), (256, 1280, 256), (512, 1536, 4), (516, 2180, 12), (528, 1988, 64), (592, 2116, 64), (656, 2192, 512)]
    for c in range(NKC):
        for (o, c0, n) in FM_RUNS:
            s.dma("pool", wfm[c][:, o:o + n], w_in[c * 128:(c + 1) * 128, c0:c0 + n], writes=[wfm[c]])
        for (o, c0, n) in TM_RUNS:
            s.dma("pool", wtm[c][:, o:o + n], w_in[c * 128:(c + 1) * 128, c0:c0 + n], writes=[wtm[c]])
    xts = [[s.sb("xt%d_%d" % (k, j), [128, D], F32) for j in range(ntile)] for k in range(2)]
    hb_rr = RR([s.sb("hb%d" % j, [128, D], BF16) for j in range(2 * ntile)])
    scr = s.sb("scr", [128, D], BF16)
    stat_rr = RR([s.sb("st%d" % j, [128, 16], F32) for j in range(4)])
    hTs = [s.sb("hT%d" % k, [128, NKC, TB], BF16) for k in range(2)]
    pT_rr = RR([s.ps("pT%d" % j, [128, TB], BF16) for j in range(2)])
    pz_rr = RR([s.ps("pz%d" % j, [128, 512], F32) for j in range(2)])
    ptm_rr = RR([s.ps("ptm%d" % j, [128, 512], F32) for j in range(2)])
    pq_rr = RR([s.ps("pq%d" % j, [128, 512], F32) for j in range(2)])
    sq_rr = RR([s.sb("sq%d" % j, [128, TB], BF16) for j in range(4)])
    rs_rr = RR([s.sb("rs%d" % j, [128, TB], F32) for j in range(2)])
    zo_rr = RR([s.sb("zo%d" % j, [128, TB], BF16) for j in range(4)])
    vab_rr = RR([s.sb("vabt%d" % j, [128, 512], BF16) for j in range(2)])
    vsw_rr = RR([s.sb("vswt%d" % j, [128, 128], BF16) for j in range(2)])
    msc_rr = RR([s.sb("msct%d" % j, [128, 16], F32) for j in range(2)])
    f_rr = RR([s.sb("gf%d" % j, [128, 512], F32) for j in range(4)])
    ge_rr = RR([s.sb("ge%d" % j, [128, 512], F32) for j in range(3)])
    zs_rr = RR([s.sb("zs%d" % j, [128, 512], F32) for j in range(2)])
    zf_rr = RR([s.sb("zf%d" % j, [128, TB], F32) for j in range(3)])
    vn_rr = RR([s.sb("vn%d" % j, [128, 256], BF16) for j in range(3)])
    od_rr = RR([s.sb("odt%d" % j, [128, 256], BF16) for j in range(2)])
    nblk = NT // TB

    def prep(tb):
        xt = xts[tb % 2]
        for j in range(ntile):
            s.dma("sp", xt[j][:, :], x[tb * TB + j * 128:tb * TB + (j + 1) * 128, :], writes=[xt[j]])
        emit_rmsnorm_T(s, epsc, xt, gcol, hTs[tb % 2], ident_b, pT_rr, hb_rr, scr, stat_rr, ntile)

    prep(0)
    pipe = Pipe(2)
    for tb in range(nblk):
        t0 = tb * TB
        hT = hTs[tb % 2]
        for i in range(NFM):
            pz = pz_rr.next()
            for c in range(NKC):
                s.op("pe", lambda: nc.tensor.matmul(pz[:, 0:TB], lhsT=wfm[c][:, i * 128:(i + 1) * 128], rhs=hT[:, c, :],
                                                    start=(c == 0), stop=(c == NKC - 1)),
                     reads=[wfm[c], hT], writes=[pz], inc=(c == NKC - 1))
            zo = zo_rr.next()
            if FM_GCOL[i] is None:
                s.op("dve", lambda: nc.vector.tensor_copy(out=zo[:, :], in_=pz[:, 0:TB]), reads=[pz], writes=[zo])
                s.dma("sp", zfm[i, :, t0:t0 + TB], zo[:, :], reads=[zo])
            else:
                zf = zf_rr.next()
                s.op("dve", lambda: nc.vector.tensor_copy(out=zf[:, :], in_=pz[:, 0:TB]), reads=[pz], writes=[zf])
                sq = sq_rr.next()
                s.op("act", lambda: nc.scalar.activation(out=sq[:, :], in_=zf[:, :], func=AF.Square),
                     reads=[zf], writes=[sq])

                def back(i=i, zf=zf, sq=sq, zo=zo, t0=t0):
                    gs = 32.0 if FM_BLK[i] == 0 else 64.0
                    pq = pq_rr.next()
                    s.op("pe", lambda: nc.tensor.matmul(pq[:, 0:TB], lhsT=blk_b[FM_BLK[i]][:, :], rhs=sq[:, :],
                                                        start=True, stop=True), reads=[blk_b[FM_BLK[i]], sq], writes=[pq])
                    rs = rs_rr.next()
                    s.op("act", lambda: nc.scalar.activation(out=rs[:, :], in_=pq[:, 0:TB], func=AF.Ln, bias=epsc[:, 0:1],
                                                             scale=1.0 / gs), reads=[pq, epsc], writes=[rs])
                    s.op("act", lambda: nc.scalar.activation(out=rs[:, :], in_=rs[:, :], func=AF.Exp, scale=-0.5),
                         reads=[rs], writes=[rs])
                    gc = FM_GCOL[i]
                    s.op("dve", lambda: nc.vector.scalar_tensor_tensor(out=zo[:, :], in0=zf[:, :], scalar=gn[:, gc:gc + 1],
                                                                       in1=rs[:, :], op0=ALU.mult, op1=ALU.mult),
                         reads=[zf, gn, rs], writes=[zo])
                    s.dma("sp", zfm[i, :, t0:t0 + TB], zo[:, :], reads=[zo])
                pipe.push(back)
        if tb + 1 < nblk:
            prep(tb + 1)
        for j in range(ntile):
            r0 = t0 + j * 128
            pz = ptm_rr.next()
            for c in range(NKC):
                s.op("pe", lambda: nc.tensor.matmul(pz[:, :], lhsT=hT[:, c, j * 128:(j + 1) * 128], rhs=wtm[c][:, 0:512],
                                                    start=(c == 0), stop=(c == NKC - 1)),
                     reads=[wtm[c], hT], writes=[pz], inc=(c == NKC - 1))
            vt = vab_rr.next()
            s.op("act", lambda: nc.scalar.copy(out=vt[:, :], in_=pz[:, :]), reads=[pz], writes=[vt])
            s.dma("sp", vab[r0:r0 + 128, :], vt[:, :], reads=[vt])
            pz = ptm_rr.next()
            for c in range(NKC):
                s.op("pe", lambda: nc.tensor.matmul(pz[:, 0:144], lhsT=hT[:, c, j * 128:(j + 1) * 128], rhs=wtm[c][:, 512:656],
                                                    start=(c == 0), stop=(c == NKC - 1)),
                     reads=[wtm[c], hT], writes=[pz], inc=(c == NKC - 1))
            mt = msc_rr.next()
            vs_ = vsw_rr.next()
            s.op("dve", lambda: nc.vector.tensor_copy(out=mt[:, :], in_=pz[:, 0:16]), reads=[pz], writes=[mt])
            s.op("dve", lambda: nc.vector.tensor_copy(out=vs_[:, :], in_=pz[:, 16:144]), reads=[pz], writes=[vs_])
            s.dma("sp", misc[r0:r0 + 128, :], mt[:, :], reads=[mt])
            s.dma("sp", vsw[r0:r0 + 128, :], vs_[:, :], reads=[vs_])
            pz = ptm_rr.next()
            for c in range(NKC):
                s.op("pe", lambda: nc.tensor.matmul(pz[:, :], lhsT=hT[:, c, j * 128:(j + 1) * 128], rhs=wtm[c][:, 656:1168],
                                                    start=(c == 0), stop=(c == NKC - 1)),
                     reads=[wtm[c], hT], writes=[pz], inc=(c == NKC - 1))
            zs = zs_rr.next()
            s.op("act", lambda: nc.scalar.copy(out=zs[:, :], in_=pz[:, :]), reads=[pz], writes=[zs])
            z2 = f_rr.next()
            s.op("act", lambda: nc.scalar.activation(out=z2[:, :], in_=zs[:, :], func=AF.Square), reads=[zs], writes=[z2])
            s.op("dve", lambda: nc.vector.tensor_scalar(out=z2[:, :], in0=z2[:, :], scalar1=0.044715, scalar2=1.0,
                                                        op0=ALU.mult, op1=ALU.add), reads=[z2], writes=[z2])
            s.op("dve", lambda: nc.vector.tensor_tensor(out=z2[:, :], in0=z2[:, :], in1=zs[:, :], op=ALU.mult),
                 reads=[z2, zs], writes=[z2])
            s.op("act", lambda: nc.scalar.activation(out=z2[:, :], in_=z2[:, :], func=AF.Exp, scale=-GELU_C),
                 reads=[z2], writes=[z2])
            s.op("act", lambda: nc.scalar.activation(out=z2[:, :], in_=z2[:, :], func=AF.Ln, bias=1.0, scale=1.0),
                 reads=[z2], writes=[z2])
            s.op("act", lambda: nc.scalar.activation(out=z2[:, :], in_=z2[:, :], func=AF.Exp, scale=-1.0),
                 reads=[z2], writes=[z2])
            ge = ge_rr.next()
            s.op("dve", lambda: nc.vector.tensor_tensor(out=ge[:, :], in0=z2[:, :], in1=zs[:, :], op=ALU.mult),
                 reads=[z2, zs], writes=[ge])
            sqv = f_rr.next()
            st = stat_rr.next()
            s.op("act", lambda: nc.scalar.activation(out=sqv[:, 0:256], in_=ge[:, 256:512], func=AF.Square),
                 reads=[ge], writes=[sqv])
            s.op("dve", lambda: nc.vector.tensor_reduce(out=st[:, 0:4], in_=sqv[:, 0:256].rearrange("p (g d) -> p g d", g=4),
                                                        axis=AX.X, op=ALU.add), reads=[sqv], writes=[st])
            s.op("act", lambda: nc.scalar.activation(out=st[:, 0:4], in_=st[:, 0:4], func=AF.Ln, bias=epsc[:, 0:1],
                                                     scale=1.0 / 64.0), reads=[st, epsc], writes=[st])
            s.op("act", lambda: nc.scalar.activation(out=st[:, 0:4], in_=st[:, 0:4], func=AF.Exp, scale=-0.5),
                 reads=[st], writes=[st])
            vn = vn_rr.next()
            for gi in range(4):
                s.op("dve", lambda: nc.vector.scalar_tensor_tensor(
                    out=vn[:, gi * 64:(gi + 1) * 64], in0=ge[:, 256 + gi * 64:256 + (gi + 1) * 64], scalar=st[:, gi:gi + 1],
                    in1=vg[:, gi * 64:(gi + 1) * 64], op0=ALU.mult, op1=ALU.mult), reads=[ge, st, vg], writes=[vn])

            def back2(vn=vn, ge=ge, r0=r0):
                pq = pq_rr.next()
                for gi in range(4):
                    s.op("pe", lambda: nc.tensor.matmul(pq[:, gi * 64:(gi + 1) * 64], lhsT=wm[gi][:, :],
                                                        rhs=vn[:, gi * 64:(gi + 1) * 64], start=True, stop=True),
                         reads=[wm[gi], vn], writes=[pq], inc=(gi == 3))
                ot = od_rr.next()
                for gi in range(4):
                    s.op("dve", lambda: nc.vector.scalar_tensor_tensor(
                        out=ot[:, gi * 64:(gi + 1) * 64], in0=pq[:, gi * 64:(gi + 1) * 64], scalar=bcol[:, gi:gi + 1],
                        in1=ge[:, gi * 64:(gi + 1) * 64], op0=ALU.add, op1=ALU.mult), reads=[pq, bcol, ge], writes=[ot])
                s.dma("sp", od[r0:r0 + 128, :], ot[:, :], reads=[ot])
            pipe.push(back2)
    pipe.flush()
    s.release(m_)


def _bf(a):
    import ml_dtypes
    return np.ascontiguousarray(a).astype(ml_dtypes.bfloat16)


def proj_consts():
    blk = np.zeros((2, 128, 128), np.float32)
    for i in range(128):
        for j in range(128):
            if i // 32 == j // 32:
                blk[0, i, j] = 1
            if i // 64 == j // 64:
                blk[1, i, j] = 1
    triu = np.triu(np.ones((128, 128), np.float32))
    return dict(ident=_bf(np.eye(128, dtype=np.float32)), blk=_bf(blk), triu=triu)


def proj_params(g, w_in, dq, dk, fq, fk, nq, nk, vgain, w_s, b_s):
    gains = np.stack([np.tile(dq, 4), np.tile(dk, 4), np.tile(fq, 2), np.tile(fk, 2), np.tile(nq, 2), np.tile(nk, 2)], 1)
    return dict(g=np.ascontiguousarray(g), w_in=np.ascontiguousarray(w_in), gains=np.ascontiguousarray(gains, dtype=np.float32),
                vgain=np.ascontiguousarray(np.broadcast_to(vgain[None, :], (128, 256))),
                wsT=np.ascontiguousarray(w_s.transpose(0, 2, 1)), bsT=np.ascontiguousarray(b_s.T))


NEG = -30000.0


def load_vt(s, vt, io, key, T, init=True, load=True):
    nc = s.nc
    NT = T // 128
    if init:
        s.op("pool", lambda: nc.gpsimd.memset(vt[:, :, 64:128], 0.0), writes=[vt])
        s.op("pool", lambda: nc.gpsimd.memset(vt[:, :, 64:65], 1.0), writes=[vt])
    if not load:
        return
    if key + "_src" in io:
        src = io[key + "_src"]
        step = 8
        for j0 in range(0, NT, step):
            j1 = min(NT, j0 + step)
            s.dma("sp", vt[:, j0:j1, 0:64], src[j0 * 128:j1 * 128, :].rearrange("(j p) d -> p j d", p=128), writes=[vt])
    else:
        s.dma("sp", vt[:, :, 0:64], io[key][:, :, 0:64], writes=[vt])


class Pipe:
    def __init__(self, lag):
        self.q = []
        self.lag = lag

    def push(self, fn):
        self.q.append(fn)
        while len(self.q) > self.lag:
            self.q.pop(0)()

    def flush(self):
        while self.q:
            self.q.pop(0)()


def emit_attn_phase(s, cm, T, nsub, qT, kT, vt, kparts, out_dram, finalize, bias_fn=None, name="a", lag=2):
    nc = s.nc
    NQB = T // 512
    pipe = Pipe(lag)
    fin_pending = None
    for qb in range(NQB):
        q0 = qb * 512
        pos = [cm["po_rr"].next() for _ in range(nsub)]
        nt = 4 * qb + 4
        njob = 0
        for t in range(nt):
            di = t - 4 * qb
            c0 = 128 * di if di > 0 else 0
            for i in range(nsub):
                kz = kT[i]
                ps = cm["ps_rr"].next()
                s.op("pe", lambda: nc.tensor.matmul(ps[:, c0:512], lhsT=kz[:, t * 128:(t + 1) * 128],
                                                    rhs=qT[:, q0 + c0:q0 + 512], start=True, stop=(di < 0)),
                     reads=[kz, qT], writes=[ps], inc=(di < 0))
                if di >= 0:
                    s.op("pe", lambda: nc.tensor.matmul(ps[:, c0:c0 + 128], lhsT=cm["ident_b"][:, :], rhs=cm["tri_b"][:, :],
                                                        start=False, stop=True),
                         reads=[cm["ident_b"], cm["tri_b"]], writes=[ps])
                pt = cm["pt_rr"].next()
                if bias_fn is None:
                    s.op("act", lambda: nc.scalar.activation(out=pt[:, c0:512], in_=ps[:, c0:512], func=AF.Exp),
                         reads=[ps], writes=[pt])
                else:
                    bb, bap = bias_fn(qb, t)
                    s.op("act", lambda: nc.scalar.activation(out=pt[:, c0:512], in_=ps[:, c0:512], func=AF.Exp, bias=bap),
                         reads=[ps, bb], writes=[pt])

                def pv(po=pos[i], t=t, c0=c0, pt=pt, nt=nt):
                    s.op("pe", lambda: nc.tensor.matmul(po[:, c0:512], lhsT=vt[:, t, :], rhs=pt[:, c0:512],
                                                        start=(t == 0), stop=(t == nt - 1)),
                         reads=[vt, pt], writes=[po])
                pipe.push(pv)
                njob += 1
                if fin_pending is not None and njob == lag:
                    fin_pending()
                    fin_pending = None
        if fin_pending is not None:
            pipe.flush()
            fin_pending()
        fin_pending = (lambda qb=qb, pos=pos: finalize(qb, pos))
    pipe.flush()
    if fin_pending is not None:
        fin_pending()


def emit_o_to_tokmajor(s, cm, po, pf, col0):
    nc = s.nc
    oc = cm["oc_rr"].next()
    s.op("dve", lambda: nc.vector.tensor_copy(out=oc[0:65, :], in_=po[0:65, :]), reads=[po], writes=[oc])
    for j in range(4):
        s.op("pe", lambda: nc.tensor.transpose(out=pf[:, j, col0:col0 + 65], in_=oc[0:65, j * 128:(j + 1) * 128],
                                               identity=cm["ident_f"][0:65, 0:65]),
             reads=[oc, cm["ident_f"]], writes=[pf], inc=(j == 3))


def build_mix_ab(T):
    nc = bass.Bass("TRN2", target_bir_lowering=False)
    io = mix_decl(nc, T, with_c=False)
    s = S(nc)
    cm = mix_common(s, io)
    emit_mix_a(s, cm, io, T)
    emit_mix_b(s, cm, io, T)
    s.finish()
    s.close()
    return nc


def mix_decl(nc, T, with_c=True):
    NT = T // 128
    io = dict(
        identb=dram_in(nc, "identb", [128, 128], BF16), identf=dram_in(nc, "identf", [128, 128]),
        trib=dram_in(nc, "trib", [128, 128], BF16),
        qa=dram_in(nc, "qa", [64, T], BF16), ka=dram_in(nc, "ka", [64, T], BF16), va=dram_in(nc, "va", [128, NT, 65], BF16),
        lamp=dram_in(nc, "lamp", [128, 4, 32]), lami=dram_in(nc, "lami", [128, 2]),
        qb=dram_in(nc, "qb", [64, T], BF16), kb=dram_in(nc, "kb", [64, T], BF16), vb=dram_in(nc, "vb", [128, NT, 65], BF16),
        flog=dram_in(nc, "flog", [128, NT]), fbias=dram_in(nc, "fbias", [128, 1]),
        triuf=dram_in(nc, "triuf", [128, 128]), onesf=dram_in(nc, "onesf", [128, 128]),
        oa=dram_out(nc, "oa", [T, 64], BF16), ob=dram_out(nc, "ob", [T, 64], BF16),
    )
    return io


def mix_common(s, io, n_ps=3, with_pf2=True):
    nc = s.nc
    cm = {}
    for nm, key, dt in (("ident_b", "identb", BF16), ("ident_f", "identf", F32), ("tri_b", "trib", BF16)):
        b = s.sb(nm, [128, 128], dt)
        s.dma("sp", b[:, :], io[key][:, :], writes=[b])
        cm[nm] = b
    cm["epsc"] = s.sb("epsc", [128, 1], F32)
    s.op("dve", lambda: nc.vector.memset(cm["epsc"][:, :], EPS), writes=[cm["epsc"]])
    cm["ps_rr"] = RR([s.ps("ps%d" % j, [128, 512], F32) for j in range(n_ps)])
    cm["po_rr"] = RR([s.ps("po%d" % j, [128, 512], F32) for j in range(3)])
    cm["pf"] = s.ps("pf", [128, 4, 128], F32)
    if with_pf2:
        cm["pf2"] = s.ps("pf2", [128, 4, 128], F32)
    cm["lag"] = n_ps - 1
    cm["o1s_rr"] = RR([s.sb("o1s%d" % j, [128, 4, 65], F32) for j in range(2)])
    cm["pt_rr"] = RR([s.sb("pt%d" % j, [128, 512], BF16) for j in range(n_ps + 2)])
    cm["oc_rr"] = RR([s.sb("oc%d" % j, [128, 512], F32) for j in range(2)])
    cm["st_rr"] = RR([s.sb("mst%d" % j, [128, 8], F32) for j in range(8)])
    cm["ot_rr"] = RR([s.sb("ot%d" % j, [128, 4, 64], BF16) for j in range(2)])
    cm["tmp_rr"] = RR([s.sb("tmp%d" % j, [128, 64], F32) for j in range(4)])
    return cm


def emit_mix_a(s, cm, io, T, stage="all", bufs=None):
    nc = s.nc
    NT = T // 128
    if stage in ("all", "alloc"):
        if stage == "all":
            m = s.mark()
        b = dict(qT=s.sb("a_q", [128, T], BF16), k1=s.sb("a_k1", [128, T], BF16), k2=s.sb("a_k2", [128, T], BF16),
                 vt=s.sb("a_v", [128, NT, 128], BF16), lp=s.sb("lp", [128, 4, 32], F32), li=s.sb("li", [128, 2], F32),
                 lw=s.sb("lw", [128, 2, 32], F32), lam=s.sb("lam", [128, 4], F32))
        s.op("pool", lambda: nc.gpsimd.memset(b["qT"][64:128, :], 0.0), writes=[b["qT"]])
        s.op("dve", lambda: nc.vector.memset(b["k1"][:, :], 0.0), writes=[b["k1"]])
        s.op("pool", lambda: nc.gpsimd.memset(b["k2"][:, :], 0.0), writes=[b["k2"]])
        load_vt(s, b["vt"], io, "va", T, init=True, load=False)
        if stage == "alloc":
            return b
        bufs = b
    qT, k1, k2, vt, lp, li, lw, lam = (bufs[k] for k in ("qT", "k1", "k2", "vt", "lp", "li", "lw", "lam"))
    if stage in ("all", "load"):
        s.dma("sp", qT[0:64, :], io["qa"][:, :], writes=[qT])
        s.dma("sp", k1[0:32, :], io["ka"][0:32, :], writes=[k1])
        s.dma("sp", k2[32:64, :], io["ka"][32:64, :], writes=[k2])
        load_vt(s, vt, io, "va", T, init=False)
        s.dma("sp", lp[:, :, :], io["lamp"][:, :, :], writes=[lp])
        s.dma("sp", li[:, :], io["lami"][:, :], writes=[li])
        s.op("dve", lambda: nc.vector.tensor_tensor(out=lw[:, 0, :], in0=lp[:, 0, :], in1=lp[:, 1, :], op=ALU.mult),
             reads=[lp], writes=[lw])
        s.op("dve", lambda: nc.vector.tensor_tensor(out=lw[:, 1, :], in0=lp[:, 2, :], in1=lp[:, 3, :], op=ALU.mult),
             reads=[lp], writes=[lw])
        s.op("dve", lambda: nc.vector.tensor_reduce(out=lam[:, 0:2], in_=lw[:, :, :], axis=AX.X, op=ALU.add),
             reads=[lw], writes=[lam])
        s.op("act", lambda: nc.scalar.activation(out=lam[:, 0:2], in_=lam[:, 0:2], func=AF.Exp), reads=[lam], writes=[lam])
        s.op("dve", lambda: nc.vector.tensor_tensor(out=lam[:, 2:3], in0=lam[:, 1:2], in1=lam[:, 0:1], op=ALU.subtract),
             reads=[lam], writes=[lam])
        s.op("dve", lambda: nc.vector.tensor_tensor(out=lam[:, 3:4], in0=lam[:, 2:3], in1=li[:, 0:1], op=ALU.subtract),
             reads=[lam, li], writes=[lam])
        if stage == "load":
            return

    def fin(qb, pos):
        if "pf2" in cm:
            pf = cm["pf"]
            pf2 = cm["pf2"]
            emit_o_to_tokmajor(s, cm, pos[0], pf, 0)
            emit_o_to_tokmajor(s, cm, pos[1], pf2, 0)
        else:
            pf2 = cm["pf"]
            emit_o_to_tokmajor(s, cm, pos[0], pf2, 0)
            pf = cm["o1s_rr"].next()
            s.op("dve", lambda: nc.vector.tensor_copy(out=pf[:, :, :], in_=pf2[:, :, 0:65]), reads=[pf2], writes=[pf])
            emit_o_to_tokmajor(s, cm, pos[1], pf2, 0)
        ot = cm["ot_rr"].next()
        for j in range(4):
            st = cm["st_rr"].next()
            s.op("dve", lambda: nc.vector.tensor_scalar(out=st[:, 0:1], in0=pf[:, j, 64:65], scalar1=1e-30, scalar2=None,
                                                        op0=ALU.max), reads=[pf], writes=[st])
            s.op("dve", lambda: nc.vector.tensor_scalar(out=st[:, 1:2], in0=pf2[:, j, 64:65], scalar1=1e-30, scalar2=None,
                                                        op0=ALU.max), reads=[pf2], writes=[st])
            s.op("dve", lambda: nc.vector.reciprocal(out=st[:, 0:2], in_=st[:, 0:2]), reads=[st], writes=[st])
            t2 = cm["tmp_rr"].next()
            o = cm["tmp_rr"].next()
            s.op("dve", lambda: nc.vector.tensor_scalar(out=t2[:, :], in0=pf2[:, j, 0:64], scalar1=st[:, 1:2],
                                                        scalar2=lam[:, 3:4], op0=ALU.mult, op1=ALU.mult),
                 reads=[pf2, st, lam], writes=[t2])
            s.op("dve", lambda: nc.vector.scalar_tensor_tensor(out=o[:, :], in0=pf[:, j, 0:64], scalar=st[:, 0:1], in1=t2[:, :],
                                                               op0=ALU.mult, op1=ALU.add), reads=[pf, st, t2], writes=[o])
            s.op("act", lambda: nc.scalar.activation(out=t2[:, :], in_=o[:, :], func=AF.Square, accum_out=st[:, 2:3]),
                 reads=[o], writes=[t2, st])
            s.op("act", lambda: nc.scalar.activation(out=st[:, 3:4], in_=st[:, 2:3], func=AF.Ln, bias=cm["epsc"][:, 0:1],
                                                     scale=1.0 / 64.0), reads=[st, cm["epsc"]], writes=[st])
            s.op("act", lambda: nc.scalar.activation(out=st[:, 4:5], in_=st[:, 3:4], func=AF.Exp, scale=-0.5),
                 reads=[st], writes=[st])
            s.op("dve", lambda: nc.vector.tensor_scalar(out=ot[:, j, :], in0=o[:, :], scalar1=st[:, 4:5], scalar2=li[:, 1:2],
                                                        op0=ALU.mult, op1=ALU.mult), reads=[o, st, li], writes=[ot])
        s.dma("sp", io["oa"][qb * 512:(qb + 1) * 512, :].rearrange("(j p) d -> p j d", p=128), ot[:, :, :], reads=[ot])

    emit_attn_phase(s, cm, T, 2, qT, [k1, k2], vt, None, io["oa"], fin, name="a", lag=cm["lag"])
    if stage == "all":
        s.release(m)


def emit_mix_b(s, cm, io, T, stage="all", bufs=None):
    nc = s.nc
    NT = T // 128
    NQB = T // 512
    if stage in ("all", "alloc"):
        if stage == "all":
            m = s.mark()
        b = dict(qT=s.sb("b_q", [128, T], BF16), kT=s.sb("b_k", [128, T], BF16), vt=s.sb("b_v", [128, NT, 128], BF16),
                 fl=s.sb("fl", [128, NT], F32), fb=s.sb("fb", [128, 2], F32), tu=s.sb("tu", [128, 128], F32),
                 on=s.sb("on", [128, 128], F32), cc=s.sb("cc", [128, NT], F32), inc=s.sb("inc", [128, NT], F32),
                 tmpc=s.sb("tmpc", [128, NT], F32), btab=s.sb("btab", [128, NQB, NT], F32))
        s.op("pool", lambda: nc.gpsimd.memset(b["qT"][64:128, :], 0.0), writes=[b["qT"]])
        s.op("dve", lambda: nc.vector.memset(b["kT"][64:128, :], 0.0), writes=[b["kT"]])
        load_vt(s, b["vt"], io, "vb", T, init=True, load=False)
        s.dma("sp", b["tu"][:, :], io["triuf"][:, :], writes=[b["tu"]])
        s.dma("sp", b["on"][:, :], io["onesf"][:, :], writes=[b["on"]])
        if stage == "alloc":
            return b
        bufs = b
    qT, kT, vt, fl, fb, tu, on, cc, inc_, tmpc, btab = (bufs[k] for k in ("qT", "kT", "vt", "fl", "fb", "tu", "on", "cc", "inc",
                                                                            "tmpc", "btab"))
    if stage in ("all", "load"):
        s.dma("sp", qT[0:64, :], io["qb"][:, :], writes=[qT])
        s.dma("sp", kT[0:64, :], io["kb"][:, :], writes=[kT])
        load_vt(s, vt, io, "vb", T, init=False)
        if stage == "load":
            return
    if "flog_sb" not in io:
        s.dma("sp", fl[:, :], io["flog"][:, :], writes=[fl])
    s.dma("sp", fb[:, 0:1], io["fbias"][:, :], writes=[fb])
    s.op("dve", lambda: nc.vector.tensor_scalar(out=fb[:, 1:2], in0=fb[:, 0:1], scalar1=-1.0, scalar2=None, op0=ALU.mult),
         reads=[fb], writes=[fb])
    if "flog_sb" in io:
        fsb, fap = io["flog_sb"]
        s.op("act", lambda: nc.scalar.activation(out=fl[:, :], in_=fap, func=AF.Exp, bias=fb[:, 1:2], scale=-1.0),
             reads=[fsb, fb], writes=[fl])
    else:
        s.op("act", lambda: nc.scalar.activation(out=fl[:, :], in_=fl[:, :], func=AF.Exp, bias=fb[:, 1:2], scale=-1.0),
             reads=[fl, fb], writes=[fl])
    s.op("act", lambda: nc.scalar.activation(out=fl[:, :], in_=fl[:, :], func=AF.Ln, bias=1.0, scale=1.0),
         reads=[fl], writes=[fl])
    pc = cm["pf"]
    pcv = pc[:, 0, :]
    s.op("pe", lambda: nc.tensor.matmul(pc[:, 0, 0:NT], lhsT=tu[:, :], rhs=fl[:, :], start=True, stop=True),
         reads=[tu, fl], writes=[pc])
    s.op("pe", lambda: nc.tensor.matmul(pc[:, 1, 0:NT], lhsT=on[:, :], rhs=fl[:, :], start=True, stop=True),
         reads=[on, fl], writes=[pc])
    s.op("dve", lambda: nc.vector.tensor_copy(out=inc_[:, :], in_=pc[:, 1, 0:NT]), reads=[pc], writes=[inc_])
    sh = 1
    while sh < NT:
        s.op("dve", lambda: nc.vector.tensor_copy(out=tmpc[:, :], in_=inc_[:, :]), reads=[inc_], writes=[tmpc])
        s.op("dve", lambda: nc.vector.tensor_tensor(out=inc_[:, sh:NT], in0=tmpc[:, sh:NT], in1=tmpc[:, 0:NT - sh], op=ALU.add),
             reads=[tmpc], writes=[inc_])
        sh *= 2
    s.op("dve", lambda: nc.vector.tensor_tensor(out=cc[:, :], in0=pc[:, 0, 0:NT], in1=inc_[:, :], op=ALU.add),
         reads=[pc, inc_], writes=[cc])
    s.op("dve", lambda: nc.vector.tensor_tensor(out=tmpc[:, :], in0=cc[:, :], in1=pc[:, 1, 0:NT], op=ALU.subtract),
         reads=[pc, cc], writes=[tmpc])
    for qb in range(NQB):
        s.op("dve", lambda: nc.vector.tensor_scalar(out=btab[:, qb, :], in0=tmpc[:, :], scalar1=inc_[:, 4 * qb + 1:4 * qb + 2],
                                                    scalar2=None, op0=ALU.subtract), reads=[tmpc, inc_], writes=[btab])

    def bias_fn(qb, t):
        return btab, btab[:, qb, t:t + 1]

    def fin(qb, pos):
        pf = cm["pf"]
        emit_o_to_tokmajor(s, cm, pos[0], pf, 0)
        ot = cm["ot_rr"].next()
        for j in range(4):
            st = cm["st_rr"].next()
            s.op("dve", lambda: nc.vector.tensor_scalar(out=st[:, 0:1], in0=pf[:, j, 64:65], scalar1=1e-30, scalar2=None,
                                                        op0=ALU.max), reads=[pf], writes=[st])
            s.op("dve", lambda: nc.vector.reciprocal(out=st[:, 0:1], in_=st[:, 0:1]), reads=[st], writes=[st])
            s.op("dve", lambda: nc.vector.tensor_scalar(out=ot[:, j, :], in0=pf[:, j, 0:64], scalar1=st[:, 0:1], scalar2=None,
                                                        op0=ALU.mult), reads=[pf, st], writes=[ot])
        s.dma("sp", io["ob"][qb * 512:(qb + 1) * 512, :].rearrange("(j p) d -> p j d", p=128), ot[:, :, :], reads=[ot])

    emit_attn_phase(s, cm, T, 1, qT, [kT], vt, None, io["ob"], fin, bias_fn=bias_fn, name="b", lag=cm["lag"])
    if stage == "all":
        s.release(m)


def mix_consts():
    k = np.arange(128)
    tri = np.where(k[:, None] > k[None, :], NEG, 0.0).astype(np.float32)
    return dict(identb=_bf(np.eye(128, dtype=np.float32)), identf=np.eye(128, dtype=np.float32), trib=_bf(tri),
                triuf=np.triu(np.ones((128, 128), np.float32)), onesf=np.ones((128, 128), np.float32))


def mix_decl_c(nc, io, T):
    NT = T // 128
    QL = NT // 4
    NCT = max(1, T // 2048)
    io.update(dict(
        qc=dram_in(nc, "qc", [128, QL, 512], BF16),
        kskw=dram_in(nc, "kskw", [128, T], BF16),
        vs=dram_in(nc, "vs", [128, NT, 65], BF16), vw=dram_in(nc, "vw", [128, NT, 65], BF16),
        kvin=dram_in(nc, "kvin", [128, T], BF16),
        w1=dram_in(nc, "w1", [2, 2048, 256]), b1=dram_in(nc, "b1", [128, 4]),
        peT=dram_in(nc, "peT", [128, 32]),
        w2=dram_in(nc, "w2", [2, 256, 64]), b2=dram_in(nc, "b2", [2, 64]), b2c=dram_in(nc, "b2c", [64, 1]),
        kgain=dram_in(nc, "kgain", [64, 1]),
        ng=dram_in(nc, "ng", [128, QL, 12]),
        cmask=dram_in(nc, "cmask", [128, QL, NCT, 128], BF16),
        smask=dram_in(nc, "smask", [128, 4, 128], BF16), wmask=dram_in(nc, "wmask", [128, 8, 128], BF16),
        impA=dram_in(nc, "impA", [128, QL, 128]), impB=dram_in(nc, "impB", [128, QL, 128]),
        emat=dram_in(nc, "emat", [128, NT, 128], BF16), ovl=dram_in(nc, "ovl", [128, NCT, 128], BF16),
        ones64=dram_in(nc, "ones64", [64, 64], BF16), onesrow=dram_in(nc, "onesrow", [1, 128], BF16),
        oc=dram_out(nc, "oc", [QL * 128, 256], BF16),
    ))
    return io


def emit_gelu(s, zin_ap, zin_b, out_ap, out_b, tmp, shape_sl):
    nc = s.nc
    t = tmp
    s.op("act", lambda: nc.scalar.activation(out=t[shape_sl], in_=zin_ap, func=AF.Square), reads=[zin_b], writes=[t])
    s.op("dve", lambda: nc.vector.tensor_scalar(out=t[shape_sl], in0=t[shape_sl], scalar1=0.044715, scalar2=1.0,
                                                op0=ALU.mult, op1=ALU.add), reads=[t], writes=[t])
    s.op("dve", lambda: nc.vector.tensor_tensor(out=t[shape_sl], in0=t[shape_sl], in1=zin_ap, op=ALU.mult),
         reads=[t, zin_b], writes=[t])
    s.op("act", lambda: nc.scalar.activation(out=t[shape_sl], in_=t[shape_sl], func=AF.Exp, scale=-GELU_C), reads=[t], writes=[t])
    s.op("dve", lambda: nc.vector.tensor_scalar(out=t[shape_sl], in0=t[shape_sl], scalar1=1.0, scalar2=None, op0=ALU.add),
         reads=[t], writes=[t])
    s.op("dve", lambda: nc.vector.reciprocal(out=t[shape_sl], in_=t[shape_sl]), reads=[t], writes=[t])
    s.op("dve", lambda: nc.vector.tensor_tensor(out=out_ap, in0=t[shape_sl], in1=zin_ap, op=ALU.mult),
         reads=[t, zin_b], writes=[out_b])


def emit_mix_c(s, cm, io, T, cs=None):
    fused = cs is not None
    cs = cs if fused else [None]
    nc = s.nc
    NT = T // 128
    QL = NT // 4
    NCT = max(1, T // 2048)
    Nc = T // 16 - 1
    NCP = NCT * 128 if Nc > 128 else 128
    NCW = min(Nc, 511)
    assert Nc <= 511
    m = s.mark()
    ident_b = cm["ident_b"]
    ps_l = cm["ps_rr"].items
    po_l = cm["po_rr"].items
    pf, pf2 = cm["pf"], cm["pf2"]

    def ld(name, shape, dt, src, q="sp"):
        b = s.sb(name, shape, dt)
        idx = tuple(slice(None) for _ in shape)
        s.dma(q, b[idx], src, writes=[b])
        return b

    qc = s.sb("c_q", [128, QL, 512], BF16)
    qc2 = s.sb("c_q2", [128, QL, 512], BF16)
    s.op("pool", lambda: nc.gpsimd.memset(qc[64:128, :, :], 0.0), writes=[qc])
    s.op("dve", lambda: nc.vector.memset(qc2[0:64, :, :], 0.0), writes=[qc2])
    kk = ld("c_kk", [128, T], BF16, io["kskw"][:, :])
    vs = s.sb("c_vs", [128, NT, 128], BF16)
    vw = s.sb("c_vw", [128, NT, 128], BF16)
    load_vt(s, vs, io, "vs", T)
    load_vt(s, vw, io, "vw", T)
    emat = ld("c_e", [128, NT, 128], BF16, io["emat"][:, :, :])
    ovl = ld("c_ovl", [128, NCT, 128], BF16, io["ovl"][:, :, :])
    smask = s.sb("c_sm", [128, 4, 128], BF16)
    wmask = s.sb("c_wm", [128, 8, 128], BF16)
    ngt = s.sb("c_ng", [128, QL, 12], F32)
    ones64 = ld("c_o64", [64, 64], BF16, io["ones64"][:, :])
    onesrow = ld("c_orow", [1, 128], BF16, io["onesrow"][:, :])
    kgain = ld("c_kg", [64, 1], F32, io["kgain"][:, :])
    b2c = ld("c_b2c", [64, 1], F32, io["b2c"][:, :])
    b1 = ld("c_b1", [128, 4], F32, io["b1"][:, :])

    ktc = s.sb("c_ktc", [128, NCP], BF16)
    vc = s.sb("c_vc", [128, NCT, 128], BF16)
    s.op("dve", lambda: nc.vector.memset(ktc[:, :], 0.0), writes=[ktc])
    s.op("dve", lambda: nc.vector.memset(vc[:, :, :], 0.0), writes=[vc])
    s.op("dve", lambda: nc.vector.memset(vc[:, :, 64:65], 1.0), writes=[vc])

    m2 = s.mark()
    kvin = ld("c_kvin", [128, T], BF16, io["kvin"][:, :])
    w1sb = s.sb("c_w1", [128, 32, 256], BF16)
    for x in range(2):
        s.dma("pool", w1sb[x * 64:(x + 1) * 64, :, :], io["w1"][x].rearrange("(j d) f -> d j f", d=64), writes=[w1sb])
    peT = s.sb("c_pe", [128, 32], BF16)
    s.dma("pool", peT[:, :], io["peT"][:, :], writes=[peT])
    w2sb = s.sb("c_w2", [128, 2, 2, 64], BF16)
    for x in range(2):
        s.dma("pool", w2sb[:, x, :, :], io["w2"][x].rearrange("(hh f) d -> f hh d", f=128), writes=[w2sb])
    b2row = s.sb("c_b2r", [1, 64], BF16)
    s.dma("pool", b2row[:, :], io["b2"][1:2, :], writes=[b2row])
    hacc = [ps_l[0], ps_l[1], ps_l[2], po_l[0]]
    pcol = po_l[1]
    for x in range(2):
        for hh in range(2):
            hp = hacc[x * 2 + hh]
            for j in range(32):
                s.op("pe", lambda: nc.tensor.matmul(hp[:, 0:NCW], lhsT=w1sb[x * 64:(x + 1) * 64, j, hh * 128:(hh + 1) * 128],
                                                    rhs=kvin[x * 64:(x + 1) * 64, j:j + 16 * (NCW - 1) + 1:16],
                                                    start=(j == 0), stop=(j == 31)),
                     reads=[w1sb, kvin], writes=[hp], inc=(j == 31))
            for j in range(32):
                s.op("pe", lambda: nc.tensor.matmul(pcol[:, x * 2 + hh:x * 2 + hh + 1],
                                                    lhsT=w1sb[x * 64:(x + 1) * 64, j, hh * 128:(hh + 1) * 128],
                                                    rhs=peT[x * 64:(x + 1) * 64, j:j + 1], start=(j == 0), stop=(j == 31)),
                     reads=[w1sb, peT], writes=[pcol], inc=(j == 31))
    hbias = s.sb("c_hb", [128, 4], F32)
    s.op("dve", lambda: nc.vector.tensor_tensor(out=hbias[:, :], in0=pcol[:, 0:4], in1=b1[:, :], op=ALU.add),
         reads=[pcol, b1], writes=[hbias])
    gh = []
    for x in range(2):
        for hh in range(2):
            k = x * 2 + hh
            z = s.sb("c_z%d" % k, [128, 512], F32)
            tmp = s.sb("c_zt%d" % k, [128, 512], F32)
            gb = s.sb("c_g%d" % k, [128, 512], BF16)
            s.op("act", lambda: nc.scalar.activation(out=z[:, 0:NCW], in_=hacc[k][:, 0:NCW], func=AF.Identity,
                                                     bias=hbias[:, k:k + 1], scale=1.0), reads=[hacc[k], hbias], writes=[z])
            emit_gelu(s, z[:, 0:NCW], z, gb[:, 0:NCW], gb, tmp, (slice(None), slice(0, NCW)))
            gh.append(gb)
    pk = po_l[2]
    for hh in range(2):
        s.op("pe", lambda: nc.tensor.matmul(pk[0:64, 0:NCW], lhsT=w2sb[:, 0, hh, :], rhs=gh[hh][:, 0:NCW],
                                            start=(hh == 0), stop=(hh == 1)), reads=[w2sb, gh[hh]], writes=[pk], inc=(hh == 1))
    kz = s.sb("c_kz", [64, 512], F32)
    ksq = s.sb("c_ksq", [64, 512], BF16)
    krs = s.sb("c_krs", [64, 512], F32)
    s.op("act", lambda: nc.scalar.activation(out=kz[:, 0:NCW], in_=pk[0:64, 0:NCW], func=AF.Identity, bias=b2c[:, 0:1], scale=1.0),
         reads=[pk, b2c], writes=[kz])
    s.op("act", lambda: nc.scalar.activation(out=ksq[:, 0:NCW], in_=kz[:, 0:NCW], func=AF.Square), reads=[kz], writes=[ksq])
    pq = ps_l[0]
    s.op("pe", lambda: nc.tensor.matmul(pq[0:64, 0:NCW], lhsT=ones64[:, :], rhs=ksq[:, 0:NCW], start=True, stop=True),
         reads=[ones64, ksq], writes=[pq])
    s.op("act", lambda: nc.scalar.activation(out=krs[:, 0:NCW], in_=pq[0:64, 0:NCW], func=AF.Ln, bias=cm["epsc"][0:64, 0:1],
                                             scale=1.0 / 64.0), reads=[pq, cm["epsc"]], writes=[krs])
    s.op("act", lambda: nc.scalar.activation(out=krs[:, 0:NCW], in_=krs[:, 0:NCW], func=AF.Exp, scale=-0.5), reads=[krs], writes=[krs])
    s.op("dve", lambda: nc.vector.scalar_tensor_tensor(out=ktc[0:64, 0:NCW], in0=kz[:, 0:NCW], scalar=kgain[:, 0:1], in1=krs[:, 0:NCW],
                                                       op0=ALU.mult, op1=ALU.mult), reads=[kz, kgain, krs], writes=[ktc])
    for nt in range(NCT):
        n0 = nt * 128
        nn = min(128, Nc - n0)
        pv = ps_l[1 + nt % 2]
        for hh in range(2):
            s.op("pe", lambda: nc.tensor.matmul(pv[0:nn, 0:64], lhsT=gh[2 + hh][:, n0:n0 + nn], rhs=w2sb[:, 1, hh, :],
                                                start=(hh == 0), stop=False), reads=[gh[2 + hh], w2sb], writes=[pv], inc=False)
        s.op("pe", lambda: nc.tensor.matmul(pv[0:nn, 0:64], lhsT=onesrow[0:1, 0:nn], rhs=b2row[0:1, :], start=False, stop=True),
             reads=[onesrow, b2row], writes=[pv])
        s.op("act", lambda: nc.scalar.copy(out=vc[0:nn, nt, 0:64], in_=pv[0:nn, 0:64]), reads=[pv], writes=[vc])
    s.release(m2)

    cmk_rr = RR([s.sb("c_cmk%d" % j, [128, NCT, 128], BF16) for j in range(2)])
    ia_rr = RR([s.sb("c_ia%d" % j, [128, 128], F32) for j in range(2)])
    ib_rr = RR([s.sb("c_ib%d" % j, [128, 128], F32) for j in range(2)])
    imp_rr = RR([s.sb("c_imp%d" % j, [128, 128], F32) for j in range(2)])
    imp2_rr = RR([s.sb("c_impb%d" % j, [128, 128], F32) for j in range(2)])
    m8_rr = RR([s.sb("c_m8%d" % j, [128, 16], F32) for j in range(2)])
    mbT_rr = RR([s.sb("c_mbT%d" % j, [128, 128], BF16) for j in range(2)])
    oco_rr = RR([s.sb("c_oc%d" % j, [128, 4, 64], F32) for j in range(2)])
    gw_rr = RR([s.sb("c_gw%d" % j, [128, 12], F32) for j in range(2)])
    oo_rr = RR([s.sb("c_oo%d" % j, [128, 4, 64], F32) for j in range(2)])
    ob_rr = RR([s.sb("c_ob%d" % j, [128, 4, 64], BF16) for j in range(2)])

    pipe = Pipe(2)

    def masked_tile(kbuf, prow, t, Q, masks, vbuf, vt_idx, po, first, last, extra=None):
        ps = cm["ps_rr"].next()
        nm = len(masks)
        s.op("pe", lambda: nc.tensor.matmul(ps[:, :], lhsT=kbuf[:, t * 128:(t + 1) * 128], rhs=Q,
                                            start=True, stop=(nm == 0)), reads=[kbuf, qc, qc2], writes=[ps], inc=(nm == 0))
        for mi, (la, lb, ra, rb) in enumerate(masks):
            for h in range(4):
                lastm = (mi == nm - 1 and h == 3)
                s.op("pe", lambda: nc.tensor.matmul(ps[:, h * 128:(h + 1) * 128], lhsT=la, rhs=ra, start=False, stop=lastm),
                     reads=[lb, rb], writes=[ps], inc=lastm)
        pt = cm["pt_rr"].next()
        s.op("act", lambda: nc.scalar.activation(out=pt[:, :], in_=ps[:, :], func=AF.Exp), reads=[ps], writes=[pt])

        def back(pt=pt, po=po, vbuf=vbuf, vt_idx=vt_idx, first=first, last=last, extra=extra):
            s.op("pe", lambda: nc.tensor.matmul(po[:, :], lhsT=vbuf[:, vt_idx, :], rhs=pt[:, :], start=first, stop=last),
                 reads=[vbuf, pt], writes=[po])
            if extra is not None:
                extra(pt)
        pipe.push(back)

    for ci in cs:
        def gk(key):
            return io[key][ci] if fused else io[key]
        if fused:
            for h in range(4):
                r0 = (h % 2) * 64
                srcq = io["zq"][h // 2][r0:r0 + 64, :].rearrange("d (i c q) -> d i c q", c=4, q=128)[:, :, ci, :]
                s.dma("sp", qc[0:64, :, h * 128:(h + 1) * 128], srcq, writes=[qc])
                s.dma("sp", qc2[64:128, :, h * 128:(h + 1) * 128], srcq, writes=[qc2])
            msb, mview = io["misc_sb"]
            s.op("act", lambda: nc.scalar.activation(out=ngt[:, :, :], in_=mview[:, ci:NT:4, 4:16], func=AF.Exp, scale=-1.0),
                 reads=[msb], writes=[ngt])
        else:
            s.dma("sp", qc[0:64, :, :], io["qc"][0:64, :, :], writes=[qc])
            s.dma("sp", qc2[64:128, :, :], io["qc"][64:128, :, :], writes=[qc2])
            s.dma("sp", ngt[:, :, :], io["ng"][:, :, :], writes=[ngt])
            s.op("act", lambda: nc.scalar.activation(out=ngt[:, :, :], in_=ngt[:, :, :], func=AF.Exp, scale=-1.0),
                 reads=[ngt], writes=[ngt])
        s.op("dve", lambda: nc.vector.tensor_scalar(out=ngt[:, :, :], in0=ngt[:, :, :], scalar1=1.0, scalar2=None, op0=ALU.add),
             reads=[ngt], writes=[ngt])
        s.op("dve", lambda: nc.vector.reciprocal(out=ngt[:, :, :], in_=ngt[:, :, :]), reads=[ngt], writes=[ngt])
        s.dma("sp", smask[:, :, :], gk("smask")[:, :, :], writes=[smask])
        s.dma("sp", wmask[:, :, :], gk("wmask")[:, :, :], writes=[wmask])
        for i in range(QL):
            Qlo = qc[:, i, :]
            Qhi = qc2[:, i, :]
            cmk = cmk_rr.next()
            ia = ia_rr.next()
            ib = ib_rr.next()
            s.dma("sp", cmk[:, :, :], gk("cmask")[:, i, :, :], writes=[cmk])
            s.dma("sp", ia[:, :], gk("impA")[:, i, :], writes=[ia])
            s.dma("sp", ib[:, :], gk("impB")[:, i, :], writes=[ib])
            po_c, po_s, po_w = po_l[0], po_l[1], po_l[2]
            nct = min(NCT, i // 4 + 1)
            for nt in range(nct):
                def imp_mm(pt, nt=nt, nct=nct):
                    for h in range(4):
                        s.op("pe", lambda: nc.tensor.matmul(pf2[:, h, :], lhsT=pt[:, h * 128:(h + 1) * 128], rhs=ovl[:, nt, :],
                                                            start=(nt == 0 and h == 0), stop=(nt == nct - 1 and h == 3),
                                                            skip_group_check=True), reads=[pt, ovl], writes=[pf2],
                             inc=(h == 3))
                masked_tile(ktc, (0, 64), nt, Qlo, [(ident_b[:, :], ident_b, cmk[:, nt, :], cmk)], vc, nt, po_c,
                            nt == 0, nt == nct - 1, extra=imp_mm)
            pipe.flush()
            emit_o_to_tokmajor(s, cm, po_c, pf, 0)
            st = cm["st_rr"].next()
            rsum = cm["st_rr"].next()
            gw = gw_rr.next()
            s.op("dve", lambda: nc.vector.tensor_scalar(out=st[:, 0:4], in0=pf[:, :, 64], scalar1=1e-30, scalar2=None, op0=ALU.max),
                 reads=[pf], writes=[st])
            s.op("dve", lambda: nc.vector.reciprocal(out=rsum[:, 0:4], in_=st[:, 0:4]), reads=[st], writes=[rsum])
            oco = oco_rr.next()
            s.op("dve", lambda: nc.vector.tensor_copy(out=oco[:, :, :], in_=pf[:, :, 0:64]), reads=[pf], writes=[oco])
            imp = imp_rr.next()
            s.op("dve", lambda: nc.vector.tensor_scalar(out=imp[:, :], in0=pf2[:, 0, :], scalar1=rsum[:, 0:1], scalar2=None, op0=ALU.mult),
                 reads=[pf2, rsum], writes=[imp])
            for h in range(1, 4):
                s.op("dve", lambda: nc.vector.scalar_tensor_tensor(out=imp[:, :], in0=pf2[:, h, :], scalar=rsum[:, h:h + 1], in1=imp[:, :],
                                                                   op0=ALU.mult, op1=ALU.add), reads=[pf2, rsum, imp], writes=[imp])
            s.op("dve", lambda: nc.vector.tensor_tensor(out=imp[:, :], in0=imp[:, :], in1=ia[:, :], op=ALU.mult), reads=[imp, ia], writes=[imp])
            s.op("dve", lambda: nc.vector.tensor_tensor(out=imp[:, :], in0=imp[:, :], in1=ib[:, :], op=ALU.add), reads=[imp, ib], writes=[imp])
            m8 = m8_rr.next()
            imp2 = imp2_rr.next()
            s.op("dve", lambda: nc.vector.max(out=m8[:, 0:8], in_=imp[:, :]), reads=[imp], writes=[m8])
            s.op("dve", lambda: nc.vector.match_replace(out=imp2[:, :], in_to_replace=m8[:, 0:8], in_values=imp[:, :], imm_value=-1e9),
                 reads=[imp, m8], writes=[imp2])
            s.op("dve", lambda: nc.vector.max(out=m8[:, 8:16], in_=imp2[:, :]), reads=[imp2], writes=[m8])
            s.op("dve", lambda: nc.vector.tensor_scalar(out=imp2[:, :], in0=imp[:, :], scalar1=m8[:, 15:16], scalar2=NEG,
                                                        op0=ALU.is_lt, op1=ALU.mult), reads=[imp, m8], writes=[imp2])
            tl = [4 * (i - 1) + u for u in range(8) if 4 * (i - 1) + u >= 0]
            for t in tl:
                u = t - 4 * (i - 1)
                masked_tile(kk, (64, 128), t, Qhi, [(ident_b[:, :], ident_b, wmask[:, u, :], wmask)], vw, t, po_w,
                            t == tl[0], t == tl[-1])
            ptr = cm["ps_rr"].next()
            s.op("pe", lambda: nc.tensor.transpose(out=ptr[:, 0:128], in_=imp2[:, :], identity=cm["ident_f"][:, :]),
                 reads=[imp2, cm["ident_f"]], writes=[ptr])
            mbT = mbT_rr.next()
            s.op("dve", lambda: nc.vector.tensor_copy(out=mbT[:, :], in_=ptr[:, 0:128]), reads=[ptr], writes=[mbT])
            nts = 4 * i + 4
            for t in range(nts):
                masks = [(emat[:, t, :], emat, mbT[:, :], mbT)]
                if t >= 4 * i:
                    masks.append((ident_b[:, :], ident_b, smask[:, t - 4 * i, :], smask))
                masked_tile(kk, (0, 64), t, Qlo, masks, vs, t, po_s, t == 0, t == nts - 1)
            pipe.flush()
            emit_o_to_tokmajor(s, cm, po_s, pf, 0)
            st2 = cm["st_rr"].next()
            s.op("dve", lambda: nc.vector.tensor_scalar(out=st2[:, 0:4], in0=pf[:, :, 64], scalar1=1e-30, scalar2=None, op0=ALU.max),
                 reads=[pf], writes=[st2])
            s.op("dve", lambda: nc.vector.reciprocal(out=st2[:, 0:4], in_=st2[:, 0:4]), reads=[st2], writes=[st2])
            gv = ngt[:, i, :].rearrange("p (h b) -> p h b", b=3)
            gwv = gw[:, :].rearrange("p (h b) -> p h b", b=3)
            s.op("dve", lambda: nc.vector.tensor_tensor(out=gwv[:, :, 0], in0=gv[:, :, 0], in1=rsum[:, 0:4], op=ALU.mult),
                 reads=[ngt, rsum], writes=[gw])
            s.op("dve", lambda: nc.vector.tensor_tensor(out=gwv[:, :, 1], in0=gv[:, :, 1], in1=st2[:, 0:4], op=ALU.mult),
                 reads=[ngt, st2], writes=[gw])
            oo = oo_rr.next()
            for h in range(4):
                s.op("dve", lambda: nc.vector.tensor_scalar(out=oo[:, h, :], in0=oco[:, h, :], scalar1=gw[:, 3 * h:3 * h + 1], scalar2=None,
                                                            op0=ALU.mult), reads=[oco, gw], writes=[oo])
                s.op("dve", lambda: nc.vector.scalar_tensor_tensor(out=oo[:, h, :], in0=pf[:, h, 0:64], scalar=gw[:, 3 * h + 1:3 * h + 2],
                                                                   in1=oo[:, h, :], op0=ALU.mult, op1=ALU.add), reads=[pf, gw, oo], writes=[oo])
            emit_o_to_tokmajor(s, cm, po_w, pf, 0)
            st3 = cm["st_rr"].next()
            s.op("dve", lambda: nc.vector.tensor_scalar(out=st3[:, 0:4], in0=pf[:, :, 64], scalar1=1e-30, scalar2=None, op0=ALU.max),
                 reads=[pf], writes=[st3])
            s.op("dve", lambda: nc.vector.reciprocal(out=st3[:, 0:4], in_=st3[:, 0:4]), reads=[st3], writes=[st3])
            s.op("dve", lambda: nc.vector.tensor_tensor(out=gwv[:, :, 2], in0=gv[:, :, 2], in1=st3[:, 0:4], op=ALU.mult),
                 reads=[ngt, st3], writes=[gw])
            ob = ob_rr.next()
            for h in range(4):
                s.op("dve", lambda: nc.vector.scalar_tensor_tensor(out=ob[:, h, :], in0=pf[:, h, 0:64], scalar=gw[:, 3 * h + 2:3 * h + 3],
                                                                   in1=oo[:, h, :], op0=ALU.mult, op1=ALU.add), reads=[pf, gw, oo], writes=[ob])
            orow = ((4 * i + ci) if fused else i) * 128
            s.dma("sp", io["oc"][orow:orow + 128, :], ob[:, :, :].rearrange("p h d -> p (h d)"), reads=[ob])
    s.release(m)


def build_mix(T, parts="abc"):
    nc = bass.Bass("TRN2", target_bir_lowering=False)
    io = mix_decl(nc, T)
    if "c" in parts:
        mix_decl_c(nc, io, T)
    s = S(nc)
    cm = mix_common(s, io)
    if "a" in parts:
        emit_mix_a(s, cm, io, T)
    if "b" in parts:
        emit_mix_b(s, cm, io, T)
    if "c" in parts:
        emit_mix_c(s, cm, io, T)
    s.finish()
    s.close()
    return nc


def mix_consts_c(T, c):
    NT = T // 128
    QL = NT // 4
    NCT = max(1, T // 2048)
    Nc = T // 16 - 1
    NS = T // 64
    ar = np.arange(128)
    cmask = np.zeros((128, QL, NCT, 128), np.float32)
    impA = np.zeros((128, QL, 128), np.float32)
    impB = np.zeros((128, QL, 128), np.float32)
    for i in range(QL):
        qpos = 128 * (4 * i + c) + ar
        for nt in range(NCT):
            n = 128 * nt + ar
            ok = (16 * n[:, None] + 31 <= qpos[None, :]) & (n[:, None] < Nc)
            cmask[:, i, nt, :] = np.where(ok, 0.0, NEG)
        j = ar
        cur = qpos // 64
        forced = (j[None, :] == 0) | (j[None, :] == cur[:, None]) | (j[None, :] == cur[:, None] - 1)
        valid = (j[None, :] * 64 <= qpos[:, None]) & (j[None, :] < NS)
        impA[:, i, :] = (valid & ~forced).astype(np.float32)
        impB[:, i, :] = np.where(forced & (j[None, :] < NS), 1.0e4, np.where(valid, 0.0, -1.0))
    smask = np.zeros((128, 4, 128), np.float32)
    for u in range(4):
        kpos = 128 * u + ar
        qp = 128 * c + ar
        smask[:, u, :] = np.where(kpos[:, None] <= qp[None, :], 0.0, NEG)
    wmask = np.zeros((128, 8, 128), np.float32)
    for u in range(8):
        dist = 128 * (c + 4 - u) + ar[None, :] - ar[:, None]
        wmask[:, u, :] = np.where((dist >= 0) & (dist < 512), 0.0, NEG)
    emat = np.zeros((128, NT, 128), np.float32)
    for t in range(NT):
        for k in range(128):
            jj = 2 * t + k // 64
            if jj < 128:
                emat[jj, t, k] = 1.0
    ovl = np.zeros((128, NCT, 128), np.float32)
    for nt in range(NCT):
        n = 128 * nt + ar
        o = (n[:, None] * 16 < (ar[None, :] + 1) * 64) & (n[:, None] * 16 + 32 > ar[None, :] * 64) & (n[:, None] < Nc) \
            & (ar[None, :] < NS)
        ovl[:, nt, :] = o
    return dict(cmask=_bf(cmask), impA=impA, impB=impB, smask=_bf(smask), wmask=_bf(wmask), emat=_bf(emat), ovl=_bf(ovl),
                ones64=_bf(np.ones((64, 64), np.float32)), onesrow=_bf(np.ones((1, 128), np.float32)))


def build_merge(NT, TB=512):
    nc = bass.Bass("TRN2", target_bir_lowering=False)
    x = dram_in(nc, "x", [NT, D])
    g = dram_in(nc, "g", [D])
    w_in = dram_in(nc, "w_in", [D, 6800])
    w_br = dram_in(nc, "w_br", [4, 256, D])
    w_o = dram_in(nc, "w_o", [D, D])
    ident = dram_in(nc, "ident", [128, 128], BF16)
    obr = dram_in(nc, "obr", [NT, D], BF16)
    y = dram_out(nc, "y", [NT, D])
    s = S(nc)
    emit_merge(s, x, g, w_in, w_br, w_o, ident, obr, y, NT, TB)
    s.finish()
    s.close()
    return nc


def emit_merge(s, x, g, w_in, w_br, w_o, ident, obr, y, NT, TB=512):
    nc = s.nc
    m_ = s.mark()
    ntile = TB // 128
    ident_b = s.sb("ident_b", [128, 128], BF16)
    s.dma("sp", ident_b[:, :], ident[:, :], writes=[ident_b])
    gcol = s.sb("gcol", [128, NKC], F32)
    s.dma("sp", gcol[:, :], g.rearrange("(c p) -> p c", p=128), writes=[gcol], allow_slow_non_contiguous=True)
    epsc = s.sb("epsc", [128, 1], F32)
    s.op("dve", lambda: nc.vector.memset(epsc[:, :], EPS), writes=[epsc])
    wg = [s.sb("wg%d" % c, [128, 4096], BF16) for c in range(NKC)]
    wb = [s.sb("wb%d" % c, [128, D], BF16) for c in range(8)]
    wo = [s.sb("wo%d" % c, [128, D], BF16) for c in range(NKC)]
    for c in range(NKC):
        for hf in range(2):
            s.dma("pool", wg[c][:, hf * 2048:(hf + 1) * 2048], w_in[c * 128:(c + 1) * 128, 2704 + hf * 2048:2704 + (hf + 1) * 2048],
                  writes=[wg[c]])
    for n in range(4):
        for cc in range(2):
            s.dma("pool", wb[2 * n + cc][:, :], w_br[n, cc * 128:(cc + 1) * 128, :], writes=[wb[2 * n + cc]])
    for c in range(NKC):
        s.dma("pool", wo[c][:, :], w_o[c * 128:(c + 1) * 128, :], writes=[wo[c]])
    xn = [s.sb("xn%d" % j, [128, D], F32) for j in range(ntile)]
    xr_rr = RR([s.sb("xr%d" % j, [128, D], F32) for j in range(2)])
    ots = [[s.sb("ot%d_%d" % (k, j), [128, D], BF16) for j in range(ntile)] for k in range(2)]
    hb_rr = RR([s.sb("hb%d" % j, [128, D], BF16) for j in range(2 * ntile)])
    stat_rr = RR([s.sb("st%d" % j, [128, 16], F32) for j in range(4)])
    hT = s.sb("hT", [128, NKC, TB], BF16)
    oT = s.sb("oT", [128, 8, TB], BF16)
    mT = [s.sb("mT%d" % c, [128, TB], BF16) for c in range(8)]
    pT_rr = RR([s.ps("pT%d" % j, [128, TB], BF16) for j in range(2)])
    pg_rr = RR([s.ps("pg%d" % j, [128, 512], F32) for j in range(2)])
    pp_rr = RR([s.ps("pp%d" % j, [128, 512], F32) for j in range(2)])
    po_rr = RR([s.ps("po%d" % j, [128, 512], F32) for j in range(2)])
    sg_rr = RR([s.sb("sg%d" % j, [128, TB], F32) for j in range(3)])
    acc_rr = RR([s.sb("acc%d" % j, [128, TB], F32) for j in range(2)])
    nblk = NT // TB

    def prep_a(tb):
        ot = ots[tb % 2]
        for j in range(ntile):
            r0 = tb * TB + j * 128
            s.dma("sp", xn[j][:, :], x[r0:r0 + 128, :], writes=[xn[j]])
            s.dma("sp", ot[j][:, :], obr[r0:r0 + 128, :], writes=[ot[j]])
        return emit_norm(s, epsc, xn, hb_rr, None, stat_rr, ntile)

    def prep_b(tb, hbs):
        ot = ots[tb % 2]
        emit_transpose_T(s, hbs, gcol, hT, ident_b, pT_rr, ntile)
        for c in range(8):
            pT = pT_rr.next()
            for j in range(ntile):
                s.op("pe", lambda: nc.tensor.transpose(out=pT[:, j * 128:(j + 1) * 128], in_=ot[j][:, c * 128:(c + 1) * 128],
                                                       identity=ident_b[:, :]), reads=[ot[j], ident_b], writes=[pT], inc=(j == ntile - 1))
            if c % 2 == 0:
                s.op("dve", lambda: nc.vector.tensor_copy(out=oT[:, c, :], in_=pT[:, 0:TB]), reads=[pT], writes=[oT])
            else:
                s.op("act", lambda: nc.scalar.copy(out=oT[:, c, :], in_=pT[:, 0:TB]), reads=[pT], writes=[oT])

    hbs_next = prep_a(0)
    prep_b(0, hbs_next)
    for tb in range(nblk):
        t0 = tb * TB
        if tb + 1 < nblk:
            hbs_next = prep_a(tb + 1)
        for dc in range(8):
            acc = acc_rr.next()
            for n in range(4):
                pg = pg_rr.next()
                pp = pp_rr.next()
                for c in range(NKC):
                    s.op("pe", lambda: nc.tensor.matmul(pg[:, 0:TB], lhsT=wg[c][:, n * 1024 + dc * 128:n * 1024 + (dc + 1) * 128],
                                                        rhs=hT[:, c, :], start=(c == 0), stop=(c == NKC - 1)),
                         reads=[wg[c], hT], writes=[pg], inc=(c == NKC - 1))
                for cc in range(2):
                    s.op("pe", lambda: nc.tensor.matmul(pp[:, 0:TB], lhsT=wb[2 * n + cc][:, dc * 128:(dc + 1) * 128],
                                                        rhs=oT[:, 2 * n + cc, :], start=(cc == 0), stop=(cc == 1)),
                         reads=[wb[2 * n + cc], oT], writes=[pp], inc=(cc == 1))
                sg = sg_rr.next()
                s.op("act", lambda: nc.scalar.activation(out=sg[:, :], in_=pg[:, 0:TB], func=AF.Sigmoid), reads=[pg], writes=[sg])
                if n == 0:
                    s.op("dve", lambda: nc.vector.tensor_tensor(out=acc[:, :], in0=sg[:, :], in1=pp[:, 0:TB], op=ALU.mult),
                         reads=[sg, pp], writes=[acc])
                else:
                    s.op("dve", lambda: nc.vector.tensor_tensor(out=sg[:, :], in0=sg[:, :], in1=pp[:, 0:TB], op=ALU.mult),
                         reads=[sg, pp], writes=[sg])
                    if n < 3:
                        s.op("pool", lambda: nc.gpsimd.tensor_tensor(out=acc[:, :], in0=acc[:, :], in1=sg[:, :], op=ALU.add),
                             reads=[acc, sg], writes=[acc])
                    else:
                        s.op("pool", lambda: nc.gpsimd.tensor_tensor(out=mT[dc][:, :], in0=acc[:, :], in1=sg[:, :], op=ALU.add),
                             reads=[acc, sg], writes=[mT[dc]])
        if tb + 1 < nblk:
            prep_b(tb + 1, hbs_next)
        for j in range(ntile):
            xr = xr_rr.next()
            s.dma("sp", xr[:, :], x[t0 + j * 128:t0 + (j + 1) * 128, :], writes=[xr])
            for hf in range(2):
                po = po_rr.next()
                for dc in range(8):
                    s.op("pe", lambda: nc.tensor.matmul(po[:, :], lhsT=mT[dc][:, j * 128:(j + 1) * 128],
                                                        rhs=wo[dc][:, hf * 512:(hf + 1) * 512], start=(dc == 0), stop=(dc == 7)),
                         reads=[mT[dc], wo[dc]], writes=[po], inc=(dc == 7))
                s.op("dve", lambda: nc.vector.tensor_tensor(out=xr[:, hf * 512:(hf + 1) * 512], in0=po[:, :],
                                                            in1=xr[:, hf * 512:(hf + 1) * 512], op=ALU.add),
                     reads=[po, xr], writes=[xr])
            s.dma("sp", y[t0 + j * 128:t0 + (j + 1) * 128, :], xr[:, :], reads=[xr])
    s.release(m_)


PARAM_SHAPES = dict(
    ffn1_norm=("L", D), ffn1_w_in=("L", D, 2 * DFF), ffn1_w_out=("L", DFF, D), mix_norm=("L", D), w_in=("L", D, 6800),
    nsa_phi_w1=("L", 2, 2048, 256), nsa_phi_w2=("L", 2, 256, 64), nsa_phi_b2=("L", 2, 64),
    w_branch=("L", 4, 256, D), w_out=("L", D, D), ffn2_norm=("L", D), ffn2_w_in=("L", D, 2 * DFF), ffn2_w_out=("L", DFF, D),
    gains=("L", 128, 6), vgain=("L", 128, 256), wsT=("L", 4, 128, 128), bsT=("L", 128, 4), lamp=("L", 128, 4, 32),
    lami=("L", 128, 2), fbias=("L", 4, 128, 1), b1l=("L", 128, 4), peT=("L", 128, 32), b2c=("L", 64, 1), kgain=("L", 64, 1),
)


def fused_const_shapes(T):
    NT = T // 128
    QL = NT // 4
    NCT = max(1, T // 2048)
    return dict(
        ident=([128, 128], BF16), identf=([128, 128], F32), blk=([2, 128, 128], BF16), triu=([128, 128], F32),
        trib=([128, 128], BF16), triuf=([128, 128], F32), onesf=([128, 128], F32), ones64=([64, 64], BF16),
        onesrow=([1, 128], BF16), cmask=([4, 128, QL, NCT, 128], BF16), smask=([4, 128, 4, 128], BF16),
        wmask=([4, 128, 8, 128], BF16), impA=([4, 128, QL, 128], F32), impB=([4, 128, QL, 128], F32),
        emat=([128, NT, 128], BF16), ovl=([128, NCT, 128], BF16))


def fused_consts(T):
    pc = proj_consts()
    mc = mix_consts()
    cc = [mix_consts_c(T, c) for c in range(4)]
    d = dict(ident=pc["ident"], identf=mc["identf"], blk=pc["blk"], triu=pc["triu"], trib=mc["trib"], triuf=mc["triuf"],
             onesf=mc["onesf"], ones64=cc[0]["ones64"], onesrow=cc[0]["onesrow"], emat=cc[0]["emat"], ovl=cc[0]["ovl"])
    for k in ("cmask", "smask", "wmask", "impA", "impB"):
        d[k] = np.ascontiguousarray(np.stack([cc[c][k] for c in range(4)], 0))
    return d


def emit_mix_fused(s, F, l, T):
    nc = s.nc
    NT = T // 128
    m = s.mark()
    io0 = dict(identb=F["ident"], identf=F["identf"], trib=F["trib"])
    misc_sb = s.sb("misc_sb", [128, NT, 16], F32)
    m_ab = s.mark()
    cm = mix_common(s, io0, n_ps=4, with_pf2=False)
    for j0 in range(0, NT, 8):
        j1 = min(NT, j0 + 8)
        s.dma("sp", misc_sb[:, j0:j1, :], F["misc"][j0 * 128:j1 * 128, :].rearrange("(j p) c -> p j c", p=128), writes=[misc_sb])
    zfm, vab, obr = F["zfm"], F["vab"], F["obr"]

    def io_a(h):
        r0 = (h % 2) * 64
        io = dict(io0)
        io.update(qa=zfm[h // 2][r0:r0 + 64, :], ka=zfm[2 + h // 2][r0:r0 + 64, :], va_src=vab[:, h * 64:(h + 1) * 64],
                  lamp=F["lamp"][l], lami=F["lami"][l], oa=obr[:, h * 64:(h + 1) * 64])
        return io

    def io_b(h):
        r0 = (h % 2) * 64
        io = dict(io0)
        io.update(qb=zfm[4 + h // 2][r0:r0 + 64, :], kb=zfm[6 + h // 2][r0:r0 + 64, :],
                  vb_src=vab[:, 256 + h * 64:256 + (h + 1) * 64], flog_sb=(misc_sb, misc_sb[:, :, h]), fbias=F["fbias"][l, h],
                  triuf=F["triuf"], onesf=F["onesf"], ob=obr[:, 256 + h * 64:256 + (h + 1) * 64])
        return io

    ba = emit_mix_a(s, cm, io_a(0), T, stage="alloc")
    bb = emit_mix_b(s, cm, io_b(0), T, stage="alloc")
    emit_mix_a(s, cm, io_a(0), T, stage="load", bufs=ba)
    emit_mix_b(s, cm, io_b(0), T, stage="load", bufs=bb)
    for h in range(4):
        emit_mix_a(s, cm, io_a(h), T, stage="compute", bufs=ba)
        if h + 1 < 4:
            emit_mix_a(s, cm, io_a(h + 1), T, stage="load", bufs=ba)
        emit_mix_b(s, cm, io_b(h), T, stage="compute", bufs=bb)
        if h + 1 < 4:
            emit_mix_b(s, cm, io_b(h + 1), T, stage="load", bufs=bb)
    s.release(m_ab)
    cm = mix_common(s, io0, n_ps=3, with_pf2=True)
    io = dict(io0)
    io.update(zq=(zfm[8], zfm[9]), kskw=zfm[10], kvin=zfm[11], vs_src=F["vsw"][:, 0:64], vw_src=F["vsw"][:, 64:128],
              misc_sb=(misc_sb, misc_sb), w1=F["nsa_phi_w1"][l], b1=F["b1l"][l], peT=F["peT"][l], w2=F["nsa_phi_w2"][l],
              b2=F["nsa_phi_b2"][l], b2c=F["b2c"][l], kgain=F["kgain"][l], oc=obr[:, 512:768])
    for k in ("cmask", "smask", "wmask", "impA", "impB", "emat", "ovl", "ones64", "onesrow"):
        io[k] = F[k]
    emit_mix_c(s, cm, io, T, cs=[0, 1, 2, 3])
    s.release(m)


def build_fused(T, L):
    nc = bass.Bass("TRN2", target_bir_lowering=False)
    F = {}
    F["x"] = dram_in(nc, "x", [T, D])
    for k, shp in PARAM_SHAPES.items():
        F[k] = dram_in(nc, k, [L if v == "L" else v for v in shp])
    for k, (shp, dt) in fused_const_shapes(T).items():
        F[k] = dram_in(nc, k, shp, dt)
    y = dram_out(nc, "y", [T, D])
    for k, shp, dt in (("xa", [T, D], F32), ("xb", [T, D], F32), ("xc", [T, D], F32), ("zfm", [NFM, 128, T], BF16),
                       ("vab", [T, 512], BF16), ("vsw", [T, 128], BF16), ("misc", [T, 16], F32), ("obr", [T, D], BF16)):
        F[k] = nc.dram_tensor("s_" + k, shp, dt).ap()
    s = S(nc)
    for l in range(L):
        x_in = F["x"] if l == 0 else F["xc"]
        emit_ffn(s, x_in, F["ffn1_norm"][l], F["ffn1_w_in"][l], F["ffn1_w_out"][l], F["ident"], F["xa"], T)
        a = dict(x=F["xa"], g=F["mix_norm"][l], w_in=F["w_in"][l], ident=F["ident"], gains=F["gains"][l], blk=F["blk"],
                 vgain=F["vgain"][l], wsT=F["wsT"][l], triu=F["triu"], bsT=F["bsT"][l], zfm=F["zfm"], vab=F["vab"],
                 vsw=F["vsw"], misc=F["misc"], od=F["obr"][:, 768:1024])
        emit_proj(s, a, T)
        emit_mix_fused(s, F, l, T)
        emit_merge(s, F["xa"], F["mix_norm"][l], F["w_in"][l], F["w_branch"][l], F["w_out"][l], F["ident"], F["obr"], F["xb"], T)
        x_out = y if l == L - 1 else F["xc"]
        emit_ffn(s, F["xb"], F["ffn2_norm"][l], F["ffn2_w_in"][l], F["ffn2_w_out"][l], F["ident"], x_out, T)
    s.finish()
    s.close()
    return nc


def fused_params(P, L):
    import math
    f32 = np.float32
    A = lambda a: np.ascontiguousarray(np.asarray(a, dtype=f32))
    d = {k: A(P[k]) for k in ("ffn1_norm", "ffn1_w_in", "ffn1_w_out", "mix_norm", "w_in", "nsa_phi_w1", "nsa_phi_w2",
                              "nsa_phi_b2", "w_branch", "w_out", "ffn2_norm", "ffn2_w_in", "ffn2_w_out")}
    tile = lambda v, n: np.tile(A(v), (1, n))
    d["gains"] = np.ascontiguousarray(np.stack([tile(P["diff_q_gain"], 4), tile(P["diff_k_gain"], 4), tile(P["fox_q_gain"], 2),
                                                tile(P["fox_k_gain"], 2), tile(P["nsa_q_gain"], 2), tile(P["nsa_k_gain"], 2)], 2))
    d["vgain"] = np.ascontiguousarray(np.broadcast_to(A(P["gmlp_v_gain"])[:, None, :], (L, 128, 256)))
    d["wsT"] = np.ascontiguousarray(A(P["gmlp_w_s"]).transpose(0, 1, 3, 2))
    d["bsT"] = np.ascontiguousarray(A(P["gmlp_b_s"]).transpose(0, 2, 1))
    d["lamp"] = np.ascontiguousarray(np.broadcast_to(A(P["diff_lambda"])[:, None], (L, 128, 4, 32)))
    li = np.array([[0.8 - 0.6 * math.exp(-0.3 * l), 1.0 - (0.8 - 0.6 * math.exp(-0.3 * l))] for l in range(L)], f32)
    d["lami"] = np.ascontiguousarray(np.broadcast_to(li[:, None, :], (L, 128, 2)))
    d["fbias"] = np.ascontiguousarray(np.broadcast_to(A(P["fox_f_bias"])[:, :, None, None], (L, 4, 128, 1)))
    d["b1l"] = np.ascontiguousarray(A(P["nsa_phi_b1"]).reshape(L, 2, 2, 128).transpose(0, 3, 1, 2).reshape(L, 128, 4))
    pe = A(P["nsa_cmp_pe"])
    d["peT"] = np.ascontiguousarray(pe.transpose(0, 1, 3, 2).reshape(L, 128, 32))
    d["b2c"] = np.ascontiguousarray(A(P["nsa_phi_b2"])[:, 0, :, None])
    d["kgain"] = np.ascontiguousarray(A(P["nsa_k_gain"])[:, :, None])
    return d


B_, T_, L_ = 2, 8192, 2
_PROG = {}


def kernel(**inputs):
    x = np.ascontiguousarray(np.asarray(inputs["x"], dtype=np.float32))
    if "fused" not in _PROG:
        _PROG["fused"] = build_fused(T_, L_)
        _PROG["consts"] = fused_consts(T_)
    nc = _PROG["fused"]
    par = fused_params(inputs, L_)
    in_maps = []
    for b in range(B_):
        d = dict(par)
        d.update(_PROG["consts"])
        d["x"] = x[b]
        in_maps.append(d)
    res = run_bass_kernel_spmd(nc, in_maps, core_ids=list(range(B_)))
    return np.stack([np.asarray(res.results[b]["y"], dtype=np.float32) for b in range(B_)], 0)
```

```python
import numpy as np
import concourse.bass as bass
import concourse.mybir as mybir
from concourse.bass_utils import run_bass_kernel_spmd

F32 = mybir.dt.float32
BF16 = mybir.dt.bfloat16
AF = mybir.ActivationFunctionType
ALU = mybir.AluOpType
AX = mybir.AxisListType

ENGS = ("pe", "act", "dve", "pool", "sp")


class Buf:
    __slots__ = ("name", "t", "w", "r", "dsem", "dcnt", "uid")
    _n = 0

    def __init__(self, name, t):
        Buf._n += 1
        self.uid = Buf._n
        self.name = name
        self.t = t
        self.w = None
        self.r = []
        self.dsem = {}
        self.dcnt = {}

    def __getitem__(self, idx):
        return self.t[idx]


class S:
    def __init__(self, nc, same_engine_sync=True):
        self.nc = nc
        self.e = {"pe": nc.tensor, "act": nc.scalar, "dve": nc.vector, "pool": nc.gpsimd, "sp": nc.sync}
        self.sem = {k: nc.alloc_semaphore("c_" + k) for k in ENGS}
        self.cnt = {k: 0 for k in ENGS}
        self.seen = {k: {} for k in ENGS}
        self.same = same_engine_sync
        self.nbuf = 0
        self.nsem = 0
        self.dma_sems = []
        self.ctx = []
        self.cbufs = []
        self.free_dsems = {"hw": [], "sw": []}

    def sb(self, name, shape, dt):
        self.nbuf += 1
        g = self.nc.sbuf_tensor("%s_%d" % (name, self.nbuf), list(shape), dt)
        t = g.__enter__()
        self.ctx.append(g)
        b = Buf(name, t)
        self.cbufs.append(b)
        return b

    def ps(self, name, shape, dt):
        self.nbuf += 1
        g = self.nc.psum_tensor("%s_%d" % (name, self.nbuf), list(shape), dt)
        t = g.__enter__()
        self.ctx.append(g)
        b = Buf(name, t)
        self.cbufs.append(b)
        return b

    def sub(self, name, ap):
        return Buf(name, ap)

    def mark(self):
        return len(self.ctx)

    def release(self, m):
        self.barrier()
        while len(self.ctx) > m:
            self.ctx.pop().__exit__(None, None, None)
            b = self.cbufs.pop()
            for kind, sem in b.dsem.items():
                self.free_dsems[kind].append((sem, b.dcnt[kind]))
                self.dma_sems.remove((b, kind))
            b.dsem = {}

    def close(self):
        for g in reversed(self.ctx):
            g.__exit__(None, None, None)
        self.ctx = []
        self.cbufs = []

    def _need(self, E, deps):
        need = {}
        for d in deps:
            if d is None:
                continue
            if d[0] == "dma":
                b, kind = d[1], d[2]
                if kind not in b.dsem:
                    continue
                key = ("dma", b.uid, kind)
                need[key] = ((b, kind), b.dcnt[kind])
            else:
                F, c = d
                if F == E and (not self.same or E == "pe" or c > self.cnt[E]):
                    continue
                if c > need.get(F, (None, 0))[1]:
                    need[F] = (None, c)
        for key, (bk, c) in need.items():
            if self.seen[E].get(key, 0) >= c:
                continue
            self.seen[E][key] = c
            if bk is not None:
                self.e[E].wait_ge(bk[0].dsem[bk[1]], c)
            else:
                self.e[E].wait_ge(self.sem[key], c)

    def op(self, E, fn, reads=(), writes=(), inc=True):
        deps = []
        for b in reads:
            deps.append(b.w)
        for b in writes:
            deps.append(b.w)
            deps.extend(b.r)
        self._need(E, deps)
        ins = fn()
        c = self.cnt[E] + 1
        if inc:
            ins.then_inc(self.sem[E], 1)
            self.cnt[E] = c
        for b in writes:
            b.w = (E, c)
            b.r = []
        for b in reads:
            if b not in writes:
                b.r = [x for x in b.r if x[0] != E] + [(E, c)]
        return ins

    def _dsem(self, owner, kind):
        if kind not in owner.dsem:
            if self.free_dsems[kind]:
                sem, cnt = self.free_dsems[kind].pop()
            else:
                self.nsem += 1
                sem, cnt = self.nc.alloc_semaphore("d%s_%d" % (kind, self.nsem)), 0
            owner.dsem[kind] = sem
            owner.dcnt[kind] = cnt
            self.dma_sems.append((owner, kind))
        return owner.dsem[kind]

    def dma(self, Q, out, in_, reads=(), writes=(), **kw):
        deps = []
        for b in reads:
            deps.append(b.w)
        for b in writes:
            deps.append(b.w)
            deps.extend(b.r)
        self._need(Q, deps)
        owner = (list(writes) + list(reads))[0]
        kind = "sw" if Q == "pool" else "hw"
        sem = self._dsem(owner, kind)
        ins = self.e[Q].dma_start(out=out, in_=in_, **kw)
        ins.then_inc(sem, 16)
        owner.dcnt[kind] += 16
        rec = ("dma", owner, kind)
        for b in writes:
            b.w = rec
            b.r = []
        for b in reads:
            if b not in writes:
                b.r = [x for x in b.r if not (x[0] == "dma" and x[1] is owner and x[2] == kind)] + [rec]
        return ins

    def barrier(self):
        for E in ENGS:
            for Fk in ENGS:
                if Fk == E:
                    continue
                c = self.cnt[Fk]
                if c and self.seen[E].get(Fk, 0) < c:
                    self.seen[E][Fk] = c
                    self.e[E].wait_ge(self.sem[Fk], c)
            for (b, kind) in self.dma_sems:
                key = ("dma", b.uid, kind)
                c = b.dcnt[kind]
                if c and self.seen[E].get(key, 0) < c:
                    self.seen[E][key] = c
                    self.e[E].wait_ge(b.dsem[kind], c)

    def finish(self):
        self.barrier()


D = 1024
DFF = 2816
NFC = DFF // 128
NKC = D // 128
EPS = 1e-6


def dram_in(nc, name, shape, dt=F32):
    return nc.dram_tensor(name, list(shape), dt, kind="ExternalInput").ap()


def dram_out(nc, name, shape, dt=F32):
    return nc.dram_tensor(name, list(shape), dt, kind="ExternalOutput").ap()


class RR:
    def __init__(self, items):
        self.items = items
        self.i = 0

    def next(self):
        b = self.items[self.i % len(self.items)]
        self.i += 1
        return b


def emit_norm(s, epsc, xt, hb_rr, scr, stat_rr, ntile):
    nc = s.nc
    st = stat_rr.next()
    hbs = [hb_rr.next() for _ in range(ntile)]
    for j in range(ntile):
        s.op("act", lambda: nc.scalar.activation(out=hbs[j][:, :], in_=xt[j][:, :], func=AF.Square, scale=1.0 / 32.0,
                                                 accum_out=st[:, j:j + 1]),
             reads=[xt[j]], writes=[hbs[j], st])
    s.op("act", lambda: nc.scalar.activation(out=st[:, 4:4 + ntile], in_=st[:, 0:ntile], func=AF.Ln, bias=epsc[:, 0:1], scale=1.0),
         reads=[st, epsc], writes=[st])
    s.op("act", lambda: nc.scalar.activation(out=st[:, 8:8 + ntile], in_=st[:, 4:4 + ntile], func=AF.Exp, scale=-0.5),
         reads=[st], writes=[st])
    for j in range(ntile):
        s.op("dve", lambda: nc.vector.tensor_scalar(out=hbs[j][:, :], in0=xt[j][:, :], scalar1=st[:, 8 + j:9 + j], scalar2=None,
                                                    op0=ALU.mult), reads=[xt[j], st], writes=[hbs[j]])
    return hbs


def emit_transpose_T(s, hbs, gcol, hT, ident_b, pT_rr, ntile, evac_engs=("dve", "act")):
    nc = s.nc
    k = 0
    for c in range(NKC):
        pT = pT_rr.next()
        for j in range(ntile):
            s.op("pe", lambda: nc.tensor.transpose(out=pT[:, j * 128:(j + 1) * 128], in_=hbs[j][:, c * 128:(c + 1) * 128],
                                                   identity=ident_b[:, :]),
                 reads=[hbs[j], ident_b], writes=[pT], inc=(j == ntile - 1))
        eng = evac_engs[k % len(evac_engs)]
        k += 1
        if eng == "dve":
            s.op("dve", lambda: nc.vector.tensor_scalar(out=hT[:, c, 0:ntile * 128], in0=pT[:, 0:ntile * 128],
                                                        scalar1=gcol[:, c:c + 1], scalar2=None, op0=ALU.mult),
                 reads=[pT, gcol], writes=[hT])
        else:
            s.op("act", lambda: nc.scalar.activation(out=hT[:, c, 0:ntile * 128], in_=pT[:, 0:ntile * 128],
                                                     func=AF.Copy, scale=gcol[:, c:c + 1]),
                 reads=[pT, gcol], writes=[hT])


def emit_rmsnorm_T(s, epsc, xt, gcol, hT, ident_b, pT_rr, hb_rr, scr, stat_rr, ntile, evac_engs=("dve", "act")):
    hbs = emit_norm(s, epsc, xt, hb_rr, scr, stat_rr, ntile)
    emit_transpose_T(s, hbs, gcol, hT, ident_b, pT_rr, ntile, evac_engs)


def build_ffn(NT, TB=512):
    nc = bass.Bass("TRN2", target_bir_lowering=False)
    x = dram_in(nc, "x", [NT, D])
    g = dram_in(nc, "g", [D])
    w_in = dram_in(nc, "w_in", [D, 2 * DFF])
    w_out = dram_in(nc, "w_out", [DFF, D])
    ident = dram_in(nc, "ident", [128, 128], BF16)
    y = dram_out(nc, "y", [NT, D])
    s = S(nc)
    emit_ffn(s, x, g, w_in, w_out, ident, y, NT, TB)
    s.finish()
    s.close()
    return nc


def emit_ffn(s, x, g, w_in, w_out, ident, y, NT, TB=512):
    nc = s.nc
    ntile = TB // 128
    m_ = s.mark()
    ident_b = s.sb("ident_b", [128, 128], BF16)
    s.dma("sp", ident_b[:, :], ident[:, :], writes=[ident_b])
    gcol = s.sb("gcol", [128, NKC], F32)
    epsc = s.sb("epsc", [128, 1], F32)
    s.op("dve", lambda: nc.vector.memset(epsc[:, :], EPS), writes=[epsc])
    s.dma("sp", gcol[:, :], g.rearrange("(c p) -> p c", p=128), writes=[gcol], allow_slow_non_contiguous=True)
    win_b = [s.sb("win_b%d" % c, [128, 2 * DFF], BF16) for c in range(NKC)]
    wout_b = [s.sb("wout_b%d" % f, [128, D], BF16) for f in range(NFC)]
    for c in range(NKC):
        for hf in range(2):
            s.dma("pool", win_b[c][:, hf * DFF:(hf + 1) * DFF], w_in[c * 128:(c + 1) * 128, hf * DFF:(hf + 1) * DFF],
                  writes=[win_b[c]])
    for f in range(NFC):
        s.dma("pool", wout_b[f][:, :], w_out[f * 128:(f + 1) * 128, :], writes=[wout_b[f]])
    xn = [s.sb("xn%d" % j, [128, D], F32) for j in range(ntile)]
    xr_rr = RR([s.sb("xr%d" % j, [128, D], F32) for j in range(1)])
    hb_rr = RR([s.sb("hb%d" % j, [128, D], BF16) for j in range(2 * ntile)])
    stat_rr = RR([s.sb("st%d" % j, [128, 16], F32) for j in range(4)])
    hT = s.sb("hT", [128, NKC, TB], BF16)
    pT_rr = RR([s.ps("pT%d" % j, [128, TB], BF16) for j in range(2)])
    pa_rr = RR([s.ps("pa%d" % j, [128, TB], F32) for j in range(2)])
    pb_rr = RR([s.ps("pb%d" % j, [128, TB], F32) for j in range(2)])
    po_rr = RR([s.ps("po%d" % j, [128, 512], F32) for j in range(2)])
    sa_rr = RR([s.sb("sa%d" % j, [128, TB], F32) for j in range(2)])
    act = [s.sb("actT%d" % f, [128, TB], BF16) for f in range(NFC)]
    nblk = NT // TB

    def prep_a_rot(tb):
        for j in range(ntile):
            r0 = tb * TB + j * 128
            s.dma("sp", xn[j][:, :], x[r0:r0 + 128, :], writes=[xn[j]])
        return emit_norm(s, epsc, xn, hb_rr, None, stat_rr, ntile)

    hbs_next = prep_a_rot(0)
    emit_transpose_T(s, hbs_next, gcol, hT, ident_b, pT_rr, ntile)
    for tb in range(nblk):
        if tb + 1 < nblk:
            hbs_next = prep_a_rot(tb + 1)
        for f in range(NFC):
            pa = pa_rr.next()
            pb = pb_rr.next()
            for c in range(NKC):
                s.op("pe", lambda: nc.tensor.matmul(pa[:, :], lhsT=win_b[c][:, f * 128:(f + 1) * 128], rhs=hT[:, c, :],
                                                    start=(c == 0), stop=(c == NKC - 1)),
                     reads=[win_b[c], hT], writes=[pa], inc=(c == NKC - 1))
            for c in range(NKC):
                s.op("pe", lambda: nc.tensor.matmul(pb[:, :], lhsT=win_b[c][:, DFF + f * 128:DFF + (f + 1) * 128],
                                                    rhs=hT[:, c, :], start=(c == 0), stop=(c == NKC - 1)),
                     reads=[win_b[c], hT], writes=[pb], inc=(c == NKC - 1))
            sa = sa_rr.next()
            s.op("act", lambda: nc.scalar.activation(out=sa[:, :], in_=pa[:, :], func=AF.Silu), reads=[pa], writes=[sa])
            s.op("dve", lambda: nc.vector.tensor_tensor(out=act[f][:, :], in0=sa[:, :], in1=pb[:, :], op=ALU.mult),
                 reads=[sa, pb], writes=[act[f]])
        if tb + 1 < nblk:
            emit_transpose_T(s, hbs_next, gcol, hT, ident_b, pT_rr, ntile)
        for j in range(ntile):
            r0 = tb * TB + j * 128
            xr = xr_rr.next()
            s.dma("sp", xr[:, :], x[r0:r0 + 128, :], writes=[xr])
            for hf in range(2):
                po = po_rr.next()
                for f in range(NFC):
                    s.op("pe", lambda: nc.tensor.matmul(po[:, :], lhsT=act[f][:, j * 128:(j + 1) * 128],
                                                        rhs=wout_b[f][:, hf * 512:(hf + 1) * 512],
                                                        start=(f == 0), stop=(f == NFC - 1)),
                         reads=[act[f], wout_b[f]], writes=[po], inc=(f == NFC - 1))
                s.op("dve", lambda: nc.vector.scalar_tensor_tensor(out=xr[:, hf * 512:(hf + 1) * 512], in0=po[:, :],
                                                                   scalar=0.5, in1=xr[:, hf * 512:(hf + 1) * 512],
                                                                   op0=ALU.mult, op1=ALU.add),
                     reads=[po, xr], writes=[xr])
            s.dma("sp", y[r0:r0 + 128, :], xr[:, :], reads=[xr])
    s.release(m_)


FM_SRC = [[(0, 128)], [(128, 128)], [(256, 128)], [(384, 128)],
          [(768, 128)], [(896, 128)], [(1024, 128)], [(1152, 128)],
          [(1540, 128)], [(1668, 128)], [(1924, 64), (2052, 64)], [(1796, 128)]]
FM_GCOL = [0, 0, 1, 1, 2, 2, 3, 3, 4, 4, 5, None]
FM_BLK = [0, 0, 0, 0, 1, 1, 1, 1, 1, 1, 1, None]
TM_SRC = [[(512, 256), (1280, 256)],
          [(1536, 4), (2180, 12), (1988, 64), (2116, 64)],
          [(2192, 512)]]
NFM = 12
GELU_C = 1.5957691216057308


def build_proj(NT, TB=512):
    nc = bass.Bass("TRN2", target_bir_lowering=False)
    a = dict(
        x=dram_in(nc, "x", [NT, D]), g=dram_in(nc, "g", [D]), w_in=dram_in(nc, "w_in", [D, 6800]),
        ident=dram_in(nc, "ident", [128, 128], BF16), gains=dram_in(nc, "gains", [128, 6]),
        blk=dram_in(nc, "blk", [2, 128, 128], BF16), vgain=dram_in(nc, "vgain", [128, 256]),
        wsT=dram_in(nc, "wsT", [4, 128, 128]), triu=dram_in(nc, "triu", [128, 128]), bsT=dram_in(nc, "bsT", [128, 4]),
        zfm=dram_out(nc, "zfm", [NFM, 128, NT], BF16), vab=dram_out(nc, "vab", [NT, 512], BF16),
        vsw=dram_out(nc, "vsw", [NT, 128], BF16), misc=dram_out(nc, "misc", [NT, 16]), od=dram_out(nc, "od", [NT, 256], BF16))
    s = S(nc)
    emit_proj(s, a, NT, TB)
    s.finish()
    s.close()
    return nc


def emit_proj(s, a, NT, TB=512):
    nc = s.nc
    x, g, w_in, ident, gains, blk, vgain, wsT, triu, bsT = (a[k] for k in
                                                            ("x", "g", "w_in", "ident", "gains", "blk", "vgain", "wsT", "triu", "bsT"))
    zfm, vab, vsw, misc, od = (a[k] for k in ("zfm", "vab", "vsw", "misc", "od"))
    m_ = s.mark()
    ntile = TB // 128
    ident_b = s.sb("ident_b", [128, 128], BF16)
    s.dma("sp", ident_b[:, :], ident[:, :], writes=[ident_b])
    gcol = s.sb("gcol", [128, NKC], F32)
    s.dma("sp", gcol[:, :], g.rearrange("(c p) -> p c", p=128), writes=[gcol], allow_slow_non_contiguous=True)
    epsc = s.sb("epsc", [128, 1], F32)
    s.op("dve", lambda: nc.vector.memset(epsc[:, :], EPS), writes=[epsc])
    gn = s.sb("gn", [128, 6], F32)
    s.dma("sp", gn[:, :], gains[:, :], writes=[gn])
    for col, sc in ((0, 32.0 ** -0.5), (2, 0.125), (4, 0.125)):
        s.op("dve", lambda: nc.vector.tensor_scalar(out=gn[:, col:col + 1], in0=gn[:, col:col + 1], scalar1=sc,
                                                    scalar2=None, op0=ALU.mult), reads=[gn], writes=[gn])
    blk_b = [s.sb("blk%d" % i, [128, 128], BF16) for i in range(2)]
    for i in range(2):
        s.dma("sp", blk_b[i][:, :], blk[i], writes=[blk_b[i]])
    vg = s.sb("vg", [128, 256], F32)
    s.dma("sp", vg[:, :], vgain[:, :], writes=[vg])
    bcol = s.sb("bcol", [128, 4], F32)
    s.dma("sp", bcol[:, :], bsT[:, :], writes=[bcol])
    tri = s.sb("tri", [128, 128], F32)
    s.dma("sp", tri[:, :], triu[:, :], writes=[tri])
    wm = []
    wtmp = s.sb("wtmp", [128, 128], F32)
    for gi in range(4):
        w = s.sb("wm%d" % gi, [128, 128], BF16)
        s.dma("sp", wtmp[:, :], wsT[gi], writes=[wtmp])
        s.op("dve", lambda: nc.vector.tensor_tensor(out=w[:, :], in0=wtmp[:, :], in1=tri[:, :], op=ALU.mult),
             reads=[wtmp, tri], writes=[w])
        wm.append(w)
    wfm = [s.sb("wfm%d" % c, [128, NFM * 128], BF16) for c in range(NKC)]
    wtm = [s.sb("wtm%d" % c, [128, 1168], BF16) for c in range(NKC)]
    FM_RUNS = [(0, 0, 512), (512, 768, 512), (1024, 1540, 256), (1280, 1924, 64), (1344, 2052, 64), (1408, 1796, 128)]
    TM_RUNS = [(0, 512, 256), (256, 1280, 256), (512, 1536, 4), (516, 2180, 12), (528, 1988, 64), (592, 2116, 64), (656, 2192, 512)]
    for c in range(NKC):
        for (o, c0, n) in FM_RUNS:
            s.dma("pool", wfm[c][:, o:o + n], w_in[c * 128:(c + 1) * 128, c0:c0 + n], writes=[wfm[c]])
        for (o, c0, n) in TM_RUNS:
            s.dma("pool", wtm[c][:, o:o + n], w_in[c * 128:(c + 1) * 128, c0:c0 + n], writes=[wtm[c]])
    xts = [[s.sb("xt%d_%d" % (k, j), [128, D], F32) for j in range(ntile)] for k in range(2)]
    hb_rr = RR([s.sb("hb%d" % j, [128, D], BF16) for j in range(2 * ntile)])
    scr = s.sb("scr", [128, D], BF16)
    stat_rr = RR([s.sb("st%d" % j, [128, 16], F32) for j in range(4)])
    hTs = [s.sb("hT%d" % k, [128, NKC, TB], BF16) for k in range(2)]
    pT_rr = RR([s.ps("pT%d" % j, [128, TB], BF16) for j in range(2)])
    pz_rr = RR([s.ps("pz%d" % j, [128, 512], F32) for j in range(2)])
    ptm_rr = RR([s.ps("ptm%d" % j, [128, 512], F32) for j in range(2)])
    pq_rr = RR([s.ps("pq%d" % j, [128, 512], F32) for j in range(2)])
    sq_rr = RR([s.sb("sq%d" % j, [128, TB], BF16) for j in range(4)])
    rs_rr = RR([s.sb("rs%d" % j, [128, TB], F32) for j in range(2)])
    zo_rr = RR([s.sb("zo%d" % j, [128, TB], BF16) for j in range(4)])
    vab_rr = RR([s.sb("vabt%d" % j, [128, 512], BF16) for j in range(2)])
    vsw_rr = RR([s.sb("vswt%d" % j, [128, 128], BF16) for j in range(2)])
    msc_rr = RR([s.sb("msct%d" % j, [128, 16], F32) for j in range(2)])
    f_rr = RR([s.sb("gf%d" % j, [128, 512], F32) for j in range(4)])
    ge_rr = RR([s.sb("ge%d" % j, [128, 512], F32) for j in range(3)])
    zs_rr = RR([s.sb("zs%d" % j, [128, 512], F32) for j in range(2)])
    zf_rr = RR([s.sb("zf%d" % j, [128, TB], F32) for j in range(3)])
    vn_rr = RR([s.sb("vn%d" % j, [128, 256], BF16) for j in range(3)])
    od_rr = RR([s.sb("odt%d" % j, [128, 256], BF16) for j in range(2)])
    nblk = NT // TB

    def prep(tb):
        xt = xts[tb % 2]
        for j in range(ntile):
            s.dma("sp", xt[j][:, :], x[tb * TB + j * 128:tb * TB + (j + 1) * 128, :], writes=[xt[j]])
        emit_rmsnorm_T(s, epsc, xt, gcol, hTs[tb % 2], ident_b, pT_rr, hb_rr, scr, stat_rr, ntile)

    prep(0)
    pipe = Pipe(2)
    for tb in range(nblk):
        t0 = tb * TB
        hT = hTs[tb % 2]
        for i in range(NFM):
            pz = pz_rr.next()
            for c in range(NKC):
                s.op("pe", lambda: nc.tensor.matmul(pz[:, 0:TB], lhsT=wfm[c][:, i * 128:(i + 1) * 128], rhs=hT[:, c, :],
                                                    start=(c == 0), stop=(c == NKC - 1)),
                     reads=[wfm[c], hT], writes=[pz], inc=(c == NKC - 1))
            zo = zo_rr.next()
            if FM_GCOL[i] is None:
                s.op("dve", lambda: nc.vector.tensor_copy(out=zo[:, :], in_=pz[:, 0:TB]), reads=[pz], writes=[zo])
                s.dma("sp", zfm[i, :, t0:t0 + TB], zo[:, :], reads=[zo])
            else:
                zf = zf_rr.next()
                s.op("dve", lambda: nc.vector.tensor_copy(out=zf[:, :], in_=pz[:, 0:TB]), reads=[pz], writes=[zf])
                sq = sq_rr.next()
                s.op("act", lambda: nc.scalar.activation(out=sq[:, :], in_=zf[:, :], func=AF.Square),
                     reads=[zf], writes=[sq])

                def back(i=i, zf=zf, sq=sq, zo=zo, t0=t0):
                    gs = 32.0 if FM_BLK[i] == 0 else 64.0
                    pq = pq_rr.next()
                    s.op("pe", lambda: nc.tensor.matmul(pq[:, 0:TB], lhsT=blk_b[FM_BLK[i]][:, :], rhs=sq[:, :],
                                                        start=True, stop=True), reads=[blk_b[FM_BLK[i]], sq], writes=[pq])
                    rs = rs_rr.next()
                    s.op("act", lambda: nc.scalar.activation(out=rs[:, :], in_=pq[:, 0:TB], func=AF.Ln, bias=epsc[:, 0:1],
                                                             scale=1.0 / gs), reads=[pq, epsc], writes=[rs])
                    s.op("act", lambda: nc.scalar.activation(out=rs[:, :], in_=rs[:, :], func=AF.Exp, scale=-0.5),
                         reads=[rs], writes=[rs])
                    gc = FM_GCOL[i]
                    s.op("dve", lambda: nc.vector.scalar_tensor_tensor(out=zo[:, :], in0=zf[:, :], scalar=gn[:, gc:gc + 1],
                                                                       in1=rs[:, :], op0=ALU.mult, op1=ALU.mult),
                         reads=[zf, gn, rs], writes=[zo])
                    s.dma("sp", zfm[i, :, t0:t0 + TB], zo[:, :], reads=[zo])
                pipe.push(back)
        if tb + 1 < nblk:
            prep(tb + 1)
        for j in range(ntile):
            r0 = t0 + j * 128
            pz = ptm_rr.next()
            for c in range(NKC):
                s.op("pe", lambda: nc.tensor.matmul(pz[:, :], lhsT=hT[:, c, j * 128:(j + 1) * 128], rhs=wtm[c][:, 0:512],
                                                    start=(c == 0), stop=(c == NKC - 1)),
                     reads=[wtm[c], hT], writes=[pz], inc=(c == NKC - 1))
            vt = vab_rr.next()
            s.op("act", lambda: nc.scalar.copy(out=vt[:, :], in_=pz[:, :]), reads=[pz], writes=[vt])
            s.dma("sp", vab[r0:r0 + 128, :], vt[:, :], reads=[vt])
            pz = ptm_rr.next()
            for c in range(NKC):
                s.op("pe", lambda: nc.tensor.matmul(pz[:, 0:144], lhsT=hT[:, c, j * 128:(j + 1) * 128], rhs=wtm[c][:, 512:656],
                                                    start=(c == 0), stop=(c == NKC - 1)),
                     reads=[wtm[c], hT], writes=[pz], inc=(c == NKC - 1))
            mt = msc_rr.next()
            vs_ = vsw_rr.next()
            s.op("dve", lambda: nc.vector.tensor_copy(out=mt[:, :], in_=pz[:, 0:16]), reads=[pz], writes=[mt])
            s.op("dve", lambda: nc.vector.tensor_copy(out=vs_[:, :], in_=pz[:, 16:144]), reads=[pz], writes=[vs_])
            s.dma("sp", misc[r0:r0 + 128, :], mt[:, :], reads=[mt])
            s.dma("sp", vsw[r0:r0 + 128, :], vs_[:, :], reads=[vs_])
            pz = ptm_rr.next()
            for c in range(NKC):
                s.op("pe", lambda: nc.tensor.matmul(pz[:, :], lhsT=hT[:, c, j * 128:(j + 1) * 128], rhs=wtm[c][:, 656:1168],
                                                    start=(c == 0), stop=(c == NKC - 1)),
                     reads=[wtm[c], hT], writes=[pz], inc=(c == NKC - 1))
            zs = zs_rr.next()
            s.op("act", lambda: nc.scalar.copy(out=zs[:, :], in_=pz[:, :]), reads=[pz], writes=[zs])
            z2 = f_rr.next()
            s.op("act", lambda: nc.scalar.activation(out=z2[:, :], in_=zs[:, :], func=AF.Square), reads=[zs], writes=[z2])
            s.op("dve", lambda: nc.vector.tensor_scalar(out=z2[:, :], in0=z2[:, :], scalar1=0.044715, scalar2=1.0,
                                                        op0=ALU.mult, op1=ALU.add), reads=[z2], writes=[z2])
            s.op("dve", lambda: nc.vector.tensor_tensor(out=z2[:, :], in0=z2[:, :], in1=zs[:, :], op=ALU.mult),
                 reads=[z2, zs], writes=[z2])
            s.op("act", lambda: nc.scalar.activation(out=z2[:, :], in_=z2[:, :], func=AF.Exp, scale=-GELU_C),
                 reads=[z2], writes=[z2])
            s.op("act", lambda: nc.scalar.activation(out=z2[:, :], in_=z2[:, :], func=AF.Ln, bias=1.0, scale=1.0),
                 reads=[z2], writes=[z2])
            s.op("act", lambda: nc.scalar.activation(out=z2[:, :], in_=z2[:, :], func=AF.Exp, scale=-1.0),
                 reads=[z2], writes=[z2])
            ge = ge_rr.next()
            s.op("dve", lambda: nc.vector.tensor_tensor(out=ge[:, :], in0=z2[:, :], in1=zs[:, :], op=ALU.mult),
                 reads=[z2, zs], writes=[ge])
            sqv = f_rr.next()
            st = stat_rr.next()
            s.op("act", lambda: nc.scalar.activation(out=sqv[:, 0:256], in_=ge[:, 256:512], func=AF.Square),
                 reads=[ge], writes=[sqv])
            s.op("dve", lambda: nc.vector.tensor_reduce(out=st[:, 0:4], in_=sqv[:, 0:256].rearrange("p (g d) -> p g d", g=4),
                                                        axis=AX.X, op=ALU.add), reads=[sqv], writes=[st])
            s.op("act", lambda: nc.scalar.activation(out=st[:, 0:4], in_=st[:, 0:4], func=AF.Ln, bias=epsc[:, 0:1],
                                                     scale=1.0 / 64.0), reads=[st, epsc], writes=[st])
            s.op("act", lambda: nc.scalar.activation(out=st[:, 0:4], in_=st[:, 0:4], func=AF.Exp, scale=-0.5),
                 reads=[st], writes=[st])
            vn = vn_rr.next()
            for gi in range(4):
                s.op("dve", lambda: nc.vector.scalar_tensor_tensor(
                    out=vn[:, gi * 64:(gi + 1) * 64], in0=ge[:, 256 + gi * 64:256 + (gi + 1) * 64], scalar=st[:, gi:gi + 1],
                    in1=vg[:, gi * 64:(gi + 1) * 64], op0=ALU.mult, op1=ALU.mult), reads=[ge, st, vg], writes=[vn])

            def back2(vn=vn, ge=ge, r0=r0):
                pq = pq_rr.next()
                for gi in range(4):
                    s.op("pe", lambda: nc.tensor.matmul(pq[:, gi * 64:(gi + 1) * 64], lhsT=wm[gi][:, :],
                                                        rhs=vn[:, gi * 64:(gi + 1) * 64], start=True, stop=True),
                         reads=[wm[gi], vn], writes=[pq], inc=(gi == 3))
                ot = od_rr.next()
                for gi in range(4):
                    s.op("dve", lambda: nc.vector.scalar_tensor_tensor(
                        out=ot[:, gi * 64:(gi + 1) * 64], in0=pq[:, gi * 64:(gi + 1) * 64], scalar=bcol[:, gi:gi + 1],
                        in1=ge[:, gi * 64:(gi + 1) * 64], op0=ALU.add, op1=ALU.mult), reads=[pq, bcol, ge], writes=[ot])
                s.dma("sp", od[r0:r0 + 128, :], ot[:, :], reads=[ot])
            pipe.push(back2)
    pipe.flush()
    s.release(m_)


def _bf(a):
    import ml_dtypes
    return np.ascontiguousarray(a).astype(ml_dtypes.bfloat16)


def proj_consts():
    blk = np.zeros((2, 128, 128), np.float32)
    for i in range(128):
        for j in range(128):
            if i // 32 == j // 32:
                blk[0, i, j] = 1
            if i // 64 == j // 64:
                blk[1, i, j] = 1
    triu = np.triu(np.ones((128, 128), np.float32))
    return dict(ident=_bf(np.eye(128, dtype=np.float32)), blk=_bf(blk), triu=triu)


def proj_params(g, w_in, dq, dk, fq, fk, nq, nk, vgain, w_s, b_s):
    gains = np.stack([np.tile(dq, 4), np.tile(dk, 4), np.tile(fq, 2), np.tile(fk, 2), np.tile(nq, 2), np.tile(nk, 2)], 1)
    return dict(g=np.ascontiguousarray(g), w_in=np.ascontiguousarray(w_in), gains=np.ascontiguousarray(gains, dtype=np.float32),
                vgain=np.ascontiguousarray(np.broadcast_to(vgain[None, :], (128, 256))),
                wsT=np.ascontiguousarray(w_s.transpose(0, 2, 1)), bsT=np.ascontiguousarray(b_s.T))


NEG = -30000.0


def load_vt(s, vt, io, key, T, init=True, load=True):
    nc = s.nc
    NT = T // 128
    if init:
        s.op("pool", lambda: nc.gpsimd.memset(vt[:, :, 64:128], 0.0), writes=[vt])
        s.op("pool", lambda: nc.gpsimd.memset(vt[:, :, 64:65], 1.0), writes=[vt])
    if not load:
        return
    if key + "_src" in io:
        src = io[key + "_src"]
        step = 8
        for j0 in range(0, NT, step):
            j1 = min(NT, j0 + step)
            s.dma("sp", vt[:, j0:j1, 0:64], src[j0 * 128:j1 * 128, :].rearrange("(j p) d -> p j d", p=128), writes=[vt])
    else:
        s.dma("sp", vt[:, :, 0:64], io[key][:, :, 0:64], writes=[vt])


class Pipe:
    def __init__(self, lag):
        self.q = []
        self.lag = lag

    def push(self, fn):
        self.q.append(fn)
        while len(self.q) > self.lag:
            self.q.pop(0)()

    def flush(self):
        while self.q:
            self.q.pop(0)()


def emit_attn_phase(s, cm, T, nsub, qT, kT, vt, kparts, out_dram, finalize, bias_fn=None, name="a", lag=2):
    nc = s.nc
    NQB = T // 512
    pipe = Pipe(lag)
    fin_pending = None
    for qb in range(NQB):
        q0 = qb * 512
        pos = [cm["po_rr"].next() for _ in range(nsub)]
        nt = 4 * qb + 4
        njob = 0
        for t in range(nt):
            di = t - 4 * qb
            c0 = 128 * di if di > 0 else 0
            for i in range(nsub):
                kz = kT[i]
                ps = cm["ps_rr"].next()
                s.op("pe", lambda: nc.tensor.matmul(ps[:, c0:512], lhsT=kz[:, t * 128:(t + 1) * 128],
                                                    rhs=qT[:, q0 + c0:q0 + 512], start=True, stop=(di < 0)),
                     reads=[kz, qT], writes=[ps], inc=(di < 0))
                if di >= 0:
                    s.op("pe", lambda: nc.tensor.matmul(ps[:, c0:c0 + 128], lhsT=cm["ident_b"][:, :], rhs=cm["tri_b"][:, :],
                                                        start=False, stop=True),
                         reads=[cm["ident_b"], cm["tri_b"]], writes=[ps])
                pt = cm["pt_rr"].next()
                if bias_fn is None:
                    s.op("act", lambda: nc.scalar.activation(out=pt[:, c0:512], in_=ps[:, c0:512], func=AF.Exp),
                         reads=[ps], writes=[pt])
                else:
                    bb, bap = bias_fn(qb, t)
                    s.op("act", lambda: nc.scalar.activation(out=pt[:, c0:512], in_=ps[:, c0:512], func=AF.Exp, bias=bap),
                         reads=[ps, bb], writes=[pt])

                def pv(po=pos[i], t=t, c0=c0, pt=pt, nt=nt):
                    s.op("pe", lambda: nc.tensor.matmul(po[:, c0:512], lhsT=vt[:, t, :], rhs=pt[:, c0:512],
                                                        start=(t == 0), stop=(t == nt - 1)),
                         reads=[vt, pt], writes=[po])
                pipe.push(pv)
                njob += 1
                if fin_pending is not None and njob == lag:
                    fin_pending()
                    fin_pending = None
        if fin_pending is not None:
            pipe.flush()
            fin_pending()
        fin_pending = (lambda qb=qb, pos=pos: finalize(qb, pos))
    pipe.flush()
    if fin_pending is not None:
        fin_pending()


def emit_o_to_tokmajor(s, cm, po, pf, col0):
    nc = s.nc
    oc = cm["oc_rr"].next()
    s.op("dve", lambda: nc.vector.tensor_copy(out=oc[0:65, :], in_=po[0:65, :]), reads=[po], writes=[oc])
    for j in range(4):
        s.op("pe", lambda: nc.tensor.transpose(out=pf[:, j, col0:col0 + 65], in_=oc[0:65, j * 128:(j + 1) * 128],
                                               identity=cm["ident_f"][0:65, 0:65]),
             reads=[oc, cm["ident_f"]], writes=[pf], inc=(j == 3))


def build_mix_ab(T):
    nc = bass.Bass("TRN2", target_bir_lowering=False)
    io = mix_decl(nc, T, with_c=False)
    s = S(nc)
    cm = mix_common(s, io)
    emit_mix_a(s, cm, io, T)
    emit_mix_b(s, cm, io, T)
    s.finish()
    s.close()
    return nc


def mix_decl(nc, T, with_c=True):
    NT = T // 128
    io = dict(
        identb=dram_in(nc, "identb", [128, 128], BF16), identf=dram_in(nc, "identf", [128, 128]),
        trib=dram_in(nc, "trib", [128, 128], BF16),
        qa=dram_in(nc, "qa", [64, T], BF16), ka=dram_in(nc, "ka", [64, T], BF16), va=dram_in(nc, "va", [128, NT, 65], BF16),
        lamp=dram_in(nc, "lamp", [128, 4, 32]), lami=dram_in(nc, "lami", [128, 2]),
        qb=dram_in(nc, "qb", [64, T], BF16), kb=dram_in(nc, "kb", [64, T], BF16), vb=dram_in(nc, "vb", [128, NT, 65], BF16),
        flog=dram_in(nc, "flog", [128, NT]), fbias=dram_in(nc, "fbias", [128, 1]),
        triuf=dram_in(nc, "triuf", [128, 128]), onesf=dram_in(nc, "onesf", [128, 128]),
        oa=dram_out(nc, "oa", [T, 64], BF16), ob=dram_out(nc, "ob", [T, 64], BF16),
    )
    return io


def mix_common(s, io, n_ps=3, with_pf2=True):
    nc = s.nc
    cm = {}
    for nm, key, dt in (("ident_b", "identb", BF16), ("ident_f", "identf", F32), ("tri_b", "trib", BF16)):
        b = s.sb(nm, [128, 128], dt)
        s.dma("sp", b[:, :], io[key][:, :], writes=[b])
        cm[nm] = b
    cm["epsc"] = s.sb("epsc", [128, 1], F32)
    s.op("dve", lambda: nc.vector.memset(cm["epsc"][:, :], EPS), writes=[cm["epsc"]])
    cm["ps_rr"] = RR([s.ps("ps%d" % j, [128, 512], F32) for j in range(n_ps)])
    cm["po_rr"] = RR([s.ps("po%d" % j, [128, 512], F32) for j in range(3)])
    cm["pf"] = s.ps("pf", [128, 4, 128], F32)
    if with_pf2:
        cm["pf2"] = s.ps("pf2", [128, 4, 128], F32)
    cm["lag"] = n_ps - 1
    cm["o1s_rr"] = RR([s.sb("o1s%d" % j, [128, 4, 65], F32) for j in range(2)])
    cm["pt_rr"] = RR([s.sb("pt%d" % j, [128, 512], BF16) for j in range(n_ps + 2)])
    cm["oc_rr"] = RR([s.sb("oc%d" % j, [128, 512], F32) for j in range(2)])
    cm["st_rr"] = RR([s.sb("mst%d" % j, [128, 8], F32) for j in range(8)])
    cm["ot_rr"] = RR([s.sb("ot%d" % j, [128, 4, 64], BF16) for j in range(2)])
    cm["tmp_rr"] = RR([s.sb("tmp%d" % j, [128, 64], F32) for j in range(4)])
    return cm


def emit_mix_a(s, cm, io, T, stage="all", bufs=None):
    nc = s.nc
    NT = T // 128
    if stage in ("all", "alloc"):
        if stage == "all":
            m = s.mark()
        b = dict(qT=s.sb("a_q", [128, T], BF16), k1=s.sb("a_k1", [128, T], BF16), k2=s.sb("a_k2", [128, T], BF16),
                 vt=s.sb("a_v", [128, NT, 128], BF16), lp=s.sb("lp", [128, 4, 32], F32), li=s.sb("li", [128, 2], F32),
                 lw=s.sb("lw", [128, 2, 32], F32), lam=s.sb("lam", [128, 4], F32))
        s.op("pool", lambda: nc.gpsimd.memset(b["qT"][64:128, :], 0.0), writes=[b["qT"]])
        s.op("dve", lambda: nc.vector.memset(b["k1"][:, :], 0.0), writes=[b["k1"]])
        s.op("pool", lambda: nc.gpsimd.memset(b["k2"][:, :], 0.0), writes=[b["k2"]])
        load_vt(s, b["vt"], io, "va", T, init=True, load=False)
        if stage == "alloc":
            return b
        bufs = b
    qT, k1, k2, vt, lp, li, lw, lam = (bufs[k] for k in ("qT", "k1", "k2", "vt", "lp", "li", "lw", "lam"))
    if stage in ("all", "load"):
        s.dma("sp", qT[0:64, :], io["qa"][:, :], writes=[qT])
        s.dma("sp", k1[0:32, :], io["ka"][0:32, :], writes=[k1])
        s.dma("sp", k2[32:64, :], io["ka"][32:64, :], writes=[k2])
        load_vt(s, vt, io, "va", T, init=False)
        s.dma("sp", lp[:, :, :], io["lamp"][:, :, :], writes=[lp])
        s.dma("sp", li[:, :], io["lami"][:, :], writes=[li])
        s.op("dve", lambda: nc.vector.tensor_tensor(out=lw[:, 0, :], in0=lp[:, 0, :], in1=lp[:, 1, :], op=ALU.mult),
             reads=[lp], writes=[lw])
        s.op("dve", lambda: nc.vector.tensor_tensor(out=lw[:, 1, :], in0=lp[:, 2, :], in1=lp[:, 3, :], op=ALU.mult),
             reads=[lp], writes=[lw])
        s.op("dve", lambda: nc.vector.tensor_reduce(out=lam[:, 0:2], in_=lw[:, :, :], axis=AX.X, op=ALU.add),
             reads=[lw], writes=[lam])
        s.op("act", lambda: nc.scalar.activation(out=lam[:, 0:2], in_=lam[:, 0:2], func=AF.Exp), reads=[lam], writes=[lam])
        s.op("dve", lambda: nc.vector.tensor_tensor(out=lam[:, 2:3], in0=lam[:, 1:2], in1=lam[:, 0:1], op=ALU.subtract),
             reads=[lam], writes=[lam])
        s.op("dve", lambda: nc.vector.tensor_tensor(out=lam[:, 3:4], in0=lam[:, 2:3], in1=li[:, 0:1], op=ALU.subtract),
             reads=[lam, li], writes=[lam])
        if stage == "load":
            return

    def fin(qb, pos):
        if "pf2" in cm:
            pf = cm["pf"]
            pf2 = cm["pf2"]
            emit_o_to_tokmajor(s, cm, pos[0], pf, 0)
            emit_o_to_tokmajor(s, cm, pos[1], pf2, 0)
        else:
            pf2 = cm["pf"]
            emit_o_to_tokmajor(s, cm, pos[0], pf2, 0)
            pf = cm["o1s_rr"].next()
            s.op("dve", lambda: nc.vector.tensor_copy(out=pf[:, :, :], in_=pf2[:, :, 0:65]), reads=[pf2], writes=[pf])
            emit_o_to_tokmajor(s, cm, pos[1], pf2, 0)
        ot = cm["ot_rr"].next()
        for j in range(4):
            st = cm["st_rr"].next()
            s.op("dve", lambda: nc.vector.tensor_scalar(out=st[:, 0:1], in0=pf[:, j, 64:65], scalar1=1e-30, scalar2=None,
                                                        op0=ALU.max), reads=[pf], writes=[st])
            s.op("dve", lambda: nc.vector.tensor_scalar(out=st[:, 1:2], in0=pf2[:, j, 64:65], scalar1=1e-30, scalar2=None,
                                                        op0=ALU.max), reads=[pf2], writes=[st])
            s.op("dve", lambda: nc.vector.reciprocal(out=st[:, 0:2], in_=st[:, 0:2]), reads=[st], writes=[st])
            t2 = cm["tmp_rr"].next()
            o = cm["tmp_rr"].next()
            s.op("dve", lambda: nc.vector.tensor_scalar(out=t2[:, :], in0=pf2[:, j, 0:64], scalar1=st[:, 1:2],
                                                        scalar2=lam[:, 3:4], op0=ALU.mult, op1=ALU.mult),
                 reads=[pf2, st, lam], writes=[t2])
            s.op("dve", lambda: nc.vector.scalar_tensor_tensor(out=o[:, :], in0=pf[:, j, 0:64], scalar=st[:, 0:1], in1=t2[:, :],
                                                               op0=ALU.mult, op1=ALU.add), reads=[pf, st, t2], writes=[o])
            s.op("act", lambda: nc.scalar.activation(out=t2[:, :], in_=o[:, :], func=AF.Square, accum_out=st[:, 2:3]),
                 reads=[o], writes=[t2, st])
            s.op("act", lambda: nc.scalar.activation(out=st[:, 3:4], in_=st[:, 2:3], func=AF.Ln, bias=cm["epsc"][:, 0:1],
                                                     scale=1.0 / 64.0), reads=[st, cm["epsc"]], writes=[st])
            s.op("act", lambda: nc.scalar.activation(out=st[:, 4:5], in_=st[:, 3:4], func=AF.Exp, scale=-0.5),
                 reads=[st], writes=[st])
            s.op("dve", lambda: nc.vector.tensor_scalar(out=ot[:, j, :], in0=o[:, :], scalar1=st[:, 4:5], scalar2=li[:, 1:2],
                                                        op0=ALU.mult, op1=ALU.mult), reads=[o, st, li], writes=[ot])
        s.dma("sp", io["oa"][qb * 512:(qb + 1) * 512, :].rearrange("(j p) d -> p j d", p=128), ot[:, :, :], reads=[ot])

    emit_attn_phase(s, cm, T, 2, qT, [k1, k2], vt, None, io["oa"], fin, name="a", lag=cm["lag"])
    if stage == "all":
        s.release(m)


def emit_mix_b(s, cm, io, T, stage="all", bufs=None):
    nc = s.nc
    NT = T // 128
    NQB = T // 512
    if stage in ("all", "alloc"):
        if stage == "all":
            m = s.mark()
        b = dict(qT=s.sb("b_q", [128, T], BF16), kT=s.sb("b_k", [128, T], BF16), vt=s.sb("b_v", [128, NT, 128], BF16),
                 fl=s.sb("fl", [128, NT], F32), fb=s.sb("fb", [128, 2], F32), tu=s.sb("tu", [128, 128], F32),
                 on=s.sb("on", [128, 128], F32), cc=s.sb("cc", [128, NT], F32), inc=s.sb("inc", [128, NT], F32),
                 tmpc=s.sb("tmpc", [128, NT], F32), btab=s.sb("btab", [128, NQB, NT], F32))
        s.op("pool", lambda: nc.gpsimd.memset(b["qT"][64:128, :], 0.0), writes=[b["qT"]])
        s.op("dve", lambda: nc.vector.memset(b["kT"][64:128, :], 0.0), writes=[b["kT"]])
        load_vt(s, b["vt"], io, "vb", T, init=True, load=False)
        s.dma("sp", b["tu"][:, :], io["triuf"][:, :], writes=[b["tu"]])
        s.dma("sp", b["on"][:, :], io["onesf"][:, :], writes=[b["on"]])
        if stage == "alloc":
            return b
        bufs = b
    qT, kT, vt, fl, fb, tu, on, cc, inc_, tmpc, btab = (bufs[k] for k in ("qT", "kT", "vt", "fl", "fb", "tu", "on", "cc", "inc",
                                                                            "tmpc", "btab"))
    if stage in ("all", "load"):
        s.dma("sp", qT[0:64, :], io["qb"][:, :], writes=[qT])
        s.dma("sp", kT[0:64, :], io["kb"][:, :], writes=[kT])
        load_vt(s, vt, io, "vb", T, init=False)
        if stage == "load":
            return
    if "flog_sb" not in io:
        s.dma("sp", fl[:, :], io["flog"][:, :], writes=[fl])
    s.dma("sp", fb[:, 0:1], io["fbias"][:, :], writes=[fb])
    s.op("dve", lambda: nc.vector.tensor_scalar(out=fb[:, 1:2], in0=fb[:, 0:1], scalar1=-1.0, scalar2=None, op0=ALU.mult),
         reads=[fb], writes=[fb])
    if "flog_sb" in io:
        fsb, fap = io["flog_sb"]
        s.op("act", lambda: nc.scalar.activation(out=fl[:, :], in_=fap, func=AF.Exp, bias=fb[:, 1:2], scale=-1.0),
             reads=[fsb, fb], writes=[fl])
    else:
        s.op("act", lambda: nc.scalar.activation(out=fl[:, :], in_=fl[:, :], func=AF.Exp, bias=fb[:, 1:2], scale=-1.0),
             reads=[fl, fb], writes=[fl])
    s.op("act", lambda: nc.scalar.activation(out=fl[:, :], in_=fl[:, :], func=AF.Ln, bias=1.0, scale=1.0),
         reads=[fl], writes=[fl])
    pc = cm["pf"]
    pcv = pc[:, 0, :]
    s.op("pe", lambda: nc.tensor.matmul(pc[:, 0, 0:NT], lhsT=tu[:, :], rhs=fl[:, :], start=True, stop=True),
         reads=[tu, fl], writes=[pc])
    s.op("pe", lambda: nc.tensor.matmul(pc[:, 1, 0:NT], lhsT=on[:, :], rhs=fl[:, :], start=True, stop=True),
         reads=[on, fl], writes=[pc])
    s.op("dve", lambda: nc.vector.tensor_copy(out=inc_[:, :], in_=pc[:, 1, 0:NT]), reads=[pc], writes=[inc_])
    sh = 1
    while sh < NT:
        s.op("dve", lambda: nc.vector.tensor_copy(out=tmpc[:, :], in_=inc_[:, :]), reads=[inc_], writes=[tmpc])
        s.op("dve", lambda: nc.vector.tensor_tensor(out=inc_[:, sh:NT], in0=tmpc[:, sh:NT], in1=tmpc[:, 0:NT - sh], op=ALU.add),
             reads=[tmpc], writes=[inc_])
        sh *= 2
    s.op("dve", lambda: nc.vector.tensor_tensor(out=cc[:, :], in0=pc[:, 0, 0:NT], in1=inc_[:, :], op=ALU.add),
         reads=[pc, inc_], writes=[cc])
    s.op("dve", lambda: nc.vector.tensor_tensor(out=tmpc[:, :], in0=cc[:, :], in1=pc[:, 1, 0:NT], op=ALU.subtract),
         reads=[pc, cc], writes=[tmpc])
    for qb in range(NQB):
        s.op("dve", lambda: nc.vector.tensor_scalar(out=btab[:, qb, :], in0=tmpc[:, :], scalar1=inc_[:, 4 * qb + 1:4 * qb + 2],
                                                    scalar2=None, op0=ALU.subtract), reads=[tmpc, inc_], writes=[btab])

    def bias_fn(qb, t):
        return btab, btab[:, qb, t:t + 1]

    def fin(qb, pos):
        pf = cm["pf"]
        emit_o_to_tokmajor(s, cm, pos[0], pf, 0)
        ot = cm["ot_rr"].next()
        for j in range(4):
            st = cm["st_rr"].next()
            s.op("dve", lambda: nc.vector.tensor_scalar(out=st[:, 0:1], in0=pf[:, j, 64:65], scalar1=1e-30, scalar2=None,
                                                        op0=ALU.max), reads=[pf], writes=[st])
            s.op("dve", lambda: nc.vector.reciprocal(out=st[:, 0:1], in_=st[:, 0:1]), reads=[st], writes=[st])
            s.op("dve", lambda: nc.vector.tensor_scalar(out=ot[:, j, :], in0=pf[:, j, 0:64], scalar1=st[:, 0:1], scalar2=None,
                                                        op0=ALU.mult), reads=[pf, st], writes=[ot])
        s.dma("sp", io["ob"][qb * 512:(qb + 1) * 512, :].rearrange("(j p) d -> p j d", p=128), ot[:, :, :], reads=[ot])

    emit_attn_phase(s, cm, T, 1, qT, [kT], vt, None, io["ob"], fin, bias_fn=bias_fn, name="b", lag=cm["lag"])
    if stage == "all":
        s.release(m)


def mix_consts():
    k = np.arange(128)
    tri = np.where(k[:, None] > k[None, :], NEG, 0.0).astype(np.float32)
    return dict(identb=_bf(np.eye(128, dtype=np.float32)), identf=np.eye(128, dtype=np.float32), trib=_bf(tri),
                triuf=np.triu(np.ones((128, 128), np.float32)), onesf=np.ones((128, 128), np.float32))


def mix_decl_c(nc, io, T):
    NT = T // 128
    QL = NT // 4
    NCT = max(1, T // 2048)
    io.update(dict(
        qc=dram_in(nc, "qc", [128, QL, 512], BF16),
        kskw=dram_in(nc, "kskw", [128, T], BF16),
        vs=dram_in(nc, "vs", [128, NT, 65], BF16), vw=dram_in(nc, "vw", [128, NT, 65], BF16),
        kvin=dram_in(nc, "kvin", [128, T], BF16),
        w1=dram_in(nc, "w1", [2, 2048, 256]), b1=dram_in(nc, "b1", [128, 4]),
        peT=dram_in(nc, "peT", [128, 32]),
        w2=dram_in(nc, "w2", [2, 256, 64]), b2=dram_in(nc, "b2", [2, 64]), b2c=dram_in(nc, "b2c", [64, 1]),
        kgain=dram_in(nc, "kgain", [64, 1]),
        ng=dram_in(nc, "ng", [128, QL, 12]),
        cmask=dram_in(nc, "cmask", [128, QL, NCT, 128], BF16),
        smask=dram_in(nc, "smask", [128, 4, 128], BF16), wmask=dram_in(nc, "wmask", [128, 8, 128], BF16),
        impA=dram_in(nc, "impA", [128, QL, 128]), impB=dram_in(nc, "impB", [128, QL, 128]),
        emat=dram_in(nc, "emat", [128, NT, 128], BF16), ovl=dram_in(nc, "ovl", [128, NCT, 128], BF16),
        ones64=dram_in(nc, "ones64", [64, 64], BF16), onesrow=dram_in(nc, "onesrow", [1, 128], BF16),
        oc=dram_out(nc, "oc", [QL * 128, 256], BF16),
    ))
    return io


def emit_gelu(s, zin_ap, zin_b, out_ap, out_b, tmp, shape_sl):
    nc = s.nc
    t = tmp
    s.op("act", lambda: nc.scalar.activation(out=t[shape_sl], in_=zin_ap, func=AF.Square), reads=[zin_b], writes=[t])
    s.op("dve", lambda: nc.vector.tensor_scalar(out=t[shape_sl], in0=t[shape_sl], scalar1=0.044715, scalar2=1.0,
                                                op0=ALU.mult, op1=ALU.add), reads=[t], writes=[t])
    s.op("dve", lambda: nc.vector.tensor_tensor(out=t[shape_sl], in0=t[shape_sl], in1=zin_ap, op=ALU.mult),
         reads=[t, zin_b], writes=[t])
    s.op("act", lambda: nc.scalar.activation(out=t[shape_sl], in_=t[shape_sl], func=AF.Exp, scale=-GELU_C), reads=[t], writes=[t])
    s.op("dve", lambda: nc.vector.tensor_scalar(out=t[shape_sl], in0=t[shape_sl], scalar1=1.0, scalar2=None, op0=ALU.add),
         reads=[t], writes=[t])
    s.op("dve", lambda: nc.vector.reciprocal(out=t[shape_sl], in_=t[shape_sl]), reads=[t], writes=[t])
    s.op("dve", lambda: nc.vector.tensor_tensor(out=out_ap, in0=t[shape_sl], in1=zin_ap, op=ALU.mult),
         reads=[t, zin_b], writes=[out_b])


def emit_mix_c(s, cm, io, T, cs=None):
    fused = cs is not None
    cs = cs if fused else [None]
    nc = s.nc
    NT = T // 128
    QL = NT // 4
    NCT = max(1, T // 2048)
    Nc = T // 16 - 1
    NCP = NCT * 128 if Nc > 128 else 128
    NCW = min(Nc, 511)
    assert Nc <= 511
    m = s.mark()
    ident_b = cm["ident_b"]
    ps_l = cm["ps_rr"].items
    po_l = cm["po_rr"].items
    pf, pf2 = cm["pf"], cm["pf2"]

    def ld(name, shape, dt, src, q="sp"):
        b = s.sb(name, shape, dt)
        idx = tuple(slice(None) for _ in shape)
        s.dma(q, b[idx], src, writes=[b])
        return b

    qc = s.sb("c_q", [128, QL, 512], BF16)
    qc2 = s.sb("c_q2", [128, QL, 512], BF16)
    s.op("pool", lambda: nc.gpsimd.memset(qc[64:128, :, :], 0.0), writes=[qc])
    s.op("dve", lambda: nc.vector.memset(qc2[0:64, :, :], 0.0), writes=[qc2])
    kk = ld("c_kk", [128, T], BF16, io["kskw"][:, :])
    vs = s.sb("c_vs", [128, NT, 128], BF16)
    vw = s.sb("c_vw", [128, NT, 128], BF16)
    load_vt(s, vs, io, "vs", T)
    load_vt(s, vw, io, "vw", T)
    emat = ld("c_e", [128, NT, 128], BF16, io["emat"][:, :, :])
    ovl = ld("c_ovl", [128, NCT, 128], BF16, io["ovl"][:, :, :])
    smask = s.sb("c_sm", [128, 4, 128], BF16)
    wmask = s.sb("c_wm", [128, 8, 128], BF16)
    ngt = s.sb("c_ng", [128, QL, 12], F32)
    ones64 = ld("c_o64", [64, 64], BF16, io["ones64"][:, :])
    onesrow = ld("c_orow", [1, 128], BF16, io["onesrow"][:, :])
    kgain = ld("c_kg", [64, 1], F32, io["kgain"][:, :])
    b2c = ld("c_b2c", [64, 1], F32, io["b2c"][:, :])
    b1 = ld("c_b1", [128, 4], F32, io["b1"][:, :])

    ktc = s.sb("c_ktc", [128, NCP], BF16)
    vc = s.sb("c_vc", [128, NCT, 128], BF16)
    s.op("dve", lambda: nc.vector.memset(ktc[:, :], 0.0), writes=[ktc])
    s.op("dve", lambda: nc.vector.memset(vc[:, :, :], 0.0), writes=[vc])
    s.op("dve", lambda: nc.vector.memset(vc[:, :, 64:65], 1.0), writes=[vc])

    m2 = s.mark()
    kvin = ld("c_kvin", [128, T], BF16, io["kvin"][:, :])
    w1sb = s.sb("c_w1", [128, 32, 256], BF16)
    for x in range(2):
        s.dma("pool", w1sb[x * 64:(x + 1) * 64, :, :], io["w1"][x].rearrange("(j d) f -> d j f", d=64), writes=[w1sb])
    peT = s.sb("c_pe", [128, 32], BF16)
    s.dma("pool", peT[:, :], io["peT"][:, :], writes=[peT])
    w2sb = s.sb("c_w2", [128, 2, 2, 64], BF16)
    for x in range(2):
        s.dma("pool", w2sb[:, x, :, :], io["w2"][x].rearrange("(hh f) d -> f hh d", f=128), writes=[w2sb])
    b2row = s.sb("c_b2r", [1, 64], BF16)
    s.dma("pool", b2row[:, :], io["b2"][1:2, :], writes=[b2row])
    hacc = [ps_l[0], ps_l[1], ps_l[2], po_l[0]]
    pcol = po_l[1]
    for x in range(2):
        for hh in range(2):
            hp = hacc[x * 2 + hh]
            for j in range(32):
                s.op("pe", lambda: nc.tensor.matmul(hp[:, 0:NCW], lhsT=w1sb[x * 64:(x + 1) * 64, j, hh * 128:(hh + 1) * 128],
                                                    rhs=kvin[x * 64:(x + 1) * 64, j:j + 16 * (NCW - 1) + 1:16],
                                                    start=(j == 0), stop=(j == 31)),
                     reads=[w1sb, kvin], writes=[hp], inc=(j == 31))
            for j in range(32):
                s.op("pe", lambda: nc.tensor.matmul(pcol[:, x * 2 + hh:x * 2 + hh + 1],
                                                    lhsT=w1sb[x * 64:(x + 1) * 64, j, hh * 128:(hh + 1) * 128],
                                                    rhs=peT[x * 64:(x + 1) * 64, j:j + 1], start=(j == 0), stop=(j == 31)),
                     reads=[w1sb, peT], writes=[pcol], inc=(j == 31))
    hbias = s.sb("c_hb", [128, 4], F32)
    s.op("dve", lambda: nc.vector.tensor_tensor(out=hbias[:, :], in0=pcol[:, 0:4], in1=b1[:, :], op=ALU.add),
         reads=[pcol, b1], writes=[hbias])
    gh = []
    for x in range(2):
        for hh in range(2):
            k = x * 2 + hh
            z = s.sb("c_z%d" % k, [128, 512], F32)
            tmp = s.sb("c_zt%d" % k, [128, 512], F32)
            gb = s.sb("c_g%d" % k, [128, 512], BF16)
            s.op("act", lambda: nc.scalar.activation(out=z[:, 0:NCW], in_=hacc[k][:, 0:NCW], func=AF.Identity,
                                                     bias=hbias[:, k:k + 1], scale=1.0), reads=[hacc[k], hbias], writes=[z])
            emit_gelu(s, z[:, 0:NCW], z, gb[:, 0:NCW], gb, tmp, (slice(None), slice(0, NCW)))
            gh.append(gb)
    pk = po_l[2]
    for hh in range(2):
        s.op("pe", lambda: nc.tensor.matmul(pk[0:64, 0:NCW], lhsT=w2sb[:, 0, hh, :], rhs=gh[hh][:, 0:NCW],
                                            start=(hh == 0), stop=(hh == 1)), reads=[w2sb, gh[hh]], writes=[pk], inc=(hh == 1))
    kz = s.sb("c_kz", [64, 512], F32)
    ksq = s.sb("c_ksq", [64, 512], BF16)
    krs = s.sb("c_krs", [64, 512], F32)
    s.op("act", lambda: nc.scalar.activation(out=kz[:, 0:NCW], in_=pk[0:64, 0:NCW], func=AF.Identity, bias=b2c[:, 0:1], scale=1.0),
         reads=[pk, b2c], writes=[kz])
    s.op("act", lambda: nc.scalar.activation(out=ksq[:, 0:NCW], in_=kz[:, 0:NCW], func=AF.Square), reads=[kz], writes=[ksq])
    pq = ps_l[0]
    s.op("pe", lambda: nc.tensor.matmul(pq[0:64, 0:NCW], lhsT=ones64[:, :], rhs=ksq[:, 0:NCW], start=True, stop=True),
         reads=[ones64, ksq], writes=[pq])
    s.op("act", lambda: nc.scalar.activation(out=krs[:, 0:NCW], in_=pq[0:64, 0:NCW], func=AF.Ln, bias=cm["epsc"][0:64, 0:1],
                                             scale=1.0 / 64.0), reads=[pq, cm["epsc"]], writes=[krs])
    s.op("act", lambda: nc.scalar.activation(out=krs[:, 0:NCW], in_=krs[:, 0:NCW], func=AF.Exp, scale=-0.5), reads=[krs], writes=[krs])
    s.op("dve", lambda: nc.vector.scalar_tensor_tensor(out=ktc[0:64, 0:NCW], in0=kz[:, 0:NCW], scalar=kgain[:, 0:1], in1=krs[:, 0:NCW],
                                                       op0=ALU.mult, op1=ALU.mult), reads=[kz, kgain, krs], writes=[ktc])
    for nt in range(NCT):
        n0 = nt * 128
        nn = min(128, Nc - n0)
        pv = ps_l[1 + nt % 2]
        for hh in range(2):
            s.op("pe", lambda: nc.tensor.matmul(pv[0:nn, 0:64], lhsT=gh[2 + hh][:, n0:n0 + nn], rhs=w2sb[:, 1, hh, :],
                                                start=(hh == 0), stop=False), reads=[gh[2 + hh], w2sb], writes=[pv], inc=False)
        s.op("pe", lambda: nc.tensor.matmul(pv[0:nn, 0:64], lhsT=onesrow[0:1, 0:nn], rhs=b2row[0:1, :], start=False, stop=True),
             reads=[onesrow, b2row], writes=[pv])
        s.op("act", lambda: nc.scalar.copy(out=vc[0:nn, nt, 0:64], in_=pv[0:nn, 0:64]), reads=[pv], writes=[vc])
    s.release(m2)

    cmk_rr = RR([s.sb("c_cmk%d" % j, [128, NCT, 128], BF16) for j in range(2)])
    ia_rr = RR([s.sb("c_ia%d" % j, [128, 128], F32) for j in range(2)])
    ib_rr = RR([s.sb("c_ib%d" % j, [128, 128], F32) for j in range(2)])
    imp_rr = RR([s.sb("c_imp%d" % j, [128, 128], F32) for j in range(2)])
    imp2_rr = RR([s.sb("c_impb%d" % j, [128, 128], F32) for j in range(2)])
    m8_rr = RR([s.sb("c_m8%d" % j, [128, 16], F32) for j in range(2)])
    mbT_rr = RR([s.sb("c_mbT%d" % j, [128, 128], BF16) for j in range(2)])
    oco_rr = RR([s.sb("c_oc%d" % j, [128, 4, 64], F32) for j in range(2)])
    gw_rr = RR([s.sb("c_gw%d" % j, [128, 12], F32) for j in range(2)])
    oo_rr = RR([s.sb("c_oo%d" % j, [128, 4, 64], F32) for j in range(2)])
    ob_rr = RR([s.sb("c_ob%d" % j, [128, 4, 64], BF16) for j in range(2)])

    pipe = Pipe(2)

    def masked_tile(kbuf, prow, t, Q, masks, vbuf, vt_idx, po, first, last, extra=None):
        ps = cm["ps_rr"].next()
        nm = len(masks)
        s.op("pe", lambda: nc.tensor.matmul(ps[:, :], lhsT=kbuf[:, t * 128:(t + 1) * 128], rhs=Q,
                                            start=True, stop=(nm == 0)), reads=[kbuf, qc, qc2], writes=[ps], inc=(nm == 0))
        for mi, (la, lb, ra, rb) in enumerate(masks):
            for h in range(4):
                lastm = (mi == nm - 1 and h == 3)
                s.op("pe", lambda: nc.tensor.matmul(ps[:, h * 128:(h + 1) * 128], lhsT=la, rhs=ra, start=False, stop=lastm),
                     reads=[lb, rb], writes=[ps], inc=lastm)
        pt = cm["pt_rr"].next()
        s.op("act", lambda: nc.scalar.activation(out=pt[:, :], in_=ps[:, :], func=AF.Exp), reads=[ps], writes=[pt])

        def back(pt=pt, po=po, vbuf=vbuf, vt_idx=vt_idx, first=first, last=last, extra=extra):
            s.op("pe", lambda: nc.tensor.matmul(po[:, :], lhsT=vbuf[:, vt_idx, :], rhs=pt[:, :], start=first, stop=last),
                 reads=[vbuf, pt], writes=[po])
            if extra is not None:
                extra(pt)
        pipe.push(back)

    for ci in cs:
        def gk(key):
            return io[key][ci] if fused else io[key]
        if fused:
            for h in range(4):
                r0 = (h % 2) * 64
                srcq = io["zq"][h // 2][r0:r0 + 64, :].rearrange("d (i c q) -> d i c q", c=4, q=128)[:, :, ci, :]
                s.dma("sp", qc[0:64, :, h * 128:(h + 1) * 128], srcq, writes=[qc])
                s.dma("sp", qc2[64:128, :, h * 128:(h + 1) * 128], srcq, writes=[qc2])
            msb, mview = io["misc_sb"]
            s.op("act", lambda: nc.scalar.activation(out=ngt[:, :, :], in_=mview[:, ci:NT:4, 4:16], func=AF.Exp, scale=-1.0),
                 reads=[msb], writes=[ngt])
        else:
            s.dma("sp", qc[0:64, :, :], io["qc"][0:64, :, :], writes=[qc])
            s.dma("sp", qc2[64:128, :, :], io["qc"][64:128, :, :], writes=[qc2])
            s.dma("sp", ngt[:, :, :], io["ng"][:, :, :], writes=[ngt])
            s.op("act", lambda: nc.scalar.activation(out=ngt[:, :, :], in_=ngt[:, :, :], func=AF.Exp, scale=-1.0),
                 reads=[ngt], writes=[ngt])
        s.op("dve", lambda: nc.vector.tensor_scalar(out=ngt[:, :, :], in0=ngt[:, :, :], scalar1=1.0, scalar2=None, op0=ALU.add),
             reads=[ngt], writes=[ngt])
        s.op("dve", lambda: nc.vector.reciprocal(out=ngt[:, :, :], in_=ngt[:, :, :]), reads=[ngt], writes=[ngt])
        s.dma("sp", smask[:, :, :], gk("smask")[:, :, :], writes=[smask])
        s.dma("sp", wmask[:, :, :], gk("wmask")[:, :, :], writes=[wmask])
        for i in range(QL):
            Qlo = qc[:, i, :]
            Qhi = qc2[:, i, :]
            cmk = cmk_rr.next()
            ia = ia_rr.next()
            ib = ib_rr.next()
            s.dma("sp", cmk[:, :, :], gk("cmask")[:, i, :, :], writes=[cmk])
            s.dma("sp", ia[:, :], gk("impA")[:, i, :], writes=[ia])
            s.dma("sp", ib[:, :], gk("impB")[:, i, :], writes=[ib])
            po_c, po_s, po_w = po_l[0], po_l[1], po_l[2]
            nct = min(NCT, i // 4 + 1)
            for nt in range(nct):
                def imp_mm(pt, nt=nt, nct=nct):
                    for h in range(4):
                        s.op("pe", lambda: nc.tensor.matmul(pf2[:, h, :], lhsT=pt[:, h * 128:(h + 1) * 128], rhs=ovl[:, nt, :],
                                                            start=(nt == 0 and h == 0), stop=(nt == nct - 1 and h == 3),
                                                            skip_group_check=True), reads=[pt, ovl], writes=[pf2],
                             inc=(h == 3))
                masked_tile(ktc, (0, 64), nt, Qlo, [(ident_b[:, :], ident_b, cmk[:, nt, :], cmk)], vc, nt, po_c,
                            nt == 0, nt == nct - 1, extra=imp_mm)
            pipe.flush()
            emit_o_to_tokmajor(s, cm, po_c, pf, 0)
            st = cm["st_rr"].next()
            rsum = cm["st_rr"].next()
            gw = gw_rr.next()
            s.op("dve", lambda: nc.vector.tensor_scalar(out=st[:, 0:4], in0=pf[:, :, 64], scalar1=1e-30, scalar2=None, op0=ALU.max),
                 reads=[pf], writes=[st])
            s.op("dve", lambda: nc.vector.reciprocal(out=rsum[:, 0:4], in_=st[:, 0:4]), reads=[st], writes=[rsum])
            oco = oco_rr.next()
            s.op("dve", lambda: nc.vector.tensor_copy(out=oco[:, :, :], in_=pf[:, :, 0:64]), reads=[pf], writes=[oco])
            imp = imp_rr.next()
            s.op("dve", lambda: nc.vector.tensor_scalar(out=imp[:, :], in0=pf2[:, 0, :], scalar1=rsum[:, 0:1], scalar2=None, op0=ALU.mult),
                 reads=[pf2, rsum], writes=[imp])
            for h in range(1, 4):
                s.op("dve", lambda: nc.vector.scalar_tensor_tensor(out=imp[:, :], in0=pf2[:, h, :], scalar=rsum[:, h:h + 1], in1=imp[:, :],
                                                                   op0=ALU.mult, op1=ALU.add), reads=[pf2, rsum, imp], writes=[imp])
            s.op("dve", lambda: nc.vector.tensor_tensor(out=imp[:, :], in0=imp[:, :], in1=ia[:, :], op=ALU.mult), reads=[imp, ia], writes=[imp])
            s.op("dve", lambda: nc.vector.tensor_tensor(out=imp[:, :], in0=imp[:, :], in1=ib[:, :], op=ALU.add), reads=[imp, ib], writes=[imp])
            m8 = m8_rr.next()
            imp2 = imp2_rr.next()
            s.op("dve", lambda: nc.vector.max(out=m8[:, 0:8], in_=imp[:, :]), reads=[imp], writes=[m8])
            s.op("dve", lambda: nc.vector.match_replace(out=imp2[:, :], in_to_replace=m8[:, 0:8], in_values=imp[:, :], imm_value=-1e9),
                 reads=[imp, m8], writes=[imp2])
            s.op("dve", lambda: nc.vector.max(out=m8[:, 8:16], in_=imp2[:, :]), reads=[imp2], writes=[m8])
            s.op("dve", lambda: nc.vector.tensor_scalar(out=imp2[:, :], in0=imp[:, :], scalar1=m8[:, 15:16], scalar2=NEG,
                                                        op0=ALU.is_lt, op1=ALU.mult), reads=[imp, m8], writes=[imp2])
            tl = [4 * (i - 1) + u for u in range(8) if 4 * (i - 1) + u >= 0]
            for t in tl:
                u = t - 4 * (i - 1)
                masked_tile(kk, (64, 128), t, Qhi, [(ident_b[:, :], ident_b, wmask[:, u, :], wmask)], vw, t, po_w,
                            t == tl[0], t == tl[-1])
            ptr = cm["ps_rr"].next()
            s.op("pe", lambda: nc.tensor.transpose(out=ptr[:, 0:128], in_=imp2[:, :], identity=cm["ident_f"][:, :]),
                 reads=[imp2, cm["ident_f"]], writes=[ptr])
            mbT = mbT_rr.next()
            s.op("dve", lambda: nc.vector.tensor_copy(out=mbT[:, :], in_=ptr[:, 0:128]), reads=[ptr], writes=[mbT])
            nts = 4 * i + 4
            for t in range(nts):
                masks = [(emat[:, t, :], emat, mbT[:, :], mbT)]
                if t >= 4 * i:
                    masks.append((ident_b[:, :], ident_b, smask[:, t - 4 * i, :], smask))
                masked_tile(kk, (0, 64), t, Qlo, masks, vs, t, po_s, t == 0, t == nts - 1)
            pipe.flush()
            emit_o_to_tokmajor(s, cm, po_s, pf, 0)
            st2 = cm["st_rr"].next()
            s.op("dve", lambda: nc.vector.tensor_scalar(out=st2[:, 0:4], in0=pf[:, :, 64], scalar1=1e-30, scalar2=None, op0=ALU.max),
                 reads=[pf], writes=[st2])
            s.op("dve", lambda: nc.vector.reciprocal(out=st2[:, 0:4], in_=st2[:, 0:4]), reads=[st2], writes=[st2])
            gv = ngt[:, i, :].rearrange("p (h b) -> p h b", b=3)
            gwv = gw[:, :].rearrange("p (h b) -> p h b", b=3)
            s.op("dve", lambda: nc.vector.tensor_tensor(out=gwv[:, :, 0], in0=gv[:, :, 0], in1=rsum[:, 0:4], op=ALU.mult),
                 reads=[ngt, rsum], writes=[gw])
            s.op("dve", lambda: nc.vector.tensor_tensor(out=gwv[:, :, 1], in0=gv[:, :, 1], in1=st2[:, 0:4], op=ALU.mult),
                 reads=[ngt, st2], writes=[gw])
            oo = oo_rr.next()
            for h in range(4):
                s.op("dve", lambda: nc.vector.tensor_scalar(out=oo[:, h, :], in0=oco[:, h, :], scalar1=gw[:, 3 * h:3 * h + 1], scalar2=None,
                                                            op0=ALU.mult), reads=[oco, gw], writes=[oo])
                s.op("dve", lambda: nc.vector.scalar_tensor_tensor(out=oo[:, h, :], in0=pf[:, h, 0:64], scalar=gw[:, 3 * h + 1:3 * h + 2],
                                                                   in1=oo[:, h, :], op0=ALU.mult, op1=ALU.add), reads=[pf, gw, oo], writes=[oo])
            emit_o_to_tokmajor(s, cm, po_w, pf, 0)
            st3 = cm["st_rr"].next()
            s.op("dve", lambda: nc.vector.tensor_scalar(out=st3[:, 0:4], in0=pf[:, :, 64], scalar1=1e-30, scalar2=None, op0=ALU.max),
                 reads=[pf], writes=[st3])
            s.op("dve", lambda: nc.vector.reciprocal(out=st3[:, 0:4], in_=st3[:, 0:4]), reads=[st3], writes=[st3])
            s.op("dve", lambda: nc.vector.tensor_tensor(out=gwv[:, :, 2], in0=gv[:, :, 2], in1=st3[:, 0:4], op=ALU.mult),
                 reads=[ngt, st3], writes=[gw])
            ob = ob_rr.next()
            for h in range(4):
                s.op("dve", lambda: nc.vector.scalar_tensor_tensor(out=ob[:, h, :], in0=pf[:, h, 0:64], scalar=gw[:, 3 * h + 2:3 * h + 3],
                                                                   in1=oo[:, h, :], op0=ALU.mult, op1=ALU.add), reads=[pf, gw, oo], writes=[ob])
            orow = ((4 * i + ci) if fused else i) * 128
            s.dma("sp", io["oc"][orow:orow + 128, :], ob[:, :, :].rearrange("p h d -> p (h d)"), reads=[ob])
    s.release(m)


def build_mix(T, parts="abc"):
    nc = bass.Bass("TRN2", target_bir_lowering=False)
    io = mix_decl(nc, T)
    if "c" in parts:
        mix_decl_c(nc, io, T)
    s = S(nc)
    cm = mix_common(s, io)
    if "a" in parts:
        emit_mix_a(s, cm, io, T)
    if "b" in parts:
        emit_mix_b(s, cm, io, T)
    if "c" in parts:
        emit_mix_c(s, cm, io, T)
    s.finish()
    s.close()
    return nc


def mix_consts_c(T, c):
    NT = T // 128
    QL = NT // 4
    NCT = max(1, T // 2048)
    Nc = T // 16 - 1
    NS = T // 64
    ar = np.arange(128)
    cmask = np.zeros((128, QL, NCT, 128), np.float32)
    impA = np.zeros((128, QL, 128), np.float32)
    impB = np.zeros((128, QL, 128), np.float32)
    for i in range(QL):
        qpos = 128 * (4 * i + c) + ar
        for nt in range(NCT):
            n = 128 * nt + ar
            ok = (16 * n[:, None] + 31 <= qpos[None, :]) & (n[:, None] < Nc)
            cmask[:, i, nt, :] = np.where(ok, 0.0, NEG)
        j = ar
        cur = qpos // 64
        forced = (j[None, :] == 0) | (j[None, :] == cur[:, None]) | (j[None, :] == cur[:, None] - 1)
        valid = (j[None, :] * 64 <= qpos[:, None]) & (j[None, :] < NS)
        impA[:, i, :] = (valid & ~forced).astype(np.float32)
        impB[:, i, :] = np.where(forced & (j[None, :] < NS), 1.0e4, np.where(valid, 0.0, -1.0))
    smask = np.zeros((128, 4, 128), np.float32)
    for u in range(4):
        kpos = 128 * u + ar
        qp = 128 * c + ar
        smask[:, u, :] = np.where(kpos[:, None] <= qp[None, :], 0.0, NEG)
    wmask = np.zeros((128, 8, 128), np.float32)
    for u in range(8):
        dist = 128 * (c + 4 - u) + ar[None, :] - ar[:, None]
        wmask[:, u, :] = np.where((dist >= 0) & (dist < 512), 0.0, NEG)
    emat = np.zeros((128, NT, 128), np.float32)
    for t in range(NT):
        for k in range(128):
            jj = 2 * t + k // 64
            if jj < 128:
                emat[jj, t, k] = 1.0
    ovl = np.zeros((128, NCT, 128), np.float32)
    for nt in range(NCT):
        n = 128 * nt + ar
        o = (n[:, None] * 16 < (ar[None, :] + 1) * 64) & (n[:, None] * 16 + 32 > ar[None, :] * 64) & (n[:, None] < Nc) \
            & (ar[None, :] < NS)
        ovl[:, nt, :] = o
    return dict(cmask=_bf(cmask), impA=impA, impB=impB, smask=_bf(smask), wmask=_bf(wmask), emat=_bf(emat), ovl=_bf(ovl),
                ones64=_bf(np.ones((64, 64), np.float32)), onesrow=_bf(np.ones((1, 128), np.float32)))


def build_merge(NT, TB=512):
    nc = bass.Bass("TRN2", target_bir_lowering=False)
    x = dram_in(nc, "x", [NT, D])
    g = dram_in(nc, "g", [D])
    w_in = dram_in(nc, "w_in", [D, 6800])
    w_br = dram_in(nc, "w_br", [4, 256, D])
    w_o = dram_in(nc, "w_o", [D, D])
    ident = dram_in(nc, "ident", [128, 128], BF16)
    obr = dram_in(nc, "obr", [NT, D], BF16)
    y = dram_out(nc, "y", [NT, D])
    s = S(nc)
    emit_merge(s, x, g, w_in, w_br, w_o, ident, obr, y, NT, TB)
    s.finish()
    s.close()
    return nc


def emit_merge(s, x, g, w_in, w_br, w_o, ident, obr, y, NT, TB=512):
    nc = s.nc
    m_ = s.mark()
    ntile = TB // 128
    ident_b = s.sb("ident_b", [128, 128], BF16)
    s.dma("sp", ident_b[:, :], ident[:, :], writes=[ident_b])
    gcol = s.sb("gcol", [128, NKC], F32)
    s.dma("sp", gcol[:, :], g.rearrange("(c p) -> p c", p=128), writes=[gcol], allow_slow_non_contiguous=True)
    epsc = s.sb("epsc", [128, 1], F32)
    s.op("dve", lambda: nc.vector.memset(epsc[:, :], EPS), writes=[epsc])
    wg = [s.sb("wg%d" % c, [128, 4096], BF16) for c in range(NKC)]
    wb = [s.sb("wb%d" % c, [128, D], BF16) for c in range(8)]
    wo = [s.sb("wo%d" % c, [128, D], BF16) for c in range(NKC)]
    for c in range(NKC):
        for hf in range(2):
            s.dma("pool", wg[c][:, hf * 2048:(hf + 1) * 2048], w_in[c * 128:(c + 1) * 128, 2704 + hf * 2048:2704 + (hf + 1) * 2048],
                  writes=[wg[c]])
    for n in range(4):
        for cc in range(2):
            s.dma("pool", wb[2 * n + cc][:, :], w_br[n, cc * 128:(cc + 1) * 128, :], writes=[wb[2 * n + cc]])
    for c in range(NKC):
        s.dma("pool", wo[c][:, :], w_o[c * 128:(c + 1) * 128, :], writes=[wo[c]])
    xn = [s.sb("xn%d" % j, [128, D], F32) for j in range(ntile)]
    xr_rr = RR([s.sb("xr%d" % j, [128, D], F32) for j in range(2)])
    ots = [[s.sb("ot%d_%d" % (k, j), [128, D], BF16) for j in range(ntile)] for k in range(2)]
    hb_rr = RR([s.sb("hb%d" % j, [128, D], BF16) for j in range(2 * ntile)])
    stat_rr = RR([s.sb("st%d" % j, [128, 16], F32) for j in range(4)])
    hT = s.sb("hT", [128, NKC, TB], BF16)
    oT = s.sb("oT", [128, 8, TB], BF16)
    mT = [s.sb("mT%d" % c, [128, TB], BF16) for c in range(8)]
    pT_rr = RR([s.ps("pT%d" % j, [128, TB], BF16) for j in range(2)])
    pg_rr = RR([s.ps("pg%d" % j, [128, 512], F32) for j in range(2)])
    pp_rr = RR([s.ps("pp%d" % j, [128, 512], F32) for j in range(2)])
    po_rr = RR([s.ps("po%d" % j, [128, 512], F32) for j in range(2)])
    sg_rr = RR([s.sb("sg%d" % j, [128, TB], F32) for j in range(3)])
    acc_rr = RR([s.sb("acc%d" % j, [128, TB], F32) for j in range(2)])
    nblk = NT // TB

    def prep_a(tb):
        ot = ots[tb % 2]
        for j in range(ntile):
            r0 = tb * TB + j * 128
            s.dma("sp", xn[j][:, :], x[r0:r0 + 128, :], writes=[xn[j]])
            s.dma("sp", ot[j][:, :], obr[r0:r0 + 128, :], writes=[ot[j]])
        return emit_norm(s, epsc, xn, hb_rr, None, stat_rr, ntile)

    def prep_b(tb, hbs):
        ot = ots[tb % 2]
        emit_transpose_T(s, hbs, gcol, hT, ident_b, pT_rr, ntile)
        for c in range(8):
            pT = pT_rr.next()
            for j in range(ntile):
                s.op("pe", lambda: nc.tensor.transpose(out=pT[:, j * 128:(j + 1) * 128], in_=ot[j][:, c * 128:(c + 1) * 128],
                                                       identity=ident_b[:, :]), reads=[ot[j], ident_b], writes=[pT], inc=(j == ntile - 1))
            if c % 2 == 0:
                s.op("dve", lambda: nc.vector.tensor_copy(out=oT[:, c, :], in_=pT[:, 0:TB]), reads=[pT], writes=[oT])
            else:
                s.op("act", lambda: nc.scalar.copy(out=oT[:, c, :], in_=pT[:, 0:TB]), reads=[pT], writes=[oT])

    hbs_next = prep_a(0)
    prep_b(0, hbs_next)
    for tb in range(nblk):
        t0 = tb * TB
        if tb + 1 < nblk:
            hbs_next = prep_a(tb + 1)
        for dc in range(8):
            acc = acc_rr.next()
            for n in range(4):
                pg = pg_rr.next()
                pp = pp_rr.next()
                for c in range(NKC):
                    s.op("pe", lambda: nc.tensor.matmul(pg[:, 0:TB], lhsT=wg[c][:, n * 1024 + dc * 128:n * 1024 + (dc + 1) * 128],
                                                        rhs=hT[:, c, :], start=(c == 0), stop=(c == NKC - 1)),
                         reads=[wg[c], hT], writes=[pg], inc=(c == NKC - 1))
                for cc in range(2):
                    s.op("pe", lambda: nc.tensor.matmul(pp[:, 0:TB], lhsT=wb[2 * n + cc][:, dc * 128:(dc + 1) * 128],
                                                        rhs=oT[:, 2 * n + cc, :], start=(cc == 0), stop=(cc == 1)),
                         reads=[wb[2 * n + cc], oT], writes=[pp], inc=(cc == 1))
                sg = sg_rr.next()
                s.op("act", lambda: nc.scalar.activation(out=sg[:, :], in_=pg[:, 0:TB], func=AF.Sigmoid), reads=[pg], writes=[sg])
                if n == 0:
                    s.op("dve", lambda: nc.vector.tensor_tensor(out=acc[:, :], in0=sg[:, :], in1=pp[:, 0:TB], op=ALU.mult),
                         reads=[sg, pp], writes=[acc])
                else:
                    s.op("dve", lambda: nc.vector.tensor_tensor(out=sg[:, :], in0=sg[:, :], in1=pp[:, 0:TB], op=ALU.mult),
                         reads=[sg, pp], writes=[sg])
                    if n < 3:
                        s.op("pool", lambda: nc.gpsimd.tensor_tensor(out=acc[:, :], in0=acc[:, :], in1=sg[:, :], op=ALU.add),
                             reads=[acc, sg], writes=[acc])
                    else:
                        s.op("pool", lambda: nc.gpsimd.tensor_tensor(out=mT[dc][:, :], in0=acc[:, :], in1=sg[:, :], op=ALU.add),
                             reads=[acc, sg], writes=[mT[dc]])
        if tb + 1 < nblk:
            prep_b(tb + 1, hbs_next)
        for j in range(ntile):
            xr = xr_rr.next()
            s.dma("sp", xr[:, :], x[t0 + j * 128:t0 + (j + 1) * 128, :], writes=[xr])
            for hf in range(2):
                po = po_rr.next()
                for dc in range(8):
                    s.op("pe", lambda: nc.tensor.matmul(po[:, :], lhsT=mT[dc][:, j * 128:(j + 1) * 128],
                                                        rhs=wo[dc][:, hf * 512:(hf + 1) * 512], start=(dc == 0), stop=(dc == 7)),
                         reads=[mT[dc], wo[dc]], writes=[po], inc=(dc == 7))
                s.op("dve", lambda: nc.vector.tensor_tensor(out=xr[:, hf * 512:(hf + 1) * 512], in0=po[:, :],
                                                            in1=xr[:, hf * 512:(hf + 1) * 512], op=ALU.add),
                     reads=[po, xr], writes=[xr])
            s.dma("sp", y[t0 + j * 128:t0 + (j + 1) * 128, :], xr[:, :], reads=[xr])
    s.release(m_)


PARAM_SHAPES = dict(
    ffn1_norm=("L", D), ffn1_w_in=("L", D, 2 * DFF), ffn1_w_out=("L", DFF, D), mix_norm=("L", D), w_in=("L", D, 6800),
    nsa_phi_w1=("L", 2, 2048, 256), nsa_phi_w2=("L", 2, 256, 64), nsa_phi_b2=("L", 2, 64),
    w_branch=("L", 4, 256, D), w_out=("L", D, D), ffn2_norm=("L", D), ffn2_w_in=("L", D, 2 * DFF), ffn2_w_out=("L", DFF, D),
    gains=("L", 128, 6), vgain=("L", 128, 256), wsT=("L", 4, 128, 128), bsT=("L", 128, 4), lamp=("L", 128, 4, 32),
    lami=("L", 128, 2), fbias=("L", 4, 128, 1), b1l=("L", 128, 4), peT=("L", 128, 32), b2c=("L", 64, 1), kgain=("L", 64, 1),
)


def fused_const_shapes(T):
    NT = T // 128
    QL = NT // 4
    NCT = max(1, T // 2048)
    return dict(
        ident=([128, 128], BF16), identf=([128, 128], F32), blk=([2, 128, 128], BF16), triu=([128, 128], F32),
        trib=([128, 128], BF16), triuf=([128, 128], F32), onesf=([128, 128], F32), ones64=([64, 64], BF16),
        onesrow=([1, 128], BF16), cmask=([4, 128, QL, NCT, 128], BF16), smask=([4, 128, 4, 128], BF16),
        wmask=([4, 128, 8, 128], BF16), impA=([4, 128, QL, 128], F32), impB=([4, 128, QL, 128], F32),
        emat=([128, NT, 128], BF16), ovl=([128, NCT, 128], BF16))


def fused_consts(T):
    pc = proj_consts()
    mc = mix_consts()
    cc = [mix_consts_c(T, c) for c in range(4)]
    d = dict(ident=pc["ident"], identf=mc["identf"], blk=pc["blk"], triu=pc["triu"], trib=mc["trib"], triuf=mc["triuf"],
             onesf=mc["onesf"], ones64=cc[0]["ones64"], onesrow=cc[0]["onesrow"], emat=cc[0]["emat"], ovl=cc[0]["ovl"])
    for k in ("cmask", "smask", "wmask", "impA", "impB"):
        d[k] = np.ascontiguousarray(np.stack([cc[c][k] for c in range(4)], 0))
    return d


def emit_mix_fused(s, F, l, T):
    nc = s.nc
    NT = T // 128
    m = s.mark()
    io0 = dict(identb=F["ident"], identf=F["identf"], trib=F["trib"])
    misc_sb = s.sb("misc_sb", [128, NT, 16], F32)
    m_ab = s.mark()
    cm = mix_common(s, io0, n_ps=4, with_pf2=False)
    for j0 in range(0, NT, 8):
        j1 = min(NT, j0 + 8)
        s.dma("sp", misc_sb[:, j0:j1, :], F["misc"][j0 * 128:j1 * 128, :].rearrange("(j p) c -> p j c", p=128), writes=[misc_sb])
    zfm, vab, obr = F["zfm"], F["vab"], F["obr"]

    def io_a(h):
        r0 = (h % 2) * 64
        io = dict(io0)
        io.update(qa=zfm[h // 2][r0:r0 + 64, :], ka=zfm[2 + h // 2][r0:r0 + 64, :], va_src=vab[:, h * 64:(h + 1) * 64],
                  lamp=F["lamp"][l], lami=F["lami"][l], oa=obr[:, h * 64:(h + 1) * 64])
        return io

    def io_b(h):
        r0 = (h % 2) * 64
        io = dict(io0)
        io.update(qb=zfm[4 + h // 2][r0:r0 + 64, :], kb=zfm[6 + h // 2][r0:r0 + 64, :],
                  vb_src=vab[:, 256 + h * 64:256 + (h + 1) * 64], flog_sb=(misc_sb, misc_sb[:, :, h]), fbias=F["fbias"][l, h],
                  triuf=F["triuf"], onesf=F["onesf"], ob=obr[:, 256 + h * 64:256 + (h + 1) * 64])
        return io

    ba = emit_mix_a(s, cm, io_a(0), T, stage="alloc")
    bb = emit_mix_b(s, cm, io_b(0), T, stage="alloc")
    emit_mix_a(s, cm, io_a(0), T, stage="load", bufs=ba)
    emit_mix_b(s, cm, io_b(0), T, stage="load", bufs=bb)
    for h in range(4):
        emit_mix_a(s, cm, io_a(h), T, stage="compute", bufs=ba)
        if h + 1 < 4:
            emit_mix_a(s, cm, io_a(h + 1), T, stage="load", bufs=ba)
        emit_mix_b(s, cm, io_b(h), T, stage="compute", bufs=bb)
        if h + 1 < 4:
            emit_mix_b(s, cm, io_b(h + 1), T, stage="load", bufs=bb)
    s.release(m_ab)
    cm = mix_common(s, io0, n_ps=3, with_pf2=True)
    io = dict(io0)
    io.update(zq=(zfm[8], zfm[9]), kskw=zfm[10], kvin=zfm[11], vs_src=F["vsw"][:, 0:64], vw_src=F["vsw"][:, 64:128],
              misc_sb=(misc_sb, misc_sb), w1=F["nsa_phi_w1"][l], b1=F["b1l"][l], peT=F["peT"][l], w2=F["nsa_phi_w2"][l],
              b2=F["nsa_phi_b2"][l], b2c=F["b2c"][l], kgain=F["kgain"][l], oc=obr[:, 512:768])
    for k in ("cmask", "smask", "wmask", "impA", "impB", "emat", "ovl", "ones64", "onesrow"):
        io[k] = F[k]
    emit_mix_c(s, cm, io, T, cs=[0, 1, 2, 3])
    s.release(m)


def build_fused(T, L):
    nc = bass.Bass("TRN2", target_bir_lowering=False)
    F = {}
    F["x"] = dram_in(nc, "x", [T, D])
    for k, shp in PARAM_SHAPES.items():
        F[k] = dram_in(nc, k, [L if v == "L" else v for v in shp])
    for k, (shp, dt) in fused_const_shapes(T).items():
        F[k] = dram_in(nc, k, shp, dt)
    y = dram_out(nc, "y", [T, D])
    for k, shp, dt in (("xa", [T, D], F32), ("xb", [T, D], F32), ("xc", [T, D], F32), ("zfm", [NFM, 128, T], BF16),
                       ("vab", [T, 512], BF16), ("vsw", [T, 128], BF16), ("misc", [T, 16], F32), ("obr", [T, D], BF16)):
        F[k] = nc.dram_tensor("s_" + k, shp, dt).ap()
    s = S(nc)
    for l in range(L):
        x_in = F["x"] if l == 0 else F["xc"]
        emit_ffn(s, x_in, F["ffn1_norm"][l], F["ffn1_w_in"][l], F["ffn1_w_out"][l], F["ident"], F["xa"], T)
        a = dict(x=F["xa"], g=F["mix_norm"][l], w_in=F["w_in"][l], ident=F["ident"], gains=F["gains"][l], blk=F["blk"],
                 vgain=F["vgain"][l], wsT=F["wsT"][l], triu=F["triu"], bsT=F["bsT"][l], zfm=F["zfm"], vab=F["vab"],
                 vsw=F["vsw"], misc=F["misc"], od=F["obr"][:, 768:1024])
        emit_proj(s, a, T)
        emit_mix_fused(s, F, l, T)
        emit_merge(s, F["xa"], F["mix_norm"][l], F["w_in"][l], F["w_branch"][l], F["w_out"][l], F["ident"], F["obr"], F["xb"], T)
        x_out = y if l == L - 1 else F["xc"]
        emit_ffn(s, F["xb"], F["ffn2_norm"][l], F["ffn2_w_in"][l], F["ffn2_w_out"][l], F["ident"], x_out, T)
    s.finish()
    s.close()
    return nc


def fused_params(P, L):
    import math
    f32 = np.float32
    A = lambda a: np.ascontiguousarray(np.asarray(a, dtype=f32))
    d = {k: A(P[k]) for k in ("ffn1_norm", "ffn1_w_in", "ffn1_w_out", "mix_norm", "w_in", "nsa_phi_w1", "nsa_phi_w2",
                              "nsa_phi_b2", "w_branch", "w_out", "ffn2_norm", "ffn2_w_in", "ffn2_w_out")}
    tile = lambda v, n: np.tile(A(v), (1, n))
    d["gains"] = np.ascontiguousarray(np.stack([tile(P["diff_q_gain"], 4), tile(P["diff_k_gain"], 4), tile(P["fox_q_gain"], 2),
                                                tile(P["fox_k_gain"], 2), tile(P["nsa_q_gain"], 2), tile(P["nsa_k_gain"], 2)], 2))
    d["vgain"] = np.ascontiguousarray(np.broadcast_to(A(P["gmlp_v_gain"])[:, None, :], (L, 128, 256)))
    d["wsT"] = np.ascontiguousarray(A(P["gmlp_w_s"]).transpose(0, 1, 3, 2))
    d["bsT"] = np.ascontiguousarray(A(P["gmlp_b_s"]).transpose(0, 2, 1))
    d["lamp"] = np.ascontiguousarray(np.broadcast_to(A(P["diff_lambda"])[:, None], (L, 128, 4, 32)))
    li = np.array([[0.8 - 0.6 * math.exp(-0.3 * l), 1.0 - (0.8 - 0.6 * math.exp(-0.3 * l))] for l in range(L)], f32)
    d["lami"] = np.ascontiguousarray(np.broadcast_to(li[:, None, :], (L, 128, 2)))
    d["fbias"] = np.ascontiguousarray(np.broadcast_to(A(P["fox_f_bias"])[:, :, None, None], (L, 4, 128, 1)))
    d["b1l"] = np.ascontiguousarray(A(P["nsa_phi_b1"]).reshape(L, 2, 2, 128).transpose(0, 3, 1, 2).reshape(L, 128, 4))
    pe = A(P["nsa_cmp_pe"])
    d["peT"] = np.ascontiguousarray(pe.transpose(0, 1, 3, 2).reshape(L, 128, 32))
    d["b2c"] = np.ascontiguousarray(A(P["nsa_phi_b2"])[:, 0, :, None])
    d["kgain"] = np.ascontiguousarray(A(P["nsa_k_gain"])[:, :, None])
    return d


B_, T_, L_ = 2, 8192, 2
_PROG = {}


def kernel(**inputs):
    x = np.ascontiguousarray(np.asarray(inputs["x"], dtype=np.float32))
    if "fused" not in _PROG:
        _PROG["fused"] = build_fused(T_, L_)
        _PROG["consts"] = fused_consts(T_)
    nc = _PROG["fused"]
    par = fused_params(inputs, L_)
    in_maps = []
    for b in range(B_):
        d = dict(par)
        d.update(_PROG["consts"])
        d["x"] = x[b]
        in_maps.append(d)
    res = run_bass_kernel_spmd(nc, in_maps, core_ids=list(range(B_)))
    return np.stack([np.asarray(res.results[b]["y"], dtype=np.float32) for b in range(B_)], 0)
```

```python
import numpy as np
import concourse.bass as bass
import concourse.mybir as mybir
from concourse.bass_utils import run_bass_kernel_spmd

F32 = mybir.dt.float32
BF16 = mybir.dt.bfloat16
AF = mybir.ActivationFunctionType
ALU = mybir.AluOpType
AX = mybir.AxisListType

ENGS = ("pe", "act", "dve", "pool", "sp")


class Buf:
    __slots__ = ("name", "t", "w", "r", "dsem", "dcnt", "uid")
    _n = 0

    def __init__(self, name, t):
        Buf._n += 1
        self.uid = Buf._n
        self.name = name
        self.t = t
        self.w = None
        self.r = []
        self.dsem = {}
        self.dcnt = {}

    def __getitem__(self, idx):
        return self.t[idx]


class S:
    def __init__(self, nc, same_engine_sync=True):
        self.nc = nc
        self.e = {"pe": nc.tensor, "act": nc.scalar, "dve": nc.vector, "pool": nc.gpsimd, "sp": nc.sync}
        self.sem = {k: nc.alloc_semaphore("c_" + k) for k in ENGS}
        self.cnt = {k: 0 for k in ENGS}
        self.seen = {k: {} for k in ENGS}
        self.same = same_engine_sync
        self.nbuf = 0
        self.nsem = 0
        self.dma_sems = []
        self.ctx = []
        self.cbufs = []
        self.free_dsems = {"hw": [], "sw": []}

    def sb(self, name, shape, dt):
        self.nbuf += 1
        g = self.nc.sbuf_tensor("%s_%d" % (name, self.nbuf), list(shape), dt)
        t = g.__enter__()
        self.ctx.append(g)
        b = Buf(name, t)
        self.cbufs.append(b)
        return b

    def ps(self, name, shape, dt):
        self.nbuf += 1
        g = self.nc.psum_tensor("%s_%d" % (name, self.nbuf), list(shape), dt)
        t = g.__enter__()
        self.ctx.append(g)
        b = Buf(name, t)
        self.cbufs.append(b)
        return b

    def sub(self, name, ap):
        return Buf(name, ap)

    def mark(self):
        return len(self.ctx)

    def release(self, m):
        self.barrier()
        while len(self.ctx) > m:
            self.ctx.pop().__exit__(None, None, None)
            b = self.cbufs.pop()
            for kind, sem in b.dsem.items():
                self.free_dsems[kind].append((sem, b.dcnt[kind]))
                self.dma_sems.remove((b, kind))
            b.dsem = {}

    def close(self):
        for g in reversed(self.ctx):
            g.__exit__(None, None, None)
        self.ctx = []
        self.cbufs = []

    def _need(self, E, deps):
        need = {}
        for d in deps:
            if d is None:
                continue
            if d[0] == "dma":
                b, kind = d[1], d[2]
                if kind not in b.dsem:
                    continue
                key = ("dma", b.uid, kind)
                need[key] = ((b, kind), b.dcnt[kind])
            else:
                F, c = d
                if F == E and (not self.same or E == "pe" or c > self.cnt[E]):
                    continue
                if c > need.get(F, (None, 0))[1]:
                    need[F] = (None, c)
        for key, (bk, c) in need.items():
            if self.seen[E].get(key, 0) >= c:
                continue
            self.seen[E][key] = c
            if bk is not None:
                self.e[E].wait_ge(bk[0].dsem[bk[1]], c)
            else:
                self.e[E].wait_ge(self.sem[key], c)

    def op(self, E, fn, reads=(), writes=(), inc=True):
        deps = []
        for b in reads:
            deps.append(b.w)
        for b in writes:
            deps.append(b.w)
            deps.extend(b.r)
        self._need(E, deps)
        ins = fn()
        c = self.cnt[E] + 1
        if inc:
            ins.then_inc(self.sem[E], 1)
            self.cnt[E] = c
        for b in writes:
            b.w = (E, c)
            b.r = []
        for b in reads:
            if b not in writes:
                b.r = [x for x in b.r if x[0] != E] + [(E, c)]
        return ins

    def _dsem(self, owner, kind):
        if kind not in owner.dsem:
            if self.free_dsems[kind]:
                sem, cnt = self.free_dsems[kind].pop()
            else:
                self.nsem += 1
                sem, cnt = self.nc.alloc_semaphore("d%s_%d" % (kind, self.nsem)), 0
            owner.dsem[kind] = sem
            owner.dcnt[kind] = cnt
            self.dma_sems.append((owner, kind))
        return owner.dsem[kind]

    def dma(self, Q, out, in_, reads=(), writes=(), **kw):
        deps = []
        for b in reads:
            deps.append(b.w)
        for b in writes:
            deps.append(b.w)
            deps.extend(b.r)
        self._need(Q, deps)
        owner = (list(writes) + list(reads))[0]
        kind = "sw" if Q == "pool" else "hw"
        sem = self._dsem(owner, kind)
        ins = self.e[Q].dma_start(out=out, in_=in_, **kw)
        ins.then_inc(sem, 16)
        owner.dcnt[kind] += 16
        rec = ("dma", owner, kind)
        for b in writes:
            b.w = rec
            b.r = []
        for b in reads:
            if b not in writes:
                b.r = [x for x in b.r if not (x[0] == "dma" and x[1] is owner and x[2] == kind)] + [rec]
        return ins

    def barrier(self):
        for E in ENGS:
            for Fk in ENGS:
                if Fk == E:
                    continue
                c = self.cnt[Fk]
                if c and self.seen[E].get(Fk, 0) < c:
                    self.seen[E][Fk] = c
                    self.e[E].wait_ge(self.sem[Fk], c)
            for (b, kind) in self.dma_sems:
                key = ("dma", b.uid, kind)
                c = b.dcnt[kind]
                if c and self.seen[E].get(key, 0) < c:
                    self.seen[E][key] = c
                    self.e[E].wait_ge(b.dsem[kind], c)

    def finish(self):
        self.barrier()


D = 1024
DFF = 2816
NFC = DFF // 128
NKC = D // 128
EPS = 1e-6


def dram_in(nc, name, shape, dt=F32):
    return nc.dram_tensor(name, list(shape), dt, kind="ExternalInput").ap()


def dram_out(nc, name, shape, dt=F32):
    return nc.dram_tensor(name, list(shape), dt, kind="ExternalOutput").ap()


class RR:
    def __init__(self, items):
        self.items = items
        self.i = 0

    def next(self):
        b = self.items[self.i % len(self.items)]
        self.i += 1
        return b


def emit_norm(s, epsc, xt, hb_rr, scr, stat_rr, ntile):
    nc = s.nc
    st = stat_rr.next()
    hbs = [hb_rr.next() for _ in range(ntile)]
    for j in range(ntile):
        s.op("act", lambda: nc.scalar.activation(out=hbs[j][:, :], in_=xt[j][:, :], func=AF.Square, scale=1.0 / 32.0,
                                                 accum_out=st[:, j:j + 1]),
             reads=[xt[j]], writes=[hbs[j], st])
    s.op("act", lambda: nc.scalar.activation(out=st[:, 4:4 + ntile], in_=st[:, 0:ntile], func=AF.Ln, bias=epsc[:, 0:1], scale=1.0),
         reads=[st, epsc], writes=[st])
    s.op("act", lambda: nc.scalar.activation(out=st[:, 8:8 + ntile], in_=st[:, 4:4 + ntile], func=AF.Exp, scale=-0.5),
         reads=[st], writes=[st])
    for j in range(ntile):
        s.op("dve", lambda: nc.vector.tensor_scalar(out=hbs[j][:, :], in0=xt[j][:, :], scalar1=st[:, 8 + j:9 + j], scalar2=None,
                                                    op0=ALU.mult), reads=[xt[j], st], writes=[hbs[j]])
    return hbs


def emit_transpose_T(s, hbs, gcol, hT, ident_b, pT_rr, ntile, evac_engs=("dve",)):
    nc = s.nc
    k = 0
    for c in range(NKC):
        pT = pT_rr.next()
        for j in range(ntile):
            s.op("pe", lambda: nc.tensor.transpose(out=pT[:, j * 128:(j + 1) * 128], in_=hbs[j][:, c * 128:(c + 1) * 128],
                                                   identity=ident_b[:, :]),
                 reads=[hbs[j], ident_b], writes=[pT], inc=(j == ntile - 1))
        eng = evac_engs[k % len(evac_engs)]
        k += 1
        if eng == "dve":
            s.op("dve", lambda: nc.vector.tensor_scalar(out=hT[:, c, 0:ntile * 128], in0=pT[:, 0:ntile * 128],
                                                        scalar1=gcol[:, c:c + 1], scalar2=None, op0=ALU.mult),
                 reads=[pT, gcol], writes=[hT])
        else:
            s.op("act", lambda: nc.scalar.activation(out=hT[:, c, 0:ntile * 128], in_=pT[:, 0:ntile * 128],
                                                     func=AF.Copy, scale=gcol[:, c:c + 1]),
                 reads=[pT, gcol], writes=[hT])


def emit_rmsnorm_T(s, epsc, xt, gcol, hT, ident_b, pT_rr, hb_rr, scr, stat_rr, ntile, evac_engs=("dve",)):
    hbs = emit_norm(s, epsc, xt, hb_rr, scr, stat_rr, ntile)
    emit_transpose_T(s, hbs, gcol, hT, ident_b, pT_rr, ntile, evac_engs)


def build_ffn(NT, TB=512):
    nc = bass.Bass("TRN2", target_bir_lowering=False)
    x = dram_in(nc, "x", [NT, D])
    g = dram_in(nc, "g", [D])
    w_in = dram_in(nc, "w_in", [D, 2 * DFF])
    w_out = dram_in(nc, "w_out", [DFF, D])
    ident = dram_in(nc, "ident", [128, 128], BF16)
    y = dram_out(nc, "y", [NT, D])
    s = S(nc)
    emit_ffn(s, x, g, w_in, w_out, ident, y, NT, TB)
    s.finish()
    s.close()
    return nc


def emit_ffn(s, x, g, w_in, w_out, ident, y, NT, TB=512):
    nc = s.nc
    ntile = TB // 128
    m_ = s.mark()
    ident_b = s.sb("ident_b", [128, 128], BF16)
    s.dma("sp", ident_b[:, :], ident[:, :], writes=[ident_b])
    gcol = s.sb("gcol", [128, NKC], F32)
    epsc = s.sb("epsc", [128, 1], F32)
    s.op("dve", lambda: nc.vector.memset(epsc[:, :], EPS), writes=[epsc])
    s.dma("sp", gcol[:, :], g.rearrange("(c p) -> p c", p=128), writes=[gcol], allow_slow_non_contiguous=True)
    win_b = [s.sb("win_b%d" % c, [128, 2 * DFF], BF16) for c in range(NKC)]
    wout_b = [s.sb("wout_b%d" % f, [128, D], BF16) for f in range(NFC)]
    for c in range(NKC):
        for hf in range(2):
            s.dma("pool", win_b[c][:, hf * DFF:(hf + 1) * DFF], w_in[c * 128:(c + 1) * 128, hf * DFF:(hf + 1) * DFF],
                  writes=[win_b[c]])
    for f in range(NFC):
        s.dma("pool", wout_b[f][:, :], w_out[f * 128:(f + 1) * 128, :], writes=[wout_b[f]])
    xn = [s.sb("xn%d" % j, [128, D], F32) for j in range(ntile)]
    xr_rr = RR([s.sb("xr%d" % j, [128, D], F32) for j in range(1)])
    hb_rr = RR([s.sb("hb%d" % j, [128, D], BF16) for j in range(2 * ntile)])
    stat_rr = RR([s.sb("st%d" % j, [128, 16], F32) for j in range(4)])
    hT = s.sb("hT", [128, NKC, TB], BF16)
    pT_rr = RR([s.ps("pT%d" % j, [128, TB], BF16) for j in range(2)])
    pa_rr = RR([s.ps("pa%d" % j, [128, TB], F32) for j in range(2)])
    pb_rr = RR([s.ps("pb%d" % j, [128, TB], F32) for j in range(2)])
    po_rr = RR([s.ps("po%d" % j, [128, 512], F32) for j in range(2)])
    sa_rr = RR([s.sb("sa%d" % j, [128, TB], F32) for j in range(2)])
    act = [s.sb("actT%d" % f, [128, TB], BF16) for f in range(NFC)]
    nblk = NT // TB

    def prep_a_rot(tb):
        for j in range(ntile):
            r0 = tb * TB + j * 128
            s.dma("sp", xn[j][:, :], x[r0:r0 + 128, :], writes=[xn[j]])
        return emit_norm(s, epsc, xn, hb_rr, None, stat_rr, ntile)

    hbs_next = prep_a_rot(0)
    emit_transpose_T(s, hbs_next, gcol, hT, ident_b, pT_rr, ntile)
    for tb in range(nblk):
        if tb + 1 < nblk:
            hbs_next = prep_a_rot(tb + 1)
        for f in range(NFC):
            pa = pa_rr.next()
            pb = pb_rr.next()
            for c in range(NKC):
                s.op("pe", lambda: nc.tensor.matmul(pa[:, :], lhsT=win_b[c][:, f * 128:(f + 1) * 128], rhs=hT[:, c, :],
                                                    start=(c == 0), stop=(c == NKC - 1)),
                     reads=[win_b[c], hT], writes=[pa], inc=(c == NKC - 1))
            for c in range(NKC):
                s.op("pe", lambda: nc.tensor.matmul(pb[:, :], lhsT=win_b[c][:, DFF + f * 128:DFF + (f + 1) * 128],
                                                    rhs=hT[:, c, :], start=(c == 0), stop=(c == NKC - 1)),
                     reads=[win_b[c], hT], writes=[pb], inc=(c == NKC - 1))
            sa = sa_rr.next()
            s.op("act", lambda: nc.scalar.activation(out=sa[:, :], in_=pa[:, :], func=AF.Silu), reads=[pa], writes=[sa])
            s.op("dve", lambda: nc.vector.tensor_tensor(out=act[f][:, :], in0=sa[:, :], in1=pb[:, :], op=ALU.mult),
                 reads=[sa, pb], writes=[act[f]])
        if tb + 1 < nblk:
            emit_transpose_T(s, hbs_next, gcol, hT, ident_b, pT_rr, ntile)
        for j in range(ntile):
            r0 = tb * TB + j * 128
            xr = xr_rr.next()
            s.dma("sp", xr[:, :], x[r0:r0 + 128, :], writes=[xr])
            for hf in range(2):
                po = po_rr.next()
                for f in range(NFC):
                    s.op("pe", lambda: nc.tensor.matmul(po[:, :], lhsT=act[f][:, j * 128:(j + 1) * 128],
                                                        rhs=wout_b[f][:, hf * 512:(hf + 1) * 512],
                                                        start=(f == 0), stop=(f == NFC - 1)),
                         reads=[act[f], wout_b[f]], writes=[po], inc=(f == NFC - 1))
                s.op("dve", lambda: nc.vector.scalar_tensor_tensor(out=xr[:, hf * 512:(hf + 1) * 512], in0=po[:, :],
                                                                   scalar=0.5, in1=xr[:, hf * 512:(hf + 1) * 512],
                                                                   op0=ALU.mult, op1=ALU.add),
                     reads=[po, xr], writes=[xr])
            s.dma("sp", y[r0:r0 + 128, :], xr[:, :], reads=[xr])
    s.release(m_)


FM_SRC = [[(0, 128)], [(128, 128)], [(256, 128)], [(384, 128)],
          [(768, 128)], [(896, 128)], [(1024, 128)], [(1152, 128)],
          [(1540, 128)], [(1668, 128)], [(1924, 64), (2052, 64)], [(1796, 128)]]
FM_GCOL = [0, 0, 1, 1, 2, 2, 3, 3, 4, 4, 5, None]
FM_BLK = [0, 0, 0, 0, 1, 1, 1, 1, 1, 1, 1, None]
TM_SRC = [[(512, 256), (1280, 256)],
          [(1536, 4), (2180, 12), (1988, 64), (2116, 64)],
          [(2192, 512)]]
NFM = 12
GELU_C = 1.5957691216057308


def build_proj(NT, TB=512):
    nc = bass.Bass("TRN2", target_bir_lowering=False)
    a = dict(
        x=dram_in(nc, "x", [NT, D]), g=dram_in(nc, "g", [D]), w_in=dram_in(nc, "w_in", [D, 6800]),
        ident=dram_in(nc, "ident", [128, 128], BF16), gains=dram_in(nc, "gains", [128, 6]),
        blk=dram_in(nc, "blk", [2, 128, 128], BF16), vgain=dram_in(nc, "vgain", [128, 256]),
        wsT=dram_in(nc, "wsT", [4, 128, 128]), triu=dram_in(nc, "triu", [128, 128]), bsT=dram_in(nc, "bsT", [128, 4]),
        zfm=dram_out(nc, "zfm", [NFM, 128, NT], BF16), vab=dram_out(nc, "vab", [NT, 512], BF16),
        vsw=dram_out(nc, "vsw", [NT, 128], BF16), misc=dram_out(nc, "misc", [NT, 16]), od=dram_out(nc, "od", [NT, 256], BF16))
    s = S(nc)
    emit_proj(s, a, NT, TB)
    s.finish()
    s.close()
    return nc


def emit_proj(s, a, NT, TB=512):
    nc = s.nc
    x, g, w_in, ident, gains, blk, vgain, wsT, triu, bsT = (a[k] for k in
                                                            ("x", "g", "w_in", "ident", "gains", "blk", "vgain", "wsT", "triu", "bsT"))
    zfm, vab, vsw, misc, od = (a[k] for k in ("zfm", "vab", "vsw", "misc", "od"))
    m_ = s.mark()
    ntile = TB // 128
    ident_b = s.sb("ident_b", [128, 128], BF16)
    s.dma("sp", ident_b[:, :], ident[:, :], writes=[ident_b])
    gcol = s.sb("gcol", [128, NKC], F32)
    s.dma("sp", gcol[:, :], g.rearrange("(c p) -> p c", p=128), writes=[gcol], allow_slow_non_contiguous=True)
    epsc = s.sb("epsc", [128, 1], F32)
    s.op("dve", lambda: nc.vector.memset(epsc[:, :], EPS), writes=[epsc])
    gn = s.sb("gn", [128, 6], F32)
    s.dma("sp", gn[:, :], gains[:, :], writes=[gn])
    for col, sc in ((0, 32.0 ** -0.5), (2, 0.125), (4, 0.125)):
        s.op("dve", lambda: nc.vector.tensor_scalar(out=gn[:, col:col + 1], in0=gn[:, col:col + 1], scalar1=sc,
                                                    scalar2=None, op0=ALU.mult), reads=[gn], writes=[gn])
    blk_b = [s.sb("blk%d" % i, [128, 128], BF16) for i in range(2)]
    for i in range(2):
        s.dma("sp", blk_b[i][:, :], blk[i], writes=[blk_b[i]])
    vg = s.sb("vg", [128, 256], F32)
    s.dma("sp", vg[:, :], vgain[:, :], writes=[vg])
    bcol = s.sb("bcol", [128, 4], F32)
    s.dma("sp", bcol[:, :], bsT[:, :], writes=[bcol])
    tri = s.sb("tri", [128, 128], F32)
    s.dma("sp", tri[:, :], triu[:, :], writes=[tri])
    wm = []
    wtmp = s.sb("wtmp", [128, 128], F32)
    for gi in range(4):
        w = s.sb("wm%d" % gi, [128, 128], BF16)
        s.dma("sp", wtmp[:, :], wsT[gi], writes=[wtmp])
        s.op("dve", lambda: nc.vector.tensor_tensor(out=w[:, :], in0=wtmp[:, :], in1=tri[:, :], op=ALU.mult),
             reads=[wtmp, tri], writes=[w])
        wm.append(w)
    wfm = [s.sb("wfm%d" % c, [128, NFM * 128], BF16) for c in range(NKC)]
    wtm = [s.sb("wtm%d" % c, [128, 1168], BF16) for c in range(NKC)]
    FM_RUNS = [(0, 0, 512), (512, 768, 512), (1024, 1540, 256), (1280, 1924, 64), (1344, 2052, 64), (1408, 1796, 128)]
    TM_RUNS = [(0, 512, 256), (256, 1280, 256), (512, 1536, 4), (516, 2180, 12), (528, 1988, 64), (592, 2116, 64), (656, 2192, 512)]
    for c in range(NKC):
        for (o, c0, n) in FM_RUNS:
            s.dma("pool", wfm[c][:, o:o + n], w_in[c * 128:(c + 1) * 128, c0:c0 + n], writes=[wfm[c]])
        for (o, c0, n) in TM_RUNS:
            s.dma("pool", wtm[c][:, o:o + n], w_in[c * 128:(c + 1) * 128, c0:c0 + n], writes=[wtm[c]])
    xts = [[s.sb("xt%d_%d" % (k, j), [128, D], F32) for j in range(ntile)] for k in range(2)]
    hb_rr = RR([s.sb("hb%d" % j, [128, D], BF16) for j in range(2 * ntile)])
    scr = s.sb("scr", [128, D], BF16)
    stat_rr = RR([s.sb("st%d" % j, [128, 16], F32) for j in range(4)])
    hTs = [s.sb("hT%d" % k, [128, NKC, TB], BF16) for k in range(2)]
    pT_rr = RR([s.ps("pT%d" % j, [128, TB], BF16) for j in range(2)])
    pz_rr = RR([s.ps("pz%d" % j, [128, 512], F32) for j in range(2)])
    ptm_rr = RR([s.ps("ptm%d" % j, [128, 512], F32) for j in range(2)])
    pq_rr = RR([s.ps("pq%d" % j, [128, 512], F32) for j in range(2)])
    sq_rr = RR([s.sb("sq%d" % j, [128, TB], BF16) for j in range(4)])
    rs_rr = RR([s.sb("rs%d" % j, [128, TB], F32) for j in range(2)])
    zo_rr = RR([s.sb("zo%d" % j, [128, TB], BF16) for j in range(4)])
    vab_rr = RR([s.sb("vabt%d" % j, [128, 512], BF16) for j in range(2)])
    vsw_rr = RR([s.sb("vswt%d" % j, [128, 128], BF16) for j in range(2)])
    msc_rr = RR([s.sb("msct%d" % j, [128, 16], F32) for j in range(2)])
    f_rr = RR([s.sb("gf%d" % j, [128, 512], F32) for j in range(4)])
    ge_rr = RR([s.sb("ge%d" % j, [128, 512], F32) for j in range(3)])
    zs_rr = RR([s.sb("zs%d" % j, [128, 512], F32) for j in range(2)])
    zf_rr = RR([s.sb("zf%d" % j, [128, TB], F32) for j in range(3)])
    vn_rr = RR([s.sb("vn%d" % j, [128, 256], BF16) for j in range(3)])
    od_rr = RR([s.sb("odt%d" % j, [128, 256], BF16) for j in range(2)])
    nblk = NT // TB

    def prep(tb):
        xt = xts[tb % 2]
        for j in range(ntile):
            s.dma("sp", xt[j][:, :], x[tb * TB + j * 128:tb * TB + (j + 1) * 128, :], writes=[xt[j]])
        emit_rmsnorm_T(s, epsc, xt, gcol, hTs[tb % 2], ident_b, pT_rr, hb_rr, scr, stat_rr, ntile)

    prep(0)
    pipe = Pipe(2)
    for tb in range(nblk):
        t0 = tb * TB
        hT = hTs[tb % 2]
        for i in range(NFM):
            pz = pz_rr.next()
            for c in range(NKC):
                s.op("pe", lambda: nc.tensor.matmul(pz[:, 0:TB], lhsT=wfm[c][:, i * 128:(i + 1) * 128], rhs=hT[:, c, :],
                                                    start=(c == 0), stop=(c == NKC - 1)),
                     reads=[wfm[c], hT], writes=[pz], inc=(c == NKC - 1))
            zo = zo_rr.next()
            if FM_GCOL[i] is None:
                s.op("dve", lambda: nc.vector.tensor_copy(out=zo[:, :], in_=pz[:, 0:TB]), reads=[pz], writes=[zo])
                s.dma("sp", zfm[i, :, t0:t0 + TB], zo[:, :], reads=[zo])
            else:
                zf = zf_rr.next()
                s.op("dve", lambda: nc.vector.tensor_copy(out=zf[:, :], in_=pz[:, 0:TB]), reads=[pz], writes=[zf])
                sq = sq_rr.next()
                s.op("act", lambda: nc.scalar.activation(out=sq[:, :], in_=zf[:, :], func=AF.Square),
                     reads=[zf], writes=[sq])

                def back(i=i, zf=zf, sq=sq, zo=zo, t0=t0):
                    gs = 32.0 if FM_BLK[i] == 0 else 64.0
                    pq = pq_rr.next()
                    s.op("pe", lambda: nc.tensor.matmul(pq[:, 0:TB], lhsT=blk_b[FM_BLK[i]][:, :], rhs=sq[:, :],
                                                        start=True, stop=True), reads=[blk_b[FM_BLK[i]], sq], writes=[pq])
                    rs = rs_rr.next()
                    s.op("act", lambda: nc.scalar.activation(out=rs[:, :], in_=pq[:, 0:TB], func=AF.Ln, bias=epsc[:, 0:1],
                                                             scale=1.0 / gs), reads=[pq, epsc], writes=[rs])
                    s.op("act", lambda: nc.scalar.activation(out=rs[:, :], in_=rs[:, :], func=AF.Exp, scale=-0.5),
                         reads=[rs], writes=[rs])
                    gc = FM_GCOL[i]
                    s.op("dve", lambda: nc.vector.scalar_tensor_tensor(out=zo[:, :], in0=zf[:, :], scalar=gn[:, gc:gc + 1],
                                                                       in1=rs[:, :], op0=ALU.mult, op1=ALU.mult),
                         reads=[zf, gn, rs], writes=[zo])
                    s.dma("sp", zfm[i, :, t0:t0 + TB], zo[:, :], reads=[zo])
                pipe.push(back)
        if tb + 1 < nblk:
            prep(tb + 1)
        for j in range(ntile):
            r0 = t0 + j * 128
            pz = ptm_rr.next()
            for c in range(NKC):
                s.op("pe", lambda: nc.tensor.matmul(pz[:, :], lhsT=hT[:, c, j * 128:(j + 1) * 128], rhs=wtm[c][:, 0:512],
                                                    start=(c == 0), stop=(c == NKC - 1)),
                     reads=[wtm[c], hT], writes=[pz], inc=(c == NKC - 1))
            vt = vab_rr.next()
            s.op("act", lambda: nc.scalar.copy(out=vt[:, :], in_=pz[:, :]), reads=[pz], writes=[vt])
            s.dma("sp", vab[r0:r0 + 128, :], vt[:, :], reads=[vt])
            pz = ptm_rr.next()
            for c in range(NKC):
                s.op("pe", lambda: nc.tensor.matmul(pz[:, 0:144], lhsT=hT[:, c, j * 128:(j + 1) * 128], rhs=wtm[c][:, 512:656],
                                                    start=(c == 0), stop=(c == NKC - 1)),
                     reads=[wtm[c], hT], writes=[pz], inc=(c == NKC - 1))
            mt = msc_rr.next()
            vs_ = vsw_rr.next()
            s.op("dve", lambda: nc.vector.tensor_copy(out=mt[:, :], in_=pz[:, 0:16]), reads=[pz], writes=[mt])
            s.op("dve", lambda: nc.vector.tensor_copy(out=vs_[:, :], in_=pz[:, 16:144]), reads=[pz], writes=[vs_])
            s.dma("sp", misc[r0:r0 + 128, :], mt[:, :], reads=[mt])
            s.dma("sp", vsw[r0:r0 + 128, :], vs_[:, :], reads=[vs_])
            pz = ptm_rr.next()
            for c in range(NKC):
                s.op("pe", lambda: nc.tensor.matmul(pz[:, :], lhsT=hT[:, c, j * 128:(j + 1) * 128], rhs=wtm[c][:, 656:1168],
                                                    start=(c == 0), stop=(c == NKC - 1)),
                     reads=[wtm[c], hT], writes=[pz], inc=(c == NKC - 1))
            zs = zs_rr.next()
            s.op("act", lambda: nc.scalar.copy(out=zs[:, :], in_=pz[:, :]), reads=[pz], writes=[zs])
            z2 = f_rr.next()
            s.op("act", lambda: nc.scalar.activation(out=z2[:, :], in_=zs[:, :], func=AF.Square), reads=[zs], writes=[z2])
            s.op("dve", lambda: nc.vector.tensor_scalar(out=z2[:, :], in0=z2[:, :], scalar1=0.044715, scalar2=1.0,
                                                        op0=ALU.mult, op1=ALU.add), reads=[z2], writes=[z2])
            s.op("dve", lambda: nc.vector.tensor_tensor(out=z2[:, :], in0=z2[:, :], in1=zs[:, :], op=ALU.mult),
                 reads=[z2, zs], writes=[z2])
            s.op("act", lambda: nc.scalar.activation(out=z2[:, :], in_=z2[:, :], func=AF.Exp, scale=-GELU_C),
                 reads=[z2], writes=[z2])
            s.op("act", lambda: nc.scalar.activation(out=z2[:, :], in_=z2[:, :], func=AF.Ln, bias=1.0, scale=1.0),
                 reads=[z2], writes=[z2])
            s.op("act", lambda: nc.scalar.activation(out=z2[:, :], in_=z2[:, :], func=AF.Exp, scale=-1.0),
                 reads=[z2], writes=[z2])
            ge = ge_rr.next()
            s.op("dve", lambda: nc.vector.tensor_tensor(out=ge[:, :], in0=z2[:, :], in1=zs[:, :], op=ALU.mult),
                 reads=[z2, zs], writes=[ge])
            sqv = f_rr.next()
            st = stat_rr.next()
            s.op("act", lambda: nc.scalar.activation(out=sqv[:, 0:256], in_=ge[:, 256:512], func=AF.Square),
                 reads=[ge], writes=[sqv])
            s.op("dve", lambda: nc.vector.tensor_reduce(out=st[:, 0:4], in_=sqv[:, 0:256].rearrange("p (g d) -> p g d", g=4),
                                                        axis=AX.X, op=ALU.add), reads=[sqv], writes=[st])
            s.op("act", lambda: nc.scalar.activation(out=st[:, 0:4], in_=st[:, 0:4], func=AF.Ln, bias=epsc[:, 0:1],
                                                     scale=1.0 / 64.0), reads=[st, epsc], writes=[st])
            s.op("act", lambda: nc.scalar.activation(out=st[:, 0:4], in_=st[:, 0:4], func=AF.Exp, scale=-0.5),
                 reads=[st], writes=[st])
            vn = vn_rr.next()
            for gi in range(4):
                s.op("dve", lambda: nc.vector.scalar_tensor_tensor(
                    out=vn[:, gi * 64:(gi + 1) * 64], in0=ge[:, 256 + gi * 64:256 + (gi + 1) * 64], scalar=st[:, gi:gi + 1],
                    in1=vg[:, gi * 64:(gi + 1) * 64], op0=ALU.mult, op1=ALU.mult), reads=[ge, st, vg], writes=[vn])

            def back2(vn=vn, ge=ge, r0=r0):
                pq = pq_rr.next()
                for gi in range(4):
                    s.op("pe", lambda: nc.tensor.matmul(pq[:, gi * 64:(gi + 1) * 64], lhsT=wm[gi][:, :],
                                                        rhs=vn[:, gi * 64:(gi + 1) * 64], start=True, stop=True),
                         reads=[wm[gi], vn], writes=[pq], inc=(gi == 3))
                ot = od_rr.next()
                for gi in range(4):
                    s.op("dve", lambda: nc.vector.scalar_tensor_tensor(
                        out=ot[:, gi * 64:(gi + 1) * 64], in0=pq[:, gi * 64:(gi + 1) * 64], scalar=bcol[:, gi:gi + 1],
                        in1=ge[:, gi * 64:(gi + 1) * 64], op0=ALU.add, op1=ALU.mult), reads=[pq, bcol, ge], writes=[ot])
                s.dma("sp", od[r0:r0 + 128, :], ot[:, :], reads=[ot])
            pipe.push(back2)
    pipe.flush()
    s.release(m_)


def _bf(a):
    import ml_dtypes
    return np.ascontiguousarray(a).astype(ml_dtypes.bfloat16)


def proj_consts():
    blk = np.zeros((2, 128, 128), np.float32)
    for i in range(128):
        for j in range(128):
            if i // 32 == j // 32:
                blk[0, i, j] = 1
            if i // 64 == j // 64:
                blk[1, i, j] = 1
    triu = np.triu(np.ones((128, 128), np.float32))
    return dict(ident=_bf(np.eye(128, dtype=np.float32)), blk=_bf(blk), triu=triu)


def proj_params(g, w_in, dq, dk, fq, fk, nq, nk, vgain, w_s, b_s):
    gains = np.stack([np.tile(dq, 4), np.tile(dk, 4), np.tile(fq, 2), np.tile(fk, 2), np.tile(nq, 2), np.tile(nk, 2)], 1)
    return dict(g=np.ascontiguousarray(g), w_in=np.ascontiguousarray(w_in), gains=np.ascontiguousarray(gains, dtype=np.float32),
                vgain=np.ascontiguousarray(np.broadcast_to(vgain[None, :], (128, 256))),
                wsT=np.ascontiguousarray(w_s.transpose(0, 2, 1)), bsT=np.ascontiguousarray(b_s.T))


NEG = -30000.0


def load_vt(s, vt, io, key, T, init=True, load=True):
    nc = s.nc
    NT = T // 128
    if init:
        s.op("pool", lambda: nc.gpsimd.memset(vt[:, :, 64:128], 0.0), writes=[vt])
        s.op("pool", lambda: nc.gpsimd.memset(vt[:, :, 64:65], 1.0), writes=[vt])
    if not load:
        return
    if key + "_src" in io:
        src = io[key + "_src"]
        step = 8
        for j0 in range(0, NT, step):
            j1 = min(NT, j0 + step)
            s.dma("sp", vt[:, j0:j1, 0:64], src[j0 * 128:j1 * 128, :].rearrange("(j p) d -> p j d", p=128), writes=[vt])
    else:
        s.dma("sp", vt[:, :, 0:64], io[key][:, :, 0:64], writes=[vt])


class Pipe:
    def __init__(self, lag):
        self.q = []
        self.lag = lag

    def push(self, fn):
        self.q.append(fn)
        while len(self.q) > self.lag:
            self.q.pop(0)()

    def flush(self):
        while self.q:
            self.q.pop(0)()


def emit_attn_phase(s, cm, T, nsub, qT, kT, vt, kparts, out_dram, finalize, bias_fn=None, name="a", lag=2):
    nc = s.nc
    NQB = T // 512
    pipe = Pipe(lag)
    fin_pending = None
    for qb in range(NQB):
        q0 = qb * 512
        pos = [cm["po_rr"].next() for _ in range(nsub)]
        nt = 4 * qb + 4
        njob = 0
        for t in range(nt):
            di = t - 4 * qb
            c0 = 128 * di if di > 0 else 0
            for i in range(nsub):
                kz = kT[i]
                ps = cm["ps_rr"].next()
                s.op("pe", lambda: nc.tensor.matmul(ps[:, c0:512], lhsT=kz[:, t * 128:(t + 1) * 128],
                                                    rhs=qT[:, q0 + c0:q0 + 512], start=True, stop=(di < 0)),
                     reads=[kz, qT], writes=[ps], inc=(di < 0))
                if di >= 0:
                    s.op("pe", lambda: nc.tensor.matmul(ps[:, c0:c0 + 128], lhsT=cm["ident_b"][:, :], rhs=cm["tri_b"][:, :],
                                                        start=False, stop=True),
                         reads=[cm["ident_b"], cm["tri_b"]], writes=[ps])
                pt = cm["pt_rr"].next()
                if bias_fn is None:
                    s.op("act", lambda: nc.scalar.activation(out=pt[:, c0:512], in_=ps[:, c0:512], func=AF.Exp),
                         reads=[ps], writes=[pt])
                else:
                    bb, bap = bias_fn(qb, t)
                    s.op("act", lambda: nc.scalar.activation(out=pt[:, c0:512], in_=ps[:, c0:512], func=AF.Exp, bias=bap),
                         reads=[ps, bb], writes=[pt])

                def pv(po=pos[i], t=t, c0=c0, pt=pt, nt=nt):
                    s.op("pe", lambda: nc.tensor.matmul(po[:, c0:512], lhsT=vt[:, t, :], rhs=pt[:, c0:512],
                                                        start=(t == 0), stop=(t == nt - 1)),
                         reads=[vt, pt], writes=[po])
                pipe.push(pv)
                njob += 1
                if fin_pending is not None and njob == lag:
                    fin_pending()
                    fin_pending = None
        if fin_pending is not None:
            pipe.flush()
            fin_pending()
        fin_pending = (lambda qb=qb, pos=pos: finalize(qb, pos))
    pipe.flush()
    if fin_pending is not None:
        fin_pending()


def emit_o_to_tokmajor(s, cm, po, pf, col0):
    nc = s.nc
    oc = cm["oc_rr"].next()
    s.op("dve", lambda: nc.vector.tensor_copy(out=oc[0:65, :], in_=po[0:65, :]), reads=[po], writes=[oc])
    for j in range(4):
        s.op("pe", lambda: nc.tensor.transpose(out=pf[:, j, col0:col0 + 65], in_=oc[0:65, j * 128:(j + 1) * 128],
                                               identity=cm["ident_f"][0:65, 0:65]),
             reads=[oc, cm["ident_f"]], writes=[pf], inc=(j == 3))


def build_mix_ab(T):
    nc = bass.Bass("TRN2", target_bir_lowering=False)
    io = mix_decl(nc, T, with_c=False)
    s = S(nc)
    cm = mix_common(s, io)
    emit_mix_a(s, cm, io, T)
    emit_mix_b(s, cm, io, T)
    s.finish()
    s.close()
    return nc


def mix_decl(nc, T, with_c=True):
    NT = T // 128
    io = dict(
        identb=dram_in(nc, "identb", [128, 128], BF16), identf=dram_in(nc, "identf", [128, 128]),
        trib=dram_in(nc, "trib", [128, 128], BF16),
        qa=dram_in(nc, "qa", [64, T], BF16), ka=dram_in(nc, "ka", [64, T], BF16), va=dram_in(nc, "va", [128, NT, 65], BF16),
        lamp=dram_in(nc, "lamp", [128, 4, 32]), lami=dram_in(nc, "lami", [128, 2]),
        qb=dram_in(nc, "qb", [64, T], BF16), kb=dram_in(nc, "kb", [64, T], BF16), vb=dram_in(nc, "vb", [128, NT, 65], BF16),
        flog=dram_in(nc, "flog", [128, NT]), fbias=dram_in(nc, "fbias", [128, 1]),
        triuf=dram_in(nc, "triuf", [128, 128]), onesf=dram_in(nc, "onesf", [128, 128]),
        oa=dram_out(nc, "oa", [T, 64], BF16), ob=dram_out(nc, "ob", [T, 64], BF16),
    )
    return io


def mix_common(s, io, n_ps=3, with_pf2=True):
    nc = s.nc
    cm = {}
    for nm, key, dt in (("ident_b", "identb", BF16), ("ident_f", "identf", F32), ("tri_b", "trib", BF16)):
        b = s.sb(nm, [128, 128], dt)
        s.dma("sp", b[:, :], io[key][:, :], writes=[b])
        cm[nm] = b
    cm["epsc"] = s.sb("epsc", [128, 1], F32)
    s.op("dve", lambda: nc.vector.memset(cm["epsc"][:, :], EPS), writes=[cm["epsc"]])
    cm["ps_rr"] = RR([s.ps("ps%d" % j, [128, 512], F32) for j in range(n_ps)])
    cm["po_rr"] = RR([s.ps("po%d" % j, [128, 512], F32) for j in range(3)])
    cm["pf"] = s.ps("pf", [128, 4, 128], F32)
    if with_pf2:
        cm["pf2"] = s.ps("pf2", [128, 4, 128], F32)
    cm["lag"] = n_ps - 1
    cm["o1s_rr"] = RR([s.sb("o1s%d" % j, [128, 4, 65], F32) for j in range(2)])
    cm["pt_rr"] = RR([s.sb("pt%d" % j, [128, 512], BF16) for j in range(n_ps + 2)])
    cm["oc_rr"] = RR([s.sb("oc%d" % j, [128, 512], F32) for j in range(2)])
    cm["st_rr"] = RR([s.sb("mst%d" % j, [128, 8], F32) for j in range(8)])
    cm["ot_rr"] = RR([s.sb("ot%d" % j, [128, 4, 64], BF16) for j in range(2)])
    cm["tmp_rr"] = RR([s.sb("tmp%d" % j, [128, 64], F32) for j in range(4)])
    return cm


def emit_mix_a(s, cm, io, T, stage="all", bufs=None):
    nc = s.nc
    NT = T // 128
    if stage in ("all", "alloc"):
        if stage == "all":
            m = s.mark()
        b = dict(qT=s.sb("a_q", [128, T], BF16), k1=s.sb("a_k1", [128, T], BF16), k2=s.sb("a_k2", [128, T], BF16),
                 vt=s.sb("a_v", [128, NT, 128], BF16), lp=s.sb("lp", [128, 4, 32], F32), li=s.sb("li", [128, 2], F32),
                 lw=s.sb("lw", [128, 2, 32], F32), lam=s.sb("lam", [128, 4], F32))
        s.op("pool", lambda: nc.gpsimd.memset(b["qT"][64:128, :], 0.0), writes=[b["qT"]])
        s.op("dve", lambda: nc.vector.memset(b["k1"][:, :], 0.0), writes=[b["k1"]])
        s.op("pool", lambda: nc.gpsimd.memset(b["k2"][:, :], 0.0), writes=[b["k2"]])
        load_vt(s, b["vt"], io, "va", T, init=True, load=False)
        if stage == "alloc":
            return b
        bufs = b
    qT, k1, k2, vt, lp, li, lw, lam = (bufs[k] for k in ("qT", "k1", "k2", "vt", "lp", "li", "lw", "lam"))
    if stage in ("all", "load"):
        s.dma("sp", qT[0:64, :], io["qa"][:, :], writes=[qT])
        s.dma("sp", k1[0:32, :], io["ka"][0:32, :], writes=[k1])
        s.dma("sp", k2[32:64, :], io["ka"][32:64, :], writes=[k2])
        load_vt(s, vt, io, "va", T, init=False)
        s.dma("sp", lp[:, :, :], io["lamp"][:, :, :], writes=[lp])
        s.dma("sp", li[:, :], io["lami"][:, :], writes=[li])
        s.op("dve", lambda: nc.vector.tensor_tensor(out=lw[:, 0, :], in0=lp[:, 0, :], in1=lp[:, 1, :], op=ALU.mult),
             reads=[lp], writes=[lw])
        s.op("dve", lambda: nc.vector.tensor_tensor(out=lw[:, 1, :], in0=lp[:, 2, :], in1=lp[:, 3, :], op=ALU.mult),
             reads=[lp], writes=[lw])
        s.op("dve", lambda: nc.vector.tensor_reduce(out=lam[:, 0:2], in_=lw[:, :, :], axis=AX.X, op=ALU.add),
             reads=[lw], writes=[lam])
        s.op("act", lambda: nc.scalar.activation(out=lam[:, 0:2], in_=lam[:, 0:2], func=AF.Exp), reads=[lam], writes=[lam])
        s.op("dve", lambda: nc.vector.tensor_tensor(out=lam[:, 2:3], in0=lam[:, 1:2], in1=lam[:, 0:1], op=ALU.subtract),
             reads=[lam], writes=[lam])
        s.op("dve", lambda: nc.vector.tensor_tensor(out=lam[:, 3:4], in0=lam[:, 2:3], in1=li[:, 0:1], op=ALU.subtract),
             reads=[lam, li], writes=[lam])
        if stage == "load":
            return

    def fin(qb, pos):
        if "pf2" in cm:
            pf = cm["pf"]
            pf2 = cm["pf2"]
            emit_o_to_tokmajor(s, cm, pos[0], pf, 0)
            emit_o_to_tokmajor(s, cm, pos[1], pf2, 0)
        else:
            pf2 = cm["pf"]
            emit_o_to_tokmajor(s, cm, pos[0], pf2, 0)
            pf = cm["o1s_rr"].next()
            s.op("dve", lambda: nc.vector.tensor_copy(out=pf[:, :, :], in_=pf2[:, :, 0:65]), reads=[pf2], writes=[pf])
            emit_o_to_tokmajor(s, cm, pos[1], pf2, 0)
        ot = cm["ot_rr"].next()
        for j in range(4):
            st = cm["st_rr"].next()
            s.op("dve", lambda: nc.vector.tensor_scalar(out=st[:, 0:1], in0=pf[:, j, 64:65], scalar1=1e-30, scalar2=None,
                                                        op0=ALU.max), reads=[pf], writes=[st])
            s.op("dve", lambda: nc.vector.tensor_scalar(out=st[:, 1:2], in0=pf2[:, j, 64:65], scalar1=1e-30, scalar2=None,
                                                        op0=ALU.max), reads=[pf2], writes=[st])
            s.op("dve", lambda: nc.vector.reciprocal(out=st[:, 0:2], in_=st[:, 0:2]), reads=[st], writes=[st])
            t2 = cm["tmp_rr"].next()
            o = cm["tmp_rr"].next()
            s.op("dve", lambda: nc.vector.tensor_scalar(out=t2[:, :], in0=pf2[:, j, 0:64], scalar1=st[:, 1:2],
                                                        scalar2=lam[:, 3:4], op0=ALU.mult, op1=ALU.mult),
                 reads=[pf2, st, lam], writes=[t2])
            s.op("dve", lambda: nc.vector.scalar_tensor_tensor(out=o[:, :], in0=pf[:, j, 0:64], scalar=st[:, 0:1], in1=t2[:, :],
                                                               op0=ALU.mult, op1=ALU.add), reads=[pf, st, t2], writes=[o])
            s.op("act", lambda: nc.scalar.activation(out=t2[:, :], in_=o[:, :], func=AF.Square, accum_out=st[:, 2:3]),
                 reads=[o], writes=[t2, st])
            s.op("act", lambda: nc.scalar.activation(out=st[:, 3:4], in_=st[:, 2:3], func=AF.Ln, bias=cm["epsc"][:, 0:1],
                                                     scale=1.0 / 64.0), reads=[st, cm["epsc"]], writes=[st])
            s.op("act", lambda: nc.scalar.activation(out=st[:, 4:5], in_=st[:, 3:4], func=AF.Exp, scale=-0.5),
                 reads=[st], writes=[st])
            s.op("dve", lambda: nc.vector.tensor_scalar(out=ot[:, j, :], in0=o[:, :], scalar1=st[:, 4:5], scalar2=li[:, 1:2],
                                                        op0=ALU.mult, op1=ALU.mult), reads=[o, st, li], writes=[ot])
        s.dma("sp", io["oa"][qb * 512:(qb + 1) * 512, :].rearrange("(j p) d -> p j d", p=128), ot[:, :, :], reads=[ot])

    emit_attn_phase(s, cm, T, 2, qT, [k1, k2], vt, None, io["oa"], fin, name="a", lag=cm["lag"])
    if stage == "all":
        s.release(m)


def emit_mix_b(s, cm, io, T, stage="all", bufs=None):
    nc = s.nc
    NT = T // 128
    NQB = T // 512
    if stage in ("all", "alloc"):
        if stage == "all":
            m = s.mark()
        b = dict(qT=s.sb("b_q", [128, T], BF16), kT=s.sb("b_k", [128, T], BF16), vt=s.sb("b_v", [128, NT, 128], BF16),
                 fl=s.sb("fl", [128, NT], F32), fb=s.sb("fb", [128, 2], F32), tu=s.sb("tu", [128, 128], F32),
                 on=s.sb("on", [128, 128], F32), cc=s.sb("cc", [128, NT], F32), inc=s.sb("inc", [128, NT], F32),
                 tmpc=s.sb("tmpc", [128, NT], F32), btab=s.sb("btab", [128, NQB, NT], F32))
        s.op("pool", lambda: nc.gpsimd.memset(b["qT"][64:128, :], 0.0), writes=[b["qT"]])
        s.op("dve", lambda: nc.vector.memset(b["kT"][64:128, :], 0.0), writes=[b["kT"]])
        load_vt(s, b["vt"], io, "vb", T, init=True, load=False)
        s.dma("sp", b["tu"][:, :], io["triuf"][:, :], writes=[b["tu"]])
        s.dma("sp", b["on"][:, :], io["onesf"][:, :], writes=[b["on"]])
        if stage == "alloc":
            return b
        bufs = b
    qT, kT, vt, fl, fb, tu, on, cc, inc_, tmpc, btab = (bufs[k] for k in ("qT", "kT", "vt", "fl", "fb", "tu", "on", "cc", "inc",
                                                                            "tmpc", "btab"))
    if stage in ("all", "load"):
        s.dma("sp", qT[0:64, :], io["qb"][:, :], writes=[qT])
        s.dma("sp", kT[0:64, :], io["kb"][:, :], writes=[kT])
        load_vt(s, vt, io, "vb", T, init=False)
        if stage == "load":
            return
    if "flog_sb" not in io:
        s.dma("sp", fl[:, :], io["flog"][:, :], writes=[fl])
    s.dma("sp", fb[:, 0:1], io["fbias"][:, :], writes=[fb])
    s.op("dve", lambda: nc.vector.tensor_scalar(out=fb[:, 1:2], in0=fb[:, 0:1], scalar1=-1.0, scalar2=None, op0=ALU.mult),
         reads=[fb], writes=[fb])
    if "flog_sb" in io:
        fsb, fap = io["flog_sb"]
        s.op("act", lambda: nc.scalar.activation(out=fl[:, :], in_=fap, func=AF.Exp, bias=fb[:, 1:2], scale=-1.0),
             reads=[fsb, fb], writes=[fl])
    else:
        s.op("act", lambda: nc.scalar.activation(out=fl[:, :], in_=fl[:, :], func=AF.Exp, bias=fb[:, 1:2], scale=-1.0),
             reads=[fl, fb], writes=[fl])
    s.op("act", lambda: nc.scalar.activation(out=fl[:, :], in_=fl[:, :], func=AF.Ln, bias=1.0, scale=1.0),
         reads=[fl], writes=[fl])
    pc = cm["pf"]
    pcv = pc[:, 0, :]
    s.op("pe", lambda: nc.tensor.matmul(pc[:, 0, 0:NT], lhsT=tu[:, :], rhs=fl[:, :], start=True, stop=True),
         reads=[tu, fl], writes=[pc])
    s.op("pe", lambda: nc.tensor.matmul(pc[:, 1, 0:NT], lhsT=on[:, :], rhs=fl[:, :], start=True, stop=True),
         reads=[on, fl], writes=[pc])
    s.op("dve", lambda: nc.vector.tensor_copy(out=inc_[:, :], in_=pc[:, 1, 0:NT]), reads=[pc], writes=[inc_])
    sh = 1
    while sh < NT:
        s.op("dve", lambda: nc.vector.tensor_copy(out=tmpc[:, :], in_=inc_[:, :]), reads=[inc_], writes=[tmpc])
        s.op("dve", lambda: nc.vector.tensor_tensor(out=inc_[:, sh:NT], in0=tmpc[:, sh:NT], in1=tmpc[:, 0:NT - sh], op=ALU.add),
             reads=[tmpc], writes=[inc_])
        sh *= 2
    s.op("dve", lambda: nc.vector.tensor_tensor(out=cc[:, :], in0=pc[:, 0, 0:NT], in1=inc_[:, :], op=ALU.add),
         reads=[pc, inc_], writes=[cc])
    s.op("dve", lambda: nc.vector.tensor_tensor(out=tmpc[:, :], in0=cc[:, :], in1=pc[:, 1, 0:NT], op=ALU.subtract),
         reads=[pc, cc], writes=[tmpc])
    for qb in range(NQB):
        s.op("dve", lambda: nc.vector.tensor_scalar(out=btab[:, qb, :], in0=tmpc[:, :], scalar1=inc_[:, 4 * qb + 1:4 * qb + 2],
                                                    scalar2=None, op0=ALU.subtract), reads=[tmpc, inc_], writes=[btab])

    def bias_fn(qb, t):
        return btab, btab[:, qb, t:t + 1]

    def fin(qb, pos):
        pf = cm["pf"]
        emit_o_to_tokmajor(s, cm, pos[0], pf, 0)
        ot = cm["ot_rr"].next()
        for j in range(4):
            st = cm["st_rr"].next()
            s.op("dve", lambda: nc.vector.tensor_scalar(out=st[:, 0:1], in0=pf[:, j, 64:65], scalar1=1e-30, scalar2=None,
                                                        op0=ALU.max), reads=[pf], writes=[st])
            s.op("dve", lambda: nc.vector.reciprocal(out=st[:, 0:1], in_=st[:, 0:1]), reads=[st], writes=[st])
            s.op("dve", lambda: nc.vector.tensor_scalar(out=ot[:, j, :], in0=pf[:, j, 0:64], scalar1=st[:, 0:1], scalar2=None,
                                                        op0=ALU.mult), reads=[pf, st], writes=[ot])
        s.dma("sp", io["ob"][qb * 512:(qb + 1) * 512, :].rearrange("(j p) d -> p j d", p=128), ot[:, :, :], reads=[ot])

    emit_attn_phase(s, cm, T, 1, qT, [kT], vt, None, io["ob"], fin, bias_fn=bias_fn, name="b", lag=cm["lag"])
    if stage == "all":
        s.release(m)


def mix_consts():
    k = np.arange(128)
    tri = np.where(k[:, None] > k[None, :], NEG, 0.0).astype(np.float32)
    return dict(identb=_bf(np.eye(128, dtype=np.float32)), identf=np.eye(128, dtype=np.float32), trib=_bf(tri),
                triuf=np.triu(np.ones((128, 128), np.float32)), onesf=np.ones((128, 128), np.float32))


def mix_decl_c(nc, io, T):
    NT = T // 128
    QL = NT // 4
    NCT = max(1, T // 2048)
    io.update(dict(
        qc=dram_in(nc, "qc", [128, QL, 512], BF16),
        kskw=dram_in(nc, "kskw", [128, T], BF16),
        vs=dram_in(nc, "vs", [128, NT, 65], BF16), vw=dram_in(nc, "vw", [128, NT, 65], BF16),
        kvin=dram_in(nc, "kvin", [128, T], BF16),
        w1=dram_in(nc, "w1", [2, 2048, 256]), b1=dram_in(nc, "b1", [128, 4]),
        peT=dram_in(nc, "peT", [128, 32]),
        w2=dram_in(nc, "w2", [2, 256, 64]), b2=dram_in(nc, "b2", [2, 64]), b2c=dram_in(nc, "b2c", [64, 1]),
        kgain=dram_in(nc, "kgain", [64, 1]),
        ng=dram_in(nc, "ng", [128, QL, 12]),
        cmask=dram_in(nc, "cmask", [128, QL, NCT, 128], BF16),
        smask=dram_in(nc, "smask", [128, 4, 128], BF16), wmask=dram_in(nc, "wmask", [128, 8, 128], BF16),
        impA=dram_in(nc, "impA", [128, QL, 128]), impB=dram_in(nc, "impB", [128, QL, 128]),
        emat=dram_in(nc, "emat", [128, NT, 128], BF16), ovl=dram_in(nc, "ovl", [128, NCT, 128], BF16),
        ones64=dram_in(nc, "ones64", [64, 64], BF16), onesrow=dram_in(nc, "onesrow", [1, 128], BF16),
        oc=dram_out(nc, "oc", [QL * 128, 256], BF16),
    ))
    return io


def emit_gelu(s, zin_ap, zin_b, out_ap, out_b, tmp, shape_sl):
    nc = s.nc
    t = tmp
    s.op("act", lambda: nc.scalar.activation(out=t[shape_sl], in_=zin_ap, func=AF.Square), reads=[zin_b], writes=[t])
    s.op("dve", lambda: nc.vector.tensor_scalar(out=t[shape_sl], in0=t[shape_sl], scalar1=0.044715, scalar2=1.0,
                                                op0=ALU.mult, op1=ALU.add), reads=[t], writes=[t])
    s.op("dve", lambda: nc.vector.tensor_tensor(out=t[shape_sl], in0=t[shape_sl], in1=zin_ap, op=ALU.mult),
         reads=[t, zin_b], writes=[t])
    s.op("act", lambda: nc.scalar.activation(out=t[shape_sl], in_=t[shape_sl], func=AF.Exp, scale=-GELU_C), reads=[t], writes=[t])
    s.op("dve", lambda: nc.vector.tensor_scalar(out=t[shape_sl], in0=t[shape_sl], scalar1=1.0, scalar2=None, op0=ALU.add),
         reads=[t], writes=[t])
    s.op("dve", lambda: nc.vector.reciprocal(out=t[shape_sl], in_=t[shape_sl]), reads=[t], writes=[t])
    s.op("dve", lambda: nc.vector.tensor_tensor(out=out_ap, in0=t[shape_sl], in1=zin_ap, op=ALU.mult),
         reads=[t, zin_b], writes=[out_b])


def emit_mix_c(s, cm, io, T, cs=None):
    fused = cs is not None
    cs = cs if fused else [None]
    nc = s.nc
    NT = T // 128
    QL = NT // 4
    NCT = max(1, T // 2048)
    Nc = T // 16 - 1
    NCP = NCT * 128 if Nc > 128 else 128
    NCW = min(Nc, 511)
    assert Nc <= 511
    m = s.mark()
    ident_b = cm["ident_b"]
    ps_l = cm["ps_rr"].items
    po_l = cm["po_rr"].items
    pf, pf2 = cm["pf"], cm["pf2"]

    def ld(name, shape, dt, src, q="sp"):
        b = s.sb(name, shape, dt)
        idx = tuple(slice(None) for _ in shape)
        s.dma(q, b[idx], src, writes=[b])
        return b

    qc = s.sb("c_q", [128, QL, 512], BF16)
    qc2 = s.sb("c_q2", [128, QL, 512], BF16)
    s.op("pool", lambda: nc.gpsimd.memset(qc[64:128, :, :], 0.0), writes=[qc])
    s.op("dve", lambda: nc.vector.memset(qc2[0:64, :, :], 0.0), writes=[qc2])
    kk = ld("c_kk", [128, T], BF16, io["kskw"][:, :])
    vs = s.sb("c_vs", [128, NT, 128], BF16)
    vw = s.sb("c_vw", [128, NT, 128], BF16)
    load_vt(s, vs, io, "vs", T)
    load_vt(s, vw, io, "vw", T)
    emat = ld("c_e", [128, NT, 128], BF16, io["emat"][:, :, :])
    ovl = ld("c_ovl", [128, NCT, 128], BF16, io["ovl"][:, :, :])
    smask = s.sb("c_sm", [128, 4, 128], BF16)
    wmask = s.sb("c_wm", [128, 8, 128], BF16)
    ngt = s.sb("c_ng", [128, QL, 12], F32)
    ones64 = ld("c_o64", [64, 64], BF16, io["ones64"][:, :])
    onesrow = ld("c_orow", [1, 128], BF16, io["onesrow"][:, :])
    kgain = ld("c_kg", [64, 1], F32, io["kgain"][:, :])
    b2c = ld("c_b2c", [64, 1], F32, io["b2c"][:, :])
    b1 = ld("c_b1", [128, 4], F32, io["b1"][:, :])

    ktc = s.sb("c_ktc", [128, NCP], BF16)
    vc = s.sb("c_vc", [128, NCT, 128], BF16)
    s.op("dve", lambda: nc.vector.memset(ktc[:, :], 0.0), writes=[ktc])
    s.op("dve", lambda: nc.vector.memset(vc[:, :, :], 0.0), writes=[vc])
    s.op("dve", lambda: nc.vector.memset(vc[:, :, 64:65], 1.0), writes=[vc])

    m2 = s.mark()
    kvin = ld("c_kvin", [128, T], BF16, io["kvin"][:, :])
    w1sb = s.sb("c_w1", [128, 32, 256], BF16)
    for x in range(2):
        s.dma("pool", w1sb[x * 64:(x + 1) * 64, :, :], io["w1"][x].rearrange("(j d) f -> d j f", d=64), writes=[w1sb])
    peT = s.sb("c_pe", [128, 32], BF16)
    s.dma("pool", peT[:, :], io["peT"][:, :], writes=[peT])
    w2sb = s.sb("c_w2", [128, 2, 2, 64], BF16)
    for x in range(2):
        s.dma("pool", w2sb[:, x, :, :], io["w2"][x].rearrange("(hh f) d -> f hh d", f=128), writes=[w2sb])
    b2row = s.sb("c_b2r", [1, 64], BF16)
    s.dma("pool", b2row[:, :], io["b2"][1:2, :], writes=[b2row])
    hacc = [ps_l[0], ps_l[1], ps_l[2], po_l[0]]
    pcol = po_l[1]
    for x in range(2):
        for hh in range(2):
            hp = hacc[x * 2 + hh]
            for j in range(32):
                s.op("pe", lambda: nc.tensor.matmul(hp[:, 0:NCW], lhsT=w1sb[x * 64:(x + 1) * 64, j, hh * 128:(hh + 1) * 128],
                                                    rhs=kvin[x * 64:(x + 1) * 64, j:j + 16 * (NCW - 1) + 1:16],
                                                    start=(j == 0), stop=(j == 31)),
                     reads=[w1sb, kvin], writes=[hp], inc=(j == 31))
            for j in range(32):
                s.op("pe", lambda: nc.tensor.matmul(pcol[:, x * 2 + hh:x * 2 + hh + 1],
                                                    lhsT=w1sb[x * 64:(x + 1) * 64, j, hh * 128:(hh + 1) * 128],
                                                    rhs=peT[x * 64:(x + 1) * 64, j:j + 1], start=(j == 0), stop=(j == 31)),
                     reads=[w1sb, peT], writes=[pcol], inc=(j == 31))
    hbias = s.sb("c_hb", [128, 4], F32)
    s.op("dve", lambda: nc.vector.tensor_tensor(out=hbias[:, :], in0=pcol[:, 0:4], in1=b1[:, :], op=ALU.add),
         reads=[pcol, b1], writes=[hbias])
    gh = []
    for x in range(2):
        for hh in range(2):
            k = x * 2 + hh
            z = s.sb("c_z%d" % k, [128, 512], F32)
            tmp = s.sb("c_zt%d" % k, [128, 512], F32)
            gb = s.sb("c_g%d" % k, [128, 512], BF16)
            s.op("act", lambda: nc.scalar.activation(out=z[:, 0:NCW], in_=hacc[k][:, 0:NCW], func=AF.Identity,
                                                     bias=hbias[:, k:k + 1], scale=1.0), reads=[hacc[k], hbias], writes=[z])
            emit_gelu(s, z[:, 0:NCW], z, gb[:, 0:NCW], gb, tmp, (slice(None), slice(0, NCW)))
            gh.append(gb)
    pk = po_l[2]
    for hh in range(2):
        s.op("pe", lambda: nc.tensor.matmul(pk[0:64, 0:NCW], lhsT=w2sb[:, 0, hh, :], rhs=gh[hh][:, 0:NCW],
                                            start=(hh == 0), stop=(hh == 1)), reads=[w2sb, gh[hh]], writes=[pk], inc=(hh == 1))
    kz = s.sb("c_kz", [64, 512], F32)
    ksq = s.sb("c_ksq", [64, 512], BF16)
    krs = s.sb("c_krs", [64, 512], F32)
    s.op("act", lambda: nc.scalar.activation(out=kz[:, 0:NCW], in_=pk[0:64, 0:NCW], func=AF.Identity, bias=b2c[:, 0:1], scale=1.0),
         reads=[pk, b2c], writes=[kz])
    s.op("act", lambda: nc.scalar.activation(out=ksq[:, 0:NCW], in_=kz[:, 0:NCW], func=AF.Square), reads=[kz], writes=[ksq])
    pq = ps_l[0]
    s.op("pe", lambda: nc.tensor.matmul(pq[0:64, 0:NCW], lhsT=ones64[:, :], rhs=ksq[:, 0:NCW], start=True, stop=True),
         reads=[ones64, ksq], writes=[pq])
    s.op("act", lambda: nc.scalar.activation(out=krs[:, 0:NCW], in_=pq[0:64, 0:NCW], func=AF.Ln, bias=cm["epsc"][0:64, 0:1],
                                             scale=1.0 / 64.0), reads=[pq, cm["epsc"]], writes=[krs])
    s.op("act", lambda: nc.scalar.activation(out=krs[:, 0:NCW], in_=krs[:, 0:NCW], func=AF.Exp, scale=-0.5), reads=[krs], writes=[krs])
    s.op("dve", lambda: nc.vector.scalar_tensor_tensor(out=ktc[0:64, 0:NCW], in0=kz[:, 0:NCW], scalar=kgain[:, 0:1], in1=krs[:, 0:NCW],
                                                       op0=ALU.mult, op1=ALU.mult), reads=[kz, kgain, krs], writes=[ktc])
    for nt in range(NCT):
        n0 = nt * 128
        nn = min(128, Nc - n0)
        pv = ps_l[1 + nt % 2]
        for hh in range(2):
            s.op("pe", lambda: nc.tensor.matmul(pv[0:nn, 0:64], lhsT=gh[2 + hh][:, n0:n0 + nn], rhs=w2sb[:, 1, hh, :],
                                                start=(hh == 0), stop=False), reads=[gh[2 + hh], w2sb], writes=[pv], inc=False)
        s.op("pe", lambda: nc.tensor.matmul(pv[0:nn, 0:64], lhsT=onesrow[0:1, 0:nn], rhs=b2row[0:1, :], start=False, stop=True),
             reads=[onesrow, b2row], writes=[pv])
        s.op("act", lambda: nc.scalar.copy(out=vc[0:nn, nt, 0:64], in_=pv[0:nn, 0:64]), reads=[pv], writes=[vc])
    s.release(m2)

    cmk_rr = RR([s.sb("c_cmk%d" % j, [128, NCT, 128], BF16) for j in range(2)])
    ia_rr = RR([s.sb("c_ia%d" % j, [128, 128], F32) for j in range(2)])
    ib_rr = RR([s.sb("c_ib%d" % j, [128, 128], F32) for j in range(2)])
    imp_rr = RR([s.sb("c_imp%d" % j, [128, 128], F32) for j in range(2)])
    imp2_rr = RR([s.sb("c_impb%d" % j, [128, 128], F32) for j in range(2)])
    m8_rr = RR([s.sb("c_m8%d" % j, [128, 16], F32) for j in range(2)])
    mbT_rr = RR([s.sb("c_mbT%d" % j, [128, 128], BF16) for j in range(2)])
    oco_rr = RR([s.sb("c_oc%d" % j, [128, 4, 64], F32) for j in range(2)])
    gw_rr = RR([s.sb("c_gw%d" % j, [128, 12], F32) for j in range(2)])
    oo_rr = RR([s.sb("c_oo%d" % j, [128, 4, 64], F32) for j in range(2)])
    ob_rr = RR([s.sb("c_ob%d" % j, [128, 4, 64], BF16) for j in range(2)])

    pipe = Pipe(2)

    def masked_tile(kbuf, prow, t, Q, masks, vbuf, vt_idx, po, first, last, extra=None):
        ps = cm["ps_rr"].next()
        nm = len(masks)
        s.op("pe", lambda: nc.tensor.matmul(ps[:, :], lhsT=kbuf[:, t * 128:(t + 1) * 128], rhs=Q,
                                            start=True, stop=(nm == 0)), reads=[kbuf, qc, qc2], writes=[ps], inc=(nm == 0))
        for mi, (la, lb, ra, rb) in enumerate(masks):
            for h in range(4):
                lastm = (mi == nm - 1 and h == 3)
                s.op("pe", lambda: nc.tensor.matmul(ps[:, h * 128:(h + 1) * 128], lhsT=la, rhs=ra, start=False, stop=lastm),
                     reads=[lb, rb], writes=[ps], inc=lastm)
        pt = cm["pt_rr"].next()
        s.op("act", lambda: nc.scalar.activation(out=pt[:, :], in_=ps[:, :], func=AF.Exp), reads=[ps], writes=[pt])

        def back(pt=pt, po=po, vbuf=vbuf, vt_idx=vt_idx, first=first, last=last, extra=extra):
            s.op("pe", lambda: nc.tensor.matmul(po[:, :], lhsT=vbuf[:, vt_idx, :], rhs=pt[:, :], start=first, stop=last),
                 reads=[vbuf, pt], writes=[po])
            if extra is not None:
                extra(pt)
        pipe.push(back)

    for ci in cs:
        def gk(key):
            return io[key][ci] if fused else io[key]
        if fused:
            for h in range(4):
                r0 = (h % 2) * 64
                srcq = io["zq"][h // 2][r0:r0 + 64, :].rearrange("d (i c q) -> d i c q", c=4, q=128)[:, :, ci, :]
                s.dma("sp", qc[0:64, :, h * 128:(h + 1) * 128], srcq, writes=[qc])
                s.dma("sp", qc2[64:128, :, h * 128:(h + 1) * 128], srcq, writes=[qc2])
            msb, mview = io["misc_sb"]
            s.op("act", lambda: nc.scalar.activation(out=ngt[:, :, :], in_=mview[:, ci:NT:4, 4:16], func=AF.Exp, scale=-1.0),
                 reads=[msb], writes=[ngt])
        else:
            s.dma("sp", qc[0:64, :, :], io["qc"][0:64, :, :], writes=[qc])
            s.dma("sp", qc2[64:128, :, :], io["qc"][64:128, :, :], writes=[qc2])
            s.dma("sp", ngt[:, :, :], io["ng"][:, :, :], writes=[ngt])
            s.op("act", lambda: nc.scalar.activation(out=ngt[:, :, :], in_=ngt[:, :, :], func=AF.Exp, scale=-1.0),
                 reads=[ngt], writes=[ngt])
        s.op("dve", lambda: nc.vector.tensor_scalar(out=ngt[:, :, :], in0=ngt[:, :, :], scalar1=1.0, scalar2=None, op0=ALU.add),
             reads=[ngt], writes=[ngt])
        s.op("dve", lambda: nc.vector.reciprocal(out=ngt[:, :, :], in_=ngt[:, :, :]), reads=[ngt], writes=[ngt])
        s.dma("sp", smask[:, :, :], gk("smask")[:, :, :], writes=[smask])
        s.dma("sp", wmask[:, :, :], gk("wmask")[:, :, :], writes=[wmask])
        for i in range(QL):
            Qlo = qc[:, i, :]
            Qhi = qc2[:, i, :]
            cmk = cmk_rr.next()
            ia = ia_rr.next()
            ib = ib_rr.next()
            s.dma("sp", cmk[:, :, :], gk("cmask")[:, i, :, :], writes=[cmk])
            s.dma("sp", ia[:, :], gk("impA")[:, i, :], writes=[ia])
            s.dma("sp", ib[:, :], gk("impB")[:, i, :], writes=[ib])
            po_c, po_s, po_w = po_l[0], po_l[1], po_l[2]
            nct = min(NCT, i // 4 + 1)
            for nt in range(nct):
                def imp_mm(pt, nt=nt, nct=nct):
                    for h in range(4):
                        s.op("pe", lambda: nc.tensor.matmul(pf2[:, h, :], lhsT=pt[:, h * 128:(h + 1) * 128], rhs=ovl[:, nt, :],
                                                            start=(nt == 0 and h == 0), stop=(nt == nct - 1 and h == 3),
                                                            skip_group_check=True), reads=[pt, ovl], writes=[pf2],
                             inc=(h == 3))
                masked_tile(ktc, (0, 64), nt, Qlo, [(ident_b[:, :], ident_b, cmk[:, nt, :], cmk)], vc, nt, po_c,
                            nt == 0, nt == nct - 1, extra=imp_mm)
            pipe.flush()
            emit_o_to_tokmajor(s, cm, po_c, pf, 0)
            st = cm["st_rr"].next()
            rsum = cm["st_rr"].next()
            gw = gw_rr.next()
            s.op("dve", lambda: nc.vector.tensor_scalar(out=st[:, 0:4], in0=pf[:, :, 64], scalar1=1e-30, scalar2=None, op0=ALU.max),
                 reads=[pf], writes=[st])
            s.op("dve", lambda: nc.vector.reciprocal(out=rsum[:, 0:4], in_=st[:, 0:4]), reads=[st], writes=[rsum])
            oco = oco_rr.next()
            s.op("dve", lambda: nc.vector.tensor_copy(out=oco[:, :, :], in_=pf[:, :, 0:64]), reads=[pf], writes=[oco])
            imp = imp_rr.next()
            s.op("dve", lambda: nc.vector.tensor_scalar(out=imp[:, :], in0=pf2[:, 0, :], scalar1=rsum[:, 0:1], scalar2=None, op0=ALU.mult),
                 reads=[pf2, rsum], writes=[imp])
            for h in range(1, 4):
                s.op("dve", lambda: nc.vector.scalar_tensor_tensor(out=imp[:, :], in0=pf2[:, h, :], scalar=rsum[:, h:h + 1], in1=imp[:, :],
                                                                   op0=ALU.mult, op1=ALU.add), reads=[pf2, rsum, imp], writes=[imp])
            s.op("dve", lambda: nc.vector.tensor_tensor(out=imp[:, :], in0=imp[:, :], in1=ia[:, :], op=ALU.mult), reads=[imp, ia], writes=[imp])
            s.op("dve", lambda: nc.vector.tensor_tensor(out=imp[:, :], in0=imp[:, :], in1=ib[:, :], op=ALU.add), reads=[imp, ib], writes=[imp])
            m8 = m8_rr.next()
            imp2 = imp2_rr.next()
            s.op("dve", lambda: nc.vector.max(out=m8[:, 0:8], in_=imp[:, :]), reads=[imp], writes=[m8])
            s.op("dve", lambda: nc.vector.match_replace(out=imp2[:, :], in_to_replace=m8[:, 0:8], in_values=imp[:, :], imm_value=-1e9),
                 reads=[imp, m8], writes=[imp2])
            s.op("dve", lambda: nc.vector.max(out=m8[:, 8:16], in_=imp2[:, :]), reads=[imp2], writes=[m8])
            s.op("dve", lambda: nc.vector.tensor_scalar(out=imp2[:, :], in0=imp[:, :], scalar1=m8[:, 15:16], scalar2=NEG,
                                                        op0=ALU.is_lt, op1=ALU.mult), reads=[imp, m8], writes=[imp2])
            tl = [4 * (i - 1) + u for u in range(8) if 4 * (i - 1) + u >= 0]
            for t in tl:
                u = t - 4 * (i - 1)
                masked_tile(kk, (64, 128), t, Qhi, [(ident_b[:, :], ident_b, wmask[:, u, :], wmask)], vw, t, po_w,
                            t == tl[0], t == tl[-1])
            ptr = cm["ps_rr"].next()
            s.op("pe", lambda: nc.tensor.transpose(out=ptr[:, 0:128], in_=imp2[:, :], identity=cm["ident_f"][:, :]),
                 reads=[imp2, cm["ident_f"]], writes=[ptr])
            mbT = mbT_rr.next()
            s.op("dve", lambda: nc.vector.tensor_copy(out=mbT[:, :], in_=ptr[:, 0:128]), reads=[ptr], writes=[mbT])
            nts = 4 * i + 4
            for t in range(nts):
                masks = [(emat[:, t, :], emat, mbT[:, :], mbT)]
                if t >= 4 * i:
                    masks.append((ident_b[:, :], ident_b, smask[:, t - 4 * i, :], smask))
                masked_tile(kk, (0, 64), t, Qlo, masks, vs, t, po_s, t == 0, t == nts - 1)
            pipe.flush()
            emit_o_to_tokmajor(s, cm, po_s, pf, 0)
            st2 = cm["st_rr"].next()
            s.op("dve", lambda: nc.vector.tensor_scalar(out=st2[:, 0:4], in0=pf[:, :, 64], scalar1=1e-30, scalar2=None, op0=ALU.max),
                 reads=[pf], writes=[st2])
            s.op("dve", lambda: nc.vector.reciprocal(out=st2[:, 0:4], in_=st2[:, 0:4]), reads=[st2], writes=[st2])
            gv = ngt[:, i, :].rearrange("p (h b) -> p h b", b=3)
            gwv = gw[:, :].rearrange("p (h b) -> p h b", b=3)
            s.op("dve", lambda: nc.vector.tensor_tensor(out=gwv[:, :, 0], in0=gv[:, :, 0], in1=rsum[:, 0:4], op=ALU.mult),
                 reads=[ngt, rsum], writes=[gw])
            s.op("dve", lambda: nc.vector.tensor_tensor(out=gwv[:, :, 1], in0=gv[:, :, 1], in1=st2[:, 0:4], op=ALU.mult),
                 reads=[ngt, st2], writes=[gw])
            oo = oo_rr.next()
            for h in range(4):
                s.op("dve", lambda: nc.vector.tensor_scalar(out=oo[:, h, :], in0=oco[:, h, :], scalar1=gw[:, 3 * h:3 * h + 1], scalar2=None,
                                                            op0=ALU.mult), reads=[oco, gw], writes=[oo])
                s.op("dve", lambda: nc.vector.scalar_tensor_tensor(out=oo[:, h, :], in0=pf[:, h, 0:64], scalar=gw[:, 3 * h + 1:3 * h + 2],
                                                                   in1=oo[:, h, :], op0=ALU.mult, op1=ALU.add), reads=[pf, gw, oo], writes=[oo])
            emit_o_to_tokmajor(s, cm, po_w, pf, 0)
            st3 = cm["st_rr"].next()
            s.op("dve", lambda: nc.vector.tensor_scalar(out=st3[:, 0:4], in0=pf[:, :, 64], scalar1=1e-30, scalar2=None, op0=ALU.max),
                 reads=[pf], writes=[st3])
            s.op("dve", lambda: nc.vector.reciprocal(out=st3[:, 0:4], in_=st3[:, 0:4]), reads=[st3], writes=[st3])
            s.op("dve", lambda: nc.vector.tensor_tensor(out=gwv[:, :, 2], in0=gv[:, :, 2], in1=st3[:, 0:4], op=ALU.mult),
                 reads=[ngt, st3], writes=[gw])
            ob = ob_rr.next()
            for h in range(4):
                s.op("dve", lambda: nc.vector.scalar_tensor_tensor(out=ob[:, h, :], in0=pf[:, h, 0:64], scalar=gw[:, 3 * h + 2:3 * h + 3],
                                                                   in1=oo[:, h, :], op0=ALU.mult, op1=ALU.add), reads=[pf, gw, oo], writes=[ob])
            orow = ((4 * i + ci) if fused else i) * 128
            s.dma("sp", io["oc"][orow:orow + 128, :], ob[:, :, :].rearrange("p h d -> p (h d)"), reads=[ob])
    s.release(m)


def build_mix(T, parts="abc"):
    nc = bass.Bass("TRN2", target_bir_lowering=False)
    io = mix_decl(nc, T)
    if "c" in parts:
        mix_decl_c(nc, io, T)
    s = S(nc)
    cm = mix_common(s, io)
    if "a" in parts:
        emit_mix_a(s, cm, io, T)
    if "b" in parts:
        emit_mix_b(s, cm, io, T)
    if "c" in parts:
        emit_mix_c(s, cm, io, T)
    s.finish()
    s.close()
    return nc


def mix_consts_c(T, c):
    NT = T // 128
    QL = NT // 4
    NCT = max(1, T // 2048)
    Nc = T // 16 - 1
    NS = T // 64
    ar = np.arange(128)
    cmask = np.zeros((128, QL, NCT, 128), np.float32)
    impA = np.zeros((128, QL, 128), np.float32)
    impB = np.zeros((128, QL, 128), np.float32)
    for i in range(QL):
        qpos = 128 * (4 * i + c) + ar
        for nt in range(NCT):
            n = 128 * nt + ar
            ok = (16 * n[:, None] + 31 <= qpos[None, :]) & (n[:, None] < Nc)
            cmask[:, i, nt, :] = np.where(ok, 0.0, NEG)
        j = ar
        cur = qpos // 64
        forced = (j[None, :] == 0) | (j[None, :] == cur[:, None]) | (j[None, :] == cur[:, None] - 1)
        valid = (j[None, :] * 64 <= qpos[:, None]) & (j[None, :] < NS)
        impA[:, i, :] = (valid & ~forced).astype(np.float32)
        impB[:, i, :] = np.where(forced & (j[None, :] < NS), 1.0e4, np.where(valid, 0.0, -1.0))
    smask = np.zeros((128, 4, 128), np.float32)
    for u in range(4):
        kpos = 128 * u + ar
        qp = 128 * c + ar
        smask[:, u, :] = np.where(kpos[:, None] <= qp[None, :], 0.0, NEG)
    wmask = np.zeros((128, 8, 128), np.float32)
    for u in range(8):
        dist = 128 * (c + 4 - u) + ar[None, :] - ar[:, None]
        wmask[:, u, :] = np.where((dist >= 0) & (dist < 512), 0.0, NEG)
    emat = np.zeros((128, NT, 128), np.float32)
    for t in range(NT):
        for k in range(128):
            jj = 2 * t + k // 64
            if jj < 128:
                emat[jj, t, k] = 1.0
    ovl = np.zeros((128, NCT, 128), np.float32)
    for nt in range(NCT):
        n = 128 * nt + ar
        o = (n[:, None] * 16 < (ar[None, :] + 1) * 64) & (n[:, None] * 16 + 32 > ar[None, :] * 64) & (n[:, None] < Nc) \
            & (ar[None, :] < NS)
        ovl[:, nt, :] = o
    return dict(cmask=_bf(cmask), impA=impA, impB=impB, smask=_bf(smask), wmask=_bf(wmask), emat=_bf(emat), ovl=_bf(ovl),
                ones64=_bf(np.ones((64, 64), np.float32)), onesrow=_bf(np.ones((1, 128), np.float32)))


def build_merge(NT, TB=512):
    nc = bass.Bass("TRN2", target_bir_lowering=False)
    x = dram_in(nc, "x", [NT, D])
    g = dram_in(nc, "g", [D])
    w_in = dram_in(nc, "w_in", [D, 6800])
    w_br = dram_in(nc, "w_br", [4, 256, D])
    w_o = dram_in(nc, "w_o", [D, D])
    ident = dram_in(nc, "ident", [128, 128], BF16)
    obr = dram_in(nc, "obr", [NT, D], BF16)
    y = dram_out(nc, "y", [NT, D])
    s = S(nc)
    emit_merge(s, x, g, w_in, w_br, w_o, ident, obr, y, NT, TB)
    s.finish()
    s.close()
    return nc


def emit_merge(s, x, g, w_in, w_br, w_o, ident, obr, y, NT, TB=512):
    nc = s.nc
    m_ = s.mark()
    ntile = TB // 128
    ident_b = s.sb("ident_b", [128, 128], BF16)
    s.dma("sp", ident_b[:, :], ident[:, :], writes=[ident_b])
    gcol = s.sb("gcol", [128, NKC], F32)
    s.dma("sp", gcol[:, :], g.rearrange("(c p) -> p c", p=128), writes=[gcol], allow_slow_non_contiguous=True)
    epsc = s.sb("epsc", [128, 1], F32)
    s.op("dve", lambda: nc.vector.memset(epsc[:, :], EPS), writes=[epsc])
    wg = [s.sb("wg%d" % c, [128, 4096], BF16) for c in range(NKC)]
    wb = [s.sb("wb%d" % c, [128, D], BF16) for c in range(8)]
    wo = [s.sb("wo%d" % c, [128, D], BF16) for c in range(NKC)]
    for c in range(NKC):
        for hf in range(2):
            s.dma("pool", wg[c][:, hf * 2048:(hf + 1) * 2048], w_in[c * 128:(c + 1) * 128, 2704 + hf * 2048:2704 + (hf + 1) * 2048],
                  writes=[wg[c]])
    for n in range(4):
        for cc in range(2):
            s.dma("pool", wb[2 * n + cc][:, :], w_br[n, cc * 128:(cc + 1) * 128, :], writes=[wb[2 * n + cc]])
    for c in range(NKC):
        s.dma("pool", wo[c][:, :], w_o[c * 128:(c + 1) * 128, :], writes=[wo[c]])
    xn = [s.sb("xn%d" % j, [128, D], F32) for j in range(ntile)]
    xr_rr = RR([s.sb("xr%d" % j, [128, D], F32) for j in range(2)])
    ots = [[s.sb("ot%d_%d" % (k, j), [128, D], BF16) for j in range(ntile)] for k in range(2)]
    hb_rr = RR([s.sb("hb%d" % j, [128, D], BF16) for j in range(2 * ntile)])
    stat_rr = RR([s.sb("st%d" % j, [128, 16], F32) for j in range(4)])
    hT = s.sb("hT", [128, NKC, TB], BF16)
    oT = s.sb("oT", [128, 8, TB], BF16)
    mT = [s.sb("mT%d" % c, [128, TB], BF16) for c in range(8)]
    pT_rr = RR([s.ps("pT%d" % j, [128, TB], BF16) for j in range(2)])
    pg_rr = RR([s.ps("pg%d" % j, [128, 512], F32) for j in range(2)])
    pp_rr = RR([s.ps("pp%d" % j, [128, 512], F32) for j in range(2)])
    po_rr = RR([s.ps("po%d" % j, [128, 512], F32) for j in range(2)])
    sg_rr = RR([s.sb("sg%d" % j, [128, TB], F32) for j in range(3)])
    acc_rr = RR([s.sb("acc%d" % j, [128, TB], F32) for j in range(2)])
    nblk = NT // TB

    def prep_a(tb):
        ot = ots[tb % 2]
        for j in range(ntile):
            r0 = tb * TB + j * 128
            s.dma("sp", xn[j][:, :], x[r0:r0 + 128, :], writes=[xn[j]])
            s.dma("sp", ot[j][:, :], obr[r0:r0 + 128, :], writes=[ot[j]])
        return emit_norm(s, epsc, xn, hb_rr, None, stat_rr, ntile)

    def prep_b(tb, hbs):
        ot = ots[tb % 2]
        emit_transpose_T(s, hbs, gcol, hT, ident_b, pT_rr, ntile)
        for c in range(8):
            pT = pT_rr.next()
            for j in range(ntile):
                s.op("pe", lambda: nc.tensor.transpose(out=pT[:, j * 128:(j + 1) * 128], in_=ot[j][:, c * 128:(c + 1) * 128],
                                                       identity=ident_b[:, :]), reads=[ot[j], ident_b], writes=[pT], inc=(j == ntile - 1))
            s.op("dve", lambda: nc.vector.tensor_copy(out=oT[:, c, :], in_=pT[:, 0:TB]), reads=[pT], writes=[oT])

    hbs_next = prep_a(0)
    prep_b(0, hbs_next)
    for tb in range(nblk):
        t0 = tb * TB
        if tb + 1 < nblk:
            hbs_next = prep_a(tb + 1)
        for dc in range(8):
            acc = acc_rr.next()
            for n in range(4):
                pg = pg_rr.next()
                pp = pp_rr.next()
                for c in range(NKC):
                    s.op("pe", lambda: nc.tensor.matmul(pg[:, 0:TB], lhsT=wg[c][:, n * 1024 + dc * 128:n * 1024 + (dc + 1) * 128],
                                                        rhs=hT[:, c, :], start=(c == 0), stop=(c == NKC - 1)),
                         reads=[wg[c], hT], writes=[pg], inc=(c == NKC - 1))
                for cc in range(2):
                    s.op("pe", lambda: nc.tensor.matmul(pp[:, 0:TB], lhsT=wb[2 * n + cc][:, dc * 128:(dc + 1) * 128],
                                                        rhs=oT[:, 2 * n + cc, :], start=(cc == 0), stop=(cc == 1)),
                         reads=[wb[2 * n + cc], oT], writes=[pp], inc=(cc == 1))
                sg = sg_rr.next()
                s.op("act", lambda: nc.scalar.activation(out=sg[:, :], in_=pg[:, 0:TB], func=AF.Sigmoid), reads=[pg], writes=[sg])
                if n == 0:
                    s.op("dve", lambda: nc.vector.tensor_tensor(out=acc[:, :], in0=sg[:, :], in1=pp[:, 0:TB], op=ALU.mult),
                         reads=[sg, pp], writes=[acc])
                else:
                    s.op("dve", lambda: nc.vector.tensor_tensor(out=sg[:, :], in0=sg[:, :], in1=pp[:, 0:TB], op=ALU.mult),
                         reads=[sg, pp], writes=[sg])
                    if n < 3:
                        s.op("pool", lambda: nc.gpsimd.tensor_tensor(out=acc[:, :], in0=acc[:, :], in1=sg[:, :], op=ALU.add),
                             reads=[acc, sg], writes=[acc])
                    else:
                        s.op("pool", lambda: nc.gpsimd.tensor_tensor(out=mT[dc][:, :], in0=acc[:, :], in1=sg[:, :], op=ALU.add),
                             reads=[acc, sg], writes=[mT[dc]])
        if tb + 1 < nblk:
            prep_b(tb + 1, hbs_next)
        for j in range(ntile):
            xr = xr_rr.next()
            s.dma("sp", xr[:, :], x[t0 + j * 128:t0 + (j + 1) * 128, :], writes=[xr])
            for hf in range(2):
                po = po_rr.next()
                for dc in range(8):
                    s.op("pe", lambda: nc.tensor.matmul(po[:, :], lhsT=mT[dc][:, j * 128:(j + 1) * 128],
                                                        rhs=wo[dc][:, hf * 512:(hf + 1) * 512], start=(dc == 0), stop=(dc == 7)),
                         reads=[mT[dc], wo[dc]], writes=[po], inc=(dc == 7))
                s.op("dve", lambda: nc.vector.tensor_tensor(out=xr[:, hf * 512:(hf + 1) * 512], in0=po[:, :],
                                                            in1=xr[:, hf * 512:(hf + 1) * 512], op=ALU.add),
                     reads=[po, xr], writes=[xr])
            s.dma("sp", y[t0 + j * 128:t0 + (j + 1) * 128, :], xr[:, :], reads=[xr])
    s.release(m_)


PARAM_SHAPES = dict(
    ffn1_norm=("L", D), ffn1_w_in=("L", D, 2 * DFF), ffn1_w_out=("L", DFF, D), mix_norm=("L", D), w_in=("L", D, 6800),
    nsa_phi_w1=("L", 2, 2048, 256), nsa_phi_w2=("L", 2, 256, 64), nsa_phi_b2=("L", 2, 64),
    w_branch=("L", 4, 256, D), w_out=("L", D, D), ffn2_norm=("L", D), ffn2_w_in=("L", D, 2 * DFF), ffn2_w_out=("L", DFF, D),
    gains=("L", 128, 6), vgain=("L", 128, 256), wsT=("L", 4, 128, 128), bsT=("L", 128, 4), lamp=("L", 128, 4, 32),
    lami=("L", 128, 2), fbias=("L", 4, 128, 1), b1l=("L", 128, 4), peT=("L", 128, 32), b2c=("L", 64, 1), kgain=("L", 64, 1),
)


def fused_const_shapes(T):
    NT = T // 128
    QL = NT // 4
    NCT = max(1, T // 2048)
    return dict(
        ident=([128, 128], BF16), identf=([128, 128], F32), blk=([2, 128, 128], BF16), triu=([128, 128], F32),
        trib=([128, 128], BF16), triuf=([128, 128], F32), onesf=([128, 128], F32), ones64=([64, 64], BF16),
        onesrow=([1, 128], BF16), cmask=([4, 128, QL, NCT, 128], BF16), smask=([4, 128, 4, 128], BF16),
        wmask=([4, 128, 8, 128], BF16), impA=([4, 128, QL, 128], F32), impB=([4, 128, QL, 128], F32),
        emat=([128, NT, 128], BF16), ovl=([128, NCT, 128], BF16))


def fused_consts(T):
    pc = proj_consts()
    mc = mix_consts()
    cc = [mix_consts_c(T, c) for c in range(4)]
    d = dict(ident=pc["ident"], identf=mc["identf"], blk=pc["blk"], triu=pc["triu"], trib=mc["trib"], triuf=mc["triuf"],
             onesf=mc["onesf"], ones64=cc[0]["ones64"], onesrow=cc[0]["onesrow"], emat=cc[0]["emat"], ovl=cc[0]["ovl"])
    for k in ("cmask", "smask", "wmask", "impA", "impB"):
        d[k] = np.ascontiguousarray(np.stack([cc[c][k] for c in range(4)], 0))
    return d


def emit_mix_fused(s, F, l, T):
    nc = s.nc
    NT = T // 128
    m = s.mark()
    io0 = dict(identb=F["ident"], identf=F["identf"], trib=F["trib"])
    misc_sb = s.sb("misc_sb", [128, NT, 16], F32)
    m_ab = s.mark()
    cm = mix_common(s, io0, n_ps=4, with_pf2=False)
    for j0 in range(0, NT, 8):
        j1 = min(NT, j0 + 8)
        s.dma("sp", misc_sb[:, j0:j1, :], F["misc"][j0 * 128:j1 * 128, :].rearrange("(j p) c -> p j c", p=128), writes=[misc_sb])
    zfm, vab, obr = F["zfm"], F["vab"], F["obr"]

    def io_a(h):
        r0 = (h % 2) * 64
        io = dict(io0)
        io.update(qa=zfm[h // 2][r0:r0 + 64, :], ka=zfm[2 + h // 2][r0:r0 + 64, :], va_src=vab[:, h * 64:(h + 1) * 64],
                  lamp=F["lamp"][l], lami=F["lami"][l], oa=obr[:, h * 64:(h + 1) * 64])
        return io

    def io_b(h):
        r0 = (h % 2) * 64
        io = dict(io0)
        io.update(qb=zfm[4 + h // 2][r0:r0 + 64, :], kb=zfm[6 + h // 2][r0:r0 + 64, :],
                  vb_src=vab[:, 256 + h * 64:256 + (h + 1) * 64], flog_sb=(misc_sb, misc_sb[:, :, h]), fbias=F["fbias"][l, h],
                  triuf=F["triuf"], onesf=F["onesf"], ob=obr[:, 256 + h * 64:256 + (h + 1) * 64])
        return io

    ba = emit_mix_a(s, cm, io_a(0), T, stage="alloc")
    bb = emit_mix_b(s, cm, io_b(0), T, stage="alloc")
    emit_mix_a(s, cm, io_a(0), T, stage="load", bufs=ba)
    emit_mix_b(s, cm, io_b(0), T, stage="load", bufs=bb)
    for h in range(4):
        emit_mix_a(s, cm, io_a(h), T, stage="compute", bufs=ba)
        if h + 1 < 4:
            emit_mix_a(s, cm, io_a(h + 1), T, stage="load", bufs=ba)
        emit_mix_b(s, cm, io_b(h), T, stage="compute", bufs=bb)
        if h + 1 < 4:
            emit_mix_b(s, cm, io_b(h + 1), T, stage="load", bufs=bb)
    s.release(m_ab)
    cm = mix_common(s, io0, n_ps=3, with_pf2=True)
    io = dict(io0)
    io.update(zq=(zfm[8], zfm[9]), kskw=zfm[10], kvin=zfm[11], vs_src=F["vsw"][:, 0:64], vw_src=F["vsw"][:, 64:128],
              misc_sb=(misc_sb, misc_sb), w1=F["nsa_phi_w1"][l], b1=F["b1l"][l], peT=F["peT"][l], w2=F["nsa_phi_w2"][l],
              b2=F["nsa_phi_b2"][l], b2c=F["b2c"][l], kgain=F["kgain"][l], oc=obr[:, 512:768])
    for k in ("cmask", "smask", "wmask", "impA", "impB", "emat", "ovl", "ones64", "onesrow"):
        io[k] = F[k]
    emit_mix_c(s, cm, io, T, cs=[0, 1, 2, 3])
    s.release(m)


def build_fused(T, L):
    nc = bass.Bass("TRN2", target_bir_lowering=False)
    F = {}
    F["x"] = dram_in(nc, "x", [T, D])
    for k, shp in PARAM_SHAPES.items():
        F[k] = dram_in(nc, k, [L if v == "L" else v for v in shp])
    for k, (shp, dt) in fused_const_shapes(T).items():
        F[k] = dram_in(nc, k, shp, dt)
    y = dram_out(nc, "y", [T, D])
    for k, shp, dt in (("xa", [T, D], F32), ("xb", [T, D], F32), ("xc", [T, D], F32), ("zfm", [NFM, 128, T], BF16),
                       ("vab", [T, 512], BF16), ("vsw", [T, 128], BF16), ("misc", [T, 16], F32), ("obr", [T, D], BF16)):
        F[k] = nc.dram_tensor("s_" + k, shp, dt).ap()
    s = S(nc)
    for l in range(L):
        x_in = F["x"] if l == 0 else F["xc"]
        emit_ffn(s, x_in, F["ffn1_norm"][l], F["ffn1_w_in"][l], F["ffn1_w_out"][l], F["ident"], F["xa"], T)
        a = dict(x=F["xa"], g=F["mix_norm"][l], w_in=F["w_in"][l], ident=F["ident"], gains=F["gains"][l], blk=F["blk"],
                 vgain=F["vgain"][l], wsT=F["wsT"][l], triu=F["triu"], bsT=F["bsT"][l], zfm=F["zfm"], vab=F["vab"],
                 vsw=F["vsw"], misc=F["misc"], od=F["obr"][:, 768:1024])
        emit_proj(s, a, T)
        emit_mix_fused(s, F, l, T)
        emit_merge(s, F["xa"], F["mix_norm"][l], F["w_in"][l], F["w_branch"][l], F["w_out"][l], F["ident"], F["obr"], F["xb"], T)
        x_out = y if l == L - 1 else F["xc"]
        emit_ffn(s, F["xb"], F["ffn2_norm"][l], F["ffn2_w_in"][l], F["ffn2_w_out"][l], F["ident"], x_out, T)
    s.finish()
    s.close()
    return nc


def fused_params(P, L):
    import math
    f32 = np.float32
    A = lambda a: np.ascontiguousarray(np.asarray(a, dtype=f32))
    d = {k: A(P[k]) for k in ("ffn1_norm", "ffn1_w_in", "ffn1_w_out", "mix_norm", "w_in", "nsa_phi_w1", "nsa_phi_w2",
                              "nsa_phi_b2", "w_branch", "w_out", "ffn2_norm", "ffn2_w_in", "ffn2_w_out")}
    tile = lambda v, n: np.tile(A(v), (1, n))
    d["gains"] = np.ascontiguousarray(np.stack([tile(P["diff_q_gain"], 4), tile(P["diff_k_gain"], 4), tile(P["fox_q_gain"], 2),
                                                tile(P["fox_k_gain"], 2), tile(P["nsa_q_gain"], 2), tile(P["nsa_k_gain"], 2)], 2))
    d["vgain"] = np.ascontiguousarray(np.broadcast_to(A(P["gmlp_v_gain"])[:, None, :], (L, 128, 256)))
    d["wsT"] = np.ascontiguousarray(A(P["gmlp_w_s"]).transpose(0, 1, 3, 2))
    d["bsT"] = np.ascontiguousarray(A(P["gmlp_b_s"]).transpose(0, 2, 1))
    d["lamp"] = np.ascontiguousarray(np.broadcast_to(A(P["diff_lambda"])[:, None], (L, 128, 4, 32)))
    li = np.array([[0.8 - 0.6 * math.exp(-0.3 * l), 1.0 - (0.8 - 0.6 * math.exp(-0.3 * l))] for l in range(L)], f32)
    d["lami"] = np.ascontiguousarray(np.broadcast_to(li[:, None, :], (L, 128, 2)))
    d["fbias"] = np.ascontiguousarray(np.broadcast_to(A(P["fox_f_bias"])[:, :, None, None], (L, 4, 128, 1)))
    d["b1l"] = np.ascontiguousarray(A(P["nsa_phi_b1"]).reshape(L, 2, 2, 128).transpose(0, 3, 1, 2).reshape(L, 128, 4))
    pe = A(P["nsa_cmp_pe"])
    d["peT"] = np.ascontiguousarray(pe.transpose(0, 1, 3, 2).reshape(L, 128, 32))
    d["b2c"] = np.ascontiguousarray(A(P["nsa_phi_b2"])[:, 0, :, None])
    d["kgain"] = np.ascontiguousarray(A(P["nsa_k_gain"])[:, :, None])
    return d


B_, T_, L_ = 2, 8192, 2
_PROG = {}


def kernel(**inputs):
    x = np.ascontiguousarray(np.asarray(inputs["x"], dtype=np.float32))
    if "fused" not in _PROG:
        _PROG["fused"] = build_fused(T_, L_)
        _PROG["consts"] = fused_consts(T_)
    nc = _PROG["fused"]
    par = fused_params(inputs, L_)
    in_maps = []
    for b in range(B_):
        d = dict(par)
        d.update(_PROG["consts"])
        d["x"] = x[b]
        in_maps.append(d)
    res = run_bass_kernel_spmd(nc, in_maps, core_ids=list(range(B_)))
    return np.stack([np.asarray(res.results[b]["y"], dtype=np.float32) for b in range(B_)], 0)
```

```python
import numpy as np
import concourse.bass as bass
import concourse.mybir as mybir
from concourse.bass_utils import run_bass_kernel_spmd

F32 = mybir.dt.float32
BF16 = mybir.dt.bfloat16
AF = mybir.ActivationFunctionType
ALU = mybir.AluOpType
AX = mybir.AxisListType

ENGS = ("pe", "act", "dve", "pool", "sp")


class Buf:
    __slots__ = ("name", "t", "w", "r", "dsem", "dcnt", "uid")
    _n = 0

    def __init__(self, name, t):
        Buf._n += 1
        self.uid = Buf._n
        self.name = name
        self.t = t
        self.w = None
        self.r = []
        self.dsem = {}
        self.dcnt = {}

    def __getitem__(self, idx):
        return self.t[idx]


class S:
    def __init__(self, nc, same_engine_sync=True):
        self.nc = nc
        self.e = {"pe": nc.tensor, "act": nc.scalar, "dve": nc.vector, "pool": nc.gpsimd, "sp": nc.sync}
        self.sem = {k: nc.alloc_semaphore("c_" + k) for k in ENGS}
        self.cnt = {k: 0 for k in ENGS}
        self.seen = {k: {} for k in ENGS}
        self.same = same_engine_sync
        self.nbuf = 0
        self.nsem = 0
        self.dma_sems = []
        self.ctx = []
        self.cbufs = []
        self.free_dsems = {"hw": [], "sw": []}

    def sb(self, name, shape, dt):
        self.nbuf += 1
        g = self.nc.sbuf_tensor("%s_%d" % (name, self.nbuf), list(shape), dt)
        t = g.__enter__()
        self.ctx.append(g)
        b = Buf(name, t)
        self.cbufs.append(b)
        return b

    def ps(self, name, shape, dt):
        self.nbuf += 1
        g = self.nc.psum_tensor("%s_%d" % (name, self.nbuf), list(shape), dt)
        t = g.__enter__()
        self.ctx.append(g)
        b = Buf(name, t)
        self.cbufs.append(b)
        return b

    def sub(self, name, ap):
        return Buf(name, ap)

    def mark(self):
        return len(self.ctx)

    def release(self, m):
        self.barrier()
        while len(self.ctx) > m:
            self.ctx.pop().__exit__(None, None, None)
            b = self.cbufs.pop()
            for kind, sem in b.dsem.items():
                self.free_dsems[kind].append((sem, b.dcnt[kind]))
                self.dma_sems.remove((b, kind))
            b.dsem = {}

    def close(self):
        for g in reversed(self.ctx):
            g.__exit__(None, None, None)
        self.ctx = []
        self.cbufs = []

    def _need(self, E, deps):
        need = {}
        for d in deps:
            if d is None:
                continue
            if d[0] == "dma":
                b, kind = d[1], d[2]
                if kind not in b.dsem:
                    continue
                key = ("dma", b.uid, kind)
                need[key] = ((b, kind), b.dcnt[kind])
            else:
                F, c = d
                if F == E and (not self.same or E == "pe" or c > self.cnt[E]):
                    continue
                if c > need.get(F, (None, 0))[1]:
                    need[F] = (None, c)
        for key, (bk, c) in need.items():
            if self.seen[E].get(key, 0) >= c:
                continue
            self.seen[E][key] = c
            if bk is not None:
                self.e[E].wait_ge(bk[0].dsem[bk[1]], c)
            else:
                self.e[E].wait_ge(self.sem[key], c)

    def op(self, E, fn, reads=(), writes=(), inc=True):
        deps = []
        for b in reads:
            deps.append(b.w)
        for b in writes:
            deps.append(b.w)
            deps.extend(b.r)
        self._need(E, deps)
        ins = fn()
        c = self.cnt[E] + 1
        if inc:
            ins.then_inc(self.sem[E], 1)
            self.cnt[E] = c
        for b in writes:
            b.w = (E, c)
            b.r = []
        for b in reads:
            if b not in writes:
                b.r = [x for x in b.r if x[0] != E] + [(E, c)]
        return ins

    def _dsem(self, owner, kind):
        if kind not in owner.dsem:
            if self.free_dsems[kind]:
                sem, cnt = self.free_dsems[kind].pop()
            else:
                self.nsem += 1
                sem, cnt = self.nc.alloc_semaphore("d%s_%d" % (kind, self.nsem)), 0
            owner.dsem[kind] = sem
            owner.dcnt[kind] = cnt
            self.dma_sems.append((owner, kind))
        return owner.dsem[kind]

    def dma(self, Q, out, in_, reads=(), writes=(), **kw):
        deps = []
        for b in reads:
            deps.append(b.w)
        for b in writes:
            deps.append(b.w)
            deps.extend(b.r)
        self._need(Q, deps)
        owner = (list(writes) + list(reads))[0]
        kind = "sw" if Q == "pool" else "hw"
        sem = self._dsem(owner, kind)
        ins = self.e[Q].dma_start(out=out, in_=in_, **kw)
        ins.then_inc(sem, 16)
        owner.dcnt[kind] += 16
        rec = ("dma", owner, kind)
        for b in writes:
            b.w = rec
            b.r = []
        for b in reads:
            if b not in writes:
                b.r = [x for x in b.r if not (x[0] == "dma" and x[1] is owner and x[2] == kind)] + [rec]
        return ins

    def barrier(self):
        for E in ENGS:
            for Fk in ENGS:
                if Fk == E:
                    continue
                c = self.cnt[Fk]
                if c and self.seen[E].get(Fk, 0) < c:
                    self.seen[E][Fk] = c
                    self.e[E].wait_ge(self.sem[Fk], c)
            for (b, kind) in self.dma_sems:
                key = ("dma", b.uid, kind)
                c = b.dcnt[kind]
                if c and self.seen[E].get(key, 0) < c:
                    self.seen[E][key] = c
                    self.e[E].wait_ge(b.dsem[kind], c)

    def finish(self):
        self.barrier()


D = 1024
DFF = 2816
NFC = DFF // 128
NKC = D // 128
EPS = 1e-6


def dram_in(nc, name, shape, dt=F32):
    return nc.dram_tensor(name, list(shape), dt, kind="ExternalInput").ap()


def dram_out(nc, name, shape, dt=F32):
    return nc.dram_tensor(name, list(shape), dt, kind="ExternalOutput").ap()


class RR:
    def __init__(self, items):
        self.items = items
        self.i = 0

    def next(self):
        b = self.items[self.i % len(self.items)]
        self.i += 1
        return b


def emit_norm(s, epsc, xt, hb_rr, scr, stat_rr, ntile):
    nc = s.nc
    st = stat_rr.next()
    hbs = [hb_rr.next() for _ in range(ntile)]
    for j in range(ntile):
        s.op("act", lambda: nc.scalar.activation(out=hbs[j][:, :], in_=xt[j][:, :], func=AF.Square, scale=1.0 / 32.0,
                                                 accum_out=st[:, j:j + 1]),
             reads=[xt[j]], writes=[hbs[j], st])
    s.op("act", lambda: nc.scalar.activation(out=st[:, 4:4 + ntile], in_=st[:, 0:ntile], func=AF.Ln, bias=epsc[:, 0:1], scale=1.0),
         reads=[st, epsc], writes=[st])
    s.op("act", lambda: nc.scalar.activation(out=st[:, 8:8 + ntile], in_=st[:, 4:4 + ntile], func=AF.Exp, scale=-0.5),
         reads=[st], writes=[st])
    for j in range(ntile):
        s.op("dve", lambda: nc.vector.tensor_scalar(out=hbs[j][:, :], in0=xt[j][:, :], scalar1=st[:, 8 + j:9 + j], scalar2=None,
                                                    op0=ALU.mult), reads=[xt[j], st], writes=[hbs[j]])
    return hbs


def emit_transpose_T(s, hbs, gcol, hT, ident_b, pT_rr, ntile, evac_engs=("dve",)):
    nc = s.nc
    k = 0
    for c in range(NKC):
        pT = pT_rr.next()
        for j in range(ntile):
            s.op("pe", lambda: nc.tensor.transpose(out=pT[:, j * 128:(j + 1) * 128], in_=hbs[j][:, c * 128:(c + 1) * 128],
                                                   identity=ident_b[:, :]),
                 reads=[hbs[j], ident_b], writes=[pT], inc=(j == ntile - 1))
        eng = evac_engs[k % len(evac_engs)]
        k += 1
        if eng == "dve":
            s.op("dve", lambda: nc.vector.tensor_scalar(out=hT[:, c, 0:ntile * 128], in0=pT[:, 0:ntile * 128],
                                                        scalar1=gcol[:, c:c + 1], scalar2=None, op0=ALU.mult),
                 reads=[pT, gcol], writes=[hT])
        else:
            s.op("act", lambda: nc.scalar.activation(out=hT[:, c, 0:ntile * 128], in_=pT[:, 0:ntile * 128],
                                                     func=AF.Copy, scale=gcol[:, c:c + 1]),
                 reads=[pT, gcol], writes=[hT])


def emit_rmsnorm_T(s, epsc, xt, gcol, hT, ident_b, pT_rr, hb_rr, scr, stat_rr, ntile, evac_engs=("dve",)):
    hbs = emit_norm(s, epsc, xt, hb_rr, scr, stat_rr, ntile)
    emit_transpose_T(s, hbs, gcol, hT, ident_b, pT_rr, ntile, evac_engs)


def build_ffn(NT, TB=512):
    nc = bass.Bass("TRN2", target_bir_lowering=False)
    x = dram_in(nc, "x", [NT, D])
    g = dram_in(nc, "g", [D])
    w_in = dram_in(nc, "w_in", [D, 2 * DFF])
    w_out = dram_in(nc, "w_out", [DFF, D])
    ident = dram_in(nc, "ident", [128, 128], BF16)
    y = dram_out(nc, "y", [NT, D])
    s = S(nc)
    emit_ffn(s, x, g, w_in, w_out, ident, y, NT, TB)
    s.finish()
    s.close()
    return nc


def emit_ffn(s, x, g, w_in, w_out, ident, y, NT, TB=512):
    nc = s.nc
    ntile = TB // 128
    m_ = s.mark()
    ident_b = s.sb("ident_b", [128, 128], BF16)
    s.dma("sp", ident_b[:, :], ident[:, :], writes=[ident_b])
    gcol = s.sb("gcol", [128, NKC], F32)
    epsc = s.sb("epsc", [128, 1], F32)
    s.op("dve", lambda: nc.vector.memset(epsc[:, :], EPS), writes=[epsc])
    s.dma("sp", gcol[:, :], g.rearrange("(c p) -> p c", p=128), writes=[gcol], allow_slow_non_contiguous=True)
    win_b = [s.sb("win_b%d" % c, [128, 2 * DFF], BF16) for c in range(NKC)]
    wout_b = [s.sb("wout_b%d" % f, [128, D], BF16) for f in range(NFC)]
    for c in range(NKC):
        for hf in range(2):
            s.dma("pool", win_b[c][:, hf * DFF:(hf + 1) * DFF], w_in[c * 128:(c + 1) * 128, hf * DFF:(hf + 1) * DFF],
                  writes=[win_b[c]])
    for f in range(NFC):
        s.dma("pool", wout_b[f][:, :], w_out[f * 128:(f + 1) * 128, :], writes=[wout_b[f]])
    xn = [s.sb("xn%d" % j, [128, D], F32) for j in range(ntile)]
    xr_rr = RR([s.sb("xr%d" % j, [128, D], F32) for j in range(1)])
    hb_rr = RR([s.sb("hb%d" % j, [128, D], BF16) for j in range(2 * ntile)])
    stat_rr = RR([s.sb("st%d" % j, [128, 16], F32) for j in range(4)])
    hT = s.sb("hT", [128, NKC, TB], BF16)
    pT_rr = RR([s.ps("pT%d" % j, [128, TB], BF16) for j in range(2)])
    pa_rr = RR([s.ps("pa%d" % j, [128, TB], F32) for j in range(2)])
    pb_rr = RR([s.ps("pb%d" % j, [128, TB], F32) for j in range(2)])
    po_rr = RR([s.ps("po%d" % j, [128, 512], F32) for j in range(2)])
    sa_rr = RR([s.sb("sa%d" % j, [128, TB], F32) for j in range(2)])
    act = [s.sb("actT%d" % f, [128, TB], BF16) for f in range(NFC)]
    nblk = NT // TB

    def prep_a_rot(tb):
        for j in range(ntile):
            r0 = tb * TB + j * 128
            s.dma("sp", xn[j][:, :], x[r0:r0 + 128, :], writes=[xn[j]])
        return emit_norm(s, epsc, xn, hb_rr, None, stat_rr, ntile)

    hbs_next = prep_a_rot(0)
    emit_transpose_T(s, hbs_next, gcol, hT, ident_b, pT_rr, ntile)
    if nblk > 1:
        hbs_next = prep_a_rot(1)
    for tb in range(nblk):
        for f in range(NFC):
            pa = pa_rr.next()
            pb = pb_rr.next()
            for c in range(NKC):
                s.op("pe", lambda: nc.tensor.matmul(pa[:, :], lhsT=win_b[c][:, f * 128:(f + 1) * 128], rhs=hT[:, c, :],
                                                    start=(c == 0), stop=(c == NKC - 1)),
                     reads=[win_b[c], hT], writes=[pa], inc=(c == NKC - 1))
            for c in range(NKC):
                s.op("pe", lambda: nc.tensor.matmul(pb[:, :], lhsT=win_b[c][:, DFF + f * 128:DFF + (f + 1) * 128],
                                                    rhs=hT[:, c, :], start=(c == 0), stop=(c == NKC - 1)),
                     reads=[win_b[c], hT], writes=[pb], inc=(c == NKC - 1))
            sa = sa_rr.next()
            s.op("act", lambda: nc.scalar.activation(out=sa[:, :], in_=pa[:, :], func=AF.Silu), reads=[pa], writes=[sa])
            s.op("dve", lambda: nc.vector.tensor_tensor(out=act[f][:, :], in0=sa[:, :], in1=pb[:, :], op=ALU.mult),
                 reads=[sa, pb], writes=[act[f]])
        if tb + 1 < nblk:
            emit_transpose_T(s, hbs_next, gcol, hT, ident_b, pT_rr, ntile)
        if tb + 2 < nblk:
            hbs_next = prep_a_rot(tb + 2)
        for j in range(ntile):
            r0 = tb * TB + j * 128
            xr = xr_rr.next()
            s.dma("sp", xr[:, :], x[r0:r0 + 128, :], writes=[xr])
            for hf in range(2):
                po = po_rr.next()
                for f in range(NFC):
                    s.op("pe", lambda: nc.tensor.matmul(po[:, :], lhsT=act[f][:, j * 128:(j + 1) * 128],
                                                        rhs=wout_b[f][:, hf * 512:(hf + 1) * 512],
                                                        start=(f == 0), stop=(f == NFC - 1)),
                         reads=[act[f], wout_b[f]], writes=[po], inc=(f == NFC - 1))
                s.op("dve", lambda: nc.vector.scalar_tensor_tensor(out=xr[:, hf * 512:(hf + 1) * 512], in0=po[:, :],
                                                                   scalar=0.5, in1=xr[:, hf * 512:(hf + 1) * 512],
                                                                   op0=ALU.mult, op1=ALU.add),
                     reads=[po, xr], writes=[xr])
            s.dma("sp", y[r0:r0 + 128, :], xr[:, :], reads=[xr])
    s.release(m_)


FM_SRC = [[(0, 128)], [(128, 128)], [(256, 128)], [(384, 128)],
          [(768, 128)], [(896, 128)], [(1024, 128)], [(1152, 128)],
          [(1540, 128)], [(1668, 128)], [(1924, 64), (2052, 64)], [(1796, 128)]]
FM_GCOL = [0, 0, 1, 1, 2, 2, 3, 3, 4, 4, 5, None]
FM_BLK = [0, 0, 0, 0, 1, 1, 1, 1, 1, 1, 1, None]
TM_SRC = [[(512, 256), (1280, 256)],
          [(1536, 4), (2180, 12), (1988, 64), (2116, 64)],
          [(2192, 512)]]
NFM = 12
GELU_C = 1.5957691216057308


def build_proj(NT, TB=512):
    nc = bass.Bass("TRN2", target_bir_lowering=False)
    a = dict(
        x=dram_in(nc, "x", [NT, D]), g=dram_in(nc, "g", [D]), w_in=dram_in(nc, "w_in", [D, 6800]),
        ident=dram_in(nc, "ident", [128, 128], BF16), gains=dram_in(nc, "gains", [128, 6]),
        blk=dram_in(nc, "blk", [2, 128, 128], BF16), vgain=dram_in(nc, "vgain", [128, 256]),
        wsT=dram_in(nc, "wsT", [4, 128, 128]), triu=dram_in(nc, "triu", [128, 128]), bsT=dram_in(nc, "bsT", [128, 4]),
        zfm=dram_out(nc, "zfm", [NFM, 128, NT], BF16), vab=dram_out(nc, "vab", [NT, 512], BF16),
        vsw=dram_out(nc, "vsw", [NT, 128], BF16), misc=dram_out(nc, "misc", [NT, 16]), od=dram_out(nc, "od", [NT, 256], BF16))
    s = S(nc)
    emit_proj(s, a, NT, TB)
    s.finish()
    s.close()
    return nc


def emit_proj(s, a, NT, TB=512):
    nc = s.nc
    x, g, w_in, ident, gains, blk, vgain, wsT, triu, bsT = (a[k] for k in
                                                            ("x", "g", "w_in", "ident", "gains", "blk", "vgain", "wsT", "triu", "bsT"))
    zfm, vab, vsw, misc, od = (a[k] for k in ("zfm", "vab", "vsw", "misc", "od"))
    m_ = s.mark()
    ntile = TB // 128
    ident_b = s.sb("ident_b", [128, 128], BF16)
    s.dma("sp", ident_b[:, :], ident[:, :], writes=[ident_b])
    gcol = s.sb("gcol", [128, NKC], F32)
    s.dma("sp", gcol[:, :], g.rearrange("(c p) -> p c", p=128), writes=[gcol], allow_slow_non_contiguous=True)
    epsc = s.sb("epsc", [128, 1], F32)
    s.op("dve", lambda: nc.vector.memset(epsc[:, :], EPS), writes=[epsc])
    gn = s.sb("gn", [128, 6], F32)
    s.dma("sp", gn[:, :], gains[:, :], writes=[gn])
    for col, sc in ((0, 32.0 ** -0.5), (2, 0.125), (4, 0.125)):
        s.op("dve", lambda: nc.vector.tensor_scalar(out=gn[:, col:col + 1], in0=gn[:, col:col + 1], scalar1=sc,
                                                    scalar2=None, op0=ALU.mult), reads=[gn], writes=[gn])
    blk_b = [s.sb("blk%d" % i, [128, 128], BF16) for i in range(2)]
    for i in range(2):
        s.dma("sp", blk_b[i][:, :], blk[i], writes=[blk_b[i]])
    vg = s.sb("vg", [128, 256], F32)
    s.dma("sp", vg[:, :], vgain[:, :], writes=[vg])
    bcol = s.sb("bcol", [128, 4], F32)
    s.dma("sp", bcol[:, :], bsT[:, :], writes=[bcol])
    tri = s.sb("tri", [128, 128], F32)
    s.dma("sp", tri[:, :], triu[:, :], writes=[tri])
    wm = []
    wtmp = s.sb("wtmp", [128, 128], F32)
    for gi in range(4):
        w = s.sb("wm%d" % gi, [128, 128], BF16)
        s.dma("sp", wtmp[:, :], wsT[gi], writes=[wtmp])
        s.op("dve", lambda: nc.vector.tensor_tensor(out=w[:, :], in0=wtmp[:, :], in1=tri[:, :], op=ALU.mult),
             reads=[wtmp, tri], writes=[w])
        wm.append(w)
    wfm = [s.sb("wfm%d" % c, [128, NFM * 128], BF16) for c in range(NKC)]
    wtm = [s.sb("wtm%d" % c, [128, 1168], BF16) for c in range(NKC)]
    FM_RUNS = [(0, 0, 512), (512, 768, 512), (1024, 1540, 256), (1280, 1924, 64), (1344, 2052, 64), (1408, 1796, 128)]
    TM_RUNS = [(0, 512, 256), (256, 1280, 256), (512, 1536, 4), (516, 2180, 12), (528, 1988, 64), (592, 2116, 64), (656, 2192, 512)]
    for c in range(NKC):
        for (o, c0, n) in FM_RUNS:
            s.dma("pool", wfm[c][:, o:o + n], w_in[c * 128:(c + 1) * 128, c0:c0 + n], writes=[wfm[c]])
        for (o, c0, n) in TM_RUNS:
            s.dma("pool", wtm[c][:, o:o + n], w_in[c * 128:(c + 1) * 128, c0:c0 + n], writes=[wtm[c]])
    xts = [[s.sb("xt%d_%d" % (k, j), [128, D], F32) for j in range(ntile)] for k in range(2)]
    hb_rr = RR([s.sb("hb%d" % j, [128, D], BF16) for j in range(2 * ntile)])
    scr = s.sb("scr", [128, D], BF16)
    stat_rr = RR([s.sb("st%d" % j, [128, 16], F32) for j in range(4)])
    hTs = [s.sb("hT%d" % k, [128, NKC, TB], BF16) for k in range(2)]
    pT_rr = RR([s.ps("pT%d" % j, [128, TB], BF16) for j in range(2)])
    pz_rr = RR([s.ps("pz%d" % j, [128, 512], F32) for j in range(2)])
    ptm_rr = RR([s.ps("ptm%d" % j, [128, 512], F32) for j in range(2)])
    pq_rr = RR([s.ps("pq%d" % j, [128, 512], F32) for j in range(2)])
    sq_rr = RR([s.sb("sq%d" % j, [128, TB], BF16) for j in range(4)])
    rs_rr = RR([s.sb("rs%d" % j, [128, TB], F32) for j in range(2)])
    zo_rr = RR([s.sb("zo%d" % j, [128, TB], BF16) for j in range(4)])
    vab_rr = RR([s.sb("vabt%d" % j, [128, 512], BF16) for j in range(2)])
    vsw_rr = RR([s.sb("vswt%d" % j, [128, 128], BF16) for j in range(2)])
    msc_rr = RR([s.sb("msct%d" % j, [128, 16], F32) for j in range(2)])
    f_rr = RR([s.sb("gf%d" % j, [128, 512], F32) for j in range(4)])
    ge_rr = RR([s.sb("ge%d" % j, [128, 512], F32) for j in range(3)])
    zs_rr = RR([s.sb("zs%d" % j, [128, 512], F32) for j in range(2)])
    zf_rr = RR([s.sb("zf%d" % j, [128, TB], F32) for j in range(3)])
    vn_rr = RR([s.sb("vn%d" % j, [128, 256], BF16) for j in range(3)])
    od_rr = RR([s.sb("odt%d" % j, [128, 256], BF16) for j in range(2)])
    nblk = NT // TB

    def prep(tb):
        xt = xts[tb % 2]
        for j in range(ntile):
            s.dma("sp", xt[j][:, :], x[tb * TB + j * 128:tb * TB + (j + 1) * 128, :], writes=[xt[j]])
        emit_rmsnorm_T(s, epsc, xt, gcol, hTs[tb % 2], ident_b, pT_rr, hb_rr, scr, stat_rr, ntile)

    prep(0)
    pipe = Pipe(2)
    for tb in range(nblk):
        t0 = tb * TB
        hT = hTs[tb % 2]
        for i in range(NFM):
            pz = pz_rr.next()
            for c in range(NKC):
                s.op("pe", lambda: nc.tensor.matmul(pz[:, 0:TB], lhsT=wfm[c][:, i * 128:(i + 1) * 128], rhs=hT[:, c, :],
                                                    start=(c == 0), stop=(c == NKC - 1)),
                     reads=[wfm[c], hT], writes=[pz], inc=(c == NKC - 1))
            zo = zo_rr.next()
            if FM_GCOL[i] is None:
                s.op("dve", lambda: nc.vector.tensor_copy(out=zo[:, :], in_=pz[:, 0:TB]), reads=[pz], writes=[zo])
                s.dma("sp", zfm[i, :, t0:t0 + TB], zo[:, :], reads=[zo])
            else:
                zf = zf_rr.next()
                s.op("dve", lambda: nc.vector.tensor_copy(out=zf[:, :], in_=pz[:, 0:TB]), reads=[pz], writes=[zf])
                sq = sq_rr.next()
                s.op("act", lambda: nc.scalar.activation(out=sq[:, :], in_=zf[:, :], func=AF.Square),
                     reads=[zf], writes=[sq])

                def back(i=i, zf=zf, sq=sq, zo=zo, t0=t0):
                    gs = 32.0 if FM_BLK[i] == 0 else 64.0
                    pq = pq_rr.next()
                    s.op("pe", lambda: nc.tensor.matmul(pq[:, 0:TB], lhsT=blk_b[FM_BLK[i]][:, :], rhs=sq[:, :],
                                                        start=True, stop=True), reads=[blk_b[FM_BLK[i]], sq], writes=[pq])
                    rs = rs_rr.next()
                    s.op("act", lambda: nc.scalar.activation(out=rs[:, :], in_=pq[:, 0:TB], func=AF.Ln, bias=epsc[:, 0:1],
                                                             scale=1.0 / gs), reads=[pq, epsc], writes=[rs])
                    s.op("act", lambda: nc.scalar.activation(out=rs[:, :], in_=rs[:, :], func=AF.Exp, scale=-0.5),
                         reads=[rs], writes=[rs])
                    gc = FM_GCOL[i]
                    s.op("dve", lambda: nc.vector.scalar_tensor_tensor(out=zo[:, :], in0=zf[:, :], scalar=gn[:, gc:gc + 1],
                                                                       in1=rs[:, :], op0=ALU.mult, op1=ALU.mult),
                         reads=[zf, gn, rs], writes=[zo])
                    s.dma("sp", zfm[i, :, t0:t0 + TB], zo[:, :], reads=[zo])
                pipe.push(back)
        if tb + 1 < nblk:
            prep(tb + 1)
        for j in range(ntile):
            r0 = t0 + j * 128
            pz = ptm_rr.next()
            for c in range(NKC):
                s.op("pe", lambda: nc.tensor.matmul(pz[:, :], lhsT=hT[:, c, j * 128:(j + 1) * 128], rhs=wtm[c][:, 0:512],
                                                    start=(c == 0), stop=(c == NKC - 1)),
                     reads=[wtm[c], hT], writes=[pz], inc=(c == NKC - 1))
            vt = vab_rr.next()
            s.op("act", lambda: nc.scalar.copy(out=vt[:, :], in_=pz[:, :]), reads=[pz], writes=[vt])
            s.dma("sp", vab[r0:r0 + 128, :], vt[:, :], reads=[vt])
            pz = ptm_rr.next()
            for c in range(NKC):
                s.op("pe", lambda: nc.tensor.matmul(pz[:, 0:144], lhsT=hT[:, c, j * 128:(j + 1) * 128], rhs=wtm[c][:, 512:656],
                                                    start=(c == 0), stop=(c == NKC - 1)),
                     reads=[wtm[c], hT], writes=[pz], inc=(c == NKC - 1))
            mt = msc_rr.next()
            vs_ = vsw_rr.next()
            s.op("dve", lambda: nc.vector.tensor_copy(out=mt[:, :], in_=pz[:, 0:16]), reads=[pz], writes=[mt])
            s.op("dve", lambda: nc.vector.tensor_copy(out=vs_[:, :], in_=pz[:, 16:144]), reads=[pz], writes=[vs_])
            s.dma("sp", misc[r0:r0 + 128, :], mt[:, :], reads=[mt])
            s.dma("sp", vsw[r0:r0 + 128, :], vs_[:, :], reads=[vs_])
            pz = ptm_rr.next()
            for c in range(NKC):
                s.op("pe", lambda: nc.tensor.matmul(pz[:, :], lhsT=hT[:, c, j * 128:(j + 1) * 128], rhs=wtm[c][:, 656:1168],
                                                    start=(c == 0), stop=(c == NKC - 1)),
                     reads=[wtm[c], hT], writes=[pz], inc=(c == NKC - 1))
            zs = zs_rr.next()
            s.op("act", lambda: nc.scalar.copy(out=zs[:, :], in_=pz[:, :]), reads=[pz], writes=[zs])
            z2 = f_rr.next()
            s.op("act", lambda: nc.scalar.activation(out=z2[:, :], in_=zs[:, :], func=AF.Square), reads=[zs], writes=[z2])
            s.op("dve", lambda: nc.vector.tensor_scalar(out=z2[:, :], in0=z2[:, :], scalar1=0.044715, scalar2=1.0,
                                                        op0=ALU.mult, op1=ALU.add), reads=[z2], writes=[z2])
            s.op("dve", lambda: nc.vector.tensor_tensor(out=z2[:, :], in0=z2[:, :], in1=zs[:, :], op=ALU.mult),
                 reads=[z2, zs], writes=[z2])
            s.op("act", lambda: nc.scalar.activation(out=z2[:, :], in_=z2[:, :], func=AF.Exp, scale=-GELU_C),
                 reads=[z2], writes=[z2])
            s.op("act", lambda: nc.scalar.activation(out=z2[:, :], in_=z2[:, :], func=AF.Ln, bias=1.0, scale=1.0),
                 reads=[z2], writes=[z2])
            s.op("act", lambda: nc.scalar.activation(out=z2[:, :], in_=z2[:, :], func=AF.Exp, scale=-1.0),
                 reads=[z2], writes=[z2])
            ge = ge_rr.next()
            s.op("dve", lambda: nc.vector.tensor_tensor(out=ge[:, :], in0=z2[:, :], in1=zs[:, :], op=ALU.mult),
                 reads=[z2, zs], writes=[ge])
            sqv = f_rr.next()
            st = stat_rr.next()
            s.op("act", lambda: nc.scalar.activation(out=sqv[:, 0:256], in_=ge[:, 256:512], func=AF.Square),
                 reads=[ge], writes=[sqv])
            s.op("dve", lambda: nc.vector.tensor_reduce(out=st[:, 0:4], in_=sqv[:, 0:256].rearrange("p (g d) -> p g d", g=4),
                                                        axis=AX.X, op=ALU.add), reads=[sqv], writes=[st])
            s.op("act", lambda: nc.scalar.activation(out=st[:, 0:4], in_=st[:, 0:4], func=AF.Ln, bias=epsc[:, 0:1],
                                                     scale=1.0 / 64.0), reads=[st, epsc], writes=[st])
            s.op("act", lambda: nc.scalar.activation(out=st[:, 0:4], in_=st[:, 0:4], func=AF.Exp, scale=-0.5),
                 reads=[st], writes=[st])
            vn = vn_rr.next()
            for gi in range(4):
                s.op("dve", lambda: nc.vector.scalar_tensor_tensor(
                    out=vn[:, gi * 64:(gi + 1) * 64], in0=ge[:, 256 + gi * 64:256 + (gi + 1) * 64], scalar=st[:, gi:gi + 1],
                    in1=vg[:, gi * 64:(gi + 1) * 64], op0=ALU.mult, op1=ALU.mult), reads=[ge, st, vg], writes=[vn])

            def back2(vn=vn, ge=ge, r0=r0):
                pq = pq_rr.next()
                for gi in range(4):
                    s.op("pe", lambda: nc.tensor.matmul(pq[:, gi * 64:(gi + 1) * 64], lhsT=wm[gi][:, :],
                                                        rhs=vn[:, gi * 64:(gi + 1) * 64], start=True, stop=True),
                         reads=[wm[gi], vn], writes=[pq], inc=(gi == 3))
                ot = od_rr.next()
                for gi in range(4):
                    s.op("dve", lambda: nc.vector.scalar_tensor_tensor(
                        out=ot[:, gi * 64:(gi + 1) * 64], in0=pq[:, gi * 64:(gi + 1) * 64], scalar=bcol[:, gi:gi + 1],
                        in1=ge[:, gi * 64:(gi + 1) * 64], op0=ALU.add, op1=ALU.mult), reads=[pq, bcol, ge], writes=[ot])
                s.dma("sp", od[r0:r0 + 128, :], ot[:, :], reads=[ot])
            pipe.push(back2)
    pipe.flush()
    s.release(m_)


def _bf(a):
    import ml_dtypes
    return np.ascontiguousarray(a).astype(ml_dtypes.bfloat16)


def proj_consts():
    blk = np.zeros((2, 128, 128), np.float32)
    for i in range(128):
        for j in range(128):
            if i // 32 == j // 32:
                blk[0, i, j] = 1
            if i // 64 == j // 64:
                blk[1, i, j] = 1
    triu = np.triu(np.ones((128, 128), np.float32))
    return dict(ident=_bf(np.eye(128, dtype=np.float32)), blk=_bf(blk), triu=triu)


def proj_params(g, w_in, dq, dk, fq, fk, nq, nk, vgain, w_s, b_s):
    gains = np.stack([np.tile(dq, 4), np.tile(dk, 4), np.tile(fq, 2), np.tile(fk, 2), np.tile(nq, 2), np.tile(nk, 2)], 1)
    return dict(g=np.ascontiguousarray(g), w_in=np.ascontiguousarray(w_in), gains=np.ascontiguousarray(gains, dtype=np.float32),
                vgain=np.ascontiguousarray(np.broadcast_to(vgain[None, :], (128, 256))),
                wsT=np.ascontiguousarray(w_s.transpose(0, 2, 1)), bsT=np.ascontiguousarray(b_s.T))


NEG = -30000.0


def load_vt(s, vt, io, key, T, init=True, load=True):
    nc = s.nc
    NT = T // 128
    if init:
        s.op("pool", lambda: nc.gpsimd.memset(vt[:, :, 64:128], 0.0), writes=[vt])
        s.op("pool", lambda: nc.gpsimd.memset(vt[:, :, 64:65], 1.0), writes=[vt])
    if not load:
        return
    if key + "_src" in io:
        src = io[key + "_src"]
        step = 8
        for j0 in range(0, NT, step):
            j1 = min(NT, j0 + step)
            s.dma("sp", vt[:, j0:j1, 0:64], src[j0 * 128:j1 * 128, :].rearrange("(j p) d -> p j d", p=128), writes=[vt])
    else:
        s.dma("sp", vt[:, :, 0:64], io[key][:, :, 0:64], writes=[vt])


class Pipe:
    def __init__(self, lag):
        self.q = []
        self.lag = lag

    def push(self, fn):
        self.q.append(fn)
        while len(self.q) > self.lag:
            self.q.pop(0)()

    def flush(self):
        while self.q:
            self.q.pop(0)()


def emit_attn_phase(s, cm, T, nsub, qT, kT, vt, kparts, out_dram, finalize, bias_fn=None, name="a", lag=2):
    nc = s.nc
    NQB = T // 512
    pipe = Pipe(lag)
    fin_pending = None
    for qb in range(NQB):
        q0 = qb * 512
        pos = [cm["po_rr"].next() for _ in range(nsub)]
        nt = 4 * qb + 4
        njob = 0
        for t in range(nt):
            di = t - 4 * qb
            c0 = 128 * di if di > 0 else 0
            for i in range(nsub):
                kz = kT[i]
                ps = cm["ps_rr"].next()
                s.op("pe", lambda: nc.tensor.matmul(ps[:, c0:512], lhsT=kz[:, t * 128:(t + 1) * 128],
                                                    rhs=qT[:, q0 + c0:q0 + 512], start=True, stop=(di < 0)),
                     reads=[kz, qT], writes=[ps], inc=(di < 0))
                if di >= 0:
                    s.op("pe", lambda: nc.tensor.matmul(ps[:, c0:c0 + 128], lhsT=cm["ident_b"][:, :], rhs=cm["tri_b"][:, :],
                                                        start=False, stop=True),
                         reads=[cm["ident_b"], cm["tri_b"]], writes=[ps])
                pt = cm["pt_rr"].next()
                if bias_fn is None:
                    s.op("act", lambda: nc.scalar.activation(out=pt[:, c0:512], in_=ps[:, c0:512], func=AF.Exp),
                         reads=[ps], writes=[pt])
                else:
                    bb, bap = bias_fn(qb, t)
                    s.op("act", lambda: nc.scalar.activation(out=pt[:, c0:512], in_=ps[:, c0:512], func=AF.Exp, bias=bap),
                         reads=[ps, bb], writes=[pt])

                def pv(po=pos[i], t=t, c0=c0, pt=pt, nt=nt):
                    s.op("pe", lambda: nc.tensor.matmul(po[:, c0:512], lhsT=vt[:, t, :], rhs=pt[:, c0:512],
                                                        start=(t == 0), stop=(t == nt - 1)),
                         reads=[vt, pt], writes=[po])
                pipe.push(pv)
                njob += 1
                if fin_pending is not None and njob == lag:
                    fin_pending()
                    fin_pending = None
        if fin_pending is not None:
            pipe.flush()
            fin_pending()
        fin_pending = (lambda qb=qb, pos=pos: finalize(qb, pos))
    pipe.flush()
    if fin_pending is not None:
        fin_pending()


def emit_o_to_tokmajor(s, cm, po, pf, col0):
    nc = s.nc
    oc = cm["oc_rr"].next()
    s.op("dve", lambda: nc.vector.tensor_copy(out=oc[0:65, :], in_=po[0:65, :]), reads=[po], writes=[oc])
    for j in range(4):
        s.op("pe", lambda: nc.tensor.transpose(out=pf[:, j, col0:col0 + 65], in_=oc[0:65, j * 128:(j + 1) * 128],
                                               identity=cm["ident_f"][0:65, 0:65]),
             reads=[oc, cm["ident_f"]], writes=[pf], inc=(j == 3))


def build_mix_ab(T):
    nc = bass.Bass("TRN2", target_bir_lowering=False)
    io = mix_decl(nc, T, with_c=False)
    s = S(nc)
    cm = mix_common(s, io)
    emit_mix_a(s, cm, io, T)
    emit_mix_b(s, cm, io, T)
    s.finish()
    s.close()
    return nc


def mix_decl(nc, T, with_c=True):
    NT = T // 128
    io = dict(
        identb=dram_in(nc, "identb", [128, 128], BF16), identf=dram_in(nc, "identf", [128, 128]),
        trib=dram_in(nc, "trib", [128, 128], BF16),
        qa=dram_in(nc, "qa", [64, T], BF16), ka=dram_in(nc, "ka", [64, T], BF16), va=dram_in(nc, "va", [128, NT, 65], BF16),
        lamp=dram_in(nc, "lamp", [128, 4, 32]), lami=dram_in(nc, "lami", [128, 2]),
        qb=dram_in(nc, "qb", [64, T], BF16), kb=dram_in(nc, "kb", [64, T], BF16), vb=dram_in(nc, "vb", [128, NT, 65], BF16),
        flog=dram_in(nc, "flog", [128, NT]), fbias=dram_in(nc, "fbias", [128, 1]),
        triuf=dram_in(nc, "triuf", [128, 128]), onesf=dram_in(nc, "onesf", [128, 128]),
        oa=dram_out(nc, "oa", [T, 64], BF16), ob=dram_out(nc, "ob", [T, 64], BF16),
    )
    return io


def mix_common(s, io, n_ps=3, with_pf2=True):
    nc = s.nc
    cm = {}
    for nm, key, dt in (("ident_b", "identb", BF16), ("ident_f", "identf", F32), ("tri_b", "trib", BF16)):
        b = s.sb(nm, [128, 128], dt)
        s.dma("sp", b[:, :], io[key][:, :], writes=[b])
        cm[nm] = b
    cm["epsc"] = s.sb("epsc", [128, 1], F32)
    s.op("dve", lambda: nc.vector.memset(cm["epsc"][:, :], EPS), writes=[cm["epsc"]])
    cm["ps_rr"] = RR([s.ps("ps%d" % j, [128, 512], F32) for j in range(n_ps)])
    cm["po_rr"] = RR([s.ps("po%d" % j, [128, 512], F32) for j in range(3)])
    cm["pf"] = s.ps("pf", [128, 4, 128], F32)
    if with_pf2:
        cm["pf2"] = s.ps("pf2", [128, 4, 128], F32)
    cm["lag"] = n_ps - 1
    cm["o1s_rr"] = RR([s.sb("o1s%d" % j, [128, 4, 65], F32) for j in range(2)])
    cm["pt_rr"] = RR([s.sb("pt%d" % j, [128, 512], BF16) for j in range(n_ps + 2)])
    cm["oc_rr"] = RR([s.sb("oc%d" % j, [128, 512], F32) for j in range(2)])
    cm["st_rr"] = RR([s.sb("mst%d" % j, [128, 8], F32) for j in range(8)])
    cm["ot_rr"] = RR([s.sb("ot%d" % j, [128, 4, 64], BF16) for j in range(2)])
    cm["tmp_rr"] = RR([s.sb("tmp%d" % j, [128, 64], F32) for j in range(4)])
    return cm


def emit_mix_a(s, cm, io, T, stage="all", bufs=None):
    nc = s.nc
    NT = T // 128
    if stage in ("all", "alloc"):
        if stage == "all":
            m = s.mark()
        b = dict(qT=s.sb("a_q", [128, T], BF16), k1=s.sb("a_k1", [128, T], BF16), k2=s.sb("a_k2", [128, T], BF16),
                 vt=s.sb("a_v", [128, NT, 128], BF16), lp=s.sb("lp", [128, 4, 32], F32), li=s.sb("li", [128, 2], F32),
                 lw=s.sb("lw", [128, 2, 32], F32), lam=s.sb("lam", [128, 4], F32))
        s.op("pool", lambda: nc.gpsimd.memset(b["qT"][64:128, :], 0.0), writes=[b["qT"]])
        s.op("dve", lambda: nc.vector.memset(b["k1"][:, :], 0.0), writes=[b["k1"]])
        s.op("pool", lambda: nc.gpsimd.memset(b["k2"][:, :], 0.0), writes=[b["k2"]])
        load_vt(s, b["vt"], io, "va", T, init=True, load=False)
        if stage == "alloc":
            return b
        bufs = b
    qT, k1, k2, vt, lp, li, lw, lam = (bufs[k] for k in ("qT", "k1", "k2", "vt", "lp", "li", "lw", "lam"))
    if stage in ("all", "load"):
        s.dma("sp", qT[0:64, :], io["qa"][:, :], writes=[qT])
        s.dma("sp", k1[0:32, :], io["ka"][0:32, :], writes=[k1])
        s.dma("sp", k2[32:64, :], io["ka"][32:64, :], writes=[k2])
        load_vt(s, vt, io, "va", T, init=False)
        s.dma("sp", lp[:, :, :], io["lamp"][:, :, :], writes=[lp])
        s.dma("sp", li[:, :], io["lami"][:, :], writes=[li])
        s.op("dve", lambda: nc.vector.tensor_tensor(out=lw[:, 0, :], in0=lp[:, 0, :], in1=lp[:, 1, :], op=ALU.mult),
             reads=[lp], writes=[lw])
        s.op("dve", lambda: nc.vector.tensor_tensor(out=lw[:, 1, :], in0=lp[:, 2, :], in1=lp[:, 3, :], op=ALU.mult),
             reads=[lp], writes=[lw])
        s.op("dve", lambda: nc.vector.tensor_reduce(out=lam[:, 0:2], in_=lw[:, :, :], axis=AX.X, op=ALU.add),
             reads=[lw], writes=[lam])
        s.op("act", lambda: nc.scalar.activation(out=lam[:, 0:2], in_=lam[:, 0:2], func=AF.Exp), reads=[lam], writes=[lam])
        s.op("dve", lambda: nc.vector.tensor_tensor(out=lam[:, 2:3], in0=lam[:, 1:2], in1=lam[:, 0:1], op=ALU.subtract),
             reads=[lam], writes=[lam])
        s.op("dve", lambda: nc.vector.tensor_tensor(out=lam[:, 3:4], in0=lam[:, 2:3], in1=li[:, 0:1], op=ALU.subtract),
             reads=[lam, li], writes=[lam])
        if stage == "load":
            return

    def fin(qb, pos):
        if "pf2" in cm:
            pf = cm["pf"]
            pf2 = cm["pf2"]
            emit_o_to_tokmajor(s, cm, pos[0], pf, 0)
            emit_o_to_tokmajor(s, cm, pos[1], pf2, 0)
        else:
            pf2 = cm["pf"]
            emit_o_to_tokmajor(s, cm, pos[0], pf2, 0)
            pf = cm["o1s_rr"].next()
            s.op("dve", lambda: nc.vector.tensor_copy(out=pf[:, :, :], in_=pf2[:, :, 0:65]), reads=[pf2], writes=[pf])
            emit_o_to_tokmajor(s, cm, pos[1], pf2, 0)
        ot = cm["ot_rr"].next()
        for j in range(4):
            st = cm["st_rr"].next()
            s.op("dve", lambda: nc.vector.tensor_scalar(out=st[:, 0:1], in0=pf[:, j, 64:65], scalar1=1e-30, scalar2=None,
                                                        op0=ALU.max), reads=[pf], writes=[st])
            s.op("dve", lambda: nc.vector.tensor_scalar(out=st[:, 1:2], in0=pf2[:, j, 64:65], scalar1=1e-30, scalar2=None,
                                                        op0=ALU.max), reads=[pf2], writes=[st])
            s.op("dve", lambda: nc.vector.reciprocal(out=st[:, 0:2], in_=st[:, 0:2]), reads=[st], writes=[st])
            t2 = cm["tmp_rr"].next()
            o = cm["tmp_rr"].next()
            s.op("dve", lambda: nc.vector.tensor_scalar(out=t2[:, :], in0=pf2[:, j, 0:64], scalar1=st[:, 1:2],
                                                        scalar2=lam[:, 3:4], op0=ALU.mult, op1=ALU.mult),
                 reads=[pf2, st, lam], writes=[t2])
            s.op("dve", lambda: nc.vector.scalar_tensor_tensor(out=o[:, :], in0=pf[:, j, 0:64], scalar=st[:, 0:1], in1=t2[:, :],
                                                               op0=ALU.mult, op1=ALU.add), reads=[pf, st, t2], writes=[o])
            s.op("act", lambda: nc.scalar.activation(out=t2[:, :], in_=o[:, :], func=AF.Square, accum_out=st[:, 2:3]),
                 reads=[o], writes=[t2, st])
            s.op("act", lambda: nc.scalar.activation(out=st[:, 3:4], in_=st[:, 2:3], func=AF.Ln, bias=cm["epsc"][:, 0:1],
                                                     scale=1.0 / 64.0), reads=[st, cm["epsc"]], writes=[st])
            s.op("act", lambda: nc.scalar.activation(out=st[:, 4:5], in_=st[:, 3:4], func=AF.Exp, scale=-0.5),
                 reads=[st], writes=[st])
            s.op("dve", lambda: nc.vector.tensor_scalar(out=ot[:, j, :], in0=o[:, :], scalar1=st[:, 4:5], scalar2=li[:, 1:2],
                                                        op0=ALU.mult, op1=ALU.mult), reads=[o, st, li], writes=[ot])
        s.dma("sp", io["oa"][qb * 512:(qb + 1) * 512, :].rearrange("(j p) d -> p j d", p=128), ot[:, :, :], reads=[ot])

    emit_attn_phase(s, cm, T, 2, qT, [k1, k2], vt, None, io["oa"], fin, name="a", lag=cm["lag"])
    if stage == "all":
        s.release(m)


def emit_mix_b(s, cm, io, T, stage="all", bufs=None):
    nc = s.nc
    NT = T // 128
    NQB = T // 512
    if stage in ("all", "alloc"):
        if stage == "all":
            m = s.mark()
        b = dict(qT=s.sb("b_q", [128, T], BF16), kT=s.sb("b_k", [128, T], BF16), vt=s.sb("b_v", [128, NT, 128], BF16),
                 fl=s.sb("fl", [128, NT], F32), fb=s.sb("fb", [128, 2], F32), tu=s.sb("tu", [128, 128], F32),
                 on=s.sb("on", [128, 128], F32), cc=s.sb("cc", [128, NT], F32), inc=s.sb("inc", [128, NT], F32),
                 tmpc=s.sb("tmpc", [128, NT], F32), btab=s.sb("btab", [128, NQB, NT], F32))
        s.op("pool", lambda: nc.gpsimd.memset(b["qT"][64:128, :], 0.0), writes=[b["qT"]])
        s.op("dve", lambda: nc.vector.memset(b["kT"][64:128, :], 0.0), writes=[b["kT"]])
        load_vt(s, b["vt"], io, "vb", T, init=True, load=False)
        s.dma("sp", b["tu"][:, :], io["triuf"][:, :], writes=[b["tu"]])
        s.dma("sp", b["on"][:, :], io["onesf"][:, :], writes=[b["on"]])
        if stage == "alloc":
            return b
        bufs = b
    qT, kT, vt, fl, fb, tu, on, cc, inc_, tmpc, btab = (bufs[k] for k in ("qT", "kT", "vt", "fl", "fb", "tu", "on", "cc", "inc",
                                                                            "tmpc", "btab"))
    if stage in ("all", "load"):
        s.dma("sp", qT[0:64, :], io["qb"][:, :], writes=[qT])
        s.dma("sp", kT[0:64, :], io["kb"][:, :], writes=[kT])
        load_vt(s, vt, io, "vb", T, init=False)
        if stage == "load":
            return
    if "flog_sb" not in io:
        s.dma("sp", fl[:, :], io["flog"][:, :], writes=[fl])
    s.dma("sp", fb[:, 0:1], io["fbias"][:, :], writes=[fb])
    s.op("dve", lambda: nc.vector.tensor_scalar(out=fb[:, 1:2], in0=fb[:, 0:1], scalar1=-1.0, scalar2=None, op0=ALU.mult),
         reads=[fb], writes=[fb])
    if "flog_sb" in io:
        fsb, fap = io["flog_sb"]
        s.op("act", lambda: nc.scalar.activation(out=fl[:, :], in_=fap, func=AF.Exp, bias=fb[:, 1:2], scale=-1.0),
             reads=[fsb, fb], writes=[fl])
    else:
        s.op("act", lambda: nc.scalar.activation(out=fl[:, :], in_=fl[:, :], func=AF.Exp, bias=fb[:, 1:2], scale=-1.0),
             reads=[fl, fb], writes=[fl])
    s.op("act", lambda: nc.scalar.activation(out=fl[:, :], in_=fl[:, :], func=AF.Ln, bias=1.0, scale=1.0),
         reads=[fl], writes=[fl])
    pc = cm["pf"]
    pcv = pc[:, 0, :]
    s.op("pe", lambda: nc.tensor.matmul(pc[:, 0, 0:NT], lhsT=tu[:, :], rhs=fl[:, :], start=True, stop=True),
         reads=[tu, fl], writes=[pc])
    s.op("pe", lambda: nc.tensor.matmul(pc[:, 1, 0:NT], lhsT=on[:, :], rhs=fl[:, :], start=True, stop=True),
         reads=[on, fl], writes=[pc])
    s.op("dve", lambda: nc.vector.tensor_copy(out=inc_[:, :], in_=pc[:, 1, 0:NT]), reads=[pc], writes=[inc_])
    sh = 1
    while sh < NT:
        s.op("dve", lambda: nc.vector.tensor_copy(out=tmpc[:, :], in_=inc_[:, :]), reads=[inc_], writes=[tmpc])
        s.op("dve", lambda: nc.vector.tensor_tensor(out=inc_[:, sh:NT], in0=tmpc[:, sh:NT], in1=tmpc[:, 0:NT - sh], op=ALU.add),
             reads=[tmpc], writes=[inc_])
        sh *= 2
    s.op("dve", lambda: nc.vector.tensor_tensor(out=cc[:, :], in0=pc[:, 0, 0:NT], in1=inc_[:, :], op=ALU.add),
         reads=[pc, inc_], writes=[cc])
    s.op("dve", lambda: nc.vector.tensor_tensor(out=tmpc[:, :], in0=cc[:, :], in1=pc[:, 1, 0:NT], op=ALU.subtract),
         reads=[pc, cc], writes=[tmpc])
    for qb in range(NQB):
        s.op("dve", lambda: nc.vector.tensor_scalar(out=btab[:, qb, :], in0=tmpc[:, :], scalar1=inc_[:, 4 * qb + 1:4 * qb + 2],
                                                    scalar2=None, op0=ALU.subtract), reads=[tmpc, inc_], writes=[btab])

    def bias_fn(qb, t):
        return btab, btab[:, qb, t:t + 1]

    def fin(qb, pos):
        pf = cm["pf"]
        emit_o_to_tokmajor(s, cm, pos[0], pf, 0)
        ot = cm["ot_rr"].next()
        for j in range(4):
            st = cm["st_rr"].next()
            s.op("dve", lambda: nc.vector.tensor_scalar(out=st[:, 0:1], in0=pf[:, j, 64:65], scalar1=1e-30, scalar2=None,
                                                        op0=ALU.max), reads=[pf], writes=[st])
            s.op("dve", lambda: nc.vector.reciprocal(out=st[:, 0:1], in_=st[:, 0:1]), reads=[st], writes=[st])
            s.op("dve", lambda: nc.vector.tensor_scalar(out=ot[:, j, :], in0=pf[:, j, 0:64], scalar1=st[:, 0:1], scalar2=None,
                                                        op0=ALU.mult), reads=[pf, st], writes=[ot])
        s.dma("sp", io["ob"][qb * 512:(qb + 1) * 512, :].rearrange("(j p) d -> p j d", p=128), ot[:, :, :], reads=[ot])

    emit_attn_phase(s, cm, T, 1, qT, [kT], vt, None, io["ob"], fin, bias_fn=bias_fn, name="b", lag=cm["lag"])
    if stage == "all":
        s.release(m)


def mix_consts():
    k = np.arange(128)
    tri = np.where(k[:, None] > k[None, :], NEG, 0.0).astype(np.float32)
    return dict(identb=_bf(np.eye(128, dtype=np.float32)), identf=np.eye(128, dtype=np.float32), trib=_bf(tri),
                triuf=np.triu(np.ones((128, 128), np.float32)), onesf=np.ones((128, 128), np.float32))


def mix_decl_c(nc, io, T):
    NT = T // 128
    QL = NT // 4
    NCT = max(1, T // 2048)
    io.update(dict(
        qc=dram_in(nc, "qc", [128, QL, 512], BF16),
        kskw=dram_in(nc, "kskw", [128, T], BF16),
        vs=dram_in(nc, "vs", [128, NT, 65], BF16), vw=dram_in(nc, "vw", [128, NT, 65], BF16),
        kvin=dram_in(nc, "kvin", [128, T], BF16),
        w1=dram_in(nc, "w1", [2, 2048, 256]), b1=dram_in(nc, "b1", [128, 4]),
        peT=dram_in(nc, "peT", [128, 32]),
        w2=dram_in(nc, "w2", [2, 256, 64]), b2=dram_in(nc, "b2", [2, 64]), b2c=dram_in(nc, "b2c", [64, 1]),
        kgain=dram_in(nc, "kgain", [64, 1]),
        ng=dram_in(nc, "ng", [128, QL, 12]),
        cmask=dram_in(nc, "cmask", [128, QL, NCT, 128], BF16),
        smask=dram_in(nc, "smask", [128, 4, 128], BF16), wmask=dram_in(nc, "wmask", [128, 8, 128], BF16),
        impA=dram_in(nc, "impA", [128, QL, 128]), impB=dram_in(nc, "impB", [128, QL, 128]),
        emat=dram_in(nc, "emat", [128, NT, 128], BF16), ovl=dram_in(nc, "ovl", [128, NCT, 128], BF16),
        ones64=dram_in(nc, "ones64", [64, 64], BF16), onesrow=dram_in(nc, "onesrow", [1, 128], BF16),
        oc=dram_out(nc, "oc", [QL * 128, 256], BF16),
    ))
    return io


def emit_gelu(s, zin_ap, zin_b, out_ap, out_b, tmp, shape_sl):
    nc = s.nc
    t = tmp
    s.op("act", lambda: nc.scalar.activation(out=t[shape_sl], in_=zin_ap, func=AF.Square), reads=[zin_b], writes=[t])
    s.op("dve", lambda: nc.vector.tensor_scalar(out=t[shape_sl], in0=t[shape_sl], scalar1=0.044715, scalar2=1.0,
                                                op0=ALU.mult, op1=ALU.add), reads=[t], writes=[t])
    s.op("dve", lambda: nc.vector.tensor_tensor(out=t[shape_sl], in0=t[shape_sl], in1=zin_ap, op=ALU.mult),
         reads=[t, zin_b], writes=[t])
    s.op("act", lambda: nc.scalar.activation(out=t[shape_sl], in_=t[shape_sl], func=AF.Exp, scale=-GELU_C), reads=[t], writes=[t])
    s.op("dve", lambda: nc.vector.tensor_scalar(out=t[shape_sl], in0=t[shape_sl], scalar1=1.0, scalar2=None, op0=ALU.add),
         reads=[t], writes=[t])
    s.op("dve", lambda: nc.vector.reciprocal(out=t[shape_sl], in_=t[shape_sl]), reads=[t], writes=[t])
    s.op("dve", lambda: nc.vector.tensor_tensor(out=out_ap, in0=t[shape_sl], in1=zin_ap, op=ALU.mult),
         reads=[t, zin_b], writes=[out_b])


def emit_mix_c(s, cm, io, T, cs=None):
    fused = cs is not None
    cs = cs if fused else [None]
    nc = s.nc
    NT = T // 128
    QL = NT // 4
    NCT = max(1, T // 2048)
    Nc = T // 16 - 1
    NCP = NCT * 128 if Nc > 128 else 128
    NCW = min(Nc, 511)
    assert Nc <= 511
    m = s.mark()
    ident_b = cm["ident_b"]
    ps_l = cm["ps_rr"].items
    po_l = cm["po_rr"].items
    pf, pf2 = cm["pf"], cm["pf2"]

    def ld(name, shape, dt, src, q="sp"):
        b = s.sb(name, shape, dt)
        idx = tuple(slice(None) for _ in shape)
        s.dma(q, b[idx], src, writes=[b])
        return b

    qc = s.sb("c_q", [128, QL, 512], BF16)
    qc2 = s.sb("c_q2", [128, QL, 512], BF16)
    s.op("pool", lambda: nc.gpsimd.memset(qc[64:128, :, :], 0.0), writes=[qc])
    s.op("dve", lambda: nc.vector.memset(qc2[0:64, :, :], 0.0), writes=[qc2])
    kk = ld("c_kk", [128, T], BF16, io["kskw"][:, :])
    vs = s.sb("c_vs", [128, NT, 128], BF16)
    vw = s.sb("c_vw", [128, NT, 128], BF16)
    load_vt(s, vs, io, "vs", T)
    load_vt(s, vw, io, "vw", T)
    emat = ld("c_e", [128, NT, 128], BF16, io["emat"][:, :, :])
    ovl = ld("c_ovl", [128, NCT, 128], BF16, io["ovl"][:, :, :])
    smask = s.sb("c_sm", [128, 4, 128], BF16)
    wmask = s.sb("c_wm", [128, 8, 128], BF16)
    ngt = s.sb("c_ng", [128, QL, 12], F32)
    ones64 = ld("c_o64", [64, 64], BF16, io["ones64"][:, :])
    onesrow = ld("c_orow", [1, 128], BF16, io["onesrow"][:, :])
    kgain = ld("c_kg", [64, 1], F32, io["kgain"][:, :])
    b2c = ld("c_b2c", [64, 1], F32, io["b2c"][:, :])
    b1 = ld("c_b1", [128, 4], F32, io["b1"][:, :])

    ktc = s.sb("c_ktc", [128, NCP], BF16)
    vc = s.sb("c_vc", [128, NCT, 128], BF16)
    s.op("dve", lambda: nc.vector.memset(ktc[:, :], 0.0), writes=[ktc])
    s.op("dve", lambda: nc.vector.memset(vc[:, :, :], 0.0), writes=[vc])
    s.op("dve", lambda: nc.vector.memset(vc[:, :, 64:65], 1.0), writes=[vc])

    m2 = s.mark()
    kvin = ld("c_kvin", [128, T], BF16, io["kvin"][:, :])
    w1sb = s.sb("c_w1", [128, 32, 256], BF16)
    for x in range(2):
        s.dma("pool", w1sb[x * 64:(x + 1) * 64, :, :], io["w1"][x].rearrange("(j d) f -> d j f", d=64), writes=[w1sb])
    peT = s.sb("c_pe", [128, 32], BF16)
    s.dma("pool", peT[:, :], io["peT"][:, :], writes=[peT])
    w2sb = s.sb("c_w2", [128, 2, 2, 64], BF16)
    for x in range(2):
        s.dma("pool", w2sb[:, x, :, :], io["w2"][x].rearrange("(hh f) d -> f hh d", f=128), writes=[w2sb])
    b2row = s.sb("c_b2r", [1, 64], BF16)
    s.dma("pool", b2row[:, :], io["b2"][1:2, :], writes=[b2row])
    hacc = [ps_l[0], ps_l[1], ps_l[2], po_l[0]]
    pcol = po_l[1]
    for x in range(2):
        for hh in range(2):
            hp = hacc[x * 2 + hh]
            for j in range(32):
                s.op("pe", lambda: nc.tensor.matmul(hp[:, 0:NCW], lhsT=w1sb[x * 64:(x + 1) * 64, j, hh * 128:(hh + 1) * 128],
                                                    rhs=kvin[x * 64:(x + 1) * 64, j:j + 16 * (NCW - 1) + 1:16],
                                                    start=(j == 0), stop=(j == 31)),
                     reads=[w1sb, kvin], writes=[hp], inc=(j == 31))
            for j in range(32):
                s.op("pe", lambda: nc.tensor.matmul(pcol[:, x * 2 + hh:x * 2 + hh + 1],
                                                    lhsT=w1sb[x * 64:(x + 1) * 64, j, hh * 128:(hh + 1) * 128],
                                                    rhs=peT[x * 64:(x + 1) * 64, j:j + 1], start=(j == 0), stop=(j == 31)),
                     reads=[w1sb, peT], writes=[pcol], inc=(j == 31))
    hbias = s.sb("c_hb", [128, 4], F32)
    s.op("dve", lambda: nc.vector.tensor_tensor(out=hbias[:, :], in0=pcol[:, 0:4], in1=b1[:, :], op=ALU.add),
         reads=[pcol, b1], writes=[hbias])
    gh = []
    for x in range(2):
        for hh in range(2):
            k = x * 2 + hh
            z = s.sb("c_z%d" % k, [128, 512], F32)
            tmp = s.sb("c_zt%d" % k, [128, 512], F32)
            gb = s.sb("c_g%d" % k, [128, 512], BF16)
            s.op("act", lambda: nc.scalar.activation(out=z[:, 0:NCW], in_=hacc[k][:, 0:NCW], func=AF.Identity,
                                                     bias=hbias[:, k:k + 1], scale=1.0), reads=[hacc[k], hbias], writes=[z])
            emit_gelu(s, z[:, 0:NCW], z, gb[:, 0:NCW], gb, tmp, (slice(None), slice(0, NCW)))
            gh.append(gb)
    pk = po_l[2]
    for hh in range(2):
        s.op("pe", lambda: nc.tensor.matmul(pk[0:64, 0:NCW], lhsT=w2sb[:, 0, hh, :], rhs=gh[hh][:, 0:NCW],
                                            start=(hh == 0), stop=(hh == 1)), reads=[w2sb, gh[hh]], writes=[pk], inc=(hh == 1))
    kz = s.sb("c_kz", [64, 512], F32)
    ksq = s.sb("c_ksq", [64, 512], BF16)
    krs = s.sb("c_krs", [64, 512], F32)
    s.op("act", lambda: nc.scalar.activation(out=kz[:, 0:NCW], in_=pk[0:64, 0:NCW], func=AF.Identity, bias=b2c[:, 0:1], scale=1.0),
         reads=[pk, b2c], writes=[kz])
    s.op("act", lambda: nc.scalar.activation(out=ksq[:, 0:NCW], in_=kz[:, 0:NCW], func=AF.Square), reads=[kz], writes=[ksq])
    pq = ps_l[0]
    s.op("pe", lambda: nc.tensor.matmul(pq[0:64, 0:NCW], lhsT=ones64[:, :], rhs=ksq[:, 0:NCW], start=True, stop=True),
         reads=[ones64, ksq], writes=[pq])
    s.op("act", lambda: nc.scalar.activation(out=krs[:, 0:NCW], in_=pq[0:64, 0:NCW], func=AF.Ln, bias=cm["epsc"][0:64, 0:1],
                                             scale=1.0 / 64.0), reads=[pq, cm["epsc"]], writes=[krs])
    s.op("act", lambda: nc.scalar.activation(out=krs[:, 0:NCW], in_=krs[:, 0:NCW], func=AF.Exp, scale=-0.5), reads=[krs], writes=[krs])
    s.op("dve", lambda: nc.vector.scalar_tensor_tensor(out=ktc[0:64, 0:NCW], in0=kz[:, 0:NCW], scalar=kgain[:, 0:1], in1=krs[:, 0:NCW],
                                                       op0=ALU.mult, op1=ALU.mult), reads=[kz, kgain, krs], writes=[ktc])
    for nt in range(NCT):
        n0 = nt * 128
        nn = min(128, Nc - n0)
        pv = ps_l[1 + nt % 2]
        for hh in range(2):
            s.op("pe", lambda: nc.tensor.matmul(pv[0:nn, 0:64], lhsT=gh[2 + hh][:, n0:n0 + nn], rhs=w2sb[:, 1, hh, :],
                                                start=(hh == 0), stop=False), reads=[gh[2 + hh], w2sb], writes=[pv], inc=False)
        s.op("pe", lambda: nc.tensor.matmul(pv[0:nn, 0:64], lhsT=onesrow[0:1, 0:nn], rhs=b2row[0:1, :], start=False, stop=True),
             reads=[onesrow, b2row], writes=[pv])
        s.op("act", lambda: nc.scalar.copy(out=vc[0:nn, nt, 0:64], in_=pv[0:nn, 0:64]), reads=[pv], writes=[vc])
    s.release(m2)

    cmk_rr = RR([s.sb("c_cmk%d" % j, [128, NCT, 128], BF16) for j in range(2)])
    ia_rr = RR([s.sb("c_ia%d" % j, [128, 128], F32) for j in range(2)])
    ib_rr = RR([s.sb("c_ib%d" % j, [128, 128], F32) for j in range(2)])
    imp_rr = RR([s.sb("c_imp%d" % j, [128, 128], F32) for j in range(2)])
    imp2_rr = RR([s.sb("c_impb%d" % j, [128, 128], F32) for j in range(2)])
    m8_rr = RR([s.sb("c_m8%d" % j, [128, 16], F32) for j in range(2)])
    mbT_rr = RR([s.sb("c_mbT%d" % j, [128, 128], BF16) for j in range(2)])
    oco_rr = RR([s.sb("c_oc%d" % j, [128, 4, 64], F32) for j in range(2)])
    gw_rr = RR([s.sb("c_gw%d" % j, [128, 12], F32) for j in range(2)])
    oo_rr = RR([s.sb("c_oo%d" % j, [128, 4, 64], F32) for j in range(2)])
    ob_rr = RR([s.sb("c_ob%d" % j, [128, 4, 64], BF16) for j in range(2)])

    pipe = Pipe(2)

    def masked_tile(kbuf, prow, t, Q, masks, vbuf, vt_idx, po, first, last, extra=None):
        ps = cm["ps_rr"].next()
        nm = len(masks)
        s.op("pe", lambda: nc.tensor.matmul(ps[:, :], lhsT=kbuf[:, t * 128:(t + 1) * 128], rhs=Q,
                                            start=True, stop=(nm == 0)), reads=[kbuf, qc, qc2], writes=[ps], inc=(nm == 0))
        for mi, (la, lb, ra, rb) in enumerate(masks):
            for h in range(4):
                lastm = (mi == nm - 1 and h == 3)
                s.op("pe", lambda: nc.tensor.matmul(ps[:, h * 128:(h + 1) * 128], lhsT=la, rhs=ra, start=False, stop=lastm),
                     reads=[lb, rb], writes=[ps], inc=lastm)
        pt = cm["pt_rr"].next()
        s.op("act", lambda: nc.scalar.activation(out=pt[:, :], in_=ps[:, :], func=AF.Exp), reads=[ps], writes=[pt])

        def back(pt=pt, po=po, vbuf=vbuf, vt_idx=vt_idx, first=first, last=last, extra=extra):
            s.op("pe", lambda: nc.tensor.matmul(po[:, :], lhsT=vbuf[:, vt_idx, :], rhs=pt[:, :], start=first, stop=last),
                 reads=[vbuf, pt], writes=[po])
            if extra is not None:
                extra(pt)
        pipe.push(back)

    for ci in cs:
        def gk(key):
            return io[key][ci] if fused else io[key]
        if fused:
            for h in range(4):
                r0 = (h % 2) * 64
                srcq = io["zq"][h // 2][r0:r0 + 64, :].rearrange("d (i c q) -> d i c q", c=4, q=128)[:, :, ci, :]
                s.dma("sp", qc[0:64, :, h * 128:(h + 1) * 128], srcq, writes=[qc])
                s.dma("sp", qc2[64:128, :, h * 128:(h + 1) * 128], srcq, writes=[qc2])
            msb, mview = io["misc_sb"]
            s.op("act", lambda: nc.scalar.activation(out=ngt[:, :, :], in_=mview[:, ci:NT:4, 4:16], func=AF.Exp, scale=-1.0),
                 reads=[msb], writes=[ngt])
        else:
            s.dma("sp", qc[0:64, :, :], io["qc"][0:64, :, :], writes=[qc])
            s.dma("sp", qc2[64:128, :, :], io["qc"][64:128, :, :], writes=[qc2])
            s.dma("sp", ngt[:, :, :], io["ng"][:, :, :], writes=[ngt])
            s.op("act", lambda: nc.scalar.activation(out=ngt[:, :, :], in_=ngt[:, :, :], func=AF.Exp, scale=-1.0),
                 reads=[ngt], writes=[ngt])
        s.op("dve", lambda: nc.vector.tensor_scalar(out=ngt[:, :, :], in0=ngt[:, :, :], scalar1=1.0, scalar2=None, op0=ALU.add),
             reads=[ngt], writes=[ngt])
        s.op("dve", lambda: nc.vector.reciprocal(out=ngt[:, :, :], in_=ngt[:, :, :]), reads=[ngt], writes=[ngt])
        s.dma("sp", smask[:, :, :], gk("smask")[:, :, :], writes=[smask])
        s.dma("sp", wmask[:, :, :], gk("wmask")[:, :, :], writes=[wmask])
        for i in range(QL):
            Qlo = qc[:, i, :]
            Qhi = qc2[:, i, :]
            cmk = cmk_rr.next()
            ia = ia_rr.next()
            ib = ib_rr.next()
            s.dma("sp", cmk[:, :, :], gk("cmask")[:, i, :, :], writes=[cmk])
            s.dma("sp", ia[:, :], gk("impA")[:, i, :], writes=[ia])
            s.dma("sp", ib[:, :], gk("impB")[:, i, :], writes=[ib])
            po_c, po_s, po_w = po_l[0], po_l[1], po_l[2]
            nct = min(NCT, i // 4 + 1)
            for nt in range(nct):
                def imp_mm(pt, nt=nt, nct=nct):
                    for h in range(4):
                        s.op("pe", lambda: nc.tensor.matmul(pf2[:, h, :], lhsT=pt[:, h * 128:(h + 1) * 128], rhs=ovl[:, nt, :],
                                                            start=(nt == 0 and h == 0), stop=(nt == nct - 1 and h == 3),
                                                            skip_group_check=True), reads=[pt, ovl], writes=[pf2],
                             inc=(h == 3))
                masked_tile(ktc, (0, 64), nt, Qlo, [(ident_b[:, :], ident_b, cmk[:, nt, :], cmk)], vc, nt, po_c,
                            nt == 0, nt == nct - 1, extra=imp_mm)
            pipe.flush()
            emit_o_to_tokmajor(s, cm, po_c, pf, 0)
            st = cm["st_rr"].next()
            rsum = cm["st_rr"].next()
            gw = gw_rr.next()
            s.op("dve", lambda: nc.vector.tensor_scalar(out=st[:, 0:4], in0=pf[:, :, 64], scalar1=1e-30, scalar2=None, op0=ALU.max),
                 reads=[pf], writes=[st])
            s.op("dve", lambda: nc.vector.reciprocal(out=rsum[:, 0:4], in_=st[:, 0:4]), reads=[st], writes=[rsum])
            oco = oco_rr.next()
            s.op("dve", lambda: nc.vector.tensor_copy(out=oco[:, :, :], in_=pf[:, :, 0:64]), reads=[pf], writes=[oco])
            imp = imp_rr.next()
            s.op("dve", lambda: nc.vector.tensor_scalar(out=imp[:, :], in0=pf2[:, 0, :], scalar1=rsum[:, 0:1], scalar2=None, op0=ALU.mult),
                 reads=[pf2, rsum], writes=[imp])
            for h in range(1, 4):
                s.op("dve", lambda: nc.vector.scalar_tensor_tensor(out=imp[:, :], in0=pf2[:, h, :], scalar=rsum[:, h:h + 1], in1=imp[:, :],
                                                                   op0=ALU.mult, op1=ALU.add), reads=[pf2, rsum, imp], writes=[imp])
            s.op("dve", lambda: nc.vector.tensor_tensor(out=imp[:, :], in0=imp[:, :], in1=ia[:, :], op=ALU.mult), reads=[imp, ia], writes=[imp])
            s.op("dve", lambda: nc.vector.tensor_tensor(out=imp[:, :], in0=imp[:, :], in1=ib[:, :], op=ALU.add), reads=[imp, ib], writes=[imp])
            m8 = m8_rr.next()
            imp2 = imp2_rr.next()
            s.op("dve", lambda: nc.vector.max(out=m8[:, 0:8], in_=imp[:, :]), reads=[imp], writes=[m8])
            s.op("dve", lambda: nc.vector.match_replace(out=imp2[:, :], in_to_replace=m8[:, 0:8], in_values=imp[:, :], imm_value=-1e9),
                 reads=[imp, m8], writes=[imp2])
            s.op("dve", lambda: nc.vector.max(out=m8[:, 8:16], in_=imp2[:, :]), reads=[imp2], writes=[m8])
            s.op("dve", lambda: nc.vector.tensor_scalar(out=imp2[:, :], in0=imp[:, :], scalar1=m8[:, 15:16], scalar2=NEG,
                                                        op0=ALU.is_lt, op1=ALU.mult), reads=[imp, m8], writes=[imp2])
            tl = [4 * (i - 1) + u for u in range(8) if 4 * (i - 1) + u >= 0]
            for t in tl:
                u = t - 4 * (i - 1)
                masked_tile(kk, (64, 128), t, Qhi, [(ident_b[:, :], ident_b, wmask[:, u, :], wmask)], vw, t, po_w,
                            t == tl[0], t == tl[-1])
            ptr = cm["ps_rr"].next()
            s.op("pe", lambda: nc.tensor.transpose(out=ptr[:, 0:128], in_=imp2[:, :], identity=cm["ident_f"][:, :]),
                 reads=[imp2, cm["ident_f"]], writes=[ptr])
            mbT = mbT_rr.next()
            s.op("dve", lambda: nc.vector.tensor_copy(out=mbT[:, :], in_=ptr[:, 0:128]), reads=[ptr], writes=[mbT])
            nts = 4 * i + 4
            for t in range(nts):
                masks = [(emat[:, t, :], emat, mbT[:, :], mbT)]
                if t >= 4 * i:
                    masks.append((ident_b[:, :], ident_b, smask[:, t - 4 * i, :], smask))
                masked_tile(kk, (0, 64), t, Qlo, masks, vs, t, po_s, t == 0, t == nts - 1)
            pipe.flush()
            emit_o_to_tokmajor(s, cm, po_s, pf, 0)
            st2 = cm["st_rr"].next()
            s.op("dve", lambda: nc.vector.tensor_scalar(out=st2[:, 0:4], in0=pf[:, :, 64], scalar1=1e-30, scalar2=None, op0=ALU.max),
                 reads=[pf], writes=[st2])
            s.op("dve", lambda: nc.vector.reciprocal(out=st2[:, 0:4], in_=st2[:, 0:4]), reads=[st2], writes=[st2])
            gv = ngt[:, i, :].rearrange("p (h b) -> p h b", b=3)
            gwv = gw[:, :].rearrange("p (h b) -> p h b", b=3)
            s.op("dve", lambda: nc.vector.tensor_tensor(out=gwv[:, :, 0], in0=gv[:, :, 0], in1=rsum[:, 0:4], op=ALU.mult),
                 reads=[ngt, rsum], writes=[gw])
            s.op("dve", lambda: nc.vector.tensor_tensor(out=gwv[:, :, 1], in0=gv[:, :, 1], in1=st2[:, 0:4], op=ALU.mult),
                 reads=[ngt, st2], writes=[gw])
            oo = oo_rr.next()
            for h in range(4):
                s.op("dve", lambda: nc.vector.tensor_scalar(out=oo[:, h, :], in0=oco[:, h, :], scalar1=gw[:, 3 * h:3 * h + 1], scalar2=None,
                                                            op0=ALU.mult), reads=[oco, gw], writes=[oo])
                s.op("dve", lambda: nc.vector.scalar_tensor_tensor(out=oo[:, h, :], in0=pf[:, h, 0:64], scalar=gw[:, 3 * h + 1:3 * h + 2],
                                                                   in1=oo[:, h, :], op0=ALU.mult, op1=ALU.add), reads=[pf, gw, oo], writes=[oo])
            emit_o_to_tokmajor(s, cm, po_w, pf, 0)
            st3 = cm["st_rr"].next()
            s.op("dve", lambda: nc.vector.tensor_scalar(out=st3[:, 0:4], in0=pf[:, :, 64], scalar1=1e-30, scalar2=None, op0=ALU.max),
                 reads=[pf], writes=[st3])
            s.op("dve", lambda: nc.vector.reciprocal(out=st3[:, 0:4], in_=st3[:, 0:4]), reads=[st3], writes=[st3])
            s.op("dve", lambda: nc.vector.tensor_tensor(out=gwv[:, :, 2], in0=gv[:, :, 2], in1=st3[:, 0:4], op=ALU.mult),
                 reads=[ngt, st3], writes=[gw])
            ob = ob_rr.next()
            for h in range(4):
                s.op("dve", lambda: nc.vector.scalar_tensor_tensor(out=ob[:, h, :], in0=pf[:, h, 0:64], scalar=gw[:, 3 * h + 2:3 * h + 3],
                                                                   in1=oo[:, h, :], op0=ALU.mult, op1=ALU.add), reads=[pf, gw, oo], writes=[ob])
            orow = ((4 * i + ci) if fused else i) * 128
            s.dma("sp", io["oc"][orow:orow + 128, :], ob[:, :, :].rearrange("p h d -> p (h d)"), reads=[ob])
    s.release(m)


def build_mix(T, parts="abc"):
    nc = bass.Bass("TRN2", target_bir_lowering=False)
    io = mix_decl(nc, T)
    if "c" in parts:
        mix_decl_c(nc, io, T)
    s = S(nc)
    cm = mix_common(s, io)
    if "a" in parts:
        emit_mix_a(s, cm, io, T)
    if "b" in parts:
        emit_mix_b(s, cm, io, T)
    if "c" in parts:
        emit_mix_c(s, cm, io, T)
    s.finish()
    s.close()
    return nc


def mix_consts_c(T, c):
    NT = T // 128
    QL = NT // 4
    NCT = max(1, T // 2048)
    Nc = T // 16 - 1
    NS = T // 64
    ar = np.arange(128)
    cmask = np.zeros((128, QL, NCT, 128), np.float32)
    impA = np.zeros((128, QL, 128), np.float32)
    impB = np.zeros((128, QL, 128), np.float32)
    for i in range(QL):
        qpos = 128 * (4 * i + c) + ar
        for nt in range(NCT):
            n = 128 * nt + ar
            ok = (16 * n[:, None] + 31 <= qpos[None, :]) & (n[:, None] < Nc)
            cmask[:, i, nt, :] = np.where(ok, 0.0, NEG)
        j = ar
        cur = qpos // 64
        forced = (j[None, :] == 0) | (j[None, :] == cur[:, None]) | (j[None, :] == cur[:, None] - 1)
        valid = (j[None, :] * 64 <= qpos[:, None]) & (j[None, :] < NS)
        impA[:, i, :] = (valid & ~forced).astype(np.float32)
        impB[:, i, :] = np.where(forced & (j[None, :] < NS), 1.0e4, np.where(valid, 0.0, -1.0))
    smask = np.zeros((128, 4, 128), np.float32)
    for u in range(4):
        kpos = 128 * u + ar
        qp = 128 * c + ar
        smask[:, u, :] = np.where(kpos[:, None] <= qp[None, :], 0.0, NEG)
    wmask = np.zeros((128, 8, 128), np.float32)
    for u in range(8):
        dist = 128 * (c + 4 - u) + ar[None, :] - ar[:, None]
        wmask[:, u, :] = np.where((dist >= 0) & (dist < 512), 0.0, NEG)
    emat = np.zeros((128, NT, 128), np.float32)
    for t in range(NT):
        for k in range(128):
            jj = 2 * t + k // 64
            if jj < 128:
                emat[jj, t, k] = 1.0
    ovl = np.zeros((128, NCT, 128), np.float32)
    for nt in range(NCT):
        n = 128 * nt + ar
        o = (n[:, None] * 16 < (ar[None, :] + 1) * 64) & (n[:, None] * 16 + 32 > ar[None, :] * 64) & (n[:, None] < Nc) \
            & (ar[None, :] < NS)
        ovl[:, nt, :] = o
    return dict(cmask=_bf(cmask), impA=impA, impB=impB, smask=_bf(smask), wmask=_bf(wmask), emat=_bf(emat), ovl=_bf(ovl),
                ones64=_bf(np.ones((64, 64), np.float32)), onesrow=_bf(np.ones((1, 128), np.float32)))


def build_merge(NT, TB=512):
    nc = bass.Bass("TRN2", target_bir_lowering=False)
    x = dram_in(nc, "x", [NT, D])
    g = dram_in(nc, "g", [D])
    w_in = dram_in(nc, "w_in", [D, 6800])
    w_br = dram_in(nc, "w_br", [4, 256, D])
    w_o = dram_in(nc, "w_o", [D, D])
    ident = dram_in(nc, "ident", [128, 128], BF16)
    obr = dram_in(nc, "obr", [NT, D], BF16)
    y = dram_out(nc, "y", [NT, D])
    s = S(nc)
    emit_merge(s, x, g, w_in, w_br, w_o, ident, obr, y, NT, TB)
    s.finish()
    s.close()
    return nc


def emit_merge(s, x, g, w_in, w_br, w_o, ident, obr, y, NT, TB=512):
    nc = s.nc
    m_ = s.mark()
    ntile = TB // 128
    ident_b = s.sb("ident_b", [128, 128], BF16)
    s.dma("sp", ident_b[:, :], ident[:, :], writes=[ident_b])
    gcol = s.sb("gcol", [128, NKC], F32)
    s.dma("sp", gcol[:, :], g.rearrange("(c p) -> p c", p=128), writes=[gcol], allow_slow_non_contiguous=True)
    epsc = s.sb("epsc", [128, 1], F32)
    s.op("dve", lambda: nc.vector.memset(epsc[:, :], EPS), writes=[epsc])
    wg = [s.sb("wg%d" % c, [128, 4096], BF16) for c in range(NKC)]
    wb = [s.sb("wb%d" % c, [128, D], BF16) for c in range(8)]
    wo = [s.sb("wo%d" % c, [128, D], BF16) for c in range(NKC)]
    for c in range(NKC):
        for hf in range(2):
            s.dma("pool", wg[c][:, hf * 2048:(hf + 1) * 2048], w_in[c * 128:(c + 1) * 128, 2704 + hf * 2048:2704 + (hf + 1) * 2048],
                  writes=[wg[c]])
    for n in range(4):
        for cc in range(2):
            s.dma("pool", wb[2 * n + cc][:, :], w_br[n, cc * 128:(cc + 1) * 128, :], writes=[wb[2 * n + cc]])
    for c in range(NKC):
        s.dma("pool", wo[c][:, :], w_o[c * 128:(c + 1) * 128, :], writes=[wo[c]])
    xn = [s.sb("xn%d" % j, [128, D], F32) for j in range(ntile)]
    xr_rr = RR([s.sb("xr%d" % j, [128, D], F32) for j in range(2)])
    ots = [[s.sb("ot%d_%d" % (k, j), [128, D], BF16) for j in range(ntile)] for k in range(2)]
    hb_rr = RR([s.sb("hb%d" % j, [128, D], BF16) for j in range(2 * ntile)])
    stat_rr = RR([s.sb("st%d" % j, [128, 16], F32) for j in range(4)])
    hT = s.sb("hT", [128, NKC, TB], BF16)
    oT = s.sb("oT", [128, 8, TB], BF16)
    mT = [s.sb("mT%d" % c, [128, TB], BF16) for c in range(8)]
    pT_rr = RR([s.ps("pT%d" % j, [128, TB], BF16) for j in range(2)])
    pg_rr = RR([s.ps("pg%d" % j, [128, 512], F32) for j in range(2)])
    pp_rr = RR([s.ps("pp%d" % j, [128, 512], F32) for j in range(2)])
    po_rr = RR([s.ps("po%d" % j, [128, 512], F32) for j in range(2)])
    sg_rr = RR([s.sb("sg%d" % j, [128, TB], F32) for j in range(3)])
    acc_rr = RR([s.sb("acc%d" % j, [128, TB], F32) for j in range(2)])
    nblk = NT // TB

    def prep_a(tb):
        ot = ots[tb % 2]
        for j in range(ntile):
            r0 = tb * TB + j * 128
            s.dma("sp", xn[j][:, :], x[r0:r0 + 128, :], writes=[xn[j]])
            s.dma("sp", ot[j][:, :], obr[r0:r0 + 128, :], writes=[ot[j]])
        return emit_norm(s, epsc, xn, hb_rr, None, stat_rr, ntile)

    def prep_b(tb, hbs):
        ot = ots[tb % 2]
        emit_transpose_T(s, hbs, gcol, hT, ident_b, pT_rr, ntile)
        for c in range(8):
            pT = pT_rr.next()
            for j in range(ntile):
                s.op("pe", lambda: nc.tensor.transpose(out=pT[:, j * 128:(j + 1) * 128], in_=ot[j][:, c * 128:(c + 1) * 128],
                                                       identity=ident_b[:, :]), reads=[ot[j], ident_b], writes=[pT], inc=(j == ntile - 1))
            s.op("dve", lambda: nc.vector.tensor_copy(out=oT[:, c, :], in_=pT[:, 0:TB]), reads=[pT], writes=[oT])

    hbs_next = prep_a(0)
    prep_b(0, hbs_next)
    for tb in range(nblk):
        t0 = tb * TB
        if tb + 1 < nblk:
            hbs_next = prep_a(tb + 1)
        for dc in range(8):
            acc = acc_rr.next()
            for n in range(4):
                pg = pg_rr.next()
                pp = pp_rr.next()
                for c in range(NKC):
                    s.op("pe", lambda: nc.tensor.matmul(pg[:, 0:TB], lhsT=wg[c][:, n * 1024 + dc * 128:n * 1024 + (dc + 1) * 128],
                                                        rhs=hT[:, c, :], start=(c == 0), stop=(c == NKC - 1)),
                         reads=[wg[c], hT], writes=[pg], inc=(c == NKC - 1))
                for cc in range(2):
                    s.op("pe", lambda: nc.tensor.matmul(pp[:, 0:TB], lhsT=wb[2 * n + cc][:, dc * 128:(dc + 1) * 128],
                                                        rhs=oT[:, 2 * n + cc, :], start=(cc == 0), stop=(cc == 1)),
                         reads=[wb[2 * n + cc], oT], writes=[pp], inc=(cc == 1))
                sg = sg_rr.next()
                s.op("act", lambda: nc.scalar.activation(out=sg[:, :], in_=pg[:, 0:TB], func=AF.Sigmoid), reads=[pg], writes=[sg])
                if n == 0:
                    s.op("dve", lambda: nc.vector.tensor_tensor(out=acc[:, :], in0=sg[:, :], in1=pp[:, 0:TB], op=ALU.mult),
                         reads=[sg, pp], writes=[acc])
                else:
                    s.op("dve", lambda: nc.vector.tensor_tensor(out=sg[:, :], in0=sg[:, :], in1=pp[:, 0:TB], op=ALU.mult),
                         reads=[sg, pp], writes=[sg])
                    if n < 3:
                        s.op("pool", lambda: nc.gpsimd.tensor_tensor(out=acc[:, :], in0=acc[:, :], in1=sg[:, :], op=ALU.add),
                             reads=[acc, sg], writes=[acc])
                    else:
                        s.op("pool", lambda: nc.gpsimd.tensor_tensor(out=mT[dc][:, :], in0=acc[:, :], in1=sg[:, :], op=ALU.add),
                             reads=[acc, sg], writes=[mT[dc]])
        if tb + 1 < nblk:
            prep_b(tb + 1, hbs_next)
        for j in range(ntile):
            xr = xr_rr.next()
            s.dma("sp", xr[:, :], x[t0 + j * 128:t0 + (j + 1) * 128, :], writes=[xr])
            for hf in range(2):
                po = po_rr.next()
                for dc in range(8):
                    s.op("pe", lambda: nc.tensor.matmul(po[:, :], lhsT=mT[dc][:, j * 128:(j + 1) * 128],
                                                        rhs=wo[dc][:, hf * 512:(hf + 1) * 512], start=(dc == 0), stop=(dc == 7)),
                         reads=[mT[dc], wo[dc]], writes=[po], inc=(dc == 7))
                s.op("dve", lambda: nc.vector.tensor_tensor(out=xr[:, hf * 512:(hf + 1) * 512], in0=po[:, :],
                                                            in1=xr[:, hf * 512:(hf + 1) * 512], op=ALU.add),
                     reads=[po, xr], writes=[xr])
            s.dma("sp", y[t0 + j * 128:t0 + (j + 1) * 128, :], xr[:, :], reads=[xr])
    s.release(m_)


PARAM_SHAPES = dict(
    ffn1_norm=("L", D), ffn1_w_in=("L", D, 2 * DFF), ffn1_w_out=("L", DFF, D), mix_norm=("L", D), w_in=("L", D, 6800),
    nsa_phi_w1=("L", 2, 2048, 256), nsa_phi_w2=("L", 2, 256, 64), nsa_phi_b2=("L", 2, 64),
    w_branch=("L", 4, 256, D), w_out=("L", D, D), ffn2_norm=("L", D), ffn2_w_in=("L", D, 2 * DFF), ffn2_w_out=("L", DFF, D),
    gains=("L", 128, 6), vgain=("L", 128, 256), wsT=("L", 4, 128, 128), bsT=("L", 128, 4), lamp=("L", 128, 4, 32),
    lami=("L", 128, 2), fbias=("L", 4, 128, 1), b1l=("L", 128, 4), peT=("L", 128, 32), b2c=("L", 64, 1), kgain=("L", 64, 1),
)


def fused_const_shapes(T):
    NT = T // 128
    QL = NT // 4
    NCT = max(1, T // 2048)
    return dict(
        ident=([128, 128], BF16), identf=([128, 128], F32), blk=([2, 128, 128], BF16), triu=([128, 128], F32),
        trib=([128, 128], BF16), triuf=([128, 128], F32), onesf=([128, 128], F32), ones64=([64, 64], BF16),
        onesrow=([1, 128], BF16), cmask=([4, 128, QL, NCT, 128], BF16), smask=([4, 128, 4, 128], BF16),
        wmask=([4, 128, 8, 128], BF16), impA=([4, 128, QL, 128], F32), impB=([4, 128, QL, 128], F32),
        emat=([128, NT, 128], BF16), ovl=([128, NCT, 128], BF16))


def fused_consts(T):
    pc = proj_consts()
    mc = mix_consts()
    cc = [mix_consts_c(T, c) for c in range(4)]
    d = dict(ident=pc["ident"], identf=mc["identf"], blk=pc["blk"], triu=pc["triu"], trib=mc["trib"], triuf=mc["triuf"],
             onesf=mc["onesf"], ones64=cc[0]["ones64"], onesrow=cc[0]["onesrow"], emat=cc[0]["emat"], ovl=cc[0]["ovl"])
    for k in ("cmask", "smask", "wmask", "impA", "impB"):
        d[k] = np.ascontiguousarray(np.stack([cc[c][k] for c in range(4)], 0))
    return d


def emit_mix_fused(s, F, l, T):
    nc = s.nc
    NT = T // 128
    m = s.mark()
    io0 = dict(identb=F["ident"], identf=F["identf"], trib=F["trib"])
    misc_sb = s.sb("misc_sb", [128, NT, 16], F32)
    m_ab = s.mark()
    cm = mix_common(s, io0, n_ps=4, with_pf2=False)
    for j0 in range(0, NT, 8):
        j1 = min(NT, j0 + 8)
        s.dma("sp", misc_sb[:, j0:j1, :], F["misc"][j0 * 128:j1 * 128, :].rearrange("(j p) c -> p j c", p=128), writes=[misc_sb])
    zfm, vab, obr = F["zfm"], F["vab"], F["obr"]

    def io_a(h):
        r0 = (h % 2) * 64
        io = dict(io0)
        io.update(qa=zfm[h // 2][r0:r0 + 64, :], ka=zfm[2 + h // 2][r0:r0 + 64, :], va_src=vab[:, h * 64:(h + 1) * 64],
                  lamp=F["lamp"][l], lami=F["lami"][l], oa=obr[:, h * 64:(h + 1) * 64])
        return io

    def io_b(h):
        r0 = (h % 2) * 64
        io = dict(io0)
        io.update(qb=zfm[4 + h // 2][r0:r0 + 64, :], kb=zfm[6 + h // 2][r0:r0 + 64, :],
                  vb_src=vab[:, 256 + h * 64:256 + (h + 1) * 64], flog_sb=(misc_sb, misc_sb[:, :, h]), fbias=F["fbias"][l, h],
                  triuf=F["triuf"], onesf=F["onesf"], ob=obr[:, 256 + h * 64:256 + (h + 1) * 64])
        return io

    ba = emit_mix_a(s, cm, io_a(0), T, stage="alloc")
    bb = emit_mix_b(s, cm, io_b(0), T, stage="alloc")
    emit_mix_a(s, cm, io_a(0), T, stage="load", bufs=ba)
    emit_mix_b(s, cm, io_b(0), T, stage="load", bufs=bb)
    for h in range(4):
        emit_mix_a(s, cm, io_a(h), T, stage="compute", bufs=ba)
        if h + 1 < 4:
            emit_mix_a(s, cm, io_a(h + 1), T, stage="load", bufs=ba)
        emit_mix_b(s, cm, io_b(h), T, stage="compute", bufs=bb)
        if h + 1 < 4:
            emit_mix_b(s, cm, io_b(h + 1), T, stage="load", bufs=bb)
    s.release(m_ab)
    cm = mix_common(s, io0, n_ps=3, with_pf2=True)
    io = dict(io0)
    io.update(zq=(zfm[8], zfm[9]), kskw=zfm[10], kvin=zfm[11], vs_src=F["vsw"][:, 0:64], vw_src=F["vsw"][:, 64:128],
              misc_sb=(misc_sb, misc_sb), w1=F["nsa_phi_w1"][l], b1=F["b1l"][l], peT=F["peT"][l], w2=F["nsa_phi_w2"][l],
              b2=F["nsa_phi_b2"][l], b2c=F["b2c"][l], kgain=F["kgain"][l], oc=obr[:, 512:768])
    for k in ("cmask", "smask", "wmask", "impA", "impB", "emat", "ovl", "ones64", "onesrow"):
        io[k] = F[k]
    emit_mix_c(s, cm, io, T, cs=[0, 1, 2, 3])
    s.release(m)


def build_fused(T, L):
    nc = bass.Bass("TRN2", target_bir_lowering=False)
    F = {}
    F["x"] = dram_in(nc, "x", [T, D])
    for k, shp in PARAM_SHAPES.items():
        F[k] = dram_in(nc, k, [L if v == "L" else v for v in shp])
    for k, (shp, dt) in fused_const_shapes(T).items():
        F[k] = dram_in(nc, k, shp, dt)
    y = dram_out(nc, "y", [T, D])
    for k, shp, dt in (("xa", [T, D], F32), ("xb", [T, D], F32), ("xc", [T, D], F32), ("zfm", [NFM, 128, T], BF16),
                       ("vab", [T, 512], BF16), ("vsw", [T, 128], BF16), ("misc", [T, 16], F32), ("obr", [T, D], BF16)):
        F[k] = nc.dram_tensor("s_" + k, shp, dt).ap()
    s = S(nc)
    for l in range(L):
        x_in = F["x"] if l == 0 else F["xc"]
        emit_ffn(s, x_in, F["ffn1_norm"][l], F["ffn1_w_in"][l], F["ffn1_w_out"][l], F["ident"], F["xa"], T)
        a = dict(x=F["xa"], g=F["mix_norm"][l], w_in=F["w_in"][l], ident=F["ident"], gains=F["gains"][l], blk=F["blk"],
                 vgain=F["vgain"][l], wsT=F["wsT"][l], triu=F["triu"], bsT=F["bsT"][l], zfm=F["zfm"], vab=F["vab"],
                 vsw=F["vsw"], misc=F["misc"], od=F["obr"][:, 768:1024])
        emit_proj(s, a, T)
        emit_mix_fused(s, F, l, T)
        emit_merge(s, F["xa"], F["mix_norm"][l], F["w_in"][l], F["w_branch"][l], F["w_out"][l], F["ident"], F["obr"], F["xb"], T)
        x_out = y if l == L - 1 else F["xc"]
        emit_ffn(s, F["xb"], F["ffn2_norm"][l], F["ffn2_w_in"][l], F["ffn2_w_out"][l], F["ident"], x_out, T)
    s.finish()
    s.close()
    return nc


def fused_params(P, L):
    import math
    f32 = np.float32
    A = lambda a: np.ascontiguousarray(np.asarray(a, dtype=f32))
    d = {k: A(P[k]) for k in ("ffn1_norm", "ffn1_w_in", "ffn1_w_out", "mix_norm", "w_in", "nsa_phi_w1", "nsa_phi_w2",
                              "nsa_phi_b2", "w_branch", "w_out", "ffn2_norm", "ffn2_w_in", "ffn2_w_out")}
    tile = lambda v, n: np.tile(A(v), (1, n))
    d["gains"] = np.ascontiguousarray(np.stack([tile(P["diff_q_gain"], 4), tile(P["diff_k_gain"], 4), tile(P["fox_q_gain"], 2),
                                                tile(P["fox_k_gain"], 2), tile(P["nsa_q_gain"], 2), tile(P["nsa_k_gain"], 2)], 2))
    d["vgain"] = np.ascontiguousarray(np.broadcast_to(A(P["gmlp_v_gain"])[:, None, :], (L, 128, 256)))
    d["wsT"] = np.ascontiguousarray(A(P["gmlp_w_s"]).transpose(0, 1, 3, 2))
    d["bsT"] = np.ascontiguousarray(A(P["gmlp_b_s"]).transpose(0, 2, 1))
    d["lamp"] = np.ascontiguousarray(np.broadcast_to(A(P["diff_lambda"])[:, None], (L, 128, 4, 32)))
    li = np.array([[0.8 - 0.6 * math.exp(-0.3 * l), 1.0 - (0.8 - 0.6 * math.exp(-0.3 * l))] for l in range(L)], f32)
    d["lami"] = np.ascontiguousarray(np.broadcast_to(li[:, None, :], (L, 128, 2)))
    d["fbias"] = np.ascontiguousarray(np.broadcast_to(A(P["fox_f_bias"])[:, :, None, None], (L, 4, 128, 1)))
    d["b1l"] = np.ascontiguousarray(A(P["nsa_phi_b1"]).reshape(L, 2, 2, 128).transpose(0, 3, 1, 2).reshape(L, 128, 4))
    pe = A(P["nsa_cmp_pe"])
    d["peT"] = np.ascontiguousarray(pe.transpose(0, 1, 3, 2).reshape(L, 128, 32))
    d["b2c"] = np.ascontiguousarray(A(P["nsa_phi_b2"])[:, 0, :, None])
    d["kgain"] = np.ascontiguousarray(A(P["nsa_k_gain"])[:, :, None])
    return d


B_, T_, L_ = 2, 8192, 2
_PROG = {}


def kernel(**inputs):
    x = np.ascontiguousarray(np.asarray(inputs["x"], dtype=np.float32))
    if "fused" not in _PROG:
        _PROG["fused"] = build_fused(T_, L_)
        _PROG["consts"] = fused_consts(T_)
    nc = _PROG["fused"]
    par = fused_params(inputs, L_)
    in_maps = []
    for b in range(B_):
        d = dict(par)
        d.update(_PROG["consts"])
        d["x"] = x[b]
        in_maps.append(d)
    res = run_bass_kernel_spmd(nc, in_maps, core_ids=list(range(B_)))
    return np.stack([np.asarray(res.results[b]["y"], dtype=np.float32) for b in range(B_)], 0)
```
